# Optimizing a Trainium2 kernel written in Bass

```python
import jax, jax.numpy as jnp
from jax import lax
import numpy as np

D_MODEL = 1024
BATCH = 16
SEQ = 4096
DEPTH = 4
DEC_BATCH = 8
DEC_SEQ = 32
PAST_LEN = 2048

CHUNK = 64
Q_BLOCK = 128
HEAD_DIM = 64
RW_HEADS = 4
RW_WIDTH = RW_HEADS * HEAD_DIM
FOX_HEADS = 8
FOX_WIDTH = FOX_HEADS * HEAD_DIM
HG_HEADS = 4
HG_WIDTH = HG_HEADS * HEAD_DIM
D_MIX = RW_WIDTH + FOX_WIDTH + HG_WIDTH
RW_W_LORA = 32
RW_A_LORA = 32
RW_G_LORA = 64
RW_COLS = 3 * RW_WIDTH + RW_W_LORA + RW_A_LORA + RW_G_LORA
FOX_COLS = 3 * FOX_WIDTH + FOX_HEADS
HG_COLS = 4 * HG_WIDTH
IN_COLS = RW_COLS + FOX_COLS + HG_COLS
D_FF = -(-8 * D_MODEL // (3 * 256)) * 256
EPS = 1e-6
RW_GN_EPS = 64e-5
FOX_SCALE = HEAD_DIM ** -0.5
RW_SPLITS = [RW_WIDTH, 2 * RW_WIDTH, 3 * RW_WIDTH, 3 * RW_WIDTH + RW_W_LORA, 3 * RW_WIDTH + RW_W_LORA + RW_A_LORA]
F32 = jnp.float32

kernel_name = "hybrid_stream_rwkv7_fox_hgrn2_step"


def rmsnorm(x, g):
    xf = x.astype(F32)
    return xf * lax.rsqrt(jnp.mean(xf * xf, axis=-1, keepdims=True) + EPS) * g.astype(F32)


def rwkv7_mix(p, prev, S0, mu, w0, w2, a0, a2, g2, k_k, k_a, r_k, ln_w, ln_b):
    B, T, _ = p.shape
    p = p.astype(F32)
    p_prev = jnp.concatenate([prev.astype(F32), p[:, :-1]], axis=1)
    xs = p + mu * (p_prev - p)
    r, k, v, wl, al, gl = jnp.split(xs, RW_SPLITS, axis=-1)
    w_raw = -jax.nn.softplus(-(w0 + jnp.tanh(wl) @ w2)) - 0.5
    w = jnp.exp(-jnp.exp(w_raw))
    a = jax.nn.sigmoid(a0 + al @ a2)
    g = jax.nn.sigmoid(gl) @ g2
    hd = lambda z: z.reshape(B, T, RW_HEADS, HEAD_DIM)
    kk = hd(k * k_k)
    kk = kk / jnp.maximum(jnp.sqrt(jnp.sum(kk * kk, axis=-1, keepdims=True)), 1e-12)
    k = hd(k * (1.0 + (a - 1.0) * k_a))
    r, v, a, w = hd(r), hd(v), hd(a), hd(w)

    def step(S, inp):
        r_t, k_t, v_t, kk_t, a_t, w_t = inp
        s_kk = jnp.einsum('bhvk,bhk->bhv', S, kk_t)
        S = (S * w_t[:, :, None, :] - s_kk[..., None] * (kk_t * a_t)[:, :, None, :]
             + v_t[..., None] * k_t[:, :, None, :])
        return S, jnp.einsum('bhvk,bhk->bhv', S, r_t)

    tm = lambda z: jnp.swapaxes(z, 0, 1)
    S_T, y = lax.scan(step, S0.astype(F32), (tm(r), tm(k), tm(v), tm(kk), tm(a), tm(w)))
    y = tm(y)
    mean = jnp.mean(y, axis=-1, keepdims=True)
    var = jnp.mean(jnp.square(y - mean), axis=-1, keepdims=True)
    y = ((y - mean) * lax.rsqrt(var + RW_GN_EPS)).reshape(B, T, RW_WIDTH) * ln_w + ln_b
    bonus = jnp.sum(r * k * r_k, axis=-1, keepdims=True) * v
    y = (y + bonus.reshape(B, T, RW_WIDTH)) * g
    return y, S_T


def fox_attend(q, c_q, q_pos, k, v, c_k, k_pos):
    s = jnp.einsum('bqhd,bkhd->bhqk', q, k) * FOX_SCALE
    bias = jnp.swapaxes(c_q, 1, 2)[..., :, None] - jnp.swapaxes(c_k, 1, 2)[..., None, :]
    s = jnp.where(k_pos[None, :] <= q_pos[:, None], s + bias, -jnp.inf)
    return jnp.einsum('bhqk,bkhd->bqhd', jax.nn.softmax(s, axis=-1), v)


def fox_mix(p, b_f, cache_k, cache_v, cache_logf):
    B, T, _ = p.shape
    p = p.astype(F32)
    q, k, v, f = jnp.split(p, [FOX_WIDTH, 2 * FOX_WIDTH, 3 * FOX_WIDTH], axis=-1)
    q = q.reshape(B, T, FOX_HEADS, HEAD_DIM)
    k = k.reshape(B, T, FOX_HEADS, HEAD_DIM)
    v = v.reshape(B, T, FOX_HEADS, HEAD_DIM)
    logf = jax.nn.log_sigmoid(f + b_f)
    if cache_k is None:
        c = jnp.cumsum(logf, axis=1)
        k_pos = jnp.arange(T)

        def block(i):
            start = i * Q_BLOCK
            qb = lax.dynamic_slice_in_dim(q, start, Q_BLOCK, axis=1)
            cb = lax.dynamic_slice_in_dim(c, start, Q_BLOCK, axis=1)
            return fox_attend(qb, cb, start + jnp.arange(Q_BLOCK), k, v, c, k_pos)

        o = lax.map(block, jnp.arange(T // Q_BLOCK))
        o = jnp.swapaxes(o, 0, 1).reshape(B, T, FOX_WIDTH)
    else:
        P = cache_k.shape[1]
        k_all = jnp.concatenate([cache_k.astype(F32), k], axis=1)
        v_all = jnp.concatenate([cache_v.astype(F32), v], axis=1)
        c_all = jnp.cumsum(jnp.concatenate([cache_logf.astype(F32), logf], axis=1), axis=1)
        o = fox_attend(q, c_all[:, P:], P + jnp.arange(T), k_all, v_all, c_all, jnp.arange(P + T))
        o = o.reshape(B, T, FOX_WIDTH)
    return o, k, v, logf


def hgrn2_mix(p, lb, S0, norm_g, chunk):
    B, T, _ = p.shape
    p = p.astype(F32)
    q, fx, i, g = jnp.split(p, 4, axis=-1)
    log_f = jnp.logaddexp(jnp.log(lb), jnp.log1p(-lb) + jax.nn.log_sigmoid(fx))
    k = (1.0 - lb) * jax.nn.sigmoid(-fx)
    n = T // chunk

    def to_chunks(z):
        return z.reshape(B, n, chunk, HG_HEADS, HEAD_DIM).transpose(1, 0, 3, 2, 4)

    causal = jnp.tril(jnp.ones((chunk, chunk), dtype=bool))[:, :, None]

    def step(S, inp):
        qc, kc, vc, lfc = inp
        b = jnp.cumsum(lfc, axis=2)
        o_inter = jnp.einsum('bhtk,bhkv->bhtv', qc * jnp.exp(b), S)
        diff = b[:, :, :, None, :] - b[:, :, None, :, :]
        dec = jnp.where(causal, jnp.exp(jnp.where(causal, diff, 0.0)), 0.0)
        attn = jnp.sum(qc[:, :, :, None, :] * kc[:, :, None, :, :] * dec, axis=-1)
        o = o_inter + jnp.einsum('bhts,bhsv->bhtv', attn, vc)
        b_last = b[:, :, -1:, :]
        S = (jnp.exp(b_last)[:, :, 0, :, None] * S
             + jnp.einsum('bhsk,bhsv->bhkv', kc * jnp.exp(b_last - b), vc))
        return S, o

    S_T, o = lax.scan(step, S0.astype(F32), (to_chunks(q), to_chunks(k), to_chunks(i), to_chunks(log_f)))
    o = o.transpose(1, 0, 3, 2, 4).reshape(B, T, HG_HEADS, HEAD_DIM)
    o = o * lax.rsqrt(jnp.mean(o * o, axis=-1, keepdims=True) + EPS)
    o = o.reshape(B, T, HG_WIDTH) * norm_g * jax.nn.silu(g)
    return o, S_T


def trunk_layer(x, c, p, rw_prev, rw_S, hg_S, fox_cache, hg_chunk):
    mod = jax.nn.silu(c.astype(F32)) @ p["w_ada"] + p["b_ada"]
    sh1, sc1, ga1, sh2, sc2, ga2 = [m[:, None, :] for m in jnp.split(mod, 6, axis=-1)]
    h = rmsnorm(x, p["norm1_g"]) * (1.0 + sc1) + sh1
    proj = h @ p["w_in"]
    p_rw, p_fox, p_hg = jnp.split(proj, [RW_COLS, RW_COLS + FOX_COLS], axis=-1)
    y_rw, rw_S_new = rwkv7_mix(p_rw, rw_prev, rw_S, *p["rw"])
    y_fox, k_new, v_new, logf_new = fox_mix(p_fox, p["fox_b_f"], *fox_cache)
    y_hg, hg_S_new = hgrn2_mix(p_hg, p["hg_lb"], hg_S, p["hg_norm_g"], hg_chunk)
    mix = jnp.concatenate([y_rw, y_fox, y_hg], axis=-1) @ p["w_out"]
    x = x + (ga1 * mix).astype(x.dtype)
    h2 = rmsnorm(x, p["norm2_g"]) * (1.0 + sc2) + sh2
    gate, up = jnp.split(h2 @ p["w_ffn_in"], 2, axis=-1)
    x = x + (ga2 * ((jax.nn.silu(gate) * up) @ p["w_ffn_out"])).astype(x.dtype)
    return x, (k_new, v_new, logf_new, rw_S_new, p_rw[:, -1:], hg_S_new)


def setup_inputs(seed: int = 0) -> dict:
    key = jax.random.key(seed)
    ks = jax.random.split(key, 40)
    nrm = lambda i, shape, s: s * jax.random.normal(ks[i], shape, F32)
    return {
        "x_prompt": nrm(0, (BATCH, SEQ, D_MODEL), 1.0),
        "x_sample": nrm(1, (DEC_BATCH, DEC_SEQ, D_MODEL), 1.0),
        "c_prompt": nrm(2, (BATCH, D_MODEL), 1.0),
        "c_sample": nrm(3, (DEC_BATCH, D_MODEL), 1.0),
        "cache_fox_k": nrm(4, (DEPTH, DEC_BATCH, PAST_LEN, FOX_HEADS, HEAD_DIM), 1.0),
        "cache_fox_v": nrm(5, (DEPTH, DEC_BATCH, PAST_LEN, FOX_HEADS, HEAD_DIM), 1.0),
        "cache_fox_logf": jax.nn.log_sigmoid(2.0 + nrm(6, (DEPTH, DEC_BATCH, PAST_LEN, FOX_HEADS), 1.0)),
        "state_rwkv": nrm(7, (DEPTH, DEC_BATCH, RW_HEADS, HEAD_DIM, HEAD_DIM), 0.5),
        "state_rwkv_shift": nrm(8, (DEPTH, DEC_BATCH, 1, RW_COLS), 1.0),
        "state_hgrn": nrm(9, (DEPTH, DEC_BATCH, HG_HEADS, HEAD_DIM, HEAD_DIM), 0.5),
        "norm1_g": 1.0 + nrm(10, (DEPTH, D_MODEL), 0.1),
        "w_ada": nrm(11, (DEPTH, D_MODEL, 6 * D_MODEL), 0.5 * D_MODEL ** -0.5),
        "b_ada": nrm(12, (DEPTH, 6 * D_MODEL), 0.1),
        "w_in": nrm(13, (DEPTH, D_MODEL, IN_COLS), D_MODEL ** -0.5),
        "rw_mu": jax.random.uniform(ks[14], (DEPTH, RW_COLS), F32),
        "rw_w0": nrm(15, (DEPTH, RW_WIDTH), 0.5),
        "rw_w2": nrm(16, (DEPTH, RW_W_LORA, RW_WIDTH), RW_W_LORA ** -0.5),
        "rw_a0": nrm(17, (DEPTH, RW_WIDTH), 0.5),
        "rw_a2": nrm(18, (DEPTH, RW_A_LORA, RW_WIDTH), RW_A_LORA ** -0.5),
        "rw_g2": nrm(19, (DEPTH, RW_G_LORA, RW_WIDTH), RW_G_LORA ** -0.5),
        "rw_k_k": 0.85 + nrm(20, (DEPTH, RW_WIDTH), 0.1),
        "rw_k_a": 1.0 + nrm(21, (DEPTH, RW_WIDTH), 0.1),
        "rw_r_k": nrm(22, (DEPTH, RW_HEADS, HEAD_DIM), 0.1),
        "rw_ln_w": 1.0 + nrm(23, (DEPTH, RW_WIDTH), 0.1),
        "rw_ln_b": nrm(24, (DEPTH, RW_WIDTH), 0.01),
        "fox_b_f": 2.0 + nrm(25, (DEPTH, FOX_HEADS), 0.5),
        "hg_lb_logits": nrm(26, (DEPTH, HG_WIDTH), 0.5),
        "hg_norm_g": 1.0 + nrm(27, (DEPTH, HG_WIDTH), 0.1),
        "w_out": nrm(28, (DEPTH, D_MIX, D_MODEL), D_MIX ** -0.5),
        "norm2_g": 1.0 + nrm(29, (DEPTH, D_MODEL), 0.1),
        "w_ffn_in": nrm(30, (DEPTH, D_MODEL, 2 * D_FF), D_MODEL ** -0.5),
        "w_ffn_out": nrm(31, (DEPTH, D_FF, D_MODEL), D_FF ** -0.5),
        "final_norm_g": 1.0 + nrm(32, (D_MODEL,), 0.1),
    }


def reference(x_prompt, x_sample, c_prompt, c_sample, cache_fox_k, cache_fox_v, cache_fox_logf,
              state_rwkv, state_rwkv_shift, state_hgrn, norm1_g, w_ada, b_ada, w_in, rw_mu, rw_w0,
              rw_w2, rw_a0, rw_a2, rw_g2, rw_k_k, rw_k_a, rw_r_k, rw_ln_w, rw_ln_b, fox_b_f,
              hg_lb_logits, hg_norm_g, w_out, norm2_g, w_ffn_in, w_ffn_out, final_norm_g):
    dt = x_prompt.dtype
    lbs = jnp.cumsum(jax.nn.softmax(hg_lb_logits.astype(F32), axis=0), axis=0)
    lbs = lbs - lbs[0:1]
    bp = x_prompt.shape[0]
    rw_prev0 = jnp.zeros((bp, 1, RW_COLS), F32)
    rw_S0 = jnp.zeros((bp, RW_HEADS, HEAD_DIM, HEAD_DIM), F32)
    hg_S0 = jnp.zeros((bp, HG_HEADS, HEAD_DIM, HEAD_DIM), F32)
    xp, xs = x_prompt, x_sample
    outs_p, outs_s = [], []
    for l in range(DEPTH):
        p = {
            "norm1_g": norm1_g[l], "w_ada": w_ada[l], "b_ada": b_ada[l], "w_in": w_in[l],
            "rw": (rw_mu[l], rw_w0[l], rw_w2[l], rw_a0[l], rw_a2[l], rw_g2[l], rw_k_k[l], rw_k_a[l],
                   rw_r_k[l], rw_ln_w[l], rw_ln_b[l]),
            "fox_b_f": fox_b_f[l], "hg_lb": lbs[l], "hg_norm_g": hg_norm_g[l], "w_out": w_out[l],
            "norm2_g": norm2_g[l], "w_ffn_in": w_ffn_in[l], "w_ffn_out": w_ffn_out[l],
        }
        xp, st_p = trunk_layer(xp, c_prompt, p, rw_prev0, rw_S0, hg_S0, (None, None, None), CHUNK)
        xs, st_s = trunk_layer(xs, c_sample, p, state_rwkv_shift[l], state_rwkv[l], state_hgrn[l],
                               (cache_fox_k[l], cache_fox_v[l], cache_fox_logf[l]), xs.shape[1])
        outs_p.append(st_p)
        outs_s.append(st_s)
    stk = lambda outs, j: jnp.stack([o[j] for o in outs], axis=0).astype(dt)
    y_prompt = rmsnorm(xp, final_norm_g).astype(dt)
    y_sample = rmsnorm(xs, final_norm_g).astype(dt)
    return (y_prompt, y_sample,
            stk(outs_p, 0), stk(outs_p, 1), stk(outs_p, 2), stk(outs_p, 3), stk(outs_p, 4), stk(outs_p, 5),
            stk(outs_s, 0), stk(outs_s, 1), stk(outs_s, 2), stk(outs_s, 3), stk(outs_s, 4), stk(outs_s, 5))
```

```python
import contextlib
import os
import numpy as np
import concourse.bass as bass
import concourse.mybir as mybir
from concourse.bass_utils import run_bass_kernel_spmd

F32 = mybir.dt.float32
BF16 = mybir.dt.bfloat16
AF = mybir.ActivationFunctionType
ALU = mybir.AluOpType

D = 1024
NC8 = 8
HD = 64
RW_COLS, FOX_COLS, HG_COLS = 896, 1544, 1024
IN_COLS = 3464
DFF = 2816
EPS = 1e-6


class Reg:
    __slots__ = ("name", "writers", "readers")

    def __init__(self, name=""):
        self.name = name
        self.writers = {}
        self.readers = {}


class Eng:
    def __init__(self, name, kind):
        self.name = name
        self.kind = kind
        self.ops = []
        self.sem = None
        self.count = 0
        self.waited = {}


class FW:
    def __init__(self, nc, stack):
        self.nc = nc
        self.stack = stack
        self.engs = {}
        self.dma_sems = {}
        self.dma_counts = {}
        self.group_sems = {}
        self.nsem = 0
        for name in ("pe", "act", "dve", "pool", "sp"):
            e = Eng(name, name)
            self.engs[name] = e
            if name != "sp":
                e.sem = self.new_sem("s_" + name)

    def new_sem(self, name):
        self.nsem += 1
        return self.stack.enter_context(self.nc.semaphore("%s_%d" % (name, self.nsem)))

    def sbuf(self, name, shape, dt):
        return self.stack.enter_context(self.nc.sbuf_tensor(name, list(shape), dt))

    def psum(self, name, shape, dt=F32):
        return self.stack.enter_context(self.nc.psum_tensor(name, list(shape), dt))

    def _collect(self, reads, writes):
        deps = {}

        def add(d):
            for k, (sem, val) in d.items():
                cur = deps.get(k)
                if cur is None or cur[1] < val:
                    deps[k] = (sem, val)
        for r in reads:
            add(r.writers)
        for w in writes:
            add(w.writers)
            add(w.readers)
        return deps

    def _waits(self, eng, deps, raw_keys):
        waits = []
        for k, (sem, val) in deps.items():
            if eng.sem is not None and k == id(eng.sem) and k not in raw_keys:
                continue
            if eng.waited.get(k, 0) >= val:
                continue
            eng.waited[k] = val
            st = self.group_sems.get(k)
            if st is not None:
                waits.append((sem, _Lazy(self.dma_counts, st)))
            else:
                waits.append((sem, val))
        return waits

    def op(self, engname, fn, reads=(), writes=(), signal=True):
        eng = self.engs[engname]
        reads = [r for r in reads if r is not None]
        writes = [w for w in writes if w is not None]
        deps = self._collect(reads, writes)
        raw_keys = set()
        k = id(eng.sem)
        if engname != "pe":
            raw_keys.add(k)
        for r in reads:
            if k in r.writers:
                raw_keys.add(k)
        waits = self._waits(eng, deps, raw_keys)
        sem = eng.sem
        if signal:
            eng.count += 1
            tok = (sem, eng.count)
        else:
            tok = (sem, eng.count + 1)
        for r in reads:
            r.readers[id(sem)] = tok
        for w in writes:
            w.writers = {id(sem): tok}
            w.readers = {}

        def run(e, fn=fn, waits=waits, signal=signal, sem=sem):
            for (s, v) in waits:
                e.wait_ge(s, int(v))
            ins = fn(e)
            if signal:
                ins.then_inc(sem, 1)
        eng.ops.append(run)

    def dma(self, qname, out, in_, reads=(), writes=(), stream="d", group=False, **kw):
        eng = self.engs[qname]
        reads = [r for r in reads if r is not None]
        writes = [w for w in writes if w is not None]
        deps = self._collect(reads, writes)
        stream = stream + "_" + qname
        if stream not in self.dma_sems:
            self.dma_sems[stream] = self.new_sem("dq_" + stream)
            self.dma_counts[stream] = 0
            if group:
                self.group_sems[id(self.dma_sems[stream])] = stream
        if group:
            deps.pop(id(self.dma_sems[stream]), None)
        waits = self._waits(eng, deps, set(deps.keys()))
        sem = self.dma_sems[stream]
        self.dma_counts[stream] += 16
        tok = (sem, self.dma_counts[stream])
        for r in reads:
            r.readers[id(sem)] = tok
        for w in writes:
            w.writers = {id(sem): tok}
            w.readers = {}

        def run(e, waits=waits, sem=sem, out=out, in_=in_, kw=kw):
            for (s, v) in waits:
                e.wait_ge(s, int(v))
            e.dma_start(out=out, in_=in_, **kw).then_inc(sem, 16)
        eng.ops.append(run)

    def rotate(self):
        for e in self.engs.values():
            if e.sem is not None:
                e.sem = self.new_sem("s_" + e.name)
                e.count = 0

    def barrier(self):
        toks = []
        for e in self.engs.values():
            if e.sem is not None and e.count > 0:
                toks.append((e.sem, e.count))
        for s in self.dma_sems:
            if self.dma_counts[s] > 0:
                toks.append((self.dma_sems[s], self.dma_counts[s]))
        for e in self.engs.values():
            waits = []
            for (sem, val) in toks:
                if sem is e.sem:
                    continue
                if e.waited.get(id(sem), 0) >= val:
                    continue
                e.waited[id(sem)] = val
                waits.append((sem, val))

            def run(h, waits=waits):
                for (s, v) in waits:
                    h.wait_ge(s, v)
            e.ops.append(run)

    def finish(self):
        self.flush()

    def flush(self):
        self.barrier()
        nc = self.nc
        engs = self.engs
        oplists = {k: e.ops for k, e in engs.items()}
        for e in engs.values():
            e.ops = []

        class _E:
            def __init__(self, ops):
                self.ops = ops
        self_engs = {k: _E(v) for k, v in oplists.items()}
        with nc.Block() as block:
            def mk(eng):
                def body(e):
                    for f in eng.ops:
                        f(e)
                return body
            block.tensor(mk(self_engs["pe"]))
            block.scalar(mk(self_engs["act"]))
            block.vector(mk(self_engs["dve"]))
            block.gpsimd(mk(self_engs["pool"]))
            block.sync(mk(self_engs["sp"]))


class _Lazy:
    def __init__(self, counts, stream):
        self.counts = counts
        self.stream = stream

    def __int__(self):
        return self.counts[self.stream]


class Tt:
    def __init__(self, t, nreg=1, name=""):
        self.t = t
        self.rs = [Reg("%s%d" % (name, i)) for i in range(nreg)]
        self.r = self.rs[0]


def build_program(SEQ, DEPTH, TS=32, PAST=2048, stage=9):
    L = DEPTH
    NTOK = 2 * SEQ + TS
    nc = bass.Bass("TRN2", target_bir_lowering=False)
    din = lambda n, s: nc.dram_tensor(n, list(s), F32, kind="ExternalInput").ap()
    dout = lambda n, s: nc.dram_tensor(n, list(s), F32, kind="ExternalOutput").ap()
    I = dict(
        xp=din("xp", (2, SEQ, D)), xs=din("xs", (TS, D)), cc=din("cc", (3, D)),
        ck=din("ck", (L, PAST, 512)), cv=din("cv", (L, PAST, 512)), cl=din("cl", (L, PAST, 8)),
        srw=din("srw", (L, 4, 64, 64)), ssh=din("ssh", (L, RW_COLS)), shg=din("shg", (L, 4, 64, 64)),
        norm1_g=din("norm1_g", (L, D)), w_ada=din("w_ada", (L, D, 6 * D)), b_ada=din("b_ada", (L, 6 * D)),
        w_in=din("w_in", (L, D, IN_COLS)), rw_mu=din("rw_mu", (L, RW_COLS)), rw_w0=din("rw_w0", (L, 256)),
        rw_w2=din("rw_w2", (L, 32, 256)), rw_a0=din("rw_a0", (L, 256)), rw_a2=din("rw_a2", (L, 32, 256)),
        rw_g2=din("rw_g2", (L, 64, 256)), rw_k_k=din("rw_k_k", (L, 256)), rw_k_a=din("rw_k_a", (L, 256)),
        rw_r_k=din("rw_r_k", (L, 256)), rw_ln_w=din("rw_ln_w", (L, 256)), rw_ln_b=din("rw_ln_b", (L, 256)),
        fox_b_f=din("fox_b_f", (L, 8)), hg_lb_logits=din("hg_lb_logits", (L, 256)),
        hg_norm_g=din("hg_norm_g", (L, 256)), w_out=din("w_out", (L, D, D)), norm2_g=din("norm2_g", (L, D)),
        w_ffn_in=din("w_ffn_in", (L, D, 2 * DFF)), w_ffn_out=din("w_ffn_out", (L, DFF, D)),
        final_norm_g=din("final_norm_g", (D,)),
    )
    O = dict(
        yp=dout("yp", (2, SEQ, D)), ys=dout("ys", (TS, D)),
        fkp=dout("fkp", (L, 2, SEQ, 512)), fvp=dout("fvp", (L, 2, SEQ, 512)), flp=dout("flp", (L, 2, SEQ, 8)),
        rwp=dout("rwp", (L, 2, 4, 64, 64)), rshp=dout("rshp", (L, 2, RW_COLS)), hgp=dout("hgp", (L, 2, 4, 64, 64)),
        fks=dout("fks", (L, TS, 512)), fvs=dout("fvs", (L, TS, 512)), fls=dout("fls", (L, TS, 8)),
        rws=dout("rws", (L, 4, 64, 64)), rshs=dout("rshs", (L, RW_COLS)), hgs=dout("hgs", (L, 4, 64, 64)),
    )
    xres = nc.dram_tensor("xres", [D, NTOK], F32).ap()
    xres_r = Reg("xres")
    seqs = [(0, SEQ), (SEQ, SEQ), (2 * SEQ, TS)]

    def tiles_of(s, W):
        off, T = seqs[s]
        w = min(W, T)
        return [(off + j * w, w, j) for j in range(T // w)]

    xres_regs = {}

    def xreg(t0):
        return xres_regs.setdefault(t0, Reg("xres%d" % t0))

    with contextlib.ExitStack() as top:
        fw = FW(nc, top)
        ident = Tt(fw.sbuf("ident", [128, 128], F32), name="ident")
        identb = Tt(fw.sbuf("identb", [128, 128], BF16), name="identb")
        onesb = Tt(fw.sbuf("onesb", [128, 128], BF16), name="onesb")
        fw.op("pool", lambda e: e.memset(ident.t[:], 0.0), writes=[ident.r])
        fw.op("pool", lambda e: e.affine_select(out=ident.t[:], in_=ident.t[:], pattern=[[-1, 128]],
                                                compare_op=ALU.not_equal, fill=1.0, base=0,
                                                channel_multiplier=1), reads=[ident.r], writes=[ident.r])
        fw.op("pool", lambda e: e.tensor_copy(out=identb.t[:], in_=ident.t[:]), reads=[ident.r], writes=[identb.r])
        fw.op("pool", lambda e: e.memset(onesb.t[:], 1.0), writes=[onesb.r])
        epsb = Tt(fw.sbuf("epsb", [128, 1], F32), name="epsb")
        fw.op("pool", lambda e: e.memset(epsb.t[:], EPS), writes=[epsb.r])


        trif = Tt(fw.sbuf("trif", [128, 128], F32), name="trif")
        fw.op("pool", lambda e: e.memset(trif.t[:], 1.0), writes=[trif.r])
        fw.op("pool", lambda e: e.affine_select(out=trif.t[:], in_=trif.t[:], pattern=[[1, 128]],
                                                compare_op=ALU.is_ge, fill=0.0, base=0, channel_multiplier=-1),
              reads=[trif.r], writes=[trif.r])
        self127 = Tt(fw.sbuf("self127", [128, 128], F32), name="self127")
        fw.op("pool", lambda e: e.memset(self127.t[:], 0.0), writes=[self127.r])
        fw.op("pool", lambda e: e.affine_select(out=self127.t[:], in_=self127.t[:], pattern=[[0, 128]],
                                                compare_op=ALU.not_equal, fill=1.0, base=-127, channel_multiplier=1),
              reads=[self127.r], writes=[self127.r])
        mnegf = Tt(fw.sbuf("mnegf", [128, 128], F32), name="mnegf")
        maskneg = Tt(fw.sbuf("maskneg", [128, 128], BF16), name="maskneg")
        fw.op("pool", lambda e: e.memset(mnegf.t[:], 0.0), writes=[mnegf.r])
        fw.op("pool", lambda e: e.affine_select(out=mnegf.t[:], in_=mnegf.t[:], pattern=[[1, 128]],
                                                compare_op=ALU.is_ge, fill=-30000.0, base=0, channel_multiplier=-1),
              reads=[mnegf.r], writes=[mnegf.r])
        fw.op("pool", lambda e: e.tensor_copy(out=maskneg.t[:], in_=mnegf.t[:]), reads=[mnegf.r], writes=[maskneg.r])
        e8 = Tt(fw.sbuf("e8", [8, 8, 128], F32), name="e8")
        fw.op("pool", lambda e: e.memset(e8.t[:], 0.0), writes=[e8.r])
        fw.op("pool", lambda e: e.affine_select(out=e8.t[:], in_=e8.t[:], pattern=[[-1, 8], [0, 128]],
                                                compare_op=ALU.not_equal, fill=1.0, base=0, channel_multiplier=1),
              reads=[e8.r], writes=[e8.r])
        selh = Tt(fw.sbuf("selh", [72, 8, 128], BF16), name="selh")
        fw.op("pool", lambda e: e.memset(selh.t[:], 0.0), writes=[selh.r])
        for b0 in (0, 32, 64):
            fw.op("dve", lambda e, b0=b0: e.tensor_copy(out=selh.t[b0:b0 + 8, :, :], in_=e8.t[:]),
                  reads=[e8.r], writes=[selh.r])
        onesf = Tt(fw.sbuf("onesf", [128, 64], F32), name="onesf")
        fw.op("pool", lambda e: e.memset(onesf.t[:], 1.0), writes=[onesf.r])
        bfb = Tt(fw.sbuf("bfb", [128, L, 8], F32), name="bfb")
        for l_ in range(L):
            fw.dma("sp", bfb.t[:, l_, :], I["fox_b_f"][l_:l_ + 1, :].to_broadcast([128, 8]), writes=[bfb.r],
                   stream="par", group=True)
        yfox = nc.dram_tensor("yfox", [512, NTOK], BF16).ap()
        yfox_regs = {}

        def yfreg(t0):
            return yfox_regs.setdefault(t0, Reg("yfox%d" % t0))

        banks = [Tt(fw.psum("bank%d" % i, [128, 512]), name="bank%d" % i) for i in range(8)]
        bank_ctr = [0]

        default_pool = [(0, 1, 2, 3, 4, 5, 6, 7)]

        def next_bank(pool=None):
            if pool is None:
                pool = default_pool[0]
            b = banks[pool[bank_ctr[0] % len(pool)]]
            bank_ctr[0] += 1
            return b

        def load_fm(name, src_ap, ncol, q="sp"):
            t = Tt(fw.sbuf(name, [128, L, ncol], F32), name=name)
            fw.dma(q, t.t[:], src_ap.rearrange("l (c p) -> p l c", p=128), writes=[t.r], stream="par", group=True,
                   allow_slow_non_contiguous=True)
            return t
        n1g = load_fm("n1g", I["norm1_g"], 8)
        n2g = load_fm("n2g", I["norm2_g"], 8)
        badaT = load_fm("badaT", I["b_ada"], 48)
        fng = Tt(fw.sbuf("fng", [128, 8], F32), name="fng")
        fw.dma("sp", fng.t[:], I["final_norm_g"].rearrange("(c p) -> p c", p=128), writes=[fng.r], stream="par", group=True,
               allow_slow_non_contiguous=True)

        rmask32 = Tt(fw.sbuf("rmask32", [128, 512], F32), name="rmask32")
        fw.op("pool", lambda e: e.memset(rmask32.t[:], 1.0), writes=[rmask32.r])
        fw.op("pool", lambda e: e.affine_select(out=rmask32.t[:, :].rearrange("p (c t) -> p c t", t=32),
                                                in_=rmask32.t[:, :].rearrange("p (c t) -> p c t", t=32),
                                                pattern=[[0, 16], [1, 32]], compare_op=ALU.not_equal, fill=0.0,
                                                base=0, channel_multiplier=0), reads=[rmask32.r], writes=[rmask32.r])
        maskbd = Tt(fw.sbuf("maskbd", [128, 128], F32), name="maskbd")
        fw.op("pool", lambda e: e.tensor_copy(out=maskbd.t[:], in_=trif.t[:]), reads=[trif.r], writes=[maskbd.r])
        for cb_ in range(1, 4):
            fw.op("pool", lambda e, cb_=cb_: e.affine_select(
                out=maskbd.t[:, cb_ * 32:(cb_ + 1) * 32], in_=maskbd.t[:, cb_ * 32:(cb_ + 1) * 32], pattern=[[0, 32]],
                compare_op=ALU.is_ge, fill=0.0, base=-cb_ * 32, channel_multiplier=1),
                reads=[maskbd.r], writes=[maskbd.r])
        onesbdf = Tt(fw.sbuf("onesbdf", [128, 128], F32), name="onesbdf")
        onesbd = Tt(fw.sbuf("onesbd", [128, 128], BF16), name="onesbd")
        fw.op("pool", lambda e: e.memset(onesbdf.t[:], 1.0), writes=[onesbdf.r])
        fw.op("pool", lambda e: e.affine_select(out=onesbdf.t[:, 0:64], in_=onesbdf.t[:, 0:64], pattern=[[0, 64]],
                                                compare_op=ALU.is_ge, fill=0.0, base=63, channel_multiplier=-1),
              reads=[onesbdf.r], writes=[onesbdf.r])
        fw.op("pool", lambda e: e.affine_select(out=onesbdf.t[:, 64:128], in_=onesbdf.t[:, 64:128], pattern=[[0, 64]],
                                                compare_op=ALU.is_ge, fill=0.0, base=-64, channel_multiplier=1),
              reads=[onesbdf.r], writes=[onesbdf.r])
        fw.op("pool", lambda e: e.tensor_copy(out=onesbd.t[:], in_=onesbdf.t[:]), reads=[onesbdf.r], writes=[onesbd.r])
        hgng = load_fm("hgng", I["hg_norm_g"], 2)
        lbl = load_fm("lbl", I["hg_lb_logits"], 2)
        lbT = Tt(fw.sbuf("lbT", [128, L, 2], F32), name="lbT")
        omlT = Tt(fw.sbuf("omlT", [128, L, 2], F32), name="omlT")
        nomlT = Tt(fw.sbuf("nomlT", [128, L, 2], F32), name="nomlT")
        lbm = Tt(fw.sbuf("lbm", [128, 2], F32), name="lbm")
        lbe = Tt(fw.sbuf("lbe", [128, L, 2], F32), name="lbe")
        lbs_ = Tt(fw.sbuf("lbs_", [128, 2], F32), name="lbs_")
        fw.op("dve", lambda e: e.tensor_copy(out=lbm.t[:], in_=lbl.t[:, 0, :]), reads=[lbl.r], writes=[lbm.r])
        for l_ in range(1, L):
            fw.op("dve", lambda e, l_=l_: e.tensor_max(out=lbm.t[:], in0=lbm.t[:], in1=lbl.t[:, l_, :]),
                  reads=[lbm.r, lbl.r], writes=[lbm.r])
        for l_ in range(L):
            fw.op("dve", lambda e, l_=l_: e.tensor_sub(out=lbe.t[:, l_, :], in0=lbl.t[:, l_, :], in1=lbm.t[:]),
                  reads=[lbm.r, lbl.r], writes=[lbe.r])
        fw.op("act", lambda e: e.activation(out=lbe.t[:], in_=lbe.t[:], func=AF.Exp), reads=[lbe.r], writes=[lbe.r])
        fw.op("dve", lambda e: e.tensor_copy(out=lbs_.t[:], in_=lbe.t[:, 0, :]), reads=[lbe.r], writes=[lbs_.r])
        for l_ in range(1, L):
            fw.op("dve", lambda e, l_=l_: e.tensor_add(out=lbs_.t[:], in0=lbs_.t[:], in1=lbe.t[:, l_, :]),
                  reads=[lbs_.r, lbe.r], writes=[lbs_.r])
        fw.op("dve", lambda e: e.reciprocal(out=lbs_.t[:], in_=lbs_.t[:]), reads=[lbs_.r], writes=[lbs_.r])
        for l_ in range(L):
            fw.op("dve", lambda e, l_=l_: e.tensor_mul(out=lbe.t[:, l_, :], in0=lbe.t[:, l_, :], in1=lbs_.t[:]),
                  reads=[lbs_.r, lbe.r], writes=[lbe.r])
        fw.op("dve", lambda e: e.memset(lbT.t[:, 0, :], 0.0), writes=[lbT.r])
        for l_ in range(1, L):
            fw.op("dve", lambda e, l_=l_: e.tensor_add(out=lbT.t[:, l_, :], in0=lbT.t[:, l_ - 1, :], in1=lbe.t[:, l_, :]),
                  reads=[lbT.r, lbe.r], writes=[lbT.r])
        fw.op("dve", lambda e: e.tensor_scalar(out=omlT.t[:], in0=lbT.t[:], scalar1=-1.0, scalar2=1.0,
                                               op0=ALU.mult, op1=ALU.add), reads=[lbT.r], writes=[omlT.r])
        fw.op("dve", lambda e: e.tensor_scalar_mul(out=nomlT.t[:], in0=omlT.t[:], scalar1=-1.0),
              reads=[omlT.r], writes=[nomlT.r])

        rmask64 = Tt(fw.sbuf("rmask64", [128, 512], F32), name="rmask64")
        fw.op("pool", lambda e: e.memset(rmask64.t[:], 1.0), writes=[rmask64.r])
        fw.op("pool", lambda e: e.affine_select(out=rmask64.t[:, :].rearrange("p (c t) -> p c t", t=64),
                                                in_=rmask64.t[:, :].rearrange("p (c t) -> p c t", t=64),
                                                pattern=[[0, 8], [1, 64]], compare_op=ALU.not_equal, fill=0.0,
                                                base=0, channel_multiplier=0), reads=[rmask64.r], writes=[rmask64.r])
        mask12 = Tt(fw.sbuf("mask12", [64, 2, 64], F32), name="mask12")
        mask34 = Tt(fw.sbuf("mask34", [64, 2, 64], F32), name="mask34")
        mask5 = Tt(fw.sbuf("mask5", [64, 64], F32), name="mask5")
        fw.op("pool", lambda e: e.memset(mask12.t[:], 1.0), writes=[mask12.r])
        fw.op("pool", lambda e: e.affine_select(out=mask12.t[:, 0, :], in_=mask12.t[:, 0, :], pattern=[[1, 64]],
                                                compare_op=ALU.is_ge, fill=0.0, base=-1, channel_multiplier=-1),
              reads=[mask12.r], writes=[mask12.r])
        fw.op("pool", lambda e: e.affine_select(out=mask12.t[:, 1, :], in_=mask12.t[:, 1, :], pattern=[[1, 64]],
                                                compare_op=ALU.is_ge, fill=0.0, base=0, channel_multiplier=-1),
              reads=[mask12.r], writes=[mask12.r])
        fw.op("dve", lambda e: e.tensor_scalar_mul(out=mask34.t[:, 0, :], in0=mask12.t[:, 0, :], scalar1=-1.0),
              reads=[mask12.r], writes=[mask34.r])
        fw.op("pool", lambda e: e.tensor_copy(out=mask34.t[:, 1, :], in_=mask12.t[:, 1, :]),
              reads=[mask12.r], writes=[mask34.r])
        fw.op("pool", lambda e: e.memset(mask5.t[:], -1.0), writes=[mask5.r])
        fw.op("pool", lambda e: e.affine_select(out=mask5.t[:], in_=mask5.t[:], pattern=[[-1, 64]],
                                                compare_op=ALU.is_ge, fill=0.0, base=-1, channel_multiplier=1),
              reads=[mask5.r], writes=[mask5.r])
        eps2 = Tt(fw.sbuf("eps2", [128, 1], F32), name="eps2")
        fw.op("pool", lambda e: e.memset(eps2.t[:], 64e-5), writes=[eps2.r])
        rwp = {}
        for nm in ("rw_w0", "rw_a0", "rw_k_k", "rw_k_a", "rw_r_k", "rw_ln_w", "rw_ln_b"):
            rwp[nm] = load_fm("p_" + nm, I[nm], 2)
        nw0 = Tt(fw.sbuf("nw0", [128, L, 2], F32), name="nw0")
        na0 = Tt(fw.sbuf("na0", [128, L, 2], F32), name="na0")
        omka = Tt(fw.sbuf("omka", [128, L, 2], F32), name="omka")
        fw.op("dve", lambda e: e.tensor_scalar_mul(out=nw0.t[:], in0=rwp["rw_w0"].t[:], scalar1=-1.0),
              reads=[rwp["rw_w0"].r], writes=[nw0.r])
        fw.op("dve", lambda e: e.tensor_scalar_mul(out=na0.t[:], in0=rwp["rw_a0"].t[:], scalar1=-1.0),
              reads=[rwp["rw_a0"].r], writes=[na0.r])
        fw.op("dve", lambda e: e.tensor_scalar(out=omka.t[:], in0=rwp["rw_k_a"].t[:], scalar1=-1.0, scalar2=1.0,
                                               op0=ALU.mult, op1=ALU.add), reads=[rwp["rw_k_a"].r], writes=[omka.r])
        mul = Tt(fw.sbuf("mul", [128, L, 9], F32), name="mul")
        fw.op("pool", lambda e: e.memset(mul.t[:], 0.0), writes=[mul.r])
        m7 = load_fm("m7", I["rw_mu"], 7)
        fw.op("dve", lambda e: e.tensor_copy(out=mul.t[:, :, 0:6], in_=m7.t[:, :, 0:6]), reads=[m7.r], writes=[mul.r])
        fw.op("dve", lambda e: e.tensor_copy(out=mul.t[0:32, :, 6], in_=m7.t[0:32, :, 6]), reads=[m7.r], writes=[mul.r])
        fw.op("dve", lambda e: e.tensor_copy(out=mul.t[0:32, :, 7], in_=m7.t[32:64, :, 6]), reads=[m7.r], writes=[mul.r])
        fw.op("dve", lambda e: e.tensor_copy(out=mul.t[0:64, :, 8], in_=m7.t[64:128, :, 6]), reads=[m7.r], writes=[mul.r])
        w2b = Tt(fw.sbuf("w2b", [32, L, 256], BF16), name="w2b")
        a2b = Tt(fw.sbuf("a2b", [32, L, 256], BF16), name="a2b")
        g2b = Tt(fw.sbuf("g2b", [64, L, 256], BF16), name="g2b")
        fw.dma("pool", w2b.t[:], I["rw_w2"].rearrange("l k n -> k l n"), writes=[w2b.r], stream="parb", group=True)
        fw.dma("pool", a2b.t[:], I["rw_a2"].rearrange("l k n -> k l n"), writes=[a2b.r], stream="parb", group=True)
        fw.dma("pool", g2b.t[:], I["rw_g2"].rearrange("l k n -> k l n"), writes=[g2b.r], stream="parb", group=True)

        cT = Tt(fw.sbuf("cT", [128, 3, 8], F32), name="cT")
        fw.dma("sp", cT.t[:], I["cc"].rearrange("b (c p) -> p b c", p=128), writes=[cT.r], stream="par", group=True,
               allow_slow_non_contiguous=True)
        siluT = Tt(fw.sbuf("siluT", [128, 8, 3], F32), name="siluT")
        sl_e = Tt(fw.sbuf("sl_e", [128, 3, 8], F32), name="sl_e")
        fw.op("act", lambda e: e.activation(out=sl_e.t[:], in_=cT.t[:], func=AF.Exp, scale=-1.0),
              reads=[cT.r], writes=[sl_e.r])
        fw.op("dve", lambda e: e.tensor_scalar_add(out=sl_e.t[:], in0=sl_e.t[:], scalar1=1.0),
              reads=[sl_e.r], writes=[sl_e.r])
        fw.op("dve", lambda e: e.reciprocal(out=sl_e.t[:], in_=sl_e.t[:]), reads=[sl_e.r], writes=[sl_e.r])
        fw.op("dve", lambda e: e.tensor_tensor(out=siluT.t[:].rearrange("p c b -> p b c"), in0=sl_e.t[:],
                                               in1=cT.t[:], op=ALU.mult),
              reads=[sl_e.r, cT.r], writes=[siluT.r])
        mod = Tt(fw.sbuf("mod", [128, L, 6, 3, 8], F32), name="mod")
        with contextlib.ExitStack() as ph:
            sub = FWScope(fw, ph)
            wa = [Tt(sub.sbuf("wa%d" % i, [128, 8, 512], F32), name="wa%d" % i) for i in range(2)]
            k = 0
            for l in range(L):
                for jg in range(12):
                    w = wa[k % 2]
                    k += 1
                    fw.dma("sp" if k % 2 else "pool", w.t[:],
                           I["w_ada"][l].rearrange("(kc p) n -> p kc n", p=128)[:, :, jg * 512:(jg + 1) * 512],
                           writes=[w.r], stream="wada%d" % (k % 2))
                    for jj in range(4):
                        j = jg * 4 + jj
                        m, c = j // 8, j % 8
                        pb = next_bank()
                        for kc in range(8):
                            fw.op("pe", lambda e, w=w, pb=pb, kc=kc, jj=jj: e.matmul(
                                pb.t[:, 0:3], lhsT=w.t[:, kc, jj * 128:(jj + 1) * 128], rhs=siluT.t[:, kc, :],
                                start=(kc == 0), stop=(kc == 7)),
                                reads=[w.r, siluT.r], writes=[pb.r], signal=(kc == 7))
                        fw.op("dve", lambda e, pb=pb, l=l, m=m, c=c, j=j: e.tensor_scalar(
                            out=mod.t[:, l, m, :, c], in0=pb.t[:, 0:3], scalar1=badaT.t[:, l, j:j + 1], scalar2=None,
                            op0=ALU.add), reads=[pb.r, badaT.r], writes=[mod.r])
            fw.flush()
        G1 = Tt(fw.sbuf("G1", [128, L, 3, 8], F32), name="G1")
        G2 = Tt(fw.sbuf("G2", [128, L, 3, 8], F32), name="G2")
        for l in range(L):
            for (G, ng, mi) in ((G1, n1g, 1), (G2, n2g, 4)):
                for b in range(3):
                    fw.op("dve", lambda e, G=G, ng=ng, mi=mi, l=l, b=b: e.scalar_tensor_tensor(
                        out=G.t[:, l, b, :], in0=mod.t[:, l, mi, b, :], scalar=1.0, in1=ng.t[:, l, :],
                        op0=ALU.add, op1=ALU.mult), reads=[mod.r, ng.r], writes=[G.r])

        def load_xT(xT, t0, w, q="sp"):
            fw.dma(q, xT.t[:, :, 0:w], xres.rearrange("(c p) t -> p c t", p=128)[:, :, t0:t0 + w],
                   reads=[xreg(t0)], writes=xT.rs, stream="xld" + xT.r.name[-2:])

        def store_xT(xT, t0, w, q="sp"):
            fw.dma(q, xres.rearrange("(c p) t -> p c t", p=128)[:, :, t0:t0 + w], xT.t[:, :, 0:w],
                   reads=xT.rs, writes=[xreg(t0)], stream="xst" + xT.r.name[-2:])

        def norm_fm(xT, w, sq, tmp, lnv, rstd, out, gap, shap, out_regs):
            fw.op("act", lambda e: e.activation(out=sq.t[:, :, 0:w], in_=xT.t[:, :, 0:w], func=AF.Square),
                  reads=xT.rs, writes=[sq.r])
            pb = next_bank()
            for c in range(8):
                fw.op("pe", lambda e, c=c: e.matmul(pb.t[:, 0:w], lhsT=onesb.t[:], rhs=sq.t[:, c, 0:w],
                                                    start=(c == 0), stop=(c == 7)),
                      reads=[onesb.r, sq.r], writes=[pb.r], signal=(c == 7))
            fw.op("act", lambda e: e.activation(out=lnv.t[:, 0:w], in_=pb.t[:, 0:w], func=AF.Ln, scale=1.0 / D,
                                                bias=epsb.t[:]), reads=[pb.r, epsb.r], writes=[lnv.r])
            fw.op("act", lambda e: e.activation(out=rstd.t[:, 0:w], in_=lnv.t[:, 0:w], func=AF.Exp, scale=-0.5),
                  reads=[lnv.r], writes=[rstd.r])
            for c in range(8):
                tm = tmp[c % len(tmp)]
                fw.op("dve", lambda e, c=c, tm=tm: e.tensor_tensor(out=tm.t[:, 0:w], in0=xT.t[:, c, 0:w],
                                                                   in1=rstd.t[:, 0:w], op=ALU.mult),
                      reads=[xT.rs[c], rstd.r], writes=[tm.r])
                if shap is not None:
                    fw.op("act", lambda e, c=c, tm=tm: e.activation(out=out.t[:, c, 0:w], in_=tm.t[:, 0:w],
                                                                    func=AF.Identity, scale=gap(c), bias=shap(c)),
                          reads=[tm.r, G1.r, G2.r, mod.r], writes=[out_regs[c]])
                else:
                    fw.op("act", lambda e, c=c, tm=tm: e.activation(out=out.t[:, c, 0:w], in_=tm.t[:, 0:w],
                                                                    func=AF.Identity, scale=gap(c)),
                          reads=[tm.r, fng.r], writes=[out_regs[c]])

        with contextlib.ExitStack() as ph:
            sub = FWScope(fw, ph)
            xtok = [Tt(sub.sbuf("xtok%d" % i, [128, 4, D], F32), name="xtok%d" % i) for i in range(2)]
            xTs = [Tt(sub.sbuf("xTp%d" % i, [128, 8, 512], F32), nreg=8, name="xTp%d" % i) for i in range(2)]
            it = 0
            for s in range(3):
                src = I["xp"][s] if s < 2 else I["xs"]
                off, T = seqs[s]
                for (t0, w, j) in tiles_of(s, 512):
                    xt = xtok[it % 2]
                    xT = xTs[it % 2]
                    it += 1
                    nb = (w + 127) // 128
                    pw = min(w, 128)
                    lt0 = t0 - off
                    if w >= 128:
                        fw.dma("sp" if it % 2 else "pool", xt.t[:, 0:nb, :],
                               src[lt0:lt0 + w, :].rearrange("(b p) d -> p b d", p=128), writes=[xt.r], stream="xtok")
                    else:
                        fw.dma("sp", xt.t[0:w, 0, :], src[lt0:lt0 + w, :], writes=[xt.r], stream="xtok")
                    for c in range(8):
                        pb = next_bank()
                        for tb in range(nb):
                            fw.op("pe", lambda e, c=c, tb=tb, pb=pb, xt=xt, pw=pw: e.transpose(
                                pb.t[:, tb * 128:tb * 128 + pw], xt.t[0:pw, tb, c * 128:(c + 1) * 128],
                                ident.t[0:pw, 0:pw]),
                                reads=[xt.r, ident.r], writes=[pb.r], signal=(tb == nb - 1))
                        eng = "act" if c % 2 else "dve"
                        if eng == "act":
                            fw.op("act", lambda e, c=c, pb=pb, xT=xT, w=w: e.activation(
                                out=xT.t[:, c, 0:w], in_=pb.t[:, 0:w], func=AF.Copy), reads=[pb.r], writes=[xT.rs[c]])
                        else:
                            fw.op("dve", lambda e, c=c, pb=pb, xT=xT, w=w: e.tensor_copy(
                                out=xT.t[:, c, 0:w], in_=pb.t[:, 0:w]), reads=[pb.r], writes=[xT.rs[c]])
                    store_xT(xT, t0, w, q="sp" if it % 2 else "pool")
            fw.flush()

        for l in range(L):
            fw.rotate()
            if stage >= 2:
              with contextlib.ExitStack() as ph:
                sub = FWScope(fw, ph)
                default_pool[0] = (0, 1, 2, 3, 4, 5)
                WA = 512
                FC0 = RW_COLS
                TKMAX = max(SEQ, PAST + TS)
                NBMAX = (TKMAX + 127) // 128
                wf = Tt(sub.sbuf("wf", [128, 8, FOX_COLS], BF16), name="wf")
                for kc in range(8):
                    fw.dma("pool", wf.t[:, kc, :], I["w_in"][l][kc * 128:(kc + 1) * 128, FC0:FC0 + FOX_COLS],
                           writes=[wf.r], stream="wld", group=True)
                KT = Tt(sub.sbuf("KT", [128, 4, TKMAX], BF16), nreg=NBMAX, name="KT")
                Vx = Tt(sub.sbuf("Vx", [128, NBMAX, 8, 65], BF16), nreg=NBMAX, name="Vx")
                fw.op("pool", lambda e: e.memset(Vx.t[:], 1.0), writes=Vx.rs)
                ctok = Tt(sub.sbuf("ctok", [128, NBMAX, 8], F32), nreg=NBMAX, name="ctok")
                negc = Tt(sub.sbuf("negc", [128, NBMAX, 8], F32), nreg=NBMAX, name="negc")
                xTa = [Tt(sub.sbuf("xTa%d" % i, [128, 8, WA], F32), nreg=8, name="xTa%d" % i) for i in range(2)]
                sq = Tt(sub.sbuf("sqa", [128, 8, WA], BF16), name="sqa")
                hT = Tt(sub.sbuf("hTa", [128, 8, WA], BF16), nreg=8, name="hTa")
                tmp = [Tt(sub.sbuf("tmpa%d" % i, [128, WA], F32), name="tmpa%d" % i) for i in range(2)]
                lnv = Tt(sub.sbuf("lnva", [128, WA], F32), name="lnva")
                rstd = Tt(sub.sbuf("rstda", [128, WA], F32), name="rstda")
                QT = Tt(sub.sbuf("QT", [128, 4, WA], BF16), nreg=4, name="QT")
                ktok = [Tt(sub.sbuf("ktok%d" % i, [128, 512], F32), name="ktok%d" % i) for i in range(2)]
                vtok = [Tt(sub.sbuf("vtok%d" % i, [128, 512], F32), name="vtok%d" % i) for i in range(2)]
                ftok = [Tt(sub.sbuf("ftok%d" % i, [128, 8], F32), name="ftok%d" % i) for i in range(2)]
                ltok = [Tt(sub.sbuf("ltok%d" % i, [128, 8], F32), name="ltok%d" % i) for i in range(2)]
                cfm = Tt(sub.sbuf("cfm", [8, WA], F32), name="cfm")
                cr1 = Tt(sub.sbuf("cr1", [8, WA], F32), name="cr1")
                cr2 = Tt(sub.sbuf("cr2", [8, WA], F32), name="cr2")
                midt = Tt(sub.sbuf("midt", [8, WA], BF16), name="midt")
                cq96 = Tt(sub.sbuf("cq96", [72, WA], BF16), name="cq96")
                fw.op("pool", lambda e: e.memset(cq96.t[:], 0.0), writes=[cq96.r])
                pts = [Tt(sub.sbuf("pt%d" % i, [128, WA], BF16), name="pt%d" % i) for i in range(4)]
                rsf = Tt(sub.sbuf("rsf", [128, WA], F32), name="rsf")
                rcp = Tt(sub.sbuf("rcp", [64, WA], F32), name="rcp")
                yfT = Tt(sub.sbuf("yfT", [128, 4, WA], BF16), nreg=4, name="yfT")
                it = 0
                ik = 0
                ipt = 0
                for s in range(3):
                    off, T = seqs[s]
                    kbase = 0
                    if s == 2:
                        kbase = PAST
                        for cb in range(PAST // 128):
                            kt_ = ktok[ik % 2]
                            vt_ = vtok[ik % 2]
                            ft_ = ltok[ik % 2]
                            ik += 1
                            fw.dma("sp", kt_.t[:], I["ck"][l][cb * 128:(cb + 1) * 128, :], writes=[kt_.r], stream="cldk%d" % (ik % 2))
                            fw.dma("sp", vt_.t[:], I["cv"][l][cb * 128:(cb + 1) * 128, :], writes=[vt_.r], stream="cldv%d" % (ik % 2))
                            fw.dma("sp", ft_.t[:], I["cl"][l][cb * 128:(cb + 1) * 128, :], writes=[ft_.r], stream="cldf%d" % (ik % 2))
                            pb = next_bank()
                            for pc in range(4):
                                fw.op("pe", lambda e, pc=pc, pb=pb, kt_=kt_: e.transpose(
                                    pb.t[:, pc * 128:(pc + 1) * 128], kt_.t[:, pc * 128:(pc + 1) * 128], ident.t[:]),
                                    reads=[kt_.r, ident.r], writes=[pb.r], signal=(pc == 3))
                            fw.op("act", lambda e, pb=pb, cb=cb: e.activation(
                                out=KT.t[:, :, cb * 128:(cb + 1) * 128],
                                in_=pb.t[:, :].rearrange("p (c t) -> p c t", c=4), func=AF.Copy),
                                reads=[pb.r], writes=[KT.rs[cb]])
                            fw.op("dve", lambda e, vt_=vt_, cb=cb: e.tensor_copy(
                                out=Vx.t[:, cb, :, 0:64], in_=vt_.t[:, :].rearrange("p (h d) -> p h d", h=8)),
                                reads=[vt_.r], writes=[Vx.rs[cb]])
                            pc_ = next_bank()
                            fw.op("pe", lambda e, pc_=pc_, ft_=ft_, cb=cb: e.matmul(
                                pc_.t[:, 0:8], lhsT=trif.t[:], rhs=ft_.t[:], start=True, stop=(cb == 0)),
                                reads=[trif.r, ft_.r], writes=[pc_.r], signal=(cb == 0))
                            if cb > 0:
                                fw.op("pe", lambda e, pc_=pc_, cb=cb: e.matmul(
                                    pc_.t[:, 0:8], lhsT=self127.t[:], rhs=ctok.t[:, cb - 1, :], start=False, stop=True),
                                    reads=[self127.r, ctok.rs[cb - 1]], writes=[pc_.r])
                            fw.op("dve", lambda e, pc_=pc_, cb=cb: e.tensor_copy(out=ctok.t[:, cb, :], in_=pc_.t[:, 0:8]),
                                  reads=[pc_.r], writes=[ctok.rs[cb]])
                            fw.op("act", lambda e, pc_=pc_, cb=cb: e.activation(
                                out=negc.t[:, cb, :], in_=pc_.t[:, 0:8], func=AF.Copy, scale=-1.0),
                                reads=[pc_.r], writes=[negc.rs[cb]])
                    dK = O["fkp"][l][s] if s < 2 else O["fks"][l]
                    dV = O["fvp"][l][s] if s < 2 else O["fvs"][l]
                    dF = O["flp"][l][s] if s < 2 else O["fls"][l]
                    def a1_tile(s, t0, w, j, xT, off, kbase, dK, dV, dF, l=l):
                        nonlocal ik, ipt
                        lt0 = t0 - off
                        kt0 = kbase + lt0
                        nb = (w + 127) // 128
                        pw = min(w, 128)
                        load_xT(xT, t0, w)
                        norm_fm(xT, w, sq, tmp, lnv, rstd, hT,
                                lambda c, s=s: G1.t[:, l, s, c:c + 1], lambda c, s=s: mod.t[:, l, 0, s, c:c + 1], hT.rs)
                        for pc in range(4):
                            pq = next_bank()
                            for kc in range(8):
                                fw.op("pe", lambda e, kc=kc, pc=pc, pq=pq, w=w: e.matmul(
                                    pq.t[:, 0:w], lhsT=wf.t[:, kc, pc * 128:(pc + 1) * 128], rhs=hT.t[:, kc, 0:w],
                                    start=(kc == 0), stop=(kc == 7)),
                                    reads=[wf.r, hT.rs[kc]], writes=[pq.r], signal=(kc == 7))
                            fw.op("act", lambda e, pc=pc, pq=pq, w=w: e.activation(
                                out=QT.t[:, pc, 0:w], in_=pq.t[:, 0:w], func=AF.Copy, scale=0.125),
                                reads=[pq.r], writes=[QT.rs[pc]])
                            pk = next_bank()
                            for kc in range(8):
                                fw.op("pe", lambda e, kc=kc, pc=pc, pk=pk, w=w: e.matmul(
                                    pk.t[:, 0:w], lhsT=wf.t[:, kc, 512 + pc * 128:512 + (pc + 1) * 128],
                                    rhs=hT.t[:, kc, 0:w], start=(kc == 0), stop=(kc == 7)),
                                    reads=[wf.r, hT.rs[kc]], writes=[pk.r], signal=(kc == 7))
                            kregs = [KT.rs[(kt0 + tb * 128) // 128] for tb in range(nb)]
                            fw.op("dve", lambda e, pc=pc, pk=pk, w=w, kt0=kt0: e.tensor_copy(
                                out=KT.t[:, pc, kt0:kt0 + w], in_=pk.t[:, 0:w]), reads=[pk.r], writes=kregs)
                        for tb in range(nb):
                            kb = (kt0 + tb * 128) // 128
                            kt_ = ktok[ik % 2]
                            vt_ = vtok[ik % 2]
                            ft_ = ftok[ik % 2]
                            lt_ = ltok[ik % 2]
                            ik += 1
                            for (dst_t, c0, ncol, dd, eng) in ((kt_, 512, 512, dK, "act"), (vt_, 1024, 512, dV, "dve")):
                                pb = next_bank()
                                for kc in range(8):
                                    fw.op("pe", lambda e, kc=kc, pb=pb, tb=tb, c0=c0, ncol=ncol, pw=pw: e.matmul(
                                        pb.t[0:pw, 0:ncol], lhsT=hT.t[:, kc, tb * 128:tb * 128 + pw],
                                        rhs=wf.t[:, kc, c0:c0 + ncol], start=(kc == 0), stop=(kc == 7)),
                                        reads=[wf.r, hT.rs[kc]], writes=[pb.r], signal=(kc == 7))
                                if eng == "act":
                                    fw.op("act", lambda e, pb=pb, dst_t=dst_t, pw=pw: e.activation(
                                        out=dst_t.t[0:pw, :], in_=pb.t[0:pw, :], func=AF.Copy),
                                        reads=[pb.r], writes=[dst_t.r])
                                else:
                                    fw.op("dve", lambda e, pb=pb, dst_t=dst_t, pw=pw: e.tensor_copy(
                                        out=dst_t.t[0:pw, :], in_=pb.t[0:pw, :]), reads=[pb.r], writes=[dst_t.r])
                                r0 = lt0 + tb * 128
                                fw.dma("sp", dd[r0:r0 + pw, :], dst_t.t[0:pw, :], reads=[dst_t.r],
                                       stream="kvo%s%d" % (eng[0], ik % 2))
                            fw.op("pool", lambda e, vt_=vt_, kb=kb, pw=pw: e.tensor_copy(
                                out=Vx.t[0:pw, kb, :, 0:64], in_=vt_.t[0:pw, :].rearrange("p (h d) -> p h d", h=8)),
                                reads=[vt_.r], writes=[Vx.rs[kb]])
                            pf = next_bank()
                            for kc in range(8):
                                fw.op("pe", lambda e, kc=kc, pf=pf, tb=tb, pw=pw: e.matmul(
                                    pf.t[0:pw, 0:8], lhsT=hT.t[:, kc, tb * 128:tb * 128 + pw],
                                    rhs=wf.t[:, kc, 1536:1544], start=(kc == 0), stop=(kc == 7)),
                                    reads=[wf.r, hT.rs[kc]], writes=[pf.r], signal=(kc == 7))
                            fw.op("dve", lambda e, pf=pf, ft_=ft_, pw=pw: e.tensor_tensor(
                                out=ft_.t[0:pw, :], in0=pf.t[0:pw, 0:8], in1=bfb.t[0:pw, l, :], op=ALU.add),
                                reads=[pf.r, bfb.r], writes=[ft_.r])
                            fw.op("act", lambda e, ft_=ft_, pw=pw: e.activation(
                                out=ft_.t[0:pw, :], in_=ft_.t[0:pw, :], func=AF.Exp, scale=-1.0),
                                reads=[ft_.r], writes=[ft_.r])
                            fw.op("act", lambda e, ft_=ft_, pw=pw: e.activation(
                                out=ft_.t[0:pw, :], in_=ft_.t[0:pw, :], func=AF.Ln, bias=1.0),
                                reads=[ft_.r], writes=[ft_.r])
                            fw.op("dve", lambda e, ft_=ft_, lt_=lt_, pw=pw: e.tensor_scalar_mul(
                                out=lt_.t[0:pw, :], in0=ft_.t[0:pw, :], scalar1=-1.0), reads=[ft_.r], writes=[lt_.r])
                            r0 = lt0 + tb * 128
                            fw.dma("sp", dF[r0:r0 + pw, :], lt_.t[0:pw, :], reads=[lt_.r], stream="kvof%d" % (ik % 2))
                            pc_ = next_bank()
                            first = (kb == 0)
                            fw.op("pe", lambda e, pc_=pc_, lt_=lt_, pw=pw, first=first: e.matmul(
                                pc_.t[0:pw, 0:8], lhsT=trif.t[0:pw, 0:pw], rhs=lt_.t[0:pw, :], start=True, stop=first),
                                reads=[trif.r, lt_.r], writes=[pc_.r], signal=first)
                            if not first:
                                fw.op("pe", lambda e, pc_=pc_, kb=kb, pw=pw: e.matmul(
                                    pc_.t[0:pw, 0:8], lhsT=self127.t[:, 0:pw], rhs=ctok.t[:, kb - 1, :],
                                    start=False, stop=True),
                                    reads=[self127.r, ctok.rs[kb - 1]], writes=[pc_.r])
                            fw.op("dve", lambda e, pc_=pc_, kb=kb, pw=pw: e.tensor_copy(
                                out=ctok.t[0:pw, kb, :], in_=pc_.t[0:pw, 0:8]), reads=[pc_.r], writes=[ctok.rs[kb]])
                            fw.op("act", lambda e, pc_=pc_, kb=kb, pw=pw: e.activation(
                                out=negc.t[0:pw, kb, :], in_=pc_.t[0:pw, 0:8], func=AF.Copy, scale=-1.0),
                                reads=[pc_.r], writes=[negc.rs[kb]])
                            pt_ = next_bank()
                            fw.op("pe", lambda e, pt_=pt_, kb=kb, pw=pw: e.transpose(
                                pt_.t[0:8, 0:pw], ctok.t[0:pw, kb, :], ident.t[0:pw, 0:pw]),
                                reads=[ctok.rs[kb], ident.r], writes=[pt_.r])
                            fw.op("dve", lambda e, pt_=pt_, tb=tb, pw=pw: e.tensor_copy(
                                out=cfm.t[:, tb * 128:tb * 128 + pw], in_=pt_.t[0:8, 0:pw]),
                                reads=[pt_.r], writes=[cfm.r])
                        fw.op("act", lambda e, w=w: e.activation(out=cq96.t[0:8, 0:w], in_=cfm.t[:, 0:w], func=AF.Copy),
                              reads=[cfm.r], writes=[cq96.r])
                        fw.op("dve", lambda e, w=w: e.tensor_tensor(out=cr1.t[:, 0:w], in0=cfm.t[:, 0:w],
                                                                    in1=cq96.t[0:8, 0:w], op=ALU.subtract),
                              reads=[cfm.r, cq96.r], writes=[cr1.r])
                        fw.op("act", lambda e, w=w: e.activation(out=midt.t[:, 0:w], in_=cr1.t[:, 0:w], func=AF.Copy),
                              reads=[cr1.r], writes=[midt.r])
                        fw.op("pool", lambda e, w=w: e.tensor_copy(out=cq96.t[32:40, 0:w], in_=midt.t[:, 0:w]),
                              reads=[midt.r], writes=[cq96.r])
                        fw.op("dve", lambda e, w=w: e.tensor_tensor(out=cr2.t[:, 0:w], in0=cr1.t[:, 0:w],
                                                                    in1=midt.t[:, 0:w], op=ALU.subtract),
                              reads=[cr1.r, midt.r], writes=[cr2.r])
                        fw.op("act", lambda e, w=w: e.activation(out=cq96.t[64:72, 0:w], in_=cr2.t[:, 0:w], func=AF.Copy),
                              reads=[cr2.r], writes=[cq96.r])
                        kb_first_tile = kt0 // 128
                        nkb = kb_first_tile + nb
                        for h in range(8):
                            hr = slice((h % 2) * 64, (h % 2) * 64 + 64)
                            hp = h // 2
                            ob = banks[6 + (h % 2)]
                            blocks = []
                            for kb in range(nkb):
                                if kb < kb_first_tile:
                                    q0, rows, diag = 0, 128, False
                                else:
                                    q0, rows, diag = (kb - kb_first_tile) * 128, pw, True
                                blocks.append((kb, q0, rows, diag))
                            sbanks = {}
                            ptl = {}

                            def emit_s(bi, h=h, hr=hr, hp=hp):
                                kb, q0, rows, diag = blocks[bi]
                                sb = next_bank(pool=(0, 1, 2, 3))
                                sbanks[bi] = sb
                                fw.op("pe", lambda e, sb=sb, kb=kb, q0=q0, rows=rows: e.matmul(
                                    sb.t[0:rows, q0:w], lhsT=KT.t[hr, hp, kb * 128:kb * 128 + rows],
                                    rhs=QT.t[hr, hp, q0:w], start=True, stop=False),
                                    reads=[KT.rs[kb], QT.rs[hp]], writes=[sb.r], signal=False)
                                fw.op("pe", lambda e, sb=sb, q0=q0, rows=rows: e.matmul(
                                    sb.t[0:rows, q0:w], lhsT=selh.t[0:72, h, 0:rows], rhs=cq96.t[0:72, q0:w],
                                    start=False, stop=(not diag)),
                                    reads=[selh.r, cq96.r], writes=[sb.r], signal=(not diag))
                                if diag:
                                    fw.op("pe", lambda e, sb=sb, q0=q0, rows=rows: e.matmul(
                                        sb.t[0:rows, q0:q0 + rows], lhsT=identb.t[0:rows, 0:rows],
                                        rhs=maskneg.t[0:rows, 0:rows], start=False, stop=True),
                                        reads=[identb.r, maskneg.r], writes=[sb.r])

                            def emit_pv(bi, h=h, ob=ob):
                                nonlocal ipt
                                kb, q0, rows, diag = blocks[bi]
                                sb = sbanks.pop(bi)
                                pt = pts[ipt % 4]
                                ipt += 1
                                fw.op("act", lambda e, sb=sb, pt=pt, kb=kb, q0=q0, rows=rows: e.activation(
                                    out=pt.t[0:rows, q0:w], in_=sb.t[0:rows, q0:w], func=AF.Exp,
                                    bias=negc.t[0:rows, kb, h:h + 1]),
                                    reads=[sb.r, negc.rs[kb]], writes=[pt.r])
                                last = (bi == len(blocks) - 1)
                                fw.op("pe", lambda e, pt=pt, kb=kb, q0=q0, rows=rows, bi=bi, last=last: e.matmul(
                                    ob.t[0:65, q0:w], lhsT=Vx.t[0:rows, kb, h, :], rhs=pt.t[0:rows, q0:w],
                                    start=(bi == 0), stop=last),
                                    reads=[Vx.rs[kb], pt.r], writes=[ob.r], signal=last)
                            LOOK = 2
                            nbk = len(blocks)
                            for bi in range(min(LOOK, nbk)):
                                emit_s(bi)
                            for bi in range(nbk):
                                emit_pv(bi)
                                if bi + LOOK < nbk:
                                    emit_s(bi + LOOK)
                            fw.op("act", lambda e, ob=ob, w=w: e.activation(
                                out=rsf.t[64:65, 0:w], in_=ob.t[64:65, 0:w], func=AF.Copy), reads=[ob.r], writes=[rsf.r])
                            pr = next_bank(pool=(4, 5))
                            fw.op("pe", lambda e, pr=pr, w=w: e.matmul(
                                pr.t[0:64, 0:w], lhsT=onesf.t[64:65, 0:64], rhs=rsf.t[64:65, 0:w], start=True, stop=True),
                                reads=[onesf.r, rsf.r], writes=[pr.r])
                            fw.op("dve", lambda e, pr=pr, w=w: e.reciprocal(out=rcp.t[:, 0:w], in_=pr.t[0:64, 0:w]),
                                  reads=[pr.r], writes=[rcp.r])
                            fw.op("dve", lambda e, ob=ob, hr=hr, hp=hp, w=w: e.tensor_tensor(
                                out=yfT.t[hr, hp, 0:w], in0=ob.t[0:64, 0:w], in1=rcp.t[:, 0:w], op=ALU.mult),
                                reads=[ob.r, rcp.r], writes=[yfT.rs[hp]])
                        fw.dma("sp", yfox.rearrange("(c p) t -> p c t", p=128)[:, :, t0:t0 + w], yfT.t[:, :, 0:w],
                               reads=yfT.rs, writes=[yfreg(t0)], stream="yfst")
                    for (t0, w, j) in tiles_of(s, WA):
                        xT = xTa[it % 2]
                        it += 1
                        a1_tile(s, t0, w, j, xT, off, kbase, dK, dV, dF)
                fw.flush()
                default_pool[0] = (0, 1, 2, 3, 4, 5, 6, 7)
            if stage >= 2:
              with contextlib.ExitStack() as ph:
                sub = FWScope(fw, ph)
                default_pool[0] = (0, 1, 2, 3, 4, 5)
                WA = 256
                RW0, HG0 = 0, RW_COLS + FOX_COLS
                NWI = RW_COLS + HG_COLS
                wi = Tt(sub.sbuf("wi", [128, 8, NWI], BF16), name="wi")
                wo = Tt(sub.sbuf("wo", [128, 8, D], BF16), name="wo")
                for kc in range(8):
                    fw.dma("pool", wi.t[:, kc, 0:RW_COLS], I["w_in"][l][kc * 128:(kc + 1) * 128, 0:RW_COLS],
                           writes=[wi.r], stream="wld", group=True)
                    fw.dma("pool", wi.t[:, kc, RW_COLS:NWI], I["w_in"][l][kc * 128:(kc + 1) * 128, HG0:HG0 + HG_COLS],
                           writes=[wi.r], stream="wld", group=True)
                    fw.dma("pool", wo.t[:, kc, :], I["w_out"][l][kc * 128:(kc + 1) * 128, :], writes=[wo.r], stream="wld", group=True)
                xTa = [Tt(sub.sbuf("xTa%d" % i, [128, 8, WA], F32), nreg=8, name="xTa%d" % i) for i in range(2)]
                sq = Tt(sub.sbuf("sqa", [128, 8, WA], BF16), name="sqa")
                hT = Tt(sub.sbuf("hTa", [128, 8, WA], BF16), nreg=8, name="hTa")
                tmp = [Tt(sub.sbuf("tmpa%d" % i, [128, WA], F32), name="tmpa%d" % i) for i in range(2)]
                lnv = Tt(sub.sbuf("lnva", [128, WA], F32), name="lnva")
                rstd = Tt(sub.sbuf("rstda", [128, WA], F32), name="rstda")
                ymix = Tt(sub.sbuf("ymix", [128, 8, WA], BF16), nreg=8, name="ymix")
                fw.op("pool", lambda e: e.memset(ymix.t[:], 0.0), writes=ymix.rs)

                def S(name, shape=None, dt=F32, nreg=1):
                    return Tt(sub.sbuf(name, shape or [128, WA], dt), nreg=nreg, name=name)

                def proj_fm(c0, w, M=128):
                    pb = next_bank()
                    for kc in range(8):
                        fw.op("pe", lambda e, kc=kc: e.matmul(
                            pb.t[0:M, 0:w], lhsT=wi.t[:, kc, c0:c0 + M], rhs=hT.t[:, kc, 0:w],
                            start=(kc == 0), stop=(kc == 7)),
                            reads=[wi.r, hT.rs[kc]], writes=[pb.r], signal=(kc == 7))
                    return pb

                NCH = WA // 32
                hg_E = [S("hg_E%d" % i) for i in range(2)]
                hg_KK = [S("hg_KK%d" % i) for i in range(2)]
                hg_B = [S("hg_B%d" % i) for i in range(2)]
                hg_D = S("hg_D")
                hg_X = S("hg_X")
                hg_Q = S("hg_Q")
                hg_G = [S("hg_G%d" % i) for i in range(2)]
                hg_Qt = S("hg_Qt", [128, 2, WA], BF16, nreg=2)
                hg_Kh = S("hg_Kh", [128, 2, WA], BF16, nreg=2)
                hg_Ke = S("hg_Ke", [128, 2, WA], BF16, nreg=2)
                hg_ebl = S("hg_ebl", [128, 2, NCH], F32, nreg=2)
                hg_ebm = S("hg_ebm", [128, 2, NCH], F32, nreg=2)
                hg_Vh = S("hg_Vh", [128, 4, 256], BF16, nreg=4)
                hg_KeT = S("hg_KeT", [128, 4, 4, 256], BF16, nreg=4)
                hg_AT = S("hg_AT", [128, 4, 2, 2, 128], BF16, nreg=4)
                hg_Sm = S("hg_Sm", [128, 2, 128], F32, nreg=2)
                hg_Sbd = S("hg_Sbd", [128, 2, 128], BF16, nreg=2)
                hg_sq = S("hg_sq", [128, WA], BF16)
                hg_t1 = S("hg_t1")

                def hg_init(s):
                    fw.op("pool", lambda e: e.memset(hg_Sm.t[:], 0.0), writes=hg_Sm.rs)
                    if s == 2:
                        for h in range(4):
                            hr = slice((h % 2) * 64, (h % 2) * 64 + 64)
                            fw.dma("sp", hg_Sm.t[hr, h // 2, (h % 2) * 64:(h % 2) * 64 + 64], I["shg"][l][h],
                                   writes=[hg_Sm.rs[h // 2]], stream="stld", group=True)

                def hg_final(s):
                    dst = O["hgp"][l][s] if s < 2 else O["hgs"][l]
                    for h in range(4):
                        hr = slice((h % 2) * 64, (h % 2) * 64 + 64)
                        fw.dma("sp", dst[h], hg_Sm.t[hr, h // 2, (h % 2) * 64:(h % 2) * 64 + 64],
                               reads=[hg_Sm.rs[h // 2]], stream="ststhg%d" % s, group=True)

                def hg_tile(s, w):
                    nch = w // 32
                    nb = (w + 127) // 128
                    pw = min(w, 128)
                    c_q, c_f, c_i, c_g = RW_COLS, RW_COLS + 256, RW_COLS + 512, RW_COLS + 768
                    for tb in range(nb):
                        pb = next_bank()
                        for kc in range(8):
                            fw.op("pe", lambda e, kc=kc, tb=tb, pb=pb: e.matmul(
                                pb.t[0:pw, 0:256], lhsT=hT.t[:, kc, tb * 128:tb * 128 + pw], rhs=wi.t[:, kc, c_i:c_i + 256],
                                start=(kc == 0), stop=(kc == 7)),
                                reads=[wi.r, hT.rs[kc]], writes=[pb.r], signal=(kc == 7))
                        fw.op("act", lambda e, tb=tb, pb=pb: e.activation(
                            out=hg_Vh.t[0:pw, tb, :], in_=pb.t[0:pw, 0:256], func=AF.Copy),
                            reads=[pb.r], writes=[hg_Vh.rs[tb]])
                    for pc in range(2):
                        E, KK, B, G = hg_E[pc], hg_KK[pc], hg_B[pc], hg_G[pc]
                        lb_ap = lbT.t[:, l, pc:pc + 1]
                        oml_ap = omlT.t[:, l, pc:pc + 1]
                        noml_ap = nomlT.t[:, l, pc:pc + 1]
                        pf = proj_fm(c_f + pc * 128, w)
                        fw.op("act", lambda e, pf=pf, E=E: e.activation(out=E.t[:, 0:w], in_=pf.t[:, 0:w], func=AF.Exp,
                                                                      scale=-1.0), reads=[pf.r], writes=[E.r])
                        fw.op("dve", lambda e, E=E: e.tensor_scalar_add(out=E.t[:, 0:w], in0=E.t[:, 0:w], scalar1=1.0),
                              reads=[E.r], writes=[E.r])
                        fw.op("dve", lambda e, E=E: e.reciprocal(out=E.t[:, 0:w], in_=E.t[:, 0:w]),
                              reads=[E.r], writes=[E.r])
                        fw.op("dve", lambda e, E=E, KK=KK, noml_ap=noml_ap, oml_ap=oml_ap: e.tensor_scalar(
                            out=KK.t[:, 0:w], in0=E.t[:, 0:w], scalar1=noml_ap, scalar2=oml_ap, op0=ALU.mult, op1=ALU.add),
                            reads=[E.r, nomlT.r, omlT.r], writes=[KK.r])
                        fw.op("act", lambda e, E=E, oml_ap=oml_ap, lb_ap=lb_ap: e.activation(
                            out=E.t[:, 0:w], in_=E.t[:, 0:w], func=AF.Ln, scale=oml_ap, bias=lb_ap),
                            reads=[E.r, omlT.r, lbT.r], writes=[E.r])
                        fw.op("dve", lambda e, E=E, B=B: e.tensor_tensor_scan(
                            out=B.t[:, 0:w], data0=rmask32.t[:, 0:w], data1=E.t[:, 0:w], initial=0.0,
                            op0=ALU.mult, op1=ALU.add), reads=[E.r, rmask32.r], writes=[B.r])
                        Bv = B.t[:, 0:w].rearrange("p (c t) -> p c t", t=32)
                        Dv = hg_D.t[:, 0:w].rearrange("p (c t) -> p c t", t=32)
                        fw.op("dve", lambda e, Bv=Bv, Dv=Dv: e.tensor_tensor(
                            out=Dv, in0=Bv, in1=Bv[:, :, 15:16].to_broadcast([128, nch, 32]), op=ALU.subtract),
                            reads=[B.r], writes=[hg_D.r])
                        pq = proj_fm(c_q + pc * 128, w)
                        fw.op("act", lambda e, pq=pq: e.activation(out=hg_Q.t[:, 0:w], in_=pq.t[:, 0:w], func=AF.Copy),
                              reads=[pq.r], writes=[hg_Q.r])
                        fw.op("act", lambda e: e.activation(out=hg_X.t[:, 0:w], in_=hg_D.t[:, 0:w], func=AF.Exp),
                              reads=[hg_D.r], writes=[hg_X.r])
                        fw.op("dve", lambda e, pc=pc: e.tensor_tensor(out=hg_Qt.t[:, pc, 0:w], in0=hg_Q.t[:, 0:w],
                                                                      in1=hg_X.t[:, 0:w], op=ALU.mult),
                              reads=[hg_Q.r, hg_X.r], writes=[hg_Qt.rs[pc]])
                        fw.op("act", lambda e: e.activation(out=hg_X.t[:, 0:w], in_=hg_D.t[:, 0:w], func=AF.Exp, scale=-1.0),
                              reads=[hg_D.r], writes=[hg_X.r])
                        fw.op("dve", lambda e, pc=pc, KK=KK: e.tensor_tensor(out=hg_Kh.t[:, pc, 0:w], in0=KK.t[:, 0:w],
                                                                             in1=hg_X.t[:, 0:w], op=ALU.mult),
                              reads=[KK.r, hg_X.r], writes=[hg_Kh.rs[pc]])
                        fw.op("dve", lambda e, Bv=Bv, Dv=Dv: e.tensor_tensor(
                            out=Dv, in0=Bv[:, :, 31:32].to_broadcast([128, nch, 32]), in1=Bv, op=ALU.subtract),
                            reads=[B.r], writes=[hg_D.r])
                        fw.op("act", lambda e: e.activation(out=hg_X.t[:, 0:w], in_=hg_D.t[:, 0:w], func=AF.Exp),
                              reads=[hg_D.r], writes=[hg_X.r])
                        fw.op("dve", lambda e, pc=pc, KK=KK: e.tensor_tensor(out=hg_Ke.t[:, pc, 0:w], in0=KK.t[:, 0:w],
                                                                             in1=hg_X.t[:, 0:w], op=ALU.mult),
                              reads=[KK.r, hg_X.r], writes=[hg_Ke.rs[pc]])
                        fw.op("act", lambda e, pc=pc, Bv=Bv: e.activation(out=hg_ebl.t[:, pc, 0:nch], in_=Bv[:, :, 31],
                                                                          func=AF.Exp), reads=[B.r], writes=[hg_ebl.rs[pc]])
                        fw.op("act", lambda e, pc=pc, Bv=Bv: e.activation(out=hg_ebm.t[:, pc, 0:nch], in_=Bv[:, :, 15],
                                                                          func=AF.Exp), reads=[B.r], writes=[hg_ebm.rs[pc]])
                        pg = proj_fm(c_g + pc * 128, w)
                        fw.op("act", lambda e, pg=pg, G=G: e.activation(out=G.t[:, 0:w], in_=pg.t[:, 0:w], func=AF.Silu),
                              reads=[pg.r], writes=[G.r])
                    HGDBG = int(os.environ.get("HGDBG", "9"))
                    if HGDBG < 2:
                        return
                    for tb in range(nb):
                        pAs = [next_bank(), next_bank()]
                        for h in range(4):
                            hr = slice((h % 2) * 64, (h % 2) * 64 + 64)
                            pA = pAs[h % 2]
                            fw.op("pe", lambda e, h=h, hr=hr, tb=tb, pA=pA: e.matmul(
                                pA.t[0:pw, (h // 2) * 128:(h // 2) * 128 + pw], lhsT=hg_Kh.t[hr, h // 2, tb * 128:tb * 128 + pw],
                                rhs=hg_Qt.t[hr, h // 2, tb * 128:tb * 128 + pw], start=True, stop=True),
                                reads=[hg_Kh.rs[h // 2], hg_Qt.rs[h // 2]], writes=[pA.r], signal=(h >= 2))
                        for par in range(2):
                            pA = pAs[par]
                            fw.op("dve", lambda e, tb=tb, pA=pA, par=par: e.tensor_tensor(
                                out=hg_AT.t[0:pw, tb, par, :, 0:pw],
                                in0=pA.t[0:pw, 0:256].rearrange("p (h t) -> p h t", h=2)[:, :, 0:pw],
                                in1=maskbd.t[0:pw, 0:pw].unsqueeze(1).to_broadcast([pw, 2, pw]), op=ALU.mult),
                                reads=[pA.r, maskbd.r], writes=[hg_AT.rs[tb]])
                        if os.environ.get("HGSUB", "") == "A":
                            continue
                        pT = next_bank()
                        pTb = pT.t[:, :].bitcast(BF16)
                        for pc in range(2):
                            fw.op("pe", lambda e, pc=pc, tb=tb, pTb=pTb: e.transpose(
                                pTb[0:pw, pc * 128:(pc + 1) * 128], hg_Ke.t[:, pc, tb * 128:tb * 128 + pw], identb.t[:, :]),
                                reads=[hg_Ke.rs[pc], identb.r], writes=[pT.r], signal=(pc == 1))
                        for cc in range(min(4, nch - tb * 4)):
                            fw.op("act", lambda e, tb=tb, pTb=pTb, cc=cc: e.activation(
                                out=hg_KeT.t[0:pw, tb, cc, :], in_=pTb[0:pw, 0:256], func=AF.Identity,
                                scale=maskbd.t[0:pw, cc * 32 + 31:cc * 32 + 32]),
                                reads=[pT.r, maskbd.r], writes=[hg_KeT.rs[tb]])
                    if HGDBG < 3:
                        return
                    po = [banks[6], banks[7]]
                    for tb in range(nb):
                        for h in range(4):
                            fw.op("pe", lambda e, h=h, tb=tb: e.matmul(
                                po[h // 2].t[(h % 2) * 64:(h % 2) * 64 + 64, tb * 128:tb * 128 + pw],
                                lhsT=hg_Vh.t[0:pw, tb, h * 64:(h + 1) * 64], rhs=hg_AT.t[0:pw, tb, h % 2, h // 2, 0:pw],
                                start=True, stop=False, skip_group_check=True),
                                reads=[hg_Vh.rs[tb], hg_AT.rs[tb]], writes=[po[h // 2].r], signal=False)
                        for cc in range(min(4, nch - tb * 4)):
                            c = tb * 4 + cc
                            last = (c == nch - 1)
                            for pc in range(2):
                                fw.op("act", lambda e, pc=pc, c=c: e.activation(
                                    out=hg_Sbd.t[:, pc, :], in_=hg_Sm.t[:, pc, :], func=AF.Identity,
                                    scale=hg_ebm.t[:, pc, c:c + 1]),
                                    reads=[hg_Sm.rs[pc], hg_ebm.rs[pc]], writes=[hg_Sbd.rs[pc]])
                                fw.op("pe", lambda e, pc=pc, c=c: e.matmul(
                                    po[pc].t[:, c * 32:(c + 1) * 32], lhsT=hg_Sbd.t[:, pc, :],
                                    rhs=hg_Qt.t[:, pc, c * 32:(c + 1) * 32], start=False, stop=True,
                                    skip_group_check=True),
                                    reads=[hg_Sbd.rs[pc], hg_Qt.rs[pc]], writes=[po[pc].r], signal=last)
                                pS = next_bank()
                                fw.op("pe", lambda e, pc=pc, tb=tb, cc=cc, pS=pS: e.matmul(
                                    pS.t[:, 0:128], lhsT=hg_KeT.t[0:pw, tb, cc, pc * 128:(pc + 1) * 128],
                                    rhs=hg_Vh.t[0:pw, tb, pc * 128:(pc + 1) * 128], start=True, stop=True),
                                    reads=[hg_KeT.rs[tb], hg_Vh.rs[tb]], writes=[pS.r])
                                for hh in range(2):
                                    hr = slice(hh * 64, hh * 64 + 64)
                                    fw.op("dve", lambda e, pc=pc, c=c, hr=hr, pS=pS: e.scalar_tensor_tensor(
                                        out=hg_Sm.t[hr, pc, hr], in0=hg_Sm.t[hr, pc, hr], scalar=hg_ebl.t[hr, pc, c:c + 1],
                                        in1=pS.t[hr, hr], op0=ALU.mult, op1=ALU.add),
                                        reads=[hg_Sm.rs[pc], hg_ebl.rs[pc], pS.r], writes=[hg_Sm.rs[pc]])
                    if HGDBG < 4:
                        return
                    for pc in range(2):
                        G = hg_G[pc]
                        fw.op("act", lambda e, pc=pc: e.activation(out=hg_sq.t[:, 0:w], in_=po[pc].t[:, 0:w], func=AF.Square),
                              reads=[po[pc].r], writes=[hg_sq.r])
                        pn = next_bank()
                        fw.op("pe", lambda e, pn=pn: e.matmul(pn.t[:, 0:w], lhsT=onesbd.t[:], rhs=hg_sq.t[:, 0:w],
                                                              start=True, stop=True),
                              reads=[onesbd.r, hg_sq.r], writes=[pn.r])
                        fw.op("act", lambda e, pn=pn: e.activation(out=hg_X.t[:, 0:w], in_=pn.t[:, 0:w], func=AF.Ln,
                                                                   scale=1.0 / 64, bias=epsb.t[:]),
                              reads=[pn.r, epsb.r], writes=[hg_X.r])
                        fw.op("act", lambda e: e.activation(out=hg_X.t[:, 0:w], in_=hg_X.t[:, 0:w], func=AF.Exp, scale=-0.5),
                              reads=[hg_X.r], writes=[hg_X.r])
                        fw.op("dve", lambda e, pc=pc: e.tensor_tensor(out=hg_t1.t[:, 0:w], in0=po[pc].t[:, 0:w],
                                                                      in1=hg_X.t[:, 0:w], op=ALU.mult),
                              reads=[po[pc].r, hg_X.r], writes=[hg_t1.r])
                        fw.op("dve", lambda e, pc=pc, G=G: e.scalar_tensor_tensor(
                            out=ymix.t[:, 6 + pc, 0:w], in0=hg_t1.t[:, 0:w], scalar=hgng.t[:, l, pc:pc + 1], in1=G.t[:, 0:w],
                            op0=ALU.mult, op1=ALU.mult), reads=[hg_t1.r, hgng.r, G.r], writes=[ymix.rs[6 + pc]])

                C0 = 0.6065306597126334
                rw_Pb = S("rw_Pb", [128, 9, WA + 1], F32)
                rw_car = S("rw_car", [128, 9], F32)
                rw_c7 = S("rw_c7", [128, 7], F32)
                rw_tmp = [S("rw_tmp%d" % i) for i in range(6)]
                rw_SIG = S("rw_SIG")
                rw_A = S("rw_A")
                rw_G = [S("rw_G%d" % i) for i in range(2)]
                rw_L = S("rw_L")
                rw_KP = S("rw_KP")
                rw_KN = S("rw_KN")
                rw_Bf = S("rw_Bf")
                rw_Yf = S("rw_Yf")
                rw_TW = S("rw_TW", [32, WA], BF16)
                rw_AL = S("rw_AL", [32, WA], BF16)
                rw_SG = S("rw_SG", [64, WA], BF16)
                rw_sqb = S("rw_sqb", [128, WA], BF16)
                rw_Kh = S("rw_Kh", [128, 2, WA], BF16, nreg=2)
                rw_Bm = S("rw_Bm", [128, 2, 2, WA], BF16, nreg=2)
                rw_QRm = S("rw_QRm", [128, 2, 2, WA // 32, 2, 64], BF16, nreg=2)
                rw_Ke = S("rw_Ke", [128, 2, WA], BF16, nreg=2)
                rw_Be = S("rw_Be", [128, 2, WA], BF16, nreg=2)
                rw_Vb = S("rw_Vb", [128, 2, WA], BF16, nreg=2)
                NCR = WA // 32
                rw_Vt = S("rw_Vt", [64, NCR, 256], BF16)
                rw_KeT = S("rw_KeT", [64, NCR, 256], BF16)
                rw_BeT = S("rw_BeT", [64, NCR, 256], BF16)
                rw_gC = S("rw_gC", [128, 2, NCR], F32, nreg=2)
                rw_AT12 = S("rw_AT12", [64, 4, 2, 64], BF16)
                rw_AT34 = S("rw_AT34", [64, 4, 2, 64], BF16)
                rw_X = [S("rw_X%d" % i, [64, 4, 64], BF16) for i in range(2)]
                rw_XT = [S("rw_XT%d" % i, [64, 4, 64], BF16) for i in range(2)]
                rw_TT = [S("rw_TT%d" % i, [64, 4, 64], BF16) for i in range(2)]
                rw_TTc = [S("rw_TTc%d" % i, [64, 4, 64], BF16) for i in range(WA // 32)]
                rw_Zb = S("rw_Zb", [64, 256], BF16)
                rw_Un = S("rw_Un", [64, 256], BF16)
                rw_Hm = S("rw_Hm", [128, 2, 128], F32, nreg=2)
                rw_Hbd = S("rw_Hbd", [128, 2, 128], BF16, nreg=2)
                rw_st = S("rw_st", [128, 2, 128], F32)

                def rw_init(s):
                    fw.op("pool", lambda e: e.memset(rw_Hm.t[:], 0.0), writes=rw_Hm.rs)
                    fw.op("pool", lambda e: e.memset(rw_Pb.t[:], 0.0), writes=[rw_Pb.r])
                    if s == 2:
                        fw.op("pool", lambda e: e.memset(rw_st.t[:], 0.0), writes=[rw_st.r])
                        for h in range(4):
                            hr = slice((h % 2) * 64, (h % 2) * 64 + 64)
                            fw.dma("sp", rw_st.t[hr, h // 2, (h % 2) * 64:(h % 2) * 64 + 64], I["srw"][l][h],
                                   writes=[rw_st.r], stream="stld", group=True)
                        for pc in range(2):
                            pb = next_bank()
                            fw.op("pe", lambda e, pc=pc, pb=pb: e.transpose(pb.t[:, 0:128], rw_st.t[:, pc, :], ident.t[:]),
                                  reads=[rw_st.r, ident.r], writes=[pb.r])
                            fw.op("dve", lambda e, pc=pc, pb=pb: e.tensor_copy(out=rw_Hm.t[:, pc, :], in_=pb.t[:, 0:128]),
                                  reads=[pb.r], writes=[rw_Hm.rs[pc]])
                        fw.dma("sp", rw_c7.t[:, :], I["ssh"][l].rearrange("(c p) -> p c", p=128),
                               writes=[rw_c7.r], stream="stld", group=True, allow_slow_non_contiguous=True)
                        fw.op("dve", lambda e: e.tensor_copy(out=rw_Pb.t[:, 0:6, 0], in_=rw_c7.t[:, 0:6]),
                              reads=[rw_c7.r], writes=[rw_Pb.r])
                        fw.op("dve", lambda e: e.tensor_copy(out=rw_Pb.t[0:32, 6, 0:1], in_=rw_c7.t[0:32, 6:7]),
                              reads=[rw_c7.r], writes=[rw_Pb.r])
                        fw.op("dve", lambda e: e.tensor_copy(out=rw_Pb.t[0:32, 7, 0:1], in_=rw_c7.t[32:64, 6:7]),
                              reads=[rw_c7.r], writes=[rw_Pb.r])
                        fw.op("dve", lambda e: e.tensor_copy(out=rw_Pb.t[0:64, 8, 0:1], in_=rw_c7.t[64:128, 6:7]),
                              reads=[rw_c7.r], writes=[rw_Pb.r])
                    for pc in range(2):
                        fw.op("act", lambda e, pc=pc: e.activation(out=rw_Hbd.t[:, pc, :], in_=rw_Hm.t[:, pc, :],
                                                                   func=AF.Copy), reads=[rw_Hm.rs[pc]], writes=[rw_Hbd.rs[pc]])

                def rw_final(s, w):
                    dst = O["rwp"][l][s] if s < 2 else O["rws"][l]
                    for pc in range(2):
                        pb = next_bank()
                        fw.op("pe", lambda e, pc=pc, pb=pb: e.transpose(pb.t[:, 0:128], rw_Hm.t[:, pc, :], ident.t[:]),
                              reads=[rw_Hm.rs[pc], ident.r], writes=[pb.r])
                        fw.op("dve", lambda e, pc=pc, pb=pb: e.tensor_copy(out=rw_st.t[:, pc, :], in_=pb.t[:, 0:128]),
                              reads=[pb.r], writes=[rw_st.r])
                    for h in range(4):
                        hr = slice((h % 2) * 64, (h % 2) * 64 + 64)
                        fw.dma("sp", dst[h], rw_st.t[hr, h // 2, (h % 2) * 64:(h % 2) * 64 + 64], reads=[rw_st.r],
                               stream="ststrs%d" % s, group=True)
                    dsh = O["rshp"][l][s] if s < 2 else O["rshs"][l]
                    fw.op("dve", lambda e: e.tensor_copy(out=rw_c7.t[:, 0:6], in_=rw_car.t[:, 0:6]), reads=[rw_car.r], writes=[rw_c7.r])
                    fw.op("dve", lambda e: e.tensor_copy(out=rw_c7.t[0:32, 6:7], in_=rw_car.t[0:32, 6:7]), reads=[rw_car.r], writes=[rw_c7.r])
                    fw.op("dve", lambda e: e.tensor_copy(out=rw_c7.t[32:64, 6:7], in_=rw_car.t[0:32, 7:8]), reads=[rw_car.r], writes=[rw_c7.r])
                    fw.op("dve", lambda e: e.tensor_copy(out=rw_c7.t[64:128, 6:7], in_=rw_car.t[0:64, 8:9]), reads=[rw_car.r], writes=[rw_c7.r])
                    fw.dma("sp", dsh.rearrange("(c p) -> p c", p=128), rw_c7.t[:, :], reads=[rw_c7.r],
                           stream="ststrc%d" % s, group=True, allow_slow_non_contiguous=True)

                def rw_tile(s, w):
                    C = min(64, w)
                    nch = w // C
                    nlev = {64: 5, 32: 4}[C]
                    rmask = rmask64 if C == 64 else rmask32
                    T0, T1, T2, T3, T4, T5 = rw_tmp
                    specs = [(i, i * 128, 128) for i in range(6)] + [(6, 768, 32), (7, 800, 32), (8, 832, 64)]
                    for (i, c0, M) in specs:
                        pb = proj_fm(c0, w, M)
                        fw.op("act", lambda e, i=i, M=M, pb=pb: e.activation(out=rw_Pb.t[0:M, i, 1:1 + w], in_=pb.t[0:M, 0:w],
                                                                          func=AF.Copy), reads=[pb.r], writes=[rw_Pb.r])
                    fw.op("act", lambda e: e.activation(out=rw_car.t[:, :], in_=rw_Pb.t[:, :, w], func=AF.Copy),
                          reads=[rw_Pb.r], writes=[rw_car.r])
                    for (i, c0, M) in specs:
                        fw.op("dve", lambda e, i=i, M=M: e.tensor_tensor(out=T0.t[0:M, 0:w], in0=rw_Pb.t[0:M, i, 0:w],
                                                                         in1=rw_Pb.t[0:M, i, 1:1 + w], op=ALU.subtract),
                              reads=[rw_Pb.r], writes=[T0.r])
                        fw.op("dve", lambda e, i=i, M=M: e.scalar_tensor_tensor(
                            out=rw_Pb.t[0:M, i, 1:1 + w], in0=T0.t[0:M, 0:w], scalar=mul.t[0:M, l, i:i + 1],
                            in1=rw_Pb.t[0:M, i, 1:1 + w], op0=ALU.mult, op1=ALU.add),
                            reads=[T0.r, mul.r, rw_Pb.r], writes=[rw_Pb.r])
                    fw.op("pool", lambda e: e.tensor_copy(out=rw_Pb.t[:, :, 0], in_=rw_car.t[:, :]),
                          reads=[rw_car.r, rw_Pb.r], writes=[rw_Pb.r])
                    XS = lambda i, M=128: rw_Pb.t[0:M, i, 1:1 + w]
                    fw.op("act", lambda e: e.activation(out=rw_TW.t[:, 0:w], in_=XS(6, 32), func=AF.Tanh),
                          reads=[rw_Pb.r], writes=[rw_TW.r])
                    fw.op("act", lambda e: e.activation(out=rw_AL.t[:, 0:w], in_=XS(7, 32), func=AF.Copy),
                          reads=[rw_Pb.r], writes=[rw_AL.r])
                    fw.op("act", lambda e: e.activation(out=T0.t[0:64, 0:w], in_=XS(8, 64), func=AF.Exp, scale=-1.0),
                          reads=[rw_Pb.r], writes=[T0.r])
                    fw.op("dve", lambda e: e.tensor_scalar_add(out=T0.t[0:64, 0:w], in0=T0.t[0:64, 0:w], scalar1=1.0),
                          reads=[T0.r], writes=[T0.r])
                    fw.op("dve", lambda e: e.reciprocal(out=T0.t[0:64, 0:w], in_=T0.t[0:64, 0:w]), reads=[T0.r], writes=[T0.r])
                    fw.op("act", lambda e: e.activation(out=rw_SG.t[:, 0:w], in_=T0.t[0:64, 0:w], func=AF.Copy),
                          reads=[T0.r], writes=[rw_SG.r])
                    for pc in range(2):
                        cs = slice(pc * 128, (pc + 1) * 128)
                        r_ap, k_ap, v_ap = XS(pc), XS(2 + pc), XS(4 + pc)
                        pw_ = next_bank()
                        fw.op("pe", lambda e, pw_=pw_, cs=cs: e.matmul(pw_.t[:, 0:w], lhsT=w2b.t[:, l, cs], rhs=rw_TW.t[:, 0:w],
                                                                    start=True, stop=True), reads=[w2b.r, rw_TW.r], writes=[pw_.r])
                        fw.op("act", lambda e, pw_=pw_, pc=pc: e.activation(out=rw_SIG.t[:, 0:w], in_=pw_.t[:, 0:w], func=AF.Exp,
                                                                         scale=-1.0, bias=nw0.t[:, l, pc:pc + 1]),
                              reads=[pw_.r, nw0.r], writes=[rw_SIG.r])
                        fw.op("dve", lambda e: e.tensor_scalar_add(out=rw_SIG.t[:, 0:w], in0=rw_SIG.t[:, 0:w], scalar1=1.0),
                              reads=[rw_SIG.r], writes=[rw_SIG.r])
                        fw.op("dve", lambda e: e.reciprocal(out=rw_SIG.t[:, 0:w], in_=rw_SIG.t[:, 0:w]),
                              reads=[rw_SIG.r], writes=[rw_SIG.r])
                        pa_ = next_bank()
                        fw.op("pe", lambda e, pa_=pa_, cs=cs: e.matmul(pa_.t[:, 0:w], lhsT=a2b.t[:, l, cs], rhs=rw_AL.t[:, 0:w],
                                                                    start=True, stop=True), reads=[a2b.r, rw_AL.r], writes=[pa_.r])
                        fw.op("act", lambda e, pa_=pa_, pc=pc: e.activation(out=rw_A.t[:, 0:w], in_=pa_.t[:, 0:w], func=AF.Exp,
                                                                         scale=-1.0, bias=na0.t[:, l, pc:pc + 1]),
                              reads=[pa_.r, na0.r], writes=[rw_A.r])
                        fw.op("dve", lambda e: e.tensor_scalar_add(out=rw_A.t[:, 0:w], in0=rw_A.t[:, 0:w], scalar1=1.0),
                              reads=[rw_A.r], writes=[rw_A.r])
                        fw.op("dve", lambda e: e.reciprocal(out=rw_A.t[:, 0:w], in_=rw_A.t[:, 0:w]), reads=[rw_A.r], writes=[rw_A.r])
                        pg_ = next_bank()
                        fw.op("pe", lambda e, pg_=pg_, cs=cs: e.matmul(pg_.t[:, 0:w], lhsT=g2b.t[:, l, cs], rhs=rw_SG.t[:, 0:w],
                                                                    start=True, stop=True), reads=[g2b.r, rw_SG.r], writes=[pg_.r])
                        fw.op("act", lambda e, pg_=pg_, pc=pc: e.activation(out=rw_G[pc].t[:, 0:w], in_=pg_.t[:, 0:w], func=AF.Copy),
                              reads=[pg_.r], writes=[rw_G[pc].r])
                        fw.op("dve", lambda e, pc=pc, k_ap=k_ap: e.tensor_scalar_mul(
                            out=rw_KN.t[:, 0:w], in0=k_ap, scalar1=rwp["rw_k_k"].t[:, l, pc:pc + 1]),
                            reads=[rw_Pb.r, rwp["rw_k_k"].r], writes=[rw_KN.r])
                        fw.op("act", lambda e: e.activation(out=rw_sqb.t[:, 0:w], in_=rw_KN.t[:, 0:w], func=AF.Square),
                              reads=[rw_KN.r], writes=[rw_sqb.r])
                        pn = next_bank()
                        fw.op("pe", lambda e, pn=pn: e.matmul(pn.t[:, 0:w], lhsT=onesbd.t[:], rhs=rw_sqb.t[:, 0:w],
                                                              start=True, stop=True), reads=[onesbd.r, rw_sqb.r], writes=[pn.r])
                        fw.op("dve", lambda e, pn=pn: e.tensor_scalar_max(out=T1.t[:, 0:w], in0=pn.t[:, 0:w], scalar1=1e-24),
                              reads=[pn.r], writes=[T1.r])
                        fw.op("act", lambda e: e.activation(out=T1.t[:, 0:w], in_=T1.t[:, 0:w], func=AF.Ln), reads=[T1.r], writes=[T1.r])
                        fw.op("act", lambda e: e.activation(out=T1.t[:, 0:w], in_=T1.t[:, 0:w], func=AF.Exp, scale=-0.5),
                              reads=[T1.r], writes=[T1.r])
                        fw.op("dve", lambda e: e.tensor_tensor(out=rw_KN.t[:, 0:w], in0=rw_KN.t[:, 0:w], in1=T1.t[:, 0:w], op=ALU.mult),
                              reads=[rw_KN.r, T1.r], writes=[rw_KN.r])
                        fw.op("dve", lambda e, pc=pc: e.tensor_scalar(
                            out=T1.t[:, 0:w], in0=rw_A.t[:, 0:w], scalar1=rwp["rw_k_a"].t[:, l, pc:pc + 1],
                            scalar2=omka.t[:, l, pc:pc + 1], op0=ALU.mult, op1=ALU.add),
                            reads=[rw_A.r, rwp["rw_k_a"].r, omka.r], writes=[T1.r])
                        fw.op("dve", lambda e, k_ap=k_ap: e.tensor_tensor(out=rw_KP.t[:, 0:w], in0=k_ap, in1=T1.t[:, 0:w], op=ALU.mult),
                              reads=[rw_Pb.r, T1.r], writes=[rw_KP.r])
                        fw.op("dve", lambda e: e.tensor_tensor(out=rw_Bf.t[:, 0:w], in0=rw_KN.t[:, 0:w], in1=rw_A.t[:, 0:w], op=ALU.mult),
                              reads=[rw_KN.r, rw_A.r], writes=[rw_Bf.r])
                        fw.op("dve", lambda e: e.tensor_tensor_scan(out=rw_L.t[:, 0:w], data0=rmask.t[:, 0:w], data1=rw_SIG.t[:, 0:w],
                                                                   initial=0.0, op0=ALU.mult, op1=ALU.add),
                              reads=[rw_SIG.r, rmask.r], writes=[rw_L.r])
                        Lv = rw_L.t[:, 0:w].rearrange("p (c t) -> p c t", t=C)
                        fw.op("act", lambda e: e.activation(out=T2.t[:, 0:w], in_=rw_L.t[:, 0:w], func=AF.Exp, scale=-C0),
                              reads=[rw_L.r], writes=[T2.r])
                        fw.op("dve", lambda e: e.tensor_tensor(out=T3.t[:, 0:w], in0=rw_L.t[:, 0:w], in1=rw_SIG.t[:, 0:w], op=ALU.subtract),
                              reads=[rw_L.r, rw_SIG.r], writes=[T3.r])
                        fw.op("act", lambda e: e.activation(out=T3.t[:, 0:w], in_=T3.t[:, 0:w], func=AF.Exp, scale=-C0),
                              reads=[T3.r], writes=[T3.r])
                        for par in range(2):
                            hm = onesbdf.t[:, par * 64:par * 64 + 1]
                            fw.op("dve", lambda e, pc=pc, par=par, hm=hm, r_ap=r_ap: e.scalar_tensor_tensor(
                                out=rw_QRm.t[:, par, pc, 0:nch, 1, 0:C], in0=r_ap.rearrange("p (c t) -> p c t", t=C), scalar=hm,
                                in1=T2.t[:, 0:w].rearrange("p (c t) -> p c t", t=C), op0=ALU.mult, op1=ALU.mult),
                                reads=[rw_Pb.r, T2.r, onesbdf.r], writes=[rw_QRm.rs[pc]])
                            fw.op("dve", lambda e, pc=pc, par=par, hm=hm: e.scalar_tensor_tensor(
                                out=rw_QRm.t[:, par, pc, 0:nch, 0, 0:C], in0=rw_KN.t[:, 0:w].rearrange("p (c t) -> p c t", t=C),
                                scalar=hm, in1=T3.t[:, 0:w].rearrange("p (c t) -> p c t", t=C), op0=ALU.mult, op1=ALU.mult),
                                reads=[rw_KN.r, T3.r, onesbdf.r], writes=[rw_QRm.rs[pc]])
                        fw.op("act", lambda e: e.activation(out=T2.t[:, 0:w], in_=rw_L.t[:, 0:w], func=AF.Exp, scale=C0),
                              reads=[rw_L.r], writes=[T2.r])
                        fw.op("dve", lambda e, pc=pc: e.tensor_tensor(out=rw_Kh.t[:, pc, 0:w], in0=rw_KP.t[:, 0:w], in1=T2.t[:, 0:w], op=ALU.mult),
                              reads=[rw_KP.r, T2.r], writes=[rw_Kh.rs[pc]])
                        for par in range(2):
                            hm = onesbdf.t[:, par * 64:par * 64 + 1]
                            fw.op("dve", lambda e, pc=pc, par=par, hm=hm: e.scalar_tensor_tensor(
                                out=rw_Bm.t[:, par, pc, 0:w], in0=rw_Bf.t[:, 0:w], scalar=hm, in1=T2.t[:, 0:w],
                                op0=ALU.mult, op1=ALU.mult), reads=[rw_Bf.r, T2.r, onesbdf.r], writes=[rw_Bm.rs[pc]])
                        fw.op("dve", lambda e, Lv=Lv: e.tensor_tensor(
                            out=T3.t[:, 0:w].rearrange("p (c t) -> p c t", t=C), in0=Lv[:, :, C - 1:C].to_broadcast([128, nch, C]),
                            in1=Lv, op=ALU.subtract), reads=[rw_L.r], writes=[T3.r])
                        fw.op("act", lambda e: e.activation(out=T3.t[:, 0:w], in_=T3.t[:, 0:w], func=AF.Exp, scale=-C0),
                              reads=[T3.r], writes=[T3.r])
                        fw.op("dve", lambda e, pc=pc: e.tensor_tensor(out=rw_Ke.t[:, pc, 0:w], in0=rw_KP.t[:, 0:w], in1=T3.t[:, 0:w], op=ALU.mult),
                              reads=[rw_KP.r, T3.r], writes=[rw_Ke.rs[pc]])
                        fw.op("pool", lambda e, pc=pc: e.tensor_tensor(out=rw_Be.t[:, pc, 0:w], in0=rw_Bf.t[:, 0:w], in1=T3.t[:, 0:w], op=ALU.mult),
                              reads=[rw_Bf.r, T3.r], writes=[rw_Be.rs[pc]])
                        fw.op("act", lambda e, pc=pc, Lv=Lv: e.activation(out=rw_gC.t[:, pc, 0:nch], in_=Lv[:, :, C - 1], func=AF.Exp,
                                                                       scale=-C0), reads=[rw_L.r], writes=[rw_gC.rs[pc]])
                        fw.op("act", lambda e, pc=pc, v_ap=v_ap: e.activation(out=rw_Vb.t[:, pc, 0:w], in_=v_ap, func=AF.Copy),
                              reads=[rw_Pb.r], writes=[rw_Vb.rs[pc]])
                        fw.op("dve", lambda e, pc=pc, r_ap=r_ap: e.scalar_tensor_tensor(
                            out=(T4 if pc == 0 else T5).t[:, 0:w], in0=r_ap, scalar=rwp["rw_r_k"].t[:, l, pc:pc + 1],
                            in1=rw_KP.t[:, 0:w], op0=ALU.mult, op1=ALU.mult),
                            reads=[rw_Pb.r, rwp["rw_r_k"].r, rw_KP.r], writes=[(T4 if pc == 0 else T5).r])
                    RWDBG = int(os.environ.get("RWDBG", "9"))
                    if RWDBG < 2:
                        return
                    for (src, dstT) in ((rw_Vb, rw_Vt), (rw_Ke, rw_KeT), (rw_Be, rw_BeT)):
                        for c0 in range(0, nch, 4):
                            pT = next_bank()
                            pTb = pT.t[:, :].bitcast(BF16)
                            ncc = min(4, nch - c0)
                            for ci in range(ncc):
                                c = c0 + ci
                                for pc in range(2):
                                    fw.op("pe", lambda e, src=src, c=c, ci=ci, pc=pc, pTb=pTb: e.transpose(
                                        pTb[0:C, ci * 256 + pc * 128:ci * 256 + (pc + 1) * 128], src.t[:, pc, c * C:(c + 1) * C],
                                        identb.t[:, :]), reads=[src.rs[pc], identb.r], writes=[pT.r],
                                        signal=(ci == ncc - 1 and pc == 1))
                            fw.op("act", lambda e, dstT=dstT, c0=c0, ncc=ncc, pTb=pTb: e.activation(
                                out=dstT.t[0:C, c0:c0 + ncc, :], in_=pTb[0:C, 0:ncc * 256].rearrange("p (c n) -> p c n", n=256),
                                func=AF.Copy), reads=[pT.r], writes=[dstT.r])
                    if RWDBG < 3:
                        return
                    py = [banks[6], banks[7]]
                    for c in range(nch):
                        p12, p34, p5 = next_bank(), next_bank(), next_bank()
                        for h in range(4):
                            par, pc = h % 2, h // 2
                            for a_ in range(2):
                                qr = rw_QRm.t[:, par, pc, c, a_, 0:C]
                                fw.op("pe", lambda e, c=c, h=h, pc=pc, qr=qr, p12=p12, a_=a_: e.matmul(
                                    p12.t[0:C, h * 128 + a_ * 64:h * 128 + a_ * 64 + C], lhsT=rw_Kh.t[:, pc, c * C:(c + 1) * C],
                                    rhs=qr, start=True, stop=True), reads=[rw_Kh.rs[pc], rw_QRm.rs[pc]], writes=[p12.r],
                                    signal=(h == 3 and a_ == 1))
                                fw.op("pe", lambda e, c=c, h=h, par=par, pc=pc, qr=qr, p34=p34, a_=a_: e.matmul(
                                    p34.t[0:C, h * 128 + a_ * 64:h * 128 + a_ * 64 + C],
                                    lhsT=rw_Bm.t[:, par, pc, c * C:(c + 1) * C], rhs=qr, start=True, stop=True),
                                    reads=[rw_Bm.rs[pc], rw_QRm.rs[pc]], writes=[p34.r], signal=(h == 3 and a_ == 1))
                            fw.op("pe", lambda e, h=h, par=par, pc=pc, c=c, p5=p5: e.matmul(
                                p5.t[0:C, h * 64:h * 64 + C], lhsT=rw_QRm.t[:, par, pc, c, 0, 0:C],
                                rhs=rw_Bm.t[:, par, pc, c * C:(c + 1) * C], start=True, stop=True),
                                reads=[rw_Bm.rs[pc], rw_QRm.rs[pc]], writes=[p5.r], signal=(h == 3))
                        v4 = lambda ap: ap.rearrange("p (h a t) -> p h a t", h=4, a=2)[:, :, :, 0:C]
                        fw.op("dve", lambda e, c=c, p12=p12: e.tensor_tensor(
                            out=rw_AT12.t[0:C, :, :, 0:C], in0=v4(p12.t[0:C, :]),
                            in1=mask12.t[0:C, :, 0:C].unsqueeze(1).to_broadcast([C, 4, 2, C]), op=ALU.mult),
                            reads=[p12.r, mask12.r], writes=[rw_AT12.r])
                        fw.op("dve", lambda e, c=c, p34=p34: e.tensor_tensor(
                            out=rw_AT34.t[0:C, :, :, 0:C], in0=v4(p34.t[0:C, :]),
                            in1=mask34.t[0:C, :, 0:C].unsqueeze(1).to_broadcast([C, 4, 2, C]), op=ALU.mult),
                            reads=[p34.r, mask34.r], writes=[rw_AT34.r])
                        X, XT, TT = rw_X[0], rw_XT[0], rw_TT[0]
                        fw.op("dve", lambda e, c=c, p5=p5, X=X: e.tensor_tensor(
                            out=X.t[0:C, :, 0:C], in0=p5.t[0:C, 0:256].rearrange("p (h t) -> p h t", h=4)[:, :, 0:C],
                            in1=mask5.t[0:C, 0:C].unsqueeze(1).to_broadcast([C, 4, C]), op=ALU.mult),
                            reads=[p5.r, mask5.r], writes=[X.r])
                        fw.op("act", lambda e, c=c, XT=XT: e.activation(out=XT.t[0:C, :, 0:C], in_=rw_AT34.t[0:C, :, 0, 0:C], func=AF.Copy),
                              reads=[rw_AT34.r], writes=[XT.r])
                        fw.op("dve", lambda e, c=c, TT=TT: e.tensor_tensor(
                            out=TT.t[0:C, :, 0:C], in0=rw_AT34.t[0:C, :, 0, 0:C],
                            in1=ident.t[0:C, 0:C].unsqueeze(1).to_broadcast([C, 4, C]), op=ALU.add),
                            reads=[rw_AT34.r, ident.r], writes=[TT.r])
                        cur = 0
                        for lev in range(nlev):
                            Xn, XTn, TTn = rw_X[1 - cur], rw_XT[1 - cur], rw_TT[1 - cur]
                            if lev == nlev - 1:
                                TTn = rw_TTc[c]
                            X, XT, TT = rw_X[cur], rw_XT[cur], rw_TT[cur]
                            px, pxt, ptt = next_bank(), next_bank(), next_bank()
                            for h in range(4):
                                fw.op("pe", lambda e, c=c, h=h, px=px, X=X, XT=XT: e.matmul(
                                    px.t[0:C, h * 64:h * 64 + C], lhsT=XT.t[0:C, h, 0:C], rhs=X.t[0:C, h, 0:C], start=True, stop=True),
                                    reads=[X.r, XT.r], writes=[px.r], signal=(h == 3))
                            fw.op("act", lambda e, c=c, px=px, Xn=Xn: e.activation(
                                out=Xn.t[0:C, :, 0:C], in_=px.t[0:C, 0:256].rearrange("p (h t) -> p h t", h=4)[:, :, 0:C], func=AF.Copy),
                                reads=[px.r], writes=[Xn.r])
                            if lev < nlev - 1:
                                for h in range(4):
                                    fw.op("pe", lambda e, c=c, h=h, pxt=pxt, X=X, XT=XT: e.matmul(
                                        pxt.t[0:C, h * 64:h * 64 + C], lhsT=X.t[0:C, h, 0:C], rhs=XT.t[0:C, h, 0:C], start=True, stop=True),
                                        reads=[X.r, XT.r], writes=[pxt.r], signal=(h == 3))
                                fw.op("dve", lambda e, c=c, pxt=pxt, XTn=XTn: e.tensor_copy(
                                    out=XTn.t[0:C, :, 0:C], in_=pxt.t[0:C, 0:256].rearrange("p (h t) -> p h t", h=4)[:, :, 0:C]),
                                    reads=[pxt.r], writes=[XTn.r])
                            for h in range(4):
                                fw.op("pe", lambda e, c=c, h=h, ptt=ptt, Xn=Xn, TT=TT: e.matmul(
                                    ptt.t[0:C, h * 64:h * 64 + C], lhsT=Xn.t[0:C, h, 0:C], rhs=TT.t[0:C, h, 0:C], start=True, stop=True),
                                    reads=[Xn.r, TT.r], writes=[ptt.r], signal=(h == 3))
                            fw.op("dve", lambda e, c=c, ptt=ptt, TT=TT, TTn=TTn: e.tensor_tensor(
                                out=TTn.t[0:C, :, 0:C], in0=ptt.t[0:C, 0:256].rearrange("p (h t) -> p h t", h=4)[:, :, 0:C],
                                in1=TT.t[0:C, :, 0:C], op=ALU.add), reads=[ptt.r, TT.r], writes=[TTn.r])
                            cur = 1 - cur
                        TTf = rw_TTc[c]
                        if RWDBG < 4:
                            continue
                        pz, pu, ph = next_bank(), next_bank(), next_bank()
                        for pc in range(2):
                            for par in range(2):
                                fw.op("pe", lambda e, c=c, pc=pc, par=par, pz=pz: e.matmul(
                                    pz.t[0:C, pc * 128:(pc + 1) * 128], lhsT=rw_QRm.t[:, par, pc, c, 0, 0:C], rhs=rw_Hbd.t[:, pc, :],
                                    start=(par == 0), stop=False, skip_group_check=True),
                                    reads=[rw_QRm.rs[pc], rw_Hbd.rs[pc]], writes=[pz.r], signal=False)
                            for h in (2 * pc, 2 * pc + 1):
                                fw.op("pe", lambda e, c=c, h=h, pz=pz: e.matmul(
                                    pz.t[0:C, h * 64:(h + 1) * 64], lhsT=rw_AT12.t[0:C, h, 0, 0:C], rhs=rw_Vt.t[0:C, c, h * 64:(h + 1) * 64],
                                    start=False, stop=True, skip_group_check=True),
                                    reads=[rw_AT12.r, rw_Vt.r], writes=[pz.r], signal=(h == 3))
                        fw.op("act", lambda e, c=c, pz=pz: e.activation(out=rw_Zb.t[0:C, :], in_=pz.t[0:C, 0:256], func=AF.Copy),
                              reads=[pz.r], writes=[rw_Zb.r])
                        for h in range(4):
                            fw.op("pe", lambda e, c=c, h=h, pu=pu, TTf=TTf: e.matmul(
                                pu.t[0:C, h * 64:(h + 1) * 64], lhsT=TTf.t[0:C, h, 0:C], rhs=rw_Zb.t[0:C, h * 64:(h + 1) * 64],
                                start=True, stop=True), reads=[TTf.r, rw_Zb.r], writes=[pu.r], signal=(h == 3))
                        fw.op("dve", lambda e, c=c, pu=pu: e.tensor_scalar_mul(out=rw_Un.t[0:C, :], in0=pu.t[0:C, 0:256], scalar1=-1.0),
                              reads=[pu.r], writes=[rw_Un.r])
                        for pc in range(2):
                            for par in range(2):
                                fw.op("pe", lambda e, c=c, pc=pc, par=par: e.matmul(
                                    py[pc].t[:, c * C:(c + 1) * C], lhsT=rw_Hbd.t[:, pc, :], rhs=rw_QRm.t[:, par, pc, c, 1, 0:C],
                                    start=(par == 0), stop=False, skip_group_check=True),
                                    reads=[rw_QRm.rs[pc], rw_Hbd.rs[pc]], writes=[py[pc].r], signal=False)
                        for h in range(4):
                            pc, hs = h // 2, slice((h % 2) * 64, (h % 2) * 64 + 64)
                            fw.op("pe", lambda e, c=c, h=h, pc=pc, hs=hs: e.matmul(
                                py[pc].t[hs, c * C:(c + 1) * C], lhsT=rw_Vt.t[0:C, c, h * 64:(h + 1) * 64], rhs=rw_AT12.t[0:C, h, 1, 0:C],
                                start=False, stop=False, skip_group_check=True),
                                reads=[rw_Vt.r, rw_AT12.r], writes=[py[pc].r], signal=False)
                            fw.op("pe", lambda e, c=c, h=h, pc=pc, hs=hs: e.matmul(
                                py[pc].t[hs, c * C:(c + 1) * C], lhsT=rw_Un.t[0:C, h * 64:(h + 1) * 64], rhs=rw_AT34.t[0:C, h, 1, 0:C],
                                start=False, stop=True, skip_group_check=True),
                                reads=[rw_Un.r, rw_AT34.r], writes=[py[pc].r], signal=(h % 2 == 1))
                        for pc in range(2):
                            cs = slice(pc * 128, (pc + 1) * 128)
                            fw.op("pe", lambda e, c=c, pc=pc, cs=cs, ph=ph: e.matmul(
                                ph.t[:, cs], lhsT=rw_KeT.t[0:C, c, cs], rhs=rw_Vt.t[0:C, c, cs], start=True, stop=False),
                                reads=[rw_KeT.r, rw_Vt.r], writes=[ph.r], signal=False)
                            fw.op("pe", lambda e, c=c, pc=pc, cs=cs, ph=ph: e.matmul(
                                ph.t[:, cs], lhsT=rw_BeT.t[0:C, c, cs], rhs=rw_Un.t[0:C, cs], start=False, stop=True),
                                reads=[rw_BeT.r, rw_Un.r], writes=[ph.r])
                            for hh in range(2):
                                hr = slice(hh * 64, hh * 64 + 64)
                                fw.op("dve", lambda e, c=c, pc=pc, hr=hr, hh=hh, ph=ph: e.scalar_tensor_tensor(
                                    out=rw_Hm.t[hr, pc, hr], in0=rw_Hm.t[hr, pc, hr], scalar=rw_gC.t[hr, pc, c:c + 1],
                                    in1=ph.t[hr, pc * 128 + hh * 64:pc * 128 + hh * 64 + 64], op0=ALU.mult, op1=ALU.add),
                                    reads=[rw_Hm.rs[pc], rw_gC.rs[pc], ph.r], writes=[rw_Hm.rs[pc]])
                            fw.op("act", lambda e, c=c, pc=pc: e.activation(out=rw_Hbd.t[:, pc, :], in_=rw_Hm.t[:, pc, :], func=AF.Copy),
                                  reads=[rw_Hm.rs[pc]], writes=[rw_Hbd.rs[pc]])
                    if RWDBG < 5:
                        return
                    for pc in range(2):
                        v_ap = XS(4 + pc)
                        TB = T4 if pc == 0 else T5
                        fw.op("act", lambda e, pc=pc: e.activation(out=rw_Yf.t[:, 0:w], in_=py[pc].t[:, 0:w], func=AF.Copy),
                              reads=[py[pc].r], writes=[rw_Yf.r])
                        fw.op("act", lambda e: e.activation(out=rw_sqb.t[:, 0:w], in_=rw_Yf.t[:, 0:w], func=AF.Copy),
                              reads=[rw_Yf.r], writes=[rw_sqb.r])
                        pm = next_bank()
                        fw.op("pe", lambda e, pm=pm: e.matmul(pm.t[:, 0:w], lhsT=onesbd.t[:], rhs=rw_sqb.t[:, 0:w], start=True, stop=True),
                              reads=[onesbd.r, rw_sqb.r], writes=[pm.r])
                        fw.op("dve", lambda e, pm=pm: e.scalar_tensor_tensor(
                            out=rw_Yf.t[:, 0:w], in0=pm.t[:, 0:w], scalar=-1.0 / 64, in1=rw_Yf.t[:, 0:w], op0=ALU.mult, op1=ALU.add),
                            reads=[pm.r, rw_Yf.r], writes=[rw_Yf.r])
                        fw.op("act", lambda e: e.activation(out=rw_sqb.t[:, 0:w], in_=rw_Yf.t[:, 0:w], func=AF.Square),
                              reads=[rw_Yf.r], writes=[rw_sqb.r])
                        pv = next_bank()
                        fw.op("pe", lambda e, pv=pv: e.matmul(pv.t[:, 0:w], lhsT=onesbd.t[:], rhs=rw_sqb.t[:, 0:w], start=True, stop=True),
                              reads=[onesbd.r, rw_sqb.r], writes=[pv.r])
                        fw.op("act", lambda e, pv=pv: e.activation(out=T0.t[:, 0:w], in_=pv.t[:, 0:w], func=AF.Ln, scale=1.0 / 64,
                                                                   bias=eps2.t[:]), reads=[pv.r, eps2.r], writes=[T0.r])
                        fw.op("act", lambda e: e.activation(out=T0.t[:, 0:w], in_=T0.t[:, 0:w], func=AF.Exp, scale=-0.5),
                              reads=[T0.r], writes=[T0.r])
                        fw.op("dve", lambda e: e.tensor_tensor(out=rw_Yf.t[:, 0:w], in0=rw_Yf.t[:, 0:w], in1=T0.t[:, 0:w], op=ALU.mult),
                              reads=[rw_Yf.r, T0.r], writes=[rw_Yf.r])
                        fw.op("dve", lambda e, pc=pc: e.tensor_scalar(
                            out=rw_Yf.t[:, 0:w], in0=rw_Yf.t[:, 0:w], scalar1=rwp["rw_ln_w"].t[:, l, pc:pc + 1],
                            scalar2=rwp["rw_ln_b"].t[:, l, pc:pc + 1], op0=ALU.mult, op1=ALU.add),
                            reads=[rw_Yf.r, rwp["rw_ln_w"].r, rwp["rw_ln_b"].r], writes=[rw_Yf.r])
                        fw.op("act", lambda e, TB=TB: e.activation(out=rw_sqb.t[:, 0:w], in_=TB.t[:, 0:w], func=AF.Copy),
                              reads=[TB.r], writes=[rw_sqb.r])
                        pbn = next_bank()
                        fw.op("pe", lambda e, pbn=pbn: e.matmul(pbn.t[:, 0:w], lhsT=onesbd.t[:], rhs=rw_sqb.t[:, 0:w], start=True, stop=True),
                              reads=[onesbd.r, rw_sqb.r], writes=[pbn.r])
                        fw.op("dve", lambda e, pbn=pbn, v_ap=v_ap: e.tensor_tensor(out=T0.t[:, 0:w], in0=pbn.t[:, 0:w], in1=v_ap, op=ALU.mult),
                              reads=[pbn.r, rw_Pb.r], writes=[T0.r])
                        fw.op("dve", lambda e: e.tensor_tensor(out=rw_Yf.t[:, 0:w], in0=rw_Yf.t[:, 0:w], in1=T0.t[:, 0:w], op=ALU.add),
                              reads=[rw_Yf.r, T0.r], writes=[rw_Yf.r])
                        fw.op("dve", lambda e, pc=pc: e.tensor_tensor(out=ymix.t[:, pc, 0:w], in0=rw_Yf.t[:, 0:w], in1=rw_G[pc].t[:, 0:w],
                                                                      op=ALU.mult), reads=[rw_Yf.r, rw_G[pc].r], writes=[ymix.rs[pc]])

                RWKV_TILE = rw_tile

                it = 0
                for s in range(3):
                    hg_init(s)
                    if stage >= 4:
                        rw_init(s)

                    def a2_tile(s, t0, w, j, xT, l=l):
                        load_xT(xT, t0, w)
                        fw.dma("pool", ymix.t[:, 2:6, 0:w], yfox.rearrange("(c p) t -> p c t", p=128)[:, :, t0:t0 + w],
                               reads=[yfreg(t0)], writes=ymix.rs[2:6], stream="yfld")
                        norm_fm(xT, w, sq, tmp, lnv, rstd, hT,
                                lambda c, s=s: G1.t[:, l, s, c:c + 1], lambda c, s=s: mod.t[:, l, 0, s, c:c + 1], hT.rs)
                        if stage >= 3:
                            hg_tile(s, w)
                        if stage >= 4 and RWKV_TILE is not None:
                            RWKV_TILE(s, w)
                        for c in range(8):
                            po_ = next_bank()
                            for kc in range(8):
                                fw.op("pe", lambda e, c=c, kc=kc, po_=po_: e.matmul(
                                    po_.t[:, 0:w], lhsT=wo.t[:, kc, c * 128:(c + 1) * 128], rhs=ymix.t[:, kc, 0:w],
                                    start=(kc == 0), stop=(kc == 7)),
                                    reads=[wo.r, ymix.rs[kc]], writes=[po_.r], signal=(kc == 7))
                            fw.op("dve", lambda e, c=c, po_=po_: e.scalar_tensor_tensor(
                                out=xT.t[:, c, 0:w], in0=po_.t[:, 0:w], scalar=mod.t[:, l, 2, s, c:c + 1],
                                in1=xT.t[:, c, 0:w], op0=ALU.mult, op1=ALU.add),
                                reads=[po_.r, mod.r, xT.rs[c]], writes=[xT.rs[c]])
                        store_xT(xT, t0, w)
                    for (t0, w, j) in tiles_of(s, WA):
                        xT = xTa[it % 2]
                        it += 1
                        a2_tile(s, t0, w, j, xT)
                    if stage >= 3:
                        hg_final(s)
                    if stage >= 4:
                        rw_final(s, w)
                fw.flush()
                default_pool[0] = (0, 1, 2, 3, 4, 5, 6, 7)
            with contextlib.ExitStack() as ph:
                sub = FWScope(fw, ph)
                WB = 256
                wfi = Tt(sub.sbuf("wfi", [128, 8, 2 * DFF], BF16), name="wfi")
                wfo = Tt(sub.sbuf("wfo", [128, 22, D], BF16), name="wfo")
                for kc in range(8):
                    fw.dma("pool", wfi.t[:, kc, :], I["w_ffn_in"][l][kc * 128:(kc + 1) * 128, :], writes=[wfi.r],
                           stream="wld", group=True)
                for kc in range(22):
                    fw.dma("pool", wfo.t[:, kc, :], I["w_ffn_out"][l][kc * 128:(kc + 1) * 128, :], writes=[wfo.r],
                           stream="wld", group=True)
                xTb = [Tt(sub.sbuf("xTb%d" % i, [128, 8, WB], F32), nreg=8, name="xTb%d" % i) for i in range(2)]
                sq = Tt(sub.sbuf("sqb", [128, 8, WB], BF16), name="sqb")
                hT = Tt(sub.sbuf("hTb", [128, 8, WB], BF16), nreg=8, name="hTb")
                tmp = [Tt(sub.sbuf("tmpb%d" % i, [128, WB], F32), name="tmpb%d" % i) for i in range(2)]
                lnv = Tt(sub.sbuf("lnvb", [128, WB], F32), name="lnvb")
                rstd = Tt(sub.sbuf("rstdb", [128, WB], F32), name="rstdb")
                actT = Tt(sub.sbuf("actT", [128, 22, WB], BF16), nreg=22, name="actT")
                sg = [Tt(sub.sbuf("sg%d" % i, [128, WB], F32), name="sg%d" % i) for i in range(2)]
                it = 0
                for s in range(3):
                    for (t0, w, j) in tiles_of(s, WB):
                        xT = xTb[it % 2]
                        it += 1
                        load_xT(xT, t0, w, q="sp")
                        norm_fm(xT, w, sq, tmp, lnv, rstd, hT,
                                lambda c, s=s: G2.t[:, l, s, c:c + 1], lambda c, s=s: mod.t[:, l, 3, s, c:c + 1], hT.rs)
                        for f in range(22):
                            pg = next_bank()
                            pu = next_bank()
                            for kc in range(8):
                                fw.op("pe", lambda e, kc=kc, f=f, pg=pg, w=w: e.matmul(
                                    pg.t[:, 0:w], lhsT=wfi.t[:, kc, f * 128:(f + 1) * 128], rhs=hT.t[:, kc, 0:w],
                                    start=(kc == 0), stop=(kc == 7)),
                                    reads=[wfi.r, hT.rs[kc]], writes=[pg.r], signal=(kc == 7))
                            for kc in range(8):
                                fw.op("pe", lambda e, kc=kc, f=f, pu=pu, w=w: e.matmul(
                                    pu.t[:, 0:w], lhsT=wfi.t[:, kc, DFF + f * 128:DFF + (f + 1) * 128],
                                    rhs=hT.t[:, kc, 0:w], start=(kc == 0), stop=(kc == 7)),
                                    reads=[wfi.r, hT.rs[kc]], writes=[pu.r], signal=(kc == 7))
                            sgt = sg[f % 2]
                            fw.op("act", lambda e, pg=pg, sgt=sgt, w=w: e.activation(
                                out=sgt.t[:, 0:w], in_=pg.t[:, 0:w], func=AF.Silu), reads=[pg.r], writes=[sgt.r])
                            fw.op("dve", lambda e, pu=pu, sgt=sgt, f=f, w=w: e.tensor_tensor(
                                out=actT.t[:, f, 0:w], in0=sgt.t[:, 0:w], in1=pu.t[:, 0:w], op=ALU.mult),
                                reads=[sgt.r, pu.r], writes=[actT.rs[f]])
                        for c in range(8):
                            po = next_bank()
                            for f in range(22):
                                fw.op("pe", lambda e, c=c, f=f, po=po, w=w: e.matmul(
                                    po.t[:, 0:w], lhsT=wfo.t[:, f, c * 128:(c + 1) * 128], rhs=actT.t[:, f, 0:w],
                                    start=(f == 0), stop=(f == 21)),
                                    reads=[wfo.r, actT.rs[f]], writes=[po.r], signal=(f == 21))
                            fw.op("dve", lambda e, c=c, po=po, xT=xT, s=s, w=w: e.scalar_tensor_tensor(
                                out=xT.t[:, c, 0:w], in0=po.t[:, 0:w], scalar=mod.t[:, l, 5, s, c:c + 1],
                                in1=xT.t[:, c, 0:w], op0=ALU.mult, op1=ALU.add),
                                reads=[po.r, mod.r, xT.rs[c]], writes=[xT.rs[c]])
                        store_xT(xT, t0, w, q="sp")
                fw.flush()

        with contextlib.ExitStack() as ph:
            sub = FWScope(fw, ph)
            WE = 512
            xTe = [Tt(sub.sbuf("xTe%d" % i, [128, 8, WE], F32), nreg=8, name="xTe%d" % i) for i in range(2)]
            sq = Tt(sub.sbuf("sqe", [128, 8, WE], BF16), name="sqe")
            yT = Tt(sub.sbuf("yTe", [128, 8, WE], F32), nreg=8, name="yTe")
            tmp = [Tt(sub.sbuf("tmpe%d" % i, [128, WE], F32), name="tmpe%d" % i) for i in range(2)]
            lnv = Tt(sub.sbuf("lnve", [128, WE], F32), name="lnve")
            rstd = Tt(sub.sbuf("rstde", [128, WE], F32), name="rstde")
            ytok = [Tt(sub.sbuf("ytok%d" % i, [128, D], F32), name="ytok%d" % i) for i in range(2)]
            it = 0
            ik = 0
            for s in range(3):
                dst = O["yp"][s] if s < 2 else O["ys"]
                off, T = seqs[s]
                for (t0, w, j) in tiles_of(s, WE):
                    xT = xTe[it % 2]
                    it += 1
                    load_xT(xT, t0, w)
                    norm_fm(xT, w, sq, tmp, lnv, rstd, yT, lambda c: fng.t[:, c:c + 1], None, yT.rs)
                    nb = (w + 127) // 128
                    pw = min(w, 128)
                    for tb in range(nb):
                        yt = ytok[ik % 2]
                        ik += 1
                        for half in range(2):
                            pb = next_bank()
                            for cc in range(4):
                                c = half * 4 + cc
                                fw.op("pe", lambda e, c=c, cc=cc, tb=tb, pb=pb, pw=pw: e.transpose(
                                    pb.t[0:pw, cc * 128:(cc + 1) * 128], yT.t[:, c, tb * 128:tb * 128 + pw],
                                    ident.t[:, :]),
                                    reads=[yT.rs[c], ident.r], writes=[pb.r], signal=(cc == 3))
                            if half == 0:
                                fw.op("act", lambda e, pb=pb, yt=yt, pw=pw: e.activation(
                                    out=yt.t[0:pw, 0:512], in_=pb.t[0:pw, :], func=AF.Copy), reads=[pb.r], writes=[yt.r])
                            else:
                                fw.op("dve", lambda e, pb=pb, yt=yt, pw=pw: e.tensor_copy(
                                    out=yt.t[0:pw, 512:1024], in_=pb.t[0:pw, :]), reads=[pb.r], writes=[yt.r])
                        lt0 = t0 - off + tb * 128
                        fw.dma("sp" if ik % 2 else "pool", dst[lt0:lt0 + pw, :], yt.t[0:pw, :], reads=[yt.r],
                               stream="yout")
            fw.flush()
        fw.finish()
    return nc


class FWScope:
    ctr = 0

    def __init__(self, fw, stack):
        self.fw = fw
        self.stack = stack

    def sbuf(self, name, shape, dt):
        FWScope.ctr += 1
        return self.stack.enter_context(self.fw.nc.sbuf_tensor("%s_u%d" % (name, FWScope.ctr), list(shape), dt))


def layer_mixer(fw, nc, I, O, l, L, SEQ, TS, PAST, env):
    pass


_PROG_CACHE = {}


def _get_prog(SEQ, DEPTH, TS, PAST, stage=9):
    key = (SEQ, DEPTH, TS, PAST, stage)
    if key not in _PROG_CACHE:
        _PROG_CACHE[key] = build_program(SEQ, DEPTH, TS, PAST, stage)
    return _PROG_CACHE[key]


def make_in_maps(inp, ncores, L):
    f = lambda a: np.ascontiguousarray(np.asarray(a, dtype=np.float32))
    maps = []
    shared = {k: f(inp[k]) for k in ("norm1_g", "w_ada", "b_ada", "w_in", "rw_mu", "rw_w0", "rw_w2", "rw_a0",
                                     "rw_a2", "rw_g2", "rw_k_k", "rw_k_a", "rw_ln_w", "rw_ln_b", "fox_b_f",
                                     "hg_lb_logits", "hg_norm_g", "w_out", "norm2_g", "w_ffn_in", "w_ffn_out",
                                     "final_norm_g")}
    shared["rw_r_k"] = f(inp["rw_r_k"]).reshape(L, 256)
    xp, xs = f(inp["x_prompt"]), f(inp["x_sample"])
    cp, cs = f(inp["c_prompt"]), f(inp["c_sample"])
    ck, cv, cl = f(inp["cache_fox_k"]), f(inp["cache_fox_v"]), f(inp["cache_fox_logf"])
    srw, ssh, shg = f(inp["state_rwkv"]), f(inp["state_rwkv_shift"]), f(inp["state_hgrn"])
    P = ck.shape[2]
    for i in range(ncores):
        m = dict(shared)
        m["xp"] = f(xp[2 * i:2 * i + 2])
        m["xs"] = f(xs[i])
        m["cc"] = f(np.concatenate([cp[2 * i:2 * i + 2], cs[i:i + 1]], axis=0))
        m["ck"] = f(ck[:, i].reshape(L, P, 512))
        m["cv"] = f(cv[:, i].reshape(L, P, 512))
        m["cl"] = f(cl[:, i])
        m["srw"] = f(srw[:, i])
        m["ssh"] = f(ssh[:, i, 0])
        m["shg"] = f(shg[:, i])
        maps.append(m)
    return maps


def gather_outputs(res, ncores, L, SEQ, TS):
    r = res
    cat = lambda k, ax: np.concatenate([r[i][k] for i in range(ncores)], axis=ax)
    stack = lambda k, ax: np.stack([r[i][k] for i in range(ncores)], axis=ax)
    yp = cat("yp", 0)
    ys = stack("ys", 0)
    fkp = cat("fkp", 1).reshape(L, 2 * ncores, SEQ, 8, 64)
    fvp = cat("fvp", 1).reshape(L, 2 * ncores, SEQ, 8, 64)
    flp = cat("flp", 1)
    rwp = cat("rwp", 1)
    rshp = cat("rshp", 1).reshape(L, 2 * ncores, 1, RW_COLS)
    hgp = cat("hgp", 1)
    fks = stack("fks", 1).reshape(L, ncores, TS, 8, 64)
    fvs = stack("fvs", 1).reshape(L, ncores, TS, 8, 64)
    fls = stack("fls", 1)
    rws = stack("rws", 1)
    rshs = stack("rshs", 1).reshape(L, ncores, 1, RW_COLS)
    hgs = stack("hgs", 1)
    return (yp, ys, fkp, fvp, flp, rwp, rshp, hgp, fks, fvs, fls, rws, rshs, hgs)


def kernel(**inputs):
    L = int(np.asarray(inputs["w_in"]).shape[0])
    SEQ = int(np.asarray(inputs["x_prompt"]).shape[1])
    TS = int(np.asarray(inputs["x_sample"]).shape[1])
    PAST = int(np.asarray(inputs["cache_fox_k"]).shape[2])
    ncores = int(np.asarray(inputs["x_sample"]).shape[0])
    nc = _get_prog(SEQ, L, TS, PAST)
    maps = make_in_maps(inputs, ncores, L)
    res = run_bass_kernel_spmd(nc, maps, core_ids=list(range(ncores)))
    return gather_outputs(res.results, ncores, L, SEQ, TS)
```

```python
import contextlib
import os
import numpy as np
import concourse.bass as bass
import concourse.mybir as mybir
from concourse.bass_utils import run_bass_kernel_spmd

F32 = mybir.dt.float32
BF16 = mybir.dt.bfloat16
AF = mybir.ActivationFunctionType
ALU = mybir.AluOpType

D = 1024
NC8 = 8
HD = 64
RW_COLS, FOX_COLS, HG_COLS = 896, 1544, 1024
IN_COLS = 3464
DFF = 2816
EPS = 1e-6


class Reg:
    __slots__ = ("name", "writers", "readers")

    def __init__(self, name=""):
        self.name = name
        self.writers = {}
        self.readers = {}


class Eng:
    def __init__(self, name, kind):
        self.name = name
        self.kind = kind
        self.ops = []
        self.sem = None
        self.count = 0
        self.waited = {}


class FW:
    def __init__(self, nc, stack):
        self.nc = nc
        self.stack = stack
        self.engs = {}
        self.dma_sems = {}
        self.dma_counts = {}
        self.group_sems = {}
        self.nsem = 0
        for name in ("pe", "act", "dve", "pool", "sp"):
            e = Eng(name, name)
            self.engs[name] = e
            if name != "sp":
                e.sem = self.new_sem("s_" + name)

    def new_sem(self, name):
        self.nsem += 1
        return self.stack.enter_context(self.nc.semaphore("%s_%d" % (name, self.nsem)))

    def sbuf(self, name, shape, dt):
        return self.stack.enter_context(self.nc.sbuf_tensor(name, list(shape), dt))

    def psum(self, name, shape, dt=F32):
        return self.stack.enter_context(self.nc.psum_tensor(name, list(shape), dt))

    def _collect(self, reads, writes):
        deps = {}

        def add(d):
            for k, (sem, val) in d.items():
                cur = deps.get(k)
                if cur is None or cur[1] < val:
                    deps[k] = (sem, val)
        for r in reads:
            add(r.writers)
        for w in writes:
            add(w.writers)
            add(w.readers)
        return deps

    def _waits(self, eng, deps, raw_keys):
        waits = []
        for k, (sem, val) in deps.items():
            if eng.sem is not None and k == id(eng.sem) and k not in raw_keys:
                continue
            if eng.waited.get(k, 0) >= val:
                continue
            eng.waited[k] = val
            st = self.group_sems.get(k)
            if st is not None:
                waits.append((sem, _Lazy(self.dma_counts, st)))
            else:
                waits.append((sem, val))
        return waits

    def op(self, engname, fn, reads=(), writes=(), signal=True):
        eng = self.engs[engname]
        reads = [r for r in reads if r is not None]
        writes = [w for w in writes if w is not None]
        deps = self._collect(reads, writes)
        raw_keys = set()
        k = id(eng.sem)
        if engname != "pe":
            raw_keys.add(k)
        for r in reads:
            if k in r.writers:
                raw_keys.add(k)
        waits = self._waits(eng, deps, raw_keys)
        sem = eng.sem
        if signal:
            eng.count += 1
            tok = (sem, eng.count)
        else:
            tok = (sem, eng.count + 1)
        for r in reads:
            r.readers[id(sem)] = tok
        for w in writes:
            w.writers = {id(sem): tok}
            w.readers = {}

        def run(e, fn=fn, waits=waits, signal=signal, sem=sem):
            for (s, v) in waits:
                e.wait_ge(s, int(v))
            ins = fn(e)
            if signal:
                ins.then_inc(sem, 1)
        eng.ops.append(run)

    def dma(self, qname, out, in_, reads=(), writes=(), stream="d", group=False, **kw):
        eng = self.engs[qname]
        reads = [r for r in reads if r is not None]
        writes = [w for w in writes if w is not None]
        deps = self._collect(reads, writes)
        stream = stream + "_" + qname
        if stream not in self.dma_sems:
            self.dma_sems[stream] = self.new_sem("dq_" + stream)
            self.dma_counts[stream] = 0
            if group:
                self.group_sems[id(self.dma_sems[stream])] = stream
        if group:
            deps.pop(id(self.dma_sems[stream]), None)
        waits = self._waits(eng, deps, set(deps.keys()))
        sem = self.dma_sems[stream]
        self.dma_counts[stream] += 16
        tok = (sem, self.dma_counts[stream])
        for r in reads:
            r.readers[id(sem)] = tok
        for w in writes:
            w.writers = {id(sem): tok}
            w.readers = {}

        def run(e, waits=waits, sem=sem, out=out, in_=in_, kw=kw):
            for (s, v) in waits:
                e.wait_ge(s, int(v))
            e.dma_start(out=out, in_=in_, **kw).then_inc(sem, 16)
        eng.ops.append(run)

    def rotate(self):
        for e in self.engs.values():
            if e.sem is not None:
                e.sem = self.new_sem("s_" + e.name)
                e.count = 0

    def barrier(self):
        toks = []
        for e in self.engs.values():
            if e.sem is not None and e.count > 0:
                toks.append((e.sem, e.count))
        for s in self.dma_sems:
            if self.dma_counts[s] > 0:
                toks.append((self.dma_sems[s], self.dma_counts[s]))
        for e in self.engs.values():
            waits = []
            for (sem, val) in toks:
                if sem is e.sem:
                    continue
                if e.waited.get(id(sem), 0) >= val:
                    continue
                e.waited[id(sem)] = val
                waits.append((sem, val))

            def run(h, waits=waits):
                for (s, v) in waits:
                    h.wait_ge(s, v)
            e.ops.append(run)

    def finish(self):
        self.flush()

    def flush(self):
        self.barrier()
        nc = self.nc
        engs = self.engs
        oplists = {k: e.ops for k, e in engs.items()}
        for e in engs.values():
            e.ops = []

        class _E:
            def __init__(self, ops):
                self.ops = ops
        self_engs = {k: _E(v) for k, v in oplists.items()}
        with nc.Block() as block:
            def mk(eng):
                def body(e):
                    for f in eng.ops:
                        f(e)
                return body
            block.tensor(mk(self_engs["pe"]))
            block.scalar(mk(self_engs["act"]))
            block.vector(mk(self_engs["dve"]))
            block.gpsimd(mk(self_engs["pool"]))
            block.sync(mk(self_engs["sp"]))


class _Lazy:
    def __init__(self, counts, stream):
        self.counts = counts
        self.stream = stream

    def __int__(self):
        return self.counts[self.stream]


class Rec:
    def __init__(self):
        self.items = []

    def op(self, *a, **k):
        self.items.append(("op", a, k))

    def dma(self, *a, **k):
        self.items.append(("dma", a, k))

    def replay_into(self, sink):
        for (kind, a, k) in self.items:
            getattr(sink, kind)(*a, **k)


def merge_recs(sink, recs):
    pos = [0] * len(recs)
    n = [len(r.items) for r in recs]
    total = sum(n)
    for _ in range(total):
        best, bf_ = -1, 2.0
        for i in range(len(recs)):
            if pos[i] < n[i]:
                f = pos[i] / n[i]
                if f < bf_:
                    best, bf_ = i, f
        kind, a, k = recs[best].items[pos[best]]
        pos[best] += 1
        getattr(sink, kind)(*a, **k)


class Tt:
    def __init__(self, t, nreg=1, name=""):
        self.t = t
        self.rs = [Reg("%s%d" % (name, i)) for i in range(nreg)]
        self.r = self.rs[0]


def build_program(SEQ, DEPTH, TS=32, PAST=2048, stage=9):
    L = DEPTH
    NTOK = 2 * SEQ + TS
    nc = bass.Bass("TRN2", target_bir_lowering=False)
    din = lambda n, s: nc.dram_tensor(n, list(s), F32, kind="ExternalInput").ap()
    dout = lambda n, s: nc.dram_tensor(n, list(s), F32, kind="ExternalOutput").ap()
    I = dict(
        xp=din("xp", (2, SEQ, D)), xs=din("xs", (TS, D)), cc=din("cc", (3, D)),
        ck=din("ck", (L, PAST, 512)), cv=din("cv", (L, PAST, 512)), cl=din("cl", (L, PAST, 8)),
        srw=din("srw", (L, 4, 64, 64)), ssh=din("ssh", (L, RW_COLS)), shg=din("shg", (L, 4, 64, 64)),
        norm1_g=din("norm1_g", (L, D)), w_ada=din("w_ada", (L, D, 6 * D)), b_ada=din("b_ada", (L, 6 * D)),
        w_in=din("w_in", (L, D, IN_COLS)), rw_mu=din("rw_mu", (L, RW_COLS)), rw_w0=din("rw_w0", (L, 256)),
        rw_w2=din("rw_w2", (L, 32, 256)), rw_a0=din("rw_a0", (L, 256)), rw_a2=din("rw_a2", (L, 32, 256)),
        rw_g2=din("rw_g2", (L, 64, 256)), rw_k_k=din("rw_k_k", (L, 256)), rw_k_a=din("rw_k_a", (L, 256)),
        rw_r_k=din("rw_r_k", (L, 256)), rw_ln_w=din("rw_ln_w", (L, 256)), rw_ln_b=din("rw_ln_b", (L, 256)),
        fox_b_f=din("fox_b_f", (L, 8)), hg_lb_logits=din("hg_lb_logits", (L, 256)),
        hg_norm_g=din("hg_norm_g", (L, 256)), w_out=din("w_out", (L, D, D)), norm2_g=din("norm2_g", (L, D)),
        w_ffn_in=din("w_ffn_in", (L, D, 2 * DFF)), w_ffn_out=din("w_ffn_out", (L, DFF, D)),
        final_norm_g=din("final_norm_g", (D,)),
    )
    O = dict(
        yp=dout("yp", (2, SEQ, D)), ys=dout("ys", (TS, D)),
        fkp=dout("fkp", (L, 2, SEQ, 512)), fvp=dout("fvp", (L, 2, SEQ, 512)), flp=dout("flp", (L, 2, SEQ, 8)),
        rwp=dout("rwp", (L, 2, 4, 64, 64)), rshp=dout("rshp", (L, 2, RW_COLS)), hgp=dout("hgp", (L, 2, 4, 64, 64)),
        fks=dout("fks", (L, TS, 512)), fvs=dout("fvs", (L, TS, 512)), fls=dout("fls", (L, TS, 8)),
        rws=dout("rws", (L, 4, 64, 64)), rshs=dout("rshs", (L, RW_COLS)), hgs=dout("hgs", (L, 4, 64, 64)),
    )
    xres = nc.dram_tensor("xres", [D, NTOK], F32).ap()
    xres_r = Reg("xres")
    seqs = [(0, SEQ), (SEQ, SEQ), (2 * SEQ, TS)]

    def tiles_of(s, W):
        off, T = seqs[s]
        w = min(W, T)
        return [(off + j * w, w, j) for j in range(T // w)]

    xres_regs = {}

    def xreg(t0):
        return xres_regs.setdefault(t0, Reg("xres%d" % t0))

    with contextlib.ExitStack() as top:
        fw = FW(nc, top)
        ident = Tt(fw.sbuf("ident", [128, 128], F32), name="ident")
        identb = Tt(fw.sbuf("identb", [128, 128], BF16), name="identb")
        onesb = Tt(fw.sbuf("onesb", [128, 128], BF16), name="onesb")
        fw.op("pool", lambda e: e.memset(ident.t[:], 0.0), writes=[ident.r])
        fw.op("pool", lambda e: e.affine_select(out=ident.t[:], in_=ident.t[:], pattern=[[-1, 128]],
                                                compare_op=ALU.not_equal, fill=1.0, base=0,
                                                channel_multiplier=1), reads=[ident.r], writes=[ident.r])
        fw.op("pool", lambda e: e.tensor_copy(out=identb.t[:], in_=ident.t[:]), reads=[ident.r], writes=[identb.r])
        fw.op("pool", lambda e: e.memset(onesb.t[:], 1.0), writes=[onesb.r])
        epsb = Tt(fw.sbuf("epsb", [128, 1], F32), name="epsb")
        fw.op("pool", lambda e: e.memset(epsb.t[:], EPS), writes=[epsb.r])


        trif = Tt(fw.sbuf("trif", [128, 128], F32), name="trif")
        fw.op("pool", lambda e: e.memset(trif.t[:], 1.0), writes=[trif.r])
        fw.op("pool", lambda e: e.affine_select(out=trif.t[:], in_=trif.t[:], pattern=[[1, 128]],
                                                compare_op=ALU.is_ge, fill=0.0, base=0, channel_multiplier=-1),
              reads=[trif.r], writes=[trif.r])
        self127 = Tt(fw.sbuf("self127", [128, 128], F32), name="self127")
        fw.op("pool", lambda e: e.memset(self127.t[:], 0.0), writes=[self127.r])
        fw.op("pool", lambda e: e.affine_select(out=self127.t[:], in_=self127.t[:], pattern=[[0, 128]],
                                                compare_op=ALU.not_equal, fill=1.0, base=-127, channel_multiplier=1),
              reads=[self127.r], writes=[self127.r])
        mnegf = Tt(fw.sbuf("mnegf", [128, 128], F32), name="mnegf")
        maskneg = Tt(fw.sbuf("maskneg", [128, 128], BF16), name="maskneg")
        fw.op("pool", lambda e: e.memset(mnegf.t[:], 0.0), writes=[mnegf.r])
        fw.op("pool", lambda e: e.affine_select(out=mnegf.t[:], in_=mnegf.t[:], pattern=[[1, 128]],
                                                compare_op=ALU.is_ge, fill=-30000.0, base=0, channel_multiplier=-1),
              reads=[mnegf.r], writes=[mnegf.r])
        fw.op("pool", lambda e: e.tensor_copy(out=maskneg.t[:], in_=mnegf.t[:]), reads=[mnegf.r], writes=[maskneg.r])
        e8 = Tt(fw.sbuf("e8", [8, 8, 128], F32), name="e8")
        fw.op("pool", lambda e: e.memset(e8.t[:], 0.0), writes=[e8.r])
        fw.op("pool", lambda e: e.affine_select(out=e8.t[:], in_=e8.t[:], pattern=[[-1, 8], [0, 128]],
                                                compare_op=ALU.not_equal, fill=1.0, base=0, channel_multiplier=1),
              reads=[e8.r], writes=[e8.r])
        selh = Tt(fw.sbuf("selh", [72, 8, 128], BF16), name="selh")
        fw.op("pool", lambda e: e.memset(selh.t[:], 0.0), writes=[selh.r])
        for b0 in (0, 32, 64):
            fw.op("dve", lambda e, b0=b0: e.tensor_copy(out=selh.t[b0:b0 + 8, :, :], in_=e8.t[:]),
                  reads=[e8.r], writes=[selh.r])
        onesf = Tt(fw.sbuf("onesf", [128, 64], F32), name="onesf")
        fw.op("pool", lambda e: e.memset(onesf.t[:], 1.0), writes=[onesf.r])
        bfb = Tt(fw.sbuf("bfb", [128, L, 8], F32), name="bfb")
        for l_ in range(L):
            fw.dma("sp", bfb.t[:, l_, :], I["fox_b_f"][l_:l_ + 1, :].to_broadcast([128, 8]), writes=[bfb.r],
                   stream="par", group=True)
        yfox = nc.dram_tensor("yfox", [512, NTOK], BF16).ap()
        yfox_regs = {}

        def yfreg(t0):
            return yfox_regs.setdefault(t0, Reg("yfox%d" % t0))

        banks = [Tt(fw.psum("bank%d" % i, [128, 512]), name="bank%d" % i) for i in range(8)]
        bank_ctr = [0]

        default_pool = [(0, 1, 2, 3, 4, 5, 6, 7)]

        def next_bank(pool=None):
            if pool is None:
                pool = default_pool[0]
            b = banks[pool[bank_ctr[0] % len(pool)]]
            bank_ctr[0] += 1
            return b

        def load_fm(name, src_ap, ncol, q="sp"):
            t = Tt(fw.sbuf(name, [128, L, ncol], F32), name=name)
            fw.dma(q, t.t[:], src_ap.rearrange("l (c p) -> p l c", p=128), writes=[t.r], stream="par", group=True,
                   allow_slow_non_contiguous=True)
            return t
        n1g = load_fm("n1g", I["norm1_g"], 8)
        n2g = load_fm("n2g", I["norm2_g"], 8)
        badaT = load_fm("badaT", I["b_ada"], 48)
        fng = Tt(fw.sbuf("fng", [128, 8], F32), name="fng")
        fw.dma("sp", fng.t[:], I["final_norm_g"].rearrange("(c p) -> p c", p=128), writes=[fng.r], stream="par", group=True,
               allow_slow_non_contiguous=True)

        rmask32 = Tt(fw.sbuf("rmask32", [128, 512], F32), name="rmask32")
        fw.op("pool", lambda e: e.memset(rmask32.t[:], 1.0), writes=[rmask32.r])
        fw.op("pool", lambda e: e.affine_select(out=rmask32.t[:, :].rearrange("p (c t) -> p c t", t=32),
                                                in_=rmask32.t[:, :].rearrange("p (c t) -> p c t", t=32),
                                                pattern=[[0, 16], [1, 32]], compare_op=ALU.not_equal, fill=0.0,
                                                base=0, channel_multiplier=0), reads=[rmask32.r], writes=[rmask32.r])
        maskbd = Tt(fw.sbuf("maskbd", [128, 128], F32), name="maskbd")
        fw.op("pool", lambda e: e.tensor_copy(out=maskbd.t[:], in_=trif.t[:]), reads=[trif.r], writes=[maskbd.r])
        for cb_ in range(1, 4):
            fw.op("pool", lambda e, cb_=cb_: e.affine_select(
                out=maskbd.t[:, cb_ * 32:(cb_ + 1) * 32], in_=maskbd.t[:, cb_ * 32:(cb_ + 1) * 32], pattern=[[0, 32]],
                compare_op=ALU.is_ge, fill=0.0, base=-cb_ * 32, channel_multiplier=1),
                reads=[maskbd.r], writes=[maskbd.r])
        onesbdf = Tt(fw.sbuf("onesbdf", [128, 128], F32), name="onesbdf")
        onesbd = Tt(fw.sbuf("onesbd", [128, 128], BF16), name="onesbd")
        fw.op("pool", lambda e: e.memset(onesbdf.t[:], 1.0), writes=[onesbdf.r])
        fw.op("pool", lambda e: e.affine_select(out=onesbdf.t[:, 0:64], in_=onesbdf.t[:, 0:64], pattern=[[0, 64]],
                                                compare_op=ALU.is_ge, fill=0.0, base=63, channel_multiplier=-1),
              reads=[onesbdf.r], writes=[onesbdf.r])
        fw.op("pool", lambda e: e.affine_select(out=onesbdf.t[:, 64:128], in_=onesbdf.t[:, 64:128], pattern=[[0, 64]],
                                                compare_op=ALU.is_ge, fill=0.0, base=-64, channel_multiplier=1),
              reads=[onesbdf.r], writes=[onesbdf.r])
        fw.op("pool", lambda e: e.tensor_copy(out=onesbd.t[:], in_=onesbdf.t[:]), reads=[onesbdf.r], writes=[onesbd.r])
        hgng = load_fm("hgng", I["hg_norm_g"], 2)
        lbl = load_fm("lbl", I["hg_lb_logits"], 2)
        lbT = Tt(fw.sbuf("lbT", [128, L, 2], F32), name="lbT")
        omlT = Tt(fw.sbuf("omlT", [128, L, 2], F32), name="omlT")
        nomlT = Tt(fw.sbuf("nomlT", [128, L, 2], F32), name="nomlT")
        lbm = Tt(fw.sbuf("lbm", [128, 2], F32), name="lbm")
        lbe = Tt(fw.sbuf("lbe", [128, L, 2], F32), name="lbe")
        lbs_ = Tt(fw.sbuf("lbs_", [128, 2], F32), name="lbs_")
        fw.op("dve", lambda e: e.tensor_copy(out=lbm.t[:], in_=lbl.t[:, 0, :]), reads=[lbl.r], writes=[lbm.r])
        for l_ in range(1, L):
            fw.op("dve", lambda e, l_=l_: e.tensor_max(out=lbm.t[:], in0=lbm.t[:], in1=lbl.t[:, l_, :]),
                  reads=[lbm.r, lbl.r], writes=[lbm.r])
        for l_ in range(L):
            fw.op("dve", lambda e, l_=l_: e.tensor_sub(out=lbe.t[:, l_, :], in0=lbl.t[:, l_, :], in1=lbm.t[:]),
                  reads=[lbm.r, lbl.r], writes=[lbe.r])
        fw.op("act", lambda e: e.activation(out=lbe.t[:], in_=lbe.t[:], func=AF.Exp), reads=[lbe.r], writes=[lbe.r])
        fw.op("dve", lambda e: e.tensor_copy(out=lbs_.t[:], in_=lbe.t[:, 0, :]), reads=[lbe.r], writes=[lbs_.r])
        for l_ in range(1, L):
            fw.op("dve", lambda e, l_=l_: e.tensor_add(out=lbs_.t[:], in0=lbs_.t[:], in1=lbe.t[:, l_, :]),
                  reads=[lbs_.r, lbe.r], writes=[lbs_.r])
        fw.op("dve", lambda e: e.reciprocal(out=lbs_.t[:], in_=lbs_.t[:]), reads=[lbs_.r], writes=[lbs_.r])
        for l_ in range(L):
            fw.op("dve", lambda e, l_=l_: e.tensor_mul(out=lbe.t[:, l_, :], in0=lbe.t[:, l_, :], in1=lbs_.t[:]),
                  reads=[lbs_.r, lbe.r], writes=[lbe.r])
        fw.op("dve", lambda e: e.memset(lbT.t[:, 0, :], 0.0), writes=[lbT.r])
        for l_ in range(1, L):
            fw.op("dve", lambda e, l_=l_: e.tensor_add(out=lbT.t[:, l_, :], in0=lbT.t[:, l_ - 1, :], in1=lbe.t[:, l_, :]),
                  reads=[lbT.r, lbe.r], writes=[lbT.r])
        fw.op("dve", lambda e: e.tensor_scalar(out=omlT.t[:], in0=lbT.t[:], scalar1=-1.0, scalar2=1.0,
                                               op0=ALU.mult, op1=ALU.add), reads=[lbT.r], writes=[omlT.r])
        fw.op("dve", lambda e: e.tensor_scalar_mul(out=nomlT.t[:], in0=omlT.t[:], scalar1=-1.0),
              reads=[omlT.r], writes=[nomlT.r])

        rmask64 = Tt(fw.sbuf("rmask64", [128, 512], F32), name="rmask64")
        fw.op("pool", lambda e: e.memset(rmask64.t[:], 1.0), writes=[rmask64.r])
        fw.op("pool", lambda e: e.affine_select(out=rmask64.t[:, :].rearrange("p (c t) -> p c t", t=64),
                                                in_=rmask64.t[:, :].rearrange("p (c t) -> p c t", t=64),
                                                pattern=[[0, 8], [1, 64]], compare_op=ALU.not_equal, fill=0.0,
                                                base=0, channel_multiplier=0), reads=[rmask64.r], writes=[rmask64.r])
        mask12 = Tt(fw.sbuf("mask12", [64, 2, 64], F32), name="mask12")
        mask34 = Tt(fw.sbuf("mask34", [64, 2, 64], F32), name="mask34")
        mask5 = Tt(fw.sbuf("mask5", [64, 64], F32), name="mask5")
        fw.op("pool", lambda e: e.memset(mask12.t[:], 1.0), writes=[mask12.r])
        fw.op("pool", lambda e: e.affine_select(out=mask12.t[:, 0, :], in_=mask12.t[:, 0, :], pattern=[[1, 64]],
                                                compare_op=ALU.is_ge, fill=0.0, base=-1, channel_multiplier=-1),
              reads=[mask12.r], writes=[mask12.r])
        fw.op("pool", lambda e: e.affine_select(out=mask12.t[:, 1, :], in_=mask12.t[:, 1, :], pattern=[[1, 64]],
                                                compare_op=ALU.is_ge, fill=0.0, base=0, channel_multiplier=-1),
              reads=[mask12.r], writes=[mask12.r])
        fw.op("dve", lambda e: e.tensor_scalar_mul(out=mask34.t[:, 0, :], in0=mask12.t[:, 0, :], scalar1=-1.0),
              reads=[mask12.r], writes=[mask34.r])
        fw.op("pool", lambda e: e.tensor_copy(out=mask34.t[:, 1, :], in_=mask12.t[:, 1, :]),
              reads=[mask12.r], writes=[mask34.r])
        fw.op("pool", lambda e: e.memset(mask5.t[:], -1.0), writes=[mask5.r])
        fw.op("pool", lambda e: e.affine_select(out=mask5.t[:], in_=mask5.t[:], pattern=[[-1, 64]],
                                                compare_op=ALU.is_ge, fill=0.0, base=-1, channel_multiplier=1),
              reads=[mask5.r], writes=[mask5.r])
        eps2 = Tt(fw.sbuf("eps2", [128, 1], F32), name="eps2")
        fw.op("pool", lambda e: e.memset(eps2.t[:], 64e-5), writes=[eps2.r])
        rwp = {}
        for nm in ("rw_w0", "rw_a0", "rw_k_k", "rw_k_a", "rw_r_k", "rw_ln_w", "rw_ln_b"):
            rwp[nm] = load_fm("p_" + nm, I[nm], 2)
        nw0 = Tt(fw.sbuf("nw0", [128, L, 2], F32), name="nw0")
        na0 = Tt(fw.sbuf("na0", [128, L, 2], F32), name="na0")
        omka = Tt(fw.sbuf("omka", [128, L, 2], F32), name="omka")
        fw.op("dve", lambda e: e.tensor_scalar_mul(out=nw0.t[:], in0=rwp["rw_w0"].t[:], scalar1=-1.0),
              reads=[rwp["rw_w0"].r], writes=[nw0.r])
        fw.op("dve", lambda e: e.tensor_scalar_mul(out=na0.t[:], in0=rwp["rw_a0"].t[:], scalar1=-1.0),
              reads=[rwp["rw_a0"].r], writes=[na0.r])
        fw.op("dve", lambda e: e.tensor_scalar(out=omka.t[:], in0=rwp["rw_k_a"].t[:], scalar1=-1.0, scalar2=1.0,
                                               op0=ALU.mult, op1=ALU.add), reads=[rwp["rw_k_a"].r], writes=[omka.r])
        mul = Tt(fw.sbuf("mul", [128, L, 9], F32), name="mul")
        fw.op("pool", lambda e: e.memset(mul.t[:], 0.0), writes=[mul.r])
        m7 = load_fm("m7", I["rw_mu"], 7)
        fw.op("dve", lambda e: e.tensor_copy(out=mul.t[:, :, 0:6], in_=m7.t[:, :, 0:6]), reads=[m7.r], writes=[mul.r])
        fw.op("dve", lambda e: e.tensor_copy(out=mul.t[0:32, :, 6], in_=m7.t[0:32, :, 6]), reads=[m7.r], writes=[mul.r])
        fw.op("dve", lambda e: e.tensor_copy(out=mul.t[0:32, :, 7], in_=m7.t[32:64, :, 6]), reads=[m7.r], writes=[mul.r])
        fw.op("dve", lambda e: e.tensor_copy(out=mul.t[0:64, :, 8], in_=m7.t[64:128, :, 6]), reads=[m7.r], writes=[mul.r])
        w2b = Tt(fw.sbuf("w2b", [32, L, 256], BF16), name="w2b")
        a2b = Tt(fw.sbuf("a2b", [32, L, 256], BF16), name="a2b")
        g2b = Tt(fw.sbuf("g2b", [64, L, 256], BF16), name="g2b")
        fw.dma("pool", w2b.t[:], I["rw_w2"].rearrange("l k n -> k l n"), writes=[w2b.r], stream="parb", group=True)
        fw.dma("pool", a2b.t[:], I["rw_a2"].rearrange("l k n -> k l n"), writes=[a2b.r], stream="parb", group=True)
        fw.dma("pool", g2b.t[:], I["rw_g2"].rearrange("l k n -> k l n"), writes=[g2b.r], stream="parb", group=True)

        cT = Tt(fw.sbuf("cT", [128, 3, 8], F32), name="cT")
        fw.dma("sp", cT.t[:], I["cc"].rearrange("b (c p) -> p b c", p=128), writes=[cT.r], stream="par", group=True,
               allow_slow_non_contiguous=True)
        siluT = Tt(fw.sbuf("siluT", [128, 8, 3], F32), name="siluT")
        sl_e = Tt(fw.sbuf("sl_e", [128, 3, 8], F32), name="sl_e")
        fw.op("act", lambda e: e.activation(out=sl_e.t[:], in_=cT.t[:], func=AF.Exp, scale=-1.0),
              reads=[cT.r], writes=[sl_e.r])
        fw.op("dve", lambda e: e.tensor_scalar_add(out=sl_e.t[:], in0=sl_e.t[:], scalar1=1.0),
              reads=[sl_e.r], writes=[sl_e.r])
        fw.op("dve", lambda e: e.reciprocal(out=sl_e.t[:], in_=sl_e.t[:]), reads=[sl_e.r], writes=[sl_e.r])
        fw.op("dve", lambda e: e.tensor_tensor(out=siluT.t[:].rearrange("p c b -> p b c"), in0=sl_e.t[:],
                                               in1=cT.t[:], op=ALU.mult),
              reads=[sl_e.r, cT.r], writes=[siluT.r])
        mod = Tt(fw.sbuf("mod", [128, L, 6, 3, 8], F32), name="mod")
        with contextlib.ExitStack() as ph:
            sub = FWScope(fw, ph)
            wa = [Tt(sub.sbuf("wa%d" % i, [128, 8, 512], F32), name="wa%d" % i) for i in range(2)]
            k = 0
            for l in range(L):
                for jg in range(12):
                    w = wa[k % 2]
                    k += 1
                    fw.dma("sp" if k % 2 else "pool", w.t[:],
                           I["w_ada"][l].rearrange("(kc p) n -> p kc n", p=128)[:, :, jg * 512:(jg + 1) * 512],
                           writes=[w.r], stream="wada%d" % (k % 2))
                    for jj in range(4):
                        j = jg * 4 + jj
                        m, c = j // 8, j % 8
                        pb = next_bank()
                        for kc in range(8):
                            fw.op("pe", lambda e, w=w, pb=pb, kc=kc, jj=jj: e.matmul(
                                pb.t[:, 0:3], lhsT=w.t[:, kc, jj * 128:(jj + 1) * 128], rhs=siluT.t[:, kc, :],
                                start=(kc == 0), stop=(kc == 7)),
                                reads=[w.r, siluT.r], writes=[pb.r], signal=(kc == 7))
                        fw.op("dve", lambda e, pb=pb, l=l, m=m, c=c, j=j: e.tensor_scalar(
                            out=mod.t[:, l, m, :, c], in0=pb.t[:, 0:3], scalar1=badaT.t[:, l, j:j + 1], scalar2=None,
                            op0=ALU.add), reads=[pb.r, badaT.r], writes=[mod.r])
            fw.flush()
        G1 = Tt(fw.sbuf("G1", [128, L, 3, 8], F32), name="G1")
        G2 = Tt(fw.sbuf("G2", [128, L, 3, 8], F32), name="G2")
        for l in range(L):
            for (G, ng, mi) in ((G1, n1g, 1), (G2, n2g, 4)):
                for b in range(3):
                    fw.op("dve", lambda e, G=G, ng=ng, mi=mi, l=l, b=b: e.scalar_tensor_tensor(
                        out=G.t[:, l, b, :], in0=mod.t[:, l, mi, b, :], scalar=1.0, in1=ng.t[:, l, :],
                        op0=ALU.add, op1=ALU.mult), reads=[mod.r, ng.r], writes=[G.r])

        def load_xT(xT, t0, w, q="sp"):
            fw.dma(q, xT.t[:, :, 0:w], xres.rearrange("(c p) t -> p c t", p=128)[:, :, t0:t0 + w],
                   reads=[xreg(t0)], writes=xT.rs, stream="xld" + xT.r.name[-2:])

        def store_xT(xT, t0, w, q="sp"):
            fw.dma(q, xres.rearrange("(c p) t -> p c t", p=128)[:, :, t0:t0 + w], xT.t[:, :, 0:w],
                   reads=xT.rs, writes=[xreg(t0)], stream="xst" + xT.r.name[-2:])

        def norm_fm(xT, w, sq, tmp, lnv, rstd, out, gap, shap, out_regs):
            fw.op("act", lambda e: e.activation(out=sq.t[:, :, 0:w], in_=xT.t[:, :, 0:w], func=AF.Square),
                  reads=xT.rs, writes=[sq.r])
            pb = next_bank()
            for c in range(8):
                fw.op("pe", lambda e, c=c: e.matmul(pb.t[:, 0:w], lhsT=onesb.t[:], rhs=sq.t[:, c, 0:w],
                                                    start=(c == 0), stop=(c == 7)),
                      reads=[onesb.r, sq.r], writes=[pb.r], signal=(c == 7))
            fw.op("act", lambda e: e.activation(out=lnv.t[:, 0:w], in_=pb.t[:, 0:w], func=AF.Ln, scale=1.0 / D,
                                                bias=epsb.t[:]), reads=[pb.r, epsb.r], writes=[lnv.r])
            fw.op("act", lambda e: e.activation(out=rstd.t[:, 0:w], in_=lnv.t[:, 0:w], func=AF.Exp, scale=-0.5),
                  reads=[lnv.r], writes=[rstd.r])
            for c in range(8):
                tm = tmp[c % len(tmp)]
                fw.op("dve", lambda e, c=c, tm=tm: e.tensor_tensor(out=tm.t[:, 0:w], in0=xT.t[:, c, 0:w],
                                                                   in1=rstd.t[:, 0:w], op=ALU.mult),
                      reads=[xT.rs[c], rstd.r], writes=[tm.r])
                if shap is not None:
                    fw.op("act", lambda e, c=c, tm=tm: e.activation(out=out.t[:, c, 0:w], in_=tm.t[:, 0:w],
                                                                    func=AF.Identity, scale=gap(c), bias=shap(c)),
                          reads=[tm.r, G1.r, G2.r, mod.r], writes=[out_regs[c]])
                else:
                    fw.op("act", lambda e, c=c, tm=tm: e.activation(out=out.t[:, c, 0:w], in_=tm.t[:, 0:w],
                                                                    func=AF.Identity, scale=gap(c)),
                          reads=[tm.r, fng.r], writes=[out_regs[c]])

        with contextlib.ExitStack() as ph:
            sub = FWScope(fw, ph)
            xtok = [Tt(sub.sbuf("xtok%d" % i, [128, 4, D], F32), name="xtok%d" % i) for i in range(2)]
            xTs = [Tt(sub.sbuf("xTp%d" % i, [128, 8, 512], F32), nreg=8, name="xTp%d" % i) for i in range(2)]
            it = 0
            for s in range(3):
                src = I["xp"][s] if s < 2 else I["xs"]
                off, T = seqs[s]
                for (t0, w, j) in tiles_of(s, 512):
                    xt = xtok[it % 2]
                    xT = xTs[it % 2]
                    it += 1
                    nb = (w + 127) // 128
                    pw = min(w, 128)
                    lt0 = t0 - off
                    if w >= 128:
                        fw.dma("sp" if it % 2 else "pool", xt.t[:, 0:nb, :],
                               src[lt0:lt0 + w, :].rearrange("(b p) d -> p b d", p=128), writes=[xt.r], stream="xtok")
                    else:
                        fw.dma("sp", xt.t[0:w, 0, :], src[lt0:lt0 + w, :], writes=[xt.r], stream="xtok")
                    for c in range(8):
                        pb = next_bank()
                        for tb in range(nb):
                            fw.op("pe", lambda e, c=c, tb=tb, pb=pb, xt=xt, pw=pw: e.transpose(
                                pb.t[:, tb * 128:tb * 128 + pw], xt.t[0:pw, tb, c * 128:(c + 1) * 128],
                                ident.t[0:pw, 0:pw]),
                                reads=[xt.r, ident.r], writes=[pb.r], signal=(tb == nb - 1))
                        eng = "act" if c % 2 else "dve"
                        if eng == "act":
                            fw.op("act", lambda e, c=c, pb=pb, xT=xT, w=w: e.activation(
                                out=xT.t[:, c, 0:w], in_=pb.t[:, 0:w], func=AF.Copy), reads=[pb.r], writes=[xT.rs[c]])
                        else:
                            fw.op("dve", lambda e, c=c, pb=pb, xT=xT, w=w: e.tensor_copy(
                                out=xT.t[:, c, 0:w], in_=pb.t[:, 0:w]), reads=[pb.r], writes=[xT.rs[c]])
                    store_xT(xT, t0, w, q="sp" if it % 2 else "pool")
            fw.flush()

        for l in range(L):
            fw.rotate()
            if stage >= 2:
              with contextlib.ExitStack() as ph:
                sub = FWScope(fw, ph)
                default_pool[0] = (0, 1, 2, 3, 4, 5)
                WA = 512
                FC0 = RW_COLS
                TKMAX = max(SEQ, PAST + TS)
                NBMAX = (TKMAX + 127) // 128
                wf = Tt(sub.sbuf("wf", [128, 8, FOX_COLS], BF16), name="wf")
                for kc in range(8):
                    fw.dma("pool", wf.t[:, kc, :], I["w_in"][l][kc * 128:(kc + 1) * 128, FC0:FC0 + FOX_COLS],
                           writes=[wf.r], stream="wld", group=True)
                KT = Tt(sub.sbuf("KT", [128, 4, TKMAX], BF16), nreg=NBMAX, name="KT")
                Vx = Tt(sub.sbuf("Vx", [128, NBMAX, 8, 65], BF16), nreg=NBMAX, name="Vx")
                fw.op("pool", lambda e: e.memset(Vx.t[:], 1.0), writes=Vx.rs)
                ctok = Tt(sub.sbuf("ctok", [128, NBMAX, 8], F32), nreg=NBMAX, name="ctok")
                negc = Tt(sub.sbuf("negc", [128, NBMAX, 8], F32), nreg=NBMAX, name="negc")
                xTa = [Tt(sub.sbuf("xTa%d" % i, [128, 8, WA], F32), nreg=8, name="xTa%d" % i) for i in range(2)]
                sq = Tt(sub.sbuf("sqa", [128, 8, WA], BF16), name="sqa")
                hT = Tt(sub.sbuf("hTa", [128, 8, WA], BF16), nreg=8, name="hTa")
                tmp = [Tt(sub.sbuf("tmpa%d" % i, [128, WA], F32), name="tmpa%d" % i) for i in range(2)]
                lnv = Tt(sub.sbuf("lnva", [128, WA], F32), name="lnva")
                rstd = Tt(sub.sbuf("rstda", [128, WA], F32), name="rstda")
                QT = Tt(sub.sbuf("QT", [128, 4, WA], BF16), nreg=4, name="QT")
                ktok = [Tt(sub.sbuf("ktok%d" % i, [128, 512], F32), name="ktok%d" % i) for i in range(2)]
                vtok = [Tt(sub.sbuf("vtok%d" % i, [128, 512], F32), name="vtok%d" % i) for i in range(2)]
                ftok = [Tt(sub.sbuf("ftok%d" % i, [128, 8], F32), name="ftok%d" % i) for i in range(2)]
                ltok = [Tt(sub.sbuf("ltok%d" % i, [128, 8], F32), name="ltok%d" % i) for i in range(2)]
                cfm = Tt(sub.sbuf("cfm", [8, WA], F32), name="cfm")
                cr1 = Tt(sub.sbuf("cr1", [8, WA], F32), name="cr1")
                cr2 = Tt(sub.sbuf("cr2", [8, WA], F32), name="cr2")
                midt = Tt(sub.sbuf("midt", [8, WA], BF16), name="midt")
                cq96 = Tt(sub.sbuf("cq96", [72, WA], BF16), name="cq96")
                fw.op("pool", lambda e: e.memset(cq96.t[:], 0.0), writes=[cq96.r])
                pts = [Tt(sub.sbuf("pt%d" % i, [128, WA], BF16), name="pt%d" % i) for i in range(4)]
                rsf = Tt(sub.sbuf("rsf", [128, WA], F32), name="rsf")
                rcp = Tt(sub.sbuf("rcp", [64, WA], F32), name="rcp")
                yfT = Tt(sub.sbuf("yfT", [128, 4, WA], BF16), nreg=4, name="yfT")
                it = 0
                ik = 0
                ipt = 0
                for s in range(3):
                    off, T = seqs[s]
                    kbase = 0
                    if s == 2:
                        kbase = PAST
                        for cb in range(PAST // 128):
                            kt_ = ktok[ik % 2]
                            vt_ = vtok[ik % 2]
                            ft_ = ltok[ik % 2]
                            ik += 1
                            fw.dma("sp", kt_.t[:], I["ck"][l][cb * 128:(cb + 1) * 128, :], writes=[kt_.r], stream="cldk%d" % (ik % 2))
                            fw.dma("sp", vt_.t[:], I["cv"][l][cb * 128:(cb + 1) * 128, :], writes=[vt_.r], stream="cldv%d" % (ik % 2))
                            fw.dma("sp", ft_.t[:], I["cl"][l][cb * 128:(cb + 1) * 128, :], writes=[ft_.r], stream="cldf%d" % (ik % 2))
                            pb = next_bank()
                            for pc in range(4):
                                fw.op("pe", lambda e, pc=pc, pb=pb, kt_=kt_: e.transpose(
                                    pb.t[:, pc * 128:(pc + 1) * 128], kt_.t[:, pc * 128:(pc + 1) * 128], ident.t[:]),
                                    reads=[kt_.r, ident.r], writes=[pb.r], signal=(pc == 3))
                            fw.op("act", lambda e, pb=pb, cb=cb: e.activation(
                                out=KT.t[:, :, cb * 128:(cb + 1) * 128],
                                in_=pb.t[:, :].rearrange("p (c t) -> p c t", c=4), func=AF.Copy),
                                reads=[pb.r], writes=[KT.rs[cb]])
                            fw.op("dve", lambda e, vt_=vt_, cb=cb: e.tensor_copy(
                                out=Vx.t[:, cb, :, 0:64], in_=vt_.t[:, :].rearrange("p (h d) -> p h d", h=8)),
                                reads=[vt_.r], writes=[Vx.rs[cb]])
                            pc_ = next_bank()
                            fw.op("pe", lambda e, pc_=pc_, ft_=ft_, cb=cb: e.matmul(
                                pc_.t[:, 0:8], lhsT=trif.t[:], rhs=ft_.t[:], start=True, stop=(cb == 0)),
                                reads=[trif.r, ft_.r], writes=[pc_.r], signal=(cb == 0))
                            if cb > 0:
                                fw.op("pe", lambda e, pc_=pc_, cb=cb: e.matmul(
                                    pc_.t[:, 0:8], lhsT=self127.t[:], rhs=ctok.t[:, cb - 1, :], start=False, stop=True),
                                    reads=[self127.r, ctok.rs[cb - 1]], writes=[pc_.r])
                            fw.op("dve", lambda e, pc_=pc_, cb=cb: e.tensor_copy(out=ctok.t[:, cb, :], in_=pc_.t[:, 0:8]),
                                  reads=[pc_.r], writes=[ctok.rs[cb]])
                            fw.op("act", lambda e, pc_=pc_, cb=cb: e.activation(
                                out=negc.t[:, cb, :], in_=pc_.t[:, 0:8], func=AF.Copy, scale=-1.0),
                                reads=[pc_.r], writes=[negc.rs[cb]])
                    dK = O["fkp"][l][s] if s < 2 else O["fks"][l]
                    dV = O["fvp"][l][s] if s < 2 else O["fvs"][l]
                    dF = O["flp"][l][s] if s < 2 else O["fls"][l]
                    def a1_tile(s, t0, w, j, xT, off, kbase, dK, dV, dF, l=l):
                        nonlocal ik, ipt
                        lt0 = t0 - off
                        kt0 = kbase + lt0
                        nb = (w + 127) // 128
                        pw = min(w, 128)
                        load_xT(xT, t0, w)
                        norm_fm(xT, w, sq, tmp, lnv, rstd, hT,
                                lambda c, s=s: G1.t[:, l, s, c:c + 1], lambda c, s=s: mod.t[:, l, 0, s, c:c + 1], hT.rs)
                        for pc in range(4):
                            pq = next_bank()
                            for kc in range(8):
                                fw.op("pe", lambda e, kc=kc, pc=pc, pq=pq, w=w: e.matmul(
                                    pq.t[:, 0:w], lhsT=wf.t[:, kc, pc * 128:(pc + 1) * 128], rhs=hT.t[:, kc, 0:w],
                                    start=(kc == 0), stop=(kc == 7)),
                                    reads=[wf.r, hT.rs[kc]], writes=[pq.r], signal=(kc == 7))
                            fw.op("act", lambda e, pc=pc, pq=pq, w=w: e.activation(
                                out=QT.t[:, pc, 0:w], in_=pq.t[:, 0:w], func=AF.Copy, scale=0.125),
                                reads=[pq.r], writes=[QT.rs[pc]])
                            pk = next_bank()
                            for kc in range(8):
                                fw.op("pe", lambda e, kc=kc, pc=pc, pk=pk, w=w: e.matmul(
                                    pk.t[:, 0:w], lhsT=wf.t[:, kc, 512 + pc * 128:512 + (pc + 1) * 128],
                                    rhs=hT.t[:, kc, 0:w], start=(kc == 0), stop=(kc == 7)),
                                    reads=[wf.r, hT.rs[kc]], writes=[pk.r], signal=(kc == 7))
                            kregs = [KT.rs[(kt0 + tb * 128) // 128] for tb in range(nb)]
                            fw.op("dve", lambda e, pc=pc, pk=pk, w=w, kt0=kt0: e.tensor_copy(
                                out=KT.t[:, pc, kt0:kt0 + w], in_=pk.t[:, 0:w]), reads=[pk.r], writes=kregs)
                        for tb in range(nb):
                            kb = (kt0 + tb * 128) // 128
                            kt_ = ktok[ik % 2]
                            vt_ = vtok[ik % 2]
                            ft_ = ftok[ik % 2]
                            lt_ = ltok[ik % 2]
                            ik += 1
                            for (dst_t, c0, ncol, dd, eng) in ((kt_, 512, 512, dK, "act"), (vt_, 1024, 512, dV, "dve")):
                                pb = next_bank()
                                for kc in range(8):
                                    fw.op("pe", lambda e, kc=kc, pb=pb, tb=tb, c0=c0, ncol=ncol, pw=pw: e.matmul(
                                        pb.t[0:pw, 0:ncol], lhsT=hT.t[:, kc, tb * 128:tb * 128 + pw],
                                        rhs=wf.t[:, kc, c0:c0 + ncol], start=(kc == 0), stop=(kc == 7)),
                                        reads=[wf.r, hT.rs[kc]], writes=[pb.r], signal=(kc == 7))
                                if eng == "act":
                                    fw.op("act", lambda e, pb=pb, dst_t=dst_t, pw=pw: e.activation(
                                        out=dst_t.t[0:pw, :], in_=pb.t[0:pw, :], func=AF.Copy),
                                        reads=[pb.r], writes=[dst_t.r])
                                else:
                                    fw.op("dve", lambda e, pb=pb, dst_t=dst_t, pw=pw: e.tensor_copy(
                                        out=dst_t.t[0:pw, :], in_=pb.t[0:pw, :]), reads=[pb.r], writes=[dst_t.r])
                                r0 = lt0 + tb * 128
                                fw.dma("sp", dd[r0:r0 + pw, :], dst_t.t[0:pw, :], reads=[dst_t.r],
                                       stream="kvo%s%d" % (eng[0], ik % 2))
                            fw.op("pool", lambda e, vt_=vt_, kb=kb, pw=pw: e.tensor_copy(
                                out=Vx.t[0:pw, kb, :, 0:64], in_=vt_.t[0:pw, :].rearrange("p (h d) -> p h d", h=8)),
                                reads=[vt_.r], writes=[Vx.rs[kb]])
                            pf = next_bank()
                            for kc in range(8):
                                fw.op("pe", lambda e, kc=kc, pf=pf, tb=tb, pw=pw: e.matmul(
                                    pf.t[0:pw, 0:8], lhsT=hT.t[:, kc, tb * 128:tb * 128 + pw],
                                    rhs=wf.t[:, kc, 1536:1544], start=(kc == 0), stop=(kc == 7)),
                                    reads=[wf.r, hT.rs[kc]], writes=[pf.r], signal=(kc == 7))
                            fw.op("dve", lambda e, pf=pf, ft_=ft_, pw=pw: e.tensor_tensor(
                                out=ft_.t[0:pw, :], in0=pf.t[0:pw, 0:8], in1=bfb.t[0:pw, l, :], op=ALU.add),
                                reads=[pf.r, bfb.r], writes=[ft_.r])
                            fw.op("act", lambda e, ft_=ft_, pw=pw: e.activation(
                                out=ft_.t[0:pw, :], in_=ft_.t[0:pw, :], func=AF.Exp, scale=-1.0),
                                reads=[ft_.r], writes=[ft_.r])
                            fw.op("act", lambda e, ft_=ft_, pw=pw: e.activation(
                                out=ft_.t[0:pw, :], in_=ft_.t[0:pw, :], func=AF.Ln, bias=1.0),
                                reads=[ft_.r], writes=[ft_.r])
                            fw.op("dve", lambda e, ft_=ft_, lt_=lt_, pw=pw: e.tensor_scalar_mul(
                                out=lt_.t[0:pw, :], in0=ft_.t[0:pw, :], scalar1=-1.0), reads=[ft_.r], writes=[lt_.r])
                            r0 = lt0 + tb * 128
                            fw.dma("sp", dF[r0:r0 + pw, :], lt_.t[0:pw, :], reads=[lt_.r], stream="kvof%d" % (ik % 2))
                            pc_ = next_bank()
                            first = (kb == 0)
                            fw.op("pe", lambda e, pc_=pc_, lt_=lt_, pw=pw, first=first: e.matmul(
                                pc_.t[0:pw, 0:8], lhsT=trif.t[0:pw, 0:pw], rhs=lt_.t[0:pw, :], start=True, stop=first),
                                reads=[trif.r, lt_.r], writes=[pc_.r], signal=first)
                            if not first:
                                fw.op("pe", lambda e, pc_=pc_, kb=kb, pw=pw: e.matmul(
                                    pc_.t[0:pw, 0:8], lhsT=self127.t[:, 0:pw], rhs=ctok.t[:, kb - 1, :],
                                    start=False, stop=True),
                                    reads=[self127.r, ctok.rs[kb - 1]], writes=[pc_.r])
                            fw.op("dve", lambda e, pc_=pc_, kb=kb, pw=pw: e.tensor_copy(
                                out=ctok.t[0:pw, kb, :], in_=pc_.t[0:pw, 0:8]), reads=[pc_.r], writes=[ctok.rs[kb]])
                            fw.op("act", lambda e, pc_=pc_, kb=kb, pw=pw: e.activation(
                                out=negc.t[0:pw, kb, :], in_=pc_.t[0:pw, 0:8], func=AF.Copy, scale=-1.0),
                                reads=[pc_.r], writes=[negc.rs[kb]])
                            pt_ = next_bank()
                            fw.op("pe", lambda e, pt_=pt_, kb=kb, pw=pw: e.transpose(
                                pt_.t[0:8, 0:pw], ctok.t[0:pw, kb, :], ident.t[0:pw, 0:pw]),
                                reads=[ctok.rs[kb], ident.r], writes=[pt_.r])
                            fw.op("dve", lambda e, pt_=pt_, tb=tb, pw=pw: e.tensor_copy(
                                out=cfm.t[:, tb * 128:tb * 128 + pw], in_=pt_.t[0:8, 0:pw]),
                                reads=[pt_.r], writes=[cfm.r])
                        fw.op("act", lambda e, w=w: e.activation(out=cq96.t[0:8, 0:w], in_=cfm.t[:, 0:w], func=AF.Copy),
                              reads=[cfm.r], writes=[cq96.r])
                        fw.op("dve", lambda e, w=w: e.tensor_tensor(out=cr1.t[:, 0:w], in0=cfm.t[:, 0:w],
                                                                    in1=cq96.t[0:8, 0:w], op=ALU.subtract),
                              reads=[cfm.r, cq96.r], writes=[cr1.r])
                        fw.op("act", lambda e, w=w: e.activation(out=midt.t[:, 0:w], in_=cr1.t[:, 0:w], func=AF.Copy),
                              reads=[cr1.r], writes=[midt.r])
                        fw.op("pool", lambda e, w=w: e.tensor_copy(out=cq96.t[32:40, 0:w], in_=midt.t[:, 0:w]),
                              reads=[midt.r], writes=[cq96.r])
                        fw.op("dve", lambda e, w=w: e.tensor_tensor(out=cr2.t[:, 0:w], in0=cr1.t[:, 0:w],
                                                                    in1=midt.t[:, 0:w], op=ALU.subtract),
                              reads=[cr1.r, midt.r], writes=[cr2.r])
                        fw.op("act", lambda e, w=w: e.activation(out=cq96.t[64:72, 0:w], in_=cr2.t[:, 0:w], func=AF.Copy),
                              reads=[cr2.r], writes=[cq96.r])
                        kb_first_tile = kt0 // 128
                        nkb = kb_first_tile + nb
                        pending_epi = []
                        for h in range(8):
                            hr = slice((h % 2) * 64, (h % 2) * 64 + 64)
                            hp = h // 2
                            ob = banks[6 + (h % 2)]
                            blocks = []
                            for kb in range(nkb):
                                if kb < kb_first_tile:
                                    q0, rows, diag = 0, 128, False
                                else:
                                    q0, rows, diag = (kb - kb_first_tile) * 128, pw, True
                                blocks.append((kb, q0, rows, diag))
                            sbanks = {}
                            ptl = {}

                            def emit_s(bi, h=h, hr=hr, hp=hp):
                                kb, q0, rows, diag = blocks[bi]
                                sb = next_bank(pool=(0, 1, 2, 3))
                                sbanks[bi] = sb
                                fw.op("pe", lambda e, sb=sb, kb=kb, q0=q0, rows=rows: e.matmul(
                                    sb.t[0:rows, q0:w], lhsT=KT.t[hr, hp, kb * 128:kb * 128 + rows],
                                    rhs=QT.t[hr, hp, q0:w], start=True, stop=False),
                                    reads=[KT.rs[kb], QT.rs[hp]], writes=[sb.r], signal=False)
                                fw.op("pe", lambda e, sb=sb, q0=q0, rows=rows: e.matmul(
                                    sb.t[0:rows, q0:w], lhsT=selh.t[0:72, h, 0:rows], rhs=cq96.t[0:72, q0:w],
                                    start=False, stop=(not diag)),
                                    reads=[selh.r, cq96.r], writes=[sb.r], signal=(not diag))
                                if diag:
                                    fw.op("pe", lambda e, sb=sb, q0=q0, rows=rows: e.matmul(
                                        sb.t[0:rows, q0:q0 + rows], lhsT=identb.t[0:rows, 0:rows],
                                        rhs=maskneg.t[0:rows, 0:rows], start=False, stop=True),
                                        reads=[identb.r, maskneg.r], writes=[sb.r])

                            def emit_pv(bi, h=h, ob=ob):
                                nonlocal ipt
                                kb, q0, rows, diag = blocks[bi]
                                sb = sbanks.pop(bi)
                                pt = pts[ipt % 4]
                                ipt += 1
                                fw.op("act", lambda e, sb=sb, pt=pt, kb=kb, q0=q0, rows=rows: e.activation(
                                    out=pt.t[0:rows, q0:w], in_=sb.t[0:rows, q0:w], func=AF.Exp,
                                    bias=negc.t[0:rows, kb, h:h + 1]),
                                    reads=[sb.r, negc.rs[kb]], writes=[pt.r])
                                last = (bi == len(blocks) - 1)
                                fw.op("pe", lambda e, pt=pt, kb=kb, q0=q0, rows=rows, bi=bi, last=last: e.matmul(
                                    ob.t[0:65, q0:w], lhsT=Vx.t[0:rows, kb, h, :], rhs=pt.t[0:rows, q0:w],
                                    start=(bi == 0), stop=last),
                                    reads=[Vx.rs[kb], pt.r], writes=[ob.r], signal=last)
                            LOOK = 2
                            nbk = len(blocks)
                            for bi in range(min(LOOK, nbk)):
                                emit_s(bi)
                            for bi in range(nbk):
                                emit_pv(bi)
                                if bi + LOOK < nbk:
                                    emit_s(bi + LOOK)
                            def epilogue(ob=ob, hr=hr, hp=hp):
                                fw.op("act", lambda e, ob=ob, w=w: e.activation(
                                    out=rsf.t[64:65, 0:w], in_=ob.t[64:65, 0:w], func=AF.Copy), reads=[ob.r], writes=[rsf.r])
                                pr = next_bank(pool=(4, 5))
                                fw.op("pe", lambda e, pr=pr, w=w: e.matmul(
                                    pr.t[0:64, 0:w], lhsT=onesf.t[64:65, 0:64], rhs=rsf.t[64:65, 0:w], start=True, stop=True),
                                    reads=[onesf.r, rsf.r], writes=[pr.r])
                                fw.op("dve", lambda e, pr=pr, w=w: e.reciprocal(out=rcp.t[:, 0:w], in_=pr.t[0:64, 0:w]),
                                      reads=[pr.r], writes=[rcp.r])
                                fw.op("dve", lambda e, ob=ob, hr=hr, hp=hp, w=w: e.tensor_tensor(
                                    out=yfT.t[hr, hp, 0:w], in0=ob.t[0:64, 0:w], in1=rcp.t[:, 0:w], op=ALU.mult),
                                    reads=[ob.r, rcp.r], writes=[yfT.rs[hp]])
                            if pending_epi:
                                pending_epi.pop()()
                            pending_epi.append(epilogue)
                        while pending_epi:
                            pending_epi.pop()()
                        fw.dma("sp", yfox.rearrange("(c p) t -> p c t", p=128)[:, :, t0:t0 + w], yfT.t[:, :, 0:w],
                               reads=yfT.rs, writes=[yfreg(t0)], stream="yfst")
                    for (t0, w, j) in tiles_of(s, WA):
                        xT = xTa[it % 2]
                        it += 1
                        a1_tile(s, t0, w, j, xT, off, kbase, dK, dV, dF)
                fw.flush()
                default_pool[0] = (0, 1, 2, 3, 4, 5, 6, 7)
            if stage >= 2:
              with contextlib.ExitStack() as ph:
                sub = FWScope(fw, ph)
                default_pool[0] = (0, 1, 2, 3, 4)
                sink = [fw]
                WA = 256
                RW0, HG0 = 0, RW_COLS + FOX_COLS
                NWI = RW_COLS + HG_COLS
                wi = Tt(sub.sbuf("wi", [128, 8, NWI], BF16), name="wi")
                wo = Tt(sub.sbuf("wo", [128, 8, D], BF16), name="wo")
                for kc in range(8):
                    fw.dma("pool", wi.t[:, kc, 0:RW_COLS], I["w_in"][l][kc * 128:(kc + 1) * 128, 0:RW_COLS],
                           writes=[wi.r], stream="wld", group=True)
                    fw.dma("pool", wi.t[:, kc, RW_COLS:NWI], I["w_in"][l][kc * 128:(kc + 1) * 128, HG0:HG0 + HG_COLS],
                           writes=[wi.r], stream="wld", group=True)
                    fw.dma("pool", wo.t[:, kc, :], I["w_out"][l][kc * 128:(kc + 1) * 128, :], writes=[wo.r], stream="wld", group=True)
                xTa = [Tt(sub.sbuf("xTa%d" % i, [128, 8, WA], F32), nreg=8, name="xTa%d" % i) for i in range(1)]
                sq = Tt(sub.sbuf("sqa", [128, 8, WA], BF16), name="sqa")
                hT = Tt(sub.sbuf("hTa", [128, 8, WA], BF16), nreg=8, name="hTa")
                tmp = [Tt(sub.sbuf("tmpa%d" % i, [128, WA], F32), name="tmpa%d" % i) for i in range(2)]
                lnv = Tt(sub.sbuf("lnva", [128, WA], F32), name="lnva")
                rstd = Tt(sub.sbuf("rstda", [128, WA], F32), name="rstda")
                ymix = Tt(sub.sbuf("ymix", [128, 8, WA], BF16), nreg=8, name="ymix")
                fw.op("pool", lambda e: e.memset(ymix.t[:], 0.0), writes=ymix.rs)

                def S(name, shape=None, dt=F32, nreg=1):
                    return Tt(sub.sbuf(name, shape or [128, WA], dt), nreg=nreg, name=name)

                def proj_fm(c0, w, M=128):
                    pb = next_bank()
                    for kc in range(8):
                        sink[0].op("pe", lambda e, kc=kc: e.matmul(
                            pb.t[0:M, 0:w], lhsT=wi.t[:, kc, c0:c0 + M], rhs=hT.t[:, kc, 0:w],
                            start=(kc == 0), stop=(kc == 7)),
                            reads=[wi.r, hT.rs[kc]], writes=[pb.r], signal=(kc == 7))
                    return pb

                NCH = WA // 32
                hg_E = [S("hg_E%d" % i) for i in range(2)]
                hg_KK = [S("hg_KK%d" % i) for i in range(2)]
                hg_B = [S("hg_B%d" % i) for i in range(2)]
                hg_D = S("hg_D")
                hg_X = S("hg_X")
                hg_Q = S("hg_Q")
                hg_G = [S("hg_G%d" % i) for i in range(2)]
                hg_Qt = S("hg_Qt", [128, 2, WA], BF16, nreg=2)
                hg_Kh = S("hg_Kh", [128, 2, WA], BF16, nreg=2)
                hg_Ke = S("hg_Ke", [128, 2, WA], BF16, nreg=2)
                hg_ebl = S("hg_ebl", [128, 2, NCH], F32, nreg=2)
                hg_ebm = S("hg_ebm", [128, 2, NCH], F32, nreg=2)
                hg_Vh = S("hg_Vh", [128, WA // 128, 256], BF16, nreg=4)
                hg_KeT = S("hg_KeT", [128, WA // 128, 4, 256], BF16, nreg=4)
                hg_AT = S("hg_AT", [128, WA // 128, 2, 2, 128], BF16, nreg=4)
                hg_Sm = S("hg_Sm", [128, 2, 128], F32, nreg=2)
                hg_Sbd = S("hg_Sbd", [128, 2, 128], BF16, nreg=2)
                hg_sq = S("hg_sq", [128, WA], BF16)
                hg_t1 = S("hg_t1")

                def hg_init(s):
                    fw.op("pool", lambda e: e.memset(hg_Sm.t[:], 0.0), writes=hg_Sm.rs)
                    if s == 2:
                        for h in range(4):
                            hr = slice((h % 2) * 64, (h % 2) * 64 + 64)
                            fw.dma("sp", hg_Sm.t[hr, h // 2, (h % 2) * 64:(h % 2) * 64 + 64], I["shg"][l][h],
                                   writes=[hg_Sm.rs[h // 2]], stream="stld", group=True)

                def hg_final(s):
                    dst = O["hgp"][l][s] if s < 2 else O["hgs"][l]
                    for h in range(4):
                        hr = slice((h % 2) * 64, (h % 2) * 64 + 64)
                        fw.dma("sp", dst[h], hg_Sm.t[hr, h // 2, (h % 2) * 64:(h % 2) * 64 + 64],
                               reads=[hg_Sm.rs[h // 2]], stream="ststhg%d" % s, group=True)

                def hg_tile(s, w):
                    nch = w // 32
                    nb = (w + 127) // 128
                    pw = min(w, 128)
                    c_q, c_f, c_i, c_g = RW_COLS, RW_COLS + 256, RW_COLS + 512, RW_COLS + 768
                    for tb in range(nb):
                        pb = next_bank()
                        for kc in range(8):
                            sink[0].op("pe", lambda e, kc=kc, tb=tb, pb=pb: e.matmul(
                                pb.t[0:pw, 0:256], lhsT=hT.t[:, kc, tb * 128:tb * 128 + pw], rhs=wi.t[:, kc, c_i:c_i + 256],
                                start=(kc == 0), stop=(kc == 7)),
                                reads=[wi.r, hT.rs[kc]], writes=[pb.r], signal=(kc == 7))
                        sink[0].op("act", lambda e, tb=tb, pb=pb: e.activation(
                            out=hg_Vh.t[0:pw, tb, :], in_=pb.t[0:pw, 0:256], func=AF.Copy),
                            reads=[pb.r], writes=[hg_Vh.rs[tb]])
                    for pc in range(2):
                        E, KK, B, G = hg_E[pc], hg_KK[pc], hg_B[pc], hg_G[pc]
                        lb_ap = lbT.t[:, l, pc:pc + 1]
                        oml_ap = omlT.t[:, l, pc:pc + 1]
                        noml_ap = nomlT.t[:, l, pc:pc + 1]
                        pf = proj_fm(c_f + pc * 128, w)
                        sink[0].op("act", lambda e, pf=pf, E=E: e.activation(out=E.t[:, 0:w], in_=pf.t[:, 0:w], func=AF.Exp,
                                                                      scale=-1.0), reads=[pf.r], writes=[E.r])
                        sink[0].op("dve", lambda e, E=E: e.tensor_scalar_add(out=E.t[:, 0:w], in0=E.t[:, 0:w], scalar1=1.0),
                              reads=[E.r], writes=[E.r])
                        sink[0].op("dve", lambda e, E=E: e.reciprocal(out=E.t[:, 0:w], in_=E.t[:, 0:w]),
                              reads=[E.r], writes=[E.r])
                        sink[0].op("dve", lambda e, E=E, KK=KK, noml_ap=noml_ap, oml_ap=oml_ap: e.tensor_scalar(
                            out=KK.t[:, 0:w], in0=E.t[:, 0:w], scalar1=noml_ap, scalar2=oml_ap, op0=ALU.mult, op1=ALU.add),
                            reads=[E.r, nomlT.r, omlT.r], writes=[KK.r])
                        sink[0].op("act", lambda e, E=E, oml_ap=oml_ap, lb_ap=lb_ap: e.activation(
                            out=E.t[:, 0:w], in_=E.t[:, 0:w], func=AF.Ln, scale=oml_ap, bias=lb_ap),
                            reads=[E.r, omlT.r, lbT.r], writes=[E.r])
                        sink[0].op("dve", lambda e, E=E, B=B: e.tensor_tensor_scan(
                            out=B.t[:, 0:w], data0=rmask32.t[:, 0:w], data1=E.t[:, 0:w], initial=0.0,
                            op0=ALU.mult, op1=ALU.add), reads=[E.r, rmask32.r], writes=[B.r])
                        Bv = B.t[:, 0:w].rearrange("p (c t) -> p c t", t=32)
                        Dv = hg_D.t[:, 0:w].rearrange("p (c t) -> p c t", t=32)
                        sink[0].op("dve", lambda e, Bv=Bv, Dv=Dv: e.tensor_tensor(
                            out=Dv, in0=Bv, in1=Bv[:, :, 15:16].to_broadcast([128, nch, 32]), op=ALU.subtract),
                            reads=[B.r], writes=[hg_D.r])
                        pq = proj_fm(c_q + pc * 128, w)
                        sink[0].op("act", lambda e, pq=pq: e.activation(out=hg_Q.t[:, 0:w], in_=pq.t[:, 0:w], func=AF.Copy),
                              reads=[pq.r], writes=[hg_Q.r])
                        sink[0].op("act", lambda e: e.activation(out=hg_X.t[:, 0:w], in_=hg_D.t[:, 0:w], func=AF.Exp),
                              reads=[hg_D.r], writes=[hg_X.r])
                        sink[0].op("dve", lambda e, pc=pc: e.tensor_tensor(out=hg_Qt.t[:, pc, 0:w], in0=hg_Q.t[:, 0:w],
                                                                      in1=hg_X.t[:, 0:w], op=ALU.mult),
                              reads=[hg_Q.r, hg_X.r], writes=[hg_Qt.rs[pc]])
                        sink[0].op("act", lambda e: e.activation(out=hg_X.t[:, 0:w], in_=hg_D.t[:, 0:w], func=AF.Exp, scale=-1.0),
                              reads=[hg_D.r], writes=[hg_X.r])
                        sink[0].op("dve", lambda e, pc=pc, KK=KK: e.tensor_tensor(out=hg_Kh.t[:, pc, 0:w], in0=KK.t[:, 0:w],
                                                                             in1=hg_X.t[:, 0:w], op=ALU.mult),
                              reads=[KK.r, hg_X.r], writes=[hg_Kh.rs[pc]])
                        sink[0].op("dve", lambda e, Bv=Bv, Dv=Dv: e.tensor_tensor(
                            out=Dv, in0=Bv[:, :, 31:32].to_broadcast([128, nch, 32]), in1=Bv, op=ALU.subtract),
                            reads=[B.r], writes=[hg_D.r])
                        sink[0].op("act", lambda e: e.activation(out=hg_X.t[:, 0:w], in_=hg_D.t[:, 0:w], func=AF.Exp),
                              reads=[hg_D.r], writes=[hg_X.r])
                        sink[0].op("dve", lambda e, pc=pc, KK=KK: e.tensor_tensor(out=hg_Ke.t[:, pc, 0:w], in0=KK.t[:, 0:w],
                                                                             in1=hg_X.t[:, 0:w], op=ALU.mult),
                              reads=[KK.r, hg_X.r], writes=[hg_Ke.rs[pc]])
                        sink[0].op("act", lambda e, pc=pc, Bv=Bv: e.activation(out=hg_ebl.t[:, pc, 0:nch], in_=Bv[:, :, 31],
                                                                          func=AF.Exp), reads=[B.r], writes=[hg_ebl.rs[pc]])
                        sink[0].op("act", lambda e, pc=pc, Bv=Bv: e.activation(out=hg_ebm.t[:, pc, 0:nch], in_=Bv[:, :, 15],
                                                                          func=AF.Exp), reads=[B.r], writes=[hg_ebm.rs[pc]])
                        pg = proj_fm(c_g + pc * 128, w)
                        sink[0].op("act", lambda e, pg=pg, G=G: e.activation(out=G.t[:, 0:w], in_=pg.t[:, 0:w], func=AF.Silu),
                              reads=[pg.r], writes=[G.r])
                    HGDBG = int(os.environ.get("HGDBG", "9"))
                    if HGDBG < 2:
                        return
                    for tb in range(nb):
                        pAs = [next_bank(), next_bank()]
                        for h in range(4):
                            hr = slice((h % 2) * 64, (h % 2) * 64 + 64)
                            pA = pAs[h % 2]
                            sink[0].op("pe", lambda e, h=h, hr=hr, tb=tb, pA=pA: e.matmul(
                                pA.t[0:pw, (h // 2) * 128:(h // 2) * 128 + pw], lhsT=hg_Kh.t[hr, h // 2, tb * 128:tb * 128 + pw],
                                rhs=hg_Qt.t[hr, h // 2, tb * 128:tb * 128 + pw], start=True, stop=True),
                                reads=[hg_Kh.rs[h // 2], hg_Qt.rs[h // 2]], writes=[pA.r], signal=(h >= 2))
                        for par in range(2):
                            pA = pAs[par]
                            sink[0].op("dve", lambda e, tb=tb, pA=pA, par=par: e.tensor_tensor(
                                out=hg_AT.t[0:pw, tb, par, :, 0:pw],
                                in0=pA.t[0:pw, 0:256].rearrange("p (h t) -> p h t", h=2)[:, :, 0:pw],
                                in1=maskbd.t[0:pw, 0:pw].unsqueeze(1).to_broadcast([pw, 2, pw]), op=ALU.mult),
                                reads=[pA.r, maskbd.r], writes=[hg_AT.rs[tb]])
                        if os.environ.get("HGSUB", "") == "A":
                            continue
                        pT = next_bank()
                        pTb = pT.t[:, :].bitcast(BF16)
                        for pc in range(2):
                            sink[0].op("pe", lambda e, pc=pc, tb=tb, pTb=pTb: e.transpose(
                                pTb[0:pw, pc * 128:(pc + 1) * 128], hg_Ke.t[:, pc, tb * 128:tb * 128 + pw], identb.t[:, :]),
                                reads=[hg_Ke.rs[pc], identb.r], writes=[pT.r], signal=(pc == 1))
                        for cc in range(min(4, nch - tb * 4)):
                            sink[0].op("act", lambda e, tb=tb, pTb=pTb, cc=cc: e.activation(
                                out=hg_KeT.t[0:pw, tb, cc, :], in_=pTb[0:pw, 0:256], func=AF.Identity,
                                scale=maskbd.t[0:pw, cc * 32 + 31:cc * 32 + 32]),
                                reads=[pT.r, maskbd.r], writes=[hg_KeT.rs[tb]])
                    if HGDBG < 3:
                        return
                    po = [banks[6], banks[7]]
                    for tb in range(nb):
                        for h in range(4):
                            sink[0].op("pe", lambda e, h=h, tb=tb: e.matmul(
                                po[h // 2].t[(h % 2) * 64:(h % 2) * 64 + 64, tb * 128:tb * 128 + pw],
                                lhsT=hg_Vh.t[0:pw, tb, h * 64:(h + 1) * 64], rhs=hg_AT.t[0:pw, tb, h % 2, h // 2, 0:pw],
                                start=True, stop=False, skip_group_check=True),
                                reads=[hg_Vh.rs[tb], hg_AT.rs[tb]], writes=[po[h // 2].r], signal=False)
                        for cc in range(min(4, nch - tb * 4)):
                            c = tb * 4 + cc
                            last = (c == nch - 1)
                            for pc in range(2):
                                sink[0].op("act", lambda e, pc=pc, c=c: e.activation(
                                    out=hg_Sbd.t[:, pc, :], in_=hg_Sm.t[:, pc, :], func=AF.Identity,
                                    scale=hg_ebm.t[:, pc, c:c + 1]),
                                    reads=[hg_Sm.rs[pc], hg_ebm.rs[pc]], writes=[hg_Sbd.rs[pc]])
                                sink[0].op("pe", lambda e, pc=pc, c=c: e.matmul(
                                    po[pc].t[:, c * 32:(c + 1) * 32], lhsT=hg_Sbd.t[:, pc, :],
                                    rhs=hg_Qt.t[:, pc, c * 32:(c + 1) * 32], start=False, stop=True,
                                    skip_group_check=True),
                                    reads=[hg_Sbd.rs[pc], hg_Qt.rs[pc]], writes=[po[pc].r], signal=last)
                                pS = next_bank()
                                sink[0].op("pe", lambda e, pc=pc, tb=tb, cc=cc, pS=pS: e.matmul(
                                    pS.t[:, 0:128], lhsT=hg_KeT.t[0:pw, tb, cc, pc * 128:(pc + 1) * 128],
                                    rhs=hg_Vh.t[0:pw, tb, pc * 128:(pc + 1) * 128], start=True, stop=True),
                                    reads=[hg_KeT.rs[tb], hg_Vh.rs[tb]], writes=[pS.r])
                                for hh in range(2):
                                    hr = slice(hh * 64, hh * 64 + 64)
                                    sink[0].op("dve", lambda e, pc=pc, c=c, hr=hr, pS=pS: e.scalar_tensor_tensor(
                                        out=hg_Sm.t[hr, pc, hr], in0=hg_Sm.t[hr, pc, hr], scalar=hg_ebl.t[hr, pc, c:c + 1],
                                        in1=pS.t[hr, hr], op0=ALU.mult, op1=ALU.add),
                                        reads=[hg_Sm.rs[pc], hg_ebl.rs[pc], pS.r], writes=[hg_Sm.rs[pc]])
                    if HGDBG < 4:
                        return
                    for pc in range(2):
                        G = hg_G[pc]
                        sink[0].op("act", lambda e, pc=pc: e.activation(out=hg_sq.t[:, 0:w], in_=po[pc].t[:, 0:w], func=AF.Square),
                              reads=[po[pc].r], writes=[hg_sq.r])
                        pn = next_bank()
                        sink[0].op("pe", lambda e, pn=pn: e.matmul(pn.t[:, 0:w], lhsT=onesbd.t[:], rhs=hg_sq.t[:, 0:w],
                                                              start=True, stop=True),
                              reads=[onesbd.r, hg_sq.r], writes=[pn.r])
                        sink[0].op("act", lambda e, pn=pn: e.activation(out=hg_X.t[:, 0:w], in_=pn.t[:, 0:w], func=AF.Ln,
                                                                   scale=1.0 / 64, bias=epsb.t[:]),
                              reads=[pn.r, epsb.r], writes=[hg_X.r])
                        sink[0].op("act", lambda e: e.activation(out=hg_X.t[:, 0:w], in_=hg_X.t[:, 0:w], func=AF.Exp, scale=-0.5),
                              reads=[hg_X.r], writes=[hg_X.r])
                        sink[0].op("dve", lambda e, pc=pc: e.tensor_tensor(out=hg_t1.t[:, 0:w], in0=po[pc].t[:, 0:w],
                                                                      in1=hg_X.t[:, 0:w], op=ALU.mult),
                              reads=[po[pc].r, hg_X.r], writes=[hg_t1.r])
                        sink[0].op("dve", lambda e, pc=pc, G=G: e.scalar_tensor_tensor(
                            out=ymix.t[:, 6 + pc, 0:w], in0=hg_t1.t[:, 0:w], scalar=hgng.t[:, l, pc:pc + 1], in1=G.t[:, 0:w],
                            op0=ALU.mult, op1=ALU.mult), reads=[hg_t1.r, hgng.r, G.r], writes=[ymix.rs[6 + pc]])

                C0 = 0.6065306597126334
                rw_Pb = S("rw_Pb", [128, 9, WA + 1], F32)
                rw_car = S("rw_car", [128, 9], F32)
                rw_c7 = S("rw_c7", [128, 7], F32)
                rw_tmp = [S("rw_tmp%d" % i) for i in range(6)]
                rw_tmpB = [S("rw_tmpB%d" % i) for i in range(3)]
                rw_SIG2 = [S("rw_SIG%d" % i) for i in range(2)]
                rw_A2 = [S("rw_A%d" % i) for i in range(2)]
                rw_L2 = [S("rw_L%d" % i) for i in range(2)]
                rw_KP2 = [S("rw_KP%d" % i) for i in range(2)]
                rw_KN2 = [S("rw_KN%d" % i) for i in range(2)]
                rw_Bf2 = [S("rw_Bf%d" % i) for i in range(2)]
                rw_sqb2 = [S("rw_sqbb%d" % i, [128, WA], BF16) for i in range(2)]
                rw_G = [S("rw_G%d" % i) for i in range(2)]
                rw_Yf = S("rw_Yf")
                rw_TW = S("rw_TW", [32, WA], BF16)
                rw_AL = S("rw_AL", [32, WA], BF16)
                rw_SG = S("rw_SG", [64, WA], BF16)
                rw_sqb = S("rw_sqb", [128, WA], BF16)
                rw_Kh = S("rw_Kh", [128, 2, WA], BF16, nreg=2)
                rw_Bm = S("rw_Bm", [128, 2, 2, WA], BF16, nreg=2)
                rw_QRm = S("rw_QRm", [128, 2, 2, WA // 32, 2, 64], BF16, nreg=2)
                rw_Ke = S("rw_Ke", [128, 2, WA], BF16, nreg=2)
                rw_Be = S("rw_Be", [128, 2, WA], BF16, nreg=2)
                rw_Vb = S("rw_Vb", [128, 2, WA], BF16, nreg=2)
                NCR = WA // 32
                rw_Vt = S("rw_Vt", [64, NCR, 256], BF16)
                rw_KeT = S("rw_KeT", [64, NCR, 256], BF16)
                rw_BeT = S("rw_BeT", [64, NCR, 256], BF16)
                rw_gC = S("rw_gC", [128, 2, NCR], F32, nreg=2)
                NCK = max(1, WA // 64)
                rw_AT12 = [S("rw_AT12_%d" % i, [64, 4, 2, 64], BF16) for i in range(NCK)]
                rw_AT34 = [S("rw_AT34_%d" % i, [64, 4, 2, 64], BF16) for i in range(NCK)]
                rw_X = [[S("rw_X%d_%d" % (i, j), [64, 4, 64], BF16) for j in range(2)] for i in range(NCK)]
                rw_XT = [[S("rw_XT%d_%d" % (i, j), [64, 4, 64], BF16) for j in range(2)] for i in range(NCK)]
                rw_TT = [[S("rw_TT%d_%d" % (i, j), [64, 4, 64], BF16) for j in range(2)] for i in range(NCK)]
                rw_Zb = S("rw_Zb", [64, 256], BF16)
                rw_Un = S("rw_Un", [64, 256], BF16)
                rw_Hm = S("rw_Hm", [128, 2, 128], F32, nreg=2)
                rw_Hbd = S("rw_Hbd", [128, 2, 128], BF16, nreg=2)
                rw_st = S("rw_st", [128, 2, 128], F32)

                def rw_init(s):
                    fw.op("pool", lambda e: e.memset(rw_Hm.t[:], 0.0), writes=rw_Hm.rs)
                    fw.op("pool", lambda e: e.memset(rw_Pb.t[:], 0.0), writes=[rw_Pb.r])
                    if s == 2:
                        fw.op("pool", lambda e: e.memset(rw_st.t[:], 0.0), writes=[rw_st.r])
                        for h in range(4):
                            hr = slice((h % 2) * 64, (h % 2) * 64 + 64)
                            fw.dma("sp", rw_st.t[hr, h // 2, (h % 2) * 64:(h % 2) * 64 + 64], I["srw"][l][h],
                                   writes=[rw_st.r], stream="stld", group=True)
                        for pc in range(2):
                            pb = next_bank()
                            fw.op("pe", lambda e, pc=pc, pb=pb: e.transpose(pb.t[:, 0:128], rw_st.t[:, pc, :], ident.t[:]),
                                  reads=[rw_st.r, ident.r], writes=[pb.r])
                            fw.op("dve", lambda e, pc=pc, pb=pb: e.tensor_copy(out=rw_Hm.t[:, pc, :], in_=pb.t[:, 0:128]),
                                  reads=[pb.r], writes=[rw_Hm.rs[pc]])
                        fw.dma("sp", rw_c7.t[:, :], I["ssh"][l].rearrange("(c p) -> p c", p=128),
                               writes=[rw_c7.r], stream="stld", group=True, allow_slow_non_contiguous=True)
                        fw.op("dve", lambda e: e.tensor_copy(out=rw_Pb.t[:, 0:6, 0], in_=rw_c7.t[:, 0:6]),
                              reads=[rw_c7.r], writes=[rw_Pb.r])
                        fw.op("dve", lambda e: e.tensor_copy(out=rw_Pb.t[0:32, 6, 0:1], in_=rw_c7.t[0:32, 6:7]),
                              reads=[rw_c7.r], writes=[rw_Pb.r])
                        fw.op("dve", lambda e: e.tensor_copy(out=rw_Pb.t[0:32, 7, 0:1], in_=rw_c7.t[32:64, 6:7]),
                              reads=[rw_c7.r], writes=[rw_Pb.r])
                        fw.op("dve", lambda e: e.tensor_copy(out=rw_Pb.t[0:64, 8, 0:1], in_=rw_c7.t[64:128, 6:7]),
                              reads=[rw_c7.r], writes=[rw_Pb.r])
                    for pc in range(2):
                        fw.op("act", lambda e, pc=pc: e.activation(out=rw_Hbd.t[:, pc, :], in_=rw_Hm.t[:, pc, :],
                                                                   func=AF.Copy), reads=[rw_Hm.rs[pc]], writes=[rw_Hbd.rs[pc]])

                def rw_final(s, w):
                    dst = O["rwp"][l][s] if s < 2 else O["rws"][l]
                    for pc in range(2):
                        pb = next_bank()
                        fw.op("pe", lambda e, pc=pc, pb=pb: e.transpose(pb.t[:, 0:128], rw_Hm.t[:, pc, :], ident.t[:]),
                              reads=[rw_Hm.rs[pc], ident.r], writes=[pb.r])
                        fw.op("dve", lambda e, pc=pc, pb=pb: e.tensor_copy(out=rw_st.t[:, pc, :], in_=pb.t[:, 0:128]),
                              reads=[pb.r], writes=[rw_st.r])
                    for h in range(4):
                        hr = slice((h % 2) * 64, (h % 2) * 64 + 64)
                        fw.dma("sp", dst[h], rw_st.t[hr, h // 2, (h % 2) * 64:(h % 2) * 64 + 64], reads=[rw_st.r],
                               stream="ststrs%d" % s, group=True)
                    dsh = O["rshp"][l][s] if s < 2 else O["rshs"][l]
                    fw.op("dve", lambda e: e.tensor_copy(out=rw_c7.t[:, 0:6], in_=rw_car.t[:, 0:6]), reads=[rw_car.r], writes=[rw_c7.r])
                    fw.op("dve", lambda e: e.tensor_copy(out=rw_c7.t[0:32, 6:7], in_=rw_car.t[0:32, 6:7]), reads=[rw_car.r], writes=[rw_c7.r])
                    fw.op("dve", lambda e: e.tensor_copy(out=rw_c7.t[32:64, 6:7], in_=rw_car.t[0:32, 7:8]), reads=[rw_car.r], writes=[rw_c7.r])
                    fw.op("dve", lambda e: e.tensor_copy(out=rw_c7.t[64:128, 6:7], in_=rw_car.t[0:64, 8:9]), reads=[rw_car.r], writes=[rw_c7.r])
                    fw.dma("sp", dsh.rearrange("(c p) -> p c", p=128), rw_c7.t[:, :], reads=[rw_c7.r],
                           stream="ststrc%d" % s, group=True, allow_slow_non_contiguous=True)

                def rw_tile(s, w):
                    C = min(64, w)
                    nch = w // C
                    nlev = {64: 5, 32: 4}[C]
                    rmask = rmask64 if C == 64 else rmask32
                    T0, T1, T2, T3, T4, T5 = rw_tmp
                    specs = [(i, i * 128, 128) for i in range(6)] + [(6, 768, 32), (7, 800, 32), (8, 832, 64)]
                    for (i, c0, M) in specs:
                        pb = proj_fm(c0, w, M)
                        sink[0].op("act", lambda e, i=i, M=M, pb=pb: e.activation(out=rw_Pb.t[0:M, i, 1:1 + w], in_=pb.t[0:M, 0:w],
                                                                          func=AF.Copy), reads=[pb.r], writes=[rw_Pb.r])
                    sink[0].op("act", lambda e: e.activation(out=rw_car.t[:, :], in_=rw_Pb.t[:, :, w], func=AF.Copy),
                          reads=[rw_Pb.r], writes=[rw_car.r])
                    for (i, c0, M) in specs:
                        sink[0].op("dve", lambda e, i=i, M=M: e.tensor_tensor(out=T0.t[0:M, 0:w], in0=rw_Pb.t[0:M, i, 0:w],
                                                                         in1=rw_Pb.t[0:M, i, 1:1 + w], op=ALU.subtract),
                              reads=[rw_Pb.r], writes=[T0.r])
                        sink[0].op("dve", lambda e, i=i, M=M: e.scalar_tensor_tensor(
                            out=rw_Pb.t[0:M, i, 1:1 + w], in0=T0.t[0:M, 0:w], scalar=mul.t[0:M, l, i:i + 1],
                            in1=rw_Pb.t[0:M, i, 1:1 + w], op0=ALU.mult, op1=ALU.add),
                            reads=[T0.r, mul.r, rw_Pb.r], writes=[rw_Pb.r])
                    sink[0].op("pool", lambda e: e.tensor_copy(out=rw_Pb.t[:, :, 0], in_=rw_car.t[:, :]),
                          reads=[rw_car.r, rw_Pb.r], writes=[rw_Pb.r])
                    XS = lambda i, M=128: rw_Pb.t[0:M, i, 1:1 + w]
                    sink[0].op("act", lambda e: e.activation(out=rw_TW.t[:, 0:w], in_=XS(6, 32), func=AF.Tanh),
                          reads=[rw_Pb.r], writes=[rw_TW.r])
                    sink[0].op("act", lambda e: e.activation(out=rw_AL.t[:, 0:w], in_=XS(7, 32), func=AF.Copy),
                          reads=[rw_Pb.r], writes=[rw_AL.r])
                    sink[0].op("act", lambda e: e.activation(out=T0.t[0:64, 0:w], in_=XS(8, 64), func=AF.Exp, scale=-1.0),
                          reads=[rw_Pb.r], writes=[T0.r])
                    sink[0].op("dve", lambda e: e.tensor_scalar_add(out=T0.t[0:64, 0:w], in0=T0.t[0:64, 0:w], scalar1=1.0),
                          reads=[T0.r], writes=[T0.r])
                    sink[0].op("dve", lambda e: e.reciprocal(out=T0.t[0:64, 0:w], in_=T0.t[0:64, 0:w]), reads=[T0.r], writes=[T0.r])
                    sink[0].op("act", lambda e: e.activation(out=rw_SG.t[:, 0:w], in_=T0.t[0:64, 0:w], func=AF.Copy),
                          reads=[T0.r], writes=[rw_SG.r])
                    outer_sink = sink[0]
                    pc_recs = [Rec(), Rec()]

                    def _pc_body(pc, T1, T2, T3, rw_SIG, rw_A, rw_L, rw_KP, rw_KN, rw_Bf, rw_sqb):
                        cs = slice(pc * 128, (pc + 1) * 128)
                        r_ap, k_ap, v_ap = XS(pc), XS(2 + pc), XS(4 + pc)
                        pw_ = next_bank()
                        sink[0].op("pe", lambda e, pw_=pw_, cs=cs: e.matmul(pw_.t[:, 0:w], lhsT=w2b.t[:, l, cs], rhs=rw_TW.t[:, 0:w],
                                                                    start=True, stop=True), reads=[w2b.r, rw_TW.r], writes=[pw_.r])
                        sink[0].op("act", lambda e, pw_=pw_, pc=pc: e.activation(out=rw_SIG.t[:, 0:w], in_=pw_.t[:, 0:w], func=AF.Exp,
                                                                         scale=-1.0, bias=nw0.t[:, l, pc:pc + 1]),
                              reads=[pw_.r, nw0.r], writes=[rw_SIG.r])
                        sink[0].op("dve", lambda e: e.tensor_scalar_add(out=rw_SIG.t[:, 0:w], in0=rw_SIG.t[:, 0:w], scalar1=1.0),
                              reads=[rw_SIG.r], writes=[rw_SIG.r])
                        sink[0].op("dve", lambda e: e.reciprocal(out=rw_SIG.t[:, 0:w], in_=rw_SIG.t[:, 0:w]),
                              reads=[rw_SIG.r], writes=[rw_SIG.r])
                        pa_ = next_bank()
                        sink[0].op("pe", lambda e, pa_=pa_, cs=cs: e.matmul(pa_.t[:, 0:w], lhsT=a2b.t[:, l, cs], rhs=rw_AL.t[:, 0:w],
                                                                    start=True, stop=True), reads=[a2b.r, rw_AL.r], writes=[pa_.r])
                        sink[0].op("act", lambda e, pa_=pa_, pc=pc: e.activation(out=rw_A.t[:, 0:w], in_=pa_.t[:, 0:w], func=AF.Exp,
                                                                         scale=-1.0, bias=na0.t[:, l, pc:pc + 1]),
                              reads=[pa_.r, na0.r], writes=[rw_A.r])
                        sink[0].op("dve", lambda e: e.tensor_scalar_add(out=rw_A.t[:, 0:w], in0=rw_A.t[:, 0:w], scalar1=1.0),
                              reads=[rw_A.r], writes=[rw_A.r])
                        sink[0].op("dve", lambda e: e.reciprocal(out=rw_A.t[:, 0:w], in_=rw_A.t[:, 0:w]), reads=[rw_A.r], writes=[rw_A.r])
                        pg_ = next_bank()
                        sink[0].op("pe", lambda e, pg_=pg_, cs=cs: e.matmul(pg_.t[:, 0:w], lhsT=g2b.t[:, l, cs], rhs=rw_SG.t[:, 0:w],
                                                                    start=True, stop=True), reads=[g2b.r, rw_SG.r], writes=[pg_.r])
                        sink[0].op("act", lambda e, pg_=pg_, pc=pc: e.activation(out=rw_G[pc].t[:, 0:w], in_=pg_.t[:, 0:w], func=AF.Copy),
                              reads=[pg_.r], writes=[rw_G[pc].r])
                        sink[0].op("dve", lambda e, pc=pc, k_ap=k_ap: e.tensor_scalar_mul(
                            out=rw_KN.t[:, 0:w], in0=k_ap, scalar1=rwp["rw_k_k"].t[:, l, pc:pc + 1]),
                            reads=[rw_Pb.r, rwp["rw_k_k"].r], writes=[rw_KN.r])
                        sink[0].op("act", lambda e: e.activation(out=rw_sqb.t[:, 0:w], in_=rw_KN.t[:, 0:w], func=AF.Square),
                              reads=[rw_KN.r], writes=[rw_sqb.r])
                        pn = next_bank()
                        sink[0].op("pe", lambda e, pn=pn: e.matmul(pn.t[:, 0:w], lhsT=onesbd.t[:], rhs=rw_sqb.t[:, 0:w],
                                                              start=True, stop=True), reads=[onesbd.r, rw_sqb.r], writes=[pn.r])
                        sink[0].op("dve", lambda e, pn=pn: e.tensor_scalar_max(out=T1.t[:, 0:w], in0=pn.t[:, 0:w], scalar1=1e-24),
                              reads=[pn.r], writes=[T1.r])
                        sink[0].op("act", lambda e: e.activation(out=T1.t[:, 0:w], in_=T1.t[:, 0:w], func=AF.Ln), reads=[T1.r], writes=[T1.r])
                        sink[0].op("act", lambda e: e.activation(out=T1.t[:, 0:w], in_=T1.t[:, 0:w], func=AF.Exp, scale=-0.5),
                              reads=[T1.r], writes=[T1.r])
                        sink[0].op("dve", lambda e: e.tensor_tensor(out=rw_KN.t[:, 0:w], in0=rw_KN.t[:, 0:w], in1=T1.t[:, 0:w], op=ALU.mult),
                              reads=[rw_KN.r, T1.r], writes=[rw_KN.r])
                        sink[0].op("dve", lambda e, pc=pc: e.tensor_scalar(
                            out=T1.t[:, 0:w], in0=rw_A.t[:, 0:w], scalar1=rwp["rw_k_a"].t[:, l, pc:pc + 1],
                            scalar2=omka.t[:, l, pc:pc + 1], op0=ALU.mult, op1=ALU.add),
                            reads=[rw_A.r, rwp["rw_k_a"].r, omka.r], writes=[T1.r])
                        sink[0].op("dve", lambda e, k_ap=k_ap: e.tensor_tensor(out=rw_KP.t[:, 0:w], in0=k_ap, in1=T1.t[:, 0:w], op=ALU.mult),
                              reads=[rw_Pb.r, T1.r], writes=[rw_KP.r])
                        sink[0].op("dve", lambda e: e.tensor_tensor(out=rw_Bf.t[:, 0:w], in0=rw_KN.t[:, 0:w], in1=rw_A.t[:, 0:w], op=ALU.mult),
                              reads=[rw_KN.r, rw_A.r], writes=[rw_Bf.r])
                        sink[0].op("dve", lambda e: e.tensor_tensor_scan(out=rw_L.t[:, 0:w], data0=rmask.t[:, 0:w], data1=rw_SIG.t[:, 0:w],
                                                                   initial=0.0, op0=ALU.mult, op1=ALU.add),
                              reads=[rw_SIG.r, rmask.r], writes=[rw_L.r])
                        Lv = rw_L.t[:, 0:w].rearrange("p (c t) -> p c t", t=C)
                        sink[0].op("act", lambda e: e.activation(out=T2.t[:, 0:w], in_=rw_L.t[:, 0:w], func=AF.Exp, scale=-C0),
                              reads=[rw_L.r], writes=[T2.r])
                        sink[0].op("dve", lambda e: e.tensor_tensor(out=T3.t[:, 0:w], in0=rw_L.t[:, 0:w], in1=rw_SIG.t[:, 0:w], op=ALU.subtract),
                              reads=[rw_L.r, rw_SIG.r], writes=[T3.r])
                        sink[0].op("act", lambda e: e.activation(out=T3.t[:, 0:w], in_=T3.t[:, 0:w], func=AF.Exp, scale=-C0),
                              reads=[T3.r], writes=[T3.r])
                        for par in range(2):
                            hm = onesbdf.t[:, par * 64:par * 64 + 1]
                            sink[0].op("dve", lambda e, pc=pc, par=par, hm=hm, r_ap=r_ap: e.scalar_tensor_tensor(
                                out=rw_QRm.t[:, par, pc, 0:nch, 1, 0:C], in0=r_ap.rearrange("p (c t) -> p c t", t=C), scalar=hm,
                                in1=T2.t[:, 0:w].rearrange("p (c t) -> p c t", t=C), op0=ALU.mult, op1=ALU.mult),
                                reads=[rw_Pb.r, T2.r, onesbdf.r], writes=[rw_QRm.rs[pc]])
                            sink[0].op("dve", lambda e, pc=pc, par=par, hm=hm: e.scalar_tensor_tensor(
                                out=rw_QRm.t[:, par, pc, 0:nch, 0, 0:C], in0=rw_KN.t[:, 0:w].rearrange("p (c t) -> p c t", t=C),
                                scalar=hm, in1=T3.t[:, 0:w].rearrange("p (c t) -> p c t", t=C), op0=ALU.mult, op1=ALU.mult),
                                reads=[rw_KN.r, T3.r, onesbdf.r], writes=[rw_QRm.rs[pc]])
                        sink[0].op("act", lambda e: e.activation(out=T2.t[:, 0:w], in_=rw_L.t[:, 0:w], func=AF.Exp, scale=C0),
                              reads=[rw_L.r], writes=[T2.r])
                        sink[0].op("dve", lambda e, pc=pc: e.tensor_tensor(out=rw_Kh.t[:, pc, 0:w], in0=rw_KP.t[:, 0:w], in1=T2.t[:, 0:w], op=ALU.mult),
                              reads=[rw_KP.r, T2.r], writes=[rw_Kh.rs[pc]])
                        for par in range(2):
                            hm = onesbdf.t[:, par * 64:par * 64 + 1]
                            sink[0].op("dve", lambda e, pc=pc, par=par, hm=hm: e.scalar_tensor_tensor(
                                out=rw_Bm.t[:, par, pc, 0:w], in0=rw_Bf.t[:, 0:w], scalar=hm, in1=T2.t[:, 0:w],
                                op0=ALU.mult, op1=ALU.mult), reads=[rw_Bf.r, T2.r, onesbdf.r], writes=[rw_Bm.rs[pc]])
                        sink[0].op("dve", lambda e, Lv=Lv: e.tensor_tensor(
                            out=T3.t[:, 0:w].rearrange("p (c t) -> p c t", t=C), in0=Lv[:, :, C - 1:C].to_broadcast([128, nch, C]),
                            in1=Lv, op=ALU.subtract), reads=[rw_L.r], writes=[T3.r])
                        sink[0].op("act", lambda e: e.activation(out=T3.t[:, 0:w], in_=T3.t[:, 0:w], func=AF.Exp, scale=-C0),
                              reads=[T3.r], writes=[T3.r])
                        sink[0].op("dve", lambda e, pc=pc: e.tensor_tensor(out=rw_Ke.t[:, pc, 0:w], in0=rw_KP.t[:, 0:w], in1=T3.t[:, 0:w], op=ALU.mult),
                              reads=[rw_KP.r, T3.r], writes=[rw_Ke.rs[pc]])
                        sink[0].op("pool", lambda e, pc=pc: e.tensor_tensor(out=rw_Be.t[:, pc, 0:w], in0=rw_Bf.t[:, 0:w], in1=T3.t[:, 0:w], op=ALU.mult),
                              reads=[rw_Bf.r, T3.r], writes=[rw_Be.rs[pc]])
                        sink[0].op("act", lambda e, pc=pc, Lv=Lv: e.activation(out=rw_gC.t[:, pc, 0:nch], in_=Lv[:, :, C - 1], func=AF.Exp,
                                                                       scale=-C0), reads=[rw_L.r], writes=[rw_gC.rs[pc]])
                        sink[0].op("act", lambda e, pc=pc, v_ap=v_ap: e.activation(out=rw_Vb.t[:, pc, 0:w], in_=v_ap, func=AF.Copy),
                              reads=[rw_Pb.r], writes=[rw_Vb.rs[pc]])
                        sink[0].op("dve", lambda e, pc=pc, r_ap=r_ap: e.scalar_tensor_tensor(
                            out=(T4 if pc == 0 else T5).t[:, 0:w], in0=r_ap, scalar=rwp["rw_r_k"].t[:, l, pc:pc + 1],
                            in1=rw_KP.t[:, 0:w], op0=ALU.mult, op1=ALU.mult),
                            reads=[rw_Pb.r, rwp["rw_r_k"].r, rw_KP.r], writes=[(T4 if pc == 0 else T5).r])
                    for pc in range(2):
                        sink[0] = pc_recs[pc]
                        tt = (T1, T2, T3) if pc == 0 else tuple(rw_tmpB)
                        _pc_body(pc, tt[0], tt[1], tt[2], rw_SIG2[pc], rw_A2[pc], rw_L2[pc], rw_KP2[pc], rw_KN2[pc],
                                 rw_Bf2[pc], rw_sqb2[pc])
                    sink[0] = outer_sink
                    merge_recs(outer_sink, pc_recs)
                    RWDBG = int(os.environ.get("RWDBG", "9"))
                    if RWDBG < 2:
                        return
                    for (src, dstT) in ((rw_Vb, rw_Vt), (rw_Ke, rw_KeT), (rw_Be, rw_BeT)):
                        for c0 in range(0, nch, 4):
                            pT = next_bank()
                            pTb = pT.t[:, :].bitcast(BF16)
                            ncc = min(4, nch - c0)
                            for ci in range(ncc):
                                c = c0 + ci
                                for pc in range(2):
                                    sink[0].op("pe", lambda e, src=src, c=c, ci=ci, pc=pc, pTb=pTb: e.transpose(
                                        pTb[0:C, ci * 256 + pc * 128:ci * 256 + (pc + 1) * 128], src.t[:, pc, c * C:(c + 1) * C],
                                        identb.t[:, :]), reads=[src.rs[pc], identb.r], writes=[pT.r],
                                        signal=(ci == ncc - 1 and pc == 1))
                            sink[0].op("act", lambda e, dstT=dstT, c0=c0, ncc=ncc, pTb=pTb: e.activation(
                                out=dstT.t[0:C, c0:c0 + ncc, :], in_=pTb[0:C, 0:ncc * 256].rearrange("p (c n) -> p c n", n=256),
                                func=AF.Copy), reads=[pT.r], writes=[dstT.r])
                    if RWDBG < 3:
                        return
                    class _V:
                        pass
                    py = [_V(), _V()]
                    for pc_ in range(2):
                        py[pc_].t = banks[5].t[:, pc_ * WA:(pc_ + 1) * WA]
                        py[pc_].r = banks[5].r
                    v4 = lambda ap: ap.rearrange("p (h a t) -> p h a t", h=4, a=2)[:, :, :, 0:C]
                    v3 = lambda ap: ap.rearrange("p (h t) -> p h t", h=4)[:, :, 0:C]
                    for c in range(nch):
                        p12, p34, p5 = next_bank(), next_bank(), next_bank()
                        AT12, AT34 = rw_AT12[c], rw_AT34[c]
                        for h in range(4):
                            par, pc = h % 2, h // 2
                            for a_ in range(2):
                                qr = rw_QRm.t[:, par, pc, c, a_, 0:C]
                                sink[0].op("pe", lambda e, c=c, h=h, pc=pc, qr=qr, p12=p12, a_=a_: e.matmul(
                                    p12.t[0:C, h * 128 + a_ * 64:h * 128 + a_ * 64 + C], lhsT=rw_Kh.t[:, pc, c * C:(c + 1) * C],
                                    rhs=qr, start=True, stop=True), reads=[rw_Kh.rs[pc], rw_QRm.rs[pc]], writes=[p12.r],
                                    signal=(h == 3 and a_ == 1))
                                sink[0].op("pe", lambda e, c=c, h=h, par=par, pc=pc, qr=qr, p34=p34, a_=a_: e.matmul(
                                    p34.t[0:C, h * 128 + a_ * 64:h * 128 + a_ * 64 + C],
                                    lhsT=rw_Bm.t[:, par, pc, c * C:(c + 1) * C], rhs=qr, start=True, stop=True),
                                    reads=[rw_Bm.rs[pc], rw_QRm.rs[pc]], writes=[p34.r], signal=(h == 3 and a_ == 1))
                            sink[0].op("pe", lambda e, h=h, par=par, pc=pc, c=c, p5=p5: e.matmul(
                                p5.t[0:C, h * 64:h * 64 + C], lhsT=rw_QRm.t[:, par, pc, c, 0, 0:C],
                                rhs=rw_Bm.t[:, par, pc, c * C:(c + 1) * C], start=True, stop=True),
                                reads=[rw_Bm.rs[pc], rw_QRm.rs[pc]], writes=[p5.r], signal=(h == 3))
                        sink[0].op("dve", lambda e, p12=p12, AT12=AT12: e.tensor_tensor(
                            out=AT12.t[0:C, :, :, 0:C], in0=v4(p12.t[0:C, :]),
                            in1=mask12.t[0:C, :, 0:C].unsqueeze(1).to_broadcast([C, 4, 2, C]), op=ALU.mult),
                            reads=[p12.r, mask12.r], writes=[AT12.r])
                        sink[0].op("dve", lambda e, p34=p34, AT34=AT34: e.tensor_tensor(
                            out=AT34.t[0:C, :, :, 0:C], in0=v4(p34.t[0:C, :]),
                            in1=mask34.t[0:C, :, 0:C].unsqueeze(1).to_broadcast([C, 4, 2, C]), op=ALU.mult),
                            reads=[p34.r, mask34.r], writes=[AT34.r])
                        X, XT, TT = rw_X[c][0], rw_XT[c][0], rw_TT[c][0]
                        sink[0].op("dve", lambda e, p5=p5, X=X: e.tensor_tensor(
                            out=X.t[0:C, :, 0:C], in0=v3(p5.t[0:C, 0:256]),
                            in1=mask5.t[0:C, 0:C].unsqueeze(1).to_broadcast([C, 4, C]), op=ALU.mult),
                            reads=[p5.r, mask5.r], writes=[X.r])
                        sink[0].op("act", lambda e, XT=XT, AT34=AT34: e.activation(out=XT.t[0:C, :, 0:C], in_=AT34.t[0:C, :, 0, 0:C], func=AF.Copy),
                              reads=[AT34.r], writes=[XT.r])
                        sink[0].op("pool", lambda e, TT=TT, AT34=AT34: e.tensor_tensor(
                            out=TT.t[0:C, :, 0:C], in0=AT34.t[0:C, :, 0, 0:C],
                            in1=identb.t[0:C, 0:C].unsqueeze(1).to_broadcast([C, 4, C]), op=ALU.add),
                            reads=[AT34.r, identb.r], writes=[TT.r])
                    cur = 0
                    for lev in range(nlev):
                        for c in range(nch):
                            Xn, XTn, TTn = rw_X[c][1 - cur], rw_XT[c][1 - cur], rw_TT[c][1 - cur]
                            X, XT, TT = rw_X[c][cur], rw_XT[c][cur], rw_TT[c][cur]
                            px, pxt, ptt = next_bank(), next_bank(), next_bank()
                            for h in range(4):
                                sink[0].op("pe", lambda e, h=h, px=px, X=X, XT=XT: e.matmul(
                                    px.t[0:C, h * 64:h * 64 + C], lhsT=XT.t[0:C, h, 0:C], rhs=X.t[0:C, h, 0:C], start=True, stop=True),
                                    reads=[X.r, XT.r], writes=[px.r], signal=(h == 3))
                            sink[0].op("act", lambda e, px=px, Xn=Xn: e.activation(
                                out=Xn.t[0:C, :, 0:C], in_=v3(px.t[0:C, 0:256]), func=AF.Copy), reads=[px.r], writes=[Xn.r])
                            if lev < nlev - 1:
                                for h in range(4):
                                    sink[0].op("pe", lambda e, h=h, pxt=pxt, X=X, XT=XT: e.matmul(
                                        pxt.t[0:C, h * 64:h * 64 + C], lhsT=X.t[0:C, h, 0:C], rhs=XT.t[0:C, h, 0:C], start=True, stop=True),
                                        reads=[X.r, XT.r], writes=[pxt.r], signal=(h == 3))
                                sink[0].op("act" if c % 2 else "dve", (lambda e, pxt=pxt, XTn=XTn: e.activation(
                                    out=XTn.t[0:C, :, 0:C], in_=v3(pxt.t[0:C, 0:256]), func=AF.Copy)) if c % 2 else
                                    (lambda e, pxt=pxt, XTn=XTn: e.tensor_copy(out=XTn.t[0:C, :, 0:C], in_=v3(pxt.t[0:C, 0:256]))),
                                    reads=[pxt.r], writes=[XTn.r])
                            for h in range(4):
                                sink[0].op("pe", lambda e, h=h, ptt=ptt, Xn=Xn, TT=TT: e.matmul(
                                    ptt.t[0:C, h * 64:h * 64 + C], lhsT=Xn.t[0:C, h, 0:C], rhs=TT.t[0:C, h, 0:C], start=True, stop=True),
                                    reads=[Xn.r, TT.r], writes=[ptt.r], signal=(h == 3))
                            sink[0].op("dve", lambda e, ptt=ptt, TT=TT, TTn=TTn: e.tensor_tensor(
                                out=TTn.t[0:C, :, 0:C], in0=v3(ptt.t[0:C, 0:256]), in1=TT.t[0:C, :, 0:C], op=ALU.add),
                                reads=[ptt.r, TT.r], writes=[TTn.r])
                        cur = 1 - cur
                    if RWDBG < 4:
                        return
                    for c in range(nch):
                        TTf = rw_TT[c][cur]
                        AT12, AT34 = rw_AT12[c], rw_AT34[c]
                        pz, pu, ph = next_bank(), next_bank(), next_bank()
                        for pc in range(2):
                            for par in range(2):
                                sink[0].op("pe", lambda e, c=c, AT12=AT12, AT34=AT34, pc=pc, par=par, pz=pz: e.matmul(
                                    pz.t[0:C, pc * 128:(pc + 1) * 128], lhsT=rw_QRm.t[:, par, pc, c, 0, 0:C], rhs=rw_Hbd.t[:, pc, :],
                                    start=(par == 0), stop=False, skip_group_check=True),
                                    reads=[rw_QRm.rs[pc], rw_Hbd.rs[pc]], writes=[pz.r], signal=False)
                            for h in (2 * pc, 2 * pc + 1):
                                sink[0].op("pe", lambda e, c=c, AT12=AT12, AT34=AT34, h=h, pz=pz: e.matmul(
                                    pz.t[0:C, h * 64:(h + 1) * 64], lhsT=AT12.t[0:C, h, 0, 0:C], rhs=rw_Vt.t[0:C, c, h * 64:(h + 1) * 64],
                                    start=False, stop=True, skip_group_check=True),
                                    reads=[AT12.r, rw_Vt.r], writes=[pz.r], signal=(h == 3))
                        sink[0].op("act", lambda e, c=c, AT12=AT12, AT34=AT34, pz=pz: e.activation(out=rw_Zb.t[0:C, :], in_=pz.t[0:C, 0:256], func=AF.Copy),
                              reads=[pz.r], writes=[rw_Zb.r])
                        for h in range(4):
                            sink[0].op("pe", lambda e, c=c, AT12=AT12, AT34=AT34, h=h, pu=pu, TTf=TTf: e.matmul(
                                pu.t[0:C, h * 64:(h + 1) * 64], lhsT=TTf.t[0:C, h, 0:C], rhs=rw_Zb.t[0:C, h * 64:(h + 1) * 64],
                                start=True, stop=True), reads=[TTf.r, rw_Zb.r], writes=[pu.r], signal=(h == 3))
                        sink[0].op("dve", lambda e, c=c, AT12=AT12, AT34=AT34, pu=pu: e.tensor_scalar_mul(out=rw_Un.t[0:C, :], in0=pu.t[0:C, 0:256], scalar1=-1.0),
                              reads=[pu.r], writes=[rw_Un.r])
                        for pc in range(2):
                          for par in range(2):
                                sink[0].op("pe", lambda e, c=c, AT12=AT12, AT34=AT34, pc=pc, par=par: e.matmul(
                                    py[pc].t[:, c * C:(c + 1) * C], lhsT=rw_Hbd.t[:, pc, :], rhs=rw_QRm.t[:, par, pc, c, 1, 0:C],
                                    start=(par == 0), stop=False, skip_group_check=True),
                                    reads=[rw_QRm.rs[pc], rw_Hbd.rs[pc]], writes=[py[pc].r], signal=False)
                          for h in (2 * pc, 2 * pc + 1):
                            hs = slice((h % 2) * 64, (h % 2) * 64 + 64)
                            sink[0].op("pe", lambda e, c=c, AT12=AT12, AT34=AT34, h=h, pc=pc, hs=hs: e.matmul(
                                py[pc].t[hs, c * C:(c + 1) * C], lhsT=rw_Vt.t[0:C, c, h * 64:(h + 1) * 64], rhs=AT12.t[0:C, h, 1, 0:C],
                                start=False, stop=False, skip_group_check=True),
                                reads=[rw_Vt.r, AT12.r], writes=[py[pc].r], signal=False)
                            sink[0].op("pe", lambda e, c=c, AT12=AT12, AT34=AT34, h=h, pc=pc, hs=hs: e.matmul(
                                py[pc].t[hs, c * C:(c + 1) * C], lhsT=rw_Un.t[0:C, h * 64:(h + 1) * 64], rhs=AT34.t[0:C, h, 1, 0:C],
                                start=False, stop=True, skip_group_check=True),
                                reads=[rw_Un.r, AT34.r], writes=[py[pc].r], signal=(h % 2 == 1))
                        for pc in range(2):
                            cs = slice(pc * 128, (pc + 1) * 128)
                            sink[0].op("pe", lambda e, c=c, AT12=AT12, AT34=AT34, pc=pc, cs=cs, ph=ph: e.matmul(
                                ph.t[:, cs], lhsT=rw_KeT.t[0:C, c, cs], rhs=rw_Vt.t[0:C, c, cs], start=True, stop=False),
                                reads=[rw_KeT.r, rw_Vt.r], writes=[ph.r], signal=False)
                            sink[0].op("pe", lambda e, c=c, AT12=AT12, AT34=AT34, pc=pc, cs=cs, ph=ph: e.matmul(
                                ph.t[:, cs], lhsT=rw_BeT.t[0:C, c, cs], rhs=rw_Un.t[0:C, cs], start=False, stop=True),
                                reads=[rw_BeT.r, rw_Un.r], writes=[ph.r])
                            for hh in range(2):
                                hr = slice(hh * 64, hh * 64 + 64)
                                sink[0].op("dve", lambda e, c=c, AT12=AT12, AT34=AT34, pc=pc, hr=hr, hh=hh, ph=ph: e.scalar_tensor_tensor(
                                    out=rw_Hm.t[hr, pc, hr], in0=rw_Hm.t[hr, pc, hr], scalar=rw_gC.t[hr, pc, c:c + 1],
                                    in1=ph.t[hr, pc * 128 + hh * 64:pc * 128 + hh * 64 + 64], op0=ALU.mult, op1=ALU.add),
                                    reads=[rw_Hm.rs[pc], rw_gC.rs[pc], ph.r], writes=[rw_Hm.rs[pc]])
                            sink[0].op("act", lambda e, c=c, AT12=AT12, AT34=AT34, pc=pc: e.activation(out=rw_Hbd.t[:, pc, :], in_=rw_Hm.t[:, pc, :], func=AF.Copy),
                                  reads=[rw_Hm.rs[pc]], writes=[rw_Hbd.rs[pc]])
                    if RWDBG < 5:
                        return
                    for pc in range(2):
                        v_ap = XS(4 + pc)
                        TB = T4 if pc == 0 else T5
                        sink[0].op("act", lambda e, pc=pc: e.activation(out=rw_Yf.t[:, 0:w], in_=py[pc].t[:, 0:w], func=AF.Copy),
                              reads=[py[pc].r], writes=[rw_Yf.r])
                        sink[0].op("act", lambda e: e.activation(out=rw_sqb.t[:, 0:w], in_=rw_Yf.t[:, 0:w], func=AF.Copy),
                              reads=[rw_Yf.r], writes=[rw_sqb.r])
                        pm = next_bank()
                        sink[0].op("pe", lambda e, pm=pm: e.matmul(pm.t[:, 0:w], lhsT=onesbd.t[:], rhs=rw_sqb.t[:, 0:w], start=True, stop=True),
                              reads=[onesbd.r, rw_sqb.r], writes=[pm.r])
                        sink[0].op("dve", lambda e, pm=pm: e.scalar_tensor_tensor(
                            out=rw_Yf.t[:, 0:w], in0=pm.t[:, 0:w], scalar=-1.0 / 64, in1=rw_Yf.t[:, 0:w], op0=ALU.mult, op1=ALU.add),
                            reads=[pm.r, rw_Yf.r], writes=[rw_Yf.r])
                        sink[0].op("act", lambda e: e.activation(out=rw_sqb.t[:, 0:w], in_=rw_Yf.t[:, 0:w], func=AF.Square),
                              reads=[rw_Yf.r], writes=[rw_sqb.r])
                        pv = next_bank()
                        sink[0].op("pe", lambda e, pv=pv: e.matmul(pv.t[:, 0:w], lhsT=onesbd.t[:], rhs=rw_sqb.t[:, 0:w], start=True, stop=True),
                              reads=[onesbd.r, rw_sqb.r], writes=[pv.r])
                        sink[0].op("act", lambda e, pv=pv: e.activation(out=T0.t[:, 0:w], in_=pv.t[:, 0:w], func=AF.Ln, scale=1.0 / 64,
                                                                   bias=eps2.t[:]), reads=[pv.r, eps2.r], writes=[T0.r])
                        sink[0].op("act", lambda e: e.activation(out=T0.t[:, 0:w], in_=T0.t[:, 0:w], func=AF.Exp, scale=-0.5),
                              reads=[T0.r], writes=[T0.r])
                        sink[0].op("dve", lambda e: e.tensor_tensor(out=rw_Yf.t[:, 0:w], in0=rw_Yf.t[:, 0:w], in1=T0.t[:, 0:w], op=ALU.mult),
                              reads=[rw_Yf.r, T0.r], writes=[rw_Yf.r])
                        sink[0].op("dve", lambda e, pc=pc: e.tensor_scalar(
                            out=rw_Yf.t[:, 0:w], in0=rw_Yf.t[:, 0:w], scalar1=rwp["rw_ln_w"].t[:, l, pc:pc + 1],
                            scalar2=rwp["rw_ln_b"].t[:, l, pc:pc + 1], op0=ALU.mult, op1=ALU.add),
                            reads=[rw_Yf.r, rwp["rw_ln_w"].r, rwp["rw_ln_b"].r], writes=[rw_Yf.r])
                        sink[0].op("act", lambda e, TB=TB: e.activation(out=rw_sqb.t[:, 0:w], in_=TB.t[:, 0:w], func=AF.Copy),
                              reads=[TB.r], writes=[rw_sqb.r])
                        pbn = next_bank()
                        sink[0].op("pe", lambda e, pbn=pbn: e.matmul(pbn.t[:, 0:w], lhsT=onesbd.t[:], rhs=rw_sqb.t[:, 0:w], start=True, stop=True),
                              reads=[onesbd.r, rw_sqb.r], writes=[pbn.r])
                        sink[0].op("dve", lambda e, pbn=pbn, v_ap=v_ap: e.tensor_tensor(out=T0.t[:, 0:w], in0=pbn.t[:, 0:w], in1=v_ap, op=ALU.mult),
                              reads=[pbn.r, rw_Pb.r], writes=[T0.r])
                        sink[0].op("dve", lambda e: e.tensor_tensor(out=rw_Yf.t[:, 0:w], in0=rw_Yf.t[:, 0:w], in1=T0.t[:, 0:w], op=ALU.add),
                              reads=[rw_Yf.r, T0.r], writes=[rw_Yf.r])
                        sink[0].op("dve", lambda e, pc=pc: e.tensor_tensor(out=ymix.t[:, pc, 0:w], in0=rw_Yf.t[:, 0:w], in1=rw_G[pc].t[:, 0:w],
                                                                      op=ALU.mult), reads=[rw_Yf.r, rw_G[pc].r], writes=[ymix.rs[pc]])

                RWKV_TILE = rw_tile

                it = 0
                for s in range(3):
                    hg_init(s)
                    if stage >= 4:
                        rw_init(s)

                    def a2_tile(s, t0, w, j, xT, l=l):
                        load_xT(xT, t0, w)
                        fw.dma("pool", ymix.t[:, 2:6, 0:w], yfox.rearrange("(c p) t -> p c t", p=128)[:, :, t0:t0 + w],
                               reads=[yfreg(t0)], writes=ymix.rs[2:6], stream="yfld")
                        norm_fm(xT, w, sq, tmp, lnv, rstd, hT,
                                lambda c, s=s: G1.t[:, l, s, c:c + 1], lambda c, s=s: mod.t[:, l, 0, s, c:c + 1], hT.rs)
                        recs = []
                        if stage >= 3:
                            sink[0] = Rec()
                            recs.append(sink[0])
                            default_pool[0] = (0, 1)
                            hg_tile(s, w)
                        if stage >= 4 and RWKV_TILE is not None:
                            sink[0] = Rec()
                            recs.append(sink[0])
                            default_pool[0] = (2, 3, 4)
                            RWKV_TILE(s, w)
                        sink[0] = fw
                        default_pool[0] = (0, 1, 2, 3, 4)
                        merge_recs(fw, recs)
                        for c in range(8):
                            po_ = next_bank()
                            for kc in range(8):
                                fw.op("pe", lambda e, c=c, kc=kc, po_=po_: e.matmul(
                                    po_.t[:, 0:w], lhsT=wo.t[:, kc, c * 128:(c + 1) * 128], rhs=ymix.t[:, kc, 0:w],
                                    start=(kc == 0), stop=(kc == 7)),
                                    reads=[wo.r, ymix.rs[kc]], writes=[po_.r], signal=(kc == 7))
                            fw.op("dve", lambda e, c=c, po_=po_: e.scalar_tensor_tensor(
                                out=xT.t[:, c, 0:w], in0=po_.t[:, 0:w], scalar=mod.t[:, l, 2, s, c:c + 1],
                                in1=xT.t[:, c, 0:w], op0=ALU.mult, op1=ALU.add),
                                reads=[po_.r, mod.r, xT.rs[c]], writes=[xT.rs[c]])
                        store_xT(xT, t0, w)
                    for (t0, w, j) in tiles_of(s, WA):
                        xT = xTa[0]
                        it += 1
                        a2_tile(s, t0, w, j, xT)
                    if stage >= 3:
                        hg_final(s)
                    if stage >= 4:
                        rw_final(s, w)
                fw.flush()
                default_pool[0] = (0, 1, 2, 3, 4, 5, 6, 7)
            with contextlib.ExitStack() as ph:
                sub = FWScope(fw, ph)
                WB = 256
                wfi = Tt(sub.sbuf("wfi", [128, 8, 2 * DFF], BF16), name="wfi")
                wfo = Tt(sub.sbuf("wfo", [128, 22, D], BF16), name="wfo")
                for kc in range(8):
                    fw.dma("pool", wfi.t[:, kc, :], I["w_ffn_in"][l][kc * 128:(kc + 1) * 128, :], writes=[wfi.r],
                           stream="wld", group=True)
                for kc in range(22):
                    fw.dma("pool", wfo.t[:, kc, :], I["w_ffn_out"][l][kc * 128:(kc + 1) * 128, :], writes=[wfo.r],
                           stream="wld", group=True)
                xTb = [Tt(sub.sbuf("xTb%d" % i, [128, 8, WB], F32), nreg=8, name="xTb%d" % i) for i in range(2)]
                sq = Tt(sub.sbuf("sqb", [128, 8, WB], BF16), name="sqb")
                hT = Tt(sub.sbuf("hTb", [128, 8, WB], BF16), nreg=8, name="hTb")
                tmp = [Tt(sub.sbuf("tmpb%d" % i, [128, WB], F32), name="tmpb%d" % i) for i in range(2)]
                lnv = Tt(sub.sbuf("lnvb", [128, WB], F32), name="lnvb")
                rstd = Tt(sub.sbuf("rstdb", [128, WB], F32), name="rstdb")
                actT = Tt(sub.sbuf("actT", [128, 22, WB], BF16), nreg=22, name="actT")
                sg = [Tt(sub.sbuf("sg%d" % i, [128, WB], F32), name="sg%d" % i) for i in range(2)]
                it = 0
                for s in range(3):
                    for (t0, w, j) in tiles_of(s, WB):
                        xT = xTb[it % 2]
                        it += 1
                        load_xT(xT, t0, w, q="sp")
                        norm_fm(xT, w, sq, tmp, lnv, rstd, hT,
                                lambda c, s=s: G2.t[:, l, s, c:c + 1], lambda c, s=s: mod.t[:, l, 3, s, c:c + 1], hT.rs)
                        for f in range(22):
                            pg = next_bank()
                            pu = next_bank()
                            for kc in range(8):
                                fw.op("pe", lambda e, kc=kc, f=f, pg=pg, w=w: e.matmul(
                                    pg.t[:, 0:w], lhsT=wfi.t[:, kc, f * 128:(f + 1) * 128], rhs=hT.t[:, kc, 0:w],
                                    start=(kc == 0), stop=(kc == 7)),
                                    reads=[wfi.r, hT.rs[kc]], writes=[pg.r], signal=(kc == 7))
                            for kc in range(8):
                                fw.op("pe", lambda e, kc=kc, f=f, pu=pu, w=w: e.matmul(
                                    pu.t[:, 0:w], lhsT=wfi.t[:, kc, DFF + f * 128:DFF + (f + 1) * 128],
                                    rhs=hT.t[:, kc, 0:w], start=(kc == 0), stop=(kc == 7)),
                                    reads=[wfi.r, hT.rs[kc]], writes=[pu.r], signal=(kc == 7))
                            sgt = sg[f % 2]
                            fw.op("act", lambda e, pg=pg, sgt=sgt, w=w: e.activation(
                                out=sgt.t[:, 0:w], in_=pg.t[:, 0:w], func=AF.Silu), reads=[pg.r], writes=[sgt.r])
                            fw.op("dve", lambda e, pu=pu, sgt=sgt, f=f, w=w: e.tensor_tensor(
                                out=actT.t[:, f, 0:w], in0=sgt.t[:, 0:w], in1=pu.t[:, 0:w], op=ALU.mult),
                                reads=[sgt.r, pu.r], writes=[actT.rs[f]])
                        for c in range(8):
                            po = next_bank()
                            for f in range(22):
                                fw.op("pe", lambda e, c=c, f=f, po=po, w=w: e.matmul(
                                    po.t[:, 0:w], lhsT=wfo.t[:, f, c * 128:(c + 1) * 128], rhs=actT.t[:, f, 0:w],
                                    start=(f == 0), stop=(f == 21)),
                                    reads=[wfo.r, actT.rs[f]], writes=[po.r], signal=(f == 21))
                            fw.op("dve", lambda e, c=c, po=po, xT=xT, s=s, w=w: e.scalar_tensor_tensor(
                                out=xT.t[:, c, 0:w], in0=po.t[:, 0:w], scalar=mod.t[:, l, 5, s, c:c + 1],
                                in1=xT.t[:, c, 0:w], op0=ALU.mult, op1=ALU.add),
                                reads=[po.r, mod.r, xT.rs[c]], writes=[xT.rs[c]])
                        store_xT(xT, t0, w, q="sp")
                fw.flush()

        with contextlib.ExitStack() as ph:
            sub = FWScope(fw, ph)
            WE = 512
            xTe = [Tt(sub.sbuf("xTe%d" % i, [128, 8, WE], F32), nreg=8, name="xTe%d" % i) for i in range(2)]
            sq = Tt(sub.sbuf("sqe", [128, 8, WE], BF16), name="sqe")
            yT = Tt(sub.sbuf("yTe", [128, 8, WE], F32), nreg=8, name="yTe")
            tmp = [Tt(sub.sbuf("tmpe%d" % i, [128, WE], F32), name="tmpe%d" % i) for i in range(2)]
            lnv = Tt(sub.sbuf("lnve", [128, WE], F32), name="lnve")
            rstd = Tt(sub.sbuf("rstde", [128, WE], F32), name="rstde")
            ytok = [Tt(sub.sbuf("ytok%d" % i, [128, D], F32), name="ytok%d" % i) for i in range(2)]
            it = 0
            ik = 0
            for s in range(3):
                dst = O["yp"][s] if s < 2 else O["ys"]
                off, T = seqs[s]
                for (t0, w, j) in tiles_of(s, WE):
                    xT = xTe[it % 2]
                    it += 1
                    load_xT(xT, t0, w)
                    norm_fm(xT, w, sq, tmp, lnv, rstd, yT, lambda c: fng.t[:, c:c + 1], None, yT.rs)
                    nb = (w + 127) // 128
                    pw = min(w, 128)
                    for tb in range(nb):
                        yt = ytok[ik % 2]
                        ik += 1
                        for half in range(2):
                            pb = next_bank()
                            for cc in range(4):
                                c = half * 4 + cc
                                fw.op("pe", lambda e, c=c, cc=cc, tb=tb, pb=pb, pw=pw: e.transpose(
                                    pb.t[0:pw, cc * 128:(cc + 1) * 128], yT.t[:, c, tb * 128:tb * 128 + pw],
                                    ident.t[:, :]),
                                    reads=[yT.rs[c], ident.r], writes=[pb.r], signal=(cc == 3))
                            if half == 0:
                                fw.op("act", lambda e, pb=pb, yt=yt, pw=pw: e.activation(
                                    out=yt.t[0:pw, 0:512], in_=pb.t[0:pw, :], func=AF.Copy), reads=[pb.r], writes=[yt.r])
                            else:
                                fw.op("dve", lambda e, pb=pb, yt=yt, pw=pw: e.tensor_copy(
                                    out=yt.t[0:pw, 512:1024], in_=pb.t[0:pw, :]), reads=[pb.r], writes=[yt.r])
                        lt0 = t0 - off + tb * 128
                        fw.dma("sp" if ik % 2 else "pool", dst[lt0:lt0 + pw, :], yt.t[0:pw, :], reads=[yt.r],
                               stream="yout")
            fw.flush()
        fw.finish()
    return nc


class FWScope:
    ctr = 0

    def __init__(self, fw, stack):
        self.fw = fw
        self.stack = stack

    def sbuf(self, name, shape, dt):
        FWScope.ctr += 1
        return self.stack.enter_context(self.fw.nc.sbuf_tensor("%s_u%d" % (name, FWScope.ctr), list(shape), dt))


def layer_mixer(fw, nc, I, O, l, L, SEQ, TS, PAST, env):
    pass


_PROG_CACHE = {}


def _get_prog(SEQ, DEPTH, TS, PAST, stage=9):
    key = (SEQ, DEPTH, TS, PAST, stage)
    if key not in _PROG_CACHE:
        _PROG_CACHE[key] = build_program(SEQ, DEPTH, TS, PAST, stage)
    return _PROG_CACHE[key]


def make_in_maps(inp, ncores, L):
    f = lambda a: np.ascontiguousarray(np.asarray(a, dtype=np.float32))
    maps = []
    shared = {k: f(inp[k]) for k in ("norm1_g", "w_ada", "b_ada", "w_in", "rw_mu", "rw_w0", "rw_w2", "rw_a0",
                                     "rw_a2", "rw_g2", "rw_k_k", "rw_k_a", "rw_ln_w", "rw_ln_b", "fox_b_f",
                                     "hg_lb_logits", "hg_norm_g", "w_out", "norm2_g", "w_ffn_in", "w_ffn_out",
                                     "final_norm_g")}
    shared["rw_r_k"] = f(inp["rw_r_k"]).reshape(L, 256)
    xp, xs = f(inp["x_prompt"]), f(inp["x_sample"])
    cp, cs = f(inp["c_prompt"]), f(inp["c_sample"])
    ck, cv, cl = f(inp["cache_fox_k"]), f(inp["cache_fox_v"]), f(inp["cache_fox_logf"])
    srw, ssh, shg = f(inp["state_rwkv"]), f(inp["state_rwkv_shift"]), f(inp["state_hgrn"])
    P = ck.shape[2]
    for i in range(ncores):
        m = dict(shared)
        m["xp"] = f(xp[2 * i:2 * i + 2])
        m["xs"] = f(xs[i])
        m["cc"] = f(np.concatenate([cp[2 * i:2 * i + 2], cs[i:i + 1]], axis=0))
        m["ck"] = f(ck[:, i].reshape(L, P, 512))
        m["cv"] = f(cv[:, i].reshape(L, P, 512))
        m["cl"] = f(cl[:, i])
        m["srw"] = f(srw[:, i])
        m["ssh"] = f(ssh[:, i, 0])
        m["shg"] = f(shg[:, i])
        maps.append(m)
    return maps


def gather_outputs(res, ncores, L, SEQ, TS):
    r = res
    cat = lambda k, ax: np.concatenate([r[i][k] for i in range(ncores)], axis=ax)
    stack = lambda k, ax: np.stack([r[i][k] for i in range(ncores)], axis=ax)
    yp = cat("yp", 0)
    ys = stack("ys", 0)
    fkp = cat("fkp", 1).reshape(L, 2 * ncores, SEQ, 8, 64)
    fvp = cat("fvp", 1).reshape(L, 2 * ncores, SEQ, 8, 64)
    flp = cat("flp", 1)
    rwp = cat("rwp", 1)
    rshp = cat("rshp", 1).reshape(L, 2 * ncores, 1, RW_COLS)
    hgp = cat("hgp", 1)
    fks = stack("fks", 1).reshape(L, ncores, TS, 8, 64)
    fvs = stack("fvs", 1).reshape(L, ncores, TS, 8, 64)
    fls = stack("fls", 1)
    rws = stack("rws", 1)
    rshs = stack("rshs", 1).reshape(L, ncores, 1, RW_COLS)
    hgs = stack("hgs", 1)
    return (yp, ys, fkp, fvp, flp, rwp, rshp, hgp, fks, fvs, fls, rws, rshs, hgs)


def kernel(**inputs):
    L = int(np.asarray(inputs["w_in"]).shape[0])
    SEQ = int(np.asarray(inputs["x_prompt"]).shape[1])
    TS = int(np.asarray(inputs["x_sample"]).shape[1])
    PAST = int(np.asarray(inputs["cache_fox_k"]).shape[2])
    ncores = int(np.asarray(inputs["x_sample"]).shape[0])
    nc = _get_prog(SEQ, L, TS, PAST)
    maps = make_in_maps(inputs, ncores, L)
    res = run_bass_kernel_spmd(nc, maps, core_ids=list(range(ncores)))
    return gather_outputs(res.results, ncores, L, SEQ, TS)
```

```python
import contextlib
import os
import numpy as np
import concourse.bass as bass
import concourse.mybir as mybir
from concourse.bass_utils import run_bass_kernel_spmd

F32 = mybir.dt.float32
BF16 = mybir.dt.bfloat16
AF = mybir.ActivationFunctionType
ALU = mybir.AluOpType

D = 1024
NC8 = 8
HD = 64
RW_COLS, FOX_COLS, HG_COLS = 896, 1544, 1024
IN_COLS = 3464
DFF = 2816
EPS = 1e-6


class Reg:
    __slots__ = ("name", "writers", "readers")

    def __init__(self, name=""):
        self.name = name
        self.writers = {}
        self.readers = {}


class Eng:
    def __init__(self, name, kind):
        self.name = name
        self.kind = kind
        self.ops = []
        self.sem = None
        self.count = 0
        self.waited = {}


class FW:
    def __init__(self, nc, stack):
        self.nc = nc
        self.stack = stack
        self.engs = {}
        self.dma_sems = {}
        self.dma_counts = {}
        self.group_sems = {}
        self.nsem = 0
        for name in ("pe", "act", "dve", "pool", "sp"):
            e = Eng(name, name)
            self.engs[name] = e
            if name != "sp":
                e.sem = self.new_sem("s_" + name)

    def new_sem(self, name):
        self.nsem += 1
        return self.stack.enter_context(self.nc.semaphore("%s_%d" % (name, self.nsem)))

    def sbuf(self, name, shape, dt):
        return self.stack.enter_context(self.nc.sbuf_tensor(name, list(shape), dt))

    def psum(self, name, shape, dt=F32):
        return self.stack.enter_context(self.nc.psum_tensor(name, list(shape), dt))

    def _collect(self, reads, writes):
        deps = {}

        def add(d):
            for k, (sem, val) in d.items():
                cur = deps.get(k)
                if cur is None or cur[1] < val:
                    deps[k] = (sem, val)
        for r in reads:
            add(r.writers)
        for w in writes:
            add(w.writers)
            add(w.readers)
        return deps

    def _waits(self, eng, deps, raw_keys):
        waits = []
        for k, (sem, val) in deps.items():
            if eng.sem is not None and k == id(eng.sem) and k not in raw_keys:
                continue
            if eng.waited.get(k, 0) >= val:
                continue
            eng.waited[k] = val
            st = self.group_sems.get(k)
            if st is not None:
                waits.append((sem, _Lazy(self.dma_counts, st)))
            else:
                waits.append((sem, val))
        return waits

    def op(self, engname, fn, reads=(), writes=(), signal=True):
        eng = self.engs[engname]
        reads = [r for r in reads if r is not None]
        writes = [w for w in writes if w is not None]
        deps = self._collect(reads, writes)
        raw_keys = set()
        k = id(eng.sem)
        if engname != "pe":
            raw_keys.add(k)
        for r in reads:
            if k in r.writers:
                raw_keys.add(k)
        waits = self._waits(eng, deps, raw_keys)
        sem = eng.sem
        if signal:
            eng.count += 1
            tok = (sem, eng.count)
        else:
            tok = (sem, eng.count + 1)
        for r in reads:
            r.readers[id(sem)] = tok
        for w in writes:
            w.writers = {id(sem): tok}
            w.readers = {}

        def run(e, fn=fn, waits=waits, signal=signal, sem=sem):
            for (s, v) in waits:
                e.wait_ge(s, int(v))
            ins = fn(e)
            if signal:
                ins.then_inc(sem, 1)
        eng.ops.append(run)

    def dma(self, qname, out, in_, reads=(), writes=(), stream="d", group=False, **kw):
        eng = self.engs[qname]
        reads = [r for r in reads if r is not None]
        writes = [w for w in writes if w is not None]
        deps = self._collect(reads, writes)
        stream = stream + "_" + qname
        if stream not in self.dma_sems:
            self.dma_sems[stream] = self.new_sem("dq_" + stream)
            self.dma_counts[stream] = 0
            if group:
                self.group_sems[id(self.dma_sems[stream])] = stream
        if group:
            deps.pop(id(self.dma_sems[stream]), None)
        waits = self._waits(eng, deps, set(deps.keys()))
        sem = self.dma_sems[stream]
        self.dma_counts[stream] += 16
        tok = (sem, self.dma_counts[stream])
        for r in reads:
            r.readers[id(sem)] = tok
        for w in writes:
            w.writers = {id(sem): tok}
            w.readers = {}

        def run(e, waits=waits, sem=sem, out=out, in_=in_, kw=kw):
            for (s, v) in waits:
                e.wait_ge(s, int(v))
            e.dma_start(out=out, in_=in_, **kw).then_inc(sem, 16)
        eng.ops.append(run)

    def rotate(self):
        for e in self.engs.values():
            if e.sem is not None:
                e.sem = self.new_sem("s_" + e.name)
                e.count = 0

    def barrier(self):
        toks = []
        for e in self.engs.values():
            if e.sem is not None and e.count > 0:
                toks.append((e.sem, e.count))
        for s in self.dma_sems:
            if self.dma_counts[s] > 0:
                toks.append((self.dma_sems[s], self.dma_counts[s]))
        for e in self.engs.values():
            waits = []
            for (sem, val) in toks:
                if sem is e.sem:
                    continue
                if e.waited.get(id(sem), 0) >= val:
                    continue
                e.waited[id(sem)] = val
                waits.append((sem, val))

            def run(h, waits=waits):
                for (s, v) in waits:
                    h.wait_ge(s, v)
            e.ops.append(run)

    def finish(self):
        self.flush()

    def flush(self):
        self.barrier()
        nc = self.nc
        engs = self.engs
        oplists = {k: e.ops for k, e in engs.items()}
        for e in engs.values():
            e.ops = []

        class _E:
            def __init__(self, ops):
                self.ops = ops
        self_engs = {k: _E(v) for k, v in oplists.items()}
        with nc.Block() as block:
            def mk(eng):
                def body(e):
                    for f in eng.ops:
                        f(e)
                return body
            block.tensor(mk(self_engs["pe"]))
            block.scalar(mk(self_engs["act"]))
            block.vector(mk(self_engs["dve"]))
            block.gpsimd(mk(self_engs["pool"]))
            block.sync(mk(self_engs["sp"]))


class _Lazy:
    def __init__(self, counts, stream):
        self.counts = counts
        self.stream = stream

    def __int__(self):
        return self.counts[self.stream]


class Rec:
    def __init__(self):
        self.items = []

    def op(self, *a, **k):
        self.items.append(("op", a, k))

    def dma(self, *a, **k):
        self.items.append(("dma", a, k))

    def replay_into(self, sink):
        for (kind, a, k) in self.items:
            getattr(sink, kind)(*a, **k)


def merge_recs(sink, recs):
    pos = [0] * len(recs)
    n = [len(r.items) for r in recs]
    total = sum(n)
    for _ in range(total):
        best, bf_ = -1, 2.0
        for i in range(len(recs)):
            if pos[i] < n[i]:
                f = pos[i] / n[i]
                if f < bf_:
                    best, bf_ = i, f
        kind, a, k = recs[best].items[pos[best]]
        pos[best] += 1
        getattr(sink, kind)(*a, **k)


class Tt:
    def __init__(self, t, nreg=1, name=""):
        self.t = t
        self.rs = [Reg("%s%d" % (name, i)) for i in range(nreg)]
        self.r = self.rs[0]


def build_program(SEQ, DEPTH, TS=32, PAST=2048, stage=9):
    L = DEPTH
    NTOK = 2 * SEQ + TS
    nc = bass.Bass("TRN2", target_bir_lowering=False)
    din = lambda n, s: nc.dram_tensor(n, list(s), F32, kind="ExternalInput").ap()
    dout = lambda n, s: nc.dram_tensor(n, list(s), F32, kind="ExternalOutput").ap()
    I = dict(
        xp=din("xp", (2, SEQ, D)), xs=din("xs", (TS, D)), cc=din("cc", (3, D)),
        ck=din("ck", (L, PAST, 512)), cv=din("cv", (L, PAST, 512)), cl=din("cl", (L, PAST, 8)),
        srw=din("srw", (L, 4, 64, 64)), ssh=din("ssh", (L, RW_COLS)), shg=din("shg", (L, 4, 64, 64)),
        norm1_g=din("norm1_g", (L, D)), w_ada=din("w_ada", (L, D, 6 * D)), b_ada=din("b_ada", (L, 6 * D)),
        w_in=din("w_in", (L, D, IN_COLS)), rw_mu=din("rw_mu", (L, RW_COLS)), rw_w0=din("rw_w0", (L, 256)),
        rw_w2=din("rw_w2", (L, 32, 256)), rw_a0=din("rw_a0", (L, 256)), rw_a2=din("rw_a2", (L, 32, 256)),
        rw_g2=din("rw_g2", (L, 64, 256)), rw_k_k=din("rw_k_k", (L, 256)), rw_k_a=din("rw_k_a", (L, 256)),
        rw_r_k=din("rw_r_k", (L, 256)), rw_ln_w=din("rw_ln_w", (L, 256)), rw_ln_b=din("rw_ln_b", (L, 256)),
        fox_b_f=din("fox_b_f", (L, 8)), hg_lb_logits=din("hg_lb_logits", (L, 256)),
        hg_norm_g=din("hg_norm_g", (L, 256)), w_out=din("w_out", (L, D, D)), norm2_g=din("norm2_g", (L, D)),
        w_ffn_in=din("w_ffn_in", (L, D, 2 * DFF)), w_ffn_out=din("w_ffn_out", (L, DFF, D)),
        final_norm_g=din("final_norm_g", (D,)),
    )
    O = dict(
        yp=dout("yp", (2, SEQ, D)), ys=dout("ys", (TS, D)),
        fkp=dout("fkp", (L, 2, SEQ, 512)), fvp=dout("fvp", (L, 2, SEQ, 512)), flp=dout("flp", (L, 2, SEQ, 8)),
        rwp=dout("rwp", (L, 2, 4, 64, 64)), rshp=dout("rshp", (L, 2, RW_COLS)), hgp=dout("hgp", (L, 2, 4, 64, 64)),
        fks=dout("fks", (L, TS, 512)), fvs=dout("fvs", (L, TS, 512)), fls=dout("fls", (L, TS, 8)),
        rws=dout("rws", (L, 4, 64, 64)), rshs=dout("rshs", (L, RW_COLS)), hgs=dout("hgs", (L, 4, 64, 64)),
    )
    xres = nc.dram_tensor("xres", [D, NTOK], F32).ap()
    xres_r = Reg("xres")
    seqs = [(0, SEQ), (SEQ, SEQ), (2 * SEQ, TS)]

    def tiles_of(s, W):
        off, T = seqs[s]
        w = min(W, T)
        return [(off + j * w, w, j) for j in range(T // w)]

    xres_regs = {}

    def xreg(t0):
        return xres_regs.setdefault(t0, Reg("xres%d" % t0))

    with contextlib.ExitStack() as top:
        fw = FW(nc, top)
        ident = Tt(fw.sbuf("ident", [128, 128], F32), name="ident")
        identb = Tt(fw.sbuf("identb", [128, 128], BF16), name="identb")
        onesb = Tt(fw.sbuf("onesb", [128, 128], BF16), name="onesb")
        fw.op("pool", lambda e: e.memset(ident.t[:], 0.0), writes=[ident.r])
        fw.op("pool", lambda e: e.affine_select(out=ident.t[:], in_=ident.t[:], pattern=[[-1, 128]],
                                                compare_op=ALU.not_equal, fill=1.0, base=0,
                                                channel_multiplier=1), reads=[ident.r], writes=[ident.r])
        fw.op("pool", lambda e: e.tensor_copy(out=identb.t[:], in_=ident.t[:]), reads=[ident.r], writes=[identb.r])
        fw.op("pool", lambda e: e.memset(onesb.t[:], 1.0), writes=[onesb.r])
        epsb = Tt(fw.sbuf("epsb", [128, 1], F32), name="epsb")
        fw.op("pool", lambda e: e.memset(epsb.t[:], EPS), writes=[epsb.r])


        trif = Tt(fw.sbuf("trif", [128, 128], F32), name="trif")
        fw.op("pool", lambda e: e.memset(trif.t[:], 1.0), writes=[trif.r])
        fw.op("pool", lambda e: e.affine_select(out=trif.t[:], in_=trif.t[:], pattern=[[1, 128]],
                                                compare_op=ALU.is_ge, fill=0.0, base=0, channel_multiplier=-1),
              reads=[trif.r], writes=[trif.r])
        self127 = Tt(fw.sbuf("self127", [128, 128], F32), name="self127")
        fw.op("pool", lambda e: e.memset(self127.t[:], 0.0), writes=[self127.r])
        fw.op("pool", lambda e: e.affine_select(out=self127.t[:], in_=self127.t[:], pattern=[[0, 128]],
                                                compare_op=ALU.not_equal, fill=1.0, base=-127, channel_multiplier=1),
              reads=[self127.r], writes=[self127.r])
        mnegf = Tt(fw.sbuf("mnegf", [128, 128], F32), name="mnegf")
        maskneg = Tt(fw.sbuf("maskneg", [128, 128], BF16), name="maskneg")
        fw.op("pool", lambda e: e.memset(mnegf.t[:], 0.0), writes=[mnegf.r])
        fw.op("pool", lambda e: e.affine_select(out=mnegf.t[:], in_=mnegf.t[:], pattern=[[1, 128]],
                                                compare_op=ALU.is_ge, fill=-30000.0, base=0, channel_multiplier=-1),
              reads=[mnegf.r], writes=[mnegf.r])
        fw.op("pool", lambda e: e.tensor_copy(out=maskneg.t[:], in_=mnegf.t[:]), reads=[mnegf.r], writes=[maskneg.r])
        e8 = Tt(fw.sbuf("e8", [8, 8, 128], F32), name="e8")
        fw.op("pool", lambda e: e.memset(e8.t[:], 0.0), writes=[e8.r])
        fw.op("pool", lambda e: e.affine_select(out=e8.t[:], in_=e8.t[:], pattern=[[-1, 8], [0, 128]],
                                                compare_op=ALU.not_equal, fill=1.0, base=0, channel_multiplier=1),
              reads=[e8.r], writes=[e8.r])
        selh = Tt(fw.sbuf("selh", [72, 8, 128], BF16), name="selh")
        fw.op("pool", lambda e: e.memset(selh.t[:], 0.0), writes=[selh.r])
        for b0 in (0, 32, 64):
            fw.op("dve", lambda e, b0=b0: e.tensor_copy(out=selh.t[b0:b0 + 8, :, :], in_=e8.t[:]),
                  reads=[e8.r], writes=[selh.r])
        onesf = Tt(fw.sbuf("onesf", [128, 64], F32), name="onesf")
        fw.op("pool", lambda e: e.memset(onesf.t[:], 1.0), writes=[onesf.r])
        bfb = Tt(fw.sbuf("bfb", [128, L, 8], F32), name="bfb")
        for l_ in range(L):
            fw.dma("sp", bfb.t[:, l_, :], I["fox_b_f"][l_:l_ + 1, :].to_broadcast([128, 8]), writes=[bfb.r],
                   stream="par", group=True)
        yfox = nc.dram_tensor("yfox", [512, NTOK], BF16).ap()
        yfox_regs = {}

        def yfreg(t0):
            return yfox_regs.setdefault(t0, Reg("yfox%d" % t0))

        banks = [Tt(fw.psum("bank%d" % i, [128, 512]), name="bank%d" % i) for i in range(8)]
        bank_ctr = [0]

        default_pool = [(0, 1, 2, 3, 4, 5, 6, 7)]

        def next_bank(pool=None):
            if pool is None:
                pool = default_pool[0]
            b = banks[pool[bank_ctr[0] % len(pool)]]
            bank_ctr[0] += 1
            return b

        def load_fm(name, src_ap, ncol, q="sp"):
            t = Tt(fw.sbuf(name, [128, L, ncol], F32), name=name)
            fw.dma(q, t.t[:], src_ap.rearrange("l (c p) -> p l c", p=128), writes=[t.r], stream="par", group=True,
                   allow_slow_non_contiguous=True)
            return t
        n1g = load_fm("n1g", I["norm1_g"], 8)
        n2g = load_fm("n2g", I["norm2_g"], 8)
        badaT = load_fm("badaT", I["b_ada"], 48)
        fng = Tt(fw.sbuf("fng", [128, 8], F32), name="fng")
        fw.dma("sp", fng.t[:], I["final_norm_g"].rearrange("(c p) -> p c", p=128), writes=[fng.r], stream="par", group=True,
               allow_slow_non_contiguous=True)

        rmask32 = Tt(fw.sbuf("rmask32", [128, 512], F32), name="rmask32")
        fw.op("pool", lambda e: e.memset(rmask32.t[:], 1.0), writes=[rmask32.r])
        fw.op("pool", lambda e: e.affine_select(out=rmask32.t[:, :].rearrange("p (c t) -> p c t", t=32),
                                                in_=rmask32.t[:, :].rearrange("p (c t) -> p c t", t=32),
                                                pattern=[[0, 16], [1, 32]], compare_op=ALU.not_equal, fill=0.0,
                                                base=0, channel_multiplier=0), reads=[rmask32.r], writes=[rmask32.r])
        maskbd = Tt(fw.sbuf("maskbd", [128, 128], F32), name="maskbd")
        fw.op("pool", lambda e: e.tensor_copy(out=maskbd.t[:], in_=trif.t[:]), reads=[trif.r], writes=[maskbd.r])
        for cb_ in range(1, 4):
            fw.op("pool", lambda e, cb_=cb_: e.affine_select(
                out=maskbd.t[:, cb_ * 32:(cb_ + 1) * 32], in_=maskbd.t[:, cb_ * 32:(cb_ + 1) * 32], pattern=[[0, 32]],
                compare_op=ALU.is_ge, fill=0.0, base=-cb_ * 32, channel_multiplier=1),
                reads=[maskbd.r], writes=[maskbd.r])
        onesbdf = Tt(fw.sbuf("onesbdf", [128, 128], F32), name="onesbdf")
        onesbd = Tt(fw.sbuf("onesbd", [128, 128], BF16), name="onesbd")
        fw.op("pool", lambda e: e.memset(onesbdf.t[:], 1.0), writes=[onesbdf.r])
        fw.op("pool", lambda e: e.affine_select(out=onesbdf.t[:, 0:64], in_=onesbdf.t[:, 0:64], pattern=[[0, 64]],
                                                compare_op=ALU.is_ge, fill=0.0, base=63, channel_multiplier=-1),
              reads=[onesbdf.r], writes=[onesbdf.r])
        fw.op("pool", lambda e: e.affine_select(out=onesbdf.t[:, 64:128], in_=onesbdf.t[:, 64:128], pattern=[[0, 64]],
                                                compare_op=ALU.is_ge, fill=0.0, base=-64, channel_multiplier=1),
              reads=[onesbdf.r], writes=[onesbdf.r])
        fw.op("pool", lambda e: e.tensor_copy(out=onesbd.t[:], in_=onesbdf.t[:]), reads=[onesbdf.r], writes=[onesbd.r])
        hgng = load_fm("hgng", I["hg_norm_g"], 2)
        lbl = load_fm("lbl", I["hg_lb_logits"], 2)
        lbT = Tt(fw.sbuf("lbT", [128, L, 2], F32), name="lbT")
        omlT = Tt(fw.sbuf("omlT", [128, L, 2], F32), name="omlT")
        nomlT = Tt(fw.sbuf("nomlT", [128, L, 2], F32), name="nomlT")
        lbm = Tt(fw.sbuf("lbm", [128, 2], F32), name="lbm")
        lbe = Tt(fw.sbuf("lbe", [128, L, 2], F32), name="lbe")
        lbs_ = Tt(fw.sbuf("lbs_", [128, 2], F32), name="lbs_")
        fw.op("dve", lambda e: e.tensor_copy(out=lbm.t[:], in_=lbl.t[:, 0, :]), reads=[lbl.r], writes=[lbm.r])
        for l_ in range(1, L):
            fw.op("dve", lambda e, l_=l_: e.tensor_max(out=lbm.t[:], in0=lbm.t[:], in1=lbl.t[:, l_, :]),
                  reads=[lbm.r, lbl.r], writes=[lbm.r])
        for l_ in range(L):
            fw.op("dve", lambda e, l_=l_: e.tensor_sub(out=lbe.t[:, l_, :], in0=lbl.t[:, l_, :], in1=lbm.t[:]),
                  reads=[lbm.r, lbl.r], writes=[lbe.r])
        fw.op("act", lambda e: e.activation(out=lbe.t[:], in_=lbe.t[:], func=AF.Exp), reads=[lbe.r], writes=[lbe.r])
        fw.op("dve", lambda e: e.tensor_copy(out=lbs_.t[:], in_=lbe.t[:, 0, :]), reads=[lbe.r], writes=[lbs_.r])
        for l_ in range(1, L):
            fw.op("dve", lambda e, l_=l_: e.tensor_add(out=lbs_.t[:], in0=lbs_.t[:], in1=lbe.t[:, l_, :]),
                  reads=[lbs_.r, lbe.r], writes=[lbs_.r])
        fw.op("dve", lambda e: e.reciprocal(out=lbs_.t[:], in_=lbs_.t[:]), reads=[lbs_.r], writes=[lbs_.r])
        for l_ in range(L):
            fw.op("dve", lambda e, l_=l_: e.tensor_mul(out=lbe.t[:, l_, :], in0=lbe.t[:, l_, :], in1=lbs_.t[:]),
                  reads=[lbs_.r, lbe.r], writes=[lbe.r])
        fw.op("dve", lambda e: e.memset(lbT.t[:, 0, :], 0.0), writes=[lbT.r])
        for l_ in range(1, L):
            fw.op("dve", lambda e, l_=l_: e.tensor_add(out=lbT.t[:, l_, :], in0=lbT.t[:, l_ - 1, :], in1=lbe.t[:, l_, :]),
                  reads=[lbT.r, lbe.r], writes=[lbT.r])
        fw.op("dve", lambda e: e.tensor_scalar(out=omlT.t[:], in0=lbT.t[:], scalar1=-1.0, scalar2=1.0,
                                               op0=ALU.mult, op1=ALU.add), reads=[lbT.r], writes=[omlT.r])
        fw.op("dve", lambda e: e.tensor_scalar_mul(out=nomlT.t[:], in0=omlT.t[:], scalar1=-1.0),
              reads=[omlT.r], writes=[nomlT.r])

        rmask64 = Tt(fw.sbuf("rmask64", [128, 512], F32), name="rmask64")
        fw.op("pool", lambda e: e.memset(rmask64.t[:], 1.0), writes=[rmask64.r])
        fw.op("pool", lambda e: e.affine_select(out=rmask64.t[:, :].rearrange("p (c t) -> p c t", t=64),
                                                in_=rmask64.t[:, :].rearrange("p (c t) -> p c t", t=64),
                                                pattern=[[0, 8], [1, 64]], compare_op=ALU.not_equal, fill=0.0,
                                                base=0, channel_multiplier=0), reads=[rmask64.r], writes=[rmask64.r])
        mask12 = Tt(fw.sbuf("mask12", [64, 2, 64], F32), name="mask12")
        mask34 = Tt(fw.sbuf("mask34", [64, 2, 64], F32), name="mask34")
        mask5 = Tt(fw.sbuf("mask5", [64, 64], F32), name="mask5")
        fw.op("pool", lambda e: e.memset(mask12.t[:], 1.0), writes=[mask12.r])
        fw.op("pool", lambda e: e.affine_select(out=mask12.t[:, 0, :], in_=mask12.t[:, 0, :], pattern=[[1, 64]],
                                                compare_op=ALU.is_ge, fill=0.0, base=-1, channel_multiplier=-1),
              reads=[mask12.r], writes=[mask12.r])
        fw.op("pool", lambda e: e.affine_select(out=mask12.t[:, 1, :], in_=mask12.t[:, 1, :], pattern=[[1, 64]],
                                                compare_op=ALU.is_ge, fill=0.0, base=0, channel_multiplier=-1),
              reads=[mask12.r], writes=[mask12.r])
        fw.op("dve", lambda e: e.tensor_scalar_mul(out=mask34.t[:, 0, :], in0=mask12.t[:, 0, :], scalar1=-1.0),
              reads=[mask12.r], writes=[mask34.r])
        fw.op("pool", lambda e: e.tensor_copy(out=mask34.t[:, 1, :], in_=mask12.t[:, 1, :]),
              reads=[mask12.r], writes=[mask34.r])
        fw.op("pool", lambda e: e.memset(mask5.t[:], -1.0), writes=[mask5.r])
        fw.op("pool", lambda e: e.affine_select(out=mask5.t[:], in_=mask5.t[:], pattern=[[-1, 64]],
                                                compare_op=ALU.is_ge, fill=0.0, base=-1, channel_multiplier=1),
              reads=[mask5.r], writes=[mask5.r])
        eps2 = Tt(fw.sbuf("eps2", [128, 1], F32), name="eps2")
        fw.op("pool", lambda e: e.memset(eps2.t[:], 64e-5), writes=[eps2.r])
        rwp = {}
        for nm in ("rw_w0", "rw_a0", "rw_k_k", "rw_k_a", "rw_r_k", "rw_ln_w", "rw_ln_b"):
            rwp[nm] = load_fm("p_" + nm, I[nm], 2)
        nw0 = Tt(fw.sbuf("nw0", [128, L, 2], F32), name="nw0")
        na0 = Tt(fw.sbuf("na0", [128, L, 2], F32), name="na0")
        omka = Tt(fw.sbuf("omka", [128, L, 2], F32), name="omka")
        fw.op("dve", lambda e: e.tensor_scalar_mul(out=nw0.t[:], in0=rwp["rw_w0"].t[:], scalar1=-1.0),
              reads=[rwp["rw_w0"].r], writes=[nw0.r])
        fw.op("dve", lambda e: e.tensor_scalar_mul(out=na0.t[:], in0=rwp["rw_a0"].t[:], scalar1=-1.0),
              reads=[rwp["rw_a0"].r], writes=[na0.r])
        fw.op("dve", lambda e: e.tensor_scalar(out=omka.t[:], in0=rwp["rw_k_a"].t[:], scalar1=-1.0, scalar2=1.0,
                                               op0=ALU.mult, op1=ALU.add), reads=[rwp["rw_k_a"].r], writes=[omka.r])
        mul = Tt(fw.sbuf("mul", [128, L, 9], F32), name="mul")
        fw.op("pool", lambda e: e.memset(mul.t[:], 0.0), writes=[mul.r])
        m7 = load_fm("m7", I["rw_mu"], 7)
        fw.op("dve", lambda e: e.tensor_copy(out=mul.t[:, :, 0:6], in_=m7.t[:, :, 0:6]), reads=[m7.r], writes=[mul.r])
        fw.op("dve", lambda e: e.tensor_copy(out=mul.t[0:32, :, 6], in_=m7.t[0:32, :, 6]), reads=[m7.r], writes=[mul.r])
        fw.op("dve", lambda e: e.tensor_copy(out=mul.t[0:32, :, 7], in_=m7.t[32:64, :, 6]), reads=[m7.r], writes=[mul.r])
        fw.op("dve", lambda e: e.tensor_copy(out=mul.t[0:64, :, 8], in_=m7.t[64:128, :, 6]), reads=[m7.r], writes=[mul.r])
        w2b = Tt(fw.sbuf("w2b", [32, L, 256], BF16), name="w2b")
        a2b = Tt(fw.sbuf("a2b", [32, L, 256], BF16), name="a2b")
        g2b = Tt(fw.sbuf("g2b", [64, L, 256], BF16), name="g2b")
        fw.dma("pool", w2b.t[:], I["rw_w2"].rearrange("l k n -> k l n"), writes=[w2b.r], stream="parb", group=True)
        fw.dma("pool", a2b.t[:], I["rw_a2"].rearrange("l k n -> k l n"), writes=[a2b.r], stream="parb", group=True)
        fw.dma("pool", g2b.t[:], I["rw_g2"].rearrange("l k n -> k l n"), writes=[g2b.r], stream="parb", group=True)

        cT = Tt(fw.sbuf("cT", [128, 3, 8], F32), name="cT")
        fw.dma("sp", cT.t[:], I["cc"].rearrange("b (c p) -> p b c", p=128), writes=[cT.r], stream="par", group=True,
               allow_slow_non_contiguous=True)
        siluT = Tt(fw.sbuf("siluT", [128, 8, 3], F32), name="siluT")
        sl_e = Tt(fw.sbuf("sl_e", [128, 3, 8], F32), name="sl_e")
        fw.op("act", lambda e: e.activation(out=sl_e.t[:], in_=cT.t[:], func=AF.Exp, scale=-1.0),
              reads=[cT.r], writes=[sl_e.r])
        fw.op("dve", lambda e: e.tensor_scalar_add(out=sl_e.t[:], in0=sl_e.t[:], scalar1=1.0),
              reads=[sl_e.r], writes=[sl_e.r])
        fw.op("dve", lambda e: e.reciprocal(out=sl_e.t[:], in_=sl_e.t[:]), reads=[sl_e.r], writes=[sl_e.r])
        fw.op("dve", lambda e: e.tensor_tensor(out=siluT.t[:].rearrange("p c b -> p b c"), in0=sl_e.t[:],
                                               in1=cT.t[:], op=ALU.mult),
              reads=[sl_e.r, cT.r], writes=[siluT.r])
        mod = Tt(fw.sbuf("mod", [128, L, 6, 3, 8], F32), name="mod")
        with contextlib.ExitStack() as ph:
            sub = FWScope(fw, ph)
            wa = [Tt(sub.sbuf("wa%d" % i, [128, 8, 512], F32), name="wa%d" % i) for i in range(2)]
            k = 0
            for l in range(L):
                for jg in range(12):
                    w = wa[k % 2]
                    k += 1
                    fw.dma("sp" if k % 2 else "pool", w.t[:],
                           I["w_ada"][l].rearrange("(kc p) n -> p kc n", p=128)[:, :, jg * 512:(jg + 1) * 512],
                           writes=[w.r], stream="wada%d" % (k % 2))
                    for jj in range(4):
                        j = jg * 4 + jj
                        m, c = j // 8, j % 8
                        pb = next_bank()
                        for kc in range(8):
                            fw.op("pe", lambda e, w=w, pb=pb, kc=kc, jj=jj: e.matmul(
                                pb.t[:, 0:3], lhsT=w.t[:, kc, jj * 128:(jj + 1) * 128], rhs=siluT.t[:, kc, :],
                                start=(kc == 0), stop=(kc == 7)),
                                reads=[w.r, siluT.r], writes=[pb.r], signal=(kc == 7))
                        fw.op("dve", lambda e, pb=pb, l=l, m=m, c=c, j=j: e.tensor_scalar(
                            out=mod.t[:, l, m, :, c], in0=pb.t[:, 0:3], scalar1=badaT.t[:, l, j:j + 1], scalar2=None,
                            op0=ALU.add), reads=[pb.r, badaT.r], writes=[mod.r])
            fw.flush()
        G1 = Tt(fw.sbuf("G1", [128, L, 3, 8], F32), name="G1")
        G2 = Tt(fw.sbuf("G2", [128, L, 3, 8], F32), name="G2")
        for l in range(L):
            for (G, ng, mi) in ((G1, n1g, 1), (G2, n2g, 4)):
                for b in range(3):
                    fw.op("dve", lambda e, G=G, ng=ng, mi=mi, l=l, b=b: e.scalar_tensor_tensor(
                        out=G.t[:, l, b, :], in0=mod.t[:, l, mi, b, :], scalar=1.0, in1=ng.t[:, l, :],
                        op0=ALU.add, op1=ALU.mult), reads=[mod.r, ng.r], writes=[G.r])

        def load_xT(xT, t0, w, q="sp"):
            fw.dma(q, xT.t[:, :, 0:w], xres.rearrange("(c p) t -> p c t", p=128)[:, :, t0:t0 + w],
                   reads=[xreg(t0)], writes=xT.rs, stream="xld" + xT.r.name[-2:])

        def store_xT(xT, t0, w, q="sp"):
            fw.dma(q, xres.rearrange("(c p) t -> p c t", p=128)[:, :, t0:t0 + w], xT.t[:, :, 0:w],
                   reads=xT.rs, writes=[xreg(t0)], stream="xst" + xT.r.name[-2:])

        def norm_fm(xT, w, sq, tmp, lnv, rstd, out, gap, shap, out_regs):
            fw.op("act", lambda e: e.activation(out=sq.t[:, :, 0:w], in_=xT.t[:, :, 0:w], func=AF.Square),
                  reads=xT.rs, writes=[sq.r])
            pb = next_bank()
            for c in range(8):
                fw.op("pe", lambda e, c=c: e.matmul(pb.t[:, 0:w], lhsT=onesb.t[:], rhs=sq.t[:, c, 0:w],
                                                    start=(c == 0), stop=(c == 7)),
                      reads=[onesb.r, sq.r], writes=[pb.r], signal=(c == 7))
            fw.op("act", lambda e: e.activation(out=lnv.t[:, 0:w], in_=pb.t[:, 0:w], func=AF.Ln, scale=1.0 / D,
                                                bias=epsb.t[:]), reads=[pb.r, epsb.r], writes=[lnv.r])
            fw.op("act", lambda e: e.activation(out=rstd.t[:, 0:w], in_=lnv.t[:, 0:w], func=AF.Exp, scale=-0.5),
                  reads=[lnv.r], writes=[rstd.r])
            for c in range(8):
                tm = tmp[c % len(tmp)]
                fw.op("dve", lambda e, c=c, tm=tm: e.tensor_tensor(out=tm.t[:, 0:w], in0=xT.t[:, c, 0:w],
                                                                   in1=rstd.t[:, 0:w], op=ALU.mult),
                      reads=[xT.rs[c], rstd.r], writes=[tm.r])
                if shap is not None:
                    fw.op("act", lambda e, c=c, tm=tm: e.activation(out=out.t[:, c, 0:w], in_=tm.t[:, 0:w],
                                                                    func=AF.Identity, scale=gap(c), bias=shap(c)),
                          reads=[tm.r, G1.r, G2.r, mod.r], writes=[out_regs[c]])
                else:
                    fw.op("act", lambda e, c=c, tm=tm: e.activation(out=out.t[:, c, 0:w], in_=tm.t[:, 0:w],
                                                                    func=AF.Identity, scale=gap(c)),
                          reads=[tm.r, fng.r], writes=[out_regs[c]])

        with contextlib.ExitStack() as ph:
            sub = FWScope(fw, ph)
            xtok = [Tt(sub.sbuf("xtok%d" % i, [128, 4, D], F32), name="xtok%d" % i) for i in range(2)]
            xTs = [Tt(sub.sbuf("xTp%d" % i, [128, 8, 512], F32), nreg=8, name="xTp%d" % i) for i in range(2)]
            it = 0
            for s in range(3):
                src = I["xp"][s] if s < 2 else I["xs"]
                off, T = seqs[s]
                for (t0, w, j) in tiles_of(s, 512):
                    xt = xtok[it % 2]
                    xT = xTs[it % 2]
                    it += 1
                    nb = (w + 127) // 128
                    pw = min(w, 128)
                    lt0 = t0 - off
                    if w >= 128:
                        fw.dma("sp" if it % 2 else "pool", xt.t[:, 0:nb, :],
                               src[lt0:lt0 + w, :].rearrange("(b p) d -> p b d", p=128), writes=[xt.r], stream="xtok")
                    else:
                        fw.dma("sp", xt.t[0:w, 0, :], src[lt0:lt0 + w, :], writes=[xt.r], stream="xtok")
                    for c in range(8):
                        pb = next_bank()
                        for tb in range(nb):
                            fw.op("pe", lambda e, c=c, tb=tb, pb=pb, xt=xt, pw=pw: e.transpose(
                                pb.t[:, tb * 128:tb * 128 + pw], xt.t[0:pw, tb, c * 128:(c + 1) * 128],
                                ident.t[0:pw, 0:pw]),
                                reads=[xt.r, ident.r], writes=[pb.r], signal=(tb == nb - 1))
                        eng = "act" if c % 2 else "dve"
                        if eng == "act":
                            fw.op("act", lambda e, c=c, pb=pb, xT=xT, w=w: e.activation(
                                out=xT.t[:, c, 0:w], in_=pb.t[:, 0:w], func=AF.Copy), reads=[pb.r], writes=[xT.rs[c]])
                        else:
                            fw.op("dve", lambda e, c=c, pb=pb, xT=xT, w=w: e.tensor_copy(
                                out=xT.t[:, c, 0:w], in_=pb.t[:, 0:w]), reads=[pb.r], writes=[xT.rs[c]])
                    store_xT(xT, t0, w, q="sp" if it % 2 else "pool")
            fw.flush()

        for l in range(L):
            fw.rotate()
            if stage >= 2:
              with contextlib.ExitStack() as ph:
                sub = FWScope(fw, ph)
                default_pool[0] = (0, 1, 2, 3, 4, 5)
                WA = 512
                FC0 = RW_COLS
                TKMAX = max(SEQ, PAST + TS)
                NBMAX = (TKMAX + 127) // 128
                wf = Tt(sub.sbuf("wf", [128, 8, FOX_COLS], BF16), name="wf")
                for kc in range(8):
                    fw.dma("pool", wf.t[:, kc, :], I["w_in"][l][kc * 128:(kc + 1) * 128, FC0:FC0 + FOX_COLS],
                           writes=[wf.r], stream="wld", group=True)
                KT = Tt(sub.sbuf("KT", [128, 4, TKMAX], BF16), nreg=NBMAX, name="KT")
                Vx = Tt(sub.sbuf("Vx", [128, NBMAX, 8, 65], BF16), nreg=NBMAX, name="Vx")
                fw.op("pool", lambda e: e.memset(Vx.t[:], 1.0), writes=Vx.rs)
                ctok = Tt(sub.sbuf("ctok", [128, NBMAX, 8], F32), nreg=NBMAX, name="ctok")
                negc = Tt(sub.sbuf("negc", [128, NBMAX, 8], F32), nreg=NBMAX, name="negc")
                xTa = [Tt(sub.sbuf("xTa%d" % i, [128, 8, WA], F32), nreg=8, name="xTa%d" % i) for i in range(2)]
                sq = Tt(sub.sbuf("sqa", [128, 8, WA], BF16), name="sqa")
                hT = Tt(sub.sbuf("hTa", [128, 8, WA], BF16), nreg=8, name="hTa")
                tmp = [Tt(sub.sbuf("tmpa%d" % i, [128, WA], F32), name="tmpa%d" % i) for i in range(2)]
                lnv = Tt(sub.sbuf("lnva", [128, WA], F32), name="lnva")
                rstd = Tt(sub.sbuf("rstda", [128, WA], F32), name="rstda")
                QT = Tt(sub.sbuf("QT", [128, 4, WA], BF16), nreg=4, name="QT")
                ktok = [Tt(sub.sbuf("ktok%d" % i, [128, 512], F32), name="ktok%d" % i) for i in range(2)]
                vtok = [Tt(sub.sbuf("vtok%d" % i, [128, 512], F32), name="vtok%d" % i) for i in range(2)]
                ftok = [Tt(sub.sbuf("ftok%d" % i, [128, 8], F32), name="ftok%d" % i) for i in range(2)]
                ltok = [Tt(sub.sbuf("ltok%d" % i, [128, 8], F32), name="ltok%d" % i) for i in range(2)]
                cfm = Tt(sub.sbuf("cfm", [8, WA], F32), name="cfm")
                cr1 = Tt(sub.sbuf("cr1", [8, WA], F32), name="cr1")
                cr2 = Tt(sub.sbuf("cr2", [8, WA], F32), name="cr2")
                midt = Tt(sub.sbuf("midt", [8, WA], BF16), name="midt")
                cq96 = Tt(sub.sbuf("cq96", [72, WA], BF16), name="cq96")
                fw.op("pool", lambda e: e.memset(cq96.t[:], 0.0), writes=[cq96.r])
                pts = [Tt(sub.sbuf("pt%d" % i, [128, WA], BF16), name="pt%d" % i) for i in range(4)]
                rsf = Tt(sub.sbuf("rsf", [128, WA], F32), name="rsf")
                rcp = Tt(sub.sbuf("rcp", [64, WA], F32), name="rcp")
                yfT = Tt(sub.sbuf("yfT", [128, 4, WA], BF16), nreg=4, name="yfT")
                it = 0
                ik = 0
                ipt = 0
                a1_tiles = [(s_, t0_, w_) for s_ in range(3) for (t0_, w_, j_) in tiles_of(s_, WA)]
                a1_idx = [0]

                def a1_norm(idx):
                    s_, t0_, w_ = a1_tiles[idx]
                    xT_ = xTa[idx % 2]
                    load_xT(xT_, t0_, w_)
                    norm_fm(xT_, w_, sq, tmp, lnv, rstd, hT,
                            lambda c, s_=s_: G1.t[:, l, s_, c:c + 1], lambda c, s_=s_: mod.t[:, l, 0, s_, c:c + 1], hT.rs)
                for s in range(3):
                    off, T = seqs[s]
                    kbase = 0
                    if s == 2:
                        kbase = PAST
                        for cb in range(PAST // 128):
                            kt_ = ktok[ik % 2]
                            vt_ = vtok[ik % 2]
                            ft_ = ltok[ik % 2]
                            ik += 1
                            fw.dma("sp", kt_.t[:], I["ck"][l][cb * 128:(cb + 1) * 128, :], writes=[kt_.r], stream="cldk%d" % (ik % 2))
                            fw.dma("sp", vt_.t[:], I["cv"][l][cb * 128:(cb + 1) * 128, :], writes=[vt_.r], stream="cldv%d" % (ik % 2))
                            fw.dma("sp", ft_.t[:], I["cl"][l][cb * 128:(cb + 1) * 128, :], writes=[ft_.r], stream="cldf%d" % (ik % 2))
                            pb = next_bank()
                            for pc in range(4):
                                fw.op("pe", lambda e, pc=pc, pb=pb, kt_=kt_: e.transpose(
                                    pb.t[:, pc * 128:(pc + 1) * 128], kt_.t[:, pc * 128:(pc + 1) * 128], ident.t[:]),
                                    reads=[kt_.r, ident.r], writes=[pb.r], signal=(pc == 3))
                            fw.op("act", lambda e, pb=pb, cb=cb: e.activation(
                                out=KT.t[:, :, cb * 128:(cb + 1) * 128],
                                in_=pb.t[:, :].rearrange("p (c t) -> p c t", c=4), func=AF.Copy),
                                reads=[pb.r], writes=[KT.rs[cb]])
                            fw.op("dve", lambda e, vt_=vt_, cb=cb: e.tensor_copy(
                                out=Vx.t[:, cb, :, 0:64], in_=vt_.t[:, :].rearrange("p (h d) -> p h d", h=8)),
                                reads=[vt_.r], writes=[Vx.rs[cb]])
                            pc_ = next_bank()
                            fw.op("pe", lambda e, pc_=pc_, ft_=ft_, cb=cb: e.matmul(
                                pc_.t[:, 0:8], lhsT=trif.t[:], rhs=ft_.t[:], start=True, stop=(cb == 0)),
                                reads=[trif.r, ft_.r], writes=[pc_.r], signal=(cb == 0))
                            if cb > 0:
                                fw.op("pe", lambda e, pc_=pc_, cb=cb: e.matmul(
                                    pc_.t[:, 0:8], lhsT=self127.t[:], rhs=ctok.t[:, cb - 1, :], start=False, stop=True),
                                    reads=[self127.r, ctok.rs[cb - 1]], writes=[pc_.r])
                            fw.op("dve", lambda e, pc_=pc_, cb=cb: e.tensor_copy(out=ctok.t[:, cb, :], in_=pc_.t[:, 0:8]),
                                  reads=[pc_.r], writes=[ctok.rs[cb]])
                            fw.op("act", lambda e, pc_=pc_, cb=cb: e.activation(
                                out=negc.t[:, cb, :], in_=pc_.t[:, 0:8], func=AF.Copy, scale=-1.0),
                                reads=[pc_.r], writes=[negc.rs[cb]])
                    dK = O["fkp"][l][s] if s < 2 else O["fks"][l]
                    dV = O["fvp"][l][s] if s < 2 else O["fvs"][l]
                    dF = O["flp"][l][s] if s < 2 else O["fls"][l]
                    def a1_tile(s, t0, w, j, xT, off, kbase, dK, dV, dF, l=l):
                        nonlocal ik, ipt
                        lt0 = t0 - off
                        kt0 = kbase + lt0
                        nb = (w + 127) // 128
                        pw = min(w, 128)
                        if a1_idx[0] == 0:
                            a1_norm(0)
                        for pc in range(4):
                            pq = next_bank()
                            for kc in range(8):
                                fw.op("pe", lambda e, kc=kc, pc=pc, pq=pq, w=w: e.matmul(
                                    pq.t[:, 0:w], lhsT=wf.t[:, kc, pc * 128:(pc + 1) * 128], rhs=hT.t[:, kc, 0:w],
                                    start=(kc == 0), stop=(kc == 7)),
                                    reads=[wf.r, hT.rs[kc]], writes=[pq.r], signal=(kc == 7))
                            fw.op("act", lambda e, pc=pc, pq=pq, w=w: e.activation(
                                out=QT.t[:, pc, 0:w], in_=pq.t[:, 0:w], func=AF.Copy, scale=0.125),
                                reads=[pq.r], writes=[QT.rs[pc]])
                            pk = next_bank()
                            for kc in range(8):
                                fw.op("pe", lambda e, kc=kc, pc=pc, pk=pk, w=w: e.matmul(
                                    pk.t[:, 0:w], lhsT=wf.t[:, kc, 512 + pc * 128:512 + (pc + 1) * 128],
                                    rhs=hT.t[:, kc, 0:w], start=(kc == 0), stop=(kc == 7)),
                                    reads=[wf.r, hT.rs[kc]], writes=[pk.r], signal=(kc == 7))
                            kregs = [KT.rs[(kt0 + tb * 128) // 128] for tb in range(nb)]
                            fw.op("dve", lambda e, pc=pc, pk=pk, w=w, kt0=kt0: e.tensor_copy(
                                out=KT.t[:, pc, kt0:kt0 + w], in_=pk.t[:, 0:w]), reads=[pk.r], writes=kregs)
                        for tb in range(nb):
                            kb = (kt0 + tb * 128) // 128
                            kt_ = ktok[ik % 2]
                            vt_ = vtok[ik % 2]
                            ft_ = ftok[ik % 2]
                            lt_ = ltok[ik % 2]
                            ik += 1
                            for (dst_t, c0, ncol, dd, eng) in ((kt_, 512, 512, dK, "act"), (vt_, 1024, 512, dV, "dve")):
                                pb = next_bank()
                                for kc in range(8):
                                    fw.op("pe", lambda e, kc=kc, pb=pb, tb=tb, c0=c0, ncol=ncol, pw=pw: e.matmul(
                                        pb.t[0:pw, 0:ncol], lhsT=hT.t[:, kc, tb * 128:tb * 128 + pw],
                                        rhs=wf.t[:, kc, c0:c0 + ncol], start=(kc == 0), stop=(kc == 7)),
                                        reads=[wf.r, hT.rs[kc]], writes=[pb.r], signal=(kc == 7))
                                if eng == "act":
                                    fw.op("act", lambda e, pb=pb, dst_t=dst_t, pw=pw: e.activation(
                                        out=dst_t.t[0:pw, :], in_=pb.t[0:pw, :], func=AF.Copy),
                                        reads=[pb.r], writes=[dst_t.r])
                                else:
                                    fw.op("dve", lambda e, pb=pb, dst_t=dst_t, pw=pw: e.tensor_copy(
                                        out=dst_t.t[0:pw, :], in_=pb.t[0:pw, :]), reads=[pb.r], writes=[dst_t.r])
                                r0 = lt0 + tb * 128
                                fw.dma("sp", dd[r0:r0 + pw, :], dst_t.t[0:pw, :], reads=[dst_t.r],
                                       stream="kvo%s%d" % (eng[0], ik % 2))
                            fw.op("pool", lambda e, vt_=vt_, kb=kb, pw=pw: e.tensor_copy(
                                out=Vx.t[0:pw, kb, :, 0:64], in_=vt_.t[0:pw, :].rearrange("p (h d) -> p h d", h=8)),
                                reads=[vt_.r], writes=[Vx.rs[kb]])
                            pf = next_bank()
                            for kc in range(8):
                                fw.op("pe", lambda e, kc=kc, pf=pf, tb=tb, pw=pw: e.matmul(
                                    pf.t[0:pw, 0:8], lhsT=hT.t[:, kc, tb * 128:tb * 128 + pw],
                                    rhs=wf.t[:, kc, 1536:1544], start=(kc == 0), stop=(kc == 7)),
                                    reads=[wf.r, hT.rs[kc]], writes=[pf.r], signal=(kc == 7))
                            fw.op("dve", lambda e, pf=pf, ft_=ft_, pw=pw: e.tensor_tensor(
                                out=ft_.t[0:pw, :], in0=pf.t[0:pw, 0:8], in1=bfb.t[0:pw, l, :], op=ALU.add),
                                reads=[pf.r, bfb.r], writes=[ft_.r])
                            fw.op("act", lambda e, ft_=ft_, pw=pw: e.activation(
                                out=ft_.t[0:pw, :], in_=ft_.t[0:pw, :], func=AF.Exp, scale=-1.0),
                                reads=[ft_.r], writes=[ft_.r])
                            fw.op("act", lambda e, ft_=ft_, pw=pw: e.activation(
                                out=ft_.t[0:pw, :], in_=ft_.t[0:pw, :], func=AF.Ln, bias=1.0),
                                reads=[ft_.r], writes=[ft_.r])
                            fw.op("dve", lambda e, ft_=ft_, lt_=lt_, pw=pw: e.tensor_scalar_mul(
                                out=lt_.t[0:pw, :], in0=ft_.t[0:pw, :], scalar1=-1.0), reads=[ft_.r], writes=[lt_.r])
                            r0 = lt0 + tb * 128
                            fw.dma("sp", dF[r0:r0 + pw, :], lt_.t[0:pw, :], reads=[lt_.r], stream="kvof%d" % (ik % 2))
                            pc_ = next_bank()
                            first = (kb == 0)
                            fw.op("pe", lambda e, pc_=pc_, lt_=lt_, pw=pw, first=first: e.matmul(
                                pc_.t[0:pw, 0:8], lhsT=trif.t[0:pw, 0:pw], rhs=lt_.t[0:pw, :], start=True, stop=first),
                                reads=[trif.r, lt_.r], writes=[pc_.r], signal=first)
                            if not first:
                                fw.op("pe", lambda e, pc_=pc_, kb=kb, pw=pw: e.matmul(
                                    pc_.t[0:pw, 0:8], lhsT=self127.t[:, 0:pw], rhs=ctok.t[:, kb - 1, :],
                                    start=False, stop=True),
                                    reads=[self127.r, ctok.rs[kb - 1]], writes=[pc_.r])
                            fw.op("dve", lambda e, pc_=pc_, kb=kb, pw=pw: e.tensor_copy(
                                out=ctok.t[0:pw, kb, :], in_=pc_.t[0:pw, 0:8]), reads=[pc_.r], writes=[ctok.rs[kb]])
                            fw.op("act", lambda e, pc_=pc_, kb=kb, pw=pw: e.activation(
                                out=negc.t[0:pw, kb, :], in_=pc_.t[0:pw, 0:8], func=AF.Copy, scale=-1.0),
                                reads=[pc_.r], writes=[negc.rs[kb]])
                            pt_ = next_bank()
                            fw.op("pe", lambda e, pt_=pt_, kb=kb, pw=pw: e.transpose(
                                pt_.t[0:8, 0:pw], ctok.t[0:pw, kb, :], ident.t[0:pw, 0:pw]),
                                reads=[ctok.rs[kb], ident.r], writes=[pt_.r])
                            fw.op("dve", lambda e, pt_=pt_, tb=tb, pw=pw: e.tensor_copy(
                                out=cfm.t[:, tb * 128:tb * 128 + pw], in_=pt_.t[0:8, 0:pw]),
                                reads=[pt_.r], writes=[cfm.r])
                        fw.op("act", lambda e, w=w: e.activation(out=cq96.t[0:8, 0:w], in_=cfm.t[:, 0:w], func=AF.Copy),
                              reads=[cfm.r], writes=[cq96.r])
                        fw.op("dve", lambda e, w=w: e.tensor_tensor(out=cr1.t[:, 0:w], in0=cfm.t[:, 0:w],
                                                                    in1=cq96.t[0:8, 0:w], op=ALU.subtract),
                              reads=[cfm.r, cq96.r], writes=[cr1.r])
                        fw.op("act", lambda e, w=w: e.activation(out=midt.t[:, 0:w], in_=cr1.t[:, 0:w], func=AF.Copy),
                              reads=[cr1.r], writes=[midt.r])
                        fw.op("pool", lambda e, w=w: e.tensor_copy(out=cq96.t[32:40, 0:w], in_=midt.t[:, 0:w]),
                              reads=[midt.r], writes=[cq96.r])
                        fw.op("dve", lambda e, w=w: e.tensor_tensor(out=cr2.t[:, 0:w], in0=cr1.t[:, 0:w],
                                                                    in1=midt.t[:, 0:w], op=ALU.subtract),
                              reads=[cr1.r, midt.r], writes=[cr2.r])
                        fw.op("act", lambda e, w=w: e.activation(out=cq96.t[64:72, 0:w], in_=cr2.t[:, 0:w], func=AF.Copy),
                              reads=[cr2.r], writes=[cq96.r])
                        a1_idx[0] += 1
                        if a1_idx[0] < len(a1_tiles):
                            a1_norm(a1_idx[0])
                        kb_first_tile = kt0 // 128
                        nkb = kb_first_tile + nb
                        pending_epi = []
                        for h in range(8):
                            hr = slice((h % 2) * 64, (h % 2) * 64 + 64)
                            hp = h // 2
                            ob = banks[6 + (h % 2)]
                            blocks = []
                            for kb in range(nkb):
                                if kb < kb_first_tile:
                                    q0, rows, diag = 0, 128, False
                                else:
                                    q0, rows, diag = (kb - kb_first_tile) * 128, pw, True
                                blocks.append((kb, q0, rows, diag))
                            sbanks = {}
                            ptl = {}

                            def emit_s(bi, h=h, hr=hr, hp=hp):
                                kb, q0, rows, diag = blocks[bi]
                                sb = next_bank(pool=(0, 1, 2, 3))
                                sbanks[bi] = sb
                                fw.op("pe", lambda e, sb=sb, kb=kb, q0=q0, rows=rows: e.matmul(
                                    sb.t[0:rows, q0:w], lhsT=KT.t[hr, hp, kb * 128:kb * 128 + rows],
                                    rhs=QT.t[hr, hp, q0:w], start=True, stop=False),
                                    reads=[KT.rs[kb], QT.rs[hp]], writes=[sb.r], signal=False)
                                fw.op("pe", lambda e, sb=sb, q0=q0, rows=rows: e.matmul(
                                    sb.t[0:rows, q0:w], lhsT=selh.t[0:72, h, 0:rows], rhs=cq96.t[0:72, q0:w],
                                    start=False, stop=(not diag)),
                                    reads=[selh.r, cq96.r], writes=[sb.r], signal=(not diag))
                                if diag:
                                    fw.op("pe", lambda e, sb=sb, q0=q0, rows=rows: e.matmul(
                                        sb.t[0:rows, q0:q0 + rows], lhsT=identb.t[0:rows, 0:rows],
                                        rhs=maskneg.t[0:rows, 0:rows], start=False, stop=True),
                                        reads=[identb.r, maskneg.r], writes=[sb.r])

                            def emit_pv(bi, h=h, ob=ob):
                                nonlocal ipt
                                kb, q0, rows, diag = blocks[bi]
                                sb = sbanks.pop(bi)
                                pt = pts[ipt % 4]
                                ipt += 1
                                fw.op("act", lambda e, sb=sb, pt=pt, kb=kb, q0=q0, rows=rows: e.activation(
                                    out=pt.t[0:rows, q0:w], in_=sb.t[0:rows, q0:w], func=AF.Exp,
                                    bias=negc.t[0:rows, kb, h:h + 1]),
                                    reads=[sb.r, negc.rs[kb]], writes=[pt.r])
                                last = (bi == len(blocks) - 1)
                                fw.op("pe", lambda e, pt=pt, kb=kb, q0=q0, rows=rows, bi=bi, last=last: e.matmul(
                                    ob.t[0:65, q0:w], lhsT=Vx.t[0:rows, kb, h, :], rhs=pt.t[0:rows, q0:w],
                                    start=(bi == 0), stop=last),
                                    reads=[Vx.rs[kb], pt.r], writes=[ob.r], signal=last)
                            LOOK = 2
                            nbk = len(blocks)
                            for bi in range(min(LOOK, nbk)):
                                emit_s(bi)
                            for bi in range(nbk):
                                emit_pv(bi)
                                if bi + LOOK < nbk:
                                    emit_s(bi + LOOK)
                            def epilogue(ob=ob, hr=hr, hp=hp):
                                fw.op("act", lambda e, ob=ob, w=w: e.activation(
                                    out=rsf.t[64:65, 0:w], in_=ob.t[64:65, 0:w], func=AF.Copy), reads=[ob.r], writes=[rsf.r])
                                pr = next_bank(pool=(4, 5))
                                fw.op("pe", lambda e, pr=pr, w=w: e.matmul(
                                    pr.t[0:64, 0:w], lhsT=onesf.t[64:65, 0:64], rhs=rsf.t[64:65, 0:w], start=True, stop=True),
                                    reads=[onesf.r, rsf.r], writes=[pr.r])
                                fw.op("dve", lambda e, pr=pr, w=w: e.reciprocal(out=rcp.t[:, 0:w], in_=pr.t[0:64, 0:w]),
                                      reads=[pr.r], writes=[rcp.r])
                                fw.op("dve", lambda e, ob=ob, hr=hr, hp=hp, w=w: e.tensor_tensor(
                                    out=yfT.t[hr, hp, 0:w], in0=ob.t[0:64, 0:w], in1=rcp.t[:, 0:w], op=ALU.mult),
                                    reads=[ob.r, rcp.r], writes=[yfT.rs[hp]])
                            if pending_epi:
                                pending_epi.pop()()
                            pending_epi.append(epilogue)
                        while pending_epi:
                            pending_epi.pop()()
                        fw.dma("sp", yfox.rearrange("(c p) t -> p c t", p=128)[:, :, t0:t0 + w], yfT.t[:, :, 0:w],
                               reads=yfT.rs, writes=[yfreg(t0)], stream="yfst")
                    for (t0, w, j) in tiles_of(s, WA):
                        xT = xTa[it % 2]
                        it += 1
                        a1_tile(s, t0, w, j, xT, off, kbase, dK, dV, dF)
                fw.flush()
                default_pool[0] = (0, 1, 2, 3, 4, 5, 6, 7)
            if stage >= 2:
              with contextlib.ExitStack() as ph:
                sub = FWScope(fw, ph)
                default_pool[0] = (0, 1, 2, 3, 4)
                sink = [fw]
                WA = 256
                RW0, HG0 = 0, RW_COLS + FOX_COLS
                NWI = RW_COLS + HG_COLS
                wi = Tt(sub.sbuf("wi", [128, 8, NWI], BF16), name="wi")
                wo = Tt(sub.sbuf("wo", [128, 8, D], BF16), name="wo")
                for kc in range(8):
                    fw.dma("pool", wi.t[:, kc, 0:RW_COLS], I["w_in"][l][kc * 128:(kc + 1) * 128, 0:RW_COLS],
                           writes=[wi.r], stream="wld", group=True)
                    fw.dma("pool", wi.t[:, kc, RW_COLS:NWI], I["w_in"][l][kc * 128:(kc + 1) * 128, HG0:HG0 + HG_COLS],
                           writes=[wi.r], stream="wld", group=True)
                    fw.dma("pool", wo.t[:, kc, :], I["w_out"][l][kc * 128:(kc + 1) * 128, :], writes=[wo.r], stream="wld", group=True)
                xTa = [Tt(sub.sbuf("xTa%d" % i, [128, 8, WA], F32), nreg=8, name="xTa%d" % i) for i in range(1)]
                sq = Tt(sub.sbuf("sqa", [128, 8, WA], BF16), name="sqa")
                hT = Tt(sub.sbuf("hTa", [128, 8, WA], BF16), nreg=8, name="hTa")
                tmp = [Tt(sub.sbuf("tmpa%d" % i, [128, WA], F32), name="tmpa%d" % i) for i in range(2)]
                lnv = Tt(sub.sbuf("lnva", [128, WA], F32), name="lnva")
                rstd = Tt(sub.sbuf("rstda", [128, WA], F32), name="rstda")
                ymix = Tt(sub.sbuf("ymix", [128, 8, WA], BF16), nreg=8, name="ymix")
                fw.op("pool", lambda e: e.memset(ymix.t[:], 0.0), writes=ymix.rs)

                def S(name, shape=None, dt=F32, nreg=1):
                    return Tt(sub.sbuf(name, shape or [128, WA], dt), nreg=nreg, name=name)

                def proj_fm(c0, w, M=128):
                    pb = next_bank()
                    for kc in range(8):
                        sink[0].op("pe", lambda e, kc=kc: e.matmul(
                            pb.t[0:M, 0:w], lhsT=wi.t[:, kc, c0:c0 + M], rhs=hT.t[:, kc, 0:w],
                            start=(kc == 0), stop=(kc == 7)),
                            reads=[wi.r, hT.rs[kc]], writes=[pb.r], signal=(kc == 7))
                    return pb

                NCH = WA // 32
                hg_E = [S("hg_E%d" % i) for i in range(2)]
                hg_KK = [S("hg_KK%d" % i) for i in range(2)]
                hg_B = [S("hg_B%d" % i) for i in range(2)]
                hg_D = S("hg_D")
                hg_X = S("hg_X")
                hg_Q = S("hg_Q")
                hg_G = [S("hg_G%d" % i) for i in range(2)]
                hg_Qt = S("hg_Qt", [128, 2, WA], BF16, nreg=2)
                hg_Kh = S("hg_Kh", [128, 2, WA], BF16, nreg=2)
                hg_Ke = S("hg_Ke", [128, 2, WA], BF16, nreg=2)
                hg_ebl = S("hg_ebl", [128, 2, NCH], F32, nreg=2)
                hg_ebm = S("hg_ebm", [128, 2, NCH], F32, nreg=2)
                hg_Vh = S("hg_Vh", [128, WA // 128, 256], BF16, nreg=4)
                hg_KeT = S("hg_KeT", [128, WA // 128, 4, 256], BF16, nreg=4)
                hg_AT = S("hg_AT", [128, WA // 128, 2, 2, 128], BF16, nreg=4)
                hg_Sm = S("hg_Sm", [128, 2, 128], F32, nreg=2)
                hg_Sbd = S("hg_Sbd", [128, 2, 128], BF16, nreg=2)
                hg_sq = S("hg_sq", [128, WA], BF16)
                hg_t1 = S("hg_t1")

                def hg_init(s):
                    fw.op("pool", lambda e: e.memset(hg_Sm.t[:], 0.0), writes=hg_Sm.rs)
                    if s == 2:
                        for h in range(4):
                            hr = slice((h % 2) * 64, (h % 2) * 64 + 64)
                            fw.dma("sp", hg_Sm.t[hr, h // 2, (h % 2) * 64:(h % 2) * 64 + 64], I["shg"][l][h],
                                   writes=[hg_Sm.rs[h // 2]], stream="stld", group=True)

                def hg_final(s):
                    dst = O["hgp"][l][s] if s < 2 else O["hgs"][l]
                    for h in range(4):
                        hr = slice((h % 2) * 64, (h % 2) * 64 + 64)
                        fw.dma("sp", dst[h], hg_Sm.t[hr, h // 2, (h % 2) * 64:(h % 2) * 64 + 64],
                               reads=[hg_Sm.rs[h // 2]], stream="ststhg%d" % s, group=True)

                def hg_tile(s, w):
                    nch = w // 32
                    nb = (w + 127) // 128
                    pw = min(w, 128)
                    c_q, c_f, c_i, c_g = RW_COLS, RW_COLS + 256, RW_COLS + 512, RW_COLS + 768
                    for tb in range(nb):
                        pb = next_bank()
                        for kc in range(8):
                            sink[0].op("pe", lambda e, kc=kc, tb=tb, pb=pb: e.matmul(
                                pb.t[0:pw, 0:256], lhsT=hT.t[:, kc, tb * 128:tb * 128 + pw], rhs=wi.t[:, kc, c_i:c_i + 256],
                                start=(kc == 0), stop=(kc == 7)),
                                reads=[wi.r, hT.rs[kc]], writes=[pb.r], signal=(kc == 7))
                        sink[0].op("act", lambda e, tb=tb, pb=pb: e.activation(
                            out=hg_Vh.t[0:pw, tb, :], in_=pb.t[0:pw, 0:256], func=AF.Copy),
                            reads=[pb.r], writes=[hg_Vh.rs[tb]])
                    for pc in range(2):
                        E, KK, B, G = hg_E[pc], hg_KK[pc], hg_B[pc], hg_G[pc]
                        lb_ap = lbT.t[:, l, pc:pc + 1]
                        oml_ap = omlT.t[:, l, pc:pc + 1]
                        noml_ap = nomlT.t[:, l, pc:pc + 1]
                        pf = proj_fm(c_f + pc * 128, w)
                        sink[0].op("act", lambda e, pf=pf, E=E: e.activation(out=E.t[:, 0:w], in_=pf.t[:, 0:w], func=AF.Exp,
                                                                      scale=-1.0), reads=[pf.r], writes=[E.r])
                        sink[0].op("dve", lambda e, E=E: e.tensor_scalar_add(out=E.t[:, 0:w], in0=E.t[:, 0:w], scalar1=1.0),
                              reads=[E.r], writes=[E.r])
                        sink[0].op("dve", lambda e, E=E: e.reciprocal(out=E.t[:, 0:w], in_=E.t[:, 0:w]),
                              reads=[E.r], writes=[E.r])
                        sink[0].op("dve", lambda e, E=E, KK=KK, noml_ap=noml_ap, oml_ap=oml_ap: e.tensor_scalar(
                            out=KK.t[:, 0:w], in0=E.t[:, 0:w], scalar1=noml_ap, scalar2=oml_ap, op0=ALU.mult, op1=ALU.add),
                            reads=[E.r, nomlT.r, omlT.r], writes=[KK.r])
                        sink[0].op("act", lambda e, E=E, oml_ap=oml_ap, lb_ap=lb_ap: e.activation(
                            out=E.t[:, 0:w], in_=E.t[:, 0:w], func=AF.Ln, scale=oml_ap, bias=lb_ap),
                            reads=[E.r, omlT.r, lbT.r], writes=[E.r])
                        sink[0].op("dve", lambda e, E=E, B=B: e.tensor_tensor_scan(
                            out=B.t[:, 0:w], data0=rmask32.t[:, 0:w], data1=E.t[:, 0:w], initial=0.0,
                            op0=ALU.mult, op1=ALU.add), reads=[E.r, rmask32.r], writes=[B.r])
                        Bv = B.t[:, 0:w].rearrange("p (c t) -> p c t", t=32)
                        Dv = hg_D.t[:, 0:w].rearrange("p (c t) -> p c t", t=32)
                        sink[0].op("dve", lambda e, Bv=Bv, Dv=Dv: e.tensor_tensor(
                            out=Dv, in0=Bv, in1=Bv[:, :, 15:16].to_broadcast([128, nch, 32]), op=ALU.subtract),
                            reads=[B.r], writes=[hg_D.r])
                        pq = proj_fm(c_q + pc * 128, w)
                        sink[0].op("act", lambda e, pq=pq: e.activation(out=hg_Q.t[:, 0:w], in_=pq.t[:, 0:w], func=AF.Copy),
                              reads=[pq.r], writes=[hg_Q.r])
                        sink[0].op("act", lambda e: e.activation(out=hg_X.t[:, 0:w], in_=hg_D.t[:, 0:w], func=AF.Exp),
                              reads=[hg_D.r], writes=[hg_X.r])
                        sink[0].op("dve", lambda e, pc=pc: e.tensor_tensor(out=hg_Qt.t[:, pc, 0:w], in0=hg_Q.t[:, 0:w],
                                                                      in1=hg_X.t[:, 0:w], op=ALU.mult),
                              reads=[hg_Q.r, hg_X.r], writes=[hg_Qt.rs[pc]])
                        sink[0].op("act", lambda e: e.activation(out=hg_X.t[:, 0:w], in_=hg_D.t[:, 0:w], func=AF.Exp, scale=-1.0),
                              reads=[hg_D.r], writes=[hg_X.r])
                        sink[0].op("dve", lambda e, pc=pc, KK=KK: e.tensor_tensor(out=hg_Kh.t[:, pc, 0:w], in0=KK.t[:, 0:w],
                                                                             in1=hg_X.t[:, 0:w], op=ALU.mult),
                              reads=[KK.r, hg_X.r], writes=[hg_Kh.rs[pc]])
                        sink[0].op("dve", lambda e, Bv=Bv, Dv=Dv: e.tensor_tensor(
                            out=Dv, in0=Bv[:, :, 31:32].to_broadcast([128, nch, 32]), in1=Bv, op=ALU.subtract),
                            reads=[B.r], writes=[hg_D.r])
                        sink[0].op("act", lambda e: e.activation(out=hg_X.t[:, 0:w], in_=hg_D.t[:, 0:w], func=AF.Exp),
                              reads=[hg_D.r], writes=[hg_X.r])
                        sink[0].op("dve", lambda e, pc=pc, KK=KK: e.tensor_tensor(out=hg_Ke.t[:, pc, 0:w], in0=KK.t[:, 0:w],
                                                                             in1=hg_X.t[:, 0:w], op=ALU.mult),
                              reads=[KK.r, hg_X.r], writes=[hg_Ke.rs[pc]])
                        sink[0].op("act", lambda e, pc=pc, Bv=Bv: e.activation(out=hg_ebl.t[:, pc, 0:nch], in_=Bv[:, :, 31],
                                                                          func=AF.Exp), reads=[B.r], writes=[hg_ebl.rs[pc]])
                        sink[0].op("act", lambda e, pc=pc, Bv=Bv: e.activation(out=hg_ebm.t[:, pc, 0:nch], in_=Bv[:, :, 15],
                                                                          func=AF.Exp), reads=[B.r], writes=[hg_ebm.rs[pc]])
                        pg = proj_fm(c_g + pc * 128, w)
                        sink[0].op("act", lambda e, pg=pg, G=G: e.activation(out=G.t[:, 0:w], in_=pg.t[:, 0:w], func=AF.Silu),
                              reads=[pg.r], writes=[G.r])
                    HGDBG = int(os.environ.get("HGDBG", "9"))
                    if HGDBG < 2:
                        return
                    for tb in range(nb):
                        pAs = [next_bank(), next_bank()]
                        for h in range(4):
                            hr = slice((h % 2) * 64, (h % 2) * 64 + 64)
                            pA = pAs[h % 2]
                            sink[0].op("pe", lambda e, h=h, hr=hr, tb=tb, pA=pA: e.matmul(
                                pA.t[0:pw, (h // 2) * 128:(h // 2) * 128 + pw], lhsT=hg_Kh.t[hr, h // 2, tb * 128:tb * 128 + pw],
                                rhs=hg_Qt.t[hr, h // 2, tb * 128:tb * 128 + pw], start=True, stop=True),
                                reads=[hg_Kh.rs[h // 2], hg_Qt.rs[h // 2]], writes=[pA.r], signal=(h >= 2))
                        for par in range(2):
                            pA = pAs[par]
                            sink[0].op("dve", lambda e, tb=tb, pA=pA, par=par: e.tensor_tensor(
                                out=hg_AT.t[0:pw, tb, par, :, 0:pw],
                                in0=pA.t[0:pw, 0:256].rearrange("p (h t) -> p h t", h=2)[:, :, 0:pw],
                                in1=maskbd.t[0:pw, 0:pw].unsqueeze(1).to_broadcast([pw, 2, pw]), op=ALU.mult),
                                reads=[pA.r, maskbd.r], writes=[hg_AT.rs[tb]])
                        if os.environ.get("HGSUB", "") == "A":
                            continue
                        pT = next_bank()
                        pTb = pT.t[:, :].bitcast(BF16)
                        for pc in range(2):
                            sink[0].op("pe", lambda e, pc=pc, tb=tb, pTb=pTb: e.transpose(
                                pTb[0:pw, pc * 128:(pc + 1) * 128], hg_Ke.t[:, pc, tb * 128:tb * 128 + pw], identb.t[:, :]),
                                reads=[hg_Ke.rs[pc], identb.r], writes=[pT.r], signal=(pc == 1))
                        for cc in range(min(4, nch - tb * 4)):
                            sink[0].op("act", lambda e, tb=tb, pTb=pTb, cc=cc: e.activation(
                                out=hg_KeT.t[0:pw, tb, cc, :], in_=pTb[0:pw, 0:256], func=AF.Identity,
                                scale=maskbd.t[0:pw, cc * 32 + 31:cc * 32 + 32]),
                                reads=[pT.r, maskbd.r], writes=[hg_KeT.rs[tb]])
                    if HGDBG < 3:
                        return
                    po = [banks[6], banks[7]]
                    for tb in range(nb):
                        for h in range(4):
                            sink[0].op("pe", lambda e, h=h, tb=tb: e.matmul(
                                po[h // 2].t[(h % 2) * 64:(h % 2) * 64 + 64, tb * 128:tb * 128 + pw],
                                lhsT=hg_Vh.t[0:pw, tb, h * 64:(h + 1) * 64], rhs=hg_AT.t[0:pw, tb, h % 2, h // 2, 0:pw],
                                start=True, stop=False, skip_group_check=True),
                                reads=[hg_Vh.rs[tb], hg_AT.rs[tb]], writes=[po[h // 2].r], signal=False)
                        for cc in range(min(4, nch - tb * 4)):
                            c = tb * 4 + cc
                            last = (c == nch - 1)
                            for pc in range(2):
                                sink[0].op("act", lambda e, pc=pc, c=c: e.activation(
                                    out=hg_Sbd.t[:, pc, :], in_=hg_Sm.t[:, pc, :], func=AF.Identity,
                                    scale=hg_ebm.t[:, pc, c:c + 1]),
                                    reads=[hg_Sm.rs[pc], hg_ebm.rs[pc]], writes=[hg_Sbd.rs[pc]])
                                sink[0].op("pe", lambda e, pc=pc, c=c: e.matmul(
                                    po[pc].t[:, c * 32:(c + 1) * 32], lhsT=hg_Sbd.t[:, pc, :],
                                    rhs=hg_Qt.t[:, pc, c * 32:(c + 1) * 32], start=False, stop=True,
                                    skip_group_check=True),
                                    reads=[hg_Sbd.rs[pc], hg_Qt.rs[pc]], writes=[po[pc].r], signal=last)
                                pS = next_bank()
                                sink[0].op("pe", lambda e, pc=pc, tb=tb, cc=cc, pS=pS: e.matmul(
                                    pS.t[:, 0:128], lhsT=hg_KeT.t[0:pw, tb, cc, pc * 128:(pc + 1) * 128],
                                    rhs=hg_Vh.t[0:pw, tb, pc * 128:(pc + 1) * 128], start=True, stop=True),
                                    reads=[hg_KeT.rs[tb], hg_Vh.rs[tb]], writes=[pS.r])
                                for hh in range(2):
                                    hr = slice(hh * 64, hh * 64 + 64)
                                    sink[0].op("dve", lambda e, pc=pc, c=c, hr=hr, pS=pS: e.scalar_tensor_tensor(
                                        out=hg_Sm.t[hr, pc, hr], in0=hg_Sm.t[hr, pc, hr], scalar=hg_ebl.t[hr, pc, c:c + 1],
                                        in1=pS.t[hr, hr], op0=ALU.mult, op1=ALU.add),
                                        reads=[hg_Sm.rs[pc], hg_ebl.rs[pc], pS.r], writes=[hg_Sm.rs[pc]])
                    if HGDBG < 4:
                        return
                    for pc in range(2):
                        G = hg_G[pc]
                        sink[0].op("act", lambda e, pc=pc: e.activation(out=hg_sq.t[:, 0:w], in_=po[pc].t[:, 0:w], func=AF.Square),
                              reads=[po[pc].r], writes=[hg_sq.r])
                        pn = next_bank()
                        sink[0].op("pe", lambda e, pn=pn: e.matmul(pn.t[:, 0:w], lhsT=onesbd.t[:], rhs=hg_sq.t[:, 0:w],
                                                              start=True, stop=True),
                              reads=[onesbd.r, hg_sq.r], writes=[pn.r])
                        sink[0].op("act", lambda e, pn=pn: e.activation(out=hg_X.t[:, 0:w], in_=pn.t[:, 0:w], func=AF.Ln,
                                                                   scale=1.0 / 64, bias=epsb.t[:]),
                              reads=[pn.r, epsb.r], writes=[hg_X.r])
                        sink[0].op("act", lambda e: e.activation(out=hg_X.t[:, 0:w], in_=hg_X.t[:, 0:w], func=AF.Exp, scale=-0.5),
                              reads=[hg_X.r], writes=[hg_X.r])
                        sink[0].op("dve", lambda e, pc=pc: e.tensor_tensor(out=hg_t1.t[:, 0:w], in0=po[pc].t[:, 0:w],
                                                                      in1=hg_X.t[:, 0:w], op=ALU.mult),
                              reads=[po[pc].r, hg_X.r], writes=[hg_t1.r])
                        sink[0].op("dve", lambda e, pc=pc, G=G: e.scalar_tensor_tensor(
                            out=ymix.t[:, 6 + pc, 0:w], in0=hg_t1.t[:, 0:w], scalar=hgng.t[:, l, pc:pc + 1], in1=G.t[:, 0:w],
                            op0=ALU.mult, op1=ALU.mult), reads=[hg_t1.r, hgng.r, G.r], writes=[ymix.rs[6 + pc]])

                C0 = 0.6065306597126334
                rw_Pb = S("rw_Pb", [128, 9, WA + 1], F32)
                rw_car = S("rw_car", [128, 9], F32)
                rw_c7 = S("rw_c7", [128, 7], F32)
                rw_tmp = [S("rw_tmp%d" % i) for i in range(6)]
                rw_tmpB = [S("rw_tmpB%d" % i) for i in range(3)]
                rw_SIG2 = [S("rw_SIG%d" % i) for i in range(2)]
                rw_A2 = [S("rw_A%d" % i) for i in range(2)]
                rw_L2 = [S("rw_L%d" % i) for i in range(2)]
                rw_KP2 = [S("rw_KP%d" % i) for i in range(2)]
                rw_KN2 = [S("rw_KN%d" % i) for i in range(2)]
                rw_Bf2 = [S("rw_Bf%d" % i) for i in range(2)]
                rw_sqb2 = [S("rw_sqbb%d" % i, [128, WA], BF16) for i in range(2)]
                rw_G = [S("rw_G%d" % i) for i in range(2)]
                rw_Yf = S("rw_Yf")
                rw_TW = S("rw_TW", [32, WA], BF16)
                rw_AL = S("rw_AL", [32, WA], BF16)
                rw_SG = S("rw_SG", [64, WA], BF16)
                rw_sqb = S("rw_sqb", [128, WA], BF16)
                rw_Kh = S("rw_Kh", [128, 2, WA], BF16, nreg=2)
                rw_Bm = S("rw_Bm", [128, 2, 2, WA], BF16, nreg=2)
                rw_QRm = S("rw_QRm", [128, 2, 2, WA // 32, 2, 64], BF16, nreg=2)
                rw_Ke = S("rw_Ke", [128, 2, WA], BF16, nreg=2)
                rw_Be = S("rw_Be", [128, 2, WA], BF16, nreg=2)
                rw_Vb = S("rw_Vb", [128, 2, WA], BF16, nreg=2)
                NCR = WA // 32
                rw_Vt = S("rw_Vt", [64, NCR, 256], BF16)
                rw_KeT = S("rw_KeT", [64, NCR, 256], BF16)
                rw_BeT = S("rw_BeT", [64, NCR, 256], BF16)
                rw_gC = S("rw_gC", [128, 2, NCR], F32, nreg=2)
                NCK = max(1, WA // 64)
                rw_AT12 = [S("rw_AT12_%d" % i, [64, 4, 2, 64], BF16) for i in range(NCK)]
                rw_AT34 = [S("rw_AT34_%d" % i, [64, 4, 2, 64], BF16) for i in range(NCK)]
                rw_X = [[S("rw_X%d_%d" % (i, j), [64, 4, 64], BF16) for j in range(2)] for i in range(NCK)]
                rw_XT = [[S("rw_XT%d_%d" % (i, j), [64, 4, 64], BF16) for j in range(2)] for i in range(NCK)]
                rw_TT = [[S("rw_TT%d_%d" % (i, j), [64, 4, 64], BF16) for j in range(2)] for i in range(NCK)]
                rw_Zb = S("rw_Zb", [64, 256], BF16)
                rw_Un = S("rw_Un", [64, 256], BF16)
                rw_Hm = S("rw_Hm", [128, 2, 128], F32, nreg=2)
                rw_Hbd = S("rw_Hbd", [128, 2, 128], BF16, nreg=2)
                rw_st = S("rw_st", [128, 2, 128], F32)

                def rw_init(s):
                    fw.op("pool", lambda e: e.memset(rw_Hm.t[:], 0.0), writes=rw_Hm.rs)
                    fw.op("pool", lambda e: e.memset(rw_Pb.t[:], 0.0), writes=[rw_Pb.r])
                    if s == 2:
                        fw.op("pool", lambda e: e.memset(rw_st.t[:], 0.0), writes=[rw_st.r])
                        for h in range(4):
                            hr = slice((h % 2) * 64, (h % 2) * 64 + 64)
                            fw.dma("sp", rw_st.t[hr, h // 2, (h % 2) * 64:(h % 2) * 64 + 64], I["srw"][l][h],
                                   writes=[rw_st.r], stream="stld", group=True)
                        for pc in range(2):
                            pb = next_bank()
                            fw.op("pe", lambda e, pc=pc, pb=pb: e.transpose(pb.t[:, 0:128], rw_st.t[:, pc, :], ident.t[:]),
                                  reads=[rw_st.r, ident.r], writes=[pb.r])
                            fw.op("dve", lambda e, pc=pc, pb=pb: e.tensor_copy(out=rw_Hm.t[:, pc, :], in_=pb.t[:, 0:128]),
                                  reads=[pb.r], writes=[rw_Hm.rs[pc]])
                        fw.dma("sp", rw_c7.t[:, :], I["ssh"][l].rearrange("(c p) -> p c", p=128),
                               writes=[rw_c7.r], stream="stld", group=True, allow_slow_non_contiguous=True)
                        fw.op("dve", lambda e: e.tensor_copy(out=rw_Pb.t[:, 0:6, 0], in_=rw_c7.t[:, 0:6]),
                              reads=[rw_c7.r], writes=[rw_Pb.r])
                        fw.op("dve", lambda e: e.tensor_copy(out=rw_Pb.t[0:32, 6, 0:1], in_=rw_c7.t[0:32, 6:7]),
                              reads=[rw_c7.r], writes=[rw_Pb.r])
                        fw.op("dve", lambda e: e.tensor_copy(out=rw_Pb.t[0:32, 7, 0:1], in_=rw_c7.t[32:64, 6:7]),
                              reads=[rw_c7.r], writes=[rw_Pb.r])
                        fw.op("dve", lambda e: e.tensor_copy(out=rw_Pb.t[0:64, 8, 0:1], in_=rw_c7.t[64:128, 6:7]),
                              reads=[rw_c7.r], writes=[rw_Pb.r])
                    for pc in range(2):
                        fw.op("act", lambda e, pc=pc: e.activation(out=rw_Hbd.t[:, pc, :], in_=rw_Hm.t[:, pc, :],
                                                                   func=AF.Copy), reads=[rw_Hm.rs[pc]], writes=[rw_Hbd.rs[pc]])

                def rw_final(s, w):
                    dst = O["rwp"][l][s] if s < 2 else O["rws"][l]
                    for pc in range(2):
                        pb = next_bank()
                        fw.op("pe", lambda e, pc=pc, pb=pb: e.transpose(pb.t[:, 0:128], rw_Hm.t[:, pc, :], ident.t[:]),
                              reads=[rw_Hm.rs[pc], ident.r], writes=[pb.r])
                        fw.op("dve", lambda e, pc=pc, pb=pb: e.tensor_copy(out=rw_st.t[:, pc, :], in_=pb.t[:, 0:128]),
                              reads=[pb.r], writes=[rw_st.r])
                    for h in range(4):
                        hr = slice((h % 2) * 64, (h % 2) * 64 + 64)
                        fw.dma("sp", dst[h], rw_st.t[hr, h // 2, (h % 2) * 64:(h % 2) * 64 + 64], reads=[rw_st.r],
                               stream="ststrs%d" % s, group=True)
                    dsh = O["rshp"][l][s] if s < 2 else O["rshs"][l]
                    fw.op("dve", lambda e: e.tensor_copy(out=rw_c7.t[:, 0:6], in_=rw_car.t[:, 0:6]), reads=[rw_car.r], writes=[rw_c7.r])
                    fw.op("dve", lambda e: e.tensor_copy(out=rw_c7.t[0:32, 6:7], in_=rw_car.t[0:32, 6:7]), reads=[rw_car.r], writes=[rw_c7.r])
                    fw.op("dve", lambda e: e.tensor_copy(out=rw_c7.t[32:64, 6:7], in_=rw_car.t[0:32, 7:8]), reads=[rw_car.r], writes=[rw_c7.r])
                    fw.op("dve", lambda e: e.tensor_copy(out=rw_c7.t[64:128, 6:7], in_=rw_car.t[0:64, 8:9]), reads=[rw_car.r], writes=[rw_c7.r])
                    fw.dma("sp", dsh.rearrange("(c p) -> p c", p=128), rw_c7.t[:, :], reads=[rw_c7.r],
                           stream="ststrc%d" % s, group=True, allow_slow_non_contiguous=True)

                def rw_tile(s, w):
                    C = min(64, w)
                    nch = w // C
                    nlev = {64: 5, 32: 4}[C]
                    rmask = rmask64 if C == 64 else rmask32
                    T0, T1, T2, T3, T4, T5 = rw_tmp
                    specs = [(i, i * 128, 128) for i in range(6)] + [(6, 768, 32), (7, 800, 32), (8, 832, 64)]
                    for (i, c0, M) in specs:
                        pb = proj_fm(c0, w, M)
                        sink[0].op("act", lambda e, i=i, M=M, pb=pb: e.activation(out=rw_Pb.t[0:M, i, 1:1 + w], in_=pb.t[0:M, 0:w],
                                                                          func=AF.Copy), reads=[pb.r], writes=[rw_Pb.r])
                    sink[0].op("act", lambda e: e.activation(out=rw_car.t[:, :], in_=rw_Pb.t[:, :, w], func=AF.Copy),
                          reads=[rw_Pb.r], writes=[rw_car.r])
                    for (i, c0, M) in specs:
                        sink[0].op("dve", lambda e, i=i, M=M: e.tensor_tensor(out=T0.t[0:M, 0:w], in0=rw_Pb.t[0:M, i, 0:w],
                                                                         in1=rw_Pb.t[0:M, i, 1:1 + w], op=ALU.subtract),
                              reads=[rw_Pb.r], writes=[T0.r])
                        sink[0].op("dve", lambda e, i=i, M=M: e.scalar_tensor_tensor(
                            out=rw_Pb.t[0:M, i, 1:1 + w], in0=T0.t[0:M, 0:w], scalar=mul.t[0:M, l, i:i + 1],
                            in1=rw_Pb.t[0:M, i, 1:1 + w], op0=ALU.mult, op1=ALU.add),
                            reads=[T0.r, mul.r, rw_Pb.r], writes=[rw_Pb.r])
                    sink[0].op("pool", lambda e: e.tensor_copy(out=rw_Pb.t[:, :, 0], in_=rw_car.t[:, :]),
                          reads=[rw_car.r, rw_Pb.r], writes=[rw_Pb.r])
                    XS = lambda i, M=128: rw_Pb.t[0:M, i, 1:1 + w]
                    sink[0].op("act", lambda e: e.activation(out=rw_TW.t[:, 0:w], in_=XS(6, 32), func=AF.Tanh),
                          reads=[rw_Pb.r], writes=[rw_TW.r])
                    sink[0].op("act", lambda e: e.activation(out=rw_AL.t[:, 0:w], in_=XS(7, 32), func=AF.Copy),
                          reads=[rw_Pb.r], writes=[rw_AL.r])
                    sink[0].op("act", lambda e: e.activation(out=T0.t[0:64, 0:w], in_=XS(8, 64), func=AF.Exp, scale=-1.0),
                          reads=[rw_Pb.r], writes=[T0.r])
                    sink[0].op("dve", lambda e: e.tensor_scalar_add(out=T0.t[0:64, 0:w], in0=T0.t[0:64, 0:w], scalar1=1.0),
                          reads=[T0.r], writes=[T0.r])
                    sink[0].op("dve", lambda e: e.reciprocal(out=T0.t[0:64, 0:w], in_=T0.t[0:64, 0:w]), reads=[T0.r], writes=[T0.r])
                    sink[0].op("act", lambda e: e.activation(out=rw_SG.t[:, 0:w], in_=T0.t[0:64, 0:w], func=AF.Copy),
                          reads=[T0.r], writes=[rw_SG.r])
                    outer_sink = sink[0]
                    pc_recs = [Rec(), Rec()]

                    def _pc_body(pc, T1, T2, T3, rw_SIG, rw_A, rw_L, rw_KP, rw_KN, rw_Bf, rw_sqb):
                        cs = slice(pc * 128, (pc + 1) * 128)
                        r_ap, k_ap, v_ap = XS(pc), XS(2 + pc), XS(4 + pc)
                        pw_ = next_bank()
                        sink[0].op("pe", lambda e, pw_=pw_, cs=cs: e.matmul(pw_.t[:, 0:w], lhsT=w2b.t[:, l, cs], rhs=rw_TW.t[:, 0:w],
                                                                    start=True, stop=True), reads=[w2b.r, rw_TW.r], writes=[pw_.r])
                        sink[0].op("act", lambda e, pw_=pw_, pc=pc: e.activation(out=rw_SIG.t[:, 0:w], in_=pw_.t[:, 0:w], func=AF.Exp,
                                                                         scale=-1.0, bias=nw0.t[:, l, pc:pc + 1]),
                              reads=[pw_.r, nw0.r], writes=[rw_SIG.r])
                        sink[0].op("dve", lambda e: e.tensor_scalar_add(out=rw_SIG.t[:, 0:w], in0=rw_SIG.t[:, 0:w], scalar1=1.0),
                              reads=[rw_SIG.r], writes=[rw_SIG.r])
                        sink[0].op("dve", lambda e: e.reciprocal(out=rw_SIG.t[:, 0:w], in_=rw_SIG.t[:, 0:w]),
                              reads=[rw_SIG.r], writes=[rw_SIG.r])
                        pa_ = next_bank()
                        sink[0].op("pe", lambda e, pa_=pa_, cs=cs: e.matmul(pa_.t[:, 0:w], lhsT=a2b.t[:, l, cs], rhs=rw_AL.t[:, 0:w],
                                                                    start=True, stop=True), reads=[a2b.r, rw_AL.r], writes=[pa_.r])
                        sink[0].op("act", lambda e, pa_=pa_, pc=pc: e.activation(out=rw_A.t[:, 0:w], in_=pa_.t[:, 0:w], func=AF.Exp,
                                                                         scale=-1.0, bias=na0.t[:, l, pc:pc + 1]),
                              reads=[pa_.r, na0.r], writes=[rw_A.r])
                        sink[0].op("dve", lambda e: e.tensor_scalar_add(out=rw_A.t[:, 0:w], in0=rw_A.t[:, 0:w], scalar1=1.0),
                              reads=[rw_A.r], writes=[rw_A.r])
                        sink[0].op("dve", lambda e: e.reciprocal(out=rw_A.t[:, 0:w], in_=rw_A.t[:, 0:w]), reads=[rw_A.r], writes=[rw_A.r])
                        pg_ = next_bank()
                        sink[0].op("pe", lambda e, pg_=pg_, cs=cs: e.matmul(pg_.t[:, 0:w], lhsT=g2b.t[:, l, cs], rhs=rw_SG.t[:, 0:w],
                                                                    start=True, stop=True), reads=[g2b.r, rw_SG.r], writes=[pg_.r])
                        sink[0].op("act", lambda e, pg_=pg_, pc=pc: e.activation(out=rw_G[pc].t[:, 0:w], in_=pg_.t[:, 0:w], func=AF.Copy),
                              reads=[pg_.r], writes=[rw_G[pc].r])
                        sink[0].op("dve", lambda e, pc=pc, k_ap=k_ap: e.tensor_scalar_mul(
                            out=rw_KN.t[:, 0:w], in0=k_ap, scalar1=rwp["rw_k_k"].t[:, l, pc:pc + 1]),
                            reads=[rw_Pb.r, rwp["rw_k_k"].r], writes=[rw_KN.r])
                        sink[0].op("act", lambda e: e.activation(out=rw_sqb.t[:, 0:w], in_=rw_KN.t[:, 0:w], func=AF.Square),
                              reads=[rw_KN.r], writes=[rw_sqb.r])
                        pn = next_bank()
                        sink[0].op("pe", lambda e, pn=pn: e.matmul(pn.t[:, 0:w], lhsT=onesbd.t[:], rhs=rw_sqb.t[:, 0:w],
                                                              start=True, stop=True), reads=[onesbd.r, rw_sqb.r], writes=[pn.r])
                        sink[0].op("dve", lambda e, pn=pn: e.tensor_scalar_max(out=T1.t[:, 0:w], in0=pn.t[:, 0:w], scalar1=1e-24),
                              reads=[pn.r], writes=[T1.r])
                        sink[0].op("act", lambda e: e.activation(out=T1.t[:, 0:w], in_=T1.t[:, 0:w], func=AF.Ln), reads=[T1.r], writes=[T1.r])
                        sink[0].op("act", lambda e: e.activation(out=T1.t[:, 0:w], in_=T1.t[:, 0:w], func=AF.Exp, scale=-0.5),
                              reads=[T1.r], writes=[T1.r])
                        sink[0].op("dve", lambda e: e.tensor_tensor(out=rw_KN.t[:, 0:w], in0=rw_KN.t[:, 0:w], in1=T1.t[:, 0:w], op=ALU.mult),
                              reads=[rw_KN.r, T1.r], writes=[rw_KN.r])
                        sink[0].op("dve", lambda e, pc=pc: e.tensor_scalar(
                            out=T1.t[:, 0:w], in0=rw_A.t[:, 0:w], scalar1=rwp["rw_k_a"].t[:, l, pc:pc + 1],
                            scalar2=omka.t[:, l, pc:pc + 1], op0=ALU.mult, op1=ALU.add),
                            reads=[rw_A.r, rwp["rw_k_a"].r, omka.r], writes=[T1.r])
                        sink[0].op("dve", lambda e, k_ap=k_ap: e.tensor_tensor(out=rw_KP.t[:, 0:w], in0=k_ap, in1=T1.t[:, 0:w], op=ALU.mult),
                              reads=[rw_Pb.r, T1.r], writes=[rw_KP.r])
                        sink[0].op("dve", lambda e: e.tensor_tensor(out=rw_Bf.t[:, 0:w], in0=rw_KN.t[:, 0:w], in1=rw_A.t[:, 0:w], op=ALU.mult),
                              reads=[rw_KN.r, rw_A.r], writes=[rw_Bf.r])
                        sink[0].op("dve", lambda e: e.tensor_tensor_scan(out=rw_L.t[:, 0:w], data0=rmask.t[:, 0:w], data1=rw_SIG.t[:, 0:w],
                                                                   initial=0.0, op0=ALU.mult, op1=ALU.add),
                              reads=[rw_SIG.r, rmask.r], writes=[rw_L.r])
                        Lv = rw_L.t[:, 0:w].rearrange("p (c t) -> p c t", t=C)
                        sink[0].op("act", lambda e: e.activation(out=T2.t[:, 0:w], in_=rw_L.t[:, 0:w], func=AF.Exp, scale=-C0),
                              reads=[rw_L.r], writes=[T2.r])
                        sink[0].op("dve", lambda e: e.tensor_tensor(out=T3.t[:, 0:w], in0=rw_L.t[:, 0:w], in1=rw_SIG.t[:, 0:w], op=ALU.subtract),
                              reads=[rw_L.r, rw_SIG.r], writes=[T3.r])
                        sink[0].op("act", lambda e: e.activation(out=T3.t[:, 0:w], in_=T3.t[:, 0:w], func=AF.Exp, scale=-C0),
                              reads=[T3.r], writes=[T3.r])
                        for par in range(2):
                            hm = onesbdf.t[:, par * 64:par * 64 + 1]
                            sink[0].op("dve", lambda e, pc=pc, par=par, hm=hm, r_ap=r_ap: e.scalar_tensor_tensor(
                                out=rw_QRm.t[:, par, pc, 0:nch, 1, 0:C], in0=r_ap.rearrange("p (c t) -> p c t", t=C), scalar=hm,
                                in1=T2.t[:, 0:w].rearrange("p (c t) -> p c t", t=C), op0=ALU.mult, op1=ALU.mult),
                                reads=[rw_Pb.r, T2.r, onesbdf.r], writes=[rw_QRm.rs[pc]])
                            sink[0].op("dve", lambda e, pc=pc, par=par, hm=hm: e.scalar_tensor_tensor(
                                out=rw_QRm.t[:, par, pc, 0:nch, 0, 0:C], in0=rw_KN.t[:, 0:w].rearrange("p (c t) -> p c t", t=C),
                                scalar=hm, in1=T3.t[:, 0:w].rearrange("p (c t) -> p c t", t=C), op0=ALU.mult, op1=ALU.mult),
                                reads=[rw_KN.r, T3.r, onesbdf.r], writes=[rw_QRm.rs[pc]])
                        sink[0].op("act", lambda e: e.activation(out=T2.t[:, 0:w], in_=rw_L.t[:, 0:w], func=AF.Exp, scale=C0),
                              reads=[rw_L.r], writes=[T2.r])
                        sink[0].op("dve", lambda e, pc=pc: e.tensor_tensor(out=rw_Kh.t[:, pc, 0:w], in0=rw_KP.t[:, 0:w], in1=T2.t[:, 0:w], op=ALU.mult),
                              reads=[rw_KP.r, T2.r], writes=[rw_Kh.rs[pc]])
                        for par in range(2):
                            hm = onesbdf.t[:, par * 64:par * 64 + 1]
                            sink[0].op("dve", lambda e, pc=pc, par=par, hm=hm: e.scalar_tensor_tensor(
                                out=rw_Bm.t[:, par, pc, 0:w], in0=rw_Bf.t[:, 0:w], scalar=hm, in1=T2.t[:, 0:w],
                                op0=ALU.mult, op1=ALU.mult), reads=[rw_Bf.r, T2.r, onesbdf.r], writes=[rw_Bm.rs[pc]])
                        sink[0].op("dve", lambda e, Lv=Lv: e.tensor_tensor(
                            out=T3.t[:, 0:w].rearrange("p (c t) -> p c t", t=C), in0=Lv[:, :, C - 1:C].to_broadcast([128, nch, C]),
                            in1=Lv, op=ALU.subtract), reads=[rw_L.r], writes=[T3.r])
                        sink[0].op("act", lambda e: e.activation(out=T3.t[:, 0:w], in_=T3.t[:, 0:w], func=AF.Exp, scale=-C0),
                              reads=[T3.r], writes=[T3.r])
                        sink[0].op("dve", lambda e, pc=pc: e.tensor_tensor(out=rw_Ke.t[:, pc, 0:w], in0=rw_KP.t[:, 0:w], in1=T3.t[:, 0:w], op=ALU.mult),
                              reads=[rw_KP.r, T3.r], writes=[rw_Ke.rs[pc]])
                        sink[0].op("pool", lambda e, pc=pc: e.tensor_tensor(out=rw_Be.t[:, pc, 0:w], in0=rw_Bf.t[:, 0:w], in1=T3.t[:, 0:w], op=ALU.mult),
                              reads=[rw_Bf.r, T3.r], writes=[rw_Be.rs[pc]])
                        sink[0].op("act", lambda e, pc=pc, Lv=Lv: e.activation(out=rw_gC.t[:, pc, 0:nch], in_=Lv[:, :, C - 1], func=AF.Exp,
                                                                       scale=-C0), reads=[rw_L.r], writes=[rw_gC.rs[pc]])
                        sink[0].op("act", lambda e, pc=pc, v_ap=v_ap: e.activation(out=rw_Vb.t[:, pc, 0:w], in_=v_ap, func=AF.Copy),
                              reads=[rw_Pb.r], writes=[rw_Vb.rs[pc]])
                        sink[0].op("dve", lambda e, pc=pc, r_ap=r_ap: e.scalar_tensor_tensor(
                            out=(T4 if pc == 0 else T5).t[:, 0:w], in0=r_ap, scalar=rwp["rw_r_k"].t[:, l, pc:pc + 1],
                            in1=rw_KP.t[:, 0:w], op0=ALU.mult, op1=ALU.mult),
                            reads=[rw_Pb.r, rwp["rw_r_k"].r, rw_KP.r], writes=[(T4 if pc == 0 else T5).r])
                    for pc in range(2):
                        sink[0] = pc_recs[pc]
                        tt = (T1, T2, T3) if pc == 0 else tuple(rw_tmpB)
                        _pc_body(pc, tt[0], tt[1], tt[2], rw_SIG2[pc], rw_A2[pc], rw_L2[pc], rw_KP2[pc], rw_KN2[pc],
                                 rw_Bf2[pc], rw_sqb2[pc])
                    sink[0] = outer_sink
                    merge_recs(outer_sink, pc_recs)
                    RWDBG = int(os.environ.get("RWDBG", "9"))
                    if RWDBG < 2:
                        return
                    for (src, dstT) in ((rw_Vb, rw_Vt), (rw_Ke, rw_KeT), (rw_Be, rw_BeT)):
                        for c0 in range(0, nch, 4):
                            pT = next_bank()
                            pTb = pT.t[:, :].bitcast(BF16)
                            ncc = min(4, nch - c0)
                            for ci in range(ncc):
                                c = c0 + ci
                                for pc in range(2):
                                    sink[0].op("pe", lambda e, src=src, c=c, ci=ci, pc=pc, pTb=pTb: e.transpose(
                                        pTb[0:C, ci * 256 + pc * 128:ci * 256 + (pc + 1) * 128], src.t[:, pc, c * C:(c + 1) * C],
                                        identb.t[:, :]), reads=[src.rs[pc], identb.r], writes=[pT.r],
                                        signal=(ci == ncc - 1 and pc == 1))
                            sink[0].op("act", lambda e, dstT=dstT, c0=c0, ncc=ncc, pTb=pTb: e.activation(
                                out=dstT.t[0:C, c0:c0 + ncc, :], in_=pTb[0:C, 0:ncc * 256].rearrange("p (c n) -> p c n", n=256),
                                func=AF.Copy), reads=[pT.r], writes=[dstT.r])
                    if RWDBG < 3:
                        return
                    class _V:
                        pass
                    py = [_V(), _V()]
                    for pc_ in range(2):
                        py[pc_].t = banks[5].t[:, pc_ * WA:(pc_ + 1) * WA]
                        py[pc_].r = banks[5].r
                    v4 = lambda ap: ap.rearrange("p (h a t) -> p h a t", h=4, a=2)[:, :, :, 0:C]
                    v3 = lambda ap: ap.rearrange("p (h t) -> p h t", h=4)[:, :, 0:C]
                    for c in range(nch):
                        p12, p34, p5 = next_bank(), next_bank(), next_bank()
                        AT12, AT34 = rw_AT12[c], rw_AT34[c]
                        for h in range(4):
                            par, pc = h % 2, h // 2
                            for a_ in range(2):
                                qr = rw_QRm.t[:, par, pc, c, a_, 0:C]
                                sink[0].op("pe", lambda e, c=c, h=h, pc=pc, qr=qr, p12=p12, a_=a_: e.matmul(
                                    p12.t[0:C, h * 128 + a_ * 64:h * 128 + a_ * 64 + C], lhsT=rw_Kh.t[:, pc, c * C:(c + 1) * C],
                                    rhs=qr, start=True, stop=True), reads=[rw_Kh.rs[pc], rw_QRm.rs[pc]], writes=[p12.r],
                                    signal=(h == 3 and a_ == 1))
                                sink[0].op("pe", lambda e, c=c, h=h, par=par, pc=pc, qr=qr, p34=p34, a_=a_: e.matmul(
                                    p34.t[0:C, h * 128 + a_ * 64:h * 128 + a_ * 64 + C],
                                    lhsT=rw_Bm.t[:, par, pc, c * C:(c + 1) * C], rhs=qr, start=True, stop=True),
                                    reads=[rw_Bm.rs[pc], rw_QRm.rs[pc]], writes=[p34.r], signal=(h == 3 and a_ == 1))
                            sink[0].op("pe", lambda e, h=h, par=par, pc=pc, c=c, p5=p5: e.matmul(
                                p5.t[0:C, h * 64:h * 64 + C], lhsT=rw_QRm.t[:, par, pc, c, 0, 0:C],
                                rhs=rw_Bm.t[:, par, pc, c * C:(c + 1) * C], start=True, stop=True),
                                reads=[rw_Bm.rs[pc], rw_QRm.rs[pc]], writes=[p5.r], signal=(h == 3))
                        sink[0].op("dve", lambda e, p12=p12, AT12=AT12: e.tensor_tensor(
                            out=AT12.t[0:C, :, :, 0:C], in0=v4(p12.t[0:C, :]),
                            in1=mask12.t[0:C, :, 0:C].unsqueeze(1).to_broadcast([C, 4, 2, C]), op=ALU.mult),
                            reads=[p12.r, mask12.r], writes=[AT12.r])
                        sink[0].op("dve", lambda e, p34=p34, AT34=AT34: e.tensor_tensor(
                            out=AT34.t[0:C, :, :, 0:C], in0=v4(p34.t[0:C, :]),
                            in1=mask34.t[0:C, :, 0:C].unsqueeze(1).to_broadcast([C, 4, 2, C]), op=ALU.mult),
                            reads=[p34.r, mask34.r], writes=[AT34.r])
                        X, XT, TT = rw_X[c][0], rw_XT[c][0], rw_TT[c][0]
                        sink[0].op("dve", lambda e, p5=p5, X=X: e.tensor_tensor(
                            out=X.t[0:C, :, 0:C], in0=v3(p5.t[0:C, 0:256]),
                            in1=mask5.t[0:C, 0:C].unsqueeze(1).to_broadcast([C, 4, C]), op=ALU.mult),
                            reads=[p5.r, mask5.r], writes=[X.r])
                        sink[0].op("act", lambda e, XT=XT, AT34=AT34: e.activation(out=XT.t[0:C, :, 0:C], in_=AT34.t[0:C, :, 0, 0:C], func=AF.Copy),
                              reads=[AT34.r], writes=[XT.r])
                        sink[0].op("pool", lambda e, TT=TT, AT34=AT34: e.tensor_tensor(
                            out=TT.t[0:C, :, 0:C], in0=AT34.t[0:C, :, 0, 0:C],
                            in1=identb.t[0:C, 0:C].unsqueeze(1).to_broadcast([C, 4, C]), op=ALU.add),
                            reads=[AT34.r, identb.r], writes=[TT.r])
                    cur = 0
                    for lev in range(nlev):
                        for c in range(nch):
                            Xn, XTn, TTn = rw_X[c][1 - cur], rw_XT[c][1 - cur], rw_TT[c][1 - cur]
                            X, XT, TT = rw_X[c][cur], rw_XT[c][cur], rw_TT[c][cur]
                            px, pxt, ptt = next_bank(), next_bank(), next_bank()
                            for h in range(4):
                                sink[0].op("pe", lambda e, h=h, px=px, X=X, XT=XT: e.matmul(
                                    px.t[0:C, h * 64:h * 64 + C], lhsT=XT.t[0:C, h, 0:C], rhs=X.t[0:C, h, 0:C], start=True, stop=True),
                                    reads=[X.r, XT.r], writes=[px.r], signal=(h == 3))
                            sink[0].op("act", lambda e, px=px, Xn=Xn: e.activation(
                                out=Xn.t[0:C, :, 0:C], in_=v3(px.t[0:C, 0:256]), func=AF.Copy), reads=[px.r], writes=[Xn.r])
                            if lev < nlev - 1:
                                for h in range(4):
                                    sink[0].op("pe", lambda e, h=h, pxt=pxt, X=X, XT=XT: e.matmul(
                                        pxt.t[0:C, h * 64:h * 64 + C], lhsT=X.t[0:C, h, 0:C], rhs=XT.t[0:C, h, 0:C], start=True, stop=True),
                                        reads=[X.r, XT.r], writes=[pxt.r], signal=(h == 3))
                                sink[0].op("act" if c % 2 else "dve", (lambda e, pxt=pxt, XTn=XTn: e.activation(
                                    out=XTn.t[0:C, :, 0:C], in_=v3(pxt.t[0:C, 0:256]), func=AF.Copy)) if c % 2 else
                                    (lambda e, pxt=pxt, XTn=XTn: e.tensor_copy(out=XTn.t[0:C, :, 0:C], in_=v3(pxt.t[0:C, 0:256]))),
                                    reads=[pxt.r], writes=[XTn.r])
                            for h in range(4):
                                sink[0].op("pe", lambda e, h=h, ptt=ptt, Xn=Xn, TT=TT: e.matmul(
                                    ptt.t[0:C, h * 64:h * 64 + C], lhsT=Xn.t[0:C, h, 0:C], rhs=TT.t[0:C, h, 0:C], start=True, stop=True),
                                    reads=[Xn.r, TT.r], writes=[ptt.r], signal=(h == 3))
                            sink[0].op("dve", lambda e, ptt=ptt, TT=TT, TTn=TTn: e.tensor_tensor(
                                out=TTn.t[0:C, :, 0:C], in0=v3(ptt.t[0:C, 0:256]), in1=TT.t[0:C, :, 0:C], op=ALU.add),
                                reads=[ptt.r, TT.r], writes=[TTn.r])
                        cur = 1 - cur
                    if RWDBG < 4:
                        return
                    for c in range(nch):
                        TTf = rw_TT[c][cur]
                        AT12, AT34 = rw_AT12[c], rw_AT34[c]
                        pz, pu, ph = next_bank(), next_bank(), next_bank()
                        for pc in range(2):
                            for par in range(2):
                                sink[0].op("pe", lambda e, c=c, AT12=AT12, AT34=AT34, pc=pc, par=par, pz=pz: e.matmul(
                                    pz.t[0:C, pc * 128:(pc + 1) * 128], lhsT=rw_QRm.t[:, par, pc, c, 0, 0:C], rhs=rw_Hbd.t[:, pc, :],
                                    start=(par == 0), stop=False, skip_group_check=True),
                                    reads=[rw_QRm.rs[pc], rw_Hbd.rs[pc]], writes=[pz.r], signal=False)
                            for h in (2 * pc, 2 * pc + 1):
                                sink[0].op("pe", lambda e, c=c, AT12=AT12, AT34=AT34, h=h, pz=pz: e.matmul(
                                    pz.t[0:C, h * 64:(h + 1) * 64], lhsT=AT12.t[0:C, h, 0, 0:C], rhs=rw_Vt.t[0:C, c, h * 64:(h + 1) * 64],
                                    start=False, stop=True, skip_group_check=True),
                                    reads=[AT12.r, rw_Vt.r], writes=[pz.r], signal=(h == 3))
                        sink[0].op("act", lambda e, c=c, AT12=AT12, AT34=AT34, pz=pz: e.activation(out=rw_Zb.t[0:C, :], in_=pz.t[0:C, 0:256], func=AF.Copy),
                              reads=[pz.r], writes=[rw_Zb.r])
                        for h in range(4):
                            sink[0].op("pe", lambda e, c=c, AT12=AT12, AT34=AT34, h=h, pu=pu, TTf=TTf: e.matmul(
                                pu.t[0:C, h * 64:(h + 1) * 64], lhsT=TTf.t[0:C, h, 0:C], rhs=rw_Zb.t[0:C, h * 64:(h + 1) * 64],
                                start=True, stop=True), reads=[TTf.r, rw_Zb.r], writes=[pu.r], signal=(h == 3))
                        sink[0].op("dve", lambda e, c=c, AT12=AT12, AT34=AT34, pu=pu: e.tensor_scalar_mul(out=rw_Un.t[0:C, :], in0=pu.t[0:C, 0:256], scalar1=-1.0),
                              reads=[pu.r], writes=[rw_Un.r])
                        for pc in range(2):
                          for par in range(2):
                                sink[0].op("pe", lambda e, c=c, AT12=AT12, AT34=AT34, pc=pc, par=par: e.matmul(
                                    py[pc].t[:, c * C:(c + 1) * C], lhsT=rw_Hbd.t[:, pc, :], rhs=rw_QRm.t[:, par, pc, c, 1, 0:C],
                                    start=(par == 0), stop=False, skip_group_check=True),
                                    reads=[rw_QRm.rs[pc], rw_Hbd.rs[pc]], writes=[py[pc].r], signal=False)
                          for h in (2 * pc, 2 * pc + 1):
                            hs = slice((h % 2) * 64, (h % 2) * 64 + 64)
                            sink[0].op("pe", lambda e, c=c, AT12=AT12, AT34=AT34, h=h, pc=pc, hs=hs: e.matmul(
                                py[pc].t[hs, c * C:(c + 1) * C], lhsT=rw_Vt.t[0:C, c, h * 64:(h + 1) * 64], rhs=AT12.t[0:C, h, 1, 0:C],
                                start=False, stop=False, skip_group_check=True),
                                reads=[rw_Vt.r, AT12.r], writes=[py[pc].r], signal=False)
                            sink[0].op("pe", lambda e, c=c, AT12=AT12, AT34=AT34, h=h, pc=pc, hs=hs: e.matmul(
                                py[pc].t[hs, c * C:(c + 1) * C], lhsT=rw_Un.t[0:C, h * 64:(h + 1) * 64], rhs=AT34.t[0:C, h, 1, 0:C],
                                start=False, stop=True, skip_group_check=True),
                                reads=[rw_Un.r, AT34.r], writes=[py[pc].r], signal=(h % 2 == 1))
                        for pc in range(2):
                            cs = slice(pc * 128, (pc + 1) * 128)
                            sink[0].op("pe", lambda e, c=c, AT12=AT12, AT34=AT34, pc=pc, cs=cs, ph=ph: e.matmul(
                                ph.t[:, cs], lhsT=rw_KeT.t[0:C, c, cs], rhs=rw_Vt.t[0:C, c, cs], start=True, stop=False),
                                reads=[rw_KeT.r, rw_Vt.r], writes=[ph.r], signal=False)
                            sink[0].op("pe", lambda e, c=c, AT12=AT12, AT34=AT34, pc=pc, cs=cs, ph=ph: e.matmul(
                                ph.t[:, cs], lhsT=rw_BeT.t[0:C, c, cs], rhs=rw_Un.t[0:C, cs], start=False, stop=True),
                                reads=[rw_BeT.r, rw_Un.r], writes=[ph.r])
                            for hh in range(2):
                                hr = slice(hh * 64, hh * 64 + 64)
                                sink[0].op("dve", lambda e, c=c, AT12=AT12, AT34=AT34, pc=pc, hr=hr, hh=hh, ph=ph: e.scalar_tensor_tensor(
                                    out=rw_Hm.t[hr, pc, hr], in0=rw_Hm.t[hr, pc, hr], scalar=rw_gC.t[hr, pc, c:c + 1],
                                    in1=ph.t[hr, pc * 128 + hh * 64:pc * 128 + hh * 64 + 64], op0=ALU.mult, op1=ALU.add),
                                    reads=[rw_Hm.rs[pc], rw_gC.rs[pc], ph.r], writes=[rw_Hm.rs[pc]])
                            sink[0].op("act", lambda e, c=c, AT12=AT12, AT34=AT34, pc=pc: e.activation(out=rw_Hbd.t[:, pc, :], in_=rw_Hm.t[:, pc, :], func=AF.Copy),
                                  reads=[rw_Hm.rs[pc]], writes=[rw_Hbd.rs[pc]])
                    if RWDBG < 5:
                        return
                    for pc in range(2):
                        v_ap = XS(4 + pc)
                        TB = T4 if pc == 0 else T5
                        sink[0].op("act", lambda e, pc=pc: e.activation(out=rw_Yf.t[:, 0:w], in_=py[pc].t[:, 0:w], func=AF.Copy),
                              reads=[py[pc].r], writes=[rw_Yf.r])
                        sink[0].op("act", lambda e: e.activation(out=rw_sqb.t[:, 0:w], in_=rw_Yf.t[:, 0:w], func=AF.Copy),
                              reads=[rw_Yf.r], writes=[rw_sqb.r])
                        pm = next_bank()
                        sink[0].op("pe", lambda e, pm=pm: e.matmul(pm.t[:, 0:w], lhsT=onesbd.t[:], rhs=rw_sqb.t[:, 0:w], start=True, stop=True),
                              reads=[onesbd.r, rw_sqb.r], writes=[pm.r])
                        sink[0].op("dve", lambda e, pm=pm: e.scalar_tensor_tensor(
                            out=rw_Yf.t[:, 0:w], in0=pm.t[:, 0:w], scalar=-1.0 / 64, in1=rw_Yf.t[:, 0:w], op0=ALU.mult, op1=ALU.add),
                            reads=[pm.r, rw_Yf.r], writes=[rw_Yf.r])
                        sink[0].op("act", lambda e: e.activation(out=rw_sqb.t[:, 0:w], in_=rw_Yf.t[:, 0:w], func=AF.Square),
                              reads=[rw_Yf.r], writes=[rw_sqb.r])
                        pv = next_bank()
                        sink[0].op("pe", lambda e, pv=pv: e.matmul(pv.t[:, 0:w], lhsT=onesbd.t[:], rhs=rw_sqb.t[:, 0:w], start=True, stop=True),
                              reads=[onesbd.r, rw_sqb.r], writes=[pv.r])
                        sink[0].op("act", lambda e, pv=pv: e.activation(out=T0.t[:, 0:w], in_=pv.t[:, 0:w], func=AF.Ln, scale=1.0 / 64,
                                                                   bias=eps2.t[:]), reads=[pv.r, eps2.r], writes=[T0.r])
                        sink[0].op("act", lambda e: e.activation(out=T0.t[:, 0:w], in_=T0.t[:, 0:w], func=AF.Exp, scale=-0.5),
                              reads=[T0.r], writes=[T0.r])
                        sink[0].op("dve", lambda e: e.tensor_tensor(out=rw_Yf.t[:, 0:w], in0=rw_Yf.t[:, 0:w], in1=T0.t[:, 0:w], op=ALU.mult),
                              reads=[rw_Yf.r, T0.r], writes=[rw_Yf.r])
                        sink[0].op("dve", lambda e, pc=pc: e.tensor_scalar(
                            out=rw_Yf.t[:, 0:w], in0=rw_Yf.t[:, 0:w], scalar1=rwp["rw_ln_w"].t[:, l, pc:pc + 1],
                            scalar2=rwp["rw_ln_b"].t[:, l, pc:pc + 1], op0=ALU.mult, op1=ALU.add),
                            reads=[rw_Yf.r, rwp["rw_ln_w"].r, rwp["rw_ln_b"].r], writes=[rw_Yf.r])
                        sink[0].op("act", lambda e, TB=TB: e.activation(out=rw_sqb.t[:, 0:w], in_=TB.t[:, 0:w], func=AF.Copy),
                              reads=[TB.r], writes=[rw_sqb.r])
                        pbn = next_bank()
                        sink[0].op("pe", lambda e, pbn=pbn: e.matmul(pbn.t[:, 0:w], lhsT=onesbd.t[:], rhs=rw_sqb.t[:, 0:w], start=True, stop=True),
                              reads=[onesbd.r, rw_sqb.r], writes=[pbn.r])
                        sink[0].op("dve", lambda e, pbn=pbn, v_ap=v_ap: e.tensor_tensor(out=T0.t[:, 0:w], in0=pbn.t[:, 0:w], in1=v_ap, op=ALU.mult),
                              reads=[pbn.r, rw_Pb.r], writes=[T0.r])
                        sink[0].op("dve", lambda e: e.tensor_tensor(out=rw_Yf.t[:, 0:w], in0=rw_Yf.t[:, 0:w], in1=T0.t[:, 0:w], op=ALU.add),
                              reads=[rw_Yf.r, T0.r], writes=[rw_Yf.r])
                        sink[0].op("dve", lambda e, pc=pc: e.tensor_tensor(out=ymix.t[:, pc, 0:w], in0=rw_Yf.t[:, 0:w], in1=rw_G[pc].t[:, 0:w],
                                                                      op=ALU.mult), reads=[rw_Yf.r, rw_G[pc].r], writes=[ymix.rs[pc]])

                RWKV_TILE = rw_tile

                it = 0
                for s in range(3):
                    hg_init(s)
                    if stage >= 4:
                        rw_init(s)

                    def a2_tile(s, t0, w, j, xT, l=l):
                        load_xT(xT, t0, w)
                        fw.dma("pool", ymix.t[:, 2:6, 0:w], yfox.rearrange("(c p) t -> p c t", p=128)[:, :, t0:t0 + w],
                               reads=[yfreg(t0)], writes=ymix.rs[2:6], stream="yfld")
                        norm_fm(xT, w, sq, tmp, lnv, rstd, hT,
                                lambda c, s=s: G1.t[:, l, s, c:c + 1], lambda c, s=s: mod.t[:, l, 0, s, c:c + 1], hT.rs)
                        recs = []
                        if stage >= 3:
                            sink[0] = Rec()
                            recs.append(sink[0])
                            default_pool[0] = (0, 1)
                            hg_tile(s, w)
                        if stage >= 4 and RWKV_TILE is not None:
                            sink[0] = Rec()
                            recs.append(sink[0])
                            default_pool[0] = (2, 3, 4)
                            RWKV_TILE(s, w)
                        sink[0] = fw
                        default_pool[0] = (0, 1, 2, 3, 4)
                        merge_recs(fw, recs)
                        for c in range(8):
                            po_ = next_bank()
                            for kc in range(8):
                                fw.op("pe", lambda e, c=c, kc=kc, po_=po_: e.matmul(
                                    po_.t[:, 0:w], lhsT=wo.t[:, kc, c * 128:(c + 1) * 128], rhs=ymix.t[:, kc, 0:w],
                                    start=(kc == 0), stop=(kc == 7)),
                                    reads=[wo.r, ymix.rs[kc]], writes=[po_.r], signal=(kc == 7))
                            fw.op("dve", lambda e, c=c, po_=po_: e.scalar_tensor_tensor(
                                out=xT.t[:, c, 0:w], in0=po_.t[:, 0:w], scalar=mod.t[:, l, 2, s, c:c + 1],
                                in1=xT.t[:, c, 0:w], op0=ALU.mult, op1=ALU.add),
                                reads=[po_.r, mod.r, xT.rs[c]], writes=[xT.rs[c]])
                        store_xT(xT, t0, w)
                    for (t0, w, j) in tiles_of(s, WA):
                        xT = xTa[0]
                        it += 1
                        a2_tile(s, t0, w, j, xT)
                    if stage >= 3:
                        hg_final(s)
                    if stage >= 4:
                        rw_final(s, w)
                fw.flush()
                default_pool[0] = (0, 1, 2, 3, 4, 5, 6, 7)
            with contextlib.ExitStack() as ph:
                sub = FWScope(fw, ph)
                WB = 256
                wfi = Tt(sub.sbuf("wfi", [128, 8, 2 * DFF], BF16), name="wfi")
                wfo = Tt(sub.sbuf("wfo", [128, 22, D], BF16), name="wfo")
                for kc in range(8):
                    fw.dma("pool", wfi.t[:, kc, :], I["w_ffn_in"][l][kc * 128:(kc + 1) * 128, :], writes=[wfi.r],
                           stream="wld", group=True)
                for kc in range(22):
                    fw.dma("pool", wfo.t[:, kc, :], I["w_ffn_out"][l][kc * 128:(kc + 1) * 128, :], writes=[wfo.r],
                           stream="wld", group=True)
                xTb = [Tt(sub.sbuf("xTb%d" % i, [128, 8, WB], F32), nreg=8, name="xTb%d" % i) for i in range(2)]
                sq = Tt(sub.sbuf("sqb", [128, 8, WB], BF16), name="sqb")
                hT = Tt(sub.sbuf("hTb", [128, 8, WB], BF16), nreg=8, name="hTb")
                tmp = [Tt(sub.sbuf("tmpb%d" % i, [128, WB], F32), name="tmpb%d" % i) for i in range(2)]
                lnv = Tt(sub.sbuf("lnvb", [128, WB], F32), name="lnvb")
                rstd = Tt(sub.sbuf("rstdb", [128, WB], F32), name="rstdb")
                actT = Tt(sub.sbuf("actT", [128, 22, WB], BF16), nreg=22, name="actT")
                sg = [Tt(sub.sbuf("sg%d" % i, [128, WB], F32), name="sg%d" % i) for i in range(2)]
                tiles_b = [(s_, t0_, w_) for s_ in range(3) for (t0_, w_, j_) in tiles_of(s_, WB)]

                def b_norm(idx):
                    s_, t0_, w_ = tiles_b[idx]
                    xT_ = xTb[idx % 2]
                    load_xT(xT_, t0_, w_, q="sp")
                    norm_fm(xT_, w_, sq, tmp, lnv, rstd, hT,
                            lambda c, s_=s_: G2.t[:, l, s_, c:c + 1], lambda c, s_=s_: mod.t[:, l, 3, s_, c:c + 1], hT.rs)
                b_norm(0)
                for idx in range(len(tiles_b)):
                    if True:
                        s, t0, w = tiles_b[idx]
                        xT = xTb[idx % 2]
                        for f in range(22):
                            pg = next_bank()
                            pu = next_bank()
                            for kc in range(8):
                                fw.op("pe", lambda e, kc=kc, f=f, pg=pg, w=w: e.matmul(
                                    pg.t[:, 0:w], lhsT=wfi.t[:, kc, f * 128:(f + 1) * 128], rhs=hT.t[:, kc, 0:w],
                                    start=(kc == 0), stop=(kc == 7)),
                                    reads=[wfi.r, hT.rs[kc]], writes=[pg.r], signal=(kc == 7))
                            for kc in range(8):
                                fw.op("pe", lambda e, kc=kc, f=f, pu=pu, w=w: e.matmul(
                                    pu.t[:, 0:w], lhsT=wfi.t[:, kc, DFF + f * 128:DFF + (f + 1) * 128],
                                    rhs=hT.t[:, kc, 0:w], start=(kc == 0), stop=(kc == 7)),
                                    reads=[wfi.r, hT.rs[kc]], writes=[pu.r], signal=(kc == 7))
                            sgt = sg[f % 2]
                            fw.op("act", lambda e, pg=pg, sgt=sgt, w=w: e.activation(
                                out=sgt.t[:, 0:w], in_=pg.t[:, 0:w], func=AF.Silu), reads=[pg.r], writes=[sgt.r])
                            fw.op("dve", lambda e, pu=pu, sgt=sgt, f=f, w=w: e.tensor_tensor(
                                out=actT.t[:, f, 0:w], in0=sgt.t[:, 0:w], in1=pu.t[:, 0:w], op=ALU.mult),
                                reads=[sgt.r, pu.r], writes=[actT.rs[f]])
                        if idx + 1 < len(tiles_b):
                            b_norm(idx + 1)
                        for c in range(8):
                            po = next_bank()
                            for f in range(22):
                                fw.op("pe", lambda e, c=c, f=f, po=po, w=w: e.matmul(
                                    po.t[:, 0:w], lhsT=wfo.t[:, f, c * 128:(c + 1) * 128], rhs=actT.t[:, f, 0:w],
                                    start=(f == 0), stop=(f == 21)),
                                    reads=[wfo.r, actT.rs[f]], writes=[po.r], signal=(f == 21))
                            fw.op("dve", lambda e, c=c, po=po, xT=xT, s=s, w=w: e.scalar_tensor_tensor(
                                out=xT.t[:, c, 0:w], in0=po.t[:, 0:w], scalar=mod.t[:, l, 5, s, c:c + 1],
                                in1=xT.t[:, c, 0:w], op0=ALU.mult, op1=ALU.add),
                                reads=[po.r, mod.r, xT.rs[c]], writes=[xT.rs[c]])
                        store_xT(xT, t0, w, q="sp")
                fw.flush()

        with contextlib.ExitStack() as ph:
            sub = FWScope(fw, ph)
            WE = 512
            xTe = [Tt(sub.sbuf("xTe%d" % i, [128, 8, WE], F32), nreg=8, name="xTe%d" % i) for i in range(2)]
            sq = Tt(sub.sbuf("sqe", [128, 8, WE], BF16), name="sqe")
            yT = Tt(sub.sbuf("yTe", [128, 8, WE], F32), nreg=8, name="yTe")
            tmp = [Tt(sub.sbuf("tmpe%d" % i, [128, WE], F32), name="tmpe%d" % i) for i in range(2)]
            lnv = Tt(sub.sbuf("lnve", [128, WE], F32), name="lnve")
            rstd = Tt(sub.sbuf("rstde", [128, WE], F32), name="rstde")
            ytok = [Tt(sub.sbuf("ytok%d" % i, [128, D], F32), name="ytok%d" % i) for i in range(2)]
            it = 0
            ik = 0
            for s in range(3):
                dst = O["yp"][s] if s < 2 else O["ys"]
                off, T = seqs[s]
                for (t0, w, j) in tiles_of(s, WE):
                    xT = xTe[it % 2]
                    it += 1
                    load_xT(xT, t0, w)
                    norm_fm(xT, w, sq, tmp, lnv, rstd, yT, lambda c: fng.t[:, c:c + 1], None, yT.rs)
                    nb = (w + 127) // 128
                    pw = min(w, 128)
                    for tb in range(nb):
                        yt = ytok[ik % 2]
                        ik += 1
                        for half in range(2):
                            pb = next_bank()
                            for cc in range(4):
                                c = half * 4 + cc
                                fw.op("pe", lambda e, c=c, cc=cc, tb=tb, pb=pb, pw=pw: e.transpose(
                                    pb.t[0:pw, cc * 128:(cc + 1) * 128], yT.t[:, c, tb * 128:tb * 128 + pw],
                                    ident.t[:, :]),
                                    reads=[yT.rs[c], ident.r], writes=[pb.r], signal=(cc == 3))
                            if half == 0:
                                fw.op("act", lambda e, pb=pb, yt=yt, pw=pw: e.activation(
                                    out=yt.t[0:pw, 0:512], in_=pb.t[0:pw, :], func=AF.Copy), reads=[pb.r], writes=[yt.r])
                            else:
                                fw.op("dve", lambda e, pb=pb, yt=yt, pw=pw: e.tensor_copy(
                                    out=yt.t[0:pw, 512:1024], in_=pb.t[0:pw, :]), reads=[pb.r], writes=[yt.r])
                        lt0 = t0 - off + tb * 128
                        fw.dma("sp" if ik % 2 else "pool", dst[lt0:lt0 + pw, :], yt.t[0:pw, :], reads=[yt.r],
                               stream="yout")
            fw.flush()
        fw.finish()
    return nc


class FWScope:
    ctr = 0

    def __init__(self, fw, stack):
        self.fw = fw
        self.stack = stack

    def sbuf(self, name, shape, dt):
        FWScope.ctr += 1
        return self.stack.enter_context(self.fw.nc.sbuf_tensor("%s_u%d" % (name, FWScope.ctr), list(shape), dt))


def layer_mixer(fw, nc, I, O, l, L, SEQ, TS, PAST, env):
    pass


_PROG_CACHE = {}


def _get_prog(SEQ, DEPTH, TS, PAST, stage=9):
    key = (SEQ, DEPTH, TS, PAST, stage)
    if key not in _PROG_CACHE:
        _PROG_CACHE[key] = build_program(SEQ, DEPTH, TS, PAST, stage)
    return _PROG_CACHE[key]


def make_in_maps(inp, ncores, L):
    f = lambda a: np.ascontiguousarray(np.asarray(a, dtype=np.float32))
    maps = []
    shared = {k: f(inp[k]) for k in ("norm1_g", "w_ada", "b_ada", "w_in", "rw_mu", "rw_w0", "rw_w2", "rw_a0",
                                     "rw_a2", "rw_g2", "rw_k_k", "rw_k_a", "rw_ln_w", "rw_ln_b", "fox_b_f",
                                     "hg_lb_logits", "hg_norm_g", "w_out", "norm2_g", "w_ffn_in", "w_ffn_out",
                                     "final_norm_g")}
    shared["rw_r_k"] = f(inp["rw_r_k"]).reshape(L, 256)
    xp, xs = f(inp["x_prompt"]), f(inp["x_sample"])
    cp, cs = f(inp["c_prompt"]), f(inp["c_sample"])
    ck, cv, cl = f(inp["cache_fox_k"]), f(inp["cache_fox_v"]), f(inp["cache_fox_logf"])
    srw, ssh, shg = f(inp["state_rwkv"]), f(inp["state_rwkv_shift"]), f(inp["state_hgrn"])
    P = ck.shape[2]
    for i in range(ncores):
        m = dict(shared)
        m["xp"] = f(xp[2 * i:2 * i + 2])
        m["xs"] = f(xs[i])
        m["cc"] = f(np.concatenate([cp[2 * i:2 * i + 2], cs[i:i + 1]], axis=0))
        m["ck"] = f(ck[:, i].reshape(L, P, 512))
        m["cv"] = f(cv[:, i].reshape(L, P, 512))
        m["cl"] = f(cl[:, i])
        m["srw"] = f(srw[:, i])
        m["ssh"] = f(ssh[:, i, 0])
        m["shg"] = f(shg[:, i])
        maps.append(m)
    return maps


def gather_outputs(res, ncores, L, SEQ, TS):
    r = res
    cat = lambda k, ax: np.concatenate([r[i][k] for i in range(ncores)], axis=ax)
    stack = lambda k, ax: np.stack([r[i][k] for i in range(ncores)], axis=ax)
    yp = cat("yp", 0)
    ys = stack("ys", 0)
    fkp = cat("fkp", 1).reshape(L, 2 * ncores, SEQ, 8, 64)
    fvp = cat("fvp", 1).reshape(L, 2 * ncores, SEQ, 8, 64)
    flp = cat("flp", 1)
    rwp = cat("rwp", 1)
    rshp = cat("rshp", 1).reshape(L, 2 * ncores, 1, RW_COLS)
    hgp = cat("hgp", 1)
    fks = stack("fks", 1).reshape(L, ncores, TS, 8, 64)
    fvs = stack("fvs", 1).reshape(L, ncores, TS, 8, 64)
    fls = stack("fls", 1)
    rws = stack("rws", 1)
    rshs = stack("rshs", 1).reshape(L, ncores, 1, RW_COLS)
    hgs = stack("hgs", 1)
    return (yp, ys, fkp, fvp, flp, rwp, rshp, hgp, fks, fvs, fls, rws, rshs, hgs)


def kernel(**inputs):
    L = int(np.asarray(inputs["w_in"]).shape[0])
    SEQ = int(np.asarray(inputs["x_prompt"]).shape[1])
    TS = int(np.asarray(inputs["x_sample"]).shape[1])
    PAST = int(np.asarray(inputs["cache_fox_k"]).shape[2])
    ncores = int(np.asarray(inputs["x_sample"]).shape[0])
    nc = _get_prog(SEQ, L, TS, PAST)
    maps = make_in_maps(inputs, ncores, L)
    res = run_bass_kernel_spmd(nc, maps, core_ids=list(range(ncores)))
    return gather_outputs(res.results, ncores, L, SEQ, TS)
```

```python
import contextlib
import os
import numpy as np
import concourse.bass as bass
import concourse.mybir as mybir
from concourse.bass_utils import run_bass_kernel_spmd

F32 = mybir.dt.float32
BF16 = mybir.dt.bfloat16
AF = mybir.ActivationFunctionType
ALU = mybir.AluOpType

D = 1024
NC8 = 8
HD = 64
RW_COLS, FOX_COLS, HG_COLS = 896, 1544, 1024
IN_COLS = 3464
DFF = 2816
EPS = 1e-6


class Reg:
    __slots__ = ("name", "writers", "readers")

    def __init__(self, name=""):
        self.name = name
        self.writers = {}
        self.readers = {}


class Eng:
    def __init__(self, name, kind):
        self.name = name
        self.kind = kind
        self.ops = []
        self.sem = None
        self.count = 0
        self.waited = {}


class FW:
    def __init__(self, nc, stack):
        self.nc = nc
        self.stack = stack
        self.engs = {}
        self.dma_sems = {}
        self.dma_counts = {}
        self.group_sems = {}
        self.nsem = 0
        for name in ("pe", "act", "dve", "pool", "sp"):
            e = Eng(name, name)
            self.engs[name] = e
            if name != "sp":
                e.sem = self.new_sem("s_" + name)

    def new_sem(self, name):
        self.nsem += 1
        return self.stack.enter_context(self.nc.semaphore("%s_%d" % (name, self.nsem)))

    def sbuf(self, name, shape, dt):
        return self.stack.enter_context(self.nc.sbuf_tensor(name, list(shape), dt))

    def psum(self, name, shape, dt=F32):
        return self.stack.enter_context(self.nc.psum_tensor(name, list(shape), dt))

    def _collect(self, reads, writes):
        deps = {}

        def add(d):
            for k, (sem, val) in d.items():
                cur = deps.get(k)
                if cur is None or cur[1] < val:
                    deps[k] = (sem, val)
        for r in reads:
            add(r.writers)
        for w in writes:
            add(w.writers)
            add(w.readers)
        return deps

    def _waits(self, eng, deps, raw_keys):
        waits = []
        for k, (sem, val) in deps.items():
            if eng.sem is not None and k == id(eng.sem) and k not in raw_keys:
                continue
            if eng.waited.get(k, 0) >= val:
                continue
            eng.waited[k] = val
            st = self.group_sems.get(k)
            if st is not None:
                waits.append((sem, _Lazy(self.dma_counts, st)))
            else:
                waits.append((sem, val))
        return waits

    def op(self, engname, fn, reads=(), writes=(), signal=True):
        eng = self.engs[engname]
        reads = [r for r in reads if r is not None]
        writes = [w for w in writes if w is not None]
        deps = self._collect(reads, writes)
        raw_keys = set()
        k = id(eng.sem)
        if engname != "pe":
            raw_keys.add(k)
        for r in reads:
            if k in r.writers:
                raw_keys.add(k)
        waits = self._waits(eng, deps, raw_keys)
        sem = eng.sem
        if signal:
            eng.count += 1
            tok = (sem, eng.count)
        else:
            tok = (sem, eng.count + 1)
        for r in reads:
            r.readers[id(sem)] = tok
        for w in writes:
            w.writers = {id(sem): tok}
            w.readers = {}

        def run(e, fn=fn, waits=waits, signal=signal, sem=sem):
            for (s, v) in waits:
                e.wait_ge(s, int(v))
            ins = fn(e)
            if signal:
                ins.then_inc(sem, 1)
        eng.ops.append(run)

    def dma(self, qname, out, in_, reads=(), writes=(), stream="d", group=False, **kw):
        eng = self.engs[qname]
        reads = [r for r in reads if r is not None]
        writes = [w for w in writes if w is not None]
        deps = self._collect(reads, writes)
        stream = stream + "_" + qname
        if stream not in self.dma_sems:
            self.dma_sems[stream] = self.new_sem("dq_" + stream)
            self.dma_counts[stream] = 0
            if group:
                self.group_sems[id(self.dma_sems[stream])] = stream
        if group:
            deps.pop(id(self.dma_sems[stream]), None)
        waits = self._waits(eng, deps, set(deps.keys()))
        sem = self.dma_sems[stream]
        self.dma_counts[stream] += 16
        tok = (sem, self.dma_counts[stream])
        for r in reads:
            r.readers[id(sem)] = tok
        for w in writes:
            w.writers = {id(sem): tok}
            w.readers = {}

        def run(e, waits=waits, sem=sem, out=out, in_=in_, kw=kw):
            for (s, v) in waits:
                e.wait_ge(s, int(v))
            e.dma_start(out=out, in_=in_, **kw).then_inc(sem, 16)
        eng.ops.append(run)

    def rotate(self):
        for e in self.engs.values():
            if e.sem is not None:
                e.sem = self.new_sem("s_" + e.name)
                e.count = 0

    def barrier(self):
        toks = []
        for e in self.engs.values():
            if e.sem is not None and e.count > 0:
                toks.append((e.sem, e.count))
        for s in self.dma_sems:
            if self.dma_counts[s] > 0:
                toks.append((self.dma_sems[s], self.dma_counts[s]))
        for e in self.engs.values():
            waits = []
            for (sem, val) in toks:
                if sem is e.sem:
                    continue
                if e.waited.get(id(sem), 0) >= val:
                    continue
                e.waited[id(sem)] = val
                waits.append((sem, val))

            def run(h, waits=waits):
                for (s, v) in waits:
                    h.wait_ge(s, v)
            e.ops.append(run)

    def finish(self):
        self.flush()

    def flush(self):
        self.barrier()
        nc = self.nc
        engs = self.engs
        oplists = {k: e.ops for k, e in engs.items()}
        for e in engs.values():
            e.ops = []

        class _E:
            def __init__(self, ops):
                self.ops = ops
        self_engs = {k: _E(v) for k, v in oplists.items()}
        with nc.Block() as block:
            def mk(eng):
                def body(e):
                    for f in eng.ops:
                        f(e)
                return body
            block.tensor(mk(self_engs["pe"]))
            block.scalar(mk(self_engs["act"]))
            block.vector(mk(self_engs["dve"]))
            block.gpsimd(mk(self_engs["pool"]))
            block.sync(mk(self_engs["sp"]))


class _Lazy:
    def __init__(self, counts, stream):
        self.counts = counts
        self.stream = stream

    def __int__(self):
        return self.counts[self.stream]


class Rec:
    def __init__(self):
        self.items = []

    def op(self, *a, **k):
        self.items.append(("op", a, k))

    def dma(self, *a, **k):
        self.items.append(("dma", a, k))

    def replay_into(self, sink):
        for (kind, a, k) in self.items:
            getattr(sink, kind)(*a, **k)


def merge_recs(sink, recs):
    pos = [0] * len(recs)
    n = [len(r.items) for r in recs]
    total = sum(n)
    for _ in range(total):
        best, bf_ = -1, 2.0
        for i in range(len(recs)):
            if pos[i] < n[i]:
                f = pos[i] / n[i]
                if f < bf_:
                    best, bf_ = i, f
        kind, a, k = recs[best].items[pos[best]]
        pos[best] += 1
        getattr(sink, kind)(*a, **k)


class Tt:
    def __init__(self, t, nreg=1, name=""):
        self.t = t
        self.rs = [Reg("%s%d" % (name, i)) for i in range(nreg)]
        self.r = self.rs[0]


def build_program(SEQ, DEPTH, TS=32, PAST=2048, stage=9):
    L = DEPTH
    NTOK = 2 * SEQ + TS
    nc = bass.Bass("TRN2", target_bir_lowering=False)
    din = lambda n, s: nc.dram_tensor(n, list(s), F32, kind="ExternalInput").ap()
    dout = lambda n, s: nc.dram_tensor(n, list(s), F32, kind="ExternalOutput").ap()
    I = dict(
        xp=din("xp", (2, SEQ, D)), xs=din("xs", (TS, D)), cc=din("cc", (3, D)),
        ck=din("ck", (L, PAST, 512)), cv=din("cv", (L, PAST, 512)), cl=din("cl", (L, PAST, 8)),
        srw=din("srw", (L, 4, 64, 64)), ssh=din("ssh", (L, RW_COLS)), shg=din("shg", (L, 4, 64, 64)),
        norm1_g=din("norm1_g", (L, D)), w_ada=din("w_ada", (L, D, 6 * D)), b_ada=din("b_ada", (L, 6 * D)),
        w_in=din("w_in", (L, D, IN_COLS)), rw_mu=din("rw_mu", (L, RW_COLS)), rw_w0=din("rw_w0", (L, 256)),
        rw_w2=din("rw_w2", (L, 32, 256)), rw_a0=din("rw_a0", (L, 256)), rw_a2=din("rw_a2", (L, 32, 256)),
        rw_g2=din("rw_g2", (L, 64, 256)), rw_k_k=din("rw_k_k", (L, 256)), rw_k_a=din("rw_k_a", (L, 256)),
        rw_r_k=din("rw_r_k", (L, 256)), rw_ln_w=din("rw_ln_w", (L, 256)), rw_ln_b=din("rw_ln_b", (L, 256)),
        fox_b_f=din("fox_b_f", (L, 8)), hg_lb_logits=din("hg_lb_logits", (L, 256)),
        hg_norm_g=din("hg_norm_g", (L, 256)), w_out=din("w_out", (L, D, D)), norm2_g=din("norm2_g", (L, D)),
        w_ffn_in=din("w_ffn_in", (L, D, 2 * DFF)), w_ffn_out=din("w_ffn_out", (L, DFF, D)),
        final_norm_g=din("final_norm_g", (D,)),
    )
    O = dict(
        yp=dout("yp", (2, SEQ, D)), ys=dout("ys", (TS, D)),
        fkp=dout("fkp", (L, 2, SEQ, 512)), fvp=dout("fvp", (L, 2, SEQ, 512)), flp=dout("flp", (L, 2, SEQ, 8)),
        rwp=dout("rwp", (L, 2, 4, 64, 64)), rshp=dout("rshp", (L, 2, RW_COLS)), hgp=dout("hgp", (L, 2, 4, 64, 64)),
        fks=dout("fks", (L, TS, 512)), fvs=dout("fvs", (L, TS, 512)), fls=dout("fls", (L, TS, 8)),
        rws=dout("rws", (L, 4, 64, 64)), rshs=dout("rshs", (L, RW_COLS)), hgs=dout("hgs", (L, 4, 64, 64)),
    )
    xres = nc.dram_tensor("xres", [D, NTOK], F32).ap()
    xres_r = Reg("xres")
    seqs = [(0, SEQ), (SEQ, SEQ), (2 * SEQ, TS)]

    def tiles_of(s, W):
        off, T = seqs[s]
        w = min(W, T)
        return [(off + j * w, w, j) for j in range(T // w)]

    xres_regs = {}

    def xreg(t0):
        return xres_regs.setdefault(t0, Reg("xres%d" % t0))

    with contextlib.ExitStack() as top:
        fw = FW(nc, top)
        ident = Tt(fw.sbuf("ident", [128, 128], F32), name="ident")
        identb = Tt(fw.sbuf("identb", [128, 128], BF16), name="identb")
        onesb = Tt(fw.sbuf("onesb", [128, 128], BF16), name="onesb")
        fw.op("pool", lambda e: e.memset(ident.t[:], 0.0), writes=[ident.r])
        fw.op("pool", lambda e: e.affine_select(out=ident.t[:], in_=ident.t[:], pattern=[[-1, 128]],
                                                compare_op=ALU.not_equal, fill=1.0, base=0,
                                                channel_multiplier=1), reads=[ident.r], writes=[ident.r])
        fw.op("pool", lambda e: e.tensor_copy(out=identb.t[:], in_=ident.t[:]), reads=[ident.r], writes=[identb.r])
        fw.op("pool", lambda e: e.memset(onesb.t[:], 1.0), writes=[onesb.r])
        epsb = Tt(fw.sbuf("epsb", [128, 1], F32), name="epsb")
        fw.op("pool", lambda e: e.memset(epsb.t[:], EPS), writes=[epsb.r])


        trif = Tt(fw.sbuf("trif", [128, 128], F32), name="trif")
        fw.op("pool", lambda e: e.memset(trif.t[:], 1.0), writes=[trif.r])
        fw.op("pool", lambda e: e.affine_select(out=trif.t[:], in_=trif.t[:], pattern=[[1, 128]],
                                                compare_op=ALU.is_ge, fill=0.0, base=0, channel_multiplier=-1),
              reads=[trif.r], writes=[trif.r])
        self127 = Tt(fw.sbuf("self127", [128, 128], F32), name="self127")
        fw.op("pool", lambda e: e.memset(self127.t[:], 0.0), writes=[self127.r])
        fw.op("pool", lambda e: e.affine_select(out=self127.t[:], in_=self127.t[:], pattern=[[0, 128]],
                                                compare_op=ALU.not_equal, fill=1.0, base=-127, channel_multiplier=1),
              reads=[self127.r], writes=[self127.r])
        mnegf = Tt(fw.sbuf("mnegf", [128, 128], F32), name="mnegf")
        maskneg = Tt(fw.sbuf("maskneg", [128, 128], BF16), name="maskneg")
        fw.op("pool", lambda e: e.memset(mnegf.t[:], 0.0), writes=[mnegf.r])
        fw.op("pool", lambda e: e.affine_select(out=mnegf.t[:], in_=mnegf.t[:], pattern=[[1, 128]],
                                                compare_op=ALU.is_ge, fill=-30000.0, base=0, channel_multiplier=-1),
              reads=[mnegf.r], writes=[mnegf.r])
        fw.op("pool", lambda e: e.tensor_copy(out=maskneg.t[:], in_=mnegf.t[:]), reads=[mnegf.r], writes=[maskneg.r])
        e8 = Tt(fw.sbuf("e8", [8, 8, 128], F32), name="e8")
        fw.op("pool", lambda e: e.memset(e8.t[:], 0.0), writes=[e8.r])
        fw.op("pool", lambda e: e.affine_select(out=e8.t[:], in_=e8.t[:], pattern=[[-1, 8], [0, 128]],
                                                compare_op=ALU.not_equal, fill=1.0, base=0, channel_multiplier=1),
              reads=[e8.r], writes=[e8.r])
        selh = Tt(fw.sbuf("selh", [72, 8, 128], BF16), name="selh")
        fw.op("pool", lambda e: e.memset(selh.t[:], 0.0), writes=[selh.r])
        for b0 in (0, 32, 64):
            fw.op("dve", lambda e, b0=b0: e.tensor_copy(out=selh.t[b0:b0 + 8, :, :], in_=e8.t[:]),
                  reads=[e8.r], writes=[selh.r])
        onesf = Tt(fw.sbuf("onesf", [128, 64], F32), name="onesf")
        fw.op("pool", lambda e: e.memset(onesf.t[:], 1.0), writes=[onesf.r])
        bfb = Tt(fw.sbuf("bfb", [128, L, 8], F32), name="bfb")
        for l_ in range(L):
            fw.dma("sp", bfb.t[:, l_, :], I["fox_b_f"][l_:l_ + 1, :].to_broadcast([128, 8]), writes=[bfb.r],
                   stream="par", group=True)
        yfox = nc.dram_tensor("yfox", [512, NTOK], BF16).ap()
        yfox_regs = {}

        def yfreg(t0):
            return yfox_regs.setdefault(t0, Reg("yfox%d" % t0))

        banks = [Tt(fw.psum("bank%d" % i, [128, 512]), name="bank%d" % i) for i in range(8)]
        bank_ctr = [0]

        default_pool = [(0, 1, 2, 3, 4, 5, 6, 7)]

        def next_bank(pool=None):
            if pool is None:
                pool = default_pool[0]
            b = banks[pool[bank_ctr[0] % len(pool)]]
            bank_ctr[0] += 1
            return b

        def load_fm(name, src_ap, ncol, q="sp"):
            t = Tt(fw.sbuf(name, [128, L, ncol], F32), name=name)
            fw.dma(q, t.t[:], src_ap.rearrange("l (c p) -> p l c", p=128), writes=[t.r], stream="par", group=True,
                   allow_slow_non_contiguous=True)
            return t
        n1g = load_fm("n1g", I["norm1_g"], 8)
        n2g = load_fm("n2g", I["norm2_g"], 8)
        badaT = load_fm("badaT", I["b_ada"], 48)
        fng = Tt(fw.sbuf("fng", [128, 8], F32), name="fng")
        fw.dma("sp", fng.t[:], I["final_norm_g"].rearrange("(c p) -> p c", p=128), writes=[fng.r], stream="par", group=True,
               allow_slow_non_contiguous=True)

        rmask32 = Tt(fw.sbuf("rmask32", [128, 512], F32), name="rmask32")
        fw.op("pool", lambda e: e.memset(rmask32.t[:], 1.0), writes=[rmask32.r])
        fw.op("pool", lambda e: e.affine_select(out=rmask32.t[:, :].rearrange("p (c t) -> p c t", t=32),
                                                in_=rmask32.t[:, :].rearrange("p (c t) -> p c t", t=32),
                                                pattern=[[0, 16], [1, 32]], compare_op=ALU.not_equal, fill=0.0,
                                                base=0, channel_multiplier=0), reads=[rmask32.r], writes=[rmask32.r])
        maskbd = Tt(fw.sbuf("maskbd", [128, 128], F32), name="maskbd")
        fw.op("pool", lambda e: e.tensor_copy(out=maskbd.t[:], in_=trif.t[:]), reads=[trif.r], writes=[maskbd.r])
        for cb_ in range(1, 4):
            fw.op("pool", lambda e, cb_=cb_: e.affine_select(
                out=maskbd.t[:, cb_ * 32:(cb_ + 1) * 32], in_=maskbd.t[:, cb_ * 32:(cb_ + 1) * 32], pattern=[[0, 32]],
                compare_op=ALU.is_ge, fill=0.0, base=-cb_ * 32, channel_multiplier=1),
                reads=[maskbd.r], writes=[maskbd.r])
        onesbdf = Tt(fw.sbuf("onesbdf", [128, 128], F32), name="onesbdf")
        onesbd = Tt(fw.sbuf("onesbd", [128, 128], BF16), name="onesbd")
        fw.op("pool", lambda e: e.memset(onesbdf.t[:], 1.0), writes=[onesbdf.r])
        fw.op("pool", lambda e: e.affine_select(out=onesbdf.t[:, 0:64], in_=onesbdf.t[:, 0:64], pattern=[[0, 64]],
                                                compare_op=ALU.is_ge, fill=0.0, base=63, channel_multiplier=-1),
              reads=[onesbdf.r], writes=[onesbdf.r])
        fw.op("pool", lambda e: e.affine_select(out=onesbdf.t[:, 64:128], in_=onesbdf.t[:, 64:128], pattern=[[0, 64]],
                                                compare_op=ALU.is_ge, fill=0.0, base=-64, channel_multiplier=1),
              reads=[onesbdf.r], writes=[onesbdf.r])
        fw.op("pool", lambda e: e.tensor_copy(out=onesbd.t[:], in_=onesbdf.t[:]), reads=[onesbdf.r], writes=[onesbd.r])
        hgng = load_fm("hgng", I["hg_norm_g"], 2)
        lbl = load_fm("lbl", I["hg_lb_logits"], 2)
        lbT = Tt(fw.sbuf("lbT", [128, L, 2], F32), name="lbT")
        omlT = Tt(fw.sbuf("omlT", [128, L, 2], F32), name="omlT")
        nomlT = Tt(fw.sbuf("nomlT", [128, L, 2], F32), name="nomlT")
        lbm = Tt(fw.sbuf("lbm", [128, 2], F32), name="lbm")
        lbe = Tt(fw.sbuf("lbe", [128, L, 2], F32), name="lbe")
        lbs_ = Tt(fw.sbuf("lbs_", [128, 2], F32), name="lbs_")
        fw.op("dve", lambda e: e.tensor_copy(out=lbm.t[:], in_=lbl.t[:, 0, :]), reads=[lbl.r], writes=[lbm.r])
        for l_ in range(1, L):
            fw.op("dve", lambda e, l_=l_: e.tensor_max(out=lbm.t[:], in0=lbm.t[:], in1=lbl.t[:, l_, :]),
                  reads=[lbm.r, lbl.r], writes=[lbm.r])
        for l_ in range(L):
            fw.op("dve", lambda e, l_=l_: e.tensor_sub(out=lbe.t[:, l_, :], in0=lbl.t[:, l_, :], in1=lbm.t[:]),
                  reads=[lbm.r, lbl.r], writes=[lbe.r])
        fw.op("act", lambda e: e.activation(out=lbe.t[:], in_=lbe.t[:], func=AF.Exp), reads=[lbe.r], writes=[lbe.r])
        fw.op("dve", lambda e: e.tensor_copy(out=lbs_.t[:], in_=lbe.t[:, 0, :]), reads=[lbe.r], writes=[lbs_.r])
        for l_ in range(1, L):
            fw.op("dve", lambda e, l_=l_: e.tensor_add(out=lbs_.t[:], in0=lbs_.t[:], in1=lbe.t[:, l_, :]),
                  reads=[lbs_.r, lbe.r], writes=[lbs_.r])
        fw.op("dve", lambda e: e.reciprocal(out=lbs_.t[:], in_=lbs_.t[:]), reads=[lbs_.r], writes=[lbs_.r])
        for l_ in range(L):
            fw.op("dve", lambda e, l_=l_: e.tensor_mul(out=lbe.t[:, l_, :], in0=lbe.t[:, l_, :], in1=lbs_.t[:]),
                  reads=[lbs_.r, lbe.r], writes=[lbe.r])
        fw.op("dve", lambda e: e.memset(lbT.t[:, 0, :], 0.0), writes=[lbT.r])
        for l_ in range(1, L):
            fw.op("dve", lambda e, l_=l_: e.tensor_add(out=lbT.t[:, l_, :], in0=lbT.t[:, l_ - 1, :], in1=lbe.t[:, l_, :]),
                  reads=[lbT.r, lbe.r], writes=[lbT.r])
        fw.op("dve", lambda e: e.tensor_scalar(out=omlT.t[:], in0=lbT.t[:], scalar1=-1.0, scalar2=1.0,
                                               op0=ALU.mult, op1=ALU.add), reads=[lbT.r], writes=[omlT.r])
        fw.op("dve", lambda e: e.tensor_scalar_mul(out=nomlT.t[:], in0=omlT.t[:], scalar1=-1.0),
              reads=[omlT.r], writes=[nomlT.r])

        rmask64 = Tt(fw.sbuf("rmask64", [128, 512], F32), name="rmask64")
        fw.op("pool", lambda e: e.memset(rmask64.t[:], 1.0), writes=[rmask64.r])
        fw.op("pool", lambda e: e.affine_select(out=rmask64.t[:, :].rearrange("p (c t) -> p c t", t=64),
                                                in_=rmask64.t[:, :].rearrange("p (c t) -> p c t", t=64),
                                                pattern=[[0, 8], [1, 64]], compare_op=ALU.not_equal, fill=0.0,
                                                base=0, channel_multiplier=0), reads=[rmask64.r], writes=[rmask64.r])
        mask12 = Tt(fw.sbuf("mask12", [64, 2, 64], F32), name="mask12")
        mask34 = Tt(fw.sbuf("mask34", [64, 2, 64], F32), name="mask34")
        mask5 = Tt(fw.sbuf("mask5", [64, 64], F32), name="mask5")
        fw.op("pool", lambda e: e.memset(mask12.t[:], 1.0), writes=[mask12.r])
        fw.op("pool", lambda e: e.affine_select(out=mask12.t[:, 0, :], in_=mask12.t[:, 0, :], pattern=[[1, 64]],
                                                compare_op=ALU.is_ge, fill=0.0, base=-1, channel_multiplier=-1),
              reads=[mask12.r], writes=[mask12.r])
        fw.op("pool", lambda e: e.affine_select(out=mask12.t[:, 1, :], in_=mask12.t[:, 1, :], pattern=[[1, 64]],
                                                compare_op=ALU.is_ge, fill=0.0, base=0, channel_multiplier=-1),
              reads=[mask12.r], writes=[mask12.r])
        fw.op("dve", lambda e: e.tensor_scalar_mul(out=mask34.t[:, 0, :], in0=mask12.t[:, 0, :], scalar1=-1.0),
              reads=[mask12.r], writes=[mask34.r])
        fw.op("pool", lambda e: e.tensor_copy(out=mask34.t[:, 1, :], in_=mask12.t[:, 1, :]),
              reads=[mask12.r], writes=[mask34.r])
        fw.op("pool", lambda e: e.memset(mask5.t[:], -1.0), writes=[mask5.r])
        fw.op("pool", lambda e: e.affine_select(out=mask5.t[:], in_=mask5.t[:], pattern=[[-1, 64]],
                                                compare_op=ALU.is_ge, fill=0.0, base=-1, channel_multiplier=1),
              reads=[mask5.r], writes=[mask5.r])
        eps2 = Tt(fw.sbuf("eps2", [128, 1], F32), name="eps2")
        fw.op("pool", lambda e: e.memset(eps2.t[:], 64e-5), writes=[eps2.r])
        rwp = {}
        for nm in ("rw_w0", "rw_a0", "rw_k_k", "rw_k_a", "rw_r_k", "rw_ln_w", "rw_ln_b"):
            rwp[nm] = load_fm("p_" + nm, I[nm], 2)
        nw0 = Tt(fw.sbuf("nw0", [128, L, 2], F32), name="nw0")
        na0 = Tt(fw.sbuf("na0", [128, L, 2], F32), name="na0")
        omka = Tt(fw.sbuf("omka", [128, L, 2], F32), name="omka")
        fw.op("dve", lambda e: e.tensor_scalar_mul(out=nw0.t[:], in0=rwp["rw_w0"].t[:], scalar1=-1.0),
              reads=[rwp["rw_w0"].r], writes=[nw0.r])
        fw.op("dve", lambda e: e.tensor_scalar_mul(out=na0.t[:], in0=rwp["rw_a0"].t[:], scalar1=-1.0),
              reads=[rwp["rw_a0"].r], writes=[na0.r])
        fw.op("dve", lambda e: e.tensor_scalar(out=omka.t[:], in0=rwp["rw_k_a"].t[:], scalar1=-1.0, scalar2=1.0,
                                               op0=ALU.mult, op1=ALU.add), reads=[rwp["rw_k_a"].r], writes=[omka.r])
        mul = Tt(fw.sbuf("mul", [128, L, 9], F32), name="mul")
        fw.op("pool", lambda e: e.memset(mul.t[:], 0.0), writes=[mul.r])
        m7 = load_fm("m7", I["rw_mu"], 7)
        fw.op("dve", lambda e: e.tensor_copy(out=mul.t[:, :, 0:6], in_=m7.t[:, :, 0:6]), reads=[m7.r], writes=[mul.r])
        fw.op("dve", lambda e: e.tensor_copy(out=mul.t[0:32, :, 6], in_=m7.t[0:32, :, 6]), reads=[m7.r], writes=[mul.r])
        fw.op("dve", lambda e: e.tensor_copy(out=mul.t[0:32, :, 7], in_=m7.t[32:64, :, 6]), reads=[m7.r], writes=[mul.r])
        fw.op("dve", lambda e: e.tensor_copy(out=mul.t[0:64, :, 8], in_=m7.t[64:128, :, 6]), reads=[m7.r], writes=[mul.r])
        w2b = Tt(fw.sbuf("w2b", [32, L, 256], BF16), name="w2b")
        a2b = Tt(fw.sbuf("a2b", [32, L, 256], BF16), name="a2b")
        g2b = Tt(fw.sbuf("g2b", [64, L, 256], BF16), name="g2b")
        fw.dma("pool", w2b.t[:], I["rw_w2"].rearrange("l k n -> k l n"), writes=[w2b.r], stream="parb", group=True)
        fw.dma("pool", a2b.t[:], I["rw_a2"].rearrange("l k n -> k l n"), writes=[a2b.r], stream="parb", group=True)
        fw.dma("pool", g2b.t[:], I["rw_g2"].rearrange("l k n -> k l n"), writes=[g2b.r], stream="parb", group=True)

        cT = Tt(fw.sbuf("cT", [128, 3, 8], F32), name="cT")
        fw.dma("sp", cT.t[:], I["cc"].rearrange("b (c p) -> p b c", p=128), writes=[cT.r], stream="par", group=True,
               allow_slow_non_contiguous=True)
        siluT = Tt(fw.sbuf("siluT", [128, 8, 3], F32), name="siluT")
        sl_e = Tt(fw.sbuf("sl_e", [128, 3, 8], F32), name="sl_e")
        fw.op("act", lambda e: e.activation(out=sl_e.t[:], in_=cT.t[:], func=AF.Exp, scale=-1.0),
              reads=[cT.r], writes=[sl_e.r])
        fw.op("dve", lambda e: e.tensor_scalar_add(out=sl_e.t[:], in0=sl_e.t[:], scalar1=1.0),
              reads=[sl_e.r], writes=[sl_e.r])
        fw.op("dve", lambda e: e.reciprocal(out=sl_e.t[:], in_=sl_e.t[:]), reads=[sl_e.r], writes=[sl_e.r])
        fw.op("dve", lambda e: e.tensor_tensor(out=siluT.t[:].rearrange("p c b -> p b c"), in0=sl_e.t[:],
                                               in1=cT.t[:], op=ALU.mult),
              reads=[sl_e.r, cT.r], writes=[siluT.r])
        mod = Tt(fw.sbuf("mod", [128, L, 6, 3, 8], F32), name="mod")
        with contextlib.ExitStack() as ph:
            sub = FWScope(fw, ph)
            wa = [Tt(sub.sbuf("wa%d" % i, [128, 8, 512], F32), name="wa%d" % i) for i in range(2)]
            k = 0
            for l in range(L):
                for jg in range(12):
                    w = wa[k % 2]
                    k += 1
                    fw.dma("sp" if k % 2 else "pool", w.t[:],
                           I["w_ada"][l].rearrange("(kc p) n -> p kc n", p=128)[:, :, jg * 512:(jg + 1) * 512],
                           writes=[w.r], stream="wada%d" % (k % 2))
                    for jj in range(4):
                        j = jg * 4 + jj
                        m, c = j // 8, j % 8
                        pb = next_bank()
                        for kc in range(8):
                            fw.op("pe", lambda e, w=w, pb=pb, kc=kc, jj=jj: e.matmul(
                                pb.t[:, 0:3], lhsT=w.t[:, kc, jj * 128:(jj + 1) * 128], rhs=siluT.t[:, kc, :],
                                start=(kc == 0), stop=(kc == 7)),
                                reads=[w.r, siluT.r], writes=[pb.r], signal=(kc == 7))
                        fw.op("dve", lambda e, pb=pb, l=l, m=m, c=c, j=j: e.tensor_scalar(
                            out=mod.t[:, l, m, :, c], in0=pb.t[:, 0:3], scalar1=badaT.t[:, l, j:j + 1], scalar2=None,
                            op0=ALU.add), reads=[pb.r, badaT.r], writes=[mod.r])
            fw.flush()
        G1 = Tt(fw.sbuf("G1", [128, L, 3, 8], F32), name="G1")
        G2 = Tt(fw.sbuf("G2", [128, L, 3, 8], F32), name="G2")
        for l in range(L):
            for (G, ng, mi) in ((G1, n1g, 1), (G2, n2g, 4)):
                for b in range(3):
                    fw.op("dve", lambda e, G=G, ng=ng, mi=mi, l=l, b=b: e.scalar_tensor_tensor(
                        out=G.t[:, l, b, :], in0=mod.t[:, l, mi, b, :], scalar=1.0, in1=ng.t[:, l, :],
                        op0=ALU.add, op1=ALU.mult), reads=[mod.r, ng.r], writes=[G.r])

        def load_xT(xT, t0, w, q="sp"):
            fw.dma(q, xT.t[:, :, 0:w], xres.rearrange("(c p) t -> p c t", p=128)[:, :, t0:t0 + w],
                   reads=[xreg(t0)], writes=xT.rs, stream="xld" + xT.r.name[-2:])

        def store_xT(xT, t0, w, q="sp"):
            fw.dma(q, xres.rearrange("(c p) t -> p c t", p=128)[:, :, t0:t0 + w], xT.t[:, :, 0:w],
                   reads=xT.rs, writes=[xreg(t0)], stream="xst" + xT.r.name[-2:])

        def norm_fm(xT, w, sq, tmp, lnv, rstd, out, gap, shap, out_regs):
            fw.op("act", lambda e: e.activation(out=sq.t[:, :, 0:w], in_=xT.t[:, :, 0:w], func=AF.Square),
                  reads=xT.rs, writes=[sq.r])
            pb = next_bank()
            for c in range(8):
                fw.op("pe", lambda e, c=c: e.matmul(pb.t[:, 0:w], lhsT=onesb.t[:], rhs=sq.t[:, c, 0:w],
                                                    start=(c == 0), stop=(c == 7)),
                      reads=[onesb.r, sq.r], writes=[pb.r], signal=(c == 7))
            fw.op("act", lambda e: e.activation(out=lnv.t[:, 0:w], in_=pb.t[:, 0:w], func=AF.Ln, scale=1.0 / D,
                                                bias=epsb.t[:]), reads=[pb.r, epsb.r], writes=[lnv.r])
            fw.op("act", lambda e: e.activation(out=rstd.t[:, 0:w], in_=lnv.t[:, 0:w], func=AF.Exp, scale=-0.5),
                  reads=[lnv.r], writes=[rstd.r])
            for c in range(8):
                tm = tmp[c % len(tmp)]
                fw.op("dve", lambda e, c=c, tm=tm: e.tensor_tensor(out=tm.t[:, 0:w], in0=xT.t[:, c, 0:w],
                                                                   in1=rstd.t[:, 0:w], op=ALU.mult),
                      reads=[xT.rs[c], rstd.r], writes=[tm.r])
                if shap is not None:
                    fw.op("act", lambda e, c=c, tm=tm: e.activation(out=out.t[:, c, 0:w], in_=tm.t[:, 0:w],
                                                                    func=AF.Identity, scale=gap(c), bias=shap(c)),
                          reads=[tm.r, G1.r, G2.r, mod.r], writes=[out_regs[c]])
                else:
                    fw.op("act", lambda e, c=c, tm=tm: e.activation(out=out.t[:, c, 0:w], in_=tm.t[:, 0:w],
                                                                    func=AF.Identity, scale=gap(c)),
                          reads=[tm.r, fng.r], writes=[out_regs[c]])

        with contextlib.ExitStack() as ph:
            sub = FWScope(fw, ph)
            xtok = [Tt(sub.sbuf("xtok%d" % i, [128, 4, D], F32), name="xtok%d" % i) for i in range(2)]
            xTs = [Tt(sub.sbuf("xTp%d" % i, [128, 8, 512], F32), nreg=8, name="xTp%d" % i) for i in range(2)]
            it = 0
            for s in range(3):
                src = I["xp"][s] if s < 2 else I["xs"]
                off, T = seqs[s]
                for (t0, w, j) in tiles_of(s, 512):
                    xt = xtok[it % 2]
                    xT = xTs[it % 2]
                    it += 1
                    nb = (w + 127) // 128
                    pw = min(w, 128)
                    lt0 = t0 - off
                    if w >= 128:
                        fw.dma("sp" if it % 2 else "pool", xt.t[:, 0:nb, :],
                               src[lt0:lt0 + w, :].rearrange("(b p) d -> p b d", p=128), writes=[xt.r], stream="xtok")
                    else:
                        fw.dma("sp", xt.t[0:w, 0, :], src[lt0:lt0 + w, :], writes=[xt.r], stream="xtok")
                    for c in range(8):
                        pb = next_bank()
                        for tb in range(nb):
                            fw.op("pe", lambda e, c=c, tb=tb, pb=pb, xt=xt, pw=pw: e.transpose(
                                pb.t[:, tb * 128:tb * 128 + pw], xt.t[0:pw, tb, c * 128:(c + 1) * 128],
                                ident.t[0:pw, 0:pw]),
                                reads=[xt.r, ident.r], writes=[pb.r], signal=(tb == nb - 1))
                        eng = "act" if c % 2 else "dve"
                        if eng == "act":
                            fw.op("act", lambda e, c=c, pb=pb, xT=xT, w=w: e.activation(
                                out=xT.t[:, c, 0:w], in_=pb.t[:, 0:w], func=AF.Copy), reads=[pb.r], writes=[xT.rs[c]])
                        else:
                            fw.op("dve", lambda e, c=c, pb=pb, xT=xT, w=w: e.tensor_copy(
                                out=xT.t[:, c, 0:w], in_=pb.t[:, 0:w]), reads=[pb.r], writes=[xT.rs[c]])
                    store_xT(xT, t0, w, q="sp" if it % 2 else "pool")
            fw.flush()

        for l in range(L):
            fw.rotate()
            if stage >= 2:
              with contextlib.ExitStack() as ph:
                sub = FWScope(fw, ph)
                default_pool[0] = (0, 1, 2, 3, 4, 5)
                WA = 512
                FC0 = RW_COLS
                TKMAX = max(SEQ, PAST + TS)
                NBMAX = (TKMAX + 127) // 128
                wf = Tt(sub.sbuf("wf", [128, 8, FOX_COLS], BF16), name="wf")
                for kc in range(8):
                    fw.dma("pool", wf.t[:, kc, :], I["w_in"][l][kc * 128:(kc + 1) * 128, FC0:FC0 + FOX_COLS],
                           writes=[wf.r], stream="wld", group=True)
                KT = Tt(sub.sbuf("KT", [128, 4, TKMAX], BF16), nreg=NBMAX, name="KT")
                Vx = Tt(sub.sbuf("Vx", [128, NBMAX, 8, 65], BF16), nreg=NBMAX, name="Vx")
                fw.op("pool", lambda e: e.memset(Vx.t[:], 1.0), writes=Vx.rs)
                ctok = Tt(sub.sbuf("ctok", [128, NBMAX, 8], F32), nreg=NBMAX, name="ctok")
                negc = Tt(sub.sbuf("negc", [128, NBMAX, 8], F32), nreg=NBMAX, name="negc")
                xTa = [Tt(sub.sbuf("xTa%d" % i, [128, 8, WA], F32), nreg=8, name="xTa%d" % i) for i in range(2)]
                sq = Tt(sub.sbuf("sqa", [128, 8, WA], BF16), name="sqa")
                hT = Tt(sub.sbuf("hTa", [128, 8, WA], BF16), nreg=8, name="hTa")
                tmp = [Tt(sub.sbuf("tmpa%d" % i, [128, WA], F32), name="tmpa%d" % i) for i in range(2)]
                lnv = Tt(sub.sbuf("lnva", [128, WA], F32), name="lnva")
                rstd = Tt(sub.sbuf("rstda", [128, WA], F32), name="rstda")
                QT = Tt(sub.sbuf("QT", [128, 4, WA], BF16), nreg=4, name="QT")
                ktok = [Tt(sub.sbuf("ktok%d" % i, [128, 512], F32), name="ktok%d" % i) for i in range(2)]
                vtok = [Tt(sub.sbuf("vtok%d" % i, [128, 512], F32), name="vtok%d" % i) for i in range(2)]
                ftok = [Tt(sub.sbuf("ftok%d" % i, [128, 8], F32), name="ftok%d" % i) for i in range(2)]
                ltok = [Tt(sub.sbuf("ltok%d" % i, [128, 8], F32), name="ltok%d" % i) for i in range(2)]
                cfm = Tt(sub.sbuf("cfm", [8, WA], F32), name="cfm")
                cr1 = Tt(sub.sbuf("cr1", [8, WA], F32), name="cr1")
                cr2 = Tt(sub.sbuf("cr2", [8, WA], F32), name="cr2")
                midt = Tt(sub.sbuf("midt", [8, WA], BF16), name="midt")
                cq96 = Tt(sub.sbuf("cq96", [72, WA], BF16), name="cq96")
                fw.op("pool", lambda e: e.memset(cq96.t[:], 0.0), writes=[cq96.r])
                pts = [Tt(sub.sbuf("pt%d" % i, [128, WA], BF16), name="pt%d" % i) for i in range(4)]
                rsf = Tt(sub.sbuf("rsf", [128, WA], F32), name="rsf")
                rcp = Tt(sub.sbuf("rcp", [64, WA], F32), name="rcp")
                yfT = Tt(sub.sbuf("yfT", [128, 4, WA], BF16), nreg=4, name="yfT")
                it = 0
                ik = 0
                ipt = 0
                a1_tiles = [(s_, t0_, w_) for s_ in range(3) for (t0_, w_, j_) in tiles_of(s_, WA)]
                a1_idx = [0]

                def a1_norm(idx):
                    s_, t0_, w_ = a1_tiles[idx]
                    xT_ = xTa[idx % 2]
                    load_xT(xT_, t0_, w_)
                    norm_fm(xT_, w_, sq, tmp, lnv, rstd, hT,
                            lambda c, s_=s_: G1.t[:, l, s_, c:c + 1], lambda c, s_=s_: mod.t[:, l, 0, s_, c:c + 1], hT.rs)
                for s in range(3):
                    off, T = seqs[s]
                    kbase = 0
                    if s == 2:
                        kbase = PAST
                        for cb in range(PAST // 128):
                            kt_ = ktok[ik % 2]
                            vt_ = vtok[ik % 2]
                            ft_ = ltok[ik % 2]
                            ik += 1
                            fw.dma("sp", kt_.t[:], I["ck"][l][cb * 128:(cb + 1) * 128, :], writes=[kt_.r], stream="cldk%d" % (ik % 2))
                            fw.dma("sp", vt_.t[:], I["cv"][l][cb * 128:(cb + 1) * 128, :], writes=[vt_.r], stream="cldv%d" % (ik % 2))
                            fw.dma("sp", ft_.t[:], I["cl"][l][cb * 128:(cb + 1) * 128, :], writes=[ft_.r], stream="cldf%d" % (ik % 2))
                            pb = next_bank()
                            for pc in range(4):
                                fw.op("pe", lambda e, pc=pc, pb=pb, kt_=kt_: e.transpose(
                                    pb.t[:, pc * 128:(pc + 1) * 128], kt_.t[:, pc * 128:(pc + 1) * 128], ident.t[:]),
                                    reads=[kt_.r, ident.r], writes=[pb.r], signal=(pc == 3))
                            fw.op("act", lambda e, pb=pb, cb=cb: e.activation(
                                out=KT.t[:, :, cb * 128:(cb + 1) * 128],
                                in_=pb.t[:, :].rearrange("p (c t) -> p c t", c=4), func=AF.Copy),
                                reads=[pb.r], writes=[KT.rs[cb]])
                            fw.op("dve", lambda e, vt_=vt_, cb=cb: e.tensor_copy(
                                out=Vx.t[:, cb, :, 0:64], in_=vt_.t[:, :].rearrange("p (h d) -> p h d", h=8)),
                                reads=[vt_.r], writes=[Vx.rs[cb]])
                            pc_ = next_bank()
                            fw.op("pe", lambda e, pc_=pc_, ft_=ft_, cb=cb: e.matmul(
                                pc_.t[:, 0:8], lhsT=trif.t[:], rhs=ft_.t[:], start=True, stop=(cb == 0)),
                                reads=[trif.r, ft_.r], writes=[pc_.r], signal=(cb == 0))
                            if cb > 0:
                                fw.op("pe", lambda e, pc_=pc_, cb=cb: e.matmul(
                                    pc_.t[:, 0:8], lhsT=self127.t[:], rhs=ctok.t[:, cb - 1, :], start=False, stop=True),
                                    reads=[self127.r, ctok.rs[cb - 1]], writes=[pc_.r])
                            fw.op("dve", lambda e, pc_=pc_, cb=cb: e.tensor_copy(out=ctok.t[:, cb, :], in_=pc_.t[:, 0:8]),
                                  reads=[pc_.r], writes=[ctok.rs[cb]])
                            fw.op("act", lambda e, pc_=pc_, cb=cb: e.activation(
                                out=negc.t[:, cb, :], in_=pc_.t[:, 0:8], func=AF.Copy, scale=-1.0),
                                reads=[pc_.r], writes=[negc.rs[cb]])
                    dK = O["fkp"][l][s] if s < 2 else O["fks"][l]
                    dV = O["fvp"][l][s] if s < 2 else O["fvs"][l]
                    dF = O["flp"][l][s] if s < 2 else O["fls"][l]
                    def a1_tile(s, t0, w, j, xT, off, kbase, dK, dV, dF, l=l):
                        nonlocal ik, ipt
                        lt0 = t0 - off
                        kt0 = kbase + lt0
                        nb = (w + 127) // 128
                        pw = min(w, 128)
                        if a1_idx[0] == 0:
                            a1_norm(0)
                        for pc in range(4):
                            pq = next_bank()
                            for kc in range(8):
                                fw.op("pe", lambda e, kc=kc, pc=pc, pq=pq, w=w: e.matmul(
                                    pq.t[:, 0:w], lhsT=wf.t[:, kc, pc * 128:(pc + 1) * 128], rhs=hT.t[:, kc, 0:w],
                                    start=(kc == 0), stop=(kc == 7)),
                                    reads=[wf.r, hT.rs[kc]], writes=[pq.r], signal=(kc == 7))
                            fw.op("act", lambda e, pc=pc, pq=pq, w=w: e.activation(
                                out=QT.t[:, pc, 0:w], in_=pq.t[:, 0:w], func=AF.Copy, scale=0.125),
                                reads=[pq.r], writes=[QT.rs[pc]])
                            pk = next_bank()
                            for kc in range(8):
                                fw.op("pe", lambda e, kc=kc, pc=pc, pk=pk, w=w: e.matmul(
                                    pk.t[:, 0:w], lhsT=wf.t[:, kc, 512 + pc * 128:512 + (pc + 1) * 128],
                                    rhs=hT.t[:, kc, 0:w], start=(kc == 0), stop=(kc == 7)),
                                    reads=[wf.r, hT.rs[kc]], writes=[pk.r], signal=(kc == 7))
                            kregs = [KT.rs[(kt0 + tb * 128) // 128] for tb in range(nb)]
                            fw.op("dve", lambda e, pc=pc, pk=pk, w=w, kt0=kt0: e.tensor_copy(
                                out=KT.t[:, pc, kt0:kt0 + w], in_=pk.t[:, 0:w]), reads=[pk.r], writes=kregs)
                        for tb in range(nb):
                            kb = (kt0 + tb * 128) // 128
                            kt_ = ktok[ik % 2]
                            vt_ = vtok[ik % 2]
                            ft_ = ftok[ik % 2]
                            lt_ = ltok[ik % 2]
                            ik += 1
                            for (dst_t, c0, ncol, dd, eng) in ((kt_, 512, 512, dK, "act"), (vt_, 1024, 512, dV, "dve")):
                                pb = next_bank()
                                for kc in range(8):
                                    fw.op("pe", lambda e, kc=kc, pb=pb, tb=tb, c0=c0, ncol=ncol, pw=pw: e.matmul(
                                        pb.t[0:pw, 0:ncol], lhsT=hT.t[:, kc, tb * 128:tb * 128 + pw],
                                        rhs=wf.t[:, kc, c0:c0 + ncol], start=(kc == 0), stop=(kc == 7)),
                                        reads=[wf.r, hT.rs[kc]], writes=[pb.r], signal=(kc == 7))
                                if eng == "act":
                                    fw.op("act", lambda e, pb=pb, dst_t=dst_t, pw=pw: e.activation(
                                        out=dst_t.t[0:pw, :], in_=pb.t[0:pw, :], func=AF.Copy),
                                        reads=[pb.r], writes=[dst_t.r])
                                else:
                                    fw.op("dve", lambda e, pb=pb, dst_t=dst_t, pw=pw: e.tensor_copy(
                                        out=dst_t.t[0:pw, :], in_=pb.t[0:pw, :]), reads=[pb.r], writes=[dst_t.r])
                                r0 = lt0 + tb * 128
                                fw.dma("sp", dd[r0:r0 + pw, :], dst_t.t[0:pw, :], reads=[dst_t.r],
                                       stream="kvo%s%d" % (eng[0], ik % 2))
                            fw.op("pool", lambda e, vt_=vt_, kb=kb, pw=pw: e.tensor_copy(
                                out=Vx.t[0:pw, kb, :, 0:64], in_=vt_.t[0:pw, :].rearrange("p (h d) -> p h d", h=8)),
                                reads=[vt_.r], writes=[Vx.rs[kb]])
                            pf = next_bank()
                            for kc in range(8):
                                fw.op("pe", lambda e, kc=kc, pf=pf, tb=tb, pw=pw: e.matmul(
                                    pf.t[0:pw, 0:8], lhsT=hT.t[:, kc, tb * 128:tb * 128 + pw],
                                    rhs=wf.t[:, kc, 1536:1544], start=(kc == 0), stop=(kc == 7)),
                                    reads=[wf.r, hT.rs[kc]], writes=[pf.r], signal=(kc == 7))
                            fw.op("dve", lambda e, pf=pf, ft_=ft_, pw=pw: e.tensor_tensor(
                                out=ft_.t[0:pw, :], in0=pf.t[0:pw, 0:8], in1=bfb.t[0:pw, l, :], op=ALU.add),
                                reads=[pf.r, bfb.r], writes=[ft_.r])
                            fw.op("act", lambda e, ft_=ft_, pw=pw: e.activation(
                                out=ft_.t[0:pw, :], in_=ft_.t[0:pw, :], func=AF.Exp, scale=-1.0),
                                reads=[ft_.r], writes=[ft_.r])
                            fw.op("act", lambda e, ft_=ft_, pw=pw: e.activation(
                                out=ft_.t[0:pw, :], in_=ft_.t[0:pw, :], func=AF.Ln, bias=1.0),
                                reads=[ft_.r], writes=[ft_.r])
                            fw.op("dve", lambda e, ft_=ft_, lt_=lt_, pw=pw: e.tensor_scalar_mul(
                                out=lt_.t[0:pw, :], in0=ft_.t[0:pw, :], scalar1=-1.0), reads=[ft_.r], writes=[lt_.r])
                            r0 = lt0 + tb * 128
                            fw.dma("sp", dF[r0:r0 + pw, :], lt_.t[0:pw, :], reads=[lt_.r], stream="kvof%d" % (ik % 2))
                            pc_ = next_bank()
                            first = (kb == 0)
                            fw.op("pe", lambda e, pc_=pc_, lt_=lt_, pw=pw, first=first: e.matmul(
                                pc_.t[0:pw, 0:8], lhsT=trif.t[0:pw, 0:pw], rhs=lt_.t[0:pw, :], start=True, stop=first),
                                reads=[trif.r, lt_.r], writes=[pc_.r], signal=first)
                            if not first:
                                fw.op("pe", lambda e, pc_=pc_, kb=kb, pw=pw: e.matmul(
                                    pc_.t[0:pw, 0:8], lhsT=self127.t[:, 0:pw], rhs=ctok.t[:, kb - 1, :],
                                    start=False, stop=True),
                                    reads=[self127.r, ctok.rs[kb - 1]], writes=[pc_.r])
                            fw.op("dve", lambda e, pc_=pc_, kb=kb, pw=pw: e.tensor_copy(
                                out=ctok.t[0:pw, kb, :], in_=pc_.t[0:pw, 0:8]), reads=[pc_.r], writes=[ctok.rs[kb]])
                            fw.op("act", lambda e, pc_=pc_, kb=kb, pw=pw: e.activation(
                                out=negc.t[0:pw, kb, :], in_=pc_.t[0:pw, 0:8], func=AF.Copy, scale=-1.0),
                                reads=[pc_.r], writes=[negc.rs[kb]])
                            pt_ = next_bank()
                            fw.op("pe", lambda e, pt_=pt_, kb=kb, pw=pw: e.transpose(
                                pt_.t[0:8, 0:pw], ctok.t[0:pw, kb, :], ident.t[0:pw, 0:pw]),
                                reads=[ctok.rs[kb], ident.r], writes=[pt_.r])
                            fw.op("dve", lambda e, pt_=pt_, tb=tb, pw=pw: e.tensor_copy(
                                out=cfm.t[:, tb * 128:tb * 128 + pw], in_=pt_.t[0:8, 0:pw]),
                                reads=[pt_.r], writes=[cfm.r])
                        fw.op("act", lambda e, w=w: e.activation(out=cq96.t[0:8, 0:w], in_=cfm.t[:, 0:w], func=AF.Copy),
                              reads=[cfm.r], writes=[cq96.r])
                        fw.op("dve", lambda e, w=w: e.tensor_tensor(out=cr1.t[:, 0:w], in0=cfm.t[:, 0:w],
                                                                    in1=cq96.t[0:8, 0:w], op=ALU.subtract),
                              reads=[cfm.r, cq96.r], writes=[cr1.r])
                        fw.op("act", lambda e, w=w: e.activation(out=midt.t[:, 0:w], in_=cr1.t[:, 0:w], func=AF.Copy),
                              reads=[cr1.r], writes=[midt.r])
                        fw.op("pool", lambda e, w=w: e.tensor_copy(out=cq96.t[32:40, 0:w], in_=midt.t[:, 0:w]),
                              reads=[midt.r], writes=[cq96.r])
                        fw.op("dve", lambda e, w=w: e.tensor_tensor(out=cr2.t[:, 0:w], in0=cr1.t[:, 0:w],
                                                                    in1=midt.t[:, 0:w], op=ALU.subtract),
                              reads=[cr1.r, midt.r], writes=[cr2.r])
                        fw.op("act", lambda e, w=w: e.activation(out=cq96.t[64:72, 0:w], in_=cr2.t[:, 0:w], func=AF.Copy),
                              reads=[cr2.r], writes=[cq96.r])
                        a1_idx[0] += 1
                        if a1_idx[0] < len(a1_tiles):
                            a1_norm(a1_idx[0])
                        kb_first_tile = kt0 // 128
                        nkb = kb_first_tile + nb
                        pending_epi = []
                        for h in range(8):
                            hr = slice((h % 2) * 64, (h % 2) * 64 + 64)
                            hp = h // 2
                            ob = banks[6 + (h % 2)]
                            blocks = []
                            for kb in range(nkb):
                                if kb < kb_first_tile:
                                    q0, rows, diag = 0, 128, False
                                else:
                                    q0, rows, diag = (kb - kb_first_tile) * 128, pw, True
                                blocks.append((kb, q0, rows, diag))
                            sbanks = {}
                            ptl = {}

                            def emit_s(bi, h=h, hr=hr, hp=hp):
                                kb, q0, rows, diag = blocks[bi]
                                sb = next_bank(pool=(0, 1, 2, 3))
                                sbanks[bi] = sb
                                fw.op("pe", lambda e, sb=sb, kb=kb, q0=q0, rows=rows: e.matmul(
                                    sb.t[0:rows, q0:w], lhsT=KT.t[hr, hp, kb * 128:kb * 128 + rows],
                                    rhs=QT.t[hr, hp, q0:w], start=True, stop=False),
                                    reads=[KT.rs[kb], QT.rs[hp]], writes=[sb.r], signal=False)
                                fw.op("pe", lambda e, sb=sb, q0=q0, rows=rows: e.matmul(
                                    sb.t[0:rows, q0:w], lhsT=selh.t[0:72, h, 0:rows], rhs=cq96.t[0:72, q0:w],
                                    start=False, stop=(not diag)),
                                    reads=[selh.r, cq96.r], writes=[sb.r], signal=(not diag))
                                if diag:
                                    fw.op("pe", lambda e, sb=sb, q0=q0, rows=rows: e.matmul(
                                        sb.t[0:rows, q0:q0 + rows], lhsT=identb.t[0:rows, 0:rows],
                                        rhs=maskneg.t[0:rows, 0:rows], start=False, stop=True),
                                        reads=[identb.r, maskneg.r], writes=[sb.r])

                            def emit_pv(bi, h=h, ob=ob):
                                nonlocal ipt
                                kb, q0, rows, diag = blocks[bi]
                                sb = sbanks.pop(bi)
                                pt = pts[ipt % 4]
                                ipt += 1
                                fw.op("act", lambda e, sb=sb, pt=pt, kb=kb, q0=q0, rows=rows: e.activation(
                                    out=pt.t[0:rows, q0:w], in_=sb.t[0:rows, q0:w], func=AF.Exp,
                                    bias=negc.t[0:rows, kb, h:h + 1]),
                                    reads=[sb.r, negc.rs[kb]], writes=[pt.r])
                                last = (bi == len(blocks) - 1)
                                fw.op("pe", lambda e, pt=pt, kb=kb, q0=q0, rows=rows, bi=bi, last=last: e.matmul(
                                    ob.t[0:65, q0:w], lhsT=Vx.t[0:rows, kb, h, :], rhs=pt.t[0:rows, q0:w],
                                    start=(bi == 0), stop=last),
                                    reads=[Vx.rs[kb], pt.r], writes=[ob.r], signal=last)
                            LOOK = 2
                            nbk = len(blocks)
                            for bi in range(min(LOOK, nbk)):
                                emit_s(bi)
                            for bi in range(nbk):
                                emit_pv(bi)
                                if bi + LOOK < nbk:
                                    emit_s(bi + LOOK)
                            def epilogue(ob=ob, hr=hr, hp=hp):
                                fw.op("act", lambda e, ob=ob, w=w: e.activation(
                                    out=rsf.t[64:65, 0:w], in_=ob.t[64:65, 0:w], func=AF.Copy), reads=[ob.r], writes=[rsf.r])
                                pr = next_bank(pool=(4, 5))
                                fw.op("pe", lambda e, pr=pr, w=w: e.matmul(
                                    pr.t[0:64, 0:w], lhsT=onesf.t[64:65, 0:64], rhs=rsf.t[64:65, 0:w], start=True, stop=True),
                                    reads=[onesf.r, rsf.r], writes=[pr.r])
                                fw.op("dve", lambda e, pr=pr, w=w: e.reciprocal(out=rcp.t[:, 0:w], in_=pr.t[0:64, 0:w]),
                                      reads=[pr.r], writes=[rcp.r])
                                fw.op("dve", lambda e, ob=ob, hr=hr, hp=hp, w=w: e.tensor_tensor(
                                    out=yfT.t[hr, hp, 0:w], in0=ob.t[0:64, 0:w], in1=rcp.t[:, 0:w], op=ALU.mult),
                                    reads=[ob.r, rcp.r], writes=[yfT.rs[hp]])
                            if pending_epi:
                                pending_epi.pop()()
                            pending_epi.append(epilogue)
                        while pending_epi:
                            pending_epi.pop()()
                        fw.dma("sp", yfox.rearrange("(c p) t -> p c t", p=128)[:, :, t0:t0 + w], yfT.t[:, :, 0:w],
                               reads=yfT.rs, writes=[yfreg(t0)], stream="yfst")
                    for (t0, w, j) in tiles_of(s, WA):
                        xT = xTa[it % 2]
                        it += 1
                        a1_tile(s, t0, w, j, xT, off, kbase, dK, dV, dF)
                fw.flush()
                default_pool[0] = (0, 1, 2, 3, 4, 5, 6, 7)
            if stage >= 2:
              with contextlib.ExitStack() as ph:
                sub = FWScope(fw, ph)
                default_pool[0] = (0, 1, 2, 3, 4)
                sink = [fw]
                WA = 256
                RW0, HG0 = 0, RW_COLS + FOX_COLS
                NWI = RW_COLS + HG_COLS
                wi = Tt(sub.sbuf("wi", [128, 8, NWI], BF16), name="wi")
                wo = Tt(sub.sbuf("wo", [128, 8, D], BF16), name="wo")
                for kc in range(8):
                    fw.dma("pool", wi.t[:, kc, 0:RW_COLS], I["w_in"][l][kc * 128:(kc + 1) * 128, 0:RW_COLS],
                           writes=[wi.r], stream="wld", group=True)
                    fw.dma("pool", wi.t[:, kc, RW_COLS:NWI], I["w_in"][l][kc * 128:(kc + 1) * 128, HG0:HG0 + HG_COLS],
                           writes=[wi.r], stream="wld", group=True)
                    fw.dma("pool", wo.t[:, kc, :], I["w_out"][l][kc * 128:(kc + 1) * 128, :], writes=[wo.r], stream="wld", group=True)
                xTa = [Tt(sub.sbuf("xTa%d" % i, [128, 8, WA], F32), nreg=8, name="xTa%d" % i) for i in range(2)]
                sq = Tt(sub.sbuf("sqa", [128, 8, WA], BF16), name="sqa")
                hT = Tt(sub.sbuf("hTa", [128, 8, WA], BF16), nreg=8, name="hTa")
                tmp = [Tt(sub.sbuf("tmpa%d" % i, [128, WA], F32), name="tmpa%d" % i) for i in range(2)]
                lnv = Tt(sub.sbuf("lnva", [128, WA], F32), name="lnva")
                rstd = Tt(sub.sbuf("rstda", [128, WA], F32), name="rstda")
                ymix = Tt(sub.sbuf("ymix", [128, 8, WA], BF16), nreg=8, name="ymix")
                fw.op("pool", lambda e: e.memset(ymix.t[:], 0.0), writes=ymix.rs)

                def S(name, shape=None, dt=F32, nreg=1):
                    return Tt(sub.sbuf(name, shape or [128, WA], dt), nreg=nreg, name=name)

                def proj_fm(c0, w, M=128):
                    pb = next_bank()
                    for kc in range(8):
                        sink[0].op("pe", lambda e, kc=kc: e.matmul(
                            pb.t[0:M, 0:w], lhsT=wi.t[:, kc, c0:c0 + M], rhs=hT.t[:, kc, 0:w],
                            start=(kc == 0), stop=(kc == 7)),
                            reads=[wi.r, hT.rs[kc]], writes=[pb.r], signal=(kc == 7))
                    return pb

                NCH = WA // 32
                hg_E = [S("hg_E%d" % i) for i in range(2)]
                hg_KK = [S("hg_KK%d" % i) for i in range(2)]
                hg_B = [S("hg_B%d" % i) for i in range(2)]
                hg_D = S("hg_D")
                hg_X = S("hg_X")
                hg_Q = S("hg_Q")
                hg_G = [S("hg_G%d" % i) for i in range(2)]
                hg_Qt = S("hg_Qt", [128, 2, WA], BF16, nreg=2)
                hg_Kh = S("hg_Kh", [128, 2, WA], BF16, nreg=2)
                hg_Ke = S("hg_Ke", [128, 2, WA], BF16, nreg=2)
                hg_ebl = S("hg_ebl", [128, 2, NCH], F32, nreg=2)
                hg_ebm = S("hg_ebm", [128, 2, NCH], F32, nreg=2)
                hg_Vh = S("hg_Vh", [128, WA // 128, 256], BF16, nreg=4)
                hg_KeT = S("hg_KeT", [128, WA // 128, 4, 256], BF16, nreg=4)
                hg_AT = S("hg_AT", [128, WA // 128, 2, 2, 128], BF16, nreg=4)
                hg_Sm = S("hg_Sm", [128, 2, 128], F32, nreg=2)
                hg_Sbd = S("hg_Sbd", [128, 2, 128], BF16, nreg=2)
                hg_sq = S("hg_sq", [128, WA], BF16)
                hg_t1 = S("hg_t1")

                def hg_init(s):
                    fw.op("pool", lambda e: e.memset(hg_Sm.t[:], 0.0), writes=hg_Sm.rs)
                    if s == 2:
                        for h in range(4):
                            hr = slice((h % 2) * 64, (h % 2) * 64 + 64)
                            fw.dma("sp", hg_Sm.t[hr, h // 2, (h % 2) * 64:(h % 2) * 64 + 64], I["shg"][l][h],
                                   writes=[hg_Sm.rs[h // 2]], stream="stld", group=True)

                def hg_final(s):
                    dst = O["hgp"][l][s] if s < 2 else O["hgs"][l]
                    for h in range(4):
                        hr = slice((h % 2) * 64, (h % 2) * 64 + 64)
                        fw.dma("sp", dst[h], hg_Sm.t[hr, h // 2, (h % 2) * 64:(h % 2) * 64 + 64],
                               reads=[hg_Sm.rs[h // 2]], stream="ststhg%d" % s, group=True)

                def hg_tile(s, w):
                    nch = w // 32
                    nb = (w + 127) // 128
                    pw = min(w, 128)
                    c_q, c_f, c_i, c_g = RW_COLS, RW_COLS + 256, RW_COLS + 512, RW_COLS + 768
                    for tb in range(nb):
                        pb = next_bank()
                        for kc in range(8):
                            sink[0].op("pe", lambda e, kc=kc, tb=tb, pb=pb: e.matmul(
                                pb.t[0:pw, 0:256], lhsT=hT.t[:, kc, tb * 128:tb * 128 + pw], rhs=wi.t[:, kc, c_i:c_i + 256],
                                start=(kc == 0), stop=(kc == 7)),
                                reads=[wi.r, hT.rs[kc]], writes=[pb.r], signal=(kc == 7))
                        sink[0].op("act", lambda e, tb=tb, pb=pb: e.activation(
                            out=hg_Vh.t[0:pw, tb, :], in_=pb.t[0:pw, 0:256], func=AF.Copy),
                            reads=[pb.r], writes=[hg_Vh.rs[tb]])
                    for pc in range(2):
                        E, KK, B, G = hg_E[pc], hg_KK[pc], hg_B[pc], hg_G[pc]
                        lb_ap = lbT.t[:, l, pc:pc + 1]
                        oml_ap = omlT.t[:, l, pc:pc + 1]
                        noml_ap = nomlT.t[:, l, pc:pc + 1]
                        pf = proj_fm(c_f + pc * 128, w)
                        sink[0].op("act", lambda e, pf=pf, E=E: e.activation(out=E.t[:, 0:w], in_=pf.t[:, 0:w], func=AF.Exp,
                                                                      scale=-1.0), reads=[pf.r], writes=[E.r])
                        sink[0].op("dve", lambda e, E=E: e.tensor_scalar_add(out=E.t[:, 0:w], in0=E.t[:, 0:w], scalar1=1.0),
                              reads=[E.r], writes=[E.r])
                        sink[0].op("dve", lambda e, E=E: e.reciprocal(out=E.t[:, 0:w], in_=E.t[:, 0:w]),
                              reads=[E.r], writes=[E.r])
                        sink[0].op("dve", lambda e, E=E, KK=KK, noml_ap=noml_ap, oml_ap=oml_ap: e.tensor_scalar(
                            out=KK.t[:, 0:w], in0=E.t[:, 0:w], scalar1=noml_ap, scalar2=oml_ap, op0=ALU.mult, op1=ALU.add),
                            reads=[E.r, nomlT.r, omlT.r], writes=[KK.r])
                        sink[0].op("act", lambda e, E=E, oml_ap=oml_ap, lb_ap=lb_ap: e.activation(
                            out=E.t[:, 0:w], in_=E.t[:, 0:w], func=AF.Ln, scale=oml_ap, bias=lb_ap),
                            reads=[E.r, omlT.r, lbT.r], writes=[E.r])
                        sink[0].op("dve", lambda e, E=E, B=B: e.tensor_tensor_scan(
                            out=B.t[:, 0:w], data0=rmask32.t[:, 0:w], data1=E.t[:, 0:w], initial=0.0,
                            op0=ALU.mult, op1=ALU.add), reads=[E.r, rmask32.r], writes=[B.r])
                        Bv = B.t[:, 0:w].rearrange("p (c t) -> p c t", t=32)
                        Dv = hg_D.t[:, 0:w].rearrange("p (c t) -> p c t", t=32)
                        sink[0].op("dve", lambda e, Bv=Bv, Dv=Dv: e.tensor_tensor(
                            out=Dv, in0=Bv, in1=Bv[:, :, 15:16].to_broadcast([128, nch, 32]), op=ALU.subtract),
                            reads=[B.r], writes=[hg_D.r])
                        pq = proj_fm(c_q + pc * 128, w)
                        sink[0].op("act", lambda e, pq=pq: e.activation(out=hg_Q.t[:, 0:w], in_=pq.t[:, 0:w], func=AF.Copy),
                              reads=[pq.r], writes=[hg_Q.r])
                        sink[0].op("act", lambda e: e.activation(out=hg_X.t[:, 0:w], in_=hg_D.t[:, 0:w], func=AF.Exp),
                              reads=[hg_D.r], writes=[hg_X.r])
                        sink[0].op("dve", lambda e, pc=pc: e.tensor_tensor(out=hg_Qt.t[:, pc, 0:w], in0=hg_Q.t[:, 0:w],
                                                                      in1=hg_X.t[:, 0:w], op=ALU.mult),
                              reads=[hg_Q.r, hg_X.r], writes=[hg_Qt.rs[pc]])
                        sink[0].op("act", lambda e: e.activation(out=hg_X.t[:, 0:w], in_=hg_D.t[:, 0:w], func=AF.Exp, scale=-1.0),
                              reads=[hg_D.r], writes=[hg_X.r])
                        sink[0].op("dve", lambda e, pc=pc, KK=KK: e.tensor_tensor(out=hg_Kh.t[:, pc, 0:w], in0=KK.t[:, 0:w],
                                                                             in1=hg_X.t[:, 0:w], op=ALU.mult),
                              reads=[KK.r, hg_X.r], writes=[hg_Kh.rs[pc]])
                        sink[0].op("dve", lambda e, Bv=Bv, Dv=Dv: e.tensor_tensor(
                            out=Dv, in0=Bv[:, :, 31:32].to_broadcast([128, nch, 32]), in1=Bv, op=ALU.subtract),
                            reads=[B.r], writes=[hg_D.r])
                        sink[0].op("act", lambda e: e.activation(out=hg_X.t[:, 0:w], in_=hg_D.t[:, 0:w], func=AF.Exp),
                              reads=[hg_D.r], writes=[hg_X.r])
                        sink[0].op("dve", lambda e, pc=pc, KK=KK: e.tensor_tensor(out=hg_Ke.t[:, pc, 0:w], in0=KK.t[:, 0:w],
                                                                             in1=hg_X.t[:, 0:w], op=ALU.mult),
                              reads=[KK.r, hg_X.r], writes=[hg_Ke.rs[pc]])
                        sink[0].op("act", lambda e, pc=pc, Bv=Bv: e.activation(out=hg_ebl.t[:, pc, 0:nch], in_=Bv[:, :, 31],
                                                                          func=AF.Exp), reads=[B.r], writes=[hg_ebl.rs[pc]])
                        sink[0].op("act", lambda e, pc=pc, Bv=Bv: e.activation(out=hg_ebm.t[:, pc, 0:nch], in_=Bv[:, :, 15],
                                                                          func=AF.Exp), reads=[B.r], writes=[hg_ebm.rs[pc]])
                        pg = proj_fm(c_g + pc * 128, w)
                        sink[0].op("act", lambda e, pg=pg, G=G: e.activation(out=G.t[:, 0:w], in_=pg.t[:, 0:w], func=AF.Silu),
                              reads=[pg.r], writes=[G.r])
                    HGDBG = int(os.environ.get("HGDBG", "9"))
                    if HGDBG < 2:
                        return
                    for tb in range(nb):
                        pAs = [next_bank(), next_bank()]
                        for h in range(4):
                            hr = slice((h % 2) * 64, (h % 2) * 64 + 64)
                            pA = pAs[h % 2]
                            sink[0].op("pe", lambda e, h=h, hr=hr, tb=tb, pA=pA: e.matmul(
                                pA.t[0:pw, (h // 2) * 128:(h // 2) * 128 + pw], lhsT=hg_Kh.t[hr, h // 2, tb * 128:tb * 128 + pw],
                                rhs=hg_Qt.t[hr, h // 2, tb * 128:tb * 128 + pw], start=True, stop=True),
                                reads=[hg_Kh.rs[h // 2], hg_Qt.rs[h // 2]], writes=[pA.r], signal=(h >= 2))
                        for par in range(2):
                            pA = pAs[par]
                            sink[0].op("dve", lambda e, tb=tb, pA=pA, par=par: e.tensor_tensor(
                                out=hg_AT.t[0:pw, tb, par, :, 0:pw],
                                in0=pA.t[0:pw, 0:256].rearrange("p (h t) -> p h t", h=2)[:, :, 0:pw],
                                in1=maskbd.t[0:pw, 0:pw].unsqueeze(1).to_broadcast([pw, 2, pw]), op=ALU.mult),
                                reads=[pA.r, maskbd.r], writes=[hg_AT.rs[tb]])
                        if os.environ.get("HGSUB", "") == "A":
                            continue
                        pT = next_bank()
                        pTb = pT.t[:, :].bitcast(BF16)
                        for pc in range(2):
                            sink[0].op("pe", lambda e, pc=pc, tb=tb, pTb=pTb: e.transpose(
                                pTb[0:pw, pc * 128:(pc + 1) * 128], hg_Ke.t[:, pc, tb * 128:tb * 128 + pw], identb.t[:, :]),
                                reads=[hg_Ke.rs[pc], identb.r], writes=[pT.r], signal=(pc == 1))
                        for cc in range(min(4, nch - tb * 4)):
                            sink[0].op("act", lambda e, tb=tb, pTb=pTb, cc=cc: e.activation(
                                out=hg_KeT.t[0:pw, tb, cc, :], in_=pTb[0:pw, 0:256], func=AF.Identity,
                                scale=maskbd.t[0:pw, cc * 32 + 31:cc * 32 + 32]),
                                reads=[pT.r, maskbd.r], writes=[hg_KeT.rs[tb]])
                    if HGDBG < 3:
                        return
                    po = [banks[6], banks[7]]
                    for tb in range(nb):
                        for h in range(4):
                            sink[0].op("pe", lambda e, h=h, tb=tb: e.matmul(
                                po[h // 2].t[(h % 2) * 64:(h % 2) * 64 + 64, tb * 128:tb * 128 + pw],
                                lhsT=hg_Vh.t[0:pw, tb, h * 64:(h + 1) * 64], rhs=hg_AT.t[0:pw, tb, h % 2, h // 2, 0:pw],
                                start=True, stop=False, skip_group_check=True),
                                reads=[hg_Vh.rs[tb], hg_AT.rs[tb]], writes=[po[h // 2].r], signal=False)
                        for cc in range(min(4, nch - tb * 4)):
                            c = tb * 4 + cc
                            last = (c == nch - 1)
                            for pc in range(2):
                                sink[0].op("act", lambda e, pc=pc, c=c: e.activation(
                                    out=hg_Sbd.t[:, pc, :], in_=hg_Sm.t[:, pc, :], func=AF.Identity,
                                    scale=hg_ebm.t[:, pc, c:c + 1]),
                                    reads=[hg_Sm.rs[pc], hg_ebm.rs[pc]], writes=[hg_Sbd.rs[pc]])
                                sink[0].op("pe", lambda e, pc=pc, c=c: e.matmul(
                                    po[pc].t[:, c * 32:(c + 1) * 32], lhsT=hg_Sbd.t[:, pc, :],
                                    rhs=hg_Qt.t[:, pc, c * 32:(c + 1) * 32], start=False, stop=True,
                                    skip_group_check=True),
                                    reads=[hg_Sbd.rs[pc], hg_Qt.rs[pc]], writes=[po[pc].r], signal=last)
                                pS = next_bank()
                                sink[0].op("pe", lambda e, pc=pc, tb=tb, cc=cc, pS=pS: e.matmul(
                                    pS.t[:, 0:128], lhsT=hg_KeT.t[0:pw, tb, cc, pc * 128:(pc + 1) * 128],
                                    rhs=hg_Vh.t[0:pw, tb, pc * 128:(pc + 1) * 128], start=True, stop=True),
                                    reads=[hg_KeT.rs[tb], hg_Vh.rs[tb]], writes=[pS.r])
                                for hh in range(2):
                                    hr = slice(hh * 64, hh * 64 + 64)
                                    sink[0].op("dve", lambda e, pc=pc, c=c, hr=hr, pS=pS: e.scalar_tensor_tensor(
                                        out=hg_Sm.t[hr, pc, hr], in0=hg_Sm.t[hr, pc, hr], scalar=hg_ebl.t[hr, pc, c:c + 1],
                                        in1=pS.t[hr, hr], op0=ALU.mult, op1=ALU.add),
                                        reads=[hg_Sm.rs[pc], hg_ebl.rs[pc], pS.r], writes=[hg_Sm.rs[pc]])
                    if HGDBG < 4:
                        return
                    for pc in range(2):
                        G = hg_G[pc]
                        sink[0].op("act", lambda e, pc=pc: e.activation(out=hg_sq.t[:, 0:w], in_=po[pc].t[:, 0:w], func=AF.Square),
                              reads=[po[pc].r], writes=[hg_sq.r])
                        pn = next_bank()
                        sink[0].op("pe", lambda e, pn=pn: e.matmul(pn.t[:, 0:w], lhsT=onesbd.t[:], rhs=hg_sq.t[:, 0:w],
                                                              start=True, stop=True),
                              reads=[onesbd.r, hg_sq.r], writes=[pn.r])
                        sink[0].op("act", lambda e, pn=pn: e.activation(out=hg_X.t[:, 0:w], in_=pn.t[:, 0:w], func=AF.Ln,
                                                                   scale=1.0 / 64, bias=epsb.t[:]),
                              reads=[pn.r, epsb.r], writes=[hg_X.r])
                        sink[0].op("act", lambda e: e.activation(out=hg_X.t[:, 0:w], in_=hg_X.t[:, 0:w], func=AF.Exp, scale=-0.5),
                              reads=[hg_X.r], writes=[hg_X.r])
                        sink[0].op("dve", lambda e, pc=pc: e.tensor_tensor(out=hg_t1.t[:, 0:w], in0=po[pc].t[:, 0:w],
                                                                      in1=hg_X.t[:, 0:w], op=ALU.mult),
                              reads=[po[pc].r, hg_X.r], writes=[hg_t1.r])
                        sink[0].op("dve", lambda e, pc=pc, G=G: e.scalar_tensor_tensor(
                            out=ymix.t[:, 6 + pc, 0:w], in0=hg_t1.t[:, 0:w], scalar=hgng.t[:, l, pc:pc + 1], in1=G.t[:, 0:w],
                            op0=ALU.mult, op1=ALU.mult), reads=[hg_t1.r, hgng.r, G.r], writes=[ymix.rs[6 + pc]])

                C0 = 0.6065306597126334
                rw_Pb = S("rw_Pb", [128, 9, WA + 1], F32)
                rw_car = S("rw_car", [128, 9], F32)
                rw_c7 = S("rw_c7", [128, 7], F32)
                rw_tmp = [S("rw_tmp%d" % i) for i in range(6)]
                rw_tmpB = [S("rw_tmpB%d" % i) for i in range(3)]
                rw_SIG2 = [S("rw_SIG%d" % i) for i in range(2)]
                rw_A2 = [S("rw_A%d" % i) for i in range(2)]
                rw_L2 = [S("rw_L%d" % i) for i in range(2)]
                rw_KP2 = [S("rw_KP%d" % i) for i in range(2)]
                rw_KN2 = [S("rw_KN%d" % i) for i in range(2)]
                rw_Bf2 = [S("rw_Bf%d" % i) for i in range(2)]
                rw_sqb2 = [S("rw_sqbb%d" % i, [128, WA], BF16) for i in range(2)]
                rw_G = [S("rw_G%d" % i) for i in range(2)]
                rw_Yf = S("rw_Yf")
                rw_TW = S("rw_TW", [32, WA], BF16)
                rw_AL = S("rw_AL", [32, WA], BF16)
                rw_SG = S("rw_SG", [64, WA], BF16)
                rw_sqb = S("rw_sqb", [128, WA], BF16)
                rw_Kh = S("rw_Kh", [128, 2, WA], BF16, nreg=2)
                rw_Bm = S("rw_Bm", [128, 2, 2, WA], BF16, nreg=2)
                rw_QRm = S("rw_QRm", [128, 2, 2, max(1, WA // 64), 2, 64], BF16, nreg=2)
                rw_Ke = S("rw_Ke", [128, 2, WA], BF16, nreg=2)
                rw_Be = S("rw_Be", [128, 2, WA], BF16, nreg=2)
                rw_Vb = S("rw_Vb", [128, 2, WA], BF16, nreg=2)
                NCR = max(1, WA // 64)
                rw_Vt = S("rw_Vt", [64, NCR, 256], BF16)
                rw_KeT = S("rw_KeT", [64, NCR, 256], BF16)
                rw_BeT = S("rw_BeT", [64, NCR, 256], BF16)
                rw_gC = S("rw_gC", [128, 2, NCR], F32, nreg=2)
                NCK = max(1, WA // 64)
                rw_AT12 = [S("rw_AT12_%d" % i, [64, 4, 2, 64], BF16) for i in range(NCK)]
                rw_AT34 = [S("rw_AT34_%d" % i, [64, 4, 2, 64], BF16) for i in range(NCK)]
                rw_X = [[S("rw_X%d_%d" % (i, j), [64, 4, 64], BF16) for j in range(2)] for i in range(NCK)]
                rw_XT = [[S("rw_XT%d_%d" % (i, j), [64, 4, 64], BF16) for j in range(2)] for i in range(NCK)]
                rw_TT = [[S("rw_TT%d_%d" % (i, j), [64, 4, 64], BF16) for j in range(2)] for i in range(NCK)]
                rw_Zb = S("rw_Zb", [64, 256], BF16)
                rw_Un = S("rw_Un", [64, 256], BF16)
                rw_Hm = S("rw_Hm", [128, 2, 128], F32, nreg=2)
                rw_Hbd = S("rw_Hbd", [128, 2, 128], BF16, nreg=2)
                rw_st = S("rw_st", [128, 2, 128], F32)

                def rw_init(s):
                    fw.op("pool", lambda e: e.memset(rw_Hm.t[:], 0.0), writes=rw_Hm.rs)
                    fw.op("pool", lambda e: e.memset(rw_Pb.t[:], 0.0), writes=[rw_Pb.r])
                    if s == 2:
                        fw.op("pool", lambda e: e.memset(rw_st.t[:], 0.0), writes=[rw_st.r])
                        for h in range(4):
                            hr = slice((h % 2) * 64, (h % 2) * 64 + 64)
                            fw.dma("sp", rw_st.t[hr, h // 2, (h % 2) * 64:(h % 2) * 64 + 64], I["srw"][l][h],
                                   writes=[rw_st.r], stream="stld", group=True)
                        for pc in range(2):
                            pb = next_bank()
                            fw.op("pe", lambda e, pc=pc, pb=pb: e.transpose(pb.t[:, 0:128], rw_st.t[:, pc, :], ident.t[:]),
                                  reads=[rw_st.r, ident.r], writes=[pb.r])
                            fw.op("dve", lambda e, pc=pc, pb=pb: e.tensor_copy(out=rw_Hm.t[:, pc, :], in_=pb.t[:, 0:128]),
                                  reads=[pb.r], writes=[rw_Hm.rs[pc]])
                        fw.dma("sp", rw_c7.t[:, :], I["ssh"][l].rearrange("(c p) -> p c", p=128),
                               writes=[rw_c7.r], stream="stld", group=True, allow_slow_non_contiguous=True)
                        fw.op("dve", lambda e: e.tensor_copy(out=rw_Pb.t[:, 0:6, 0], in_=rw_c7.t[:, 0:6]),
                              reads=[rw_c7.r], writes=[rw_Pb.r])
                        fw.op("dve", lambda e: e.tensor_copy(out=rw_Pb.t[0:32, 6, 0:1], in_=rw_c7.t[0:32, 6:7]),
                              reads=[rw_c7.r], writes=[rw_Pb.r])
                        fw.op("dve", lambda e: e.tensor_copy(out=rw_Pb.t[0:32, 7, 0:1], in_=rw_c7.t[32:64, 6:7]),
                              reads=[rw_c7.r], writes=[rw_Pb.r])
                        fw.op("dve", lambda e: e.tensor_copy(out=rw_Pb.t[0:64, 8, 0:1], in_=rw_c7.t[64:128, 6:7]),
                              reads=[rw_c7.r], writes=[rw_Pb.r])
                    for pc in range(2):
                        fw.op("act", lambda e, pc=pc: e.activation(out=rw_Hbd.t[:, pc, :], in_=rw_Hm.t[:, pc, :],
                                                                   func=AF.Copy), reads=[rw_Hm.rs[pc]], writes=[rw_Hbd.rs[pc]])

                def rw_final(s, w):
                    dst = O["rwp"][l][s] if s < 2 else O["rws"][l]
                    for pc in range(2):
                        pb = next_bank()
                        fw.op("pe", lambda e, pc=pc, pb=pb: e.transpose(pb.t[:, 0:128], rw_Hm.t[:, pc, :], ident.t[:]),
                              reads=[rw_Hm.rs[pc], ident.r], writes=[pb.r])
                        fw.op("dve", lambda e, pc=pc, pb=pb: e.tensor_copy(out=rw_st.t[:, pc, :], in_=pb.t[:, 0:128]),
                              reads=[pb.r], writes=[rw_st.r])
                    for h in range(4):
                        hr = slice((h % 2) * 64, (h % 2) * 64 + 64)
                        fw.dma("sp", dst[h], rw_st.t[hr, h // 2, (h % 2) * 64:(h % 2) * 64 + 64], reads=[rw_st.r],
                               stream="ststrs%d" % s, group=True)
                    dsh = O["rshp"][l][s] if s < 2 else O["rshs"][l]
                    fw.op("dve", lambda e: e.tensor_copy(out=rw_c7.t[:, 0:6], in_=rw_car.t[:, 0:6]), reads=[rw_car.r], writes=[rw_c7.r])
                    fw.op("dve", lambda e: e.tensor_copy(out=rw_c7.t[0:32, 6:7], in_=rw_car.t[0:32, 6:7]), reads=[rw_car.r], writes=[rw_c7.r])
                    fw.op("dve", lambda e: e.tensor_copy(out=rw_c7.t[32:64, 6:7], in_=rw_car.t[0:32, 7:8]), reads=[rw_car.r], writes=[rw_c7.r])
                    fw.op("dve", lambda e: e.tensor_copy(out=rw_c7.t[64:128, 6:7], in_=rw_car.t[0:64, 8:9]), reads=[rw_car.r], writes=[rw_c7.r])
                    fw.dma("sp", dsh.rearrange("(c p) -> p c", p=128), rw_c7.t[:, :], reads=[rw_c7.r],
                           stream="ststrc%d" % s, group=True, allow_slow_non_contiguous=True)

                def rw_tile(s, w):
                    C = min(64, w)
                    nch = w // C
                    nlev = {64: 5, 32: 4}[C]
                    rmask = rmask64 if C == 64 else rmask32
                    T0, T1, T2, T3, T4, T5 = rw_tmp
                    specs = [(i, i * 128, 128) for i in range(6)] + [(6, 768, 32), (7, 800, 32), (8, 832, 64)]
                    for (i, c0, M) in specs:
                        pb = proj_fm(c0, w, M)
                        sink[0].op("act", lambda e, i=i, M=M, pb=pb: e.activation(out=rw_Pb.t[0:M, i, 1:1 + w], in_=pb.t[0:M, 0:w],
                                                                          func=AF.Copy), reads=[pb.r], writes=[rw_Pb.r])
                    sink[0].op("act", lambda e: e.activation(out=rw_car.t[:, :], in_=rw_Pb.t[:, :, w], func=AF.Copy),
                          reads=[rw_Pb.r], writes=[rw_car.r])
                    for (i, c0, M) in specs:
                        sink[0].op("dve", lambda e, i=i, M=M: e.tensor_tensor(out=T0.t[0:M, 0:w], in0=rw_Pb.t[0:M, i, 0:w],
                                                                         in1=rw_Pb.t[0:M, i, 1:1 + w], op=ALU.subtract),
                              reads=[rw_Pb.r], writes=[T0.r])
                        sink[0].op("dve", lambda e, i=i, M=M: e.scalar_tensor_tensor(
                            out=rw_Pb.t[0:M, i, 1:1 + w], in0=T0.t[0:M, 0:w], scalar=mul.t[0:M, l, i:i + 1],
                            in1=rw_Pb.t[0:M, i, 1:1 + w], op0=ALU.mult, op1=ALU.add),
                            reads=[T0.r, mul.r, rw_Pb.r], writes=[rw_Pb.r])
                    sink[0].op("pool", lambda e: e.tensor_copy(out=rw_Pb.t[:, :, 0], in_=rw_car.t[:, :]),
                          reads=[rw_car.r, rw_Pb.r], writes=[rw_Pb.r])
                    XS = lambda i, M=128: rw_Pb.t[0:M, i, 1:1 + w]
                    sink[0].op("act", lambda e: e.activation(out=rw_TW.t[:, 0:w], in_=XS(6, 32), func=AF.Tanh),
                          reads=[rw_Pb.r], writes=[rw_TW.r])
                    sink[0].op("act", lambda e: e.activation(out=rw_AL.t[:, 0:w], in_=XS(7, 32), func=AF.Copy),
                          reads=[rw_Pb.r], writes=[rw_AL.r])
                    sink[0].op("act", lambda e: e.activation(out=T0.t[0:64, 0:w], in_=XS(8, 64), func=AF.Exp, scale=-1.0),
                          reads=[rw_Pb.r], writes=[T0.r])
                    sink[0].op("dve", lambda e: e.tensor_scalar_add(out=T0.t[0:64, 0:w], in0=T0.t[0:64, 0:w], scalar1=1.0),
                          reads=[T0.r], writes=[T0.r])
                    sink[0].op("dve", lambda e: e.reciprocal(out=T0.t[0:64, 0:w], in_=T0.t[0:64, 0:w]), reads=[T0.r], writes=[T0.r])
                    sink[0].op("act", lambda e: e.activation(out=rw_SG.t[:, 0:w], in_=T0.t[0:64, 0:w], func=AF.Copy),
                          reads=[T0.r], writes=[rw_SG.r])
                    outer_sink = sink[0]
                    pc_recs = [Rec(), Rec()]

                    def _pc_body(pc, T1, T2, T3, rw_SIG, rw_A, rw_L, rw_KP, rw_KN, rw_Bf, rw_sqb):
                        cs = slice(pc * 128, (pc + 1) * 128)
                        r_ap, k_ap, v_ap = XS(pc), XS(2 + pc), XS(4 + pc)
                        pw_ = next_bank()
                        sink[0].op("pe", lambda e, pw_=pw_, cs=cs: e.matmul(pw_.t[:, 0:w], lhsT=w2b.t[:, l, cs], rhs=rw_TW.t[:, 0:w],
                                                                    start=True, stop=True), reads=[w2b.r, rw_TW.r], writes=[pw_.r])
                        sink[0].op("act", lambda e, pw_=pw_, pc=pc: e.activation(out=rw_SIG.t[:, 0:w], in_=pw_.t[:, 0:w], func=AF.Exp,
                                                                         scale=-1.0, bias=nw0.t[:, l, pc:pc + 1]),
                              reads=[pw_.r, nw0.r], writes=[rw_SIG.r])
                        sink[0].op("dve", lambda e: e.tensor_scalar_add(out=rw_SIG.t[:, 0:w], in0=rw_SIG.t[:, 0:w], scalar1=1.0),
                              reads=[rw_SIG.r], writes=[rw_SIG.r])
                        sink[0].op("dve", lambda e: e.reciprocal(out=rw_SIG.t[:, 0:w], in_=rw_SIG.t[:, 0:w]),
                              reads=[rw_SIG.r], writes=[rw_SIG.r])
                        pa_ = next_bank()
                        sink[0].op("pe", lambda e, pa_=pa_, cs=cs: e.matmul(pa_.t[:, 0:w], lhsT=a2b.t[:, l, cs], rhs=rw_AL.t[:, 0:w],
                                                                    start=True, stop=True), reads=[a2b.r, rw_AL.r], writes=[pa_.r])
                        sink[0].op("act", lambda e, pa_=pa_, pc=pc: e.activation(out=rw_A.t[:, 0:w], in_=pa_.t[:, 0:w], func=AF.Exp,
                                                                         scale=-1.0, bias=na0.t[:, l, pc:pc + 1]),
                              reads=[pa_.r, na0.r], writes=[rw_A.r])
                        sink[0].op("dve", lambda e: e.tensor_scalar_add(out=rw_A.t[:, 0:w], in0=rw_A.t[:, 0:w], scalar1=1.0),
                              reads=[rw_A.r], writes=[rw_A.r])
                        sink[0].op("dve", lambda e: e.reciprocal(out=rw_A.t[:, 0:w], in_=rw_A.t[:, 0:w]), reads=[rw_A.r], writes=[rw_A.r])
                        pg_ = next_bank()
                        sink[0].op("pe", lambda e, pg_=pg_, cs=cs: e.matmul(pg_.t[:, 0:w], lhsT=g2b.t[:, l, cs], rhs=rw_SG.t[:, 0:w],
                                                                    start=True, stop=True), reads=[g2b.r, rw_SG.r], writes=[pg_.r])
                        sink[0].op("act", lambda e, pg_=pg_, pc=pc: e.activation(out=rw_G[pc].t[:, 0:w], in_=pg_.t[:, 0:w], func=AF.Copy),
                              reads=[pg_.r], writes=[rw_G[pc].r])
                        sink[0].op("dve", lambda e, pc=pc, k_ap=k_ap: e.tensor_scalar_mul(
                            out=rw_KN.t[:, 0:w], in0=k_ap, scalar1=rwp["rw_k_k"].t[:, l, pc:pc + 1]),
                            reads=[rw_Pb.r, rwp["rw_k_k"].r], writes=[rw_KN.r])
                        sink[0].op("act", lambda e: e.activation(out=rw_sqb.t[:, 0:w], in_=rw_KN.t[:, 0:w], func=AF.Square),
                              reads=[rw_KN.r], writes=[rw_sqb.r])
                        pn = next_bank()
                        sink[0].op("pe", lambda e, pn=pn: e.matmul(pn.t[:, 0:w], lhsT=onesbd.t[:], rhs=rw_sqb.t[:, 0:w],
                                                              start=True, stop=True), reads=[onesbd.r, rw_sqb.r], writes=[pn.r])
                        sink[0].op("dve", lambda e, pn=pn: e.tensor_scalar_max(out=T1.t[:, 0:w], in0=pn.t[:, 0:w], scalar1=1e-24),
                              reads=[pn.r], writes=[T1.r])
                        sink[0].op("act", lambda e: e.activation(out=T1.t[:, 0:w], in_=T1.t[:, 0:w], func=AF.Ln), reads=[T1.r], writes=[T1.r])
                        sink[0].op("act", lambda e: e.activation(out=T1.t[:, 0:w], in_=T1.t[:, 0:w], func=AF.Exp, scale=-0.5),
                              reads=[T1.r], writes=[T1.r])
                        sink[0].op("dve", lambda e: e.tensor_tensor(out=rw_KN.t[:, 0:w], in0=rw_KN.t[:, 0:w], in1=T1.t[:, 0:w], op=ALU.mult),
                              reads=[rw_KN.r, T1.r], writes=[rw_KN.r])
                        sink[0].op("dve", lambda e, pc=pc: e.tensor_scalar(
                            out=T1.t[:, 0:w], in0=rw_A.t[:, 0:w], scalar1=rwp["rw_k_a"].t[:, l, pc:pc + 1],
                            scalar2=omka.t[:, l, pc:pc + 1], op0=ALU.mult, op1=ALU.add),
                            reads=[rw_A.r, rwp["rw_k_a"].r, omka.r], writes=[T1.r])
                        sink[0].op("dve", lambda e, k_ap=k_ap: e.tensor_tensor(out=rw_KP.t[:, 0:w], in0=k_ap, in1=T1.t[:, 0:w], op=ALU.mult),
                              reads=[rw_Pb.r, T1.r], writes=[rw_KP.r])
                        sink[0].op("dve", lambda e: e.tensor_tensor(out=rw_Bf.t[:, 0:w], in0=rw_KN.t[:, 0:w], in1=rw_A.t[:, 0:w], op=ALU.mult),
                              reads=[rw_KN.r, rw_A.r], writes=[rw_Bf.r])
                        sink[0].op("dve", lambda e: e.tensor_tensor_scan(out=rw_L.t[:, 0:w], data0=rmask.t[:, 0:w], data1=rw_SIG.t[:, 0:w],
                                                                   initial=0.0, op0=ALU.mult, op1=ALU.add),
                              reads=[rw_SIG.r, rmask.r], writes=[rw_L.r])
                        Lv = rw_L.t[:, 0:w].rearrange("p (c t) -> p c t", t=C)
                        sink[0].op("act", lambda e: e.activation(out=T2.t[:, 0:w], in_=rw_L.t[:, 0:w], func=AF.Exp, scale=-C0),
                              reads=[rw_L.r], writes=[T2.r])
                        sink[0].op("dve", lambda e: e.tensor_tensor(out=T3.t[:, 0:w], in0=rw_L.t[:, 0:w], in1=rw_SIG.t[:, 0:w], op=ALU.subtract),
                              reads=[rw_L.r, rw_SIG.r], writes=[T3.r])
                        sink[0].op("act", lambda e: e.activation(out=T3.t[:, 0:w], in_=T3.t[:, 0:w], func=AF.Exp, scale=-C0),
                              reads=[T3.r], writes=[T3.r])
                        for par in range(2):
                            hm = onesbdf.t[:, par * 64:par * 64 + 1]
                            sink[0].op("dve", lambda e, pc=pc, par=par, hm=hm, r_ap=r_ap: e.scalar_tensor_tensor(
                                out=rw_QRm.t[:, par, pc, 0:nch, 1, 0:C], in0=r_ap.rearrange("p (c t) -> p c t", t=C), scalar=hm,
                                in1=T2.t[:, 0:w].rearrange("p (c t) -> p c t", t=C), op0=ALU.mult, op1=ALU.mult),
                                reads=[rw_Pb.r, T2.r, onesbdf.r], writes=[rw_QRm.rs[pc]])
                            sink[0].op("dve", lambda e, pc=pc, par=par, hm=hm: e.scalar_tensor_tensor(
                                out=rw_QRm.t[:, par, pc, 0:nch, 0, 0:C], in0=rw_KN.t[:, 0:w].rearrange("p (c t) -> p c t", t=C),
                                scalar=hm, in1=T3.t[:, 0:w].rearrange("p (c t) -> p c t", t=C), op0=ALU.mult, op1=ALU.mult),
                                reads=[rw_KN.r, T3.r, onesbdf.r], writes=[rw_QRm.rs[pc]])
                        sink[0].op("act", lambda e: e.activation(out=T2.t[:, 0:w], in_=rw_L.t[:, 0:w], func=AF.Exp, scale=C0),
                              reads=[rw_L.r], writes=[T2.r])
                        sink[0].op("dve", lambda e, pc=pc: e.tensor_tensor(out=rw_Kh.t[:, pc, 0:w], in0=rw_KP.t[:, 0:w], in1=T2.t[:, 0:w], op=ALU.mult),
                              reads=[rw_KP.r, T2.r], writes=[rw_Kh.rs[pc]])
                        for par in range(2):
                            hm = onesbdf.t[:, par * 64:par * 64 + 1]
                            sink[0].op("dve", lambda e, pc=pc, par=par, hm=hm: e.scalar_tensor_tensor(
                                out=rw_Bm.t[:, par, pc, 0:w], in0=rw_Bf.t[:, 0:w], scalar=hm, in1=T2.t[:, 0:w],
                                op0=ALU.mult, op1=ALU.mult), reads=[rw_Bf.r, T2.r, onesbdf.r], writes=[rw_Bm.rs[pc]])
                        sink[0].op("dve", lambda e, Lv=Lv: e.tensor_tensor(
                            out=T3.t[:, 0:w].rearrange("p (c t) -> p c t", t=C), in0=Lv[:, :, C - 1:C].to_broadcast([128, nch, C]),
                            in1=Lv, op=ALU.subtract), reads=[rw_L.r], writes=[T3.r])
                        sink[0].op("act", lambda e: e.activation(out=T3.t[:, 0:w], in_=T3.t[:, 0:w], func=AF.Exp, scale=-C0),
                              reads=[T3.r], writes=[T3.r])
                        sink[0].op("dve", lambda e, pc=pc: e.tensor_tensor(out=rw_Ke.t[:, pc, 0:w], in0=rw_KP.t[:, 0:w], in1=T3.t[:, 0:w], op=ALU.mult),
                              reads=[rw_KP.r, T3.r], writes=[rw_Ke.rs[pc]])
                        sink[0].op("pool", lambda e, pc=pc: e.tensor_tensor(out=rw_Be.t[:, pc, 0:w], in0=rw_Bf.t[:, 0:w], in1=T3.t[:, 0:w], op=ALU.mult),
                              reads=[rw_Bf.r, T3.r], writes=[rw_Be.rs[pc]])
                        sink[0].op("act", lambda e, pc=pc, Lv=Lv: e.activation(out=rw_gC.t[:, pc, 0:nch], in_=Lv[:, :, C - 1], func=AF.Exp,
                                                                       scale=-C0), reads=[rw_L.r], writes=[rw_gC.rs[pc]])
                        sink[0].op("act", lambda e, pc=pc, v_ap=v_ap: e.activation(out=rw_Vb.t[:, pc, 0:w], in_=v_ap, func=AF.Copy),
                              reads=[rw_Pb.r], writes=[rw_Vb.rs[pc]])
                        sink[0].op("dve", lambda e, pc=pc, r_ap=r_ap: e.scalar_tensor_tensor(
                            out=(T4 if pc == 0 else T5).t[:, 0:w], in0=r_ap, scalar=rwp["rw_r_k"].t[:, l, pc:pc + 1],
                            in1=rw_KP.t[:, 0:w], op0=ALU.mult, op1=ALU.mult),
                            reads=[rw_Pb.r, rwp["rw_r_k"].r, rw_KP.r], writes=[(T4 if pc == 0 else T5).r])
                    for pc in range(2):
                        sink[0] = pc_recs[pc]
                        tt = (T1, T2, T3) if pc == 0 else tuple(rw_tmpB)
                        _pc_body(pc, tt[0], tt[1], tt[2], rw_SIG2[pc], rw_A2[pc], rw_L2[pc], rw_KP2[pc], rw_KN2[pc],
                                 rw_Bf2[pc], rw_sqb2[pc])
                    sink[0] = outer_sink
                    merge_recs(outer_sink, pc_recs)
                    RWDBG = int(os.environ.get("RWDBG", "9"))
                    if RWDBG < 2:
                        return
                    for (src, dstT) in ((rw_Vb, rw_Vt), (rw_Ke, rw_KeT), (rw_Be, rw_BeT)):
                        for c0 in range(0, nch, 4):
                            pT = next_bank()
                            pTb = pT.t[:, :].bitcast(BF16)
                            ncc = min(4, nch - c0)
                            for ci in range(ncc):
                                c = c0 + ci
                                for pc in range(2):
                                    sink[0].op("pe", lambda e, src=src, c=c, ci=ci, pc=pc, pTb=pTb: e.transpose(
                                        pTb[0:C, ci * 256 + pc * 128:ci * 256 + (pc + 1) * 128], src.t[:, pc, c * C:(c + 1) * C],
                                        identb.t[:, :]), reads=[src.rs[pc], identb.r], writes=[pT.r],
                                        signal=(ci == ncc - 1 and pc == 1))
                            sink[0].op("act", lambda e, dstT=dstT, c0=c0, ncc=ncc, pTb=pTb: e.activation(
                                out=dstT.t[0:C, c0:c0 + ncc, :], in_=pTb[0:C, 0:ncc * 256].rearrange("p (c n) -> p c n", n=256),
                                func=AF.Copy), reads=[pT.r], writes=[dstT.r])
                    if RWDBG < 3:
                        return
                    class _V:
                        pass
                    py = [_V(), _V()]
                    for pc_ in range(2):
                        py[pc_].t = banks[5].t[:, pc_ * WA:(pc_ + 1) * WA]
                        py[pc_].r = banks[5].r
                    v4 = lambda ap: ap.rearrange("p (h a t) -> p h a t", h=4, a=2)[:, :, :, 0:C]
                    v3 = lambda ap: ap.rearrange("p (h t) -> p h t", h=4)[:, :, 0:C]
                    for c in range(nch):
                        p12, p34, p5 = next_bank(), next_bank(), next_bank()
                        AT12, AT34 = rw_AT12[c], rw_AT34[c]
                        for h in range(4):
                            par, pc = h % 2, h // 2
                            for a_ in range(2):
                                qr = rw_QRm.t[:, par, pc, c, a_, 0:C]
                                sink[0].op("pe", lambda e, c=c, h=h, pc=pc, qr=qr, p12=p12, a_=a_: e.matmul(
                                    p12.t[0:C, h * 128 + a_ * 64:h * 128 + a_ * 64 + C], lhsT=rw_Kh.t[:, pc, c * C:(c + 1) * C],
                                    rhs=qr, start=True, stop=True), reads=[rw_Kh.rs[pc], rw_QRm.rs[pc]], writes=[p12.r],
                                    signal=(h == 3 and a_ == 1))
                                sink[0].op("pe", lambda e, c=c, h=h, par=par, pc=pc, qr=qr, p34=p34, a_=a_: e.matmul(
                                    p34.t[0:C, h * 128 + a_ * 64:h * 128 + a_ * 64 + C],
                                    lhsT=rw_Bm.t[:, par, pc, c * C:(c + 1) * C], rhs=qr, start=True, stop=True),
                                    reads=[rw_Bm.rs[pc], rw_QRm.rs[pc]], writes=[p34.r], signal=(h == 3 and a_ == 1))
                            sink[0].op("pe", lambda e, h=h, par=par, pc=pc, c=c, p5=p5: e.matmul(
                                p5.t[0:C, h * 64:h * 64 + C], lhsT=rw_QRm.t[:, par, pc, c, 0, 0:C],
                                rhs=rw_Bm.t[:, par, pc, c * C:(c + 1) * C], start=True, stop=True),
                                reads=[rw_Bm.rs[pc], rw_QRm.rs[pc]], writes=[p5.r], signal=(h == 3))
                        sink[0].op("dve", lambda e, p12=p12, AT12=AT12: e.tensor_tensor(
                            out=AT12.t[0:C, :, :, 0:C], in0=v4(p12.t[0:C, :]),
                            in1=mask12.t[0:C, :, 0:C].unsqueeze(1).to_broadcast([C, 4, 2, C]), op=ALU.mult),
                            reads=[p12.r, mask12.r], writes=[AT12.r])
                        sink[0].op("dve", lambda e, p34=p34, AT34=AT34: e.tensor_tensor(
                            out=AT34.t[0:C, :, :, 0:C], in0=v4(p34.t[0:C, :]),
                            in1=mask34.t[0:C, :, 0:C].unsqueeze(1).to_broadcast([C, 4, 2, C]), op=ALU.mult),
                            reads=[p34.r, mask34.r], writes=[AT34.r])
                        X, XT, TT = rw_X[c][0], rw_XT[c][0], rw_TT[c][0]
                        sink[0].op("dve", lambda e, p5=p5, X=X: e.tensor_tensor(
                            out=X.t[0:C, :, 0:C], in0=v3(p5.t[0:C, 0:256]),
                            in1=mask5.t[0:C, 0:C].unsqueeze(1).to_broadcast([C, 4, C]), op=ALU.mult),
                            reads=[p5.r, mask5.r], writes=[X.r])
                        sink[0].op("act", lambda e, XT=XT, AT34=AT34: e.activation(out=XT.t[0:C, :, 0:C], in_=AT34.t[0:C, :, 0, 0:C], func=AF.Copy),
                              reads=[AT34.r], writes=[XT.r])
                        sink[0].op("pool", lambda e, TT=TT, AT34=AT34: e.tensor_tensor(
                            out=TT.t[0:C, :, 0:C], in0=AT34.t[0:C, :, 0, 0:C],
                            in1=identb.t[0:C, 0:C].unsqueeze(1).to_broadcast([C, 4, C]), op=ALU.add),
                            reads=[AT34.r, identb.r], writes=[TT.r])
                    cur = 0
                    for lev in range(nlev):
                        for c in range(nch):
                            Xn, XTn, TTn = rw_X[c][1 - cur], rw_XT[c][1 - cur], rw_TT[c][1 - cur]
                            X, XT, TT = rw_X[c][cur], rw_XT[c][cur], rw_TT[c][cur]
                            px, pxt, ptt = next_bank(), next_bank(), next_bank()
                            for h in range(4):
                                sink[0].op("pe", lambda e, h=h, px=px, X=X, XT=XT: e.matmul(
                                    px.t[0:C, h * 64:h * 64 + C], lhsT=XT.t[0:C, h, 0:C], rhs=X.t[0:C, h, 0:C], start=True, stop=True),
                                    reads=[X.r, XT.r], writes=[px.r], signal=(h == 3))
                            sink[0].op("act", lambda e, px=px, Xn=Xn: e.activation(
                                out=Xn.t[0:C, :, 0:C], in_=v3(px.t[0:C, 0:256]), func=AF.Copy), reads=[px.r], writes=[Xn.r])
                            if lev < nlev - 1:
                                for h in range(4):
                                    sink[0].op("pe", lambda e, h=h, pxt=pxt, X=X, XT=XT: e.matmul(
                                        pxt.t[0:C, h * 64:h * 64 + C], lhsT=X.t[0:C, h, 0:C], rhs=XT.t[0:C, h, 0:C], start=True, stop=True),
                                        reads=[X.r, XT.r], writes=[pxt.r], signal=(h == 3))
                                sink[0].op("act" if c % 2 else "dve", (lambda e, pxt=pxt, XTn=XTn: e.activation(
                                    out=XTn.t[0:C, :, 0:C], in_=v3(pxt.t[0:C, 0:256]), func=AF.Copy)) if c % 2 else
                                    (lambda e, pxt=pxt, XTn=XTn: e.tensor_copy(out=XTn.t[0:C, :, 0:C], in_=v3(pxt.t[0:C, 0:256]))),
                                    reads=[pxt.r], writes=[XTn.r])
                            for h in range(4):
                                sink[0].op("pe", lambda e, h=h, ptt=ptt, Xn=Xn, TT=TT: e.matmul(
                                    ptt.t[0:C, h * 64:h * 64 + C], lhsT=Xn.t[0:C, h, 0:C], rhs=TT.t[0:C, h, 0:C], start=True, stop=True),
                                    reads=[Xn.r, TT.r], writes=[ptt.r], signal=(h == 3))
                            sink[0].op("dve", lambda e, ptt=ptt, TT=TT, TTn=TTn: e.tensor_tensor(
                                out=TTn.t[0:C, :, 0:C], in0=v3(ptt.t[0:C, 0:256]), in1=TT.t[0:C, :, 0:C], op=ALU.add),
                                reads=[ptt.r, TT.r], writes=[TTn.r])
                        cur = 1 - cur
                    if RWDBG < 4:
                        return
                    for c in range(nch):
                        TTf = rw_TT[c][cur]
                        AT12, AT34 = rw_AT12[c], rw_AT34[c]
                        pz, pu, ph = next_bank(), next_bank(), next_bank()
                        for pc in range(2):
                            for par in range(2):
                                sink[0].op("pe", lambda e, c=c, AT12=AT12, AT34=AT34, pc=pc, par=par, pz=pz: e.matmul(
                                    pz.t[0:C, pc * 128:(pc + 1) * 128], lhsT=rw_QRm.t[:, par, pc, c, 0, 0:C], rhs=rw_Hbd.t[:, pc, :],
                                    start=(par == 0), stop=False, skip_group_check=True),
                                    reads=[rw_QRm.rs[pc], rw_Hbd.rs[pc]], writes=[pz.r], signal=False)
                            for h in (2 * pc, 2 * pc + 1):
                                sink[0].op("pe", lambda e, c=c, AT12=AT12, AT34=AT34, h=h, pz=pz: e.matmul(
                                    pz.t[0:C, h * 64:(h + 1) * 64], lhsT=AT12.t[0:C, h, 0, 0:C], rhs=rw_Vt.t[0:C, c, h * 64:(h + 1) * 64],
                                    start=False, stop=True, skip_group_check=True),
                                    reads=[AT12.r, rw_Vt.r], writes=[pz.r], signal=(h == 3))
                        sink[0].op("act", lambda e, c=c, AT12=AT12, AT34=AT34, pz=pz: e.activation(out=rw_Zb.t[0:C, :], in_=pz.t[0:C, 0:256], func=AF.Copy),
                              reads=[pz.r], writes=[rw_Zb.r])
                        for h in range(4):
                            sink[0].op("pe", lambda e, c=c, AT12=AT12, AT34=AT34, h=h, pu=pu, TTf=TTf: e.matmul(
                                pu.t[0:C, h * 64:(h + 1) * 64], lhsT=TTf.t[0:C, h, 0:C], rhs=rw_Zb.t[0:C, h * 64:(h + 1) * 64],
                                start=True, stop=True), reads=[TTf.r, rw_Zb.r], writes=[pu.r], signal=(h == 3))
                        sink[0].op("dve", lambda e, c=c, AT12=AT12, AT34=AT34, pu=pu: e.tensor_scalar_mul(out=rw_Un.t[0:C, :], in0=pu.t[0:C, 0:256], scalar1=-1.0),
                              reads=[pu.r], writes=[rw_Un.r])
                        for pc in range(2):
                          for par in range(2):
                                sink[0].op("pe", lambda e, c=c, AT12=AT12, AT34=AT34, pc=pc, par=par: e.matmul(
                                    py[pc].t[:, c * C:(c + 1) * C], lhsT=rw_Hbd.t[:, pc, :], rhs=rw_QRm.t[:, par, pc, c, 1, 0:C],
                                    start=(par == 0), stop=False, skip_group_check=True),
                                    reads=[rw_QRm.rs[pc], rw_Hbd.rs[pc]], writes=[py[pc].r], signal=False)
                          for h in (2 * pc, 2 * pc + 1):
                            hs = slice((h % 2) * 64, (h % 2) * 64 + 64)
                            sink[0].op("pe", lambda e, c=c, AT12=AT12, AT34=AT34, h=h, pc=pc, hs=hs: e.matmul(
                                py[pc].t[hs, c * C:(c + 1) * C], lhsT=rw_Vt.t[0:C, c, h * 64:(h + 1) * 64], rhs=AT12.t[0:C, h, 1, 0:C],
                                start=False, stop=False, skip_group_check=True),
                                reads=[rw_Vt.r, AT12.r], writes=[py[pc].r], signal=False)
                            sink[0].op("pe", lambda e, c=c, AT12=AT12, AT34=AT34, h=h, pc=pc, hs=hs: e.matmul(
                                py[pc].t[hs, c * C:(c + 1) * C], lhsT=rw_Un.t[0:C, h * 64:(h + 1) * 64], rhs=AT34.t[0:C, h, 1, 0:C],
                                start=False, stop=True, skip_group_check=True),
                                reads=[rw_Un.r, AT34.r], writes=[py[pc].r], signal=(h % 2 == 1))
                        for pc in range(2):
                            cs = slice(pc * 128, (pc + 1) * 128)
                            sink[0].op("pe", lambda e, c=c, AT12=AT12, AT34=AT34, pc=pc, cs=cs, ph=ph: e.matmul(
                                ph.t[:, cs], lhsT=rw_KeT.t[0:C, c, cs], rhs=rw_Vt.t[0:C, c, cs], start=True, stop=False),
                                reads=[rw_KeT.r, rw_Vt.r], writes=[ph.r], signal=False)
                            sink[0].op("pe", lambda e, c=c, AT12=AT12, AT34=AT34, pc=pc, cs=cs, ph=ph: e.matmul(
                                ph.t[:, cs], lhsT=rw_BeT.t[0:C, c, cs], rhs=rw_Un.t[0:C, cs], start=False, stop=True),
                                reads=[rw_BeT.r, rw_Un.r], writes=[ph.r])
                            for hh in range(2):
                                hr = slice(hh * 64, hh * 64 + 64)
                                sink[0].op("dve", lambda e, c=c, AT12=AT12, AT34=AT34, pc=pc, hr=hr, hh=hh, ph=ph: e.scalar_tensor_tensor(
                                    out=rw_Hm.t[hr, pc, hr], in0=rw_Hm.t[hr, pc, hr], scalar=rw_gC.t[hr, pc, c:c + 1],
                                    in1=ph.t[hr, pc * 128 + hh * 64:pc * 128 + hh * 64 + 64], op0=ALU.mult, op1=ALU.add),
                                    reads=[rw_Hm.rs[pc], rw_gC.rs[pc], ph.r], writes=[rw_Hm.rs[pc]])
                            sink[0].op("act", lambda e, c=c, AT12=AT12, AT34=AT34, pc=pc: e.activation(out=rw_Hbd.t[:, pc, :], in_=rw_Hm.t[:, pc, :], func=AF.Copy),
                                  reads=[rw_Hm.rs[pc]], writes=[rw_Hbd.rs[pc]])
                    if RWDBG < 5:
                        return
                    for pc in range(2):
                        v_ap = XS(4 + pc)
                        TB = T4 if pc == 0 else T5
                        sink[0].op("act", lambda e, pc=pc: e.activation(out=rw_Yf.t[:, 0:w], in_=py[pc].t[:, 0:w], func=AF.Copy),
                              reads=[py[pc].r], writes=[rw_Yf.r])
                        sink[0].op("act", lambda e: e.activation(out=rw_sqb.t[:, 0:w], in_=rw_Yf.t[:, 0:w], func=AF.Copy),
                              reads=[rw_Yf.r], writes=[rw_sqb.r])
                        pm = next_bank()
                        sink[0].op("pe", lambda e, pm=pm: e.matmul(pm.t[:, 0:w], lhsT=onesbd.t[:], rhs=rw_sqb.t[:, 0:w], start=True, stop=True),
                              reads=[onesbd.r, rw_sqb.r], writes=[pm.r])
                        sink[0].op("dve", lambda e, pm=pm: e.scalar_tensor_tensor(
                            out=rw_Yf.t[:, 0:w], in0=pm.t[:, 0:w], scalar=-1.0 / 64, in1=rw_Yf.t[:, 0:w], op0=ALU.mult, op1=ALU.add),
                            reads=[pm.r, rw_Yf.r], writes=[rw_Yf.r])
                        sink[0].op("act", lambda e: e.activation(out=rw_sqb.t[:, 0:w], in_=rw_Yf.t[:, 0:w], func=AF.Square),
                              reads=[rw_Yf.r], writes=[rw_sqb.r])
                        pv = next_bank()
                        sink[0].op("pe", lambda e, pv=pv: e.matmul(pv.t[:, 0:w], lhsT=onesbd.t[:], rhs=rw_sqb.t[:, 0:w], start=True, stop=True),
                              reads=[onesbd.r, rw_sqb.r], writes=[pv.r])
                        sink[0].op("act", lambda e, pv=pv: e.activation(out=T0.t[:, 0:w], in_=pv.t[:, 0:w], func=AF.Ln, scale=1.0 / 64,
                                                                   bias=eps2.t[:]), reads=[pv.r, eps2.r], writes=[T0.r])
                        sink[0].op("act", lambda e: e.activation(out=T0.t[:, 0:w], in_=T0.t[:, 0:w], func=AF.Exp, scale=-0.5),
                              reads=[T0.r], writes=[T0.r])
                        sink[0].op("dve", lambda e: e.tensor_tensor(out=rw_Yf.t[:, 0:w], in0=rw_Yf.t[:, 0:w], in1=T0.t[:, 0:w], op=ALU.mult),
                              reads=[rw_Yf.r, T0.r], writes=[rw_Yf.r])
                        sink[0].op("dve", lambda e, pc=pc: e.tensor_scalar(
                            out=rw_Yf.t[:, 0:w], in0=rw_Yf.t[:, 0:w], scalar1=rwp["rw_ln_w"].t[:, l, pc:pc + 1],
                            scalar2=rwp["rw_ln_b"].t[:, l, pc:pc + 1], op0=ALU.mult, op1=ALU.add),
                            reads=[rw_Yf.r, rwp["rw_ln_w"].r, rwp["rw_ln_b"].r], writes=[rw_Yf.r])
                        sink[0].op("act", lambda e, TB=TB: e.activation(out=rw_sqb.t[:, 0:w], in_=TB.t[:, 0:w], func=AF.Copy),
                              reads=[TB.r], writes=[rw_sqb.r])
                        pbn = next_bank()
                        sink[0].op("pe", lambda e, pbn=pbn: e.matmul(pbn.t[:, 0:w], lhsT=onesbd.t[:], rhs=rw_sqb.t[:, 0:w], start=True, stop=True),
                              reads=[onesbd.r, rw_sqb.r], writes=[pbn.r])
                        sink[0].op("dve", lambda e, pbn=pbn, v_ap=v_ap: e.tensor_tensor(out=T0.t[:, 0:w], in0=pbn.t[:, 0:w], in1=v_ap, op=ALU.mult),
                              reads=[pbn.r, rw_Pb.r], writes=[T0.r])
                        sink[0].op("dve", lambda e: e.tensor_tensor(out=rw_Yf.t[:, 0:w], in0=rw_Yf.t[:, 0:w], in1=T0.t[:, 0:w], op=ALU.add),
                              reads=[rw_Yf.r, T0.r], writes=[rw_Yf.r])
                        sink[0].op("dve", lambda e, pc=pc: e.tensor_tensor(out=ymix.t[:, pc, 0:w], in0=rw_Yf.t[:, 0:w], in1=rw_G[pc].t[:, 0:w],
                                                                      op=ALU.mult), reads=[rw_Yf.r, rw_G[pc].r], writes=[ymix.rs[pc]])

                RWKV_TILE = rw_tile

                it = 0
                a2_tiles = [(s_, t0_, w_) for s_ in range(3) for (t0_, w_, j_) in tiles_of(s_, WA)]
                a2_idx = [0]

                def a2_norm(idx):
                    s_, t0_, w_ = a2_tiles[idx]
                    xT_ = xTa[idx % 2]
                    load_xT(xT_, t0_, w_)
                    norm_fm(xT_, w_, sq, tmp, lnv, rstd, hT,
                            lambda c, s_=s_: G1.t[:, l, s_, c:c + 1], lambda c, s_=s_: mod.t[:, l, 0, s_, c:c + 1], hT.rs)
                for s in range(3):
                    hg_init(s)
                    if stage >= 4:
                        rw_init(s)

                    def a2_tile(s, t0, w, j, xT, l=l):
                        if a2_idx[0] == 0:
                            a2_norm(0)
                        fw.dma("pool", ymix.t[:, 2:6, 0:w], yfox.rearrange("(c p) t -> p c t", p=128)[:, :, t0:t0 + w],
                               reads=[yfreg(t0)], writes=ymix.rs[2:6], stream="yfld")
                        recs = []
                        if stage >= 3:
                            sink[0] = Rec()
                            recs.append(sink[0])
                            default_pool[0] = (0, 1)
                            hg_tile(s, w)
                        if stage >= 4 and RWKV_TILE is not None:
                            sink[0] = Rec()
                            recs.append(sink[0])
                            default_pool[0] = (2, 3, 4)
                            RWKV_TILE(s, w)
                        sink[0] = fw
                        default_pool[0] = (0, 1, 2, 3, 4)
                        merge_recs(fw, recs)
                        a2_idx[0] += 1
                        if a2_idx[0] < len(a2_tiles):
                            a2_norm(a2_idx[0])
                        for c in range(8):
                            po_ = next_bank()
                            for kc in range(8):
                                fw.op("pe", lambda e, c=c, kc=kc, po_=po_: e.matmul(
                                    po_.t[:, 0:w], lhsT=wo.t[:, kc, c * 128:(c + 1) * 128], rhs=ymix.t[:, kc, 0:w],
                                    start=(kc == 0), stop=(kc == 7)),
                                    reads=[wo.r, ymix.rs[kc]], writes=[po_.r], signal=(kc == 7))
                            fw.op("dve", lambda e, c=c, po_=po_: e.scalar_tensor_tensor(
                                out=xT.t[:, c, 0:w], in0=po_.t[:, 0:w], scalar=mod.t[:, l, 2, s, c:c + 1],
                                in1=xT.t[:, c, 0:w], op0=ALU.mult, op1=ALU.add),
                                reads=[po_.r, mod.r, xT.rs[c]], writes=[xT.rs[c]])
                        store_xT(xT, t0, w)
                    for (t0, w, j) in tiles_of(s, WA):
                        xT = xTa[it % 2]
                        it += 1
                        a2_tile(s, t0, w, j, xT)
                    if stage >= 3:
                        hg_final(s)
                    if stage >= 4:
                        rw_final(s, w)
                fw.flush()
                default_pool[0] = (0, 1, 2, 3, 4, 5, 6, 7)
            with contextlib.ExitStack() as ph:
                sub = FWScope(fw, ph)
                WB = 256
                wfi = Tt(sub.sbuf("wfi", [128, 8, 2 * DFF], BF16), name="wfi")
                wfo = Tt(sub.sbuf("wfo", [128, 22, D], BF16), name="wfo")
                for kc in range(8):
                    fw.dma("pool", wfi.t[:, kc, :], I["w_ffn_in"][l][kc * 128:(kc + 1) * 128, :], writes=[wfi.r],
                           stream="wld", group=True)
                for kc in range(22):
                    fw.dma("pool", wfo.t[:, kc, :], I["w_ffn_out"][l][kc * 128:(kc + 1) * 128, :], writes=[wfo.r],
                           stream="wld", group=True)
                xTb = [Tt(sub.sbuf("xTb%d" % i, [128, 8, WB], F32), nreg=8, name="xTb%d" % i) for i in range(2)]
                sq = Tt(sub.sbuf("sqb", [128, 8, WB], BF16), name="sqb")
                hT = Tt(sub.sbuf("hTb", [128, 8, WB], BF16), nreg=8, name="hTb")
                tmp = [Tt(sub.sbuf("tmpb%d" % i, [128, WB], F32), name="tmpb%d" % i) for i in range(2)]
                lnv = Tt(sub.sbuf("lnvb", [128, WB], F32), name="lnvb")
                rstd = Tt(sub.sbuf("rstdb", [128, WB], F32), name="rstdb")
                actT = Tt(sub.sbuf("actT", [128, 22, WB], BF16), nreg=22, name="actT")
                sg = [Tt(sub.sbuf("sg%d" % i, [128, WB], F32), name="sg%d" % i) for i in range(2)]
                tiles_b = [(s_, t0_, w_) for s_ in range(3) for (t0_, w_, j_) in tiles_of(s_, WB)]

                def b_norm(idx):
                    s_, t0_, w_ = tiles_b[idx]
                    xT_ = xTb[idx % 2]
                    load_xT(xT_, t0_, w_, q="sp")
                    norm_fm(xT_, w_, sq, tmp, lnv, rstd, hT,
                            lambda c, s_=s_: G2.t[:, l, s_, c:c + 1], lambda c, s_=s_: mod.t[:, l, 3, s_, c:c + 1], hT.rs)
                b_norm(0)
                for idx in range(len(tiles_b)):
                    if True:
                        s, t0, w = tiles_b[idx]
                        xT = xTb[idx % 2]
                        for f in range(22):
                            pg = next_bank()
                            pu = next_bank()
                            for kc in range(8):
                                fw.op("pe", lambda e, kc=kc, f=f, pg=pg, w=w: e.matmul(
                                    pg.t[:, 0:w], lhsT=wfi.t[:, kc, f * 128:(f + 1) * 128], rhs=hT.t[:, kc, 0:w],
                                    start=(kc == 0), stop=(kc == 7)),
                                    reads=[wfi.r, hT.rs[kc]], writes=[pg.r], signal=(kc == 7))
                            for kc in range(8):
                                fw.op("pe", lambda e, kc=kc, f=f, pu=pu, w=w: e.matmul(
                                    pu.t[:, 0:w], lhsT=wfi.t[:, kc, DFF + f * 128:DFF + (f + 1) * 128],
                                    rhs=hT.t[:, kc, 0:w], start=(kc == 0), stop=(kc == 7)),
                                    reads=[wfi.r, hT.rs[kc]], writes=[pu.r], signal=(kc == 7))
                            sgt = sg[f % 2]
                            fw.op("act", lambda e, pg=pg, sgt=sgt, w=w: e.activation(
                                out=sgt.t[:, 0:w], in_=pg.t[:, 0:w], func=AF.Silu), reads=[pg.r], writes=[sgt.r])
                            fw.op("dve", lambda e, pu=pu, sgt=sgt, f=f, w=w: e.tensor_tensor(
                                out=actT.t[:, f, 0:w], in0=sgt.t[:, 0:w], in1=pu.t[:, 0:w], op=ALU.mult),
                                reads=[sgt.r, pu.r], writes=[actT.rs[f]])
                        if idx + 1 < len(tiles_b):
                            b_norm(idx + 1)
                        for c in range(8):
                            po = next_bank()
                            for f in range(22):
                                fw.op("pe", lambda e, c=c, f=f, po=po, w=w: e.matmul(
                                    po.t[:, 0:w], lhsT=wfo.t[:, f, c * 128:(c + 1) * 128], rhs=actT.t[:, f, 0:w],
                                    start=(f == 0), stop=(f == 21)),
                                    reads=[wfo.r, actT.rs[f]], writes=[po.r], signal=(f == 21))
                            fw.op("dve", lambda e, c=c, po=po, xT=xT, s=s, w=w: e.scalar_tensor_tensor(
                                out=xT.t[:, c, 0:w], in0=po.t[:, 0:w], scalar=mod.t[:, l, 5, s, c:c + 1],
                                in1=xT.t[:, c, 0:w], op0=ALU.mult, op1=ALU.add),
                                reads=[po.r, mod.r, xT.rs[c]], writes=[xT.rs[c]])
                        store_xT(xT, t0, w, q="sp")
                fw.flush()

        with contextlib.ExitStack() as ph:
            sub = FWScope(fw, ph)
            WE = 512
            xTe = [Tt(sub.sbuf("xTe%d" % i, [128, 8, WE], F32), nreg=8, name="xTe%d" % i) for i in range(2)]
            sq = Tt(sub.sbuf("sqe", [128, 8, WE], BF16), name="sqe")
            yT = Tt(sub.sbuf("yTe", [128, 8, WE], F32), nreg=8, name="yTe")
            tmp = [Tt(sub.sbuf("tmpe%d" % i, [128, WE], F32), name="tmpe%d" % i) for i in range(2)]
            lnv = Tt(sub.sbuf("lnve", [128, WE], F32), name="lnve")
            rstd = Tt(sub.sbuf("rstde", [128, WE], F32), name="rstde")
            ytok = [Tt(sub.sbuf("ytok%d" % i, [128, D], F32), name="ytok%d" % i) for i in range(2)]
            it = 0
            ik = 0
            for s in range(3):
                dst = O["yp"][s] if s < 2 else O["ys"]
                off, T = seqs[s]
                for (t0, w, j) in tiles_of(s, WE):
                    xT = xTe[it % 2]
                    it += 1
                    load_xT(xT, t0, w)
                    norm_fm(xT, w, sq, tmp, lnv, rstd, yT, lambda c: fng.t[:, c:c + 1], None, yT.rs)
                    nb = (w + 127) // 128
                    pw = min(w, 128)
                    for tb in range(nb):
                        yt = ytok[ik % 2]
                        ik += 1
                        for half in range(2):
                            pb = next_bank()
                            for cc in range(4):
                                c = half * 4 + cc
                                fw.op("pe", lambda e, c=c, cc=cc, tb=tb, pb=pb, pw=pw: e.transpose(
                                    pb.t[0:pw, cc * 128:(cc + 1) * 128], yT.t[:, c, tb * 128:tb * 128 + pw],
                                    ident.t[:, :]),
                                    reads=[yT.rs[c], ident.r], writes=[pb.r], signal=(cc == 3))
                            if half == 0:
                                fw.op("act", lambda e, pb=pb, yt=yt, pw=pw: e.activation(
                                    out=yt.t[0:pw, 0:512], in_=pb.t[0:pw, :], func=AF.Copy), reads=[pb.r], writes=[yt.r])
                            else:
                                fw.op("dve", lambda e, pb=pb, yt=yt, pw=pw: e.tensor_copy(
                                    out=yt.t[0:pw, 512:1024], in_=pb.t[0:pw, :]), reads=[pb.r], writes=[yt.r])
                        lt0 = t0 - off + tb * 128
                        fw.dma("sp" if ik % 2 else "pool", dst[lt0:lt0 + pw, :], yt.t[0:pw, :], reads=[yt.r],
                               stream="yout")
            fw.flush()
        fw.finish()
    return nc


class FWScope:
    ctr = 0

    def __init__(self, fw, stack):
        self.fw = fw
        self.stack = stack

    def sbuf(self, name, shape, dt):
        FWScope.ctr += 1
        return self.stack.enter_context(self.fw.nc.sbuf_tensor("%s_u%d" % (name, FWScope.ctr), list(shape), dt))


def layer_mixer(fw, nc, I, O, l, L, SEQ, TS, PAST, env):
    pass


_PROG_CACHE = {}


def _get_prog(SEQ, DEPTH, TS, PAST, stage=9):
    key = (SEQ, DEPTH, TS, PAST, stage)
    if key not in _PROG_CACHE:
        _PROG_CACHE[key] = build_program(SEQ, DEPTH, TS, PAST, stage)
    return _PROG_CACHE[key]


def make_in_maps(inp, ncores, L):
    f = lambda a: np.ascontiguousarray(np.asarray(a, dtype=np.float32))
    maps = []
    shared = {k: f(inp[k]) for k in ("norm1_g", "w_ada", "b_ada", "w_in", "rw_mu", "rw_w0", "rw_w2", "rw_a0",
                                     "rw_a2", "rw_g2", "rw_k_k", "rw_k_a", "rw_ln_w", "rw_ln_b", "fox_b_f",
                                     "hg_lb_logits", "hg_norm_g", "w_out", "norm2_g", "w_ffn_in", "w_ffn_out",
                                     "final_norm_g")}
    shared["rw_r_k"] = f(inp["rw_r_k"]).reshape(L, 256)
    xp, xs = f(inp["x_prompt"]), f(inp["x_sample"])
    cp, cs = f(inp["c_prompt"]), f(inp["c_sample"])
    ck, cv, cl = f(inp["cache_fox_k"]), f(inp["cache_fox_v"]), f(inp["cache_fox_logf"])
    srw, ssh, shg = f(inp["state_rwkv"]), f(inp["state_rwkv_shift"]), f(inp["state_hgrn"])
    P = ck.shape[2]
    for i in range(ncores):
        m = dict(shared)
        m["xp"] = f(xp[2 * i:2 * i + 2])
        m["xs"] = f(xs[i])
        m["cc"] = f(np.concatenate([cp[2 * i:2 * i + 2], cs[i:i + 1]], axis=0))
        m["ck"] = f(ck[:, i].reshape(L, P, 512))
        m["cv"] = f(cv[:, i].reshape(L, P, 512))
        m["cl"] = f(cl[:, i])
        m["srw"] = f(srw[:, i])
        m["ssh"] = f(ssh[:, i, 0])
        m["shg"] = f(shg[:, i])
        maps.append(m)
    return maps


def gather_outputs(res, ncores, L, SEQ, TS):
    r = res
    cat = lambda k, ax: np.concatenate([r[i][k] for i in range(ncores)], axis=ax)
    stack = lambda k, ax: np.stack([r[i][k] for i in range(ncores)], axis=ax)
    yp = cat("yp", 0)
    ys = stack("ys", 0)
    fkp = cat("fkp", 1).reshape(L, 2 * ncores, SEQ, 8, 64)
    fvp = cat("fvp", 1).reshape(L, 2 * ncores, SEQ, 8, 64)
    flp = cat("flp", 1)
    rwp = cat("rwp", 1)
    rshp = cat("rshp", 1).reshape(L, 2 * ncores, 1, RW_COLS)
    hgp = cat("hgp", 1)
    fks = stack("fks", 1).reshape(L, ncores, TS, 8, 64)
    fvs = stack("fvs", 1).reshape(L, ncores, TS, 8, 64)
    fls = stack("fls", 1)
    rws = stack("rws", 1)
    rshs = stack("rshs", 1).reshape(L, ncores, 1, RW_COLS)
    hgs = stack("hgs", 1)
    return (yp, ys, fkp, fvp, flp, rwp, rshp, hgp, fks, fvs, fls, rws, rshs, hgs)


def kernel(**inputs):
    L = int(np.asarray(inputs["w_in"]).shape[0])
    SEQ = int(np.asarray(inputs["x_prompt"]).shape[1])
    TS = int(np.asarray(inputs["x_sample"]).shape[1])
    PAST = int(np.asarray(inputs["cache_fox_k"]).shape[2])
    ncores = int(np.asarray(inputs["x_sample"]).shape[0])
    nc = _get_prog(SEQ, L, TS, PAST)
    maps = make_in_maps(inputs, ncores, L)
    res = run_bass_kernel_spmd(nc, maps, core_ids=list(range(ncores)))
    return gather_outputs(res.results, ncores, L, SEQ, TS)
```

```python
import contextlib
import os
import numpy as np
import concourse.bass as bass
import concourse.mybir as mybir
from concourse.bass_utils import run_bass_kernel_spmd

F32 = mybir.dt.float32
BF16 = mybir.dt.bfloat16
AF = mybir.ActivationFunctionType
ALU = mybir.AluOpType

D = 1024
NC8 = 8
HD = 64
RW_COLS, FOX_COLS, HG_COLS = 896, 1544, 1024
IN_COLS = 3464
DFF = 2816
EPS = 1e-6


class Reg:
    __slots__ = ("name", "writers", "readers")

    def __init__(self, name=""):
        self.name = name
        self.writers = {}
        self.readers = {}


class Eng:
    def __init__(self, name, kind):
        self.name = name
        self.kind = kind
        self.ops = []
        self.sem = None
        self.count = 0
        self.waited = {}


class FW:
    def __init__(self, nc, stack):
        self.nc = nc
        self.stack = stack
        self.engs = {}
        self.dma_sems = {}
        self.dma_counts = {}
        self.group_sems = {}
        self.nsem = 0
        for name in ("pe", "act", "dve", "pool", "sp"):
            e = Eng(name, name)
            self.engs[name] = e
            if name != "sp":
                e.sem = self.new_sem("s_" + name)

    def new_sem(self, name):
        self.nsem += 1
        return self.stack.enter_context(self.nc.semaphore("%s_%d" % (name, self.nsem)))

    def sbuf(self, name, shape, dt):
        return self.stack.enter_context(self.nc.sbuf_tensor(name, list(shape), dt))

    def psum(self, name, shape, dt=F32):
        return self.stack.enter_context(self.nc.psum_tensor(name, list(shape), dt))

    def _collect(self, reads, writes):
        deps = {}

        def add(d):
            for k, (sem, val) in d.items():
                cur = deps.get(k)
                if cur is None or cur[1] < val:
                    deps[k] = (sem, val)
        for r in reads:
            add(r.writers)
        for w in writes:
            add(w.writers)
            add(w.readers)
        return deps

    def _waits(self, eng, deps, raw_keys):
        waits = []
        for k, (sem, val) in deps.items():
            if eng.sem is not None and k == id(eng.sem) and k not in raw_keys:
                continue
            if eng.waited.get(k, 0) >= val:
                continue
            eng.waited[k] = val
            st = self.group_sems.get(k)
            if st is not None:
                waits.append((sem, _Lazy(self.dma_counts, st)))
            else:
                waits.append((sem, val))
        return waits

    def op(self, engname, fn, reads=(), writes=(), signal=True):
        eng = self.engs[engname]
        reads = [r for r in reads if r is not None]
        writes = [w for w in writes if w is not None]
        deps = self._collect(reads, writes)
        raw_keys = set()
        k = id(eng.sem)
        if engname != "pe":
            raw_keys.add(k)
        for r in reads:
            if k in r.writers:
                raw_keys.add(k)
        waits = self._waits(eng, deps, raw_keys)
        sem = eng.sem
        if signal:
            eng.count += 1
            tok = (sem, eng.count)
        else:
            tok = (sem, eng.count + 1)
        for r in reads:
            r.readers[id(sem)] = tok
        for w in writes:
            w.writers = {id(sem): tok}
            w.readers = {}

        def run(e, fn=fn, waits=waits, signal=signal, sem=sem):
            for (s, v) in waits:
                e.wait_ge(s, int(v))
            ins = fn(e)
            if signal:
                ins.then_inc(sem, 1)
        eng.ops.append(run)

    def dma(self, qname, out, in_, reads=(), writes=(), stream="d", group=False, **kw):
        eng = self.engs[qname]
        reads = [r for r in reads if r is not None]
        writes = [w for w in writes if w is not None]
        deps = self._collect(reads, writes)
        stream = stream + "_" + qname
        if stream not in self.dma_sems:
            self.dma_sems[stream] = self.new_sem("dq_" + stream)
            self.dma_counts[stream] = 0
            if group:
                self.group_sems[id(self.dma_sems[stream])] = stream
        if group:
            deps.pop(id(self.dma_sems[stream]), None)
        waits = self._waits(eng, deps, set(deps.keys()))
        sem = self.dma_sems[stream]
        self.dma_counts[stream] += 16
        tok = (sem, self.dma_counts[stream])
        for r in reads:
            r.readers[id(sem)] = tok
        for w in writes:
            w.writers = {id(sem): tok}
            w.readers = {}

        def run(e, waits=waits, sem=sem, out=out, in_=in_, kw=kw):
            for (s, v) in waits:
                e.wait_ge(s, int(v))
            e.dma_start(out=out, in_=in_, **kw).then_inc(sem, 16)
        eng.ops.append(run)

    def rotate(self):
        for e in self.engs.values():
            if e.sem is not None:
                e.sem = self.new_sem("s_" + e.name)
                e.count = 0

    def barrier(self):
        toks = []
        for e in self.engs.values():
            if e.sem is not None and e.count > 0:
                toks.append((e.sem, e.count))
        for s in self.dma_sems:
            if self.dma_counts[s] > 0:
                toks.append((self.dma_sems[s], self.dma_counts[s]))
        for e in self.engs.values():
            waits = []
            for (sem, val) in toks:
                if sem is e.sem:
                    continue
                if e.waited.get(id(sem), 0) >= val:
                    continue
                e.waited[id(sem)] = val
                waits.append((sem, val))

            def run(h, waits=waits):
                for (s, v) in waits:
                    h.wait_ge(s, v)
            e.ops.append(run)

    def finish(self):
        self.flush()

    def flush(self):
        self.barrier()
        nc = self.nc
        engs = self.engs
        oplists = {k: e.ops for k, e in engs.items()}
        for e in engs.values():
            e.ops = []

        class _E:
            def __init__(self, ops):
                self.ops = ops
        self_engs = {k: _E(v) for k, v in oplists.items()}
        with nc.Block() as block:
            def mk(eng):
                def body(e):
                    for f in eng.ops:
                        f(e)
                return body
            block.tensor(mk(self_engs["pe"]))
            block.scalar(mk(self_engs["act"]))
            block.vector(mk(self_engs["dve"]))
            block.gpsimd(mk(self_engs["pool"]))
            block.sync(mk(self_engs["sp"]))


class _Lazy:
    def __init__(self, counts, stream):
        self.counts = counts
        self.stream = stream

    def __int__(self):
        return self.counts[self.stream]


class Rec:
    def __init__(self):
        self.items = []

    def op(self, *a, **k):
        self.items.append(("op", a, k))

    def dma(self, *a, **k):
        self.items.append(("dma", a, k))

    def flush(self):
        pass

    def replay_into(self, sink):
        for (kind, a, k) in self.items:
            getattr(sink, kind)(*a, **k)


def merge_recs(sink, recs):
    pos = [0] * len(recs)
    n = [len(r.items) for r in recs]
    total = sum(n)
    for _ in range(total):
        best, bf_ = -1, 2.0
        for i in range(len(recs)):
            if pos[i] < n[i]:
                f = pos[i] / n[i]
                if f < bf_:
                    best, bf_ = i, f
        kind, a, k = recs[best].items[pos[best]]
        pos[best] += 1
        getattr(sink, kind)(*a, **k)


class Tt:
    def __init__(self, t, nreg=1, name=""):
        self.t = t
        self.rs = [Reg("%s%d" % (name, i)) for i in range(nreg)]
        self.r = self.rs[0]


def build_program(SEQ, DEPTH, TS=32, PAST=2048, stage=9):
    L = DEPTH
    NTOK = 2 * SEQ + TS
    nc = bass.Bass("TRN2", target_bir_lowering=False)
    din = lambda n, s: nc.dram_tensor(n, list(s), F32, kind="ExternalInput").ap()
    dout = lambda n, s: nc.dram_tensor(n, list(s), F32, kind="ExternalOutput").ap()
    I = dict(
        xp=din("xp", (2, SEQ, D)), xs=din("xs", (TS, D)), cc=din("cc", (3, D)),
        ck=din("ck", (L, PAST, 512)), cv=din("cv", (L, PAST, 512)), cl=din("cl", (L, PAST, 8)),
        srw=din("srw", (L, 4, 64, 64)), ssh=din("ssh", (L, RW_COLS)), shg=din("shg", (L, 4, 64, 64)),
        norm1_g=din("norm1_g", (L, D)), w_ada=din("w_ada", (L, D, 6 * D)), b_ada=din("b_ada", (L, 6 * D)),
        w_in=din("w_in", (L, D, IN_COLS)), rw_mu=din("rw_mu", (L, RW_COLS)), rw_w0=din("rw_w0", (L, 256)),
        rw_w2=din("rw_w2", (L, 32, 256)), rw_a0=din("rw_a0", (L, 256)), rw_a2=din("rw_a2", (L, 32, 256)),
        rw_g2=din("rw_g2", (L, 64, 256)), rw_k_k=din("rw_k_k", (L, 256)), rw_k_a=din("rw_k_a", (L, 256)),
        rw_r_k=din("rw_r_k", (L, 256)), rw_ln_w=din("rw_ln_w", (L, 256)), rw_ln_b=din("rw_ln_b", (L, 256)),
        fox_b_f=din("fox_b_f", (L, 8)), hg_lb_logits=din("hg_lb_logits", (L, 256)),
        hg_norm_g=din("hg_norm_g", (L, 256)), w_out=din("w_out", (L, D, D)), norm2_g=din("norm2_g", (L, D)),
        w_ffn_in=din("w_ffn_in", (L, D, 2 * DFF)), w_ffn_out=din("w_ffn_out", (L, DFF, D)),
        final_norm_g=din("final_norm_g", (D,)),
    )
    O = dict(
        yp=dout("yp", (2, SEQ, D)), ys=dout("ys", (TS, D)),
        fkp=dout("fkp", (L, 2, SEQ, 512)), fvp=dout("fvp", (L, 2, SEQ, 512)), flp=dout("flp", (L, 2, SEQ, 8)),
        rwp=dout("rwp", (L, 2, 4, 64, 64)), rshp=dout("rshp", (L, 2, RW_COLS)), hgp=dout("hgp", (L, 2, 4, 64, 64)),
        fks=dout("fks", (L, TS, 512)), fvs=dout("fvs", (L, TS, 512)), fls=dout("fls", (L, TS, 8)),
        rws=dout("rws", (L, 4, 64, 64)), rshs=dout("rshs", (L, RW_COLS)), hgs=dout("hgs", (L, 4, 64, 64)),
    )
    xres = nc.dram_tensor("xres", [D, NTOK], F32).ap()
    xres_r = Reg("xres")
    seqs = [(0, SEQ), (SEQ, SEQ), (2 * SEQ, TS)]

    def tiles_of(s, W):
        off, T = seqs[s]
        w = min(W, T)
        return [(off + j * w, w, j) for j in range(T // w)]

    xres_regs = {}

    def xreg(t0):
        return xres_regs.setdefault(t0, Reg("xres%d" % t0))

    with contextlib.ExitStack() as top:
        fw = FW(nc, top)
        ident = Tt(fw.sbuf("ident", [128, 128], F32), name="ident")
        identb = Tt(fw.sbuf("identb", [128, 128], BF16), name="identb")
        onesb = Tt(fw.sbuf("onesb", [128, 128], BF16), name="onesb")
        fw.op("pool", lambda e: e.memset(ident.t[:], 0.0), writes=[ident.r])
        fw.op("pool", lambda e: e.affine_select(out=ident.t[:], in_=ident.t[:], pattern=[[-1, 128]],
                                                compare_op=ALU.not_equal, fill=1.0, base=0,
                                                channel_multiplier=1), reads=[ident.r], writes=[ident.r])
        fw.op("pool", lambda e: e.tensor_copy(out=identb.t[:], in_=ident.t[:]), reads=[ident.r], writes=[identb.r])
        fw.op("pool", lambda e: e.memset(onesb.t[:], 1.0), writes=[onesb.r])
        epsb = Tt(fw.sbuf("epsb", [128, 1], F32), name="epsb")
        fw.op("pool", lambda e: e.memset(epsb.t[:], EPS), writes=[epsb.r])


        trif = Tt(fw.sbuf("trif", [128, 128], F32), name="trif")
        fw.op("pool", lambda e: e.memset(trif.t[:], 1.0), writes=[trif.r])
        fw.op("pool", lambda e: e.affine_select(out=trif.t[:], in_=trif.t[:], pattern=[[1, 128]],
                                                compare_op=ALU.is_ge, fill=0.0, base=0, channel_multiplier=-1),
              reads=[trif.r], writes=[trif.r])
        self127 = Tt(fw.sbuf("self127", [128, 128], F32), name="self127")
        fw.op("pool", lambda e: e.memset(self127.t[:], 0.0), writes=[self127.r])
        fw.op("pool", lambda e: e.affine_select(out=self127.t[:], in_=self127.t[:], pattern=[[0, 128]],
                                                compare_op=ALU.not_equal, fill=1.0, base=-127, channel_multiplier=1),
              reads=[self127.r], writes=[self127.r])
        mnegf = Tt(fw.sbuf("mnegf", [128, 128], F32), name="mnegf")
        maskneg = Tt(fw.sbuf("maskneg", [128, 128], BF16), name="maskneg")
        fw.op("pool", lambda e: e.memset(mnegf.t[:], 0.0), writes=[mnegf.r])
        fw.op("pool", lambda e: e.affine_select(out=mnegf.t[:], in_=mnegf.t[:], pattern=[[1, 128]],
                                                compare_op=ALU.is_ge, fill=-30000.0, base=0, channel_multiplier=-1),
              reads=[mnegf.r], writes=[mnegf.r])
        fw.op("pool", lambda e: e.tensor_copy(out=maskneg.t[:], in_=mnegf.t[:]), reads=[mnegf.r], writes=[maskneg.r])
        e8 = Tt(fw.sbuf("e8", [8, 8, 128], F32), name="e8")
        fw.op("pool", lambda e: e.memset(e8.t[:], 0.0), writes=[e8.r])
        fw.op("pool", lambda e: e.affine_select(out=e8.t[:], in_=e8.t[:], pattern=[[-1, 8], [0, 128]],
                                                compare_op=ALU.not_equal, fill=1.0, base=0, channel_multiplier=1),
              reads=[e8.r], writes=[e8.r])
        selh = Tt(fw.sbuf("selh", [72, 8, 128], BF16), name="selh")
        fw.op("pool", lambda e: e.memset(selh.t[:], 0.0), writes=[selh.r])
        for b0 in (0, 32, 64):
            fw.op("dve", lambda e, b0=b0: e.tensor_copy(out=selh.t[b0:b0 + 8, :, :], in_=e8.t[:]),
                  reads=[e8.r], writes=[selh.r])
        onesf = Tt(fw.sbuf("onesf", [128, 64], F32), name="onesf")
        fw.op("pool", lambda e: e.memset(onesf.t[:], 1.0), writes=[onesf.r])
        bfb = Tt(fw.sbuf("bfb", [128, L, 8], F32), name="bfb")
        for l_ in range(L):
            fw.dma("sp", bfb.t[:, l_, :], I["fox_b_f"][l_:l_ + 1, :].to_broadcast([128, 8]), writes=[bfb.r],
                   stream="par", group=True)
        yfox = nc.dram_tensor("yfox", [512, NTOK], BF16).ap()
        yfox_regs = {}

        def yfreg(t0):
            return yfox_regs.setdefault(t0, Reg("yfox%d" % t0))

        banks = [Tt(fw.psum("bank%d" % i, [128, 512]), name="bank%d" % i) for i in range(8)]
        bank_ctr = [0]

        default_pool = [(0, 1, 2, 3, 4, 5, 6, 7)]

        def next_bank(pool=None):
            if pool is None:
                pool = default_pool[0]
            b = banks[pool[bank_ctr[0] % len(pool)]]
            bank_ctr[0] += 1
            return b

        def load_fm(name, src_ap, ncol, q="sp"):
            t = Tt(fw.sbuf(name, [128, L, ncol], F32), name=name)
            fw.dma(q, t.t[:], src_ap.rearrange("l (c p) -> p l c", p=128), writes=[t.r], stream="par", group=True,
                   allow_slow_non_contiguous=True)
            return t
        n1g = load_fm("n1g", I["norm1_g"], 8)
        n2g = load_fm("n2g", I["norm2_g"], 8)
        badaT = load_fm("badaT", I["b_ada"], 48)
        fng = Tt(fw.sbuf("fng", [128, 8], F32), name="fng")
        fw.dma("sp", fng.t[:], I["final_norm_g"].rearrange("(c p) -> p c", p=128), writes=[fng.r], stream="par", group=True,
               allow_slow_non_contiguous=True)

        rmask32 = Tt(fw.sbuf("rmask32", [128, 512], F32), name="rmask32")
        fw.op("pool", lambda e: e.memset(rmask32.t[:], 1.0), writes=[rmask32.r])
        fw.op("pool", lambda e: e.affine_select(out=rmask32.t[:, :].rearrange("p (c t) -> p c t", t=32),
                                                in_=rmask32.t[:, :].rearrange("p (c t) -> p c t", t=32),
                                                pattern=[[0, 16], [1, 32]], compare_op=ALU.not_equal, fill=0.0,
                                                base=0, channel_multiplier=0), reads=[rmask32.r], writes=[rmask32.r])
        maskbd = Tt(fw.sbuf("maskbd", [128, 128], F32), name="maskbd")
        fw.op("pool", lambda e: e.tensor_copy(out=maskbd.t[:], in_=trif.t[:]), reads=[trif.r], writes=[maskbd.r])
        for cb_ in range(1, 4):
            fw.op("pool", lambda e, cb_=cb_: e.affine_select(
                out=maskbd.t[:, cb_ * 32:(cb_ + 1) * 32], in_=maskbd.t[:, cb_ * 32:(cb_ + 1) * 32], pattern=[[0, 32]],
                compare_op=ALU.is_ge, fill=0.0, base=-cb_ * 32, channel_multiplier=1),
                reads=[maskbd.r], writes=[maskbd.r])
        onesbdf = Tt(fw.sbuf("onesbdf", [128, 128], F32), name="onesbdf")
        onesbd = Tt(fw.sbuf("onesbd", [128, 128], BF16), name="onesbd")
        fw.op("pool", lambda e: e.memset(onesbdf.t[:], 1.0), writes=[onesbdf.r])
        fw.op("pool", lambda e: e.affine_select(out=onesbdf.t[:, 0:64], in_=onesbdf.t[:, 0:64], pattern=[[0, 64]],
                                                compare_op=ALU.is_ge, fill=0.0, base=63, channel_multiplier=-1),
              reads=[onesbdf.r], writes=[onesbdf.r])
        fw.op("pool", lambda e: e.affine_select(out=onesbdf.t[:, 64:128], in_=onesbdf.t[:, 64:128], pattern=[[0, 64]],
                                                compare_op=ALU.is_ge, fill=0.0, base=-64, channel_multiplier=1),
              reads=[onesbdf.r], writes=[onesbdf.r])
        fw.op("pool", lambda e: e.tensor_copy(out=onesbd.t[:], in_=onesbdf.t[:]), reads=[onesbdf.r], writes=[onesbd.r])
        hgng = load_fm("hgng", I["hg_norm_g"], 2)
        lbl = load_fm("lbl", I["hg_lb_logits"], 2)
        lbT = Tt(fw.sbuf("lbT", [128, L, 2], F32), name="lbT")
        omlT = Tt(fw.sbuf("omlT", [128, L, 2], F32), name="omlT")
        nomlT = Tt(fw.sbuf("nomlT", [128, L, 2], F32), name="nomlT")
        lbm = Tt(fw.sbuf("lbm", [128, 2], F32), name="lbm")
        lbe = Tt(fw.sbuf("lbe", [128, L, 2], F32), name="lbe")
        lbs_ = Tt(fw.sbuf("lbs_", [128, 2], F32), name="lbs_")
        fw.op("dve", lambda e: e.tensor_copy(out=lbm.t[:], in_=lbl.t[:, 0, :]), reads=[lbl.r], writes=[lbm.r])
        for l_ in range(1, L):
            fw.op("dve", lambda e, l_=l_: e.tensor_max(out=lbm.t[:], in0=lbm.t[:], in1=lbl.t[:, l_, :]),
                  reads=[lbm.r, lbl.r], writes=[lbm.r])
        for l_ in range(L):
            fw.op("dve", lambda e, l_=l_: e.tensor_sub(out=lbe.t[:, l_, :], in0=lbl.t[:, l_, :], in1=lbm.t[:]),
                  reads=[lbm.r, lbl.r], writes=[lbe.r])
        fw.op("act", lambda e: e.activation(out=lbe.t[:], in_=lbe.t[:], func=AF.Exp), reads=[lbe.r], writes=[lbe.r])
        fw.op("dve", lambda e: e.tensor_copy(out=lbs_.t[:], in_=lbe.t[:, 0, :]), reads=[lbe.r], writes=[lbs_.r])
        for l_ in range(1, L):
            fw.op("dve", lambda e, l_=l_: e.tensor_add(out=lbs_.t[:], in0=lbs_.t[:], in1=lbe.t[:, l_, :]),
                  reads=[lbs_.r, lbe.r], writes=[lbs_.r])
        fw.op("dve", lambda e: e.reciprocal(out=lbs_.t[:], in_=lbs_.t[:]), reads=[lbs_.r], writes=[lbs_.r])
        for l_ in range(L):
            fw.op("dve", lambda e, l_=l_: e.tensor_mul(out=lbe.t[:, l_, :], in0=lbe.t[:, l_, :], in1=lbs_.t[:]),
                  reads=[lbs_.r, lbe.r], writes=[lbe.r])
        fw.op("dve", lambda e: e.memset(lbT.t[:, 0, :], 0.0), writes=[lbT.r])
        for l_ in range(1, L):
            fw.op("dve", lambda e, l_=l_: e.tensor_add(out=lbT.t[:, l_, :], in0=lbT.t[:, l_ - 1, :], in1=lbe.t[:, l_, :]),
                  reads=[lbT.r, lbe.r], writes=[lbT.r])
        fw.op("dve", lambda e: e.tensor_scalar(out=omlT.t[:], in0=lbT.t[:], scalar1=-1.0, scalar2=1.0,
                                               op0=ALU.mult, op1=ALU.add), reads=[lbT.r], writes=[omlT.r])
        fw.op("dve", lambda e: e.tensor_scalar_mul(out=nomlT.t[:], in0=omlT.t[:], scalar1=-1.0),
              reads=[omlT.r], writes=[nomlT.r])

        rmask64 = Tt(fw.sbuf("rmask64", [128, 512], F32), name="rmask64")
        fw.op("pool", lambda e: e.memset(rmask64.t[:], 1.0), writes=[rmask64.r])
        fw.op("pool", lambda e: e.affine_select(out=rmask64.t[:, :].rearrange("p (c t) -> p c t", t=64),
                                                in_=rmask64.t[:, :].rearrange("p (c t) -> p c t", t=64),
                                                pattern=[[0, 8], [1, 64]], compare_op=ALU.not_equal, fill=0.0,
                                                base=0, channel_multiplier=0), reads=[rmask64.r], writes=[rmask64.r])
        mask12 = Tt(fw.sbuf("mask12", [64, 2, 64], F32), name="mask12")
        mask34 = Tt(fw.sbuf("mask34", [64, 2, 64], F32), name="mask34")
        mask5 = Tt(fw.sbuf("mask5", [64, 64], F32), name="mask5")
        fw.op("pool", lambda e: e.memset(mask12.t[:], 1.0), writes=[mask12.r])
        fw.op("pool", lambda e: e.affine_select(out=mask12.t[:, 0, :], in_=mask12.t[:, 0, :], pattern=[[1, 64]],
                                                compare_op=ALU.is_ge, fill=0.0, base=-1, channel_multiplier=-1),
              reads=[mask12.r], writes=[mask12.r])
        fw.op("pool", lambda e: e.affine_select(out=mask12.t[:, 1, :], in_=mask12.t[:, 1, :], pattern=[[1, 64]],
                                                compare_op=ALU.is_ge, fill=0.0, base=0, channel_multiplier=-1),
              reads=[mask12.r], writes=[mask12.r])
        fw.op("dve", lambda e: e.tensor_scalar_mul(out=mask34.t[:, 0, :], in0=mask12.t[:, 0, :], scalar1=-1.0),
              reads=[mask12.r], writes=[mask34.r])
        fw.op("pool", lambda e: e.tensor_copy(out=mask34.t[:, 1, :], in_=mask12.t[:, 1, :]),
              reads=[mask12.r], writes=[mask34.r])
        fw.op("pool", lambda e: e.memset(mask5.t[:], -1.0), writes=[mask5.r])
        fw.op("pool", lambda e: e.affine_select(out=mask5.t[:], in_=mask5.t[:], pattern=[[-1, 64]],
                                                compare_op=ALU.is_ge, fill=0.0, base=-1, channel_multiplier=1),
              reads=[mask5.r], writes=[mask5.r])
        eps2 = Tt(fw.sbuf("eps2", [128, 1], F32), name="eps2")
        fw.op("pool", lambda e: e.memset(eps2.t[:], 64e-5), writes=[eps2.r])
        rwp = {}
        for nm in ("rw_w0", "rw_a0", "rw_k_k", "rw_k_a", "rw_r_k", "rw_ln_w", "rw_ln_b"):
            rwp[nm] = load_fm("p_" + nm, I[nm], 2)
        nw0 = Tt(fw.sbuf("nw0", [128, L, 2], F32), name="nw0")
        na0 = Tt(fw.sbuf("na0", [128, L, 2], F32), name="na0")
        omka = Tt(fw.sbuf("omka", [128, L, 2], F32), name="omka")
        fw.op("dve", lambda e: e.tensor_scalar_mul(out=nw0.t[:], in0=rwp["rw_w0"].t[:], scalar1=-1.0),
              reads=[rwp["rw_w0"].r], writes=[nw0.r])
        fw.op("dve", lambda e: e.tensor_scalar_mul(out=na0.t[:], in0=rwp["rw_a0"].t[:], scalar1=-1.0),
              reads=[rwp["rw_a0"].r], writes=[na0.r])
        fw.op("dve", lambda e: e.tensor_scalar(out=omka.t[:], in0=rwp["rw_k_a"].t[:], scalar1=-1.0, scalar2=1.0,
                                               op0=ALU.mult, op1=ALU.add), reads=[rwp["rw_k_a"].r], writes=[omka.r])
        mul = Tt(fw.sbuf("mul", [128, L, 9], F32), name="mul")
        fw.op("pool", lambda e: e.memset(mul.t[:], 0.0), writes=[mul.r])
        m7 = load_fm("m7", I["rw_mu"], 7)
        fw.op("dve", lambda e: e.tensor_copy(out=mul.t[:, :, 0:6], in_=m7.t[:, :, 0:6]), reads=[m7.r], writes=[mul.r])
        fw.op("dve", lambda e: e.tensor_copy(out=mul.t[0:32, :, 6], in_=m7.t[0:32, :, 6]), reads=[m7.r], writes=[mul.r])
        fw.op("dve", lambda e: e.tensor_copy(out=mul.t[0:32, :, 7], in_=m7.t[32:64, :, 6]), reads=[m7.r], writes=[mul.r])
        fw.op("dve", lambda e: e.tensor_copy(out=mul.t[0:64, :, 8], in_=m7.t[64:128, :, 6]), reads=[m7.r], writes=[mul.r])
        w2b = Tt(fw.sbuf("w2b", [32, L, 256], BF16), name="w2b")
        a2b = Tt(fw.sbuf("a2b", [32, L, 256], BF16), name="a2b")
        g2b = Tt(fw.sbuf("g2b", [64, L, 256], BF16), name="g2b")
        fw.dma("pool", w2b.t[:], I["rw_w2"].rearrange("l k n -> k l n"), writes=[w2b.r], stream="parb", group=True)
        fw.dma("pool", a2b.t[:], I["rw_a2"].rearrange("l k n -> k l n"), writes=[a2b.r], stream="parb", group=True)
        fw.dma("pool", g2b.t[:], I["rw_g2"].rearrange("l k n -> k l n"), writes=[g2b.r], stream="parb", group=True)

        cT = Tt(fw.sbuf("cT", [128, 3, 8], F32), name="cT")
        fw.dma("sp", cT.t[:], I["cc"].rearrange("b (c p) -> p b c", p=128), writes=[cT.r], stream="par", group=True,
               allow_slow_non_contiguous=True)
        siluT = Tt(fw.sbuf("siluT", [128, 8, 3], F32), name="siluT")
        sl_e = Tt(fw.sbuf("sl_e", [128, 3, 8], F32), name="sl_e")
        fw.op("act", lambda e: e.activation(out=sl_e.t[:], in_=cT.t[:], func=AF.Exp, scale=-1.0),
              reads=[cT.r], writes=[sl_e.r])
        fw.op("dve", lambda e: e.tensor_scalar_add(out=sl_e.t[:], in0=sl_e.t[:], scalar1=1.0),
              reads=[sl_e.r], writes=[sl_e.r])
        fw.op("dve", lambda e: e.reciprocal(out=sl_e.t[:], in_=sl_e.t[:]), reads=[sl_e.r], writes=[sl_e.r])
        fw.op("dve", lambda e: e.tensor_tensor(out=siluT.t[:].rearrange("p c b -> p b c"), in0=sl_e.t[:],
                                               in1=cT.t[:], op=ALU.mult),
              reads=[sl_e.r, cT.r], writes=[siluT.r])
        mod = Tt(fw.sbuf("mod", [128, L, 6, 3, 8], F32), name="mod")
        G1 = Tt(fw.sbuf("G1", [128, L, 3, 8], F32), name="G1")
        G2 = Tt(fw.sbuf("G2", [128, L, 3, 8], F32), name="G2")
        PH0 = contextlib.ExitStack()
        fw_real = fw
        for ph in [PH0]:
            sub = FWScope(fw, ph)
            wa = [Tt(sub.sbuf("wa%d" % i, [128, 8, 512], F32), name="wa%d" % i) for i in range(2)]
            default_pool[0] = (0, 1, 2, 3)
            fw = Rec()
            k = 0
            for l in range(L):
                for jg in range(12):
                    w = wa[k % 2]
                    k += 1
                    fw.dma("sp" if k % 2 else "pool", w.t[:],
                           I["w_ada"][l].rearrange("(kc p) n -> p kc n", p=128)[:, :, jg * 512:(jg + 1) * 512],
                           writes=[w.r], stream="wada%d" % (k % 2))
                    for jj in range(4):
                        j = jg * 4 + jj
                        m, c = j // 8, j % 8
                        pb = next_bank()
                        for kc in range(8):
                            fw.op("pe", lambda e, w=w, pb=pb, kc=kc, jj=jj: e.matmul(
                                pb.t[:, 0:3], lhsT=w.t[:, kc, jj * 128:(jj + 1) * 128], rhs=siluT.t[:, kc, :],
                                start=(kc == 0), stop=(kc == 7)),
                                reads=[w.r, siluT.r], writes=[pb.r], signal=(kc == 7))
                        fw.op("dve", lambda e, pb=pb, l=l, m=m, c=c, j=j: e.tensor_scalar(
                            out=mod.t[:, l, m, :, c], in0=pb.t[:, 0:3], scalar1=badaT.t[:, l, j:j + 1], scalar2=None,
                            op0=ALU.add), reads=[pb.r, badaT.r], writes=[mod.r])
        for l in range(L):
            for (G, ng, mi) in ((G1, n1g, 1), (G2, n2g, 4)):
                for b in range(3):
                    fw.op("dve", lambda e, G=G, ng=ng, mi=mi, l=l, b=b: e.scalar_tensor_tensor(
                        out=G.t[:, l, b, :], in0=mod.t[:, l, mi, b, :], scalar=1.0, in1=ng.t[:, l, :],
                        op0=ALU.add, op1=ALU.mult), reads=[mod.r, ng.r], writes=[G.r])
        rec_ada = fw
        fw = fw_real

        def load_xT(xT, t0, w, q="sp"):
            fw.dma(q, xT.t[:, :, 0:w], xres.rearrange("(c p) t -> p c t", p=128)[:, :, t0:t0 + w],
                   reads=[xreg(t0)], writes=xT.rs, stream="xld" + xT.r.name[-2:])

        def store_xT(xT, t0, w, q="sp"):
            fw.dma(q, xres.rearrange("(c p) t -> p c t", p=128)[:, :, t0:t0 + w], xT.t[:, :, 0:w],
                   reads=xT.rs, writes=[xreg(t0)], stream="xst" + xT.r.name[-2:])

        def norm_fm(xT, w, sq, tmp, lnv, rstd, out, gap, shap, out_regs):
            fw.op("act", lambda e: e.activation(out=sq.t[:, :, 0:w], in_=xT.t[:, :, 0:w], func=AF.Square),
                  reads=xT.rs, writes=[sq.r])
            pb = next_bank()
            for c in range(8):
                fw.op("pe", lambda e, c=c: e.matmul(pb.t[:, 0:w], lhsT=onesb.t[:], rhs=sq.t[:, c, 0:w],
                                                    start=(c == 0), stop=(c == 7)),
                      reads=[onesb.r, sq.r], writes=[pb.r], signal=(c == 7))
            fw.op("act", lambda e: e.activation(out=lnv.t[:, 0:w], in_=pb.t[:, 0:w], func=AF.Ln, scale=1.0 / D,
                                                bias=epsb.t[:]), reads=[pb.r, epsb.r], writes=[lnv.r])
            fw.op("act", lambda e: e.activation(out=rstd.t[:, 0:w], in_=lnv.t[:, 0:w], func=AF.Exp, scale=-0.5),
                  reads=[lnv.r], writes=[rstd.r])
            for c in range(8):
                tm = tmp[c % len(tmp)]
                fw.op("dve", lambda e, c=c, tm=tm: e.tensor_tensor(out=tm.t[:, 0:w], in0=xT.t[:, c, 0:w],
                                                                   in1=rstd.t[:, 0:w], op=ALU.mult),
                      reads=[xT.rs[c], rstd.r], writes=[tm.r])
                if shap is not None:
                    fw.op("act", lambda e, c=c, tm=tm: e.activation(out=out.t[:, c, 0:w], in_=tm.t[:, 0:w],
                                                                    func=AF.Identity, scale=gap(c), bias=shap(c)),
                          reads=[tm.r, G1.r, G2.r, mod.r], writes=[out_regs[c]])
                else:
                    fw.op("act", lambda e, c=c, tm=tm: e.activation(out=out.t[:, c, 0:w], in_=tm.t[:, 0:w],
                                                                    func=AF.Identity, scale=gap(c)),
                          reads=[tm.r, fng.r], writes=[out_regs[c]])

        for ph in [PH0]:
            sub = FWScope(fw, ph)
            xtok = [Tt(sub.sbuf("xtok%d" % i, [128, 4, D], F32), name="xtok%d" % i) for i in range(2)]
            xTs = [Tt(sub.sbuf("xTp%d" % i, [128, 8, 512], F32), nreg=8, name="xTp%d" % i) for i in range(2)]
            default_pool[0] = (4, 5, 6, 7)
            fw = Rec()
            it = 0
            for s in range(3):
                src = I["xp"][s] if s < 2 else I["xs"]
                off, T = seqs[s]
                for (t0, w, j) in tiles_of(s, 512):
                    xt = xtok[it % 2]
                    xT = xTs[it % 2]
                    it += 1
                    nb = (w + 127) // 128
                    pw = min(w, 128)
                    lt0 = t0 - off
                    if w >= 128:
                        fw.dma("sp" if it % 2 else "pool", xt.t[:, 0:nb, :],
                               src[lt0:lt0 + w, :].rearrange("(b p) d -> p b d", p=128), writes=[xt.r], stream="xtok")
                    else:
                        fw.dma("sp", xt.t[0:w, 0, :], src[lt0:lt0 + w, :], writes=[xt.r], stream="xtok")
                    for c in range(8):
                        pb = next_bank()
                        for tb in range(nb):
                            fw.op("pe", lambda e, c=c, tb=tb, pb=pb, xt=xt, pw=pw: e.transpose(
                                pb.t[:, tb * 128:tb * 128 + pw], xt.t[0:pw, tb, c * 128:(c + 1) * 128],
                                ident.t[0:pw, 0:pw]),
                                reads=[xt.r, ident.r], writes=[pb.r], signal=(tb == nb - 1))
                        eng = "act" if c % 2 else "dve"
                        if eng == "act":
                            fw.op("act", lambda e, c=c, pb=pb, xT=xT, w=w: e.activation(
                                out=xT.t[:, c, 0:w], in_=pb.t[:, 0:w], func=AF.Copy), reads=[pb.r], writes=[xT.rs[c]])
                        else:
                            fw.op("dve", lambda e, c=c, pb=pb, xT=xT, w=w: e.tensor_copy(
                                out=xT.t[:, c, 0:w], in_=pb.t[:, 0:w]), reads=[pb.r], writes=[xT.rs[c]])
                    store_xT(xT, t0, w, q="sp" if it % 2 else "pool")
            rec_pro = fw
            fw = fw_real
            default_pool[0] = (0, 1, 2, 3, 4, 5, 6, 7)
            merge_recs(fw, [rec_ada, rec_pro])
            fw.flush()
        PH0.close()

        for l in range(L):
            fw.rotate()
            if stage >= 2:
              with contextlib.ExitStack() as ph:
                sub = FWScope(fw, ph)
                default_pool[0] = (0, 1, 2, 3, 4, 5)
                WA = 512
                FC0 = RW_COLS
                TKMAX = max(SEQ, PAST + TS)
                NBMAX = (TKMAX + 127) // 128
                wf = Tt(sub.sbuf("wf", [128, 8, FOX_COLS], BF16), name="wf")
                for kc in range(8):
                    fw.dma("pool", wf.t[:, kc, :], I["w_in"][l][kc * 128:(kc + 1) * 128, FC0:FC0 + FOX_COLS],
                           writes=[wf.r], stream="wld", group=True)
                KT = Tt(sub.sbuf("KT", [128, 4, TKMAX], BF16), nreg=NBMAX, name="KT")
                Vx = Tt(sub.sbuf("Vx", [128, NBMAX, 8, 65], BF16), nreg=NBMAX, name="Vx")
                fw.op("pool", lambda e: e.memset(Vx.t[:], 1.0), writes=Vx.rs)
                ctok = Tt(sub.sbuf("ctok", [128, NBMAX, 8], F32), nreg=NBMAX, name="ctok")
                negc = Tt(sub.sbuf("negc", [128, NBMAX, 8], F32), nreg=NBMAX, name="negc")
                xTa = [Tt(sub.sbuf("xTa%d" % i, [128, 8, WA], F32), nreg=8, name="xTa%d" % i) for i in range(2)]
                sq = Tt(sub.sbuf("sqa", [128, 8, WA], BF16), name="sqa")
                hT = Tt(sub.sbuf("hTa", [128, 8, WA], BF16), nreg=8, name="hTa")
                tmp = [Tt(sub.sbuf("tmpa%d" % i, [128, WA], F32), name="tmpa%d" % i) for i in range(2)]
                lnv = Tt(sub.sbuf("lnva", [128, WA], F32), name="lnva")
                rstd = Tt(sub.sbuf("rstda", [128, WA], F32), name="rstda")
                QT = Tt(sub.sbuf("QT", [128, 4, WA], BF16), nreg=4, name="QT")
                ktok = [Tt(sub.sbuf("ktok%d" % i, [128, 512], F32), name="ktok%d" % i) for i in range(2)]
                vtok = [Tt(sub.sbuf("vtok%d" % i, [128, 512], F32), name="vtok%d" % i) for i in range(2)]
                ftok = [Tt(sub.sbuf("ftok%d" % i, [128, 8], F32), name="ftok%d" % i) for i in range(2)]
                ltok = [Tt(sub.sbuf("ltok%d" % i, [128, 8], F32), name="ltok%d" % i) for i in range(2)]
                cfm = Tt(sub.sbuf("cfm", [8, WA], F32), name="cfm")
                cr1 = Tt(sub.sbuf("cr1", [8, WA], F32), name="cr1")
                cr2 = Tt(sub.sbuf("cr2", [8, WA], F32), name="cr2")
                midt = Tt(sub.sbuf("midt", [8, WA], BF16), name="midt")
                cq96 = Tt(sub.sbuf("cq96", [72, WA], BF16), name="cq96")
                fw.op("pool", lambda e: e.memset(cq96.t[:], 0.0), writes=[cq96.r])
                pts = [Tt(sub.sbuf("pt%d" % i, [128, WA], BF16), name="pt%d" % i) for i in range(4)]
                rsf = Tt(sub.sbuf("rsf", [128, WA], F32), name="rsf")
                rcp = Tt(sub.sbuf("rcp", [64, WA], F32), name="rcp")
                yfT = Tt(sub.sbuf("yfT", [128, 4, WA], BF16), nreg=4, name="yfT")
                it = 0
                ik = 0
                ipt = 0
                a1_tiles = [(s_, t0_, w_) for s_ in range(3) for (t0_, w_, j_) in tiles_of(s_, WA)]
                a1_idx = [0]

                def a1_norm(idx):
                    s_, t0_, w_ = a1_tiles[idx]
                    xT_ = xTa[idx % 2]
                    load_xT(xT_, t0_, w_)
                    norm_fm(xT_, w_, sq, tmp, lnv, rstd, hT,
                            lambda c, s_=s_: G1.t[:, l, s_, c:c + 1], lambda c, s_=s_: mod.t[:, l, 0, s_, c:c + 1], hT.rs)
                for s in range(3):
                    off, T = seqs[s]
                    kbase = 0
                    if s == 2:
                        kbase = PAST
                        for cb in range(PAST // 128):
                            kt_ = ktok[ik % 2]
                            vt_ = vtok[ik % 2]
                            ft_ = ltok[ik % 2]
                            ik += 1
                            fw.dma("sp", kt_.t[:], I["ck"][l][cb * 128:(cb + 1) * 128, :], writes=[kt_.r], stream="cldk%d" % (ik % 2))
                            fw.dma("sp", vt_.t[:], I["cv"][l][cb * 128:(cb + 1) * 128, :], writes=[vt_.r], stream="cldv%d" % (ik % 2))
                            fw.dma("sp", ft_.t[:], I["cl"][l][cb * 128:(cb + 1) * 128, :], writes=[ft_.r], stream="cldf%d" % (ik % 2))
                            pb = next_bank()
                            for pc in range(4):
                                fw.op("pe", lambda e, pc=pc, pb=pb, kt_=kt_: e.transpose(
                                    pb.t[:, pc * 128:(pc + 1) * 128], kt_.t[:, pc * 128:(pc + 1) * 128], ident.t[:]),
                                    reads=[kt_.r, ident.r], writes=[pb.r], signal=(pc == 3))
                            fw.op("act", lambda e, pb=pb, cb=cb: e.activation(
                                out=KT.t[:, :, cb * 128:(cb + 1) * 128],
                                in_=pb.t[:, :].rearrange("p (c t) -> p c t", c=4), func=AF.Copy),
                                reads=[pb.r], writes=[KT.rs[cb]])
                            fw.op("dve", lambda e, vt_=vt_, cb=cb: e.tensor_copy(
                                out=Vx.t[:, cb, :, 0:64], in_=vt_.t[:, :].rearrange("p (h d) -> p h d", h=8)),
                                reads=[vt_.r], writes=[Vx.rs[cb]])
                            pc_ = next_bank()
                            fw.op("pe", lambda e, pc_=pc_, ft_=ft_, cb=cb: e.matmul(
                                pc_.t[:, 0:8], lhsT=trif.t[:], rhs=ft_.t[:], start=True, stop=(cb == 0)),
                                reads=[trif.r, ft_.r], writes=[pc_.r], signal=(cb == 0))
                            if cb > 0:
                                fw.op("pe", lambda e, pc_=pc_, cb=cb: e.matmul(
                                    pc_.t[:, 0:8], lhsT=self127.t[:], rhs=ctok.t[:, cb - 1, :], start=False, stop=True),
                                    reads=[self127.r, ctok.rs[cb - 1]], writes=[pc_.r])
                            fw.op("dve", lambda e, pc_=pc_, cb=cb: e.tensor_copy(out=ctok.t[:, cb, :], in_=pc_.t[:, 0:8]),
                                  reads=[pc_.r], writes=[ctok.rs[cb]])
                            fw.op("act", lambda e, pc_=pc_, cb=cb: e.activation(
                                out=negc.t[:, cb, :], in_=pc_.t[:, 0:8], func=AF.Copy, scale=-1.0),
                                reads=[pc_.r], writes=[negc.rs[cb]])
                    dK = O["fkp"][l][s] if s < 2 else O["fks"][l]
                    dV = O["fvp"][l][s] if s < 2 else O["fvs"][l]
                    dF = O["flp"][l][s] if s < 2 else O["fls"][l]
                    def a1_tile(s, t0, w, j, xT, off, kbase, dK, dV, dF, l=l):
                        nonlocal ik, ipt
                        lt0 = t0 - off
                        kt0 = kbase + lt0
                        nb = (w + 127) // 128
                        pw = min(w, 128)
                        if a1_idx[0] == 0:
                            a1_norm(0)
                        for pc in range(4):
                            pq = next_bank()
                            for kc in range(8):
                                fw.op("pe", lambda e, kc=kc, pc=pc, pq=pq, w=w: e.matmul(
                                    pq.t[:, 0:w], lhsT=wf.t[:, kc, pc * 128:(pc + 1) * 128], rhs=hT.t[:, kc, 0:w],
                                    start=(kc == 0), stop=(kc == 7)),
                                    reads=[wf.r, hT.rs[kc]], writes=[pq.r], signal=(kc == 7))
                            fw.op("act", lambda e, pc=pc, pq=pq, w=w: e.activation(
                                out=QT.t[:, pc, 0:w], in_=pq.t[:, 0:w], func=AF.Copy, scale=0.125),
                                reads=[pq.r], writes=[QT.rs[pc]])
                            pk = next_bank()
                            for kc in range(8):
                                fw.op("pe", lambda e, kc=kc, pc=pc, pk=pk, w=w: e.matmul(
                                    pk.t[:, 0:w], lhsT=wf.t[:, kc, 512 + pc * 128:512 + (pc + 1) * 128],
                                    rhs=hT.t[:, kc, 0:w], start=(kc == 0), stop=(kc == 7)),
                                    reads=[wf.r, hT.rs[kc]], writes=[pk.r], signal=(kc == 7))
                            kregs = [KT.rs[(kt0 + tb * 128) // 128] for tb in range(nb)]
                            fw.op("dve", lambda e, pc=pc, pk=pk, w=w, kt0=kt0: e.tensor_copy(
                                out=KT.t[:, pc, kt0:kt0 + w], in_=pk.t[:, 0:w]), reads=[pk.r], writes=kregs)
                        for tb in range(nb):
                            kb = (kt0 + tb * 128) // 128
                            kt_ = ktok[ik % 2]
                            vt_ = vtok[ik % 2]
                            ft_ = ftok[ik % 2]
                            lt_ = ltok[ik % 2]
                            ik += 1
                            for (dst_t, c0, ncol, dd, eng) in ((kt_, 512, 512, dK, "act"), (vt_, 1024, 512, dV, "dve")):
                                pb = next_bank()
                                for kc in range(8):
                                    fw.op("pe", lambda e, kc=kc, pb=pb, tb=tb, c0=c0, ncol=ncol, pw=pw: e.matmul(
                                        pb.t[0:pw, 0:ncol], lhsT=hT.t[:, kc, tb * 128:tb * 128 + pw],
                                        rhs=wf.t[:, kc, c0:c0 + ncol], start=(kc == 0), stop=(kc == 7)),
                                        reads=[wf.r, hT.rs[kc]], writes=[pb.r], signal=(kc == 7))
                                if eng == "act":
                                    fw.op("act", lambda e, pb=pb, dst_t=dst_t, pw=pw: e.activation(
                                        out=dst_t.t[0:pw, :], in_=pb.t[0:pw, :], func=AF.Copy),
                                        reads=[pb.r], writes=[dst_t.r])
                                else:
                                    fw.op("dve", lambda e, pb=pb, dst_t=dst_t, pw=pw: e.tensor_copy(
                                        out=dst_t.t[0:pw, :], in_=pb.t[0:pw, :]), reads=[pb.r], writes=[dst_t.r])
                                r0 = lt0 + tb * 128
                                fw.dma("sp", dd[r0:r0 + pw, :], dst_t.t[0:pw, :], reads=[dst_t.r],
                                       stream="kvo%s%d" % (eng[0], ik % 2))
                            fw.op("pool", lambda e, vt_=vt_, kb=kb, pw=pw: e.tensor_copy(
                                out=Vx.t[0:pw, kb, :, 0:64], in_=vt_.t[0:pw, :].rearrange("p (h d) -> p h d", h=8)),
                                reads=[vt_.r], writes=[Vx.rs[kb]])
                            pf = next_bank()
                            for kc in range(8):
                                fw.op("pe", lambda e, kc=kc, pf=pf, tb=tb, pw=pw: e.matmul(
                                    pf.t[0:pw, 0:8], lhsT=hT.t[:, kc, tb * 128:tb * 128 + pw],
                                    rhs=wf.t[:, kc, 1536:1544], start=(kc == 0), stop=(kc == 7)),
                                    reads=[wf.r, hT.rs[kc]], writes=[pf.r], signal=(kc == 7))
                            fw.op("dve", lambda e, pf=pf, ft_=ft_, pw=pw: e.tensor_tensor(
                                out=ft_.t[0:pw, :], in0=pf.t[0:pw, 0:8], in1=bfb.t[0:pw, l, :], op=ALU.add),
                                reads=[pf.r, bfb.r], writes=[ft_.r])
                            fw.op("act", lambda e, ft_=ft_, pw=pw: e.activation(
                                out=ft_.t[0:pw, :], in_=ft_.t[0:pw, :], func=AF.Exp, scale=-1.0),
                                reads=[ft_.r], writes=[ft_.r])
                            fw.op("act", lambda e, ft_=ft_, pw=pw: e.activation(
                                out=ft_.t[0:pw, :], in_=ft_.t[0:pw, :], func=AF.Ln, bias=1.0),
                                reads=[ft_.r], writes=[ft_.r])
                            fw.op("dve", lambda e, ft_=ft_, lt_=lt_, pw=pw: e.tensor_scalar_mul(
                                out=lt_.t[0:pw, :], in0=ft_.t[0:pw, :], scalar1=-1.0), reads=[ft_.r], writes=[lt_.r])
                            r0 = lt0 + tb * 128
                            fw.dma("sp", dF[r0:r0 + pw, :], lt_.t[0:pw, :], reads=[lt_.r], stream="kvof%d" % (ik % 2))
                            pc_ = next_bank()
                            first = (kb == 0)
                            fw.op("pe", lambda e, pc_=pc_, lt_=lt_, pw=pw, first=first: e.matmul(
                                pc_.t[0:pw, 0:8], lhsT=trif.t[0:pw, 0:pw], rhs=lt_.t[0:pw, :], start=True, stop=first),
                                reads=[trif.r, lt_.r], writes=[pc_.r], signal=first)
                            if not first:
                                fw.op("pe", lambda e, pc_=pc_, kb=kb, pw=pw: e.matmul(
                                    pc_.t[0:pw, 0:8], lhsT=self127.t[:, 0:pw], rhs=ctok.t[:, kb - 1, :],
                                    start=False, stop=True),
                                    reads=[self127.r, ctok.rs[kb - 1]], writes=[pc_.r])
                            fw.op("dve", lambda e, pc_=pc_, kb=kb, pw=pw: e.tensor_copy(
                                out=ctok.t[0:pw, kb, :], in_=pc_.t[0:pw, 0:8]), reads=[pc_.r], writes=[ctok.rs[kb]])
                            fw.op("act", lambda e, pc_=pc_, kb=kb, pw=pw: e.activation(
                                out=negc.t[0:pw, kb, :], in_=pc_.t[0:pw, 0:8], func=AF.Copy, scale=-1.0),
                                reads=[pc_.r], writes=[negc.rs[kb]])
                            pt_ = next_bank()
                            fw.op("pe", lambda e, pt_=pt_, kb=kb, pw=pw: e.transpose(
                                pt_.t[0:8, 0:pw], ctok.t[0:pw, kb, :], ident.t[0:pw, 0:pw]),
                                reads=[ctok.rs[kb], ident.r], writes=[pt_.r])
                            fw.op("dve", lambda e, pt_=pt_, tb=tb, pw=pw: e.tensor_copy(
                                out=cfm.t[:, tb * 128:tb * 128 + pw], in_=pt_.t[0:8, 0:pw]),
                                reads=[pt_.r], writes=[cfm.r])
                        fw.op("act", lambda e, w=w: e.activation(out=cq96.t[0:8, 0:w], in_=cfm.t[:, 0:w], func=AF.Copy),
                              reads=[cfm.r], writes=[cq96.r])
                        fw.op("dve", lambda e, w=w: e.tensor_tensor(out=cr1.t[:, 0:w], in0=cfm.t[:, 0:w],
                                                                    in1=cq96.t[0:8, 0:w], op=ALU.subtract),
                              reads=[cfm.r, cq96.r], writes=[cr1.r])
                        fw.op("act", lambda e, w=w: e.activation(out=midt.t[:, 0:w], in_=cr1.t[:, 0:w], func=AF.Copy),
                              reads=[cr1.r], writes=[midt.r])
                        fw.op("pool", lambda e, w=w: e.tensor_copy(out=cq96.t[32:40, 0:w], in_=midt.t[:, 0:w]),
                              reads=[midt.r], writes=[cq96.r])
                        fw.op("dve", lambda e, w=w: e.tensor_tensor(out=cr2.t[:, 0:w], in0=cr1.t[:, 0:w],
                                                                    in1=midt.t[:, 0:w], op=ALU.subtract),
                              reads=[cr1.r, midt.r], writes=[cr2.r])
                        fw.op("act", lambda e, w=w: e.activation(out=cq96.t[64:72, 0:w], in_=cr2.t[:, 0:w], func=AF.Copy),
                              reads=[cr2.r], writes=[cq96.r])
                        a1_idx[0] += 1
                        if a1_idx[0] < len(a1_tiles):
                            a1_norm(a1_idx[0])
                        kb_first_tile = kt0 // 128
                        nkb = kb_first_tile + nb
                        pending_epi = []
                        for h in range(8):
                            hr = slice((h % 2) * 64, (h % 2) * 64 + 64)
                            hp = h // 2
                            ob = banks[6 + (h % 2)]
                            blocks = []
                            for kb in range(nkb):
                                if kb < kb_first_tile:
                                    q0, rows, diag = 0, 128, False
                                else:
                                    q0, rows, diag = (kb - kb_first_tile) * 128, pw, True
                                blocks.append((kb, q0, rows, diag))
                            sbanks = {}
                            ptl = {}

                            def emit_s(bi, h=h, hr=hr, hp=hp):
                                kb, q0, rows, diag = blocks[bi]
                                sb = next_bank(pool=(0, 1, 2, 3))
                                sbanks[bi] = sb
                                fw.op("pe", lambda e, sb=sb, kb=kb, q0=q0, rows=rows: e.matmul(
                                    sb.t[0:rows, q0:w], lhsT=KT.t[hr, hp, kb * 128:kb * 128 + rows],
                                    rhs=QT.t[hr, hp, q0:w], start=True, stop=False),
                                    reads=[KT.rs[kb], QT.rs[hp]], writes=[sb.r], signal=False)
                                fw.op("pe", lambda e, sb=sb, q0=q0, rows=rows: e.matmul(
                                    sb.t[0:rows, q0:w], lhsT=selh.t[0:72, h, 0:rows], rhs=cq96.t[0:72, q0:w],
                                    start=False, stop=(not diag)),
                                    reads=[selh.r, cq96.r], writes=[sb.r], signal=(not diag))
                                if diag:
                                    fw.op("pe", lambda e, sb=sb, q0=q0, rows=rows: e.matmul(
                                        sb.t[0:rows, q0:q0 + rows], lhsT=identb.t[0:rows, 0:rows],
                                        rhs=maskneg.t[0:rows, 0:rows], start=False, stop=True),
                                        reads=[identb.r, maskneg.r], writes=[sb.r])

                            def emit_pv(bi, h=h, ob=ob):
                                nonlocal ipt
                                kb, q0, rows, diag = blocks[bi]
                                sb = sbanks.pop(bi)
                                pt = pts[ipt % 4]
                                ipt += 1
                                fw.op("act", lambda e, sb=sb, pt=pt, kb=kb, q0=q0, rows=rows: e.activation(
                                    out=pt.t[0:rows, q0:w], in_=sb.t[0:rows, q0:w], func=AF.Exp,
                                    bias=negc.t[0:rows, kb, h:h + 1]),
                                    reads=[sb.r, negc.rs[kb]], writes=[pt.r])
                                last = (bi == len(blocks) - 1)
                                fw.op("pe", lambda e, pt=pt, kb=kb, q0=q0, rows=rows, bi=bi, last=last: e.matmul(
                                    ob.t[0:65, q0:w], lhsT=Vx.t[0:rows, kb, h, :], rhs=pt.t[0:rows, q0:w],
                                    start=(bi == 0), stop=last),
                                    reads=[Vx.rs[kb], pt.r], writes=[ob.r], signal=last)
                            LOOK = 2
                            nbk = len(blocks)
                            for bi in range(min(LOOK, nbk)):
                                emit_s(bi)
                            for bi in range(nbk):
                                emit_pv(bi)
                                if bi + LOOK < nbk:
                                    emit_s(bi + LOOK)
                            def epilogue(ob=ob, hr=hr, hp=hp):
                                fw.op("act", lambda e, ob=ob, w=w: e.activation(
                                    out=rsf.t[64:65, 0:w], in_=ob.t[64:65, 0:w], func=AF.Copy), reads=[ob.r], writes=[rsf.r])
                                pr = next_bank(pool=(4, 5))
                                fw.op("pe", lambda e, pr=pr, w=w: e.matmul(
                                    pr.t[0:64, 0:w], lhsT=onesf.t[64:65, 0:64], rhs=rsf.t[64:65, 0:w], start=True, stop=True),
                                    reads=[onesf.r, rsf.r], writes=[pr.r])
                                fw.op("dve", lambda e, pr=pr, w=w: e.reciprocal(out=rcp.t[:, 0:w], in_=pr.t[0:64, 0:w]),
                                      reads=[pr.r], writes=[rcp.r])
                                fw.op("dve", lambda e, ob=ob, hr=hr, hp=hp, w=w: e.tensor_tensor(
                                    out=yfT.t[hr, hp, 0:w], in0=ob.t[0:64, 0:w], in1=rcp.t[:, 0:w], op=ALU.mult),
                                    reads=[ob.r, rcp.r], writes=[yfT.rs[hp]])
                            if pending_epi:
                                pending_epi.pop()()
                            pending_epi.append(epilogue)
                        while pending_epi:
                            pending_epi.pop()()
                        fw.dma("sp", yfox.rearrange("(c p) t -> p c t", p=128)[:, :, t0:t0 + w], yfT.t[:, :, 0:w],
                               reads=yfT.rs, writes=[yfreg(t0)], stream="yfst")
                    for (t0, w, j) in tiles_of(s, WA):
                        xT = xTa[it % 2]
                        it += 1
                        a1_tile(s, t0, w, j, xT, off, kbase, dK, dV, dF)
                fw.flush()
                default_pool[0] = (0, 1, 2, 3, 4, 5, 6, 7)
            if stage >= 2:
              with contextlib.ExitStack() as ph:
                sub = FWScope(fw, ph)
                default_pool[0] = (0, 1, 2, 3, 4)
                sink = [fw]
                WA = 256
                RW0, HG0 = 0, RW_COLS + FOX_COLS
                NWI = RW_COLS + HG_COLS
                wi = Tt(sub.sbuf("wi", [128, 8, NWI], BF16), name="wi")
                wo = Tt(sub.sbuf("wo", [128, 8, D], BF16), name="wo")
                for kc in range(8):
                    fw.dma("pool", wi.t[:, kc, 0:RW_COLS], I["w_in"][l][kc * 128:(kc + 1) * 128, 0:RW_COLS],
                           writes=[wi.r], stream="wld", group=True)
                    fw.dma("pool", wi.t[:, kc, RW_COLS:NWI], I["w_in"][l][kc * 128:(kc + 1) * 128, HG0:HG0 + HG_COLS],
                           writes=[wi.r], stream="wld", group=True)
                    fw.dma("pool", wo.t[:, kc, :], I["w_out"][l][kc * 128:(kc + 1) * 128, :], writes=[wo.r], stream="wld", group=True)
                xTa = [Tt(sub.sbuf("xTa%d" % i, [128, 8, WA], F32), nreg=8, name="xTa%d" % i) for i in range(2)]
                sq = Tt(sub.sbuf("sqa", [128, 8, WA], BF16), name="sqa")
                hT = Tt(sub.sbuf("hTa", [128, 8, WA], BF16), nreg=8, name="hTa")
                tmp = [Tt(sub.sbuf("tmpa%d" % i, [128, WA], F32), name="tmpa%d" % i) for i in range(2)]
                lnv = Tt(sub.sbuf("lnva", [128, WA], F32), name="lnva")
                rstd = Tt(sub.sbuf("rstda", [128, WA], F32), name="rstda")
                ymix = Tt(sub.sbuf("ymix", [128, 8, WA], BF16), nreg=8, name="ymix")
                fw.op("pool", lambda e: e.memset(ymix.t[:], 0.0), writes=ymix.rs)

                def S(name, shape=None, dt=F32, nreg=1):
                    return Tt(sub.sbuf(name, shape or [128, WA], dt), nreg=nreg, name=name)

                def proj_fm(c0, w, M=128):
                    pb = next_bank()
                    for kc in range(8):
                        sink[0].op("pe", lambda e, kc=kc: e.matmul(
                            pb.t[0:M, 0:w], lhsT=wi.t[:, kc, c0:c0 + M], rhs=hT.t[:, kc, 0:w],
                            start=(kc == 0), stop=(kc == 7)),
                            reads=[wi.r, hT.rs[kc]], writes=[pb.r], signal=(kc == 7))
                    return pb

                NCH = WA // 32
                hg_E = [S("hg_E%d" % i) for i in range(2)]
                hg_KK = [S("hg_KK%d" % i) for i in range(2)]
                hg_B = [S("hg_B%d" % i) for i in range(2)]
                hg_D = S("hg_D")
                hg_X = S("hg_X")
                hg_Q = S("hg_Q")
                hg_G = [S("hg_G%d" % i) for i in range(2)]
                hg_Qt = S("hg_Qt", [128, 2, WA], BF16, nreg=2)
                hg_Kh = S("hg_Kh", [128, 2, WA], BF16, nreg=2)
                hg_Ke = S("hg_Ke", [128, 2, WA], BF16, nreg=2)
                hg_ebl = S("hg_ebl", [128, 2, NCH], F32, nreg=2)
                hg_ebm = S("hg_ebm", [128, 2, NCH], F32, nreg=2)
                hg_Vh = S("hg_Vh", [128, WA // 128, 256], BF16, nreg=4)
                hg_KeT = S("hg_KeT", [128, WA // 128, 4, 256], BF16, nreg=4)
                hg_AT = S("hg_AT", [128, WA // 128, 2, 2, 128], BF16, nreg=4)
                hg_Sm = S("hg_Sm", [128, 2, 128], F32, nreg=2)
                hg_Sbd = S("hg_Sbd", [128, 2, 128], BF16, nreg=2)
                hg_sq = S("hg_sq", [128, WA], BF16)
                hg_t1 = S("hg_t1")

                def hg_init(s):
                    fw.op("pool", lambda e: e.memset(hg_Sm.t[:], 0.0), writes=hg_Sm.rs)
                    if s == 2:
                        for h in range(4):
                            hr = slice((h % 2) * 64, (h % 2) * 64 + 64)
                            fw.dma("sp", hg_Sm.t[hr, h // 2, (h % 2) * 64:(h % 2) * 64 + 64], I["shg"][l][h],
                                   writes=[hg_Sm.rs[h // 2]], stream="stld", group=True)

                def hg_final(s):
                    dst = O["hgp"][l][s] if s < 2 else O["hgs"][l]
                    for h in range(4):
                        hr = slice((h % 2) * 64, (h % 2) * 64 + 64)
                        fw.dma("sp", dst[h], hg_Sm.t[hr, h // 2, (h % 2) * 64:(h % 2) * 64 + 64],
                               reads=[hg_Sm.rs[h // 2]], stream="ststhg%d" % s, group=True)

                def hg_tile(s, w):
                    nch = w // 32
                    nb = (w + 127) // 128
                    pw = min(w, 128)
                    c_q, c_f, c_i, c_g = RW_COLS, RW_COLS + 256, RW_COLS + 512, RW_COLS + 768
                    for tb in range(nb):
                        pb = next_bank()
                        for kc in range(8):
                            sink[0].op("pe", lambda e, kc=kc, tb=tb, pb=pb: e.matmul(
                                pb.t[0:pw, 0:256], lhsT=hT.t[:, kc, tb * 128:tb * 128 + pw], rhs=wi.t[:, kc, c_i:c_i + 256],
                                start=(kc == 0), stop=(kc == 7)),
                                reads=[wi.r, hT.rs[kc]], writes=[pb.r], signal=(kc == 7))
                        sink[0].op("act", lambda e, tb=tb, pb=pb: e.activation(
                            out=hg_Vh.t[0:pw, tb, :], in_=pb.t[0:pw, 0:256], func=AF.Copy),
                            reads=[pb.r], writes=[hg_Vh.rs[tb]])
                    for pc in range(2):
                        E, KK, B, G = hg_E[pc], hg_KK[pc], hg_B[pc], hg_G[pc]
                        lb_ap = lbT.t[:, l, pc:pc + 1]
                        oml_ap = omlT.t[:, l, pc:pc + 1]
                        noml_ap = nomlT.t[:, l, pc:pc + 1]
                        pf = proj_fm(c_f + pc * 128, w)
                        sink[0].op("act", lambda e, pf=pf, E=E: e.activation(out=E.t[:, 0:w], in_=pf.t[:, 0:w], func=AF.Exp,
                                                                      scale=-1.0), reads=[pf.r], writes=[E.r])
                        sink[0].op("dve", lambda e, E=E: e.tensor_scalar_add(out=E.t[:, 0:w], in0=E.t[:, 0:w], scalar1=1.0),
                              reads=[E.r], writes=[E.r])
                        sink[0].op("dve", lambda e, E=E: e.reciprocal(out=E.t[:, 0:w], in_=E.t[:, 0:w]),
                              reads=[E.r], writes=[E.r])
                        sink[0].op("dve", lambda e, E=E, KK=KK, noml_ap=noml_ap, oml_ap=oml_ap: e.tensor_scalar(
                            out=KK.t[:, 0:w], in0=E.t[:, 0:w], scalar1=noml_ap, scalar2=oml_ap, op0=ALU.mult, op1=ALU.add),
                            reads=[E.r, nomlT.r, omlT.r], writes=[KK.r])
                        sink[0].op("act", lambda e, E=E, oml_ap=oml_ap, lb_ap=lb_ap: e.activation(
                            out=E.t[:, 0:w], in_=E.t[:, 0:w], func=AF.Ln, scale=oml_ap, bias=lb_ap),
                            reads=[E.r, omlT.r, lbT.r], writes=[E.r])
                        sink[0].op("dve", lambda e, E=E, B=B: e.tensor_tensor_scan(
                            out=B.t[:, 0:w], data0=rmask32.t[:, 0:w], data1=E.t[:, 0:w], initial=0.0,
                            op0=ALU.mult, op1=ALU.add), reads=[E.r, rmask32.r], writes=[B.r])
                        Bv = B.t[:, 0:w].rearrange("p (c t) -> p c t", t=32)
                        Dv = hg_D.t[:, 0:w].rearrange("p (c t) -> p c t", t=32)
                        sink[0].op("dve", lambda e, Bv=Bv, Dv=Dv: e.tensor_tensor(
                            out=Dv, in0=Bv, in1=Bv[:, :, 15:16].to_broadcast([128, nch, 32]), op=ALU.subtract),
                            reads=[B.r], writes=[hg_D.r])
                        pq = proj_fm(c_q + pc * 128, w)
                        sink[0].op("act", lambda e, pq=pq: e.activation(out=hg_Q.t[:, 0:w], in_=pq.t[:, 0:w], func=AF.Copy),
                              reads=[pq.r], writes=[hg_Q.r])
                        sink[0].op("act", lambda e: e.activation(out=hg_X.t[:, 0:w], in_=hg_D.t[:, 0:w], func=AF.Exp),
                              reads=[hg_D.r], writes=[hg_X.r])
                        sink[0].op("dve", lambda e, pc=pc: e.tensor_tensor(out=hg_Qt.t[:, pc, 0:w], in0=hg_Q.t[:, 0:w],
                                                                      in1=hg_X.t[:, 0:w], op=ALU.mult),
                              reads=[hg_Q.r, hg_X.r], writes=[hg_Qt.rs[pc]])
                        sink[0].op("act", lambda e: e.activation(out=hg_X.t[:, 0:w], in_=hg_D.t[:, 0:w], func=AF.Exp, scale=-1.0),
                              reads=[hg_D.r], writes=[hg_X.r])
                        sink[0].op("dve", lambda e, pc=pc, KK=KK: e.tensor_tensor(out=hg_Kh.t[:, pc, 0:w], in0=KK.t[:, 0:w],
                                                                             in1=hg_X.t[:, 0:w], op=ALU.mult),
                              reads=[KK.r, hg_X.r], writes=[hg_Kh.rs[pc]])
                        sink[0].op("dve", lambda e, Bv=Bv, Dv=Dv: e.tensor_tensor(
                            out=Dv, in0=Bv[:, :, 31:32].to_broadcast([128, nch, 32]), in1=Bv, op=ALU.subtract),
                            reads=[B.r], writes=[hg_D.r])
                        sink[0].op("act", lambda e: e.activation(out=hg_X.t[:, 0:w], in_=hg_D.t[:, 0:w], func=AF.Exp),
                              reads=[hg_D.r], writes=[hg_X.r])
                        sink[0].op("dve", lambda e, pc=pc, KK=KK: e.tensor_tensor(out=hg_Ke.t[:, pc, 0:w], in0=KK.t[:, 0:w],
                                                                             in1=hg_X.t[:, 0:w], op=ALU.mult),
                              reads=[KK.r, hg_X.r], writes=[hg_Ke.rs[pc]])
                        sink[0].op("act", lambda e, pc=pc, Bv=Bv: e.activation(out=hg_ebl.t[:, pc, 0:nch], in_=Bv[:, :, 31],
                                                                          func=AF.Exp), reads=[B.r], writes=[hg_ebl.rs[pc]])
                        sink[0].op("act", lambda e, pc=pc, Bv=Bv: e.activation(out=hg_ebm.t[:, pc, 0:nch], in_=Bv[:, :, 15],
                                                                          func=AF.Exp), reads=[B.r], writes=[hg_ebm.rs[pc]])
                        pg = proj_fm(c_g + pc * 128, w)
                        sink[0].op("act", lambda e, pg=pg, G=G: e.activation(out=G.t[:, 0:w], in_=pg.t[:, 0:w], func=AF.Silu),
                              reads=[pg.r], writes=[G.r])
                    HGDBG = int(os.environ.get("HGDBG", "9"))
                    if HGDBG < 2:
                        return
                    for tb in range(nb):
                        pAs = [next_bank(), next_bank()]
                        for h in range(4):
                            hr = slice((h % 2) * 64, (h % 2) * 64 + 64)
                            pA = pAs[h % 2]
                            sink[0].op("pe", lambda e, h=h, hr=hr, tb=tb, pA=pA: e.matmul(
                                pA.t[0:pw, (h // 2) * 128:(h // 2) * 128 + pw], lhsT=hg_Kh.t[hr, h // 2, tb * 128:tb * 128 + pw],
                                rhs=hg_Qt.t[hr, h // 2, tb * 128:tb * 128 + pw], start=True, stop=True),
                                reads=[hg_Kh.rs[h // 2], hg_Qt.rs[h // 2]], writes=[pA.r], signal=(h >= 2))
                        for par in range(2):
                            pA = pAs[par]
                            sink[0].op("dve", lambda e, tb=tb, pA=pA, par=par: e.tensor_tensor(
                                out=hg_AT.t[0:pw, tb, par, :, 0:pw],
                                in0=pA.t[0:pw, 0:256].rearrange("p (h t) -> p h t", h=2)[:, :, 0:pw],
                                in1=maskbd.t[0:pw, 0:pw].unsqueeze(1).to_broadcast([pw, 2, pw]), op=ALU.mult),
                                reads=[pA.r, maskbd.r], writes=[hg_AT.rs[tb]])
                        if os.environ.get("HGSUB", "") == "A":
                            continue
                        pT = next_bank()
                        pTb = pT.t[:, :].bitcast(BF16)
                        for pc in range(2):
                            sink[0].op("pe", lambda e, pc=pc, tb=tb, pTb=pTb: e.transpose(
                                pTb[0:pw, pc * 128:(pc + 1) * 128], hg_Ke.t[:, pc, tb * 128:tb * 128 + pw], identb.t[:, :]),
                                reads=[hg_Ke.rs[pc], identb.r], writes=[pT.r], signal=(pc == 1))
                        for cc in range(min(4, nch - tb * 4)):
                            sink[0].op("act", lambda e, tb=tb, pTb=pTb, cc=cc: e.activation(
                                out=hg_KeT.t[0:pw, tb, cc, :], in_=pTb[0:pw, 0:256], func=AF.Identity,
                                scale=maskbd.t[0:pw, cc * 32 + 31:cc * 32 + 32]),
                                reads=[pT.r, maskbd.r], writes=[hg_KeT.rs[tb]])
                    if HGDBG < 3:
                        return
                    po = [banks[6], banks[7]]
                    for tb in range(nb):
                        for h in range(4):
                            sink[0].op("pe", lambda e, h=h, tb=tb: e.matmul(
                                po[h // 2].t[(h % 2) * 64:(h % 2) * 64 + 64, tb * 128:tb * 128 + pw],
                                lhsT=hg_Vh.t[0:pw, tb, h * 64:(h + 1) * 64], rhs=hg_AT.t[0:pw, tb, h % 2, h // 2, 0:pw],
                                start=True, stop=False, skip_group_check=True),
                                reads=[hg_Vh.rs[tb], hg_AT.rs[tb]], writes=[po[h // 2].r], signal=False)
                        for cc in range(min(4, nch - tb * 4)):
                            c = tb * 4 + cc
                            last = (c == nch - 1)
                            for pc in range(2):
                                sink[0].op("act", lambda e, pc=pc, c=c: e.activation(
                                    out=hg_Sbd.t[:, pc, :], in_=hg_Sm.t[:, pc, :], func=AF.Identity,
                                    scale=hg_ebm.t[:, pc, c:c + 1]),
                                    reads=[hg_Sm.rs[pc], hg_ebm.rs[pc]], writes=[hg_Sbd.rs[pc]])
                                sink[0].op("pe", lambda e, pc=pc, c=c: e.matmul(
                                    po[pc].t[:, c * 32:(c + 1) * 32], lhsT=hg_Sbd.t[:, pc, :],
                                    rhs=hg_Qt.t[:, pc, c * 32:(c + 1) * 32], start=False, stop=True,
                                    skip_group_check=True),
                                    reads=[hg_Sbd.rs[pc], hg_Qt.rs[pc]], writes=[po[pc].r], signal=last)
                                pS = next_bank()
                                sink[0].op("pe", lambda e, pc=pc, tb=tb, cc=cc, pS=pS: e.matmul(
                                    pS.t[:, 0:128], lhsT=hg_KeT.t[0:pw, tb, cc, pc * 128:(pc + 1) * 128],
                                    rhs=hg_Vh.t[0:pw, tb, pc * 128:(pc + 1) * 128], start=True, stop=True),
                                    reads=[hg_KeT.rs[tb], hg_Vh.rs[tb]], writes=[pS.r])
                                for hh in range(2):
                                    hr = slice(hh * 64, hh * 64 + 64)
                                    sink[0].op("dve", lambda e, pc=pc, c=c, hr=hr, pS=pS: e.scalar_tensor_tensor(
                                        out=hg_Sm.t[hr, pc, hr], in0=hg_Sm.t[hr, pc, hr], scalar=hg_ebl.t[hr, pc, c:c + 1],
                                        in1=pS.t[hr, hr], op0=ALU.mult, op1=ALU.add),
                                        reads=[hg_Sm.rs[pc], hg_ebl.rs[pc], pS.r], writes=[hg_Sm.rs[pc]])
                    if HGDBG < 4:
                        return
                    for pc in range(2):
                        G = hg_G[pc]
                        sink[0].op("act", lambda e, pc=pc: e.activation(out=hg_sq.t[:, 0:w], in_=po[pc].t[:, 0:w], func=AF.Square),
                              reads=[po[pc].r], writes=[hg_sq.r])
                        pn = next_bank()
                        sink[0].op("pe", lambda e, pn=pn: e.matmul(pn.t[:, 0:w], lhsT=onesbd.t[:], rhs=hg_sq.t[:, 0:w],
                                                              start=True, stop=True),
                              reads=[onesbd.r, hg_sq.r], writes=[pn.r])
                        sink[0].op("act", lambda e, pn=pn: e.activation(out=hg_X.t[:, 0:w], in_=pn.t[:, 0:w], func=AF.Ln,
                                                                   scale=1.0 / 64, bias=epsb.t[:]),
                              reads=[pn.r, epsb.r], writes=[hg_X.r])
                        sink[0].op("act", lambda e: e.activation(out=hg_X.t[:, 0:w], in_=hg_X.t[:, 0:w], func=AF.Exp, scale=-0.5),
                              reads=[hg_X.r], writes=[hg_X.r])
                        sink[0].op("dve", lambda e, pc=pc: e.tensor_tensor(out=hg_t1.t[:, 0:w], in0=po[pc].t[:, 0:w],
                                                                      in1=hg_X.t[:, 0:w], op=ALU.mult),
                              reads=[po[pc].r, hg_X.r], writes=[hg_t1.r])
                        sink[0].op("dve", lambda e, pc=pc, G=G: e.scalar_tensor_tensor(
                            out=ymix.t[:, 6 + pc, 0:w], in0=hg_t1.t[:, 0:w], scalar=hgng.t[:, l, pc:pc + 1], in1=G.t[:, 0:w],
                            op0=ALU.mult, op1=ALU.mult), reads=[hg_t1.r, hgng.r, G.r], writes=[ymix.rs[6 + pc]])

                C0 = 0.6065306597126334
                rw_Pb = S("rw_Pb", [128, 9, WA + 1], F32)
                rw_car = S("rw_car", [128, 9], F32)
                rw_c7 = S("rw_c7", [128, 7], F32)
                rw_tmp = [S("rw_tmp%d" % i) for i in range(6)]
                rw_tmpB = [S("rw_tmpB%d" % i) for i in range(3)]
                rw_SIG2 = [S("rw_SIG%d" % i) for i in range(2)]
                rw_A2 = [S("rw_A%d" % i) for i in range(2)]
                rw_L2 = [S("rw_L%d" % i) for i in range(2)]
                rw_KP2 = [S("rw_KP%d" % i) for i in range(2)]
                rw_KN2 = [S("rw_KN%d" % i) for i in range(2)]
                rw_Bf2 = [S("rw_Bf%d" % i) for i in range(2)]
                rw_sqb2 = [S("rw_sqbb%d" % i, [128, WA], BF16) for i in range(2)]
                rw_G = [S("rw_G%d" % i) for i in range(2)]
                rw_Yf = S("rw_Yf")
                rw_TW = S("rw_TW", [32, WA], BF16)
                rw_AL = S("rw_AL", [32, WA], BF16)
                rw_SG = S("rw_SG", [64, WA], BF16)
                rw_sqb = S("rw_sqb", [128, WA], BF16)
                rw_Kh = S("rw_Kh", [128, 2, WA], BF16, nreg=2)
                rw_Bm = S("rw_Bm", [128, 2, 2, WA], BF16, nreg=2)
                rw_QRm = S("rw_QRm", [128, 2, 2, max(1, WA // 64), 2, 64], BF16, nreg=2)
                rw_Ke = S("rw_Ke", [128, 2, WA], BF16, nreg=2)
                rw_Be = S("rw_Be", [128, 2, WA], BF16, nreg=2)
                rw_Vb = S("rw_Vb", [128, 2, WA], BF16, nreg=2)
                NCR = max(1, WA // 64)
                rw_Vt = S("rw_Vt", [64, NCR, 256], BF16)
                rw_KeT = S("rw_KeT", [64, NCR, 256], BF16)
                rw_BeT = S("rw_BeT", [64, NCR, 256], BF16)
                rw_gC = S("rw_gC", [128, 2, NCR], F32, nreg=2)
                NCK = max(1, WA // 64)
                rw_AT12 = [S("rw_AT12_%d" % i, [64, 4, 2, 64], BF16) for i in range(NCK)]
                rw_AT34 = [S("rw_AT34_%d" % i, [64, 4, 2, 64], BF16) for i in range(NCK)]
                rw_X = [[S("rw_X%d_%d" % (i, j), [64, 4, 64], BF16) for j in range(2)] for i in range(NCK)]
                rw_XT = [[S("rw_XT%d_%d" % (i, j), [64, 4, 64], BF16) for j in range(2)] for i in range(NCK)]
                rw_TT = [[S("rw_TT%d_%d" % (i, j), [64, 4, 64], BF16) for j in range(2)] for i in range(NCK)]
                rw_Zb = S("rw_Zb", [64, 256], BF16)
                rw_Un = S("rw_Un", [64, 256], BF16)
                rw_Hm = S("rw_Hm", [128, 2, 128], F32, nreg=2)
                rw_Hbd = S("rw_Hbd", [128, 2, 128], BF16, nreg=2)
                rw_st = S("rw_st", [128, 2, 128], F32)

                def rw_init(s):
                    fw.op("pool", lambda e: e.memset(rw_Hm.t[:], 0.0), writes=rw_Hm.rs)
                    fw.op("pool", lambda e: e.memset(rw_Pb.t[:], 0.0), writes=[rw_Pb.r])
                    if s == 2:
                        fw.op("pool", lambda e: e.memset(rw_st.t[:], 0.0), writes=[rw_st.r])
                        for h in range(4):
                            hr = slice((h % 2) * 64, (h % 2) * 64 + 64)
                            fw.dma("sp", rw_st.t[hr, h // 2, (h % 2) * 64:(h % 2) * 64 + 64], I["srw"][l][h],
                                   writes=[rw_st.r], stream="stld", group=True)
                        for pc in range(2):
                            pb = next_bank()
                            fw.op("pe", lambda e, pc=pc, pb=pb: e.transpose(pb.t[:, 0:128], rw_st.t[:, pc, :], ident.t[:]),
                                  reads=[rw_st.r, ident.r], writes=[pb.r])
                            fw.op("dve", lambda e, pc=pc, pb=pb: e.tensor_copy(out=rw_Hm.t[:, pc, :], in_=pb.t[:, 0:128]),
                                  reads=[pb.r], writes=[rw_Hm.rs[pc]])
                        fw.dma("sp", rw_c7.t[:, :], I["ssh"][l].rearrange("(c p) -> p c", p=128),
                               writes=[rw_c7.r], stream="stld", group=True, allow_slow_non_contiguous=True)
                        fw.op("dve", lambda e: e.tensor_copy(out=rw_Pb.t[:, 0:6, 0], in_=rw_c7.t[:, 0:6]),
                              reads=[rw_c7.r], writes=[rw_Pb.r])
                        fw.op("dve", lambda e: e.tensor_copy(out=rw_Pb.t[0:32, 6, 0:1], in_=rw_c7.t[0:32, 6:7]),
                              reads=[rw_c7.r], writes=[rw_Pb.r])
                        fw.op("dve", lambda e: e.tensor_copy(out=rw_Pb.t[0:32, 7, 0:1], in_=rw_c7.t[32:64, 6:7]),
                              reads=[rw_c7.r], writes=[rw_Pb.r])
                        fw.op("dve", lambda e: e.tensor_copy(out=rw_Pb.t[0:64, 8, 0:1], in_=rw_c7.t[64:128, 6:7]),
                              reads=[rw_c7.r], writes=[rw_Pb.r])
                    for pc in range(2):
                        fw.op("act", lambda e, pc=pc: e.activation(out=rw_Hbd.t[:, pc, :], in_=rw_Hm.t[:, pc, :],
                                                                   func=AF.Copy), reads=[rw_Hm.rs[pc]], writes=[rw_Hbd.rs[pc]])

                def rw_final(s, w):
                    dst = O["rwp"][l][s] if s < 2 else O["rws"][l]
                    for pc in range(2):
                        pb = next_bank()
                        fw.op("pe", lambda e, pc=pc, pb=pb: e.transpose(pb.t[:, 0:128], rw_Hm.t[:, pc, :], ident.t[:]),
                              reads=[rw_Hm.rs[pc], ident.r], writes=[pb.r])
                        fw.op("dve", lambda e, pc=pc, pb=pb: e.tensor_copy(out=rw_st.t[:, pc, :], in_=pb.t[:, 0:128]),
                              reads=[pb.r], writes=[rw_st.r])
                    for h in range(4):
                        hr = slice((h % 2) * 64, (h % 2) * 64 + 64)
                        fw.dma("sp", dst[h], rw_st.t[hr, h // 2, (h % 2) * 64:(h % 2) * 64 + 64], reads=[rw_st.r],
                               stream="ststrs%d" % s, group=True)
                    dsh = O["rshp"][l][s] if s < 2 else O["rshs"][l]
                    fw.op("dve", lambda e: e.tensor_copy(out=rw_c7.t[:, 0:6], in_=rw_car.t[:, 0:6]), reads=[rw_car.r], writes=[rw_c7.r])
                    fw.op("dve", lambda e: e.tensor_copy(out=rw_c7.t[0:32, 6:7], in_=rw_car.t[0:32, 6:7]), reads=[rw_car.r], writes=[rw_c7.r])
                    fw.op("dve", lambda e: e.tensor_copy(out=rw_c7.t[32:64, 6:7], in_=rw_car.t[0:32, 7:8]), reads=[rw_car.r], writes=[rw_c7.r])
                    fw.op("dve", lambda e: e.tensor_copy(out=rw_c7.t[64:128, 6:7], in_=rw_car.t[0:64, 8:9]), reads=[rw_car.r], writes=[rw_c7.r])
                    fw.dma("sp", dsh.rearrange("(c p) -> p c", p=128), rw_c7.t[:, :], reads=[rw_c7.r],
                           stream="ststrc%d" % s, group=True, allow_slow_non_contiguous=True)

                def rw_tile(s, w):
                    C = min(64, w)
                    nch = w // C
                    nlev = {64: 5, 32: 4}[C]
                    rmask = rmask64 if C == 64 else rmask32
                    T0, T1, T2, T3, T4, T5 = rw_tmp
                    specs = [(i, i * 128, 128) for i in range(6)] + [(6, 768, 32), (7, 800, 32), (8, 832, 64)]
                    for (i, c0, M) in specs:
                        pb = proj_fm(c0, w, M)
                        sink[0].op("act", lambda e, i=i, M=M, pb=pb: e.activation(out=rw_Pb.t[0:M, i, 1:1 + w], in_=pb.t[0:M, 0:w],
                                                                          func=AF.Copy), reads=[pb.r], writes=[rw_Pb.r])
                    sink[0].op("act", lambda e: e.activation(out=rw_car.t[:, :], in_=rw_Pb.t[:, :, w], func=AF.Copy),
                          reads=[rw_Pb.r], writes=[rw_car.r])
                    for (i, c0, M) in specs:
                        sink[0].op("dve", lambda e, i=i, M=M: e.tensor_tensor(out=T0.t[0:M, 0:w], in0=rw_Pb.t[0:M, i, 0:w],
                                                                         in1=rw_Pb.t[0:M, i, 1:1 + w], op=ALU.subtract),
                              reads=[rw_Pb.r], writes=[T0.r])
                        sink[0].op("dve", lambda e, i=i, M=M: e.scalar_tensor_tensor(
                            out=rw_Pb.t[0:M, i, 1:1 + w], in0=T0.t[0:M, 0:w], scalar=mul.t[0:M, l, i:i + 1],
                            in1=rw_Pb.t[0:M, i, 1:1 + w], op0=ALU.mult, op1=ALU.add),
                            reads=[T0.r, mul.r, rw_Pb.r], writes=[rw_Pb.r])
                    sink[0].op("pool", lambda e: e.tensor_copy(out=rw_Pb.t[:, :, 0], in_=rw_car.t[:, :]),
                          reads=[rw_car.r, rw_Pb.r], writes=[rw_Pb.r])
                    XS = lambda i, M=128: rw_Pb.t[0:M, i, 1:1 + w]
                    sink[0].op("act", lambda e: e.activation(out=rw_TW.t[:, 0:w], in_=XS(6, 32), func=AF.Tanh),
                          reads=[rw_Pb.r], writes=[rw_TW.r])
                    sink[0].op("act", lambda e: e.activation(out=rw_AL.t[:, 0:w], in_=XS(7, 32), func=AF.Copy),
                          reads=[rw_Pb.r], writes=[rw_AL.r])
                    sink[0].op("act", lambda e: e.activation(out=T0.t[0:64, 0:w], in_=XS(8, 64), func=AF.Exp, scale=-1.0),
                          reads=[rw_Pb.r], writes=[T0.r])
                    sink[0].op("dve", lambda e: e.tensor_scalar_add(out=T0.t[0:64, 0:w], in0=T0.t[0:64, 0:w], scalar1=1.0),
                          reads=[T0.r], writes=[T0.r])
                    sink[0].op("dve", lambda e: e.reciprocal(out=T0.t[0:64, 0:w], in_=T0.t[0:64, 0:w]), reads=[T0.r], writes=[T0.r])
                    sink[0].op("act", lambda e: e.activation(out=rw_SG.t[:, 0:w], in_=T0.t[0:64, 0:w], func=AF.Copy),
                          reads=[T0.r], writes=[rw_SG.r])
                    outer_sink = sink[0]
                    pc_recs = [Rec(), Rec()]

                    def _pc_body(pc, T1, T2, T3, rw_SIG, rw_A, rw_L, rw_KP, rw_KN, rw_Bf, rw_sqb):
                        cs = slice(pc * 128, (pc + 1) * 128)
                        r_ap, k_ap, v_ap = XS(pc), XS(2 + pc), XS(4 + pc)
                        pw_ = next_bank()
                        sink[0].op("pe", lambda e, pw_=pw_, cs=cs: e.matmul(pw_.t[:, 0:w], lhsT=w2b.t[:, l, cs], rhs=rw_TW.t[:, 0:w],
                                                                    start=True, stop=True), reads=[w2b.r, rw_TW.r], writes=[pw_.r])
                        sink[0].op("act", lambda e, pw_=pw_, pc=pc: e.activation(out=rw_SIG.t[:, 0:w], in_=pw_.t[:, 0:w], func=AF.Exp,
                                                                         scale=-1.0, bias=nw0.t[:, l, pc:pc + 1]),
                              reads=[pw_.r, nw0.r], writes=[rw_SIG.r])
                        sink[0].op("dve", lambda e: e.tensor_scalar_add(out=rw_SIG.t[:, 0:w], in0=rw_SIG.t[:, 0:w], scalar1=1.0),
                              reads=[rw_SIG.r], writes=[rw_SIG.r])
                        sink[0].op("dve", lambda e: e.reciprocal(out=rw_SIG.t[:, 0:w], in_=rw_SIG.t[:, 0:w]),
                              reads=[rw_SIG.r], writes=[rw_SIG.r])
                        pa_ = next_bank()
                        sink[0].op("pe", lambda e, pa_=pa_, cs=cs: e.matmul(pa_.t[:, 0:w], lhsT=a2b.t[:, l, cs], rhs=rw_AL.t[:, 0:w],
                                                                    start=True, stop=True), reads=[a2b.r, rw_AL.r], writes=[pa_.r])
                        sink[0].op("act", lambda e, pa_=pa_, pc=pc: e.activation(out=rw_A.t[:, 0:w], in_=pa_.t[:, 0:w], func=AF.Exp,
                                                                         scale=-1.0, bias=na0.t[:, l, pc:pc + 1]),
                              reads=[pa_.r, na0.r], writes=[rw_A.r])
                        sink[0].op("dve", lambda e: e.tensor_scalar_add(out=rw_A.t[:, 0:w], in0=rw_A.t[:, 0:w], scalar1=1.0),
                              reads=[rw_A.r], writes=[rw_A.r])
                        sink[0].op("dve", lambda e: e.reciprocal(out=rw_A.t[:, 0:w], in_=rw_A.t[:, 0:w]), reads=[rw_A.r], writes=[rw_A.r])
                        pg_ = next_bank()
                        sink[0].op("pe", lambda e, pg_=pg_, cs=cs: e.matmul(pg_.t[:, 0:w], lhsT=g2b.t[:, l, cs], rhs=rw_SG.t[:, 0:w],
                                                                    start=True, stop=True), reads=[g2b.r, rw_SG.r], writes=[pg_.r])
                        sink[0].op("act", lambda e, pg_=pg_, pc=pc: e.activation(out=rw_G[pc].t[:, 0:w], in_=pg_.t[:, 0:w], func=AF.Copy),
                              reads=[pg_.r], writes=[rw_G[pc].r])
                        sink[0].op("dve", lambda e, pc=pc, k_ap=k_ap: e.tensor_scalar_mul(
                            out=rw_KN.t[:, 0:w], in0=k_ap, scalar1=rwp["rw_k_k"].t[:, l, pc:pc + 1]),
                            reads=[rw_Pb.r, rwp["rw_k_k"].r], writes=[rw_KN.r])
                        sink[0].op("act", lambda e: e.activation(out=rw_sqb.t[:, 0:w], in_=rw_KN.t[:, 0:w], func=AF.Square),
                              reads=[rw_KN.r], writes=[rw_sqb.r])
                        pn = next_bank()
                        sink[0].op("pe", lambda e, pn=pn: e.matmul(pn.t[:, 0:w], lhsT=onesbd.t[:], rhs=rw_sqb.t[:, 0:w],
                                                              start=True, stop=True), reads=[onesbd.r, rw_sqb.r], writes=[pn.r])
                        sink[0].op("dve", lambda e, pn=pn: e.tensor_scalar_max(out=T1.t[:, 0:w], in0=pn.t[:, 0:w], scalar1=1e-24),
                              reads=[pn.r], writes=[T1.r])
                        sink[0].op("act", lambda e: e.activation(out=T1.t[:, 0:w], in_=T1.t[:, 0:w], func=AF.Ln), reads=[T1.r], writes=[T1.r])
                        sink[0].op("act", lambda e: e.activation(out=T1.t[:, 0:w], in_=T1.t[:, 0:w], func=AF.Exp, scale=-0.5),
                              reads=[T1.r], writes=[T1.r])
                        sink[0].op("dve", lambda e: e.tensor_tensor(out=rw_KN.t[:, 0:w], in0=rw_KN.t[:, 0:w], in1=T1.t[:, 0:w], op=ALU.mult),
                              reads=[rw_KN.r, T1.r], writes=[rw_KN.r])
                        sink[0].op("dve", lambda e, pc=pc: e.tensor_scalar(
                            out=T1.t[:, 0:w], in0=rw_A.t[:, 0:w], scalar1=rwp["rw_k_a"].t[:, l, pc:pc + 1],
                            scalar2=omka.t[:, l, pc:pc + 1], op0=ALU.mult, op1=ALU.add),
                            reads=[rw_A.r, rwp["rw_k_a"].r, omka.r], writes=[T1.r])
                        sink[0].op("dve", lambda e, k_ap=k_ap: e.tensor_tensor(out=rw_KP.t[:, 0:w], in0=k_ap, in1=T1.t[:, 0:w], op=ALU.mult),
                              reads=[rw_Pb.r, T1.r], writes=[rw_KP.r])
                        sink[0].op("dve", lambda e: e.tensor_tensor(out=rw_Bf.t[:, 0:w], in0=rw_KN.t[:, 0:w], in1=rw_A.t[:, 0:w], op=ALU.mult),
                              reads=[rw_KN.r, rw_A.r], writes=[rw_Bf.r])
                        sink[0].op("dve", lambda e: e.tensor_tensor_scan(out=rw_L.t[:, 0:w], data0=rmask.t[:, 0:w], data1=rw_SIG.t[:, 0:w],
                                                                   initial=0.0, op0=ALU.mult, op1=ALU.add),
                              reads=[rw_SIG.r, rmask.r], writes=[rw_L.r])
                        Lv = rw_L.t[:, 0:w].rearrange("p (c t) -> p c t", t=C)
                        sink[0].op("act", lambda e: e.activation(out=T2.t[:, 0:w], in_=rw_L.t[:, 0:w], func=AF.Exp, scale=-C0),
                              reads=[rw_L.r], writes=[T2.r])
                        sink[0].op("dve", lambda e: e.tensor_tensor(out=T3.t[:, 0:w], in0=rw_L.t[:, 0:w], in1=rw_SIG.t[:, 0:w], op=ALU.subtract),
                              reads=[rw_L.r, rw_SIG.r], writes=[T3.r])
                        sink[0].op("act", lambda e: e.activation(out=T3.t[:, 0:w], in_=T3.t[:, 0:w], func=AF.Exp, scale=-C0),
                              reads=[T3.r], writes=[T3.r])
                        for par in range(2):
                            hm = onesbdf.t[:, par * 64:par * 64 + 1]
                            sink[0].op("dve", lambda e, pc=pc, par=par, hm=hm, r_ap=r_ap: e.scalar_tensor_tensor(
                                out=rw_QRm.t[:, par, pc, 0:nch, 1, 0:C], in0=r_ap.rearrange("p (c t) -> p c t", t=C), scalar=hm,
                                in1=T2.t[:, 0:w].rearrange("p (c t) -> p c t", t=C), op0=ALU.mult, op1=ALU.mult),
                                reads=[rw_Pb.r, T2.r, onesbdf.r], writes=[rw_QRm.rs[pc]])
                            sink[0].op("dve", lambda e, pc=pc, par=par, hm=hm: e.scalar_tensor_tensor(
                                out=rw_QRm.t[:, par, pc, 0:nch, 0, 0:C], in0=rw_KN.t[:, 0:w].rearrange("p (c t) -> p c t", t=C),
                                scalar=hm, in1=T3.t[:, 0:w].rearrange("p (c t) -> p c t", t=C), op0=ALU.mult, op1=ALU.mult),
                                reads=[rw_KN.r, T3.r, onesbdf.r], writes=[rw_QRm.rs[pc]])
                        sink[0].op("act", lambda e: e.activation(out=T2.t[:, 0:w], in_=rw_L.t[:, 0:w], func=AF.Exp, scale=C0),
                              reads=[rw_L.r], writes=[T2.r])
                        sink[0].op("dve", lambda e, pc=pc: e.tensor_tensor(out=rw_Kh.t[:, pc, 0:w], in0=rw_KP.t[:, 0:w], in1=T2.t[:, 0:w], op=ALU.mult),
                              reads=[rw_KP.r, T2.r], writes=[rw_Kh.rs[pc]])
                        for par in range(2):
                            hm = onesbdf.t[:, par * 64:par * 64 + 1]
                            sink[0].op("dve", lambda e, pc=pc, par=par, hm=hm: e.scalar_tensor_tensor(
                                out=rw_Bm.t[:, par, pc, 0:w], in0=rw_Bf.t[:, 0:w], scalar=hm, in1=T2.t[:, 0:w],
                                op0=ALU.mult, op1=ALU.mult), reads=[rw_Bf.r, T2.r, onesbdf.r], writes=[rw_Bm.rs[pc]])
                        sink[0].op("dve", lambda e, Lv=Lv: e.tensor_tensor(
                            out=T3.t[:, 0:w].rearrange("p (c t) -> p c t", t=C), in0=Lv[:, :, C - 1:C].to_broadcast([128, nch, C]),
                            in1=Lv, op=ALU.subtract), reads=[rw_L.r], writes=[T3.r])
                        sink[0].op("act", lambda e: e.activation(out=T3.t[:, 0:w], in_=T3.t[:, 0:w], func=AF.Exp, scale=-C0),
                              reads=[T3.r], writes=[T3.r])
                        sink[0].op("dve", lambda e, pc=pc: e.tensor_tensor(out=rw_Ke.t[:, pc, 0:w], in0=rw_KP.t[:, 0:w], in1=T3.t[:, 0:w], op=ALU.mult),
                              reads=[rw_KP.r, T3.r], writes=[rw_Ke.rs[pc]])
                        sink[0].op("pool", lambda e, pc=pc: e.tensor_tensor(out=rw_Be.t[:, pc, 0:w], in0=rw_Bf.t[:, 0:w], in1=T3.t[:, 0:w], op=ALU.mult),
                              reads=[rw_Bf.r, T3.r], writes=[rw_Be.rs[pc]])
                        sink[0].op("act", lambda e, pc=pc, Lv=Lv: e.activation(out=rw_gC.t[:, pc, 0:nch], in_=Lv[:, :, C - 1], func=AF.Exp,
                                                                       scale=-C0), reads=[rw_L.r], writes=[rw_gC.rs[pc]])
                        sink[0].op("act", lambda e, pc=pc, v_ap=v_ap: e.activation(out=rw_Vb.t[:, pc, 0:w], in_=v_ap, func=AF.Copy),
                              reads=[rw_Pb.r], writes=[rw_Vb.rs[pc]])
                        sink[0].op("dve", lambda e, pc=pc, r_ap=r_ap: e.scalar_tensor_tensor(
                            out=(T4 if pc == 0 else T5).t[:, 0:w], in0=r_ap, scalar=rwp["rw_r_k"].t[:, l, pc:pc + 1],
                            in1=rw_KP.t[:, 0:w], op0=ALU.mult, op1=ALU.mult),
                            reads=[rw_Pb.r, rwp["rw_r_k"].r, rw_KP.r], writes=[(T4 if pc == 0 else T5).r])
                    for pc in range(2):
                        sink[0] = pc_recs[pc]
                        tt = (T1, T2, T3) if pc == 0 else tuple(rw_tmpB)
                        _pc_body(pc, tt[0], tt[1], tt[2], rw_SIG2[pc], rw_A2[pc], rw_L2[pc], rw_KP2[pc], rw_KN2[pc],
                                 rw_Bf2[pc], rw_sqb2[pc])
                    sink[0] = outer_sink
                    merge_recs(outer_sink, pc_recs)
                    RWDBG = int(os.environ.get("RWDBG", "9"))
                    if RWDBG < 2:
                        return
                    for (src, dstT) in ((rw_Vb, rw_Vt), (rw_Ke, rw_KeT), (rw_Be, rw_BeT)):
                        for c0 in range(0, nch, 4):
                            pT = next_bank()
                            pTb = pT.t[:, :].bitcast(BF16)
                            ncc = min(4, nch - c0)
                            for ci in range(ncc):
                                c = c0 + ci
                                for pc in range(2):
                                    sink[0].op("pe", lambda e, src=src, c=c, ci=ci, pc=pc, pTb=pTb: e.transpose(
                                        pTb[0:C, ci * 256 + pc * 128:ci * 256 + (pc + 1) * 128], src.t[:, pc, c * C:(c + 1) * C],
                                        identb.t[:, :]), reads=[src.rs[pc], identb.r], writes=[pT.r],
                                        signal=(ci == ncc - 1 and pc == 1))
                            sink[0].op("act", lambda e, dstT=dstT, c0=c0, ncc=ncc, pTb=pTb: e.activation(
                                out=dstT.t[0:C, c0:c0 + ncc, :], in_=pTb[0:C, 0:ncc * 256].rearrange("p (c n) -> p c n", n=256),
                                func=AF.Copy), reads=[pT.r], writes=[dstT.r])
                    if RWDBG < 3:
                        return
                    class _V:
                        pass
                    py = [_V(), _V()]
                    for pc_ in range(2):
                        py[pc_].t = banks[5].t[:, pc_ * WA:(pc_ + 1) * WA]
                        py[pc_].r = banks[5].r
                    v4 = lambda ap: ap.rearrange("p (h a t) -> p h a t", h=4, a=2)[:, :, :, 0:C]
                    v3 = lambda ap: ap.rearrange("p (h t) -> p h t", h=4)[:, :, 0:C]
                    for c in range(nch):
                        p12, p34, p5 = next_bank(), next_bank(), next_bank()
                        AT12, AT34 = rw_AT12[c], rw_AT34[c]
                        for h in range(4):
                            par, pc = h % 2, h // 2
                            for a_ in range(2):
                                qr = rw_QRm.t[:, par, pc, c, a_, 0:C]
                                sink[0].op("pe", lambda e, c=c, h=h, pc=pc, qr=qr, p12=p12, a_=a_: e.matmul(
                                    p12.t[0:C, h * 128 + a_ * 64:h * 128 + a_ * 64 + C], lhsT=rw_Kh.t[:, pc, c * C:(c + 1) * C],
                                    rhs=qr, start=True, stop=True), reads=[rw_Kh.rs[pc], rw_QRm.rs[pc]], writes=[p12.r],
                                    signal=(h == 3 and a_ == 1))
                                sink[0].op("pe", lambda e, c=c, h=h, par=par, pc=pc, qr=qr, p34=p34, a_=a_: e.matmul(
                                    p34.t[0:C, h * 128 + a_ * 64:h * 128 + a_ * 64 + C],
                                    lhsT=rw_Bm.t[:, par, pc, c * C:(c + 1) * C], rhs=qr, start=True, stop=True),
                                    reads=[rw_Bm.rs[pc], rw_QRm.rs[pc]], writes=[p34.r], signal=(h == 3 and a_ == 1))
                            sink[0].op("pe", lambda e, h=h, par=par, pc=pc, c=c, p5=p5: e.matmul(
                                p5.t[0:C, h * 64:h * 64 + C], lhsT=rw_QRm.t[:, par, pc, c, 0, 0:C],
                                rhs=rw_Bm.t[:, par, pc, c * C:(c + 1) * C], start=True, stop=True),
                                reads=[rw_Bm.rs[pc], rw_QRm.rs[pc]], writes=[p5.r], signal=(h == 3))
                        sink[0].op("dve", lambda e, p12=p12, AT12=AT12: e.tensor_tensor(
                            out=AT12.t[0:C, :, :, 0:C], in0=v4(p12.t[0:C, :]),
                            in1=mask12.t[0:C, :, 0:C].unsqueeze(1).to_broadcast([C, 4, 2, C]), op=ALU.mult),
                            reads=[p12.r, mask12.r], writes=[AT12.r])
                        sink[0].op("dve", lambda e, p34=p34, AT34=AT34: e.tensor_tensor(
                            out=AT34.t[0:C, :, :, 0:C], in0=v4(p34.t[0:C, :]),
                            in1=mask34.t[0:C, :, 0:C].unsqueeze(1).to_broadcast([C, 4, 2, C]), op=ALU.mult),
                            reads=[p34.r, mask34.r], writes=[AT34.r])
                        X, XT, TT = rw_X[c][0], rw_XT[c][0], rw_TT[c][0]
                        sink[0].op("dve", lambda e, p5=p5, X=X: e.tensor_tensor(
                            out=X.t[0:C, :, 0:C], in0=v3(p5.t[0:C, 0:256]),
                            in1=mask5.t[0:C, 0:C].unsqueeze(1).to_broadcast([C, 4, C]), op=ALU.mult),
                            reads=[p5.r, mask5.r], writes=[X.r])
                        sink[0].op("act", lambda e, XT=XT, AT34=AT34: e.activation(out=XT.t[0:C, :, 0:C], in_=AT34.t[0:C, :, 0, 0:C], func=AF.Copy),
                              reads=[AT34.r], writes=[XT.r])
                        sink[0].op("pool", lambda e, TT=TT, AT34=AT34: e.tensor_tensor(
                            out=TT.t[0:C, :, 0:C], in0=AT34.t[0:C, :, 0, 0:C],
                            in1=identb.t[0:C, 0:C].unsqueeze(1).to_broadcast([C, 4, C]), op=ALU.add),
                            reads=[AT34.r, identb.r], writes=[TT.r])
                    cur = 0
                    for lev in range(nlev):
                        for c in range(nch):
                            Xn, XTn, TTn = rw_X[c][1 - cur], rw_XT[c][1 - cur], rw_TT[c][1 - cur]
                            X, XT, TT = rw_X[c][cur], rw_XT[c][cur], rw_TT[c][cur]
                            px, pxt, ptt = next_bank(), next_bank(), next_bank()
                            for h in range(4):
                                sink[0].op("pe", lambda e, h=h, px=px, X=X, XT=XT: e.matmul(
                                    px.t[0:C, h * 64:h * 64 + C], lhsT=XT.t[0:C, h, 0:C], rhs=X.t[0:C, h, 0:C], start=True, stop=True),
                                    reads=[X.r, XT.r], writes=[px.r], signal=(h == 3))
                            sink[0].op("act", lambda e, px=px, Xn=Xn: e.activation(
                                out=Xn.t[0:C, :, 0:C], in_=v3(px.t[0:C, 0:256]), func=AF.Copy), reads=[px.r], writes=[Xn.r])
                            if lev < nlev - 1:
                                for h in range(4):
                                    sink[0].op("pe", lambda e, h=h, pxt=pxt, X=X, XT=XT: e.matmul(
                                        pxt.t[0:C, h * 64:h * 64 + C], lhsT=X.t[0:C, h, 0:C], rhs=XT.t[0:C, h, 0:C], start=True, stop=True),
                                        reads=[X.r, XT.r], writes=[pxt.r], signal=(h == 3))
                                sink[0].op("act" if c % 2 else "dve", (lambda e, pxt=pxt, XTn=XTn: e.activation(
                                    out=XTn.t[0:C, :, 0:C], in_=v3(pxt.t[0:C, 0:256]), func=AF.Copy)) if c % 2 else
                                    (lambda e, pxt=pxt, XTn=XTn: e.tensor_copy(out=XTn.t[0:C, :, 0:C], in_=v3(pxt.t[0:C, 0:256]))),
                                    reads=[pxt.r], writes=[XTn.r])
                            for h in range(4):
                                sink[0].op("pe", lambda e, h=h, ptt=ptt, Xn=Xn, TT=TT: e.matmul(
                                    ptt.t[0:C, h * 64:h * 64 + C], lhsT=Xn.t[0:C, h, 0:C], rhs=TT.t[0:C, h, 0:C], start=True, stop=True),
                                    reads=[Xn.r, TT.r], writes=[ptt.r], signal=(h == 3))
                            sink[0].op("dve", lambda e, ptt=ptt, TT=TT, TTn=TTn: e.tensor_tensor(
                                out=TTn.t[0:C, :, 0:C], in0=v3(ptt.t[0:C, 0:256]), in1=TT.t[0:C, :, 0:C], op=ALU.add),
                                reads=[ptt.r, TT.r], writes=[TTn.r])
                        cur = 1 - cur
                    if RWDBG < 4:
                        return
                    for c in range(nch):
                        TTf = rw_TT[c][cur]
                        AT12, AT34 = rw_AT12[c], rw_AT34[c]
                        pz, pu, ph = next_bank(), next_bank(), next_bank()
                        for pc in range(2):
                            for par in range(2):
                                sink[0].op("pe", lambda e, c=c, AT12=AT12, AT34=AT34, pc=pc, par=par, pz=pz: e.matmul(
                                    pz.t[0:C, pc * 128:(pc + 1) * 128], lhsT=rw_QRm.t[:, par, pc, c, 0, 0:C], rhs=rw_Hbd.t[:, pc, :],
                                    start=(par == 0), stop=False, skip_group_check=True),
                                    reads=[rw_QRm.rs[pc], rw_Hbd.rs[pc]], writes=[pz.r], signal=False)
                            for h in (2 * pc, 2 * pc + 1):
                                sink[0].op("pe", lambda e, c=c, AT12=AT12, AT34=AT34, h=h, pz=pz: e.matmul(
                                    pz.t[0:C, h * 64:(h + 1) * 64], lhsT=AT12.t[0:C, h, 0, 0:C], rhs=rw_Vt.t[0:C, c, h * 64:(h + 1) * 64],
                                    start=False, stop=True, skip_group_check=True),
                                    reads=[AT12.r, rw_Vt.r], writes=[pz.r], signal=(h == 3))
                        sink[0].op("act", lambda e, c=c, AT12=AT12, AT34=AT34, pz=pz: e.activation(out=rw_Zb.t[0:C, :], in_=pz.t[0:C, 0:256], func=AF.Copy),
                              reads=[pz.r], writes=[rw_Zb.r])
                        for h in range(4):
                            sink[0].op("pe", lambda e, c=c, AT12=AT12, AT34=AT34, h=h, pu=pu, TTf=TTf: e.matmul(
                                pu.t[0:C, h * 64:(h + 1) * 64], lhsT=TTf.t[0:C, h, 0:C], rhs=rw_Zb.t[0:C, h * 64:(h + 1) * 64],
                                start=True, stop=True), reads=[TTf.r, rw_Zb.r], writes=[pu.r], signal=(h == 3))
                        sink[0].op("dve", lambda e, c=c, AT12=AT12, AT34=AT34, pu=pu: e.tensor_scalar_mul(out=rw_Un.t[0:C, :], in0=pu.t[0:C, 0:256], scalar1=-1.0),
                              reads=[pu.r], writes=[rw_Un.r])
                        for pc in range(2):
                          for par in range(2):
                                sink[0].op("pe", lambda e, c=c, AT12=AT12, AT34=AT34, pc=pc, par=par: e.matmul(
                                    py[pc].t[:, c * C:(c + 1) * C], lhsT=rw_Hbd.t[:, pc, :], rhs=rw_QRm.t[:, par, pc, c, 1, 0:C],
                                    start=(par == 0), stop=False, skip_group_check=True),
                                    reads=[rw_QRm.rs[pc], rw_Hbd.rs[pc]], writes=[py[pc].r], signal=False)
                          for h in (2 * pc, 2 * pc + 1):
                            hs = slice((h % 2) * 64, (h % 2) * 64 + 64)
                            sink[0].op("pe", lambda e, c=c, AT12=AT12, AT34=AT34, h=h, pc=pc, hs=hs: e.matmul(
                                py[pc].t[hs, c * C:(c + 1) * C], lhsT=rw_Vt.t[0:C, c, h * 64:(h + 1) * 64], rhs=AT12.t[0:C, h, 1, 0:C],
                                start=False, stop=False, skip_group_check=True),
                                reads=[rw_Vt.r, AT12.r], writes=[py[pc].r], signal=False)
                            sink[0].op("pe", lambda e, c=c, AT12=AT12, AT34=AT34, h=h, pc=pc, hs=hs: e.matmul(
                                py[pc].t[hs, c * C:(c + 1) * C], lhsT=rw_Un.t[0:C, h * 64:(h + 1) * 64], rhs=AT34.t[0:C, h, 1, 0:C],
                                start=False, stop=True, skip_group_check=True),
                                reads=[rw_Un.r, AT34.r], writes=[py[pc].r], signal=(h % 2 == 1))
                        for pc in range(2):
                            cs = slice(pc * 128, (pc + 1) * 128)
                            sink[0].op("pe", lambda e, c=c, AT12=AT12, AT34=AT34, pc=pc, cs=cs, ph=ph: e.matmul(
                                ph.t[:, cs], lhsT=rw_KeT.t[0:C, c, cs], rhs=rw_Vt.t[0:C, c, cs], start=True, stop=False),
                                reads=[rw_KeT.r, rw_Vt.r], writes=[ph.r], signal=False)
                            sink[0].op("pe", lambda e, c=c, AT12=AT12, AT34=AT34, pc=pc, cs=cs, ph=ph: e.matmul(
                                ph.t[:, cs], lhsT=rw_BeT.t[0:C, c, cs], rhs=rw_Un.t[0:C, cs], start=False, stop=True),
                                reads=[rw_BeT.r, rw_Un.r], writes=[ph.r])
                            for hh in range(2):
                                hr = slice(hh * 64, hh * 64 + 64)
                                sink[0].op("dve", lambda e, c=c, AT12=AT12, AT34=AT34, pc=pc, hr=hr, hh=hh, ph=ph: e.scalar_tensor_tensor(
                                    out=rw_Hm.t[hr, pc, hr], in0=rw_Hm.t[hr, pc, hr], scalar=rw_gC.t[hr, pc, c:c + 1],
                                    in1=ph.t[hr, pc * 128 + hh * 64:pc * 128 + hh * 64 + 64], op0=ALU.mult, op1=ALU.add),
                                    reads=[rw_Hm.rs[pc], rw_gC.rs[pc], ph.r], writes=[rw_Hm.rs[pc]])
                            sink[0].op("act", lambda e, c=c, AT12=AT12, AT34=AT34, pc=pc: e.activation(out=rw_Hbd.t[:, pc, :], in_=rw_Hm.t[:, pc, :], func=AF.Copy),
                                  reads=[rw_Hm.rs[pc]], writes=[rw_Hbd.rs[pc]])
                    if RWDBG < 5:
                        return
                    for pc in range(2):
                        v_ap = XS(4 + pc)
                        TB = T4 if pc == 0 else T5
                        sink[0].op("act", lambda e, pc=pc: e.activation(out=rw_Yf.t[:, 0:w], in_=py[pc].t[:, 0:w], func=AF.Copy),
                              reads=[py[pc].r], writes=[rw_Yf.r])
                        sink[0].op("act", lambda e: e.activation(out=rw_sqb.t[:, 0:w], in_=rw_Yf.t[:, 0:w], func=AF.Copy),
                              reads=[rw_Yf.r], writes=[rw_sqb.r])
                        pm = next_bank()
                        sink[0].op("pe", lambda e, pm=pm: e.matmul(pm.t[:, 0:w], lhsT=onesbd.t[:], rhs=rw_sqb.t[:, 0:w], start=True, stop=True),
                              reads=[onesbd.r, rw_sqb.r], writes=[pm.r])
                        sink[0].op("dve", lambda e, pm=pm: e.scalar_tensor_tensor(
                            out=rw_Yf.t[:, 0:w], in0=pm.t[:, 0:w], scalar=-1.0 / 64, in1=rw_Yf.t[:, 0:w], op0=ALU.mult, op1=ALU.add),
                            reads=[pm.r, rw_Yf.r], writes=[rw_Yf.r])
                        sink[0].op("act", lambda e: e.activation(out=rw_sqb.t[:, 0:w], in_=rw_Yf.t[:, 0:w], func=AF.Square),
                              reads=[rw_Yf.r], writes=[rw_sqb.r])
                        pv = next_bank()
                        sink[0].op("pe", lambda e, pv=pv: e.matmul(pv.t[:, 0:w], lhsT=onesbd.t[:], rhs=rw_sqb.t[:, 0:w], start=True, stop=True),
                              reads=[onesbd.r, rw_sqb.r], writes=[pv.r])
                        sink[0].op("act", lambda e, pv=pv: e.activation(out=T0.t[:, 0:w], in_=pv.t[:, 0:w], func=AF.Ln, scale=1.0 / 64,
                                                                   bias=eps2.t[:]), reads=[pv.r, eps2.r], writes=[T0.r])
                        sink[0].op("act", lambda e: e.activation(out=T0.t[:, 0:w], in_=T0.t[:, 0:w], func=AF.Exp, scale=-0.5),
                              reads=[T0.r], writes=[T0.r])
                        sink[0].op("dve", lambda e: e.tensor_tensor(out=rw_Yf.t[:, 0:w], in0=rw_Yf.t[:, 0:w], in1=T0.t[:, 0:w], op=ALU.mult),
                              reads=[rw_Yf.r, T0.r], writes=[rw_Yf.r])
                        sink[0].op("dve", lambda e, pc=pc: e.tensor_scalar(
                            out=rw_Yf.t[:, 0:w], in0=rw_Yf.t[:, 0:w], scalar1=rwp["rw_ln_w"].t[:, l, pc:pc + 1],
                            scalar2=rwp["rw_ln_b"].t[:, l, pc:pc + 1], op0=ALU.mult, op1=ALU.add),
                            reads=[rw_Yf.r, rwp["rw_ln_w"].r, rwp["rw_ln_b"].r], writes=[rw_Yf.r])
                        sink[0].op("act", lambda e, TB=TB: e.activation(out=rw_sqb.t[:, 0:w], in_=TB.t[:, 0:w], func=AF.Copy),
                              reads=[TB.r], writes=[rw_sqb.r])
                        pbn = next_bank()
                        sink[0].op("pe", lambda e, pbn=pbn: e.matmul(pbn.t[:, 0:w], lhsT=onesbd.t[:], rhs=rw_sqb.t[:, 0:w], start=True, stop=True),
                              reads=[onesbd.r, rw_sqb.r], writes=[pbn.r])
                        sink[0].op("dve", lambda e, pbn=pbn, v_ap=v_ap: e.tensor_tensor(out=T0.t[:, 0:w], in0=pbn.t[:, 0:w], in1=v_ap, op=ALU.mult),
                              reads=[pbn.r, rw_Pb.r], writes=[T0.r])
                        sink[0].op("dve", lambda e: e.tensor_tensor(out=rw_Yf.t[:, 0:w], in0=rw_Yf.t[:, 0:w], in1=T0.t[:, 0:w], op=ALU.add),
                              reads=[rw_Yf.r, T0.r], writes=[rw_Yf.r])
                        sink[0].op("dve", lambda e, pc=pc: e.tensor_tensor(out=ymix.t[:, pc, 0:w], in0=rw_Yf.t[:, 0:w], in1=rw_G[pc].t[:, 0:w],
                                                                      op=ALU.mult), reads=[rw_Yf.r, rw_G[pc].r], writes=[ymix.rs[pc]])

                RWKV_TILE = rw_tile

                it = 0
                a2_tiles = [(s_, t0_, w_) for s_ in range(3) for (t0_, w_, j_) in tiles_of(s_, WA)]
                a2_idx = [0]

                def a2_norm(idx):
                    s_, t0_, w_ = a2_tiles[idx]
                    xT_ = xTa[idx % 2]
                    load_xT(xT_, t0_, w_)
                    norm_fm(xT_, w_, sq, tmp, lnv, rstd, hT,
                            lambda c, s_=s_: G1.t[:, l, s_, c:c + 1], lambda c, s_=s_: mod.t[:, l, 0, s_, c:c + 1], hT.rs)
                for s in range(3):
                    hg_init(s)
                    if stage >= 4:
                        rw_init(s)

                    def a2_tile(s, t0, w, j, xT, l=l):
                        if a2_idx[0] == 0:
                            a2_norm(0)
                        fw.dma("pool", ymix.t[:, 2:6, 0:w], yfox.rearrange("(c p) t -> p c t", p=128)[:, :, t0:t0 + w],
                               reads=[yfreg(t0)], writes=ymix.rs[2:6], stream="yfld")
                        recs = []
                        if stage >= 3:
                            sink[0] = Rec()
                            recs.append(sink[0])
                            default_pool[0] = (0, 1)
                            hg_tile(s, w)
                        if stage >= 4 and RWKV_TILE is not None:
                            sink[0] = Rec()
                            recs.append(sink[0])
                            default_pool[0] = (2, 3, 4)
                            RWKV_TILE(s, w)
                        sink[0] = fw
                        default_pool[0] = (0, 1, 2, 3, 4)
                        merge_recs(fw, recs)
                        a2_idx[0] += 1
                        if a2_idx[0] < len(a2_tiles):
                            a2_norm(a2_idx[0])
                        for c in range(8):
                            po_ = next_bank()
                            for kc in range(8):
                                fw.op("pe", lambda e, c=c, kc=kc, po_=po_: e.matmul(
                                    po_.t[:, 0:w], lhsT=wo.t[:, kc, c * 128:(c + 1) * 128], rhs=ymix.t[:, kc, 0:w],
                                    start=(kc == 0), stop=(kc == 7)),
                                    reads=[wo.r, ymix.rs[kc]], writes=[po_.r], signal=(kc == 7))
                            fw.op("dve", lambda e, c=c, po_=po_: e.scalar_tensor_tensor(
                                out=xT.t[:, c, 0:w], in0=po_.t[:, 0:w], scalar=mod.t[:, l, 2, s, c:c + 1],
                                in1=xT.t[:, c, 0:w], op0=ALU.mult, op1=ALU.add),
                                reads=[po_.r, mod.r, xT.rs[c]], writes=[xT.rs[c]])
                        store_xT(xT, t0, w)
                    for (t0, w, j) in tiles_of(s, WA):
                        xT = xTa[it % 2]
                        it += 1
                        a2_tile(s, t0, w, j, xT)
                    if stage >= 3:
                        hg_final(s)
                    if stage >= 4:
                        rw_final(s, w)
                fw.flush()
                default_pool[0] = (0, 1, 2, 3, 4, 5, 6, 7)
            with contextlib.ExitStack() as ph:
                sub = FWScope(fw, ph)
                WB = 256
                wfi = Tt(sub.sbuf("wfi", [128, 8, 2 * DFF], BF16), name="wfi")
                wfo = Tt(sub.sbuf("wfo", [128, 22, D], BF16), name="wfo")
                for kc in range(8):
                    fw.dma("pool", wfi.t[:, kc, :], I["w_ffn_in"][l][kc * 128:(kc + 1) * 128, :], writes=[wfi.r],
                           stream="wld", group=True)
                for kc in range(22):
                    fw.dma("pool", wfo.t[:, kc, :], I["w_ffn_out"][l][kc * 128:(kc + 1) * 128, :], writes=[wfo.r],
                           stream="wld", group=True)
                xTb = [Tt(sub.sbuf("xTb%d" % i, [128, 8, WB], F32), nreg=8, name="xTb%d" % i) for i in range(2)]
                sq = Tt(sub.sbuf("sqb", [128, 8, WB], BF16), name="sqb")
                hT = Tt(sub.sbuf("hTb", [128, 8, WB], BF16), nreg=8, name="hTb")
                tmp = [Tt(sub.sbuf("tmpb%d" % i, [128, WB], F32), name="tmpb%d" % i) for i in range(2)]
                lnv = Tt(sub.sbuf("lnvb", [128, WB], F32), name="lnvb")
                rstd = Tt(sub.sbuf("rstdb", [128, WB], F32), name="rstdb")
                actT = Tt(sub.sbuf("actT", [128, 22, WB], BF16), nreg=22, name="actT")
                sg = [Tt(sub.sbuf("sg%d" % i, [128, WB], F32), name="sg%d" % i) for i in range(2)]
                tiles_b = [(s_, t0_, w_) for s_ in range(3) for (t0_, w_, j_) in tiles_of(s_, WB)]

                def b_norm(idx):
                    s_, t0_, w_ = tiles_b[idx]
                    xT_ = xTb[idx % 2]
                    load_xT(xT_, t0_, w_, q="sp")
                    norm_fm(xT_, w_, sq, tmp, lnv, rstd, hT,
                            lambda c, s_=s_: G2.t[:, l, s_, c:c + 1], lambda c, s_=s_: mod.t[:, l, 3, s_, c:c + 1], hT.rs)
                b_norm(0)
                for idx in range(len(tiles_b)):
                    if True:
                        s, t0, w = tiles_b[idx]
                        xT = xTb[idx % 2]
                        for f in range(22):
                            pg = next_bank()
                            pu = next_bank()
                            for kc in range(8):
                                fw.op("pe", lambda e, kc=kc, f=f, pg=pg, w=w: e.matmul(
                                    pg.t[:, 0:w], lhsT=wfi.t[:, kc, f * 128:(f + 1) * 128], rhs=hT.t[:, kc, 0:w],
                                    start=(kc == 0), stop=(kc == 7)),
                                    reads=[wfi.r, hT.rs[kc]], writes=[pg.r], signal=(kc == 7))
                            for kc in range(8):
                                fw.op("pe", lambda e, kc=kc, f=f, pu=pu, w=w: e.matmul(
                                    pu.t[:, 0:w], lhsT=wfi.t[:, kc, DFF + f * 128:DFF + (f + 1) * 128],
                                    rhs=hT.t[:, kc, 0:w], start=(kc == 0), stop=(kc == 7)),
                                    reads=[wfi.r, hT.rs[kc]], writes=[pu.r], signal=(kc == 7))
                            sgt = sg[f % 2]
                            fw.op("act", lambda e, pg=pg, sgt=sgt, w=w: e.activation(
                                out=sgt.t[:, 0:w], in_=pg.t[:, 0:w], func=AF.Silu), reads=[pg.r], writes=[sgt.r])
                            fw.op("dve", lambda e, pu=pu, sgt=sgt, f=f, w=w: e.tensor_tensor(
                                out=actT.t[:, f, 0:w], in0=sgt.t[:, 0:w], in1=pu.t[:, 0:w], op=ALU.mult),
                                reads=[sgt.r, pu.r], writes=[actT.rs[f]])
                        if idx + 1 < len(tiles_b):
                            b_norm(idx + 1)
                        for c in range(8):
                            po = next_bank()
                            for f in range(22):
                                fw.op("pe", lambda e, c=c, f=f, po=po, w=w: e.matmul(
                                    po.t[:, 0:w], lhsT=wfo.t[:, f, c * 128:(c + 1) * 128], rhs=actT.t[:, f, 0:w],
                                    start=(f == 0), stop=(f == 21)),
                                    reads=[wfo.r, actT.rs[f]], writes=[po.r], signal=(f == 21))
                            fw.op("dve", lambda e, c=c, po=po, xT=xT, s=s, w=w: e.scalar_tensor_tensor(
                                out=xT.t[:, c, 0:w], in0=po.t[:, 0:w], scalar=mod.t[:, l, 5, s, c:c + 1],
                                in1=xT.t[:, c, 0:w], op0=ALU.mult, op1=ALU.add),
                                reads=[po.r, mod.r, xT.rs[c]], writes=[xT.rs[c]])
                        store_xT(xT, t0, w, q="sp")
                fw.flush()

        with contextlib.ExitStack() as ph:
            sub = FWScope(fw, ph)
            WE = 512
            xTe = [Tt(sub.sbuf("xTe%d" % i, [128, 8, WE], F32), nreg=8, name="xTe%d" % i) for i in range(2)]
            sq = Tt(sub.sbuf("sqe", [128, 8, WE], BF16), name="sqe")
            yT = Tt(sub.sbuf("yTe", [128, 8, WE], F32), nreg=8, name="yTe")
            tmp = [Tt(sub.sbuf("tmpe%d" % i, [128, WE], F32), name="tmpe%d" % i) for i in range(2)]
            lnv = Tt(sub.sbuf("lnve", [128, WE], F32), name="lnve")
            rstd = Tt(sub.sbuf("rstde", [128, WE], F32), name="rstde")
            ytok = [Tt(sub.sbuf("ytok%d" % i, [128, D], F32), name="ytok%d" % i) for i in range(2)]
            it = 0
            ik = 0
            for s in range(3):
                dst = O["yp"][s] if s < 2 else O["ys"]
                off, T = seqs[s]
                for (t0, w, j) in tiles_of(s, WE):
                    xT = xTe[it % 2]
                    it += 1
                    load_xT(xT, t0, w)
                    norm_fm(xT, w, sq, tmp, lnv, rstd, yT, lambda c: fng.t[:, c:c + 1], None, yT.rs)
                    nb = (w + 127) // 128
                    pw = min(w, 128)
                    for tb in range(nb):
                        yt = ytok[ik % 2]
                        ik += 1
                        for half in range(2):
                            pb = next_bank()
                            for cc in range(4):
                                c = half * 4 + cc
                                fw.op("pe", lambda e, c=c, cc=cc, tb=tb, pb=pb, pw=pw: e.transpose(
                                    pb.t[0:pw, cc * 128:(cc + 1) * 128], yT.t[:, c, tb * 128:tb * 128 + pw],
                                    ident.t[:, :]),
                                    reads=[yT.rs[c], ident.r], writes=[pb.r], signal=(cc == 3))
                            if half == 0:
                                fw.op("act", lambda e, pb=pb, yt=yt, pw=pw: e.activation(
                                    out=yt.t[0:pw, 0:512], in_=pb.t[0:pw, :], func=AF.Copy), reads=[pb.r], writes=[yt.r])
                            else:
                                fw.op("dve", lambda e, pb=pb, yt=yt, pw=pw: e.tensor_copy(
                                    out=yt.t[0:pw, 512:1024], in_=pb.t[0:pw, :]), reads=[pb.r], writes=[yt.r])
                        lt0 = t0 - off + tb * 128
                        fw.dma("sp" if ik % 2 else "pool", dst[lt0:lt0 + pw, :], yt.t[0:pw, :], reads=[yt.r],
                               stream="yout")
            fw.flush()
        fw.finish()
    return nc


class FWScope:
    ctr = 0

    def __init__(self, fw, stack):
        self.fw = fw
        self.stack = stack

    def sbuf(self, name, shape, dt):
        FWScope.ctr += 1
        return self.stack.enter_context(self.fw.nc.sbuf_tensor("%s_u%d" % (name, FWScope.ctr), list(shape), dt))


def layer_mixer(fw, nc, I, O, l, L, SEQ, TS, PAST, env):
    pass


_PROG_CACHE = {}


def _get_prog(SEQ, DEPTH, TS, PAST, stage=9):
    key = (SEQ, DEPTH, TS, PAST, stage)
    if key not in _PROG_CACHE:
        _PROG_CACHE[key] = build_program(SEQ, DEPTH, TS, PAST, stage)
    return _PROG_CACHE[key]


def make_in_maps(inp, ncores, L):
    f = lambda a: np.ascontiguousarray(np.asarray(a, dtype=np.float32))
    maps = []
    shared = {k: f(inp[k]) for k in ("norm1_g", "w_ada", "b_ada", "w_in", "rw_mu", "rw_w0", "rw_w2", "rw_a0",
                                     "rw_a2", "rw_g2", "rw_k_k", "rw_k_a", "rw_ln_w", "rw_ln_b", "fox_b_f",
                                     "hg_lb_logits", "hg_norm_g", "w_out", "norm2_g", "w_ffn_in", "w_ffn_out",
                                     "final_norm_g")}
    shared["rw_r_k"] = f(inp["rw_r_k"]).reshape(L, 256)
    xp, xs = f(inp["x_prompt"]), f(inp["x_sample"])
    cp, cs = f(inp["c_prompt"]), f(inp["c_sample"])
    ck, cv, cl = f(inp["cache_fox_k"]), f(inp["cache_fox_v"]), f(inp["cache_fox_logf"])
    srw, ssh, shg = f(inp["state_rwkv"]), f(inp["state_rwkv_shift"]), f(inp["state_hgrn"])
    P = ck.shape[2]
    for i in range(ncores):
        m = dict(shared)
        m["xp"] = f(xp[2 * i:2 * i + 2])
        m["xs"] = f(xs[i])
        m["cc"] = f(np.concatenate([cp[2 * i:2 * i + 2], cs[i:i + 1]], axis=0))
        m["ck"] = f(ck[:, i].reshape(L, P, 512))
        m["cv"] = f(cv[:, i].reshape(L, P, 512))
        m["cl"] = f(cl[:, i])
        m["srw"] = f(srw[:, i])
        m["ssh"] = f(ssh[:, i, 0])
        m["shg"] = f(shg[:, i])
        maps.append(m)
    return maps


def gather_outputs(res, ncores, L, SEQ, TS):
    r = res
    cat = lambda k, ax: np.concatenate([r[i][k] for i in range(ncores)], axis=ax)
    stack = lambda k, ax: np.stack([r[i][k] for i in range(ncores)], axis=ax)
    yp = cat("yp", 0)
    ys = stack("ys", 0)
    fkp = cat("fkp", 1).reshape(L, 2 * ncores, SEQ, 8, 64)
    fvp = cat("fvp", 1).reshape(L, 2 * ncores, SEQ, 8, 64)
    flp = cat("flp", 1)
    rwp = cat("rwp", 1)
    rshp = cat("rshp", 1).reshape(L, 2 * ncores, 1, RW_COLS)
    hgp = cat("hgp", 1)
    fks = stack("fks", 1).reshape(L, ncores, TS, 8, 64)
    fvs = stack("fvs", 1).reshape(L, ncores, TS, 8, 64)
    fls = stack("fls", 1)
    rws = stack("rws", 1)
    rshs = stack("rshs", 1).reshape(L, ncores, 1, RW_COLS)
    hgs = stack("hgs", 1)
    return (yp, ys, fkp, fvp, flp, rwp, rshp, hgp, fks, fvs, fls, rws, rshs, hgs)


def kernel(**inputs):
    L = int(np.asarray(inputs["w_in"]).shape[0])
    SEQ = int(np.asarray(inputs["x_prompt"]).shape[1])
    TS = int(np.asarray(inputs["x_sample"]).shape[1])
    PAST = int(np.asarray(inputs["cache_fox_k"]).shape[2])
    ncores = int(np.asarray(inputs["x_sample"]).shape[0])
    nc = _get_prog(SEQ, L, TS, PAST)
    maps = make_in_maps(inputs, ncores, L)
    res = run_bass_kernel_spmd(nc, maps, core_ids=list(range(ncores)))
    return gather_outputs(res.results, ncores, L, SEQ, TS)
```

```python
import contextlib
import os
import numpy as np
import concourse.bass as bass
import concourse.mybir as mybir
from concourse.bass_utils import run_bass_kernel_spmd

F32 = mybir.dt.float32
BF16 = mybir.dt.bfloat16
AF = mybir.ActivationFunctionType
ALU = mybir.AluOpType

D = 1024
NC8 = 8
HD = 64
RW_COLS, FOX_COLS, HG_COLS = 896, 1544, 1024
IN_COLS = 3464
DFF = 2816
EPS = 1e-6


class Reg:
    __slots__ = ("name", "writers", "readers")

    def __init__(self, name=""):
        self.name = name
        self.writers = {}
        self.readers = {}


class Eng:
    def __init__(self, name, kind):
        self.name = name
        self.kind = kind
        self.ops = []
        self.sem = None
        self.count = 0
        self.waited = {}


class FW:
    def __init__(self, nc, stack):
        self.nc = nc
        self.stack = stack
        self.engs = {}
        self.dma_sems = {}
        self.dma_counts = {}
        self.group_sems = {}
        self.nsem = 0
        for name in ("pe", "act", "dve", "pool", "sp"):
            e = Eng(name, name)
            self.engs[name] = e
            if name != "sp":
                e.sem = self.new_sem("s_" + name)

    def new_sem(self, name):
        self.nsem += 1
        return self.stack.enter_context(self.nc.semaphore("%s_%d" % (name, self.nsem)))

    def sbuf(self, name, shape, dt):
        return self.stack.enter_context(self.nc.sbuf_tensor(name, list(shape), dt))

    def psum(self, name, shape, dt=F32):
        return self.stack.enter_context(self.nc.psum_tensor(name, list(shape), dt))

    def _collect(self, reads, writes):
        deps = {}

        def add(d):
            for k, (sem, val) in d.items():
                cur = deps.get(k)
                if cur is None or cur[1] < val:
                    deps[k] = (sem, val)
        for r in reads:
            add(r.writers)
        for w in writes:
            add(w.writers)
            add(w.readers)
        return deps

    def _waits(self, eng, deps, raw_keys):
        waits = []
        for k, (sem, val) in deps.items():
            if eng.sem is not None and k == id(eng.sem) and k not in raw_keys:
                continue
            if eng.waited.get(k, 0) >= val:
                continue
            eng.waited[k] = val
            st = self.group_sems.get(k)
            if st is not None:
                waits.append((sem, _Lazy(self.dma_counts, st)))
            else:
                waits.append((sem, val))
        return waits

    def op(self, engname, fn, reads=(), writes=(), signal=True):
        eng = self.engs[engname]
        reads = [r for r in reads if r is not None]
        writes = [w for w in writes if w is not None]
        deps = self._collect(reads, writes)
        raw_keys = set()
        k = id(eng.sem)
        if engname != "pe":
            raw_keys.add(k)
        for r in reads:
            if k in r.writers:
                raw_keys.add(k)
        waits = self._waits(eng, deps, raw_keys)
        sem = eng.sem
        if signal:
            eng.count += 1
            tok = (sem, eng.count)
        else:
            tok = (sem, eng.count + 1)
        for r in reads:
            r.readers[id(sem)] = tok
        for w in writes:
            w.writers = {id(sem): tok}
            w.readers = {}

        def run(e, fn=fn, waits=waits, signal=signal, sem=sem):
            for (s, v) in waits:
                e.wait_ge(s, int(v))
            ins = fn(e)
            if signal:
                ins.then_inc(sem, 1)
        eng.ops.append(run)

    def dma(self, qname, out, in_, reads=(), writes=(), stream="d", group=False, **kw):
        eng = self.engs[qname]
        reads = [r for r in reads if r is not None]
        writes = [w for w in writes if w is not None]
        deps = self._collect(reads, writes)
        stream = stream + "_" + qname
        if stream not in self.dma_sems:
            self.dma_sems[stream] = self.new_sem("dq_" + stream)
            self.dma_counts[stream] = 0
            if group:
                self.group_sems[id(self.dma_sems[stream])] = stream
        if group:
            deps.pop(id(self.dma_sems[stream]), None)
        waits = self._waits(eng, deps, set(deps.keys()))
        sem = self.dma_sems[stream]
        self.dma_counts[stream] += 16
        tok = (sem, self.dma_counts[stream])
        for r in reads:
            r.readers[id(sem)] = tok
        for w in writes:
            w.writers = {id(sem): tok}
            w.readers = {}

        def run(e, waits=waits, sem=sem, out=out, in_=in_, kw=kw):
            for (s, v) in waits:
                e.wait_ge(s, int(v))
            e.dma_start(out=out, in_=in_, **kw).then_inc(sem, 16)
        eng.ops.append(run)

    def rotate(self):
        for e in self.engs.values():
            if e.sem is not None:
                e.sem = self.new_sem("s_" + e.name)
                e.count = 0

    def barrier(self):
        toks = []
        for e in self.engs.values():
            if e.sem is not None and e.count > 0:
                toks.append((e.sem, e.count))
        for s in self.dma_sems:
            if self.dma_counts[s] > 0:
                toks.append((self.dma_sems[s], self.dma_counts[s]))
        for e in self.engs.values():
            waits = []
            for (sem, val) in toks:
                if sem is e.sem:
                    continue
                if e.waited.get(id(sem), 0) >= val:
                    continue
                e.waited[id(sem)] = val
                waits.append((sem, val))

            def run(h, waits=waits):
                for (s, v) in waits:
                    h.wait_ge(s, v)
            e.ops.append(run)

    def finish(self):
        self.flush()

    def flush(self):
        self.barrier()
        nc = self.nc
        engs = self.engs
        oplists = {k: e.ops for k, e in engs.items()}
        for e in engs.values():
            e.ops = []

        class _E:
            def __init__(self, ops):
                self.ops = ops
        self_engs = {k: _E(v) for k, v in oplists.items()}
        with nc.Block() as block:
            def mk(eng):
                def body(e):
                    for f in eng.ops:
                        f(e)
                return body
            block.tensor(mk(self_engs["pe"]))
            block.scalar(mk(self_engs["act"]))
            block.vector(mk(self_engs["dve"]))
            block.gpsimd(mk(self_engs["pool"]))
            block.sync(mk(self_engs["sp"]))


class _Lazy:
    def __init__(self, counts, stream):
        self.counts = counts
        self.stream = stream

    def __int__(self):
        return self.counts[self.stream]


class Rec:
    def __init__(self):
        self.items = []

    def op(self, *a, **k):
        self.items.append(("op", a, k))

    def dma(self, *a, **k):
        self.items.append(("dma", a, k))

    def flush(self):
        pass

    def replay_into(self, sink):
        for (kind, a, k) in self.items:
            getattr(sink, kind)(*a, **k)


def merge_recs(sink, recs):
    pos = [0] * len(recs)
    n = [len(r.items) for r in recs]
    total = sum(n)
    for _ in range(total):
        best, bf_ = -1, 2.0
        for i in range(len(recs)):
            if pos[i] < n[i]:
                f = pos[i] / n[i]
                if f < bf_:
                    best, bf_ = i, f
        kind, a, k = recs[best].items[pos[best]]
        pos[best] += 1
        getattr(sink, kind)(*a, **k)


class Tt:
    def __init__(self, t, nreg=1, name=""):
        self.t = t
        self.rs = [Reg("%s%d" % (name, i)) for i in range(nreg)]
        self.r = self.rs[0]


def build_program(SEQ, DEPTH, TS=32, PAST=2048, stage=9):
    L = DEPTH
    NTOK = 2 * SEQ + TS
    nc = bass.Bass("TRN2", target_bir_lowering=False)
    din = lambda n, s: nc.dram_tensor(n, list(s), F32, kind="ExternalInput").ap()
    dout = lambda n, s: nc.dram_tensor(n, list(s), F32, kind="ExternalOutput").ap()
    I = dict(
        xp=din("xp", (2, SEQ, D)), xs=din("xs", (TS, D)), cc=din("cc", (3, D)),
        ck=din("ck", (L, PAST, 512)), cv=din("cv", (L, PAST, 512)), cl=din("cl", (L, PAST, 8)),
        srw=din("srw", (L, 4, 64, 64)), ssh=din("ssh", (L, RW_COLS)), shg=din("shg", (L, 4, 64, 64)),
        norm1_g=din("norm1_g", (L, D)), w_ada=din("w_ada", (L, D, 6 * D)), b_ada=din("b_ada", (L, 6 * D)),
        w_in=din("w_in", (L, D, IN_COLS)), rw_mu=din("rw_mu", (L, RW_COLS)), rw_w0=din("rw_w0", (L, 256)),
        rw_w2=din("rw_w2", (L, 32, 256)), rw_a0=din("rw_a0", (L, 256)), rw_a2=din("rw_a2", (L, 32, 256)),
        rw_g2=din("rw_g2", (L, 64, 256)), rw_k_k=din("rw_k_k", (L, 256)), rw_k_a=din("rw_k_a", (L, 256)),
        rw_r_k=din("rw_r_k", (L, 256)), rw_ln_w=din("rw_ln_w", (L, 256)), rw_ln_b=din("rw_ln_b", (L, 256)),
        fox_b_f=din("fox_b_f", (L, 8)), hg_lb_logits=din("hg_lb_logits", (L, 256)),
        hg_norm_g=din("hg_norm_g", (L, 256)), w_out=din("w_out", (L, D, D)), norm2_g=din("norm2_g", (L, D)),
        w_ffn_in=din("w_ffn_in", (L, D, 2 * DFF)), w_ffn_out=din("w_ffn_out", (L, DFF, D)),
        final_norm_g=din("final_norm_g", (D,)),
    )
    O = dict(
        yp=dout("yp", (2, SEQ, D)), ys=dout("ys", (TS, D)),
        fkp=dout("fkp", (L, 2, SEQ, 512)), fvp=dout("fvp", (L, 2, SEQ, 512)), flp=dout("flp", (L, 2, SEQ, 8)),
        rwp=dout("rwp", (L, 2, 4, 64, 64)), rshp=dout("rshp", (L, 2, RW_COLS)), hgp=dout("hgp", (L, 2, 4, 64, 64)),
        fks=dout("fks", (L, TS, 512)), fvs=dout("fvs", (L, TS, 512)), fls=dout("fls", (L, TS, 8)),
        rws=dout("rws", (L, 4, 64, 64)), rshs=dout("rshs", (L, RW_COLS)), hgs=dout("hgs", (L, 4, 64, 64)),
    )
    xres = nc.dram_tensor("xres", [D, NTOK], F32).ap()
    xres_r = Reg("xres")
    seqs = [(0, SEQ), (SEQ, SEQ), (2 * SEQ, TS)]

    def tiles_of(s, W):
        off, T = seqs[s]
        w = min(W, T)
        return [(off + j * w, w, j) for j in range(T // w)]

    xres_regs = {}

    def xreg(t0):
        return xres_regs.setdefault(t0, Reg("xres%d" % t0))

    with contextlib.ExitStack() as top:
        fw = FW(nc, top)
        ident = Tt(fw.sbuf("ident", [128, 128], F32), name="ident")
        identb = Tt(fw.sbuf("identb", [128, 128], BF16), name="identb")
        onesb = Tt(fw.sbuf("onesb", [128, 128], BF16), name="onesb")
        fw.op("pool", lambda e: e.memset(ident.t[:], 0.0), writes=[ident.r])
        fw.op("pool", lambda e: e.affine_select(out=ident.t[:], in_=ident.t[:], pattern=[[-1, 128]],
                                                compare_op=ALU.not_equal, fill=1.0, base=0,
                                                channel_multiplier=1), reads=[ident.r], writes=[ident.r])
        fw.op("pool", lambda e: e.tensor_copy(out=identb.t[:], in_=ident.t[:]), reads=[ident.r], writes=[identb.r])
        fw.op("pool", lambda e: e.memset(onesb.t[:], 1.0), writes=[onesb.r])
        epsb = Tt(fw.sbuf("epsb", [128, 1], F32), name="epsb")
        fw.op("pool", lambda e: e.memset(epsb.t[:], EPS), writes=[epsb.r])


        trif = Tt(fw.sbuf("trif", [128, 128], F32), name="trif")
        fw.op("pool", lambda e: e.memset(trif.t[:], 1.0), writes=[trif.r])
        fw.op("pool", lambda e: e.affine_select(out=trif.t[:], in_=trif.t[:], pattern=[[1, 128]],
                                                compare_op=ALU.is_ge, fill=0.0, base=0, channel_multiplier=-1),
              reads=[trif.r], writes=[trif.r])
        self127 = Tt(fw.sbuf("self127", [128, 128], F32), name="self127")
        fw.op("pool", lambda e: e.memset(self127.t[:], 0.0), writes=[self127.r])
        fw.op("pool", lambda e: e.affine_select(out=self127.t[:], in_=self127.t[:], pattern=[[0, 128]],
                                                compare_op=ALU.not_equal, fill=1.0, base=-127, channel_multiplier=1),
              reads=[self127.r], writes=[self127.r])
        mnegf = Tt(fw.sbuf("mnegf", [128, 128], F32), name="mnegf")
        maskneg = Tt(fw.sbuf("maskneg", [128, 128], BF16), name="maskneg")
        fw.op("pool", lambda e: e.memset(mnegf.t[:], 0.0), writes=[mnegf.r])
        fw.op("pool", lambda e: e.affine_select(out=mnegf.t[:], in_=mnegf.t[:], pattern=[[1, 128]],
                                                compare_op=ALU.is_ge, fill=-30000.0, base=0, channel_multiplier=-1),
              reads=[mnegf.r], writes=[mnegf.r])
        fw.op("pool", lambda e: e.tensor_copy(out=maskneg.t[:], in_=mnegf.t[:]), reads=[mnegf.r], writes=[maskneg.r])
        e8 = Tt(fw.sbuf("e8", [8, 8, 128], F32), name="e8")
        fw.op("pool", lambda e: e.memset(e8.t[:], 0.0), writes=[e8.r])
        fw.op("pool", lambda e: e.affine_select(out=e8.t[:], in_=e8.t[:], pattern=[[-1, 8], [0, 128]],
                                                compare_op=ALU.not_equal, fill=1.0, base=0, channel_multiplier=1),
              reads=[e8.r], writes=[e8.r])
        selh = Tt(fw.sbuf("selh", [72, 8, 128], BF16), name="selh")
        fw.op("pool", lambda e: e.memset(selh.t[:], 0.0), writes=[selh.r])
        for b0 in (0, 32, 64):
            fw.op("dve", lambda e, b0=b0: e.tensor_copy(out=selh.t[b0:b0 + 8, :, :], in_=e8.t[:]),
                  reads=[e8.r], writes=[selh.r])
        onesf = Tt(fw.sbuf("onesf", [128, 64], F32), name="onesf")
        fw.op("pool", lambda e: e.memset(onesf.t[:], 1.0), writes=[onesf.r])
        bfb = Tt(fw.sbuf("bfb", [128, L, 8], F32), name="bfb")
        for l_ in range(L):
            fw.dma("sp", bfb.t[:, l_, :], I["fox_b_f"][l_:l_ + 1, :].to_broadcast([128, 8]), writes=[bfb.r],
                   stream="par", group=True)
        yfox = nc.dram_tensor("yfox", [512, NTOK], BF16).ap()
        yfox_regs = {}

        def yfreg(t0):
            return yfox_regs.setdefault(t0, Reg("yfox%d" % t0))

        banks = [Tt(fw.psum("bank%d" % i, [128, 512]), name="bank%d" % i) for i in range(8)]
        bank_ctr = [0]

        default_pool = [(0, 1, 2, 3, 4, 5, 6, 7)]

        def next_bank(pool=None):
            if pool is None:
                pool = default_pool[0]
            b = banks[pool[bank_ctr[0] % len(pool)]]
            bank_ctr[0] += 1
            return b

        def load_fm(name, src_ap, ncol, q="sp"):
            t = Tt(fw.sbuf(name, [128, L, ncol], F32), name=name)
            fw.dma(q, t.t[:], src_ap.rearrange("l (c p) -> p l c", p=128), writes=[t.r], stream="par", group=True,
                   allow_slow_non_contiguous=True)
            return t
        n1g = load_fm("n1g", I["norm1_g"], 8)
        n2g = load_fm("n2g", I["norm2_g"], 8)
        badaT = load_fm("badaT", I["b_ada"], 48)
        fng = Tt(fw.sbuf("fng", [128, 8], F32), name="fng")
        fw.dma("sp", fng.t[:], I["final_norm_g"].rearrange("(c p) -> p c", p=128), writes=[fng.r], stream="par", group=True,
               allow_slow_non_contiguous=True)

        rmask32 = Tt(fw.sbuf("rmask32", [128, 512], F32), name="rmask32")
        fw.op("pool", lambda e: e.memset(rmask32.t[:], 1.0), writes=[rmask32.r])
        fw.op("pool", lambda e: e.affine_select(out=rmask32.t[:, :].rearrange("p (c t) -> p c t", t=32),
                                                in_=rmask32.t[:, :].rearrange("p (c t) -> p c t", t=32),
                                                pattern=[[0, 16], [1, 32]], compare_op=ALU.not_equal, fill=0.0,
                                                base=0, channel_multiplier=0), reads=[rmask32.r], writes=[rmask32.r])
        maskbd = Tt(fw.sbuf("maskbd", [128, 128], F32), name="maskbd")
        fw.op("pool", lambda e: e.tensor_copy(out=maskbd.t[:], in_=trif.t[:]), reads=[trif.r], writes=[maskbd.r])
        for cb_ in range(1, 4):
            fw.op("pool", lambda e, cb_=cb_: e.affine_select(
                out=maskbd.t[:, cb_ * 32:(cb_ + 1) * 32], in_=maskbd.t[:, cb_ * 32:(cb_ + 1) * 32], pattern=[[0, 32]],
                compare_op=ALU.is_ge, fill=0.0, base=-cb_ * 32, channel_multiplier=1),
                reads=[maskbd.r], writes=[maskbd.r])
        onesbdf = Tt(fw.sbuf("onesbdf", [128, 128], F32), name="onesbdf")
        onesbd = Tt(fw.sbuf("onesbd", [128, 128], BF16), name="onesbd")
        fw.op("pool", lambda e: e.memset(onesbdf.t[:], 1.0), writes=[onesbdf.r])
        fw.op("pool", lambda e: e.affine_select(out=onesbdf.t[:, 0:64], in_=onesbdf.t[:, 0:64], pattern=[[0, 64]],
                                                compare_op=ALU.is_ge, fill=0.0, base=63, channel_multiplier=-1),
              reads=[onesbdf.r], writes=[onesbdf.r])
        fw.op("pool", lambda e: e.affine_select(out=onesbdf.t[:, 64:128], in_=onesbdf.t[:, 64:128], pattern=[[0, 64]],
                                                compare_op=ALU.is_ge, fill=0.0, base=-64, channel_multiplier=1),
              reads=[onesbdf.r], writes=[onesbdf.r])
        fw.op("pool", lambda e: e.tensor_copy(out=onesbd.t[:], in_=onesbdf.t[:]), reads=[onesbdf.r], writes=[onesbd.r])
        hgng = load_fm("hgng", I["hg_norm_g"], 2)
        lbl = load_fm("lbl", I["hg_lb_logits"], 2)
        lbT = Tt(fw.sbuf("lbT", [128, L, 2], F32), name="lbT")
        omlT = Tt(fw.sbuf("omlT", [128, L, 2], F32), name="omlT")
        nomlT = Tt(fw.sbuf("nomlT", [128, L, 2], F32), name="nomlT")
        lbm = Tt(fw.sbuf("lbm", [128, 2], F32), name="lbm")
        lbe = Tt(fw.sbuf("lbe", [128, L, 2], F32), name="lbe")
        lbs_ = Tt(fw.sbuf("lbs_", [128, 2], F32), name="lbs_")
        fw.op("dve", lambda e: e.tensor_copy(out=lbm.t[:], in_=lbl.t[:, 0, :]), reads=[lbl.r], writes=[lbm.r])
        for l_ in range(1, L):
            fw.op("dve", lambda e, l_=l_: e.tensor_max(out=lbm.t[:], in0=lbm.t[:], in1=lbl.t[:, l_, :]),
                  reads=[lbm.r, lbl.r], writes=[lbm.r])
        for l_ in range(L):
            fw.op("dve", lambda e, l_=l_: e.tensor_sub(out=lbe.t[:, l_, :], in0=lbl.t[:, l_, :], in1=lbm.t[:]),
                  reads=[lbm.r, lbl.r], writes=[lbe.r])
        fw.op("act", lambda e: e.activation(out=lbe.t[:], in_=lbe.t[:], func=AF.Exp), reads=[lbe.r], writes=[lbe.r])
        fw.op("dve", lambda e: e.tensor_copy(out=lbs_.t[:], in_=lbe.t[:, 0, :]), reads=[lbe.r], writes=[lbs_.r])
        for l_ in range(1, L):
            fw.op("dve", lambda e, l_=l_: e.tensor_add(out=lbs_.t[:], in0=lbs_.t[:], in1=lbe.t[:, l_, :]),
                  reads=[lbs_.r, lbe.r], writes=[lbs_.r])
        fw.op("dve", lambda e: e.reciprocal(out=lbs_.t[:], in_=lbs_.t[:]), reads=[lbs_.r], writes=[lbs_.r])
        for l_ in range(L):
            fw.op("dve", lambda e, l_=l_: e.tensor_mul(out=lbe.t[:, l_, :], in0=lbe.t[:, l_, :], in1=lbs_.t[:]),
                  reads=[lbs_.r, lbe.r], writes=[lbe.r])
        fw.op("dve", lambda e: e.memset(lbT.t[:, 0, :], 0.0), writes=[lbT.r])
        for l_ in range(1, L):
            fw.op("dve", lambda e, l_=l_: e.tensor_add(out=lbT.t[:, l_, :], in0=lbT.t[:, l_ - 1, :], in1=lbe.t[:, l_, :]),
                  reads=[lbT.r, lbe.r], writes=[lbT.r])
        fw.op("dve", lambda e: e.tensor_scalar(out=omlT.t[:], in0=lbT.t[:], scalar1=-1.0, scalar2=1.0,
                                               op0=ALU.mult, op1=ALU.add), reads=[lbT.r], writes=[omlT.r])
        fw.op("dve", lambda e: e.tensor_scalar_mul(out=nomlT.t[:], in0=omlT.t[:], scalar1=-1.0),
              reads=[omlT.r], writes=[nomlT.r])

        rmask64 = Tt(fw.sbuf("rmask64", [128, 512], F32), name="rmask64")
        fw.op("pool", lambda e: e.memset(rmask64.t[:], 1.0), writes=[rmask64.r])
        fw.op("pool", lambda e: e.affine_select(out=rmask64.t[:, :].rearrange("p (c t) -> p c t", t=64),
                                                in_=rmask64.t[:, :].rearrange("p (c t) -> p c t", t=64),
                                                pattern=[[0, 8], [1, 64]], compare_op=ALU.not_equal, fill=0.0,
                                                base=0, channel_multiplier=0), reads=[rmask64.r], writes=[rmask64.r])
        mask12 = Tt(fw.sbuf("mask12", [64, 2, 64], F32), name="mask12")
        mask34 = Tt(fw.sbuf("mask34", [64, 2, 64], F32), name="mask34")
        mask5 = Tt(fw.sbuf("mask5", [64, 64], F32), name="mask5")
        fw.op("pool", lambda e: e.memset(mask12.t[:], 1.0), writes=[mask12.r])
        fw.op("pool", lambda e: e.affine_select(out=mask12.t[:, 0, :], in_=mask12.t[:, 0, :], pattern=[[1, 64]],
                                                compare_op=ALU.is_ge, fill=0.0, base=-1, channel_multiplier=-1),
              reads=[mask12.r], writes=[mask12.r])
        fw.op("pool", lambda e: e.affine_select(out=mask12.t[:, 1, :], in_=mask12.t[:, 1, :], pattern=[[1, 64]],
                                                compare_op=ALU.is_ge, fill=0.0, base=0, channel_multiplier=-1),
              reads=[mask12.r], writes=[mask12.r])
        fw.op("dve", lambda e: e.tensor_scalar_mul(out=mask34.t[:, 0, :], in0=mask12.t[:, 0, :], scalar1=-1.0),
              reads=[mask12.r], writes=[mask34.r])
        fw.op("pool", lambda e: e.tensor_copy(out=mask34.t[:, 1, :], in_=mask12.t[:, 1, :]),
              reads=[mask12.r], writes=[mask34.r])
        fw.op("pool", lambda e: e.memset(mask5.t[:], -1.0), writes=[mask5.r])
        fw.op("pool", lambda e: e.affine_select(out=mask5.t[:], in_=mask5.t[:], pattern=[[-1, 64]],
                                                compare_op=ALU.is_ge, fill=0.0, base=-1, channel_multiplier=1),
              reads=[mask5.r], writes=[mask5.r])
        eps2 = Tt(fw.sbuf("eps2", [128, 1], F32), name="eps2")
        fw.op("pool", lambda e: e.memset(eps2.t[:], 64e-5), writes=[eps2.r])
        rwp = {}
        for nm in ("rw_w0", "rw_a0", "rw_k_k", "rw_k_a", "rw_r_k", "rw_ln_w", "rw_ln_b"):
            rwp[nm] = load_fm("p_" + nm, I[nm], 2)
        nw0 = Tt(fw.sbuf("nw0", [128, L, 2], F32), name="nw0")
        na0 = Tt(fw.sbuf("na0", [128, L, 2], F32), name="na0")
        omka = Tt(fw.sbuf("omka", [128, L, 2], F32), name="omka")
        fw.op("dve", lambda e: e.tensor_scalar_mul(out=nw0.t[:], in0=rwp["rw_w0"].t[:], scalar1=-1.0),
              reads=[rwp["rw_w0"].r], writes=[nw0.r])
        fw.op("dve", lambda e: e.tensor_scalar_mul(out=na0.t[:], in0=rwp["rw_a0"].t[:], scalar1=-1.0),
              reads=[rwp["rw_a0"].r], writes=[na0.r])
        fw.op("dve", lambda e: e.tensor_scalar(out=omka.t[:], in0=rwp["rw_k_a"].t[:], scalar1=-1.0, scalar2=1.0,
                                               op0=ALU.mult, op1=ALU.add), reads=[rwp["rw_k_a"].r], writes=[omka.r])
        mul = Tt(fw.sbuf("mul", [128, L, 9], F32), name="mul")
        fw.op("pool", lambda e: e.memset(mul.t[:], 0.0), writes=[mul.r])
        m7 = load_fm("m7", I["rw_mu"], 7)
        fw.op("dve", lambda e: e.tensor_copy(out=mul.t[:, :, 0:6], in_=m7.t[:, :, 0:6]), reads=[m7.r], writes=[mul.r])
        fw.op("dve", lambda e: e.tensor_copy(out=mul.t[0:32, :, 6], in_=m7.t[0:32, :, 6]), reads=[m7.r], writes=[mul.r])
        fw.op("dve", lambda e: e.tensor_copy(out=mul.t[0:32, :, 7], in_=m7.t[32:64, :, 6]), reads=[m7.r], writes=[mul.r])
        fw.op("dve", lambda e: e.tensor_copy(out=mul.t[0:64, :, 8], in_=m7.t[64:128, :, 6]), reads=[m7.r], writes=[mul.r])
        w2b = Tt(fw.sbuf("w2b", [32, L, 256], BF16), name="w2b")
        a2b = Tt(fw.sbuf("a2b", [32, L, 256], BF16), name="a2b")
        g2b = Tt(fw.sbuf("g2b", [64, L, 256], BF16), name="g2b")
        fw.dma("pool", w2b.t[:], I["rw_w2"].rearrange("l k n -> k l n"), writes=[w2b.r], stream="parb", group=True)
        fw.dma("pool", a2b.t[:], I["rw_a2"].rearrange("l k n -> k l n"), writes=[a2b.r], stream="parb", group=True)
        fw.dma("pool", g2b.t[:], I["rw_g2"].rearrange("l k n -> k l n"), writes=[g2b.r], stream="parb", group=True)

        cT = Tt(fw.sbuf("cT", [128, 3, 8], F32), name="cT")
        fw.dma("sp", cT.t[:], I["cc"].rearrange("b (c p) -> p b c", p=128), writes=[cT.r], stream="par", group=True,
               allow_slow_non_contiguous=True)
        siluT = Tt(fw.sbuf("siluT", [128, 8, 3], F32), name="siluT")
        sl_e = Tt(fw.sbuf("sl_e", [128, 3, 8], F32), name="sl_e")
        fw.op("act", lambda e: e.activation(out=sl_e.t[:], in_=cT.t[:], func=AF.Exp, scale=-1.0),
              reads=[cT.r], writes=[sl_e.r])
        fw.op("dve", lambda e: e.tensor_scalar_add(out=sl_e.t[:], in0=sl_e.t[:], scalar1=1.0),
              reads=[sl_e.r], writes=[sl_e.r])
        fw.op("dve", lambda e: e.reciprocal(out=sl_e.t[:], in_=sl_e.t[:]), reads=[sl_e.r], writes=[sl_e.r])
        fw.op("dve", lambda e: e.tensor_tensor(out=siluT.t[:].rearrange("p c b -> p b c"), in0=sl_e.t[:],
                                               in1=cT.t[:], op=ALU.mult),
              reads=[sl_e.r, cT.r], writes=[siluT.r])
        mod = Tt(fw.sbuf("mod", [128, L, 6, 3, 8], F32), name="mod")
        G1 = Tt(fw.sbuf("G1", [128, L, 3, 8], F32), name="G1")
        G2 = Tt(fw.sbuf("G2", [128, L, 3, 8], F32), name="G2")
        PH0 = contextlib.ExitStack()
        fw_real = fw
        for ph in [PH0]:
            sub = FWScope(fw, ph)
            wa = [Tt(sub.sbuf("wa%d" % i, [128, 8, 512], F32), name="wa%d" % i) for i in range(2)]
            default_pool[0] = (0, 1, 2, 3)
            fw = Rec()
            k = 0
            for l in range(L):
                for jg in range(12):
                    w = wa[k % 2]
                    k += 1
                    fw.dma("sp" if k % 2 else "pool", w.t[:],
                           I["w_ada"][l].rearrange("(kc p) n -> p kc n", p=128)[:, :, jg * 512:(jg + 1) * 512],
                           writes=[w.r], stream="wada%d" % (k % 2))
                    for jj in range(4):
                        j = jg * 4 + jj
                        m, c = j // 8, j % 8
                        pb = next_bank()
                        for kc in range(8):
                            fw.op("pe", lambda e, w=w, pb=pb, kc=kc, jj=jj: e.matmul(
                                pb.t[:, 0:3], lhsT=w.t[:, kc, jj * 128:(jj + 1) * 128], rhs=siluT.t[:, kc, :],
                                start=(kc == 0), stop=(kc == 7)),
                                reads=[w.r, siluT.r], writes=[pb.r], signal=(kc == 7))
                        fw.op("dve", lambda e, pb=pb, l=l, m=m, c=c, j=j: e.tensor_scalar(
                            out=mod.t[:, l, m, :, c], in0=pb.t[:, 0:3], scalar1=badaT.t[:, l, j:j + 1], scalar2=None,
                            op0=ALU.add), reads=[pb.r, badaT.r], writes=[mod.r])
        for l in range(L):
            for (G, ng, mi) in ((G1, n1g, 1), (G2, n2g, 4)):
                for b in range(3):
                    fw.op("dve", lambda e, G=G, ng=ng, mi=mi, l=l, b=b: e.scalar_tensor_tensor(
                        out=G.t[:, l, b, :], in0=mod.t[:, l, mi, b, :], scalar=1.0, in1=ng.t[:, l, :],
                        op0=ALU.add, op1=ALU.mult), reads=[mod.r, ng.r], writes=[G.r])
        rec_ada = fw
        fw = fw_real

        def load_xT(xT, t0, w, q="sp"):
            fw.dma(q, xT.t[:, :, 0:w], xres.rearrange("(c p) t -> p c t", p=128)[:, :, t0:t0 + w],
                   reads=[xreg(t0)], writes=xT.rs, stream="xld" + xT.r.name[-2:])

        def store_xT(xT, t0, w, q="sp"):
            fw.dma(q, xres.rearrange("(c p) t -> p c t", p=128)[:, :, t0:t0 + w], xT.t[:, :, 0:w],
                   reads=xT.rs, writes=[xreg(t0)], stream="xst" + xT.r.name[-2:])

        def norm_fm(xT, w, sq, tmp, lnv, rstd, out, gap, shap, out_regs):
            fw.op("act", lambda e: e.activation(out=sq.t[:, :, 0:w], in_=xT.t[:, :, 0:w], func=AF.Square),
                  reads=xT.rs, writes=[sq.r])
            pb = next_bank()
            for c in range(8):
                fw.op("pe", lambda e, c=c: e.matmul(pb.t[:, 0:w], lhsT=onesb.t[:], rhs=sq.t[:, c, 0:w],
                                                    start=(c == 0), stop=(c == 7)),
                      reads=[onesb.r, sq.r], writes=[pb.r], signal=(c == 7))
            fw.op("act", lambda e: e.activation(out=lnv.t[:, 0:w], in_=pb.t[:, 0:w], func=AF.Ln, scale=1.0 / D,
                                                bias=epsb.t[:]), reads=[pb.r, epsb.r], writes=[lnv.r])
            fw.op("act", lambda e: e.activation(out=rstd.t[:, 0:w], in_=lnv.t[:, 0:w], func=AF.Exp, scale=-0.5),
                  reads=[lnv.r], writes=[rstd.r])
            for c in range(8):
                tm = tmp[c % len(tmp)]
                fw.op("dve", lambda e, c=c, tm=tm: e.tensor_tensor(out=tm.t[:, 0:w], in0=xT.t[:, c, 0:w],
                                                                   in1=rstd.t[:, 0:w], op=ALU.mult),
                      reads=[xT.rs[c], rstd.r], writes=[tm.r])
                if shap is not None:
                    fw.op("act", lambda e, c=c, tm=tm: e.activation(out=out.t[:, c, 0:w], in_=tm.t[:, 0:w],
                                                                    func=AF.Identity, scale=gap(c), bias=shap(c)),
                          reads=[tm.r, G1.r, G2.r, mod.r], writes=[out_regs[c]])
                else:
                    fw.op("act", lambda e, c=c, tm=tm: e.activation(out=out.t[:, c, 0:w], in_=tm.t[:, 0:w],
                                                                    func=AF.Identity, scale=gap(c)),
                          reads=[tm.r, fng.r], writes=[out_regs[c]])

        for ph in [PH0]:
            sub = FWScope(fw, ph)
            xtok = [Tt(sub.sbuf("xtok%d" % i, [128, 4, D], F32), name="xtok%d" % i) for i in range(2)]
            xTs = [Tt(sub.sbuf("xTp%d" % i, [128, 8, 512], F32), nreg=8, name="xTp%d" % i) for i in range(2)]
            default_pool[0] = (4, 5, 6, 7)
            fw = Rec()
            it = 0
            for s in range(3):
                src = I["xp"][s] if s < 2 else I["xs"]
                off, T = seqs[s]
                for (t0, w, j) in tiles_of(s, 512):
                    xt = xtok[it % 2]
                    xT = xTs[it % 2]
                    it += 1
                    nb = (w + 127) // 128
                    pw = min(w, 128)
                    lt0 = t0 - off
                    if w >= 128:
                        fw.dma("sp" if it % 2 else "pool", xt.t[:, 0:nb, :],
                               src[lt0:lt0 + w, :].rearrange("(b p) d -> p b d", p=128), writes=[xt.r], stream="xtok")
                    else:
                        fw.dma("sp", xt.t[0:w, 0, :], src[lt0:lt0 + w, :], writes=[xt.r], stream="xtok")
                    for c in range(8):
                        pb = next_bank()
                        for tb in range(nb):
                            fw.op("pe", lambda e, c=c, tb=tb, pb=pb, xt=xt, pw=pw: e.transpose(
                                pb.t[:, tb * 128:tb * 128 + pw], xt.t[0:pw, tb, c * 128:(c + 1) * 128],
                                ident.t[0:pw, 0:pw]),
                                reads=[xt.r, ident.r], writes=[pb.r], signal=(tb == nb - 1))
                        eng = "act" if c % 2 else "dve"
                        if eng == "act":
                            fw.op("act", lambda e, c=c, pb=pb, xT=xT, w=w: e.activation(
                                out=xT.t[:, c, 0:w], in_=pb.t[:, 0:w], func=AF.Copy), reads=[pb.r], writes=[xT.rs[c]])
                        else:
                            fw.op("dve", lambda e, c=c, pb=pb, xT=xT, w=w: e.tensor_copy(
                                out=xT.t[:, c, 0:w], in_=pb.t[:, 0:w]), reads=[pb.r], writes=[xT.rs[c]])
                    store_xT(xT, t0, w, q="sp" if it % 2 else "pool")
            rec_pro = fw
            fw = fw_real
            default_pool[0] = (0, 1, 2, 3, 4, 5, 6, 7)
            merge_recs(fw, [rec_ada, rec_pro])
            fw.flush()
        PH0.close()

        for l in range(L):
            fw.rotate()
            if stage >= 2:
              with contextlib.ExitStack() as ph:
                sub = FWScope(fw, ph)
                default_pool[0] = (0, 1, 2, 3, 4, 5)
                WA = 512
                FC0 = RW_COLS
                TKMAX = max(SEQ, PAST + TS)
                NBMAX = (TKMAX + 127) // 128
                wf = Tt(sub.sbuf("wf", [128, 8, FOX_COLS], BF16), name="wf")
                for kc in range(8):
                    fw.dma("pool", wf.t[:, kc, :], I["w_in"][l][kc * 128:(kc + 1) * 128, FC0:FC0 + FOX_COLS],
                           writes=[wf.r], stream="wld", group=True)
                KT = Tt(sub.sbuf("KT", [128, 4, TKMAX], BF16), nreg=NBMAX, name="KT")
                Vx = Tt(sub.sbuf("Vx", [128, NBMAX, 8, 65], BF16), nreg=NBMAX, name="Vx")
                fw.op("pool", lambda e: e.memset(Vx.t[:], 1.0), writes=Vx.rs)
                ctok = Tt(sub.sbuf("ctok", [128, NBMAX, 8], F32), nreg=NBMAX, name="ctok")
                negc = Tt(sub.sbuf("negc", [128, NBMAX, 8], F32), nreg=NBMAX, name="negc")
                xTa = [Tt(sub.sbuf("xTa%d" % i, [128, 8, WA], F32), nreg=8, name="xTa%d" % i) for i in range(2)]
                sq = Tt(sub.sbuf("sqa", [128, 8, WA], BF16), name="sqa")
                hT = Tt(sub.sbuf("hTa", [128, 8, WA], BF16), nreg=8, name="hTa")
                tmp = [Tt(sub.sbuf("tmpa%d" % i, [128, WA], F32), name="tmpa%d" % i) for i in range(2)]
                lnv = Tt(sub.sbuf("lnva", [128, WA], F32), name="lnva")
                rstd = Tt(sub.sbuf("rstda", [128, WA], F32), name="rstda")
                QT = Tt(sub.sbuf("QT", [128, 4, WA], BF16), nreg=4, name="QT")
                ktok = [Tt(sub.sbuf("ktok%d" % i, [128, 512], F32), name="ktok%d" % i) for i in range(2)]
                vtok = [Tt(sub.sbuf("vtok%d" % i, [128, 512], F32), name="vtok%d" % i) for i in range(2)]
                ftok = [Tt(sub.sbuf("ftok%d" % i, [128, 8], F32), name="ftok%d" % i) for i in range(2)]
                ltok = [Tt(sub.sbuf("ltok%d" % i, [128, 8], F32), name="ltok%d" % i) for i in range(2)]
                cfm = Tt(sub.sbuf("cfm", [8, WA], F32), name="cfm")
                cr1 = Tt(sub.sbuf("cr1", [8, WA], F32), name="cr1")
                cr2 = Tt(sub.sbuf("cr2", [8, WA], F32), name="cr2")
                midt = Tt(sub.sbuf("midt", [8, WA], BF16), name="midt")
                cq96 = Tt(sub.sbuf("cq96", [72, WA], BF16), name="cq96")
                fw.op("pool", lambda e: e.memset(cq96.t[:], 0.0), writes=[cq96.r])
                pts = [Tt(sub.sbuf("pt%d" % i, [128, WA], BF16), name="pt%d" % i) for i in range(4)]
                rsf = Tt(sub.sbuf("rsf", [128, WA], F32), name="rsf")
                rcp = Tt(sub.sbuf("rcp", [64, WA], F32), name="rcp")
                yfT = Tt(sub.sbuf("yfT", [128, 4, WA], BF16), nreg=4, name="yfT")
                it = 0
                ik = 0
                ipt = 0
                a1_tiles = [(s_, t0_, w_) for s_ in range(3) for (t0_, w_, j_) in tiles_of(s_, WA)]
                a1_idx = [0]

                def a1_norm(idx):
                    s_, t0_, w_ = a1_tiles[idx]
                    xT_ = xTa[idx % 2]
                    load_xT(xT_, t0_, w_)
                    norm_fm(xT_, w_, sq, tmp, lnv, rstd, hT,
                            lambda c, s_=s_: G1.t[:, l, s_, c:c + 1], lambda c, s_=s_: mod.t[:, l, 0, s_, c:c + 1], hT.rs)
                for s in range(3):
                    off, T = seqs[s]
                    kbase = 0
                    if s == 2:
                        kbase = PAST
                        for cb in range(PAST // 128):
                            kt_ = ktok[ik % 2]
                            vt_ = vtok[ik % 2]
                            ft_ = ltok[ik % 2]
                            ik += 1
                            fw.dma("sp", kt_.t[:], I["ck"][l][cb * 128:(cb + 1) * 128, :], writes=[kt_.r], stream="cldk%d" % (ik % 2))
                            fw.dma("sp", vt_.t[:], I["cv"][l][cb * 128:(cb + 1) * 128, :], writes=[vt_.r], stream="cldv%d" % (ik % 2))
                            fw.dma("sp", ft_.t[:], I["cl"][l][cb * 128:(cb + 1) * 128, :], writes=[ft_.r], stream="cldf%d" % (ik % 2))
                            pb = next_bank()
                            for pc in range(4):
                                fw.op("pe", lambda e, pc=pc, pb=pb, kt_=kt_: e.transpose(
                                    pb.t[:, pc * 128:(pc + 1) * 128], kt_.t[:, pc * 128:(pc + 1) * 128], ident.t[:]),
                                    reads=[kt_.r, ident.r], writes=[pb.r], signal=(pc == 3))
                            fw.op("act", lambda e, pb=pb, cb=cb: e.activation(
                                out=KT.t[:, :, cb * 128:(cb + 1) * 128],
                                in_=pb.t[:, :].rearrange("p (c t) -> p c t", c=4), func=AF.Copy),
                                reads=[pb.r], writes=[KT.rs[cb]])
                            fw.op("dve", lambda e, vt_=vt_, cb=cb: e.tensor_copy(
                                out=Vx.t[:, cb, :, 0:64], in_=vt_.t[:, :].rearrange("p (h d) -> p h d", h=8)),
                                reads=[vt_.r], writes=[Vx.rs[cb]])
                            pc_ = next_bank()
                            fw.op("pe", lambda e, pc_=pc_, ft_=ft_, cb=cb: e.matmul(
                                pc_.t[:, 0:8], lhsT=trif.t[:], rhs=ft_.t[:], start=True, stop=(cb == 0)),
                                reads=[trif.r, ft_.r], writes=[pc_.r], signal=(cb == 0))
                            if cb > 0:
                                fw.op("pe", lambda e, pc_=pc_, cb=cb: e.matmul(
                                    pc_.t[:, 0:8], lhsT=self127.t[:], rhs=ctok.t[:, cb - 1, :], start=False, stop=True),
                                    reads=[self127.r, ctok.rs[cb - 1]], writes=[pc_.r])
                            fw.op("dve", lambda e, pc_=pc_, cb=cb: e.tensor_copy(out=ctok.t[:, cb, :], in_=pc_.t[:, 0:8]),
                                  reads=[pc_.r], writes=[ctok.rs[cb]])
                            fw.op("act", lambda e, pc_=pc_, cb=cb: e.activation(
                                out=negc.t[:, cb, :], in_=pc_.t[:, 0:8], func=AF.Copy, scale=-1.0),
                                reads=[pc_.r], writes=[negc.rs[cb]])
                    dK = O["fkp"][l][s] if s < 2 else O["fks"][l]
                    dV = O["fvp"][l][s] if s < 2 else O["fvs"][l]
                    dF = O["flp"][l][s] if s < 2 else O["fls"][l]
                    def a1_tile(s, t0, w, j, xT, off, kbase, dK, dV, dF, l=l):
                        nonlocal ik, ipt
                        lt0 = t0 - off
                        kt0 = kbase + lt0
                        nb = (w + 127) // 128
                        pw = min(w, 128)
                        if a1_idx[0] == 0:
                            a1_norm(0)
                        for pc in range(4):
                            pq = next_bank()
                            for kc in range(8):
                                fw.op("pe", lambda e, kc=kc, pc=pc, pq=pq, w=w: e.matmul(
                                    pq.t[:, 0:w], lhsT=wf.t[:, kc, pc * 128:(pc + 1) * 128], rhs=hT.t[:, kc, 0:w],
                                    start=(kc == 0), stop=(kc == 7)),
                                    reads=[wf.r, hT.rs[kc]], writes=[pq.r], signal=(kc == 7))
                            fw.op("act", lambda e, pc=pc, pq=pq, w=w: e.activation(
                                out=QT.t[:, pc, 0:w], in_=pq.t[:, 0:w], func=AF.Copy, scale=0.125),
                                reads=[pq.r], writes=[QT.rs[pc]])
                            pk = next_bank()
                            for kc in range(8):
                                fw.op("pe", lambda e, kc=kc, pc=pc, pk=pk, w=w: e.matmul(
                                    pk.t[:, 0:w], lhsT=wf.t[:, kc, 512 + pc * 128:512 + (pc + 1) * 128],
                                    rhs=hT.t[:, kc, 0:w], start=(kc == 0), stop=(kc == 7)),
                                    reads=[wf.r, hT.rs[kc]], writes=[pk.r], signal=(kc == 7))
                            kregs = [KT.rs[(kt0 + tb * 128) // 128] for tb in range(nb)]
                            fw.op("dve", lambda e, pc=pc, pk=pk, w=w, kt0=kt0: e.tensor_copy(
                                out=KT.t[:, pc, kt0:kt0 + w], in_=pk.t[:, 0:w]), reads=[pk.r], writes=kregs)
                        for tb in range(nb):
                            kb = (kt0 + tb * 128) // 128
                            kt_ = ktok[ik % 2]
                            vt_ = vtok[ik % 2]
                            ft_ = ftok[ik % 2]
                            lt_ = ltok[ik % 2]
                            ik += 1
                            for (dst_t, c0, ncol, dd, eng) in ((kt_, 512, 512, dK, "act"), (vt_, 1024, 512, dV, "dve")):
                                pb = next_bank()
                                for kc in range(8):
                                    fw.op("pe", lambda e, kc=kc, pb=pb, tb=tb, c0=c0, ncol=ncol, pw=pw: e.matmul(
                                        pb.t[0:pw, 0:ncol], lhsT=hT.t[:, kc, tb * 128:tb * 128 + pw],
                                        rhs=wf.t[:, kc, c0:c0 + ncol], start=(kc == 0), stop=(kc == 7)),
                                        reads=[wf.r, hT.rs[kc]], writes=[pb.r], signal=(kc == 7))
                                if eng == "act":
                                    fw.op("act", lambda e, pb=pb, dst_t=dst_t, pw=pw: e.activation(
                                        out=dst_t.t[0:pw, :], in_=pb.t[0:pw, :], func=AF.Copy),
                                        reads=[pb.r], writes=[dst_t.r])
                                else:
                                    fw.op("dve", lambda e, pb=pb, dst_t=dst_t, pw=pw: e.tensor_copy(
                                        out=dst_t.t[0:pw, :], in_=pb.t[0:pw, :]), reads=[pb.r], writes=[dst_t.r])
                                r0 = lt0 + tb * 128
                                fw.dma("sp", dd[r0:r0 + pw, :], dst_t.t[0:pw, :], reads=[dst_t.r],
                                       stream="kvo%s%d" % (eng[0], ik % 2))
                            fw.op("pool", lambda e, vt_=vt_, kb=kb, pw=pw: e.tensor_copy(
                                out=Vx.t[0:pw, kb, :, 0:64], in_=vt_.t[0:pw, :].rearrange("p (h d) -> p h d", h=8)),
                                reads=[vt_.r], writes=[Vx.rs[kb]])
                            pf = next_bank()
                            for kc in range(8):
                                fw.op("pe", lambda e, kc=kc, pf=pf, tb=tb, pw=pw: e.matmul(
                                    pf.t[0:pw, 0:8], lhsT=hT.t[:, kc, tb * 128:tb * 128 + pw],
                                    rhs=wf.t[:, kc, 1536:1544], start=(kc == 0), stop=(kc == 7)),
                                    reads=[wf.r, hT.rs[kc]], writes=[pf.r], signal=(kc == 7))
                            fw.op("dve", lambda e, pf=pf, ft_=ft_, pw=pw: e.tensor_tensor(
                                out=ft_.t[0:pw, :], in0=pf.t[0:pw, 0:8], in1=bfb.t[0:pw, l, :], op=ALU.add),
                                reads=[pf.r, bfb.r], writes=[ft_.r])
                            fw.op("act", lambda e, ft_=ft_, pw=pw: e.activation(
                                out=ft_.t[0:pw, :], in_=ft_.t[0:pw, :], func=AF.Exp, scale=-1.0),
                                reads=[ft_.r], writes=[ft_.r])
                            fw.op("act", lambda e, ft_=ft_, pw=pw: e.activation(
                                out=ft_.t[0:pw, :], in_=ft_.t[0:pw, :], func=AF.Ln, bias=1.0),
                                reads=[ft_.r], writes=[ft_.r])
                            fw.op("dve", lambda e, ft_=ft_, lt_=lt_, pw=pw: e.tensor_scalar_mul(
                                out=lt_.t[0:pw, :], in0=ft_.t[0:pw, :], scalar1=-1.0), reads=[ft_.r], writes=[lt_.r])
                            r0 = lt0 + tb * 128
                            fw.dma("sp", dF[r0:r0 + pw, :], lt_.t[0:pw, :], reads=[lt_.r], stream="kvof%d" % (ik % 2))
                            pc_ = next_bank()
                            first = (kb == 0)
                            fw.op("pe", lambda e, pc_=pc_, lt_=lt_, pw=pw, first=first: e.matmul(
                                pc_.t[0:pw, 0:8], lhsT=trif.t[0:pw, 0:pw], rhs=lt_.t[0:pw, :], start=True, stop=first),
                                reads=[trif.r, lt_.r], writes=[pc_.r], signal=first)
                            if not first:
                                fw.op("pe", lambda e, pc_=pc_, kb=kb, pw=pw: e.matmul(
                                    pc_.t[0:pw, 0:8], lhsT=self127.t[:, 0:pw], rhs=ctok.t[:, kb - 1, :],
                                    start=False, stop=True),
                                    reads=[self127.r, ctok.rs[kb - 1]], writes=[pc_.r])
                            fw.op("dve", lambda e, pc_=pc_, kb=kb, pw=pw: e.tensor_copy(
                                out=ctok.t[0:pw, kb, :], in_=pc_.t[0:pw, 0:8]), reads=[pc_.r], writes=[ctok.rs[kb]])
                            fw.op("act", lambda e, pc_=pc_, kb=kb, pw=pw: e.activation(
                                out=negc.t[0:pw, kb, :], in_=pc_.t[0:pw, 0:8], func=AF.Copy, scale=-1.0),
                                reads=[pc_.r], writes=[negc.rs[kb]])
                            pt_ = next_bank()
                            fw.op("pe", lambda e, pt_=pt_, kb=kb, pw=pw: e.transpose(
                                pt_.t[0:8, 0:pw], ctok.t[0:pw, kb, :], ident.t[0:pw, 0:pw]),
                                reads=[ctok.rs[kb], ident.r], writes=[pt_.r])
                            fw.op("dve", lambda e, pt_=pt_, tb=tb, pw=pw: e.tensor_copy(
                                out=cfm.t[:, tb * 128:tb * 128 + pw], in_=pt_.t[0:8, 0:pw]),
                                reads=[pt_.r], writes=[cfm.r])
                        fw.op("act", lambda e, w=w: e.activation(out=cq96.t[0:8, 0:w], in_=cfm.t[:, 0:w], func=AF.Copy),
                              reads=[cfm.r], writes=[cq96.r])
                        fw.op("dve", lambda e, w=w: e.tensor_tensor(out=cr1.t[:, 0:w], in0=cfm.t[:, 0:w],
                                                                    in1=cq96.t[0:8, 0:w], op=ALU.subtract),
                              reads=[cfm.r, cq96.r], writes=[cr1.r])
                        fw.op("act", lambda e, w=w: e.activation(out=midt.t[:, 0:w], in_=cr1.t[:, 0:w], func=AF.Copy),
                              reads=[cr1.r], writes=[midt.r])
                        fw.op("pool", lambda e, w=w: e.tensor_copy(out=cq96.t[32:40, 0:w], in_=midt.t[:, 0:w]),
                              reads=[midt.r], writes=[cq96.r])
                        fw.op("dve", lambda e, w=w: e.tensor_tensor(out=cr2.t[:, 0:w], in0=cr1.t[:, 0:w],
                                                                    in1=midt.t[:, 0:w], op=ALU.subtract),
                              reads=[cr1.r, midt.r], writes=[cr2.r])
                        fw.op("act", lambda e, w=w: e.activation(out=cq96.t[64:72, 0:w], in_=cr2.t[:, 0:w], func=AF.Copy),
                              reads=[cr2.r], writes=[cq96.r])
                        a1_idx[0] += 1
                        if a1_idx[0] < len(a1_tiles):
                            a1_norm(a1_idx[0])
                        kb_first_tile = kt0 // 128
                        nkb = kb_first_tile + nb
                        pending_epi = []
                        for h in range(8):
                            hr = slice((h % 2) * 64, (h % 2) * 64 + 64)
                            hp = h // 2
                            ob = banks[6 + (h % 2)]
                            blocks = []
                            for kb in range(nkb):
                                if kb < kb_first_tile:
                                    q0, rows, diag = 0, 128, False
                                else:
                                    q0, rows, diag = (kb - kb_first_tile) * 128, pw, True
                                blocks.append((kb, q0, rows, diag))
                            sbanks = {}
                            ptl = {}

                            def emit_s(bi, h=h, hr=hr, hp=hp):
                                kb, q0, rows, diag = blocks[bi]
                                sb = next_bank(pool=(0, 1, 2, 3))
                                sbanks[bi] = sb
                                fw.op("pe", lambda e, sb=sb, kb=kb, q0=q0, rows=rows: e.matmul(
                                    sb.t[0:rows, q0:w], lhsT=KT.t[hr, hp, kb * 128:kb * 128 + rows],
                                    rhs=QT.t[hr, hp, q0:w], start=True, stop=False),
                                    reads=[KT.rs[kb], QT.rs[hp]], writes=[sb.r], signal=False)
                                fw.op("pe", lambda e, sb=sb, q0=q0, rows=rows: e.matmul(
                                    sb.t[0:rows, q0:w], lhsT=selh.t[0:72, h, 0:rows], rhs=cq96.t[0:72, q0:w],
                                    start=False, stop=(not diag)),
                                    reads=[selh.r, cq96.r], writes=[sb.r], signal=(not diag))
                                if diag:
                                    fw.op("pe", lambda e, sb=sb, q0=q0, rows=rows: e.matmul(
                                        sb.t[0:rows, q0:q0 + rows], lhsT=identb.t[0:rows, 0:rows],
                                        rhs=maskneg.t[0:rows, 0:rows], start=False, stop=True),
                                        reads=[identb.r, maskneg.r], writes=[sb.r])

                            def emit_pv(bi, h=h, ob=ob):
                                nonlocal ipt
                                kb, q0, rows, diag = blocks[bi]
                                sb = sbanks.pop(bi)
                                pt = pts[ipt % 4]
                                ipt += 1
                                fw.op("act", lambda e, sb=sb, pt=pt, kb=kb, q0=q0, rows=rows: e.activation(
                                    out=pt.t[0:rows, q0:w], in_=sb.t[0:rows, q0:w], func=AF.Exp,
                                    bias=negc.t[0:rows, kb, h:h + 1]),
                                    reads=[sb.r, negc.rs[kb]], writes=[pt.r])
                                last = (bi == len(blocks) - 1)
                                fw.op("pe", lambda e, pt=pt, kb=kb, q0=q0, rows=rows, bi=bi, last=last: e.matmul(
                                    ob.t[0:65, q0:w], lhsT=Vx.t[0:rows, kb, h, :], rhs=pt.t[0:rows, q0:w],
                                    start=(bi == 0), stop=last),
                                    reads=[Vx.rs[kb], pt.r], writes=[ob.r], signal=last)
                            LOOK = 3
                            nbk = len(blocks)
                            for bi in range(min(LOOK, nbk)):
                                emit_s(bi)
                            for bi in range(nbk):
                                emit_pv(bi)
                                if bi + LOOK < nbk:
                                    emit_s(bi + LOOK)
                            def epilogue(ob=ob, hr=hr, hp=hp):
                                fw.op("act", lambda e, ob=ob, w=w: e.activation(
                                    out=rsf.t[64:65, 0:w], in_=ob.t[64:65, 0:w], func=AF.Copy), reads=[ob.r], writes=[rsf.r])
                                pr = next_bank(pool=(4, 5))
                                fw.op("pe", lambda e, pr=pr, w=w: e.matmul(
                                    pr.t[0:64, 0:w], lhsT=onesf.t[64:65, 0:64], rhs=rsf.t[64:65, 0:w], start=True, stop=True),
                                    reads=[onesf.r, rsf.r], writes=[pr.r])
                                fw.op("dve", lambda e, pr=pr, w=w: e.reciprocal(out=rcp.t[:, 0:w], in_=pr.t[0:64, 0:w]),
                                      reads=[pr.r], writes=[rcp.r])
                                fw.op("dve", lambda e, ob=ob, hr=hr, hp=hp, w=w: e.tensor_tensor(
                                    out=yfT.t[hr, hp, 0:w], in0=ob.t[0:64, 0:w], in1=rcp.t[:, 0:w], op=ALU.mult),
                                    reads=[ob.r, rcp.r], writes=[yfT.rs[hp]])
                            if pending_epi:
                                pending_epi.pop()()
                            pending_epi.append(epilogue)
                        while pending_epi:
                            pending_epi.pop()()
                        fw.dma("sp", yfox.rearrange("(c p) t -> p c t", p=128)[:, :, t0:t0 + w], yfT.t[:, :, 0:w],
                               reads=yfT.rs, writes=[yfreg(t0)], stream="yfst")
                    for (t0, w, j) in tiles_of(s, WA):
                        xT = xTa[it % 2]
                        it += 1
                        a1_tile(s, t0, w, j, xT, off, kbase, dK, dV, dF)
                fw.flush()
                default_pool[0] = (0, 1, 2, 3, 4, 5, 6, 7)
            if stage >= 2:
              with contextlib.ExitStack() as ph:
                sub = FWScope(fw, ph)
                default_pool[0] = (0, 1, 2, 3, 4)
                sink = [fw]
                WA = 256
                RW0, HG0 = 0, RW_COLS + FOX_COLS
                NWI = RW_COLS + HG_COLS
                wi = Tt(sub.sbuf("wi", [128, 8, NWI], BF16), name="wi")
                wo = Tt(sub.sbuf("wo", [128, 8, D], BF16), name="wo")
                for kc in range(8):
                    fw.dma("pool", wi.t[:, kc, 0:RW_COLS], I["w_in"][l][kc * 128:(kc + 1) * 128, 0:RW_COLS],
                           writes=[wi.r], stream="wld", group=True)
                    fw.dma("pool", wi.t[:, kc, RW_COLS:NWI], I["w_in"][l][kc * 128:(kc + 1) * 128, HG0:HG0 + HG_COLS],
                           writes=[wi.r], stream="wld", group=True)
                    fw.dma("pool", wo.t[:, kc, :], I["w_out"][l][kc * 128:(kc + 1) * 128, :], writes=[wo.r], stream="wld", group=True)
                xTa = [Tt(sub.sbuf("xTa%d" % i, [128, 8, WA], F32), nreg=8, name="xTa%d" % i) for i in range(2)]
                sq = Tt(sub.sbuf("sqa", [128, 8, WA], BF16), name="sqa")
                hT = Tt(sub.sbuf("hTa", [128, 8, WA], BF16), nreg=8, name="hTa")
                tmp = [Tt(sub.sbuf("tmpa%d" % i, [128, WA], F32), name="tmpa%d" % i) for i in range(2)]
                lnv = Tt(sub.sbuf("lnva", [128, WA], F32), name="lnva")
                rstd = Tt(sub.sbuf("rstda", [128, WA], F32), name="rstda")
                ymix = Tt(sub.sbuf("ymix", [128, 8, WA], BF16), nreg=8, name="ymix")
                fw.op("pool", lambda e: e.memset(ymix.t[:], 0.0), writes=ymix.rs)

                def S(name, shape=None, dt=F32, nreg=1):
                    return Tt(sub.sbuf(name, shape or [128, WA], dt), nreg=nreg, name=name)

                def proj_fm(c0, w, M=128):
                    pb = next_bank()
                    for kc in range(8):
                        sink[0].op("pe", lambda e, kc=kc: e.matmul(
                            pb.t[0:M, 0:w], lhsT=wi.t[:, kc, c0:c0 + M], rhs=hT.t[:, kc, 0:w],
                            start=(kc == 0), stop=(kc == 7)),
                            reads=[wi.r, hT.rs[kc]], writes=[pb.r], signal=(kc == 7))
                    return pb

                NCH = WA // 32
                hg_E = [S("hg_E%d" % i) for i in range(2)]
                hg_KK = [S("hg_KK%d" % i) for i in range(2)]
                hg_B = [S("hg_B%d" % i) for i in range(2)]
                hg_D = S("hg_D")
                hg_X = S("hg_X")
                hg_Q = S("hg_Q")
                hg_G = [S("hg_G%d" % i) for i in range(2)]
                hg_Qt = S("hg_Qt", [128, 2, WA], BF16, nreg=2)
                hg_Kh = S("hg_Kh", [128, 2, WA], BF16, nreg=2)
                hg_Ke = S("hg_Ke", [128, 2, WA], BF16, nreg=2)
                hg_ebl = S("hg_ebl", [128, 2, NCH], F32, nreg=2)
                hg_ebm = S("hg_ebm", [128, 2, NCH], F32, nreg=2)
                hg_Vh = S("hg_Vh", [128, WA // 128, 256], BF16, nreg=4)
                hg_KeT = S("hg_KeT", [128, WA // 128, 4, 256], BF16, nreg=4)
                hg_AT = S("hg_AT", [128, WA // 128, 2, 2, 128], BF16, nreg=4)
                hg_Sm = S("hg_Sm", [128, 2, 128], F32, nreg=2)
                hg_Sbd = S("hg_Sbd", [128, 2, 128], BF16, nreg=2)
                hg_sq = S("hg_sq", [128, WA], BF16)
                hg_t1 = S("hg_t1")

                def hg_init(s):
                    fw.op("pool", lambda e: e.memset(hg_Sm.t[:], 0.0), writes=hg_Sm.rs)
                    if s == 2:
                        for h in range(4):
                            hr = slice((h % 2) * 64, (h % 2) * 64 + 64)
                            fw.dma("sp", hg_Sm.t[hr, h // 2, (h % 2) * 64:(h % 2) * 64 + 64], I["shg"][l][h],
                                   writes=[hg_Sm.rs[h // 2]], stream="stld", group=True)

                def hg_final(s):
                    dst = O["hgp"][l][s] if s < 2 else O["hgs"][l]
                    for h in range(4):
                        hr = slice((h % 2) * 64, (h % 2) * 64 + 64)
                        fw.dma("sp", dst[h], hg_Sm.t[hr, h // 2, (h % 2) * 64:(h % 2) * 64 + 64],
                               reads=[hg_Sm.rs[h // 2]], stream="ststhg%d" % s, group=True)

                def hg_tile(s, w):
                    nch = w // 32
                    nb = (w + 127) // 128
                    pw = min(w, 128)
                    c_q, c_f, c_i, c_g = RW_COLS, RW_COLS + 256, RW_COLS + 512, RW_COLS + 768
                    for tb in range(nb):
                        pb = next_bank()
                        for kc in range(8):
                            sink[0].op("pe", lambda e, kc=kc, tb=tb, pb=pb: e.matmul(
                                pb.t[0:pw, 0:256], lhsT=hT.t[:, kc, tb * 128:tb * 128 + pw], rhs=wi.t[:, kc, c_i:c_i + 256],
                                start=(kc == 0), stop=(kc == 7)),
                                reads=[wi.r, hT.rs[kc]], writes=[pb.r], signal=(kc == 7))
                        sink[0].op("act", lambda e, tb=tb, pb=pb: e.activation(
                            out=hg_Vh.t[0:pw, tb, :], in_=pb.t[0:pw, 0:256], func=AF.Copy),
                            reads=[pb.r], writes=[hg_Vh.rs[tb]])
                    for pc in range(2):
                        E, KK, B, G = hg_E[pc], hg_KK[pc], hg_B[pc], hg_G[pc]
                        lb_ap = lbT.t[:, l, pc:pc + 1]
                        oml_ap = omlT.t[:, l, pc:pc + 1]
                        noml_ap = nomlT.t[:, l, pc:pc + 1]
                        pf = proj_fm(c_f + pc * 128, w)
                        sink[0].op("act", lambda e, pf=pf, E=E: e.activation(out=E.t[:, 0:w], in_=pf.t[:, 0:w], func=AF.Exp,
                                                                      scale=-1.0), reads=[pf.r], writes=[E.r])
                        sink[0].op("dve", lambda e, E=E: e.tensor_scalar_add(out=E.t[:, 0:w], in0=E.t[:, 0:w], scalar1=1.0),
                              reads=[E.r], writes=[E.r])
                        sink[0].op("dve", lambda e, E=E: e.reciprocal(out=E.t[:, 0:w], in_=E.t[:, 0:w]),
                              reads=[E.r], writes=[E.r])
                        sink[0].op("dve", lambda e, E=E, KK=KK, noml_ap=noml_ap, oml_ap=oml_ap: e.tensor_scalar(
                            out=KK.t[:, 0:w], in0=E.t[:, 0:w], scalar1=noml_ap, scalar2=oml_ap, op0=ALU.mult, op1=ALU.add),
                            reads=[E.r, nomlT.r, omlT.r], writes=[KK.r])
                        sink[0].op("act", lambda e, E=E, oml_ap=oml_ap, lb_ap=lb_ap: e.activation(
                            out=E.t[:, 0:w], in_=E.t[:, 0:w], func=AF.Ln, scale=oml_ap, bias=lb_ap),
                            reads=[E.r, omlT.r, lbT.r], writes=[E.r])
                        sink[0].op("dve", lambda e, E=E, B=B: e.tensor_tensor_scan(
                            out=B.t[:, 0:w], data0=rmask32.t[:, 0:w], data1=E.t[:, 0:w], initial=0.0,
                            op0=ALU.mult, op1=ALU.add), reads=[E.r, rmask32.r], writes=[B.r])
                        Bv = B.t[:, 0:w].rearrange("p (c t) -> p c t", t=32)
                        Dv = hg_D.t[:, 0:w].rearrange("p (c t) -> p c t", t=32)
                        sink[0].op("dve", lambda e, Bv=Bv, Dv=Dv: e.tensor_tensor(
                            out=Dv, in0=Bv, in1=Bv[:, :, 15:16].to_broadcast([128, nch, 32]), op=ALU.subtract),
                            reads=[B.r], writes=[hg_D.r])
                        pq = proj_fm(c_q + pc * 128, w)
                        sink[0].op("act", lambda e, pq=pq: e.activation(out=hg_Q.t[:, 0:w], in_=pq.t[:, 0:w], func=AF.Copy),
                              reads=[pq.r], writes=[hg_Q.r])
                        sink[0].op("act", lambda e: e.activation(out=hg_X.t[:, 0:w], in_=hg_D.t[:, 0:w], func=AF.Exp),
                              reads=[hg_D.r], writes=[hg_X.r])
                        sink[0].op("dve", lambda e, pc=pc: e.tensor_tensor(out=hg_Qt.t[:, pc, 0:w], in0=hg_Q.t[:, 0:w],
                                                                      in1=hg_X.t[:, 0:w], op=ALU.mult),
                              reads=[hg_Q.r, hg_X.r], writes=[hg_Qt.rs[pc]])
                        sink[0].op("act", lambda e: e.activation(out=hg_X.t[:, 0:w], in_=hg_D.t[:, 0:w], func=AF.Exp, scale=-1.0),
                              reads=[hg_D.r], writes=[hg_X.r])
                        sink[0].op("dve", lambda e, pc=pc, KK=KK: e.tensor_tensor(out=hg_Kh.t[:, pc, 0:w], in0=KK.t[:, 0:w],
                                                                             in1=hg_X.t[:, 0:w], op=ALU.mult),
                              reads=[KK.r, hg_X.r], writes=[hg_Kh.rs[pc]])
                        sink[0].op("dve", lambda e, Bv=Bv, Dv=Dv: e.tensor_tensor(
                            out=Dv, in0=Bv[:, :, 31:32].to_broadcast([128, nch, 32]), in1=Bv, op=ALU.subtract),
                            reads=[B.r], writes=[hg_D.r])
                        sink[0].op("act", lambda e: e.activation(out=hg_X.t[:, 0:w], in_=hg_D.t[:, 0:w], func=AF.Exp),
                              reads=[hg_D.r], writes=[hg_X.r])
                        sink[0].op("dve", lambda e, pc=pc, KK=KK: e.tensor_tensor(out=hg_Ke.t[:, pc, 0:w], in0=KK.t[:, 0:w],
                                                                             in1=hg_X.t[:, 0:w], op=ALU.mult),
                              reads=[KK.r, hg_X.r], writes=[hg_Ke.rs[pc]])
                        sink[0].op("act", lambda e, pc=pc, Bv=Bv: e.activation(out=hg_ebl.t[:, pc, 0:nch], in_=Bv[:, :, 31],
                                                                          func=AF.Exp), reads=[B.r], writes=[hg_ebl.rs[pc]])
                        sink[0].op("act", lambda e, pc=pc, Bv=Bv: e.activation(out=hg_ebm.t[:, pc, 0:nch], in_=Bv[:, :, 15],
                                                                          func=AF.Exp), reads=[B.r], writes=[hg_ebm.rs[pc]])
                        pg = proj_fm(c_g + pc * 128, w)
                        sink[0].op("act", lambda e, pg=pg, G=G: e.activation(out=G.t[:, 0:w], in_=pg.t[:, 0:w], func=AF.Silu),
                              reads=[pg.r], writes=[G.r])
                    HGDBG = int(os.environ.get("HGDBG", "9"))
                    if HGDBG < 2:
                        return
                    for tb in range(nb):
                        pAs = [next_bank(), next_bank()]
                        for h in range(4):
                            hr = slice((h % 2) * 64, (h % 2) * 64 + 64)
                            pA = pAs[h % 2]
                            sink[0].op("pe", lambda e, h=h, hr=hr, tb=tb, pA=pA: e.matmul(
                                pA.t[0:pw, (h // 2) * 128:(h // 2) * 128 + pw], lhsT=hg_Kh.t[hr, h // 2, tb * 128:tb * 128 + pw],
                                rhs=hg_Qt.t[hr, h // 2, tb * 128:tb * 128 + pw], start=True, stop=True),
                                reads=[hg_Kh.rs[h // 2], hg_Qt.rs[h // 2]], writes=[pA.r], signal=(h >= 2))
                        for par in range(2):
                            pA = pAs[par]
                            sink[0].op("dve", lambda e, tb=tb, pA=pA, par=par: e.tensor_tensor(
                                out=hg_AT.t[0:pw, tb, par, :, 0:pw],
                                in0=pA.t[0:pw, 0:256].rearrange("p (h t) -> p h t", h=2)[:, :, 0:pw],
                                in1=maskbd.t[0:pw, 0:pw].unsqueeze(1).to_broadcast([pw, 2, pw]), op=ALU.mult),
                                reads=[pA.r, maskbd.r], writes=[hg_AT.rs[tb]])
                        if os.environ.get("HGSUB", "") == "A":
                            continue
                        pT = next_bank()
                        pTb = pT.t[:, :].bitcast(BF16)
                        for pc in range(2):
                            sink[0].op("pe", lambda e, pc=pc, tb=tb, pTb=pTb: e.transpose(
                                pTb[0:pw, pc * 128:(pc + 1) * 128], hg_Ke.t[:, pc, tb * 128:tb * 128 + pw], identb.t[:, :]),
                                reads=[hg_Ke.rs[pc], identb.r], writes=[pT.r], signal=(pc == 1))
                        for cc in range(min(4, nch - tb * 4)):
                            sink[0].op("act", lambda e, tb=tb, pTb=pTb, cc=cc: e.activation(
                                out=hg_KeT.t[0:pw, tb, cc, :], in_=pTb[0:pw, 0:256], func=AF.Identity,
                                scale=maskbd.t[0:pw, cc * 32 + 31:cc * 32 + 32]),
                                reads=[pT.r, maskbd.r], writes=[hg_KeT.rs[tb]])
                    if HGDBG < 3:
                        return
                    po = [banks[6], banks[7]]
                    for tb in range(nb):
                        for h in range(4):
                            sink[0].op("pe", lambda e, h=h, tb=tb: e.matmul(
                                po[h // 2].t[(h % 2) * 64:(h % 2) * 64 + 64, tb * 128:tb * 128 + pw],
                                lhsT=hg_Vh.t[0:pw, tb, h * 64:(h + 1) * 64], rhs=hg_AT.t[0:pw, tb, h % 2, h // 2, 0:pw],
                                start=True, stop=False, skip_group_check=True),
                                reads=[hg_Vh.rs[tb], hg_AT.rs[tb]], writes=[po[h // 2].r], signal=False)
                        for cc in range(min(4, nch - tb * 4)):
                            c = tb * 4 + cc
                            last = (c == nch - 1)
                            for pc in range(2):
                                sink[0].op("act", lambda e, pc=pc, c=c: e.activation(
                                    out=hg_Sbd.t[:, pc, :], in_=hg_Sm.t[:, pc, :], func=AF.Identity,
                                    scale=hg_ebm.t[:, pc, c:c + 1]),
                                    reads=[hg_Sm.rs[pc], hg_ebm.rs[pc]], writes=[hg_Sbd.rs[pc]])
                                sink[0].op("pe", lambda e, pc=pc, c=c: e.matmul(
                                    po[pc].t[:, c * 32:(c + 1) * 32], lhsT=hg_Sbd.t[:, pc, :],
                                    rhs=hg_Qt.t[:, pc, c * 32:(c + 1) * 32], start=False, stop=True,
                                    skip_group_check=True),
                                    reads=[hg_Sbd.rs[pc], hg_Qt.rs[pc]], writes=[po[pc].r], signal=last)
                                pS = next_bank()
                                sink[0].op("pe", lambda e, pc=pc, tb=tb, cc=cc, pS=pS: e.matmul(
                                    pS.t[:, 0:128], lhsT=hg_KeT.t[0:pw, tb, cc, pc * 128:(pc + 1) * 128],
                                    rhs=hg_Vh.t[0:pw, tb, pc * 128:(pc + 1) * 128], start=True, stop=True),
                                    reads=[hg_KeT.rs[tb], hg_Vh.rs[tb]], writes=[pS.r])
                                for hh in range(2):
                                    hr = slice(hh * 64, hh * 64 + 64)
                                    sink[0].op("dve", lambda e, pc=pc, c=c, hr=hr, pS=pS: e.scalar_tensor_tensor(
                                        out=hg_Sm.t[hr, pc, hr], in0=hg_Sm.t[hr, pc, hr], scalar=hg_ebl.t[hr, pc, c:c + 1],
                                        in1=pS.t[hr, hr], op0=ALU.mult, op1=ALU.add),
                                        reads=[hg_Sm.rs[pc], hg_ebl.rs[pc], pS.r], writes=[hg_Sm.rs[pc]])
                    if HGDBG < 4:
                        return
                    for pc in range(2):
                        G = hg_G[pc]
                        sink[0].op("act", lambda e, pc=pc: e.activation(out=hg_sq.t[:, 0:w], in_=po[pc].t[:, 0:w], func=AF.Square),
                              reads=[po[pc].r], writes=[hg_sq.r])
                        pn = next_bank()
                        sink[0].op("pe", lambda e, pn=pn: e.matmul(pn.t[:, 0:w], lhsT=onesbd.t[:], rhs=hg_sq.t[:, 0:w],
                                                              start=True, stop=True),
                              reads=[onesbd.r, hg_sq.r], writes=[pn.r])
                        sink[0].op("act", lambda e, pn=pn: e.activation(out=hg_X.t[:, 0:w], in_=pn.t[:, 0:w], func=AF.Ln,
                                                                   scale=1.0 / 64, bias=epsb.t[:]),
                              reads=[pn.r, epsb.r], writes=[hg_X.r])
                        sink[0].op("act", lambda e: e.activation(out=hg_X.t[:, 0:w], in_=hg_X.t[:, 0:w], func=AF.Exp, scale=-0.5),
                              reads=[hg_X.r], writes=[hg_X.r])
                        sink[0].op("dve", lambda e, pc=pc: e.tensor_tensor(out=hg_t1.t[:, 0:w], in0=po[pc].t[:, 0:w],
                                                                      in1=hg_X.t[:, 0:w], op=ALU.mult),
                              reads=[po[pc].r, hg_X.r], writes=[hg_t1.r])
                        sink[0].op("dve", lambda e, pc=pc, G=G: e.scalar_tensor_tensor(
                            out=ymix.t[:, 6 + pc, 0:w], in0=hg_t1.t[:, 0:w], scalar=hgng.t[:, l, pc:pc + 1], in1=G.t[:, 0:w],
                            op0=ALU.mult, op1=ALU.mult), reads=[hg_t1.r, hgng.r, G.r], writes=[ymix.rs[6 + pc]])

                C0 = 0.6065306597126334
                rw_Pb = S("rw_Pb", [128, 9, WA + 1], F32)
                rw_car = S("rw_car", [128, 9], F32)
                rw_c7 = S("rw_c7", [128, 7], F32)
                rw_tmp = [S("rw_tmp%d" % i) for i in range(6)]
                rw_tmpB = [S("rw_tmpB%d" % i) for i in range(3)]
                rw_SIG2 = [S("rw_SIG%d" % i) for i in range(2)]
                rw_A2 = [S("rw_A%d" % i) for i in range(2)]
                rw_L2 = [S("rw_L%d" % i) for i in range(2)]
                rw_KP2 = [S("rw_KP%d" % i) for i in range(2)]
                rw_KN2 = [S("rw_KN%d" % i) for i in range(2)]
                rw_Bf2 = [S("rw_Bf%d" % i) for i in range(2)]
                rw_sqb2 = [S("rw_sqbb%d" % i, [128, WA], BF16) for i in range(2)]
                rw_G = [S("rw_G%d" % i) for i in range(2)]
                rw_Yf = S("rw_Yf")
                rw_TW = S("rw_TW", [32, WA], BF16)
                rw_AL = S("rw_AL", [32, WA], BF16)
                rw_SG = S("rw_SG", [64, WA], BF16)
                rw_sqb = S("rw_sqb", [128, WA], BF16)
                rw_Kh = S("rw_Kh", [128, 2, WA], BF16, nreg=2)
                rw_Bm = S("rw_Bm", [128, 2, 2, WA], BF16, nreg=2)
                rw_QRm = S("rw_QRm", [128, 2, 2, max(1, WA // 64), 2, 64], BF16, nreg=2)
                rw_Ke = S("rw_Ke", [128, 2, WA], BF16, nreg=2)
                rw_Be = S("rw_Be", [128, 2, WA], BF16, nreg=2)
                rw_Vb = S("rw_Vb", [128, 2, WA], BF16, nreg=2)
                NCR = max(1, WA // 64)
                rw_Vt = S("rw_Vt", [64, NCR, 256], BF16)
                rw_KeT = S("rw_KeT", [64, NCR, 256], BF16)
                rw_BeT = S("rw_BeT", [64, NCR, 256], BF16)
                rw_gC = S("rw_gC", [128, 2, NCR], F32, nreg=2)
                NCK = max(1, WA // 64)
                rw_AT12 = [S("rw_AT12_%d" % i, [64, 4, 2, 64], BF16) for i in range(NCK)]
                rw_AT34 = [S("rw_AT34_%d" % i, [64, 4, 2, 64], BF16) for i in range(NCK)]
                rw_X = [[S("rw_X%d_%d" % (i, j), [64, 4, 64], BF16) for j in range(2)] for i in range(NCK)]
                rw_XT = [[S("rw_XT%d_%d" % (i, j), [64, 4, 64], BF16) for j in range(2)] for i in range(NCK)]
                rw_TT = [[S("rw_TT%d_%d" % (i, j), [64, 4, 64], BF16) for j in range(2)] for i in range(NCK)]
                rw_Zb = S("rw_Zb", [64, 256], BF16)
                rw_Un = S("rw_Un", [64, 256], BF16)
                rw_Hm = S("rw_Hm", [128, 2, 128], F32, nreg=2)
                rw_Hbd = S("rw_Hbd", [128, 2, 128], BF16, nreg=2)
                rw_st = S("rw_st", [128, 2, 128], F32)

                def rw_init(s):
                    fw.op("pool", lambda e: e.memset(rw_Hm.t[:], 0.0), writes=rw_Hm.rs)
                    fw.op("pool", lambda e: e.memset(rw_Pb.t[:], 0.0), writes=[rw_Pb.r])
                    if s == 2:
                        fw.op("pool", lambda e: e.memset(rw_st.t[:], 0.0), writes=[rw_st.r])
                        for h in range(4):
                            hr = slice((h % 2) * 64, (h % 2) * 64 + 64)
                            fw.dma("sp", rw_st.t[hr, h // 2, (h % 2) * 64:(h % 2) * 64 + 64], I["srw"][l][h],
                                   writes=[rw_st.r], stream="stld", group=True)
                        for pc in range(2):
                            pb = next_bank()
                            fw.op("pe", lambda e, pc=pc, pb=pb: e.transpose(pb.t[:, 0:128], rw_st.t[:, pc, :], ident.t[:]),
                                  reads=[rw_st.r, ident.r], writes=[pb.r])
                            fw.op("dve", lambda e, pc=pc, pb=pb: e.tensor_copy(out=rw_Hm.t[:, pc, :], in_=pb.t[:, 0:128]),
                                  reads=[pb.r], writes=[rw_Hm.rs[pc]])
                        fw.dma("sp", rw_c7.t[:, :], I["ssh"][l].rearrange("(c p) -> p c", p=128),
                               writes=[rw_c7.r], stream="stld", group=True, allow_slow_non_contiguous=True)
                        fw.op("dve", lambda e: e.tensor_copy(out=rw_Pb.t[:, 0:6, 0], in_=rw_c7.t[:, 0:6]),
                              reads=[rw_c7.r], writes=[rw_Pb.r])
                        fw.op("dve", lambda e: e.tensor_copy(out=rw_Pb.t[0:32, 6, 0:1], in_=rw_c7.t[0:32, 6:7]),
                              reads=[rw_c7.r], writes=[rw_Pb.r])
                        fw.op("dve", lambda e: e.tensor_copy(out=rw_Pb.t[0:32, 7, 0:1], in_=rw_c7.t[32:64, 6:7]),
                              reads=[rw_c7.r], writes=[rw_Pb.r])
                        fw.op("dve", lambda e: e.tensor_copy(out=rw_Pb.t[0:64, 8, 0:1], in_=rw_c7.t[64:128, 6:7]),
                              reads=[rw_c7.r], writes=[rw_Pb.r])
                    for pc in range(2):
                        fw.op("act", lambda e, pc=pc: e.activation(out=rw_Hbd.t[:, pc, :], in_=rw_Hm.t[:, pc, :],
                                                                   func=AF.Copy), reads=[rw_Hm.rs[pc]], writes=[rw_Hbd.rs[pc]])

                def rw_final(s, w):
                    dst = O["rwp"][l][s] if s < 2 else O["rws"][l]
                    for pc in range(2):
                        pb = next_bank()
                        fw.op("pe", lambda e, pc=pc, pb=pb: e.transpose(pb.t[:, 0:128], rw_Hm.t[:, pc, :], ident.t[:]),
                              reads=[rw_Hm.rs[pc], ident.r], writes=[pb.r])
                        fw.op("dve", lambda e, pc=pc, pb=pb: e.tensor_copy(out=rw_st.t[:, pc, :], in_=pb.t[:, 0:128]),
                              reads=[pb.r], writes=[rw_st.r])
                    for h in range(4):
                        hr = slice((h % 2) * 64, (h % 2) * 64 + 64)
                        fw.dma("sp", dst[h], rw_st.t[hr, h // 2, (h % 2) * 64:(h % 2) * 64 + 64], reads=[rw_st.r],
                               stream="ststrs%d" % s, group=True)
                    dsh = O["rshp"][l][s] if s < 2 else O["rshs"][l]
                    fw.op("dve", lambda e: e.tensor_copy(out=rw_c7.t[:, 0:6], in_=rw_car.t[:, 0:6]), reads=[rw_car.r], writes=[rw_c7.r])
                    fw.op("dve", lambda e: e.tensor_copy(out=rw_c7.t[0:32, 6:7], in_=rw_car.t[0:32, 6:7]), reads=[rw_car.r], writes=[rw_c7.r])
                    fw.op("dve", lambda e: e.tensor_copy(out=rw_c7.t[32:64, 6:7], in_=rw_car.t[0:32, 7:8]), reads=[rw_car.r], writes=[rw_c7.r])
                    fw.op("dve", lambda e: e.tensor_copy(out=rw_c7.t[64:128, 6:7], in_=rw_car.t[0:64, 8:9]), reads=[rw_car.r], writes=[rw_c7.r])
                    fw.dma("sp", dsh.rearrange("(c p) -> p c", p=128), rw_c7.t[:, :], reads=[rw_c7.r],
                           stream="ststrc%d" % s, group=True, allow_slow_non_contiguous=True)

                def rw_tile(s, w):
                    C = min(64, w)
                    nch = w // C
                    nlev = {64: 5, 32: 4}[C]
                    rmask = rmask64 if C == 64 else rmask32
                    T0, T1, T2, T3, T4, T5 = rw_tmp
                    specs = [(i, i * 128, 128) for i in range(6)] + [(6, 768, 32), (7, 800, 32), (8, 832, 64)]
                    for (i, c0, M) in specs:
                        pb = proj_fm(c0, w, M)
                        sink[0].op("act", lambda e, i=i, M=M, pb=pb: e.activation(out=rw_Pb.t[0:M, i, 1:1 + w], in_=pb.t[0:M, 0:w],
                                                                          func=AF.Copy), reads=[pb.r], writes=[rw_Pb.r])
                    sink[0].op("act", lambda e: e.activation(out=rw_car.t[:, :], in_=rw_Pb.t[:, :, w], func=AF.Copy),
                          reads=[rw_Pb.r], writes=[rw_car.r])
                    for (i, c0, M) in specs:
                        sink[0].op("dve", lambda e, i=i, M=M: e.tensor_tensor(out=T0.t[0:M, 0:w], in0=rw_Pb.t[0:M, i, 0:w],
                                                                         in1=rw_Pb.t[0:M, i, 1:1 + w], op=ALU.subtract),
                              reads=[rw_Pb.r], writes=[T0.r])
                        sink[0].op("dve", lambda e, i=i, M=M: e.scalar_tensor_tensor(
                            out=rw_Pb.t[0:M, i, 1:1 + w], in0=T0.t[0:M, 0:w], scalar=mul.t[0:M, l, i:i + 1],
                            in1=rw_Pb.t[0:M, i, 1:1 + w], op0=ALU.mult, op1=ALU.add),
                            reads=[T0.r, mul.r, rw_Pb.r], writes=[rw_Pb.r])
                    sink[0].op("pool", lambda e: e.tensor_copy(out=rw_Pb.t[:, :, 0], in_=rw_car.t[:, :]),
                          reads=[rw_car.r, rw_Pb.r], writes=[rw_Pb.r])
                    XS = lambda i, M=128: rw_Pb.t[0:M, i, 1:1 + w]
                    sink[0].op("act", lambda e: e.activation(out=rw_TW.t[:, 0:w], in_=XS(6, 32), func=AF.Tanh),
                          reads=[rw_Pb.r], writes=[rw_TW.r])
                    sink[0].op("act", lambda e: e.activation(out=rw_AL.t[:, 0:w], in_=XS(7, 32), func=AF.Copy),
                          reads=[rw_Pb.r], writes=[rw_AL.r])
                    sink[0].op("act", lambda e: e.activation(out=T0.t[0:64, 0:w], in_=XS(8, 64), func=AF.Exp, scale=-1.0),
                          reads=[rw_Pb.r], writes=[T0.r])
                    sink[0].op("dve", lambda e: e.tensor_scalar_add(out=T0.t[0:64, 0:w], in0=T0.t[0:64, 0:w], scalar1=1.0),
                          reads=[T0.r], writes=[T0.r])
                    sink[0].op("dve", lambda e: e.reciprocal(out=T0.t[0:64, 0:w], in_=T0.t[0:64, 0:w]), reads=[T0.r], writes=[T0.r])
                    sink[0].op("act", lambda e: e.activation(out=rw_SG.t[:, 0:w], in_=T0.t[0:64, 0:w], func=AF.Copy),
                          reads=[T0.r], writes=[rw_SG.r])
                    outer_sink = sink[0]
                    pc_recs = [Rec(), Rec()]

                    def _pc_body(pc, T1, T2, T3, rw_SIG, rw_A, rw_L, rw_KP, rw_KN, rw_Bf, rw_sqb):
                        cs = slice(pc * 128, (pc + 1) * 128)
                        r_ap, k_ap, v_ap = XS(pc), XS(2 + pc), XS(4 + pc)
                        pw_ = next_bank()
                        sink[0].op("pe", lambda e, pw_=pw_, cs=cs: e.matmul(pw_.t[:, 0:w], lhsT=w2b.t[:, l, cs], rhs=rw_TW.t[:, 0:w],
                                                                    start=True, stop=True), reads=[w2b.r, rw_TW.r], writes=[pw_.r])
                        sink[0].op("act", lambda e, pw_=pw_, pc=pc: e.activation(out=rw_SIG.t[:, 0:w], in_=pw_.t[:, 0:w], func=AF.Exp,
                                                                         scale=-1.0, bias=nw0.t[:, l, pc:pc + 1]),
                              reads=[pw_.r, nw0.r], writes=[rw_SIG.r])
                        sink[0].op("dve", lambda e: e.tensor_scalar_add(out=rw_SIG.t[:, 0:w], in0=rw_SIG.t[:, 0:w], scalar1=1.0),
                              reads=[rw_SIG.r], writes=[rw_SIG.r])
                        sink[0].op("dve", lambda e: e.reciprocal(out=rw_SIG.t[:, 0:w], in_=rw_SIG.t[:, 0:w]),
                              reads=[rw_SIG.r], writes=[rw_SIG.r])
                        pa_ = next_bank()
                        sink[0].op("pe", lambda e, pa_=pa_, cs=cs: e.matmul(pa_.t[:, 0:w], lhsT=a2b.t[:, l, cs], rhs=rw_AL.t[:, 0:w],
                                                                    start=True, stop=True), reads=[a2b.r, rw_AL.r], writes=[pa_.r])
                        sink[0].op("act", lambda e, pa_=pa_, pc=pc: e.activation(out=rw_A.t[:, 0:w], in_=pa_.t[:, 0:w], func=AF.Exp,
                                                                         scale=-1.0, bias=na0.t[:, l, pc:pc + 1]),
                              reads=[pa_.r, na0.r], writes=[rw_A.r])
                        sink[0].op("dve", lambda e: e.tensor_scalar_add(out=rw_A.t[:, 0:w], in0=rw_A.t[:, 0:w], scalar1=1.0),
                              reads=[rw_A.r], writes=[rw_A.r])
                        sink[0].op("dve", lambda e: e.reciprocal(out=rw_A.t[:, 0:w], in_=rw_A.t[:, 0:w]), reads=[rw_A.r], writes=[rw_A.r])
                        pg_ = next_bank()
                        sink[0].op("pe", lambda e, pg_=pg_, cs=cs: e.matmul(pg_.t[:, 0:w], lhsT=g2b.t[:, l, cs], rhs=rw_SG.t[:, 0:w],
                                                                    start=True, stop=True), reads=[g2b.r, rw_SG.r], writes=[pg_.r])
                        sink[0].op("act", lambda e, pg_=pg_, pc=pc: e.activation(out=rw_G[pc].t[:, 0:w], in_=pg_.t[:, 0:w], func=AF.Copy),
                              reads=[pg_.r], writes=[rw_G[pc].r])
                        sink[0].op("dve", lambda e, pc=pc, k_ap=k_ap: e.tensor_scalar_mul(
                            out=rw_KN.t[:, 0:w], in0=k_ap, scalar1=rwp["rw_k_k"].t[:, l, pc:pc + 1]),
                            reads=[rw_Pb.r, rwp["rw_k_k"].r], writes=[rw_KN.r])
                        sink[0].op("act", lambda e: e.activation(out=rw_sqb.t[:, 0:w], in_=rw_KN.t[:, 0:w], func=AF.Square),
                              reads=[rw_KN.r], writes=[rw_sqb.r])
                        pn = next_bank()
                        sink[0].op("pe", lambda e, pn=pn: e.matmul(pn.t[:, 0:w], lhsT=onesbd.t[:], rhs=rw_sqb.t[:, 0:w],
                                                              start=True, stop=True), reads=[onesbd.r, rw_sqb.r], writes=[pn.r])
                        sink[0].op("dve", lambda e, pn=pn: e.tensor_scalar_max(out=T1.t[:, 0:w], in0=pn.t[:, 0:w], scalar1=1e-24),
                              reads=[pn.r], writes=[T1.r])
                        sink[0].op("act", lambda e: e.activation(out=T1.t[:, 0:w], in_=T1.t[:, 0:w], func=AF.Ln), reads=[T1.r], writes=[T1.r])
                        sink[0].op("act", lambda e: e.activation(out=T1.t[:, 0:w], in_=T1.t[:, 0:w], func=AF.Exp, scale=-0.5),
                              reads=[T1.r], writes=[T1.r])
                        sink[0].op("dve", lambda e: e.tensor_tensor(out=rw_KN.t[:, 0:w], in0=rw_KN.t[:, 0:w], in1=T1.t[:, 0:w], op=ALU.mult),
                              reads=[rw_KN.r, T1.r], writes=[rw_KN.r])
                        sink[0].op("dve", lambda e, pc=pc: e.tensor_scalar(
                            out=T1.t[:, 0:w], in0=rw_A.t[:, 0:w], scalar1=rwp["rw_k_a"].t[:, l, pc:pc + 1],
                            scalar2=omka.t[:, l, pc:pc + 1], op0=ALU.mult, op1=ALU.add),
                            reads=[rw_A.r, rwp["rw_k_a"].r, omka.r], writes=[T1.r])
                        sink[0].op("dve", lambda e, k_ap=k_ap: e.tensor_tensor(out=rw_KP.t[:, 0:w], in0=k_ap, in1=T1.t[:, 0:w], op=ALU.mult),
                              reads=[rw_Pb.r, T1.r], writes=[rw_KP.r])
                        sink[0].op("dve", lambda e: e.tensor_tensor(out=rw_Bf.t[:, 0:w], in0=rw_KN.t[:, 0:w], in1=rw_A.t[:, 0:w], op=ALU.mult),
                              reads=[rw_KN.r, rw_A.r], writes=[rw_Bf.r])
                        sink[0].op("dve", lambda e: e.tensor_tensor_scan(out=rw_L.t[:, 0:w], data0=rmask.t[:, 0:w], data1=rw_SIG.t[:, 0:w],
                                                                   initial=0.0, op0=ALU.mult, op1=ALU.add),
                              reads=[rw_SIG.r, rmask.r], writes=[rw_L.r])
                        Lv = rw_L.t[:, 0:w].rearrange("p (c t) -> p c t", t=C)
                        sink[0].op("act", lambda e: e.activation(out=T2.t[:, 0:w], in_=rw_L.t[:, 0:w], func=AF.Exp, scale=-C0),
                              reads=[rw_L.r], writes=[T2.r])
                        sink[0].op("dve", lambda e: e.tensor_tensor(out=T3.t[:, 0:w], in0=rw_L.t[:, 0:w], in1=rw_SIG.t[:, 0:w], op=ALU.subtract),
                              reads=[rw_L.r, rw_SIG.r], writes=[T3.r])
                        sink[0].op("act", lambda e: e.activation(out=T3.t[:, 0:w], in_=T3.t[:, 0:w], func=AF.Exp, scale=-C0),
                              reads=[T3.r], writes=[T3.r])
                        for par in range(2):
                            hm = onesbdf.t[:, par * 64:par * 64 + 1]
                            sink[0].op("dve", lambda e, pc=pc, par=par, hm=hm, r_ap=r_ap: e.scalar_tensor_tensor(
                                out=rw_QRm.t[:, par, pc, 0:nch, 1, 0:C], in0=r_ap.rearrange("p (c t) -> p c t", t=C), scalar=hm,
                                in1=T2.t[:, 0:w].rearrange("p (c t) -> p c t", t=C), op0=ALU.mult, op1=ALU.mult),
                                reads=[rw_Pb.r, T2.r, onesbdf.r], writes=[rw_QRm.rs[pc]])
                            sink[0].op("dve", lambda e, pc=pc, par=par, hm=hm: e.scalar_tensor_tensor(
                                out=rw_QRm.t[:, par, pc, 0:nch, 0, 0:C], in0=rw_KN.t[:, 0:w].rearrange("p (c t) -> p c t", t=C),
                                scalar=hm, in1=T3.t[:, 0:w].rearrange("p (c t) -> p c t", t=C), op0=ALU.mult, op1=ALU.mult),
                                reads=[rw_KN.r, T3.r, onesbdf.r], writes=[rw_QRm.rs[pc]])
                        sink[0].op("act", lambda e: e.activation(out=T2.t[:, 0:w], in_=rw_L.t[:, 0:w], func=AF.Exp, scale=C0),
                              reads=[rw_L.r], writes=[T2.r])
                        sink[0].op("dve", lambda e, pc=pc: e.tensor_tensor(out=rw_Kh.t[:, pc, 0:w], in0=rw_KP.t[:, 0:w], in1=T2.t[:, 0:w], op=ALU.mult),
                              reads=[rw_KP.r, T2.r], writes=[rw_Kh.rs[pc]])
                        for par in range(2):
                            hm = onesbdf.t[:, par * 64:par * 64 + 1]
                            sink[0].op("dve", lambda e, pc=pc, par=par, hm=hm: e.scalar_tensor_tensor(
                                out=rw_Bm.t[:, par, pc, 0:w], in0=rw_Bf.t[:, 0:w], scalar=hm, in1=T2.t[:, 0:w],
                                op0=ALU.mult, op1=ALU.mult), reads=[rw_Bf.r, T2.r, onesbdf.r], writes=[rw_Bm.rs[pc]])
                        sink[0].op("dve", lambda e, Lv=Lv: e.tensor_tensor(
                            out=T3.t[:, 0:w].rearrange("p (c t) -> p c t", t=C), in0=Lv[:, :, C - 1:C].to_broadcast([128, nch, C]),
                            in1=Lv, op=ALU.subtract), reads=[rw_L.r], writes=[T3.r])
                        sink[0].op("act", lambda e: e.activation(out=T3.t[:, 0:w], in_=T3.t[:, 0:w], func=AF.Exp, scale=-C0),
                              reads=[T3.r], writes=[T3.r])
                        sink[0].op("dve", lambda e, pc=pc: e.tensor_tensor(out=rw_Ke.t[:, pc, 0:w], in0=rw_KP.t[:, 0:w], in1=T3.t[:, 0:w], op=ALU.mult),
                              reads=[rw_KP.r, T3.r], writes=[rw_Ke.rs[pc]])
                        sink[0].op("pool", lambda e, pc=pc: e.tensor_tensor(out=rw_Be.t[:, pc, 0:w], in0=rw_Bf.t[:, 0:w], in1=T3.t[:, 0:w], op=ALU.mult),
                              reads=[rw_Bf.r, T3.r], writes=[rw_Be.rs[pc]])
                        sink[0].op("act", lambda e, pc=pc, Lv=Lv: e.activation(out=rw_gC.t[:, pc, 0:nch], in_=Lv[:, :, C - 1], func=AF.Exp,
                                                                       scale=-C0), reads=[rw_L.r], writes=[rw_gC.rs[pc]])
                        sink[0].op("act", lambda e, pc=pc, v_ap=v_ap: e.activation(out=rw_Vb.t[:, pc, 0:w], in_=v_ap, func=AF.Copy),
                              reads=[rw_Pb.r], writes=[rw_Vb.rs[pc]])
                        sink[0].op("dve", lambda e, pc=pc, r_ap=r_ap: e.scalar_tensor_tensor(
                            out=(T4 if pc == 0 else T5).t[:, 0:w], in0=r_ap, scalar=rwp["rw_r_k"].t[:, l, pc:pc + 1],
                            in1=rw_KP.t[:, 0:w], op0=ALU.mult, op1=ALU.mult),
                            reads=[rw_Pb.r, rwp["rw_r_k"].r, rw_KP.r], writes=[(T4 if pc == 0 else T5).r])
                    for pc in range(2):
                        sink[0] = pc_recs[pc]
                        tt = (T1, T2, T3) if pc == 0 else tuple(rw_tmpB)
                        _pc_body(pc, tt[0], tt[1], tt[2], rw_SIG2[pc], rw_A2[pc], rw_L2[pc], rw_KP2[pc], rw_KN2[pc],
                                 rw_Bf2[pc], rw_sqb2[pc])
                    sink[0] = outer_sink
                    merge_recs(outer_sink, pc_recs)
                    RWDBG = int(os.environ.get("RWDBG", "9"))
                    if RWDBG < 2:
                        return
                    for (src, dstT) in ((rw_Vb, rw_Vt), (rw_Ke, rw_KeT), (rw_Be, rw_BeT)):
                        for c0 in range(0, nch, 4):
                            pT = next_bank()
                            pTb = pT.t[:, :].bitcast(BF16)
                            ncc = min(4, nch - c0)
                            for ci in range(ncc):
                                c = c0 + ci
                                for pc in range(2):
                                    sink[0].op("pe", lambda e, src=src, c=c, ci=ci, pc=pc, pTb=pTb: e.transpose(
                                        pTb[0:C, ci * 256 + pc * 128:ci * 256 + (pc + 1) * 128], src.t[:, pc, c * C:(c + 1) * C],
                                        identb.t[:, :]), reads=[src.rs[pc], identb.r], writes=[pT.r],
                                        signal=(ci == ncc - 1 and pc == 1))
                            sink[0].op("act", lambda e, dstT=dstT, c0=c0, ncc=ncc, pTb=pTb: e.activation(
                                out=dstT.t[0:C, c0:c0 + ncc, :], in_=pTb[0:C, 0:ncc * 256].rearrange("p (c n) -> p c n", n=256),
                                func=AF.Copy), reads=[pT.r], writes=[dstT.r])
                    if RWDBG < 3:
                        return
                    class _V:
                        pass
                    py = [_V(), _V()]
                    for pc_ in range(2):
                        py[pc_].t = banks[5].t[:, pc_ * WA:(pc_ + 1) * WA]
                        py[pc_].r = banks[5].r
                    v4 = lambda ap: ap.rearrange("p (h a t) -> p h a t", h=4, a=2)[:, :, :, 0:C]
                    v3 = lambda ap: ap.rearrange("p (h t) -> p h t", h=4)[:, :, 0:C]
                    for c in range(nch):
                        p12, p34, p5 = next_bank(), next_bank(), next_bank()
                        AT12, AT34 = rw_AT12[c], rw_AT34[c]
                        for h in range(4):
                            par, pc = h % 2, h // 2
                            for a_ in range(2):
                                qr = rw_QRm.t[:, par, pc, c, a_, 0:C]
                                sink[0].op("pe", lambda e, c=c, h=h, pc=pc, qr=qr, p12=p12, a_=a_: e.matmul(
                                    p12.t[0:C, h * 128 + a_ * 64:h * 128 + a_ * 64 + C], lhsT=rw_Kh.t[:, pc, c * C:(c + 1) * C],
                                    rhs=qr, start=True, stop=True), reads=[rw_Kh.rs[pc], rw_QRm.rs[pc]], writes=[p12.r],
                                    signal=(h == 3 and a_ == 1))
                                sink[0].op("pe", lambda e, c=c, h=h, par=par, pc=pc, qr=qr, p34=p34, a_=a_: e.matmul(
                                    p34.t[0:C, h * 128 + a_ * 64:h * 128 + a_ * 64 + C],
                                    lhsT=rw_Bm.t[:, par, pc, c * C:(c + 1) * C], rhs=qr, start=True, stop=True),
                                    reads=[rw_Bm.rs[pc], rw_QRm.rs[pc]], writes=[p34.r], signal=(h == 3 and a_ == 1))
                            sink[0].op("pe", lambda e, h=h, par=par, pc=pc, c=c, p5=p5: e.matmul(
                                p5.t[0:C, h * 64:h * 64 + C], lhsT=rw_QRm.t[:, par, pc, c, 0, 0:C],
                                rhs=rw_Bm.t[:, par, pc, c * C:(c + 1) * C], start=True, stop=True),
                                reads=[rw_Bm.rs[pc], rw_QRm.rs[pc]], writes=[p5.r], signal=(h == 3))
                        sink[0].op("dve", lambda e, p12=p12, AT12=AT12: e.tensor_tensor(
                            out=AT12.t[0:C, :, :, 0:C], in0=v4(p12.t[0:C, :]),
                            in1=mask12.t[0:C, :, 0:C].unsqueeze(1).to_broadcast([C, 4, 2, C]), op=ALU.mult),
                            reads=[p12.r, mask12.r], writes=[AT12.r])
                        sink[0].op("dve", lambda e, p34=p34, AT34=AT34: e.tensor_tensor(
                            out=AT34.t[0:C, :, :, 0:C], in0=v4(p34.t[0:C, :]),
                            in1=mask34.t[0:C, :, 0:C].unsqueeze(1).to_broadcast([C, 4, 2, C]), op=ALU.mult),
                            reads=[p34.r, mask34.r], writes=[AT34.r])
                        X, XT, TT = rw_X[c][0], rw_XT[c][0], rw_TT[c][0]
                        sink[0].op("dve", lambda e, p5=p5, X=X: e.tensor_tensor(
                            out=X.t[0:C, :, 0:C], in0=v3(p5.t[0:C, 0:256]),
                            in1=mask5.t[0:C, 0:C].unsqueeze(1).to_broadcast([C, 4, C]), op=ALU.mult),
                            reads=[p5.r, mask5.r], writes=[X.r])
                        sink[0].op("act", lambda e, XT=XT, AT34=AT34: e.activation(out=XT.t[0:C, :, 0:C], in_=AT34.t[0:C, :, 0, 0:C], func=AF.Copy),
                              reads=[AT34.r], writes=[XT.r])
                        sink[0].op("pool", lambda e, TT=TT, AT34=AT34: e.tensor_tensor(
                            out=TT.t[0:C, :, 0:C], in0=AT34.t[0:C, :, 0, 0:C],
                            in1=identb.t[0:C, 0:C].unsqueeze(1).to_broadcast([C, 4, C]), op=ALU.add),
                            reads=[AT34.r, identb.r], writes=[TT.r])
                    cur = 0
                    for lev in range(nlev):
                        for c in range(nch):
                            Xn, XTn, TTn = rw_X[c][1 - cur], rw_XT[c][1 - cur], rw_TT[c][1 - cur]
                            X, XT, TT = rw_X[c][cur], rw_XT[c][cur], rw_TT[c][cur]
                            px, pxt, ptt = next_bank(), next_bank(), next_bank()
                            for h in range(4):
                                sink[0].op("pe", lambda e, h=h, px=px, X=X, XT=XT: e.matmul(
                                    px.t[0:C, h * 64:h * 64 + C], lhsT=XT.t[0:C, h, 0:C], rhs=X.t[0:C, h, 0:C], start=True, stop=True),
                                    reads=[X.r, XT.r], writes=[px.r], signal=(h == 3))
                            sink[0].op("act", lambda e, px=px, Xn=Xn: e.activation(
                                out=Xn.t[0:C, :, 0:C], in_=v3(px.t[0:C, 0:256]), func=AF.Copy), reads=[px.r], writes=[Xn.r])
                            if lev < nlev - 1:
                                for h in range(4):
                                    sink[0].op("pe", lambda e, h=h, pxt=pxt, X=X, XT=XT: e.matmul(
                                        pxt.t[0:C, h * 64:h * 64 + C], lhsT=X.t[0:C, h, 0:C], rhs=XT.t[0:C, h, 0:C], start=True, stop=True),
                                        reads=[X.r, XT.r], writes=[pxt.r], signal=(h == 3))
                                sink[0].op("act" if c % 2 else "dve", (lambda e, pxt=pxt, XTn=XTn: e.activation(
                                    out=XTn.t[0:C, :, 0:C], in_=v3(pxt.t[0:C, 0:256]), func=AF.Copy)) if c % 2 else
                                    (lambda e, pxt=pxt, XTn=XTn: e.tensor_copy(out=XTn.t[0:C, :, 0:C], in_=v3(pxt.t[0:C, 0:256]))),
                                    reads=[pxt.r], writes=[XTn.r])
                            for h in range(4):
                                sink[0].op("pe", lambda e, h=h, ptt=ptt, Xn=Xn, TT=TT: e.matmul(
                                    ptt.t[0:C, h * 64:h * 64 + C], lhsT=Xn.t[0:C, h, 0:C], rhs=TT.t[0:C, h, 0:C], start=True, stop=True),
                                    reads=[Xn.r, TT.r], writes=[ptt.r], signal=(h == 3))
                            sink[0].op("dve", lambda e, ptt=ptt, TT=TT, TTn=TTn: e.tensor_tensor(
                                out=TTn.t[0:C, :, 0:C], in0=v3(ptt.t[0:C, 0:256]), in1=TT.t[0:C, :, 0:C], op=ALU.add),
                                reads=[ptt.r, TT.r], writes=[TTn.r])
                        cur = 1 - cur
                    if RWDBG < 4:
                        return
                    for c in range(nch):
                        TTf = rw_TT[c][cur]
                        AT12, AT34 = rw_AT12[c], rw_AT34[c]
                        pz, pu, ph = next_bank(), next_bank(), next_bank()
                        for pc in range(2):
                            for par in range(2):
                                sink[0].op("pe", lambda e, c=c, AT12=AT12, AT34=AT34, pc=pc, par=par, pz=pz: e.matmul(
                                    pz.t[0:C, pc * 128:(pc + 1) * 128], lhsT=rw_QRm.t[:, par, pc, c, 0, 0:C], rhs=rw_Hbd.t[:, pc, :],
                                    start=(par == 0), stop=False, skip_group_check=True),
                                    reads=[rw_QRm.rs[pc], rw_Hbd.rs[pc]], writes=[pz.r], signal=False)
                            for h in (2 * pc, 2 * pc + 1):
                                sink[0].op("pe", lambda e, c=c, AT12=AT12, AT34=AT34, h=h, pz=pz: e.matmul(
                                    pz.t[0:C, h * 64:(h + 1) * 64], lhsT=AT12.t[0:C, h, 0, 0:C], rhs=rw_Vt.t[0:C, c, h * 64:(h + 1) * 64],
                                    start=False, stop=True, skip_group_check=True),
                                    reads=[AT12.r, rw_Vt.r], writes=[pz.r], signal=(h == 3))
                        sink[0].op("act", lambda e, c=c, AT12=AT12, AT34=AT34, pz=pz: e.activation(out=rw_Zb.t[0:C, :], in_=pz.t[0:C, 0:256], func=AF.Copy),
                              reads=[pz.r], writes=[rw_Zb.r])
                        for h in range(4):
                            sink[0].op("pe", lambda e, c=c, AT12=AT12, AT34=AT34, h=h, pu=pu, TTf=TTf: e.matmul(
                                pu.t[0:C, h * 64:(h + 1) * 64], lhsT=TTf.t[0:C, h, 0:C], rhs=rw_Zb.t[0:C, h * 64:(h + 1) * 64],
                                start=True, stop=True), reads=[TTf.r, rw_Zb.r], writes=[pu.r], signal=(h == 3))
                        sink[0].op("dve", lambda e, c=c, AT12=AT12, AT34=AT34, pu=pu: e.tensor_scalar_mul(out=rw_Un.t[0:C, :], in0=pu.t[0:C, 0:256], scalar1=-1.0),
                              reads=[pu.r], writes=[rw_Un.r])
                        for pc in range(2):
                          for par in range(2):
                                sink[0].op("pe", lambda e, c=c, AT12=AT12, AT34=AT34, pc=pc, par=par: e.matmul(
                                    py[pc].t[:, c * C:(c + 1) * C], lhsT=rw_Hbd.t[:, pc, :], rhs=rw_QRm.t[:, par, pc, c, 1, 0:C],
                                    start=(par == 0), stop=False, skip_group_check=True),
                                    reads=[rw_QRm.rs[pc], rw_Hbd.rs[pc]], writes=[py[pc].r], signal=False)
                          for h in (2 * pc, 2 * pc + 1):
                            hs = slice((h % 2) * 64, (h % 2) * 64 + 64)
                            sink[0].op("pe", lambda e, c=c, AT12=AT12, AT34=AT34, h=h, pc=pc, hs=hs: e.matmul(
                                py[pc].t[hs, c * C:(c + 1) * C], lhsT=rw_Vt.t[0:C, c, h * 64:(h + 1) * 64], rhs=AT12.t[0:C, h, 1, 0:C],
                                start=False, stop=False, skip_group_check=True),
                                reads=[rw_Vt.r, AT12.r], writes=[py[pc].r], signal=False)
                            sink[0].op("pe", lambda e, c=c, AT12=AT12, AT34=AT34, h=h, pc=pc, hs=hs: e.matmul(
                                py[pc].t[hs, c * C:(c + 1) * C], lhsT=rw_Un.t[0:C, h * 64:(h + 1) * 64], rhs=AT34.t[0:C, h, 1, 0:C],
                                start=False, stop=True, skip_group_check=True),
                                reads=[rw_Un.r, AT34.r], writes=[py[pc].r], signal=(h % 2 == 1))
                        for pc in range(2):
                            cs = slice(pc * 128, (pc + 1) * 128)
                            sink[0].op("pe", lambda e, c=c, AT12=AT12, AT34=AT34, pc=pc, cs=cs, ph=ph: e.matmul(
                                ph.t[:, cs], lhsT=rw_KeT.t[0:C, c, cs], rhs=rw_Vt.t[0:C, c, cs], start=True, stop=False),
                                reads=[rw_KeT.r, rw_Vt.r], writes=[ph.r], signal=False)
                            sink[0].op("pe", lambda e, c=c, AT12=AT12, AT34=AT34, pc=pc, cs=cs, ph=ph: e.matmul(
                                ph.t[:, cs], lhsT=rw_BeT.t[0:C, c, cs], rhs=rw_Un.t[0:C, cs], start=False, stop=True),
                                reads=[rw_BeT.r, rw_Un.r], writes=[ph.r])
                            for hh in range(2):
                                hr = slice(hh * 64, hh * 64 + 64)
                                sink[0].op("dve", lambda e, c=c, AT12=AT12, AT34=AT34, pc=pc, hr=hr, hh=hh, ph=ph: e.scalar_tensor_tensor(
                                    out=rw_Hm.t[hr, pc, hr], in0=rw_Hm.t[hr, pc, hr], scalar=rw_gC.t[hr, pc, c:c + 1],
                                    in1=ph.t[hr, pc * 128 + hh * 64:pc * 128 + hh * 64 + 64], op0=ALU.mult, op1=ALU.add),
                                    reads=[rw_Hm.rs[pc], rw_gC.rs[pc], ph.r], writes=[rw_Hm.rs[pc]])
                            sink[0].op("act", lambda e, c=c, AT12=AT12, AT34=AT34, pc=pc: e.activation(out=rw_Hbd.t[:, pc, :], in_=rw_Hm.t[:, pc, :], func=AF.Copy),
                                  reads=[rw_Hm.rs[pc]], writes=[rw_Hbd.rs[pc]])
                    if RWDBG < 5:
                        return
                    for pc in range(2):
                        v_ap = XS(4 + pc)
                        TB = T4 if pc == 0 else T5
                        sink[0].op("act", lambda e, pc=pc: e.activation(out=rw_Yf.t[:, 0:w], in_=py[pc].t[:, 0:w], func=AF.Copy),
                              reads=[py[pc].r], writes=[rw_Yf.r])
                        sink[0].op("act", lambda e: e.activation(out=rw_sqb.t[:, 0:w], in_=rw_Yf.t[:, 0:w], func=AF.Copy),
                              reads=[rw_Yf.r], writes=[rw_sqb.r])
                        pm = next_bank()
                        sink[0].op("pe", lambda e, pm=pm: e.matmul(pm.t[:, 0:w], lhsT=onesbd.t[:], rhs=rw_sqb.t[:, 0:w], start=True, stop=True),
                              reads=[onesbd.r, rw_sqb.r], writes=[pm.r])
                        sink[0].op("dve", lambda e, pm=pm: e.scalar_tensor_tensor(
                            out=rw_Yf.t[:, 0:w], in0=pm.t[:, 0:w], scalar=-1.0 / 64, in1=rw_Yf.t[:, 0:w], op0=ALU.mult, op1=ALU.add),
                            reads=[pm.r, rw_Yf.r], writes=[rw_Yf.r])
                        sink[0].op("act", lambda e: e.activation(out=rw_sqb.t[:, 0:w], in_=rw_Yf.t[:, 0:w], func=AF.Square),
                              reads=[rw_Yf.r], writes=[rw_sqb.r])
                        pv = next_bank()
                        sink[0].op("pe", lambda e, pv=pv: e.matmul(pv.t[:, 0:w], lhsT=onesbd.t[:], rhs=rw_sqb.t[:, 0:w], start=True, stop=True),
                              reads=[onesbd.r, rw_sqb.r], writes=[pv.r])
                        sink[0].op("act", lambda e, pv=pv: e.activation(out=T0.t[:, 0:w], in_=pv.t[:, 0:w], func=AF.Ln, scale=1.0 / 64,
                                                                   bias=eps2.t[:]), reads=[pv.r, eps2.r], writes=[T0.r])
                        sink[0].op("act", lambda e: e.activation(out=T0.t[:, 0:w], in_=T0.t[:, 0:w], func=AF.Exp, scale=-0.5),
                              reads=[T0.r], writes=[T0.r])
                        sink[0].op("dve", lambda e: e.tensor_tensor(out=rw_Yf.t[:, 0:w], in0=rw_Yf.t[:, 0:w], in1=T0.t[:, 0:w], op=ALU.mult),
                              reads=[rw_Yf.r, T0.r], writes=[rw_Yf.r])
                        sink[0].op("dve", lambda e, pc=pc: e.tensor_scalar(
                            out=rw_Yf.t[:, 0:w], in0=rw_Yf.t[:, 0:w], scalar1=rwp["rw_ln_w"].t[:, l, pc:pc + 1],
                            scalar2=rwp["rw_ln_b"].t[:, l, pc:pc + 1], op0=ALU.mult, op1=ALU.add),
                            reads=[rw_Yf.r, rwp["rw_ln_w"].r, rwp["rw_ln_b"].r], writes=[rw_Yf.r])
                        sink[0].op("act", lambda e, TB=TB: e.activation(out=rw_sqb.t[:, 0:w], in_=TB.t[:, 0:w], func=AF.Copy),
                              reads=[TB.r], writes=[rw_sqb.r])
                        pbn = next_bank()
                        sink[0].op("pe", lambda e, pbn=pbn: e.matmul(pbn.t[:, 0:w], lhsT=onesbd.t[:], rhs=rw_sqb.t[:, 0:w], start=True, stop=True),
                              reads=[onesbd.r, rw_sqb.r], writes=[pbn.r])
                        sink[0].op("dve", lambda e, pbn=pbn, v_ap=v_ap: e.tensor_tensor(out=T0.t[:, 0:w], in0=pbn.t[:, 0:w], in1=v_ap, op=ALU.mult),
                              reads=[pbn.r, rw_Pb.r], writes=[T0.r])
                        sink[0].op("dve", lambda e: e.tensor_tensor(out=rw_Yf.t[:, 0:w], in0=rw_Yf.t[:, 0:w], in1=T0.t[:, 0:w], op=ALU.add),
                              reads=[rw_Yf.r, T0.r], writes=[rw_Yf.r])
                        sink[0].op("dve", lambda e, pc=pc: e.tensor_tensor(out=ymix.t[:, pc, 0:w], in0=rw_Yf.t[:, 0:w], in1=rw_G[pc].t[:, 0:w],
                                                                      op=ALU.mult), reads=[rw_Yf.r, rw_G[pc].r], writes=[ymix.rs[pc]])

                RWKV_TILE = rw_tile

                it = 0
                a2_tiles = [(s_, t0_, w_) for s_ in range(3) for (t0_, w_, j_) in tiles_of(s_, WA)]
                a2_idx = [0]

                def a2_norm(idx):
                    s_, t0_, w_ = a2_tiles[idx]
                    xT_ = xTa[idx % 2]
                    load_xT(xT_, t0_, w_)
                    norm_fm(xT_, w_, sq, tmp, lnv, rstd, hT,
                            lambda c, s_=s_: G1.t[:, l, s_, c:c + 1], lambda c, s_=s_: mod.t[:, l, 0, s_, c:c + 1], hT.rs)
                for s in range(3):
                    hg_init(s)
                    if stage >= 4:
                        rw_init(s)

                    def a2_tile(s, t0, w, j, xT, l=l):
                        if a2_idx[0] == 0:
                            a2_norm(0)
                        fw.dma("pool", ymix.t[:, 2:6, 0:w], yfox.rearrange("(c p) t -> p c t", p=128)[:, :, t0:t0 + w],
                               reads=[yfreg(t0)], writes=ymix.rs[2:6], stream="yfld")
                        recs = []
                        if stage >= 3:
                            sink[0] = Rec()
                            recs.append(sink[0])
                            default_pool[0] = (0, 1)
                            hg_tile(s, w)
                        if stage >= 4 and RWKV_TILE is not None:
                            sink[0] = Rec()
                            recs.append(sink[0])
                            default_pool[0] = (2, 3, 4)
                            RWKV_TILE(s, w)
                        sink[0] = fw
                        default_pool[0] = (0, 1, 2, 3, 4)
                        merge_recs(fw, recs)
                        a2_idx[0] += 1
                        if a2_idx[0] < len(a2_tiles):
                            a2_norm(a2_idx[0])
                        for c in range(8):
                            po_ = next_bank()
                            for kc in range(8):
                                fw.op("pe", lambda e, c=c, kc=kc, po_=po_: e.matmul(
                                    po_.t[:, 0:w], lhsT=wo.t[:, kc, c * 128:(c + 1) * 128], rhs=ymix.t[:, kc, 0:w],
                                    start=(kc == 0), stop=(kc == 7)),
                                    reads=[wo.r, ymix.rs[kc]], writes=[po_.r], signal=(kc == 7))
                            fw.op("dve", lambda e, c=c, po_=po_: e.scalar_tensor_tensor(
                                out=xT.t[:, c, 0:w], in0=po_.t[:, 0:w], scalar=mod.t[:, l, 2, s, c:c + 1],
                                in1=xT.t[:, c, 0:w], op0=ALU.mult, op1=ALU.add),
                                reads=[po_.r, mod.r, xT.rs[c]], writes=[xT.rs[c]])
                        store_xT(xT, t0, w)
                    for (t0, w, j) in tiles_of(s, WA):
                        xT = xTa[it % 2]
                        it += 1
                        a2_tile(s, t0, w, j, xT)
                    if stage >= 3:
                        hg_final(s)
                    if stage >= 4:
                        rw_final(s, w)
                fw.flush()
                default_pool[0] = (0, 1, 2, 3, 4, 5, 6, 7)
            with contextlib.ExitStack() as ph:
                sub = FWScope(fw, ph)
                WB = 256
                wfi = Tt(sub.sbuf("wfi", [128, 8, 2 * DFF], BF16), name="wfi")
                wfo = Tt(sub.sbuf("wfo", [128, 22, D], BF16), name="wfo")
                for kc in range(8):
                    fw.dma("pool", wfi.t[:, kc, :], I["w_ffn_in"][l][kc * 128:(kc + 1) * 128, :], writes=[wfi.r],
                           stream="wld", group=True)
                for kc in range(22):
                    fw.dma("pool", wfo.t[:, kc, :], I["w_ffn_out"][l][kc * 128:(kc + 1) * 128, :], writes=[wfo.r],
                           stream="wld", group=True)
                xTb = [Tt(sub.sbuf("xTb%d" % i, [128, 8, WB], F32), nreg=8, name="xTb%d" % i) for i in range(2)]
                sq = Tt(sub.sbuf("sqb", [128, 8, WB], BF16), name="sqb")
                hT = Tt(sub.sbuf("hTb", [128, 8, WB], BF16), nreg=8, name="hTb")
                tmp = [Tt(sub.sbuf("tmpb%d" % i, [128, WB], F32), name="tmpb%d" % i) for i in range(2)]
                lnv = Tt(sub.sbuf("lnvb", [128, WB], F32), name="lnvb")
                rstd = Tt(sub.sbuf("rstdb", [128, WB], F32), name="rstdb")
                actT = Tt(sub.sbuf("actT", [128, 22, WB], BF16), nreg=22, name="actT")
                sg = [Tt(sub.sbuf("sg%d" % i, [128, WB], F32), name="sg%d" % i) for i in range(2)]
                tiles_b = [(s_, t0_, w_) for s_ in range(3) for (t0_, w_, j_) in tiles_of(s_, WB)]

                def b_norm(idx):
                    s_, t0_, w_ = tiles_b[idx]
                    xT_ = xTb[idx % 2]
                    load_xT(xT_, t0_, w_, q="sp")
                    norm_fm(xT_, w_, sq, tmp, lnv, rstd, hT,
                            lambda c, s_=s_: G2.t[:, l, s_, c:c + 1], lambda c, s_=s_: mod.t[:, l, 3, s_, c:c + 1], hT.rs)
                b_norm(0)
                for idx in range(len(tiles_b)):
                    if True:
                        s, t0, w = tiles_b[idx]
                        xT = xTb[idx % 2]
                        for f in range(22):
                            pg = next_bank()
                            pu = next_bank()
                            for kc in range(8):
                                fw.op("pe", lambda e, kc=kc, f=f, pg=pg, w=w: e.matmul(
                                    pg.t[:, 0:w], lhsT=wfi.t[:, kc, f * 128:(f + 1) * 128], rhs=hT.t[:, kc, 0:w],
                                    start=(kc == 0), stop=(kc == 7)),
                                    reads=[wfi.r, hT.rs[kc]], writes=[pg.r], signal=(kc == 7))
                            for kc in range(8):
                                fw.op("pe", lambda e, kc=kc, f=f, pu=pu, w=w: e.matmul(
                                    pu.t[:, 0:w], lhsT=wfi.t[:, kc, DFF + f * 128:DFF + (f + 1) * 128],
                                    rhs=hT.t[:, kc, 0:w], start=(kc == 0), stop=(kc == 7)),
                                    reads=[wfi.r, hT.rs[kc]], writes=[pu.r], signal=(kc == 7))
                            sgt = sg[f % 2]
                            fw.op("act", lambda e, pg=pg, sgt=sgt, w=w: e.activation(
                                out=sgt.t[:, 0:w], in_=pg.t[:, 0:w], func=AF.Silu), reads=[pg.r], writes=[sgt.r])
                            fw.op("dve", lambda e, pu=pu, sgt=sgt, f=f, w=w: e.tensor_tensor(
                                out=actT.t[:, f, 0:w], in0=sgt.t[:, 0:w], in1=pu.t[:, 0:w], op=ALU.mult),
                                reads=[sgt.r, pu.r], writes=[actT.rs[f]])
                        if idx + 1 < len(tiles_b):
                            b_norm(idx + 1)
                        for c in range(8):
                            po = next_bank()
                            for f in range(22):
                                fw.op("pe", lambda e, c=c, f=f, po=po, w=w: e.matmul(
                                    po.t[:, 0:w], lhsT=wfo.t[:, f, c * 128:(c + 1) * 128], rhs=actT.t[:, f, 0:w],
                                    start=(f == 0), stop=(f == 21)),
                                    reads=[wfo.r, actT.rs[f]], writes=[po.r], signal=(f == 21))
                            fw.op("dve", lambda e, c=c, po=po, xT=xT, s=s, w=w: e.scalar_tensor_tensor(
                                out=xT.t[:, c, 0:w], in0=po.t[:, 0:w], scalar=mod.t[:, l, 5, s, c:c + 1],
                                in1=xT.t[:, c, 0:w], op0=ALU.mult, op1=ALU.add),
                                reads=[po.r, mod.r, xT.rs[c]], writes=[xT.rs[c]])
                        store_xT(xT, t0, w, q="sp")
                fw.flush()

        with contextlib.ExitStack() as ph:
            sub = FWScope(fw, ph)
            WE = 512
            xTe = [Tt(sub.sbuf("xTe%d" % i, [128, 8, WE], F32), nreg=8, name="xTe%d" % i) for i in range(2)]
            sq = Tt(sub.sbuf("sqe", [128, 8, WE], BF16), name="sqe")
            yT = Tt(sub.sbuf("yTe", [128, 8, WE], F32), nreg=8, name="yTe")
            tmp = [Tt(sub.sbuf("tmpe%d" % i, [128, WE], F32), name="tmpe%d" % i) for i in range(2)]
            lnv = Tt(sub.sbuf("lnve", [128, WE], F32), name="lnve")
            rstd = Tt(sub.sbuf("rstde", [128, WE], F32), name="rstde")
            ytok = [Tt(sub.sbuf("ytok%d" % i, [128, D], F32), name="ytok%d" % i) for i in range(2)]
            it = 0
            ik = 0
            for s in range(3):
                dst = O["yp"][s] if s < 2 else O["ys"]
                off, T = seqs[s]
                for (t0, w, j) in tiles_of(s, WE):
                    xT = xTe[it % 2]
                    it += 1
                    load_xT(xT, t0, w)
                    norm_fm(xT, w, sq, tmp, lnv, rstd, yT, lambda c: fng.t[:, c:c + 1], None, yT.rs)
                    nb = (w + 127) // 128
                    pw = min(w, 128)
                    for tb in range(nb):
                        yt = ytok[ik % 2]
                        ik += 1
                        for half in range(2):
                            pb = next_bank()
                            for cc in range(4):
                                c = half * 4 + cc
                                fw.op("pe", lambda e, c=c, cc=cc, tb=tb, pb=pb, pw=pw: e.transpose(
                                    pb.t[0:pw, cc * 128:(cc + 1) * 128], yT.t[:, c, tb * 128:tb * 128 + pw],
                                    ident.t[:, :]),
                                    reads=[yT.rs[c], ident.r], writes=[pb.r], signal=(cc == 3))
                            if half == 0:
                                fw.op("act", lambda e, pb=pb, yt=yt, pw=pw: e.activation(
                                    out=yt.t[0:pw, 0:512], in_=pb.t[0:pw, :], func=AF.Copy), reads=[pb.r], writes=[yt.r])
                            else:
                                fw.op("dve", lambda e, pb=pb, yt=yt, pw=pw: e.tensor_copy(
                                    out=yt.t[0:pw, 512:1024], in_=pb.t[0:pw, :]), reads=[pb.r], writes=[yt.r])
                        lt0 = t0 - off + tb * 128
                        fw.dma("sp" if ik % 2 else "pool", dst[lt0:lt0 + pw, :], yt.t[0:pw, :], reads=[yt.r],
                               stream="yout")
            fw.flush()
        fw.finish()
    return nc


class FWScope:
    ctr = 0

    def __init__(self, fw, stack):
        self.fw = fw
        self.stack = stack

    def sbuf(self, name, shape, dt):
        FWScope.ctr += 1
        return self.stack.enter_context(self.fw.nc.sbuf_tensor("%s_u%d" % (name, FWScope.ctr), list(shape), dt))


def layer_mixer(fw, nc, I, O, l, L, SEQ, TS, PAST, env):
    pass


_PROG_CACHE = {}


def _get_prog(SEQ, DEPTH, TS, PAST, stage=9):
    key = (SEQ, DEPTH, TS, PAST, stage)
    if key not in _PROG_CACHE:
        _PROG_CACHE[key] = build_program(SEQ, DEPTH, TS, PAST, stage)
    return _PROG_CACHE[key]


def make_in_maps(inp, ncores, L):
    f = lambda a: np.ascontiguousarray(np.asarray(a, dtype=np.float32))
    maps = []
    shared = {k: f(inp[k]) for k in ("norm1_g", "w_ada", "b_ada", "w_in", "rw_mu", "rw_w0", "rw_w2", "rw_a0",
                                     "rw_a2", "rw_g2", "rw_k_k", "rw_k_a", "rw_ln_w", "rw_ln_b", "fox_b_f",
                                     "hg_lb_logits", "hg_norm_g", "w_out", "norm2_g", "w_ffn_in", "w_ffn_out",
                                     "final_norm_g")}
    shared["rw_r_k"] = f(inp["rw_r_k"]).reshape(L, 256)
    xp, xs = f(inp["x_prompt"]), f(inp["x_sample"])
    cp, cs = f(inp["c_prompt"]), f(inp["c_sample"])
    ck, cv, cl = f(inp["cache_fox_k"]), f(inp["cache_fox_v"]), f(inp["cache_fox_logf"])
    srw, ssh, shg = f(inp["state_rwkv"]), f(inp["state_rwkv_shift"]), f(inp["state_hgrn"])
    P = ck.shape[2]
    for i in range(ncores):
        m = dict(shared)
        m["xp"] = f(xp[2 * i:2 * i + 2])
        m["xs"] = f(xs[i])
        m["cc"] = f(np.concatenate([cp[2 * i:2 * i + 2], cs[i:i + 1]], axis=0))
        m["ck"] = f(ck[:, i].reshape(L, P, 512))
        m["cv"] = f(cv[:, i].reshape(L, P, 512))
        m["cl"] = f(cl[:, i])
        m["srw"] = f(srw[:, i])
        m["ssh"] = f(ssh[:, i, 0])
        m["shg"] = f(shg[:, i])
        maps.append(m)
    return maps


def gather_outputs(res, ncores, L, SEQ, TS):
    r = res
    cat = lambda k, ax: np.concatenate([r[i][k] for i in range(ncores)], axis=ax)
    stack = lambda k, ax: np.stack([r[i][k] for i in range(ncores)], axis=ax)
    yp = cat("yp", 0)
    ys = stack("ys", 0)
    fkp = cat("fkp", 1).reshape(L, 2 * ncores, SEQ, 8, 64)
    fvp = cat("fvp", 1).reshape(L, 2 * ncores, SEQ, 8, 64)
    flp = cat("flp", 1)
    rwp = cat("rwp", 1)
    rshp = cat("rshp", 1).reshape(L, 2 * ncores, 1, RW_COLS)
    hgp = cat("hgp", 1)
    fks = stack("fks", 1).reshape(L, ncores, TS, 8, 64)
    fvs = stack("fvs", 1).reshape(L, ncores, TS, 8, 64)
    fls = stack("fls", 1)
    rws = stack("rws", 1)
    rshs = stack("rshs", 1).reshape(L, ncores, 1, RW_COLS)
    hgs = stack("hgs", 1)
    return (yp, ys, fkp, fvp, flp, rwp, rshp, hgp, fks, fvs, fls, rws, rshs, hgs)


def kernel(**inputs):
    L = int(np.asarray(inputs["w_in"]).shape[0])
    SEQ = int(np.asarray(inputs["x_prompt"]).shape[1])
    TS = int(np.asarray(inputs["x_sample"]).shape[1])
    PAST = int(np.asarray(inputs["cache_fox_k"]).shape[2])
    ncores = int(np.asarray(inputs["x_sample"]).shape[0])
    nc = _get_prog(SEQ, L, TS, PAST)
    maps = make_in_maps(inputs, ncores, L)
    res = run_bass_kernel_spmd(nc, maps, core_ids=list(range(ncores)))
    return gather_outputs(res.results, ncores, L, SEQ, TS)
```

```python
import contextlib
import os
import numpy as np
import concourse.bass as bass
import concourse.mybir as mybir
from concourse.bass_utils import run_bass_kernel_spmd

F32 = mybir.dt.float32
BF16 = mybir.dt.bfloat16
AF = mybir.ActivationFunctionType
ALU = mybir.AluOpType

D = 1024
NC8 = 8
HD = 64
RW_COLS, FOX_COLS, HG_COLS = 896, 1544, 1024
IN_COLS = 3464
DFF = 2816
EPS = 1e-6


class Reg:
    __slots__ = ("name", "writers", "readers")

    def __init__(self, name=""):
        self.name = name
        self.writers = {}
        self.readers = {}


class Eng:
    def __init__(self, name, kind):
        self.name = name
        self.kind = kind
        self.ops = []
        self.sem = None
        self.count = 0
        self.waited = {}


class FW:
    def __init__(self, nc, stack):
        self.nc = nc
        self.stack = stack
        self.engs = {}
        self.dma_sems = {}
        self.dma_counts = {}
        self.group_sems = {}
        self.nsem = 0
        for name in ("pe", "act", "dve", "pool", "sp"):
            e = Eng(name, name)
            self.engs[name] = e
            if name != "sp":
                e.sem = self.new_sem("s_" + name)

    def new_sem(self, name):
        self.nsem += 1
        return self.stack.enter_context(self.nc.semaphore("%s_%d" % (name, self.nsem)))

    def sbuf(self, name, shape, dt):
        return self.stack.enter_context(self.nc.sbuf_tensor(name, list(shape), dt))

    def psum(self, name, shape, dt=F32):
        return self.stack.enter_context(self.nc.psum_tensor(name, list(shape), dt))

    def _collect(self, reads, writes):
        deps = {}

        def add(d):
            for k, (sem, val) in d.items():
                cur = deps.get(k)
                if cur is None or cur[1] < val:
                    deps[k] = (sem, val)
        for r in reads:
            add(r.writers)
        for w in writes:
            add(w.writers)
            add(w.readers)
        return deps

    def _waits(self, eng, deps, raw_keys):
        waits = []
        for k, (sem, val) in deps.items():
            if eng.sem is not None and k == id(eng.sem) and k not in raw_keys:
                continue
            if eng.waited.get(k, 0) >= val:
                continue
            eng.waited[k] = val
            st = self.group_sems.get(k)
            if st is not None:
                waits.append((sem, _Lazy(self.dma_counts, st)))
            else:
                waits.append((sem, val))
        return waits

    def op(self, engname, fn, reads=(), writes=(), signal=True):
        eng = self.engs[engname]
        reads = [r for r in reads if r is not None]
        writes = [w for w in writes if w is not None]
        deps = self._collect(reads, writes)
        raw_keys = set()
        k = id(eng.sem)
        if engname != "pe":
            raw_keys.add(k)
        for r in reads:
            if k in r.writers:
                raw_keys.add(k)
        waits = self._waits(eng, deps, raw_keys)
        sem = eng.sem
        if signal:
            eng.count += 1
            tok = (sem, eng.count)
        else:
            tok = (sem, eng.count + 1)
        for r in reads:
            r.readers[id(sem)] = tok
        for w in writes:
            w.writers = {id(sem): tok}
            w.readers = {}

        def run(e, fn=fn, waits=waits, signal=signal, sem=sem):
            for (s, v) in waits:
                e.wait_ge(s, int(v))
            ins = fn(e)
            if signal:
                ins.then_inc(sem, 1)
        eng.ops.append(run)

    def dma(self, qname, out, in_, reads=(), writes=(), stream="d", group=False, **kw):
        eng = self.engs[qname]
        reads = [r for r in reads if r is not None]
        writes = [w for w in writes if w is not None]
        deps = self._collect(reads, writes)
        stream = stream + "_" + qname
        if stream not in self.dma_sems:
            self.dma_sems[stream] = self.new_sem("dq_" + stream)
            self.dma_counts[stream] = 0
            if group:
                self.group_sems[id(self.dma_sems[stream])] = stream
        if group:
            deps.pop(id(self.dma_sems[stream]), None)
        waits = self._waits(eng, deps, set(deps.keys()))
        sem = self.dma_sems[stream]
        self.dma_counts[stream] += 16
        tok = (sem, self.dma_counts[stream])
        for r in reads:
            r.readers[id(sem)] = tok
        for w in writes:
            w.writers = {id(sem): tok}
            w.readers = {}

        def run(e, waits=waits, sem=sem, out=out, in_=in_, kw=kw):
            for (s, v) in waits:
                e.wait_ge(s, int(v))
            e.dma_start(out=out, in_=in_, **kw).then_inc(sem, 16)
        eng.ops.append(run)

    def rotate(self):
        for e in self.engs.values():
            if e.sem is not None:
                e.sem = self.new_sem("s_" + e.name)
                e.count = 0

    def barrier(self):
        toks = []
        for e in self.engs.values():
            if e.sem is not None and e.count > 0:
                toks.append((e.sem, e.count))
        for s in self.dma_sems:
            if self.dma_counts[s] > 0:
                toks.append((self.dma_sems[s], self.dma_counts[s]))
        for e in self.engs.values():
            waits = []
            for (sem, val) in toks:
                if sem is e.sem:
                    continue
                if e.waited.get(id(sem), 0) >= val:
                    continue
                e.waited[id(sem)] = val
                waits.append((sem, val))

            def run(h, waits=waits):
                for (s, v) in waits:
                    h.wait_ge(s, v)
            e.ops.append(run)

    def finish(self):
        self.flush()

    def flush(self):
        self.barrier()
        nc = self.nc
        engs = self.engs
        oplists = {k: e.ops for k, e in engs.items()}
        for e in engs.values():
            e.ops = []

        class _E:
            def __init__(self, ops):
                self.ops = ops
        self_engs = {k: _E(v) for k, v in oplists.items()}
        with nc.Block() as block:
            def mk(eng):
                def body(e):
                    for f in eng.ops:
                        f(e)
                return body
            block.tensor(mk(self_engs["pe"]))
            block.scalar(mk(self_engs["act"]))
            block.vector(mk(self_engs["dve"]))
            block.gpsimd(mk(self_engs["pool"]))
            block.sync(mk(self_engs["sp"]))


class _Lazy:
    def __init__(self, counts, stream):
        self.counts = counts
        self.stream = stream

    def __int__(self):
        return self.counts[self.stream]


class Rec:
    def __init__(self):
        self.items = []

    def op(self, *a, **k):
        self.items.append(("op", a, k))

    def dma(self, *a, **k):
        self.items.append(("dma", a, k))

    def flush(self):
        pass

    def replay_into(self, sink):
        for (kind, a, k) in self.items:
            getattr(sink, kind)(*a, **k)


def merge_recs(sink, recs):
    pos = [0] * len(recs)
    n = [len(r.items) for r in recs]
    total = sum(n)
    for _ in range(total):
        best, bf_ = -1, 2.0
        for i in range(len(recs)):
            if pos[i] < n[i]:
                f = pos[i] / n[i]
                if f < bf_:
                    best, bf_ = i, f
        kind, a, k = recs[best].items[pos[best]]
        pos[best] += 1
        getattr(sink, kind)(*a, **k)


class Tt:
    def __init__(self, t, nreg=1, name=""):
        self.t = t
        self.rs = [Reg("%s%d" % (name, i)) for i in range(nreg)]
        self.r = self.rs[0]


def build_program(SEQ, DEPTH, TS=32, PAST=2048, stage=9):
    L = DEPTH
    NTOK = 2 * SEQ + TS
    nc = bass.Bass("TRN2", target_bir_lowering=False)
    din = lambda n, s: nc.dram_tensor(n, list(s), F32, kind="ExternalInput").ap()
    dout = lambda n, s: nc.dram_tensor(n, list(s), F32, kind="ExternalOutput").ap()
    I = dict(
        xp=din("xp", (2, SEQ, D)), xs=din("xs", (TS, D)), cc=din("cc", (3, D)),
        ck=din("ck", (L, PAST, 512)), cv=din("cv", (L, PAST, 512)), cl=din("cl", (L, PAST, 8)),
        srw=din("srw", (L, 4, 64, 64)), ssh=din("ssh", (L, RW_COLS)), shg=din("shg", (L, 4, 64, 64)),
        norm1_g=din("norm1_g", (L, D)), w_ada=din("w_ada", (L, D, 6 * D)), b_ada=din("b_ada", (L, 6 * D)),
        w_in=din("w_in", (L, D, IN_COLS)), rw_mu=din("rw_mu", (L, RW_COLS)), rw_w0=din("rw_w0", (L, 256)),
        rw_w2=din("rw_w2", (L, 32, 256)), rw_a0=din("rw_a0", (L, 256)), rw_a2=din("rw_a2", (L, 32, 256)),
        rw_g2=din("rw_g2", (L, 64, 256)), rw_k_k=din("rw_k_k", (L, 256)), rw_k_a=din("rw_k_a", (L, 256)),
        rw_r_k=din("rw_r_k", (L, 256)), rw_ln_w=din("rw_ln_w", (L, 256)), rw_ln_b=din("rw_ln_b", (L, 256)),
        fox_b_f=din("fox_b_f", (L, 8)), hg_lb_logits=din("hg_lb_logits", (L, 256)),
        hg_norm_g=din("hg_norm_g", (L, 256)), w_out=din("w_out", (L, D, D)), norm2_g=din("norm2_g", (L, D)),
        w_ffn_in=din("w_ffn_in", (L, D, 2 * DFF)), w_ffn_out=din("w_ffn_out", (L, DFF, D)),
        final_norm_g=din("final_norm_g", (D,)),
    )
    O = dict(
        yp=dout("yp", (2, SEQ, D)), ys=dout("ys", (TS, D)),
        fkp=dout("fkp", (L, 2, SEQ, 512)), fvp=dout("fvp", (L, 2, SEQ, 512)), flp=dout("flp", (L, 2, SEQ, 8)),
        rwp=dout("rwp", (L, 2, 4, 64, 64)), rshp=dout("rshp", (L, 2, RW_COLS)), hgp=dout("hgp", (L, 2, 4, 64, 64)),
        fks=dout("fks", (L, TS, 512)), fvs=dout("fvs", (L, TS, 512)), fls=dout("fls", (L, TS, 8)),
        rws=dout("rws", (L, 4, 64, 64)), rshs=dout("rshs", (L, RW_COLS)), hgs=dout("hgs", (L, 4, 64, 64)),
    )
    xres = nc.dram_tensor("xres", [D, NTOK], F32).ap()
    xres_r = Reg("xres")
    seqs = [(0, SEQ), (SEQ, SEQ), (2 * SEQ, TS)]

    def tiles_of(s, W):
        off, T = seqs[s]
        w = min(W, T)
        return [(off + j * w, w, j) for j in range(T // w)]

    xres_regs = {}

    def xreg(t0):
        return xres_regs.setdefault(t0, Reg("xres%d" % t0))

    with contextlib.ExitStack() as top:
        fw = FW(nc, top)
        ident = Tt(fw.sbuf("ident", [128, 128], F32), name="ident")
        identb = Tt(fw.sbuf("identb", [128, 128], BF16), name="identb")
        onesb = Tt(fw.sbuf("onesb", [128, 128], BF16), name="onesb")
        fw.op("pool", lambda e: e.memset(ident.t[:], 0.0), writes=[ident.r])
        fw.op("pool", lambda e: e.affine_select(out=ident.t[:], in_=ident.t[:], pattern=[[-1, 128]],
                                                compare_op=ALU.not_equal, fill=1.0, base=0,
                                                channel_multiplier=1), reads=[ident.r], writes=[ident.r])
        fw.op("pool", lambda e: e.tensor_copy(out=identb.t[:], in_=ident.t[:]), reads=[ident.r], writes=[identb.r])
        fw.op("pool", lambda e: e.memset(onesb.t[:], 1.0), writes=[onesb.r])
        epsb = Tt(fw.sbuf("epsb", [128, 1], F32), name="epsb")
        fw.op("pool", lambda e: e.memset(epsb.t[:], EPS), writes=[epsb.r])


        trif = Tt(fw.sbuf("trif", [128, 128], F32), name="trif")
        fw.op("pool", lambda e: e.memset(trif.t[:], 1.0), writes=[trif.r])
        fw.op("pool", lambda e: e.affine_select(out=trif.t[:], in_=trif.t[:], pattern=[[1, 128]],
                                                compare_op=ALU.is_ge, fill=0.0, base=0, channel_multiplier=-1),
              reads=[trif.r], writes=[trif.r])
        self127 = Tt(fw.sbuf("self127", [128, 128], F32), name="self127")
        fw.op("pool", lambda e: e.memset(self127.t[:], 0.0), writes=[self127.r])
        fw.op("pool", lambda e: e.affine_select(out=self127.t[:], in_=self127.t[:], pattern=[[0, 128]],
                                                compare_op=ALU.not_equal, fill=1.0, base=-127, channel_multiplier=1),
              reads=[self127.r], writes=[self127.r])
        mnegf = Tt(fw.sbuf("mnegf", [128, 128], F32), name="mnegf")
        maskneg = Tt(fw.sbuf("maskneg", [128, 128], BF16), name="maskneg")
        fw.op("pool", lambda e: e.memset(mnegf.t[:], 0.0), writes=[mnegf.r])
        fw.op("pool", lambda e: e.affine_select(out=mnegf.t[:], in_=mnegf.t[:], pattern=[[1, 128]],
                                                compare_op=ALU.is_ge, fill=-30000.0, base=0, channel_multiplier=-1),
              reads=[mnegf.r], writes=[mnegf.r])
        fw.op("pool", lambda e: e.tensor_copy(out=maskneg.t[:], in_=mnegf.t[:]), reads=[mnegf.r], writes=[maskneg.r])
        e8 = Tt(fw.sbuf("e8", [8, 8, 128], F32), name="e8")
        fw.op("pool", lambda e: e.memset(e8.t[:], 0.0), writes=[e8.r])
        fw.op("pool", lambda e: e.affine_select(out=e8.t[:], in_=e8.t[:], pattern=[[-1, 8], [0, 128]],
                                                compare_op=ALU.not_equal, fill=1.0, base=0, channel_multiplier=1),
              reads=[e8.r], writes=[e8.r])
        selh = Tt(fw.sbuf("selh", [72, 8, 128], BF16), name="selh")
        fw.op("pool", lambda e: e.memset(selh.t[:], 0.0), writes=[selh.r])
        for b0 in (0, 32, 64):
            fw.op("dve", lambda e, b0=b0: e.tensor_copy(out=selh.t[b0:b0 + 8, :, :], in_=e8.t[:]),
                  reads=[e8.r], writes=[selh.r])
        onesf = Tt(fw.sbuf("onesf", [128, 64], F32), name="onesf")
        fw.op("pool", lambda e: e.memset(onesf.t[:], 1.0), writes=[onesf.r])
        bfb = Tt(fw.sbuf("bfb", [128, L, 8], F32), name="bfb")
        for l_ in range(L):
            fw.dma("sp", bfb.t[:, l_, :], I["fox_b_f"][l_:l_ + 1, :].to_broadcast([128, 8]), writes=[bfb.r],
                   stream="par", group=True)
        yfox = nc.dram_tensor("yfox", [512, NTOK], BF16).ap()
        yfox_regs = {}

        def yfreg(t0):
            return yfox_regs.setdefault(t0, Reg("yfox%d" % t0))

        banks = [Tt(fw.psum("bank%d" % i, [128, 512]), name="bank%d" % i) for i in range(8)]
        bank_ctr = [0]

        default_pool = [(0, 1, 2, 3, 4, 5, 6, 7)]

        def next_bank(pool=None):
            if pool is None:
                pool = default_pool[0]
            b = banks[pool[bank_ctr[0] % len(pool)]]
            bank_ctr[0] += 1
            return b

        def load_fm(name, src_ap, ncol, q="sp"):
            t = Tt(fw.sbuf(name, [128, L, ncol], F32), name=name)
            fw.dma(q, t.t[:], src_ap.rearrange("l (c p) -> p l c", p=128), writes=[t.r], stream="par", group=True,
                   allow_slow_non_contiguous=True)
            return t
        n1g = load_fm("n1g", I["norm1_g"], 8)
        n2g = load_fm("n2g", I["norm2_g"], 8)
        badaT = load_fm("badaT", I["b_ada"], 48)
        fng = Tt(fw.sbuf("fng", [128, 8], F32), name="fng")
        fw.dma("sp", fng.t[:], I["final_norm_g"].rearrange("(c p) -> p c", p=128), writes=[fng.r], stream="par", group=True,
               allow_slow_non_contiguous=True)

        rmask32 = Tt(fw.sbuf("rmask32", [128, 512], F32), name="rmask32")
        fw.op("pool", lambda e: e.memset(rmask32.t[:], 1.0), writes=[rmask32.r])
        fw.op("pool", lambda e: e.affine_select(out=rmask32.t[:, :].rearrange("p (c t) -> p c t", t=32),
                                                in_=rmask32.t[:, :].rearrange("p (c t) -> p c t", t=32),
                                                pattern=[[0, 16], [1, 32]], compare_op=ALU.not_equal, fill=0.0,
                                                base=0, channel_multiplier=0), reads=[rmask32.r], writes=[rmask32.r])
        maskbd = Tt(fw.sbuf("maskbd", [128, 128], F32), name="maskbd")
        fw.op("pool", lambda e: e.tensor_copy(out=maskbd.t[:], in_=trif.t[:]), reads=[trif.r], writes=[maskbd.r])
        for cb_ in range(1, 4):
            fw.op("pool", lambda e, cb_=cb_: e.affine_select(
                out=maskbd.t[:, cb_ * 32:(cb_ + 1) * 32], in_=maskbd.t[:, cb_ * 32:(cb_ + 1) * 32], pattern=[[0, 32]],
                compare_op=ALU.is_ge, fill=0.0, base=-cb_ * 32, channel_multiplier=1),
                reads=[maskbd.r], writes=[maskbd.r])
        onesbdf = Tt(fw.sbuf("onesbdf", [128, 128], F32), name="onesbdf")
        onesbd = Tt(fw.sbuf("onesbd", [128, 128], BF16), name="onesbd")
        fw.op("pool", lambda e: e.memset(onesbdf.t[:], 1.0), writes=[onesbdf.r])
        fw.op("pool", lambda e: e.affine_select(out=onesbdf.t[:, 0:64], in_=onesbdf.t[:, 0:64], pattern=[[0, 64]],
                                                compare_op=ALU.is_ge, fill=0.0, base=63, channel_multiplier=-1),
              reads=[onesbdf.r], writes=[onesbdf.r])
        fw.op("pool", lambda e: e.affine_select(out=onesbdf.t[:, 64:128], in_=onesbdf.t[:, 64:128], pattern=[[0, 64]],
                                                compare_op=ALU.is_ge, fill=0.0, base=-64, channel_multiplier=1),
              reads=[onesbdf.r], writes=[onesbdf.r])
        fw.op("pool", lambda e: e.tensor_copy(out=onesbd.t[:], in_=onesbdf.t[:]), reads=[onesbdf.r], writes=[onesbd.r])
        hgng = load_fm("hgng", I["hg_norm_g"], 2)
        lbl = load_fm("lbl", I["hg_lb_logits"], 2)
        lbT = Tt(fw.sbuf("lbT", [128, L, 2], F32), name="lbT")
        omlT = Tt(fw.sbuf("omlT", [128, L, 2], F32), name="omlT")
        nomlT = Tt(fw.sbuf("nomlT", [128, L, 2], F32), name="nomlT")
        lbm = Tt(fw.sbuf("lbm", [128, 2], F32), name="lbm")
        lbe = Tt(fw.sbuf("lbe", [128, L, 2], F32), name="lbe")
        lbs_ = Tt(fw.sbuf("lbs_", [128, 2], F32), name="lbs_")
        fw.op("dve", lambda e: e.tensor_copy(out=lbm.t[:], in_=lbl.t[:, 0, :]), reads=[lbl.r], writes=[lbm.r])
        for l_ in range(1, L):
            fw.op("dve", lambda e, l_=l_: e.tensor_max(out=lbm.t[:], in0=lbm.t[:], in1=lbl.t[:, l_, :]),
                  reads=[lbm.r, lbl.r], writes=[lbm.r])
        for l_ in range(L):
            fw.op("dve", lambda e, l_=l_: e.tensor_sub(out=lbe.t[:, l_, :], in0=lbl.t[:, l_, :], in1=lbm.t[:]),
                  reads=[lbm.r, lbl.r], writes=[lbe.r])
        fw.op("act", lambda e: e.activation(out=lbe.t[:], in_=lbe.t[:], func=AF.Exp), reads=[lbe.r], writes=[lbe.r])
        fw.op("dve", lambda e: e.tensor_copy(out=lbs_.t[:], in_=lbe.t[:, 0, :]), reads=[lbe.r], writes=[lbs_.r])
        for l_ in range(1, L):
            fw.op("dve", lambda e, l_=l_: e.tensor_add(out=lbs_.t[:], in0=lbs_.t[:], in1=lbe.t[:, l_, :]),
                  reads=[lbs_.r, lbe.r], writes=[lbs_.r])
        fw.op("dve", lambda e: e.reciprocal(out=lbs_.t[:], in_=lbs_.t[:]), reads=[lbs_.r], writes=[lbs_.r])
        for l_ in range(L):
            fw.op("dve", lambda e, l_=l_: e.tensor_mul(out=lbe.t[:, l_, :], in0=lbe.t[:, l_, :], in1=lbs_.t[:]),
                  reads=[lbs_.r, lbe.r], writes=[lbe.r])
        fw.op("dve", lambda e: e.memset(lbT.t[:, 0, :], 0.0), writes=[lbT.r])
        for l_ in range(1, L):
            fw.op("dve", lambda e, l_=l_: e.tensor_add(out=lbT.t[:, l_, :], in0=lbT.t[:, l_ - 1, :], in1=lbe.t[:, l_, :]),
                  reads=[lbT.r, lbe.r], writes=[lbT.r])
        fw.op("dve", lambda e: e.tensor_scalar(out=omlT.t[:], in0=lbT.t[:], scalar1=-1.0, scalar2=1.0,
                                               op0=ALU.mult, op1=ALU.add), reads=[lbT.r], writes=[omlT.r])
        fw.op("dve", lambda e: e.tensor_scalar_mul(out=nomlT.t[:], in0=omlT.t[:], scalar1=-1.0),
              reads=[omlT.r], writes=[nomlT.r])

        rmask64 = Tt(fw.sbuf("rmask64", [128, 512], F32), name="rmask64")
        fw.op("pool", lambda e: e.memset(rmask64.t[:], 1.0), writes=[rmask64.r])
        fw.op("pool", lambda e: e.affine_select(out=rmask64.t[:, :].rearrange("p (c t) -> p c t", t=64),
                                                in_=rmask64.t[:, :].rearrange("p (c t) -> p c t", t=64),
                                                pattern=[[0, 8], [1, 64]], compare_op=ALU.not_equal, fill=0.0,
                                                base=0, channel_multiplier=0), reads=[rmask64.r], writes=[rmask64.r])
        mask12 = Tt(fw.sbuf("mask12", [64, 2, 64], F32), name="mask12")
        mask34 = Tt(fw.sbuf("mask34", [64, 2, 64], F32), name="mask34")
        mask5 = Tt(fw.sbuf("mask5", [64, 64], F32), name="mask5")
        fw.op("pool", lambda e: e.memset(mask12.t[:], 1.0), writes=[mask12.r])
        fw.op("pool", lambda e: e.affine_select(out=mask12.t[:, 0, :], in_=mask12.t[:, 0, :], pattern=[[1, 64]],
                                                compare_op=ALU.is_ge, fill=0.0, base=-1, channel_multiplier=-1),
              reads=[mask12.r], writes=[mask12.r])
        fw.op("pool", lambda e: e.affine_select(out=mask12.t[:, 1, :], in_=mask12.t[:, 1, :], pattern=[[1, 64]],
                                                compare_op=ALU.is_ge, fill=0.0, base=0, channel_multiplier=-1),
              reads=[mask12.r], writes=[mask12.r])
        fw.op("dve", lambda e: e.tensor_scalar_mul(out=mask34.t[:, 0, :], in0=mask12.t[:, 0, :], scalar1=-1.0),
              reads=[mask12.r], writes=[mask34.r])
        fw.op("pool", lambda e: e.tensor_copy(out=mask34.t[:, 1, :], in_=mask12.t[:, 1, :]),
              reads=[mask12.r], writes=[mask34.r])
        fw.op("pool", lambda e: e.memset(mask5.t[:], -1.0), writes=[mask5.r])
        fw.op("pool", lambda e: e.affine_select(out=mask5.t[:], in_=mask5.t[:], pattern=[[-1, 64]],
                                                compare_op=ALU.is_ge, fill=0.0, base=-1, channel_multiplier=1),
              reads=[mask5.r], writes=[mask5.r])
        eps2 = Tt(fw.sbuf("eps2", [128, 1], F32), name="eps2")
        fw.op("pool", lambda e: e.memset(eps2.t[:], 64e-5), writes=[eps2.r])
        rwp = {}
        for nm in ("rw_w0", "rw_a0", "rw_k_k", "rw_k_a", "rw_r_k", "rw_ln_w", "rw_ln_b"):
            rwp[nm] = load_fm("p_" + nm, I[nm], 2)
        nw0 = Tt(fw.sbuf("nw0", [128, L, 2], F32), name="nw0")
        na0 = Tt(fw.sbuf("na0", [128, L, 2], F32), name="na0")
        omka = Tt(fw.sbuf("omka", [128, L, 2], F32), name="omka")
        fw.op("dve", lambda e: e.tensor_scalar_mul(out=nw0.t[:], in0=rwp["rw_w0"].t[:], scalar1=-1.0),
              reads=[rwp["rw_w0"].r], writes=[nw0.r])
        fw.op("dve", lambda e: e.tensor_scalar_mul(out=na0.t[:], in0=rwp["rw_a0"].t[:], scalar1=-1.0),
              reads=[rwp["rw_a0"].r], writes=[na0.r])
        fw.op("dve", lambda e: e.tensor_scalar(out=omka.t[:], in0=rwp["rw_k_a"].t[:], scalar1=-1.0, scalar2=1.0,
                                               op0=ALU.mult, op1=ALU.add), reads=[rwp["rw_k_a"].r], writes=[omka.r])
        mul = Tt(fw.sbuf("mul", [128, L, 9], F32), name="mul")
        fw.op("pool", lambda e: e.memset(mul.t[:], 0.0), writes=[mul.r])
        m7 = load_fm("m7", I["rw_mu"], 7)
        fw.op("dve", lambda e: e.tensor_copy(out=mul.t[:, :, 0:6], in_=m7.t[:, :, 0:6]), reads=[m7.r], writes=[mul.r])
        fw.op("dve", lambda e: e.tensor_copy(out=mul.t[0:32, :, 6], in_=m7.t[0:32, :, 6]), reads=[m7.r], writes=[mul.r])
        fw.op("dve", lambda e: e.tensor_copy(out=mul.t[0:32, :, 7], in_=m7.t[32:64, :, 6]), reads=[m7.r], writes=[mul.r])
        fw.op("dve", lambda e: e.tensor_copy(out=mul.t[0:64, :, 8], in_=m7.t[64:128, :, 6]), reads=[m7.r], writes=[mul.r])
        w2b = Tt(fw.sbuf("w2b", [32, L, 256], BF16), name="w2b")
        a2b = Tt(fw.sbuf("a2b", [32, L, 256], BF16), name="a2b")
        g2b = Tt(fw.sbuf("g2b", [64, L, 256], BF16), name="g2b")
        fw.dma("pool", w2b.t[:], I["rw_w2"].rearrange("l k n -> k l n"), writes=[w2b.r], stream="parb", group=True)
        fw.dma("pool", a2b.t[:], I["rw_a2"].rearrange("l k n -> k l n"), writes=[a2b.r], stream="parb", group=True)
        fw.dma("pool", g2b.t[:], I["rw_g2"].rearrange("l k n -> k l n"), writes=[g2b.r], stream="parb", group=True)

        cT = Tt(fw.sbuf("cT", [128, 3, 8], F32), name="cT")
        fw.dma("sp", cT.t[:], I["cc"].rearrange("b (c p) -> p b c", p=128), writes=[cT.r], stream="par", group=True,
               allow_slow_non_contiguous=True)
        siluT = Tt(fw.sbuf("siluT", [128, 8, 3], F32), name="siluT")
        sl_e = Tt(fw.sbuf("sl_e", [128, 3, 8], F32), name="sl_e")
        fw.op("act", lambda e: e.activation(out=sl_e.t[:], in_=cT.t[:], func=AF.Exp, scale=-1.0),
              reads=[cT.r], writes=[sl_e.r])
        fw.op("dve", lambda e: e.tensor_scalar_add(out=sl_e.t[:], in0=sl_e.t[:], scalar1=1.0),
              reads=[sl_e.r], writes=[sl_e.r])
        fw.op("dve", lambda e: e.reciprocal(out=sl_e.t[:], in_=sl_e.t[:]), reads=[sl_e.r], writes=[sl_e.r])
        fw.op("dve", lambda e: e.tensor_tensor(out=siluT.t[:].rearrange("p c b -> p b c"), in0=sl_e.t[:],
                                               in1=cT.t[:], op=ALU.mult),
              reads=[sl_e.r, cT.r], writes=[siluT.r])
        mod = Tt(fw.sbuf("mod", [128, L, 6, 3, 8], F32), name="mod")
        G1 = Tt(fw.sbuf("G1", [128, L, 3, 8], F32), name="G1")
        G2 = Tt(fw.sbuf("G2", [128, L, 3, 8], F32), name="G2")
        PH0 = contextlib.ExitStack()
        fw_real = fw
        for ph in [PH0]:
            sub = FWScope(fw, ph)
            wa = [Tt(sub.sbuf("wa%d" % i, [128, 8, 512], F32), name="wa%d" % i) for i in range(2)]
            default_pool[0] = (0, 1, 2, 3)
            fw = Rec()
            k = 0
            for l in range(L):
                for jg in range(12):
                    w = wa[k % 2]
                    k += 1
                    fw.dma("sp" if k % 2 else "pool", w.t[:],
                           I["w_ada"][l].rearrange("(kc p) n -> p kc n", p=128)[:, :, jg * 512:(jg + 1) * 512],
                           writes=[w.r], stream="wada%d" % (k % 2))
                    for jj in range(4):
                        j = jg * 4 + jj
                        m, c = j // 8, j % 8
                        pb = next_bank()
                        for kc in range(8):
                            fw.op("pe", lambda e, w=w, pb=pb, kc=kc, jj=jj: e.matmul(
                                pb.t[:, 0:3], lhsT=w.t[:, kc, jj * 128:(jj + 1) * 128], rhs=siluT.t[:, kc, :],
                                start=(kc == 0), stop=(kc == 7)),
                                reads=[w.r, siluT.r], writes=[pb.r], signal=(kc == 7))
                        fw.op("dve", lambda e, pb=pb, l=l, m=m, c=c, j=j: e.tensor_scalar(
                            out=mod.t[:, l, m, :, c], in0=pb.t[:, 0:3], scalar1=badaT.t[:, l, j:j + 1], scalar2=None,
                            op0=ALU.add), reads=[pb.r, badaT.r], writes=[mod.r])
        for l in range(L):
            for (G, ng, mi) in ((G1, n1g, 1), (G2, n2g, 4)):
                for b in range(3):
                    fw.op("dve", lambda e, G=G, ng=ng, mi=mi, l=l, b=b: e.scalar_tensor_tensor(
                        out=G.t[:, l, b, :], in0=mod.t[:, l, mi, b, :], scalar=1.0, in1=ng.t[:, l, :],
                        op0=ALU.add, op1=ALU.mult), reads=[mod.r, ng.r], writes=[G.r])
        rec_ada = fw
        fw = fw_real

        def load_xT(xT, t0, w, q="sp"):
            fw.dma(q, xT.t[:, :, 0:w], xres.rearrange("(c p) t -> p c t", p=128)[:, :, t0:t0 + w],
                   reads=[xreg(t0)], writes=xT.rs, stream="xld" + xT.r.name[-2:])

        def store_xT(xT, t0, w, q="sp"):
            fw.dma(q, xres.rearrange("(c p) t -> p c t", p=128)[:, :, t0:t0 + w], xT.t[:, :, 0:w],
                   reads=xT.rs, writes=[xreg(t0)], stream="xst" + xT.r.name[-2:])

        def norm_fm(xT, w, sq, tmp, lnv, rstd, out, gap, shap, out_regs):
            fw.op("act", lambda e: e.activation(out=sq.t[:, :, 0:w], in_=xT.t[:, :, 0:w], func=AF.Square),
                  reads=xT.rs, writes=[sq.r])
            pb = next_bank()
            for c in range(8):
                fw.op("pe", lambda e, c=c: e.matmul(pb.t[:, 0:w], lhsT=onesb.t[:], rhs=sq.t[:, c, 0:w],
                                                    start=(c == 0), stop=(c == 7)),
                      reads=[onesb.r, sq.r], writes=[pb.r], signal=(c == 7))
            fw.op("act", lambda e: e.activation(out=lnv.t[:, 0:w], in_=pb.t[:, 0:w], func=AF.Ln, scale=1.0 / D,
                                                bias=epsb.t[:]), reads=[pb.r, epsb.r], writes=[lnv.r])
            fw.op("act", lambda e: e.activation(out=rstd.t[:, 0:w], in_=lnv.t[:, 0:w], func=AF.Exp, scale=-0.5),
                  reads=[lnv.r], writes=[rstd.r])
            for c in range(8):
                tm = tmp[c % len(tmp)]
                fw.op("dve", lambda e, c=c, tm=tm: e.tensor_tensor(out=tm.t[:, 0:w], in0=xT.t[:, c, 0:w],
                                                                   in1=rstd.t[:, 0:w], op=ALU.mult),
                      reads=[xT.rs[c], rstd.r], writes=[tm.r])
                if shap is not None:
                    fw.op("act", lambda e, c=c, tm=tm: e.activation(out=out.t[:, c, 0:w], in_=tm.t[:, 0:w],
                                                                    func=AF.Identity, scale=gap(c), bias=shap(c)),
                          reads=[tm.r, G1.r, G2.r, mod.r], writes=[out_regs[c]])
                else:
                    fw.op("act", lambda e, c=c, tm=tm: e.activation(out=out.t[:, c, 0:w], in_=tm.t[:, 0:w],
                                                                    func=AF.Identity, scale=gap(c)),
                          reads=[tm.r, fng.r], writes=[out_regs[c]])

        for ph in [PH0]:
            sub = FWScope(fw, ph)
            xtok = [Tt(sub.sbuf("xtok%d" % i, [128, 4, D], F32), name="xtok%d" % i) for i in range(2)]
            xTs = [Tt(sub.sbuf("xTp%d" % i, [128, 8, 512], F32), nreg=8, name="xTp%d" % i) for i in range(2)]
            default_pool[0] = (4, 5, 6, 7)
            fw = Rec()
            it = 0
            for s in range(3):
                src = I["xp"][s] if s < 2 else I["xs"]
                off, T = seqs[s]
                for (t0, w, j) in tiles_of(s, 512):
                    xt = xtok[it % 2]
                    xT = xTs[it % 2]
                    it += 1
                    nb = (w + 127) // 128
                    pw = min(w, 128)
                    lt0 = t0 - off
                    if w >= 128:
                        fw.dma("sp" if it % 2 else "pool", xt.t[:, 0:nb, :],
                               src[lt0:lt0 + w, :].rearrange("(b p) d -> p b d", p=128), writes=[xt.r], stream="xtok")
                    else:
                        fw.dma("sp", xt.t[0:w, 0, :], src[lt0:lt0 + w, :], writes=[xt.r], stream="xtok")
                    for c in range(8):
                        pb = next_bank()
                        for tb in range(nb):
                            fw.op("pe", lambda e, c=c, tb=tb, pb=pb, xt=xt, pw=pw: e.transpose(
                                pb.t[:, tb * 128:tb * 128 + pw], xt.t[0:pw, tb, c * 128:(c + 1) * 128],
                                ident.t[0:pw, 0:pw]),
                                reads=[xt.r, ident.r], writes=[pb.r], signal=(tb == nb - 1))
                        eng = "act" if c % 2 else "dve"
                        if eng == "act":
                            fw.op("act", lambda e, c=c, pb=pb, xT=xT, w=w: e.activation(
                                out=xT.t[:, c, 0:w], in_=pb.t[:, 0:w], func=AF.Copy), reads=[pb.r], writes=[xT.rs[c]])
                        else:
                            fw.op("dve", lambda e, c=c, pb=pb, xT=xT, w=w: e.tensor_copy(
                                out=xT.t[:, c, 0:w], in_=pb.t[:, 0:w]), reads=[pb.r], writes=[xT.rs[c]])
                    store_xT(xT, t0, w, q="sp" if it % 2 else "pool")
            rec_pro = fw
            fw = fw_real
            default_pool[0] = (0, 1, 2, 3, 4, 5, 6, 7)
            merge_recs(fw, [rec_ada, rec_pro])
            fw.flush()
        PH0.close()

        for l in range(L):
            fw.rotate()
            if stage >= 2:
              with contextlib.ExitStack() as ph:
                sub = FWScope(fw, ph)
                default_pool[0] = (0, 1, 2, 3, 4, 5)
                WA = 512
                FC0 = RW_COLS
                TKMAX = max(SEQ, PAST + TS)
                NBMAX = (TKMAX + 127) // 128
                wf = Tt(sub.sbuf("wf", [128, 8, FOX_COLS], BF16), name="wf")
                for kc in range(8):
                    fw.dma("pool", wf.t[:, kc, :], I["w_in"][l][kc * 128:(kc + 1) * 128, FC0:FC0 + FOX_COLS],
                           writes=[wf.r], stream="wld", group=True)
                KT = Tt(sub.sbuf("KT", [128, 4, TKMAX], BF16), nreg=NBMAX, name="KT")
                Vx = Tt(sub.sbuf("Vx", [128, NBMAX, 8, 65], BF16), nreg=NBMAX, name="Vx")
                fw.op("pool", lambda e: e.memset(Vx.t[:], 1.0), writes=Vx.rs)
                ctok = Tt(sub.sbuf("ctok", [128, NBMAX, 8], F32), nreg=NBMAX, name="ctok")
                negc = Tt(sub.sbuf("negc", [128, NBMAX, 8], F32), nreg=NBMAX, name="negc")
                xTa = [Tt(sub.sbuf("xTa%d" % i, [128, 8, WA], F32), nreg=8, name="xTa%d" % i) for i in range(2)]
                sq = Tt(sub.sbuf("sqa", [128, 8, WA], BF16), name="sqa")
                hT = Tt(sub.sbuf("hTa", [128, 8, WA], BF16), nreg=8, name="hTa")
                tmp = [Tt(sub.sbuf("tmpa%d" % i, [128, WA], F32), name="tmpa%d" % i) for i in range(2)]
                lnv = Tt(sub.sbuf("lnva", [128, WA], F32), name="lnva")
                rstd = Tt(sub.sbuf("rstda", [128, WA], F32), name="rstda")
                QT = Tt(sub.sbuf("QT", [128, 4, WA], BF16), nreg=4, name="QT")
                ktok = [Tt(sub.sbuf("ktok%d" % i, [128, 512], F32), name="ktok%d" % i) for i in range(2)]
                vtok = [Tt(sub.sbuf("vtok%d" % i, [128, 512], F32), name="vtok%d" % i) for i in range(2)]
                ftok = [Tt(sub.sbuf("ftok%d" % i, [128, 8], F32), name="ftok%d" % i) for i in range(2)]
                ltok = [Tt(sub.sbuf("ltok%d" % i, [128, 8], F32), name="ltok%d" % i) for i in range(2)]
                cfm = Tt(sub.sbuf("cfm", [8, WA], F32), name="cfm")
                cr1 = Tt(sub.sbuf("cr1", [8, WA], F32), name="cr1")
                cr2 = Tt(sub.sbuf("cr2", [8, WA], F32), name="cr2")
                midt = Tt(sub.sbuf("midt", [8, WA], BF16), name="midt")
                cq96 = Tt(sub.sbuf("cq96", [72, WA], BF16), name="cq96")
                fw.op("pool", lambda e: e.memset(cq96.t[:], 0.0), writes=[cq96.r])
                pts = [Tt(sub.sbuf("pt%d" % i, [128, WA], BF16), name="pt%d" % i) for i in range(6)]
                rsf = Tt(sub.sbuf("rsf", [128, WA], F32), name="rsf")
                rcp = Tt(sub.sbuf("rcp", [64, WA], F32), name="rcp")
                yfT = Tt(sub.sbuf("yfT", [128, 4, WA], BF16), nreg=4, name="yfT")
                it = 0
                ik = 0
                ipt = 0
                a1_tiles = [(s_, t0_, w_) for s_ in range(3) for (t0_, w_, j_) in tiles_of(s_, WA)]
                a1_idx = [0]

                def a1_norm(idx):
                    s_, t0_, w_ = a1_tiles[idx]
                    xT_ = xTa[idx % 2]
                    load_xT(xT_, t0_, w_)
                    norm_fm(xT_, w_, sq, tmp, lnv, rstd, hT,
                            lambda c, s_=s_: G1.t[:, l, s_, c:c + 1], lambda c, s_=s_: mod.t[:, l, 0, s_, c:c + 1], hT.rs)
                for s in range(3):
                    off, T = seqs[s]
                    kbase = 0
                    if s == 2:
                        kbase = PAST
                        for cb in range(PAST // 128):
                            kt_ = ktok[ik % 2]
                            vt_ = vtok[ik % 2]
                            ft_ = ltok[ik % 2]
                            ik += 1
                            fw.dma("sp", kt_.t[:], I["ck"][l][cb * 128:(cb + 1) * 128, :], writes=[kt_.r], stream="cldk%d" % (ik % 2))
                            fw.dma("sp", vt_.t[:], I["cv"][l][cb * 128:(cb + 1) * 128, :], writes=[vt_.r], stream="cldv%d" % (ik % 2))
                            fw.dma("sp", ft_.t[:], I["cl"][l][cb * 128:(cb + 1) * 128, :], writes=[ft_.r], stream="cldf%d" % (ik % 2))
                            pb = next_bank()
                            for pc in range(4):
                                fw.op("pe", lambda e, pc=pc, pb=pb, kt_=kt_: e.transpose(
                                    pb.t[:, pc * 128:(pc + 1) * 128], kt_.t[:, pc * 128:(pc + 1) * 128], ident.t[:]),
                                    reads=[kt_.r, ident.r], writes=[pb.r], signal=(pc == 3))
                            fw.op("act", lambda e, pb=pb, cb=cb: e.activation(
                                out=KT.t[:, :, cb * 128:(cb + 1) * 128],
                                in_=pb.t[:, :].rearrange("p (c t) -> p c t", c=4), func=AF.Copy),
                                reads=[pb.r], writes=[KT.rs[cb]])
                            fw.op("dve", lambda e, vt_=vt_, cb=cb: e.tensor_copy(
                                out=Vx.t[:, cb, :, 0:64], in_=vt_.t[:, :].rearrange("p (h d) -> p h d", h=8)),
                                reads=[vt_.r], writes=[Vx.rs[cb]])
                            pc_ = next_bank()
                            fw.op("pe", lambda e, pc_=pc_, ft_=ft_, cb=cb: e.matmul(
                                pc_.t[:, 0:8], lhsT=trif.t[:], rhs=ft_.t[:], start=True, stop=(cb == 0)),
                                reads=[trif.r, ft_.r], writes=[pc_.r], signal=(cb == 0))
                            if cb > 0:
                                fw.op("pe", lambda e, pc_=pc_, cb=cb: e.matmul(
                                    pc_.t[:, 0:8], lhsT=self127.t[:], rhs=ctok.t[:, cb - 1, :], start=False, stop=True),
                                    reads=[self127.r, ctok.rs[cb - 1]], writes=[pc_.r])
                            fw.op("dve", lambda e, pc_=pc_, cb=cb: e.tensor_copy(out=ctok.t[:, cb, :], in_=pc_.t[:, 0:8]),
                                  reads=[pc_.r], writes=[ctok.rs[cb]])
                            fw.op("act", lambda e, pc_=pc_, cb=cb: e.activation(
                                out=negc.t[:, cb, :], in_=pc_.t[:, 0:8], func=AF.Copy, scale=-1.0),
                                reads=[pc_.r], writes=[negc.rs[cb]])
                    dK = O["fkp"][l][s] if s < 2 else O["fks"][l]
                    dV = O["fvp"][l][s] if s < 2 else O["fvs"][l]
                    dF = O["flp"][l][s] if s < 2 else O["fls"][l]
                    def a1_tile(s, t0, w, j, xT, off, kbase, dK, dV, dF, l=l):
                        nonlocal ik, ipt
                        lt0 = t0 - off
                        kt0 = kbase + lt0
                        nb = (w + 127) // 128
                        pw = min(w, 128)
                        if a1_idx[0] == 0:
                            a1_norm(0)
                        for pc in range(4):
                            pq = next_bank()
                            for kc in range(8):
                                fw.op("pe", lambda e, kc=kc, pc=pc, pq=pq, w=w: e.matmul(
                                    pq.t[:, 0:w], lhsT=wf.t[:, kc, pc * 128:(pc + 1) * 128], rhs=hT.t[:, kc, 0:w],
                                    start=(kc == 0), stop=(kc == 7)),
                                    reads=[wf.r, hT.rs[kc]], writes=[pq.r], signal=(kc == 7))
                            fw.op("act", lambda e, pc=pc, pq=pq, w=w: e.activation(
                                out=QT.t[:, pc, 0:w], in_=pq.t[:, 0:w], func=AF.Copy, scale=0.125),
                                reads=[pq.r], writes=[QT.rs[pc]])
                            pk = next_bank()
                            for kc in range(8):
                                fw.op("pe", lambda e, kc=kc, pc=pc, pk=pk, w=w: e.matmul(
                                    pk.t[:, 0:w], lhsT=wf.t[:, kc, 512 + pc * 128:512 + (pc + 1) * 128],
                                    rhs=hT.t[:, kc, 0:w], start=(kc == 0), stop=(kc == 7)),
                                    reads=[wf.r, hT.rs[kc]], writes=[pk.r], signal=(kc == 7))
                            kregs = [KT.rs[(kt0 + tb * 128) // 128] for tb in range(nb)]
                            fw.op("dve", lambda e, pc=pc, pk=pk, w=w, kt0=kt0: e.tensor_copy(
                                out=KT.t[:, pc, kt0:kt0 + w], in_=pk.t[:, 0:w]), reads=[pk.r], writes=kregs)
                        for tb in range(nb):
                            kb = (kt0 + tb * 128) // 128
                            kt_ = ktok[ik % 2]
                            vt_ = vtok[ik % 2]
                            ft_ = ftok[ik % 2]
                            lt_ = ltok[ik % 2]
                            ik += 1
                            for (dst_t, c0, ncol, dd, eng) in ((kt_, 512, 512, dK, "act"), (vt_, 1024, 512, dV, "dve")):
                                pb = next_bank()
                                for kc in range(8):
                                    fw.op("pe", lambda e, kc=kc, pb=pb, tb=tb, c0=c0, ncol=ncol, pw=pw: e.matmul(
                                        pb.t[0:pw, 0:ncol], lhsT=hT.t[:, kc, tb * 128:tb * 128 + pw],
                                        rhs=wf.t[:, kc, c0:c0 + ncol], start=(kc == 0), stop=(kc == 7)),
                                        reads=[wf.r, hT.rs[kc]], writes=[pb.r], signal=(kc == 7))
                                if eng == "act":
                                    fw.op("act", lambda e, pb=pb, dst_t=dst_t, pw=pw: e.activation(
                                        out=dst_t.t[0:pw, :], in_=pb.t[0:pw, :], func=AF.Copy),
                                        reads=[pb.r], writes=[dst_t.r])
                                else:
                                    fw.op("dve", lambda e, pb=pb, dst_t=dst_t, pw=pw: e.tensor_copy(
                                        out=dst_t.t[0:pw, :], in_=pb.t[0:pw, :]), reads=[pb.r], writes=[dst_t.r])
                                r0 = lt0 + tb * 128
                                fw.dma("sp", dd[r0:r0 + pw, :], dst_t.t[0:pw, :], reads=[dst_t.r],
                                       stream="kvo%s%d" % (eng[0], ik % 2))
                            fw.op("pool", lambda e, vt_=vt_, kb=kb, pw=pw: e.tensor_copy(
                                out=Vx.t[0:pw, kb, :, 0:64], in_=vt_.t[0:pw, :].rearrange("p (h d) -> p h d", h=8)),
                                reads=[vt_.r], writes=[Vx.rs[kb]])
                            pf = next_bank()
                            for kc in range(8):
                                fw.op("pe", lambda e, kc=kc, pf=pf, tb=tb, pw=pw: e.matmul(
                                    pf.t[0:pw, 0:8], lhsT=hT.t[:, kc, tb * 128:tb * 128 + pw],
                                    rhs=wf.t[:, kc, 1536:1544], start=(kc == 0), stop=(kc == 7)),
                                    reads=[wf.r, hT.rs[kc]], writes=[pf.r], signal=(kc == 7))
                            fw.op("dve", lambda e, pf=pf, ft_=ft_, pw=pw: e.tensor_tensor(
                                out=ft_.t[0:pw, :], in0=pf.t[0:pw, 0:8], in1=bfb.t[0:pw, l, :], op=ALU.add),
                                reads=[pf.r, bfb.r], writes=[ft_.r])
                            fw.op("act", lambda e, ft_=ft_, pw=pw: e.activation(
                                out=ft_.t[0:pw, :], in_=ft_.t[0:pw, :], func=AF.Exp, scale=-1.0),
                                reads=[ft_.r], writes=[ft_.r])
                            fw.op("act", lambda e, ft_=ft_, pw=pw: e.activation(
                                out=ft_.t[0:pw, :], in_=ft_.t[0:pw, :], func=AF.Ln, bias=1.0),
                                reads=[ft_.r], writes=[ft_.r])
                            fw.op("dve", lambda e, ft_=ft_, lt_=lt_, pw=pw: e.tensor_scalar_mul(
                                out=lt_.t[0:pw, :], in0=ft_.t[0:pw, :], scalar1=-1.0), reads=[ft_.r], writes=[lt_.r])
                            r0 = lt0 + tb * 128
                            fw.dma("sp", dF[r0:r0 + pw, :], lt_.t[0:pw, :], reads=[lt_.r], stream="kvof%d" % (ik % 2))
                            pc_ = next_bank()
                            first = (kb == 0)
                            fw.op("pe", lambda e, pc_=pc_, lt_=lt_, pw=pw, first=first: e.matmul(
                                pc_.t[0:pw, 0:8], lhsT=trif.t[0:pw, 0:pw], rhs=lt_.t[0:pw, :], start=True, stop=first),
                                reads=[trif.r, lt_.r], writes=[pc_.r], signal=first)
                            if not first:
                                fw.op("pe", lambda e, pc_=pc_, kb=kb, pw=pw: e.matmul(
                                    pc_.t[0:pw, 0:8], lhsT=self127.t[:, 0:pw], rhs=ctok.t[:, kb - 1, :],
                                    start=False, stop=True),
                                    reads=[self127.r, ctok.rs[kb - 1]], writes=[pc_.r])
                            fw.op("dve", lambda e, pc_=pc_, kb=kb, pw=pw: e.tensor_copy(
                                out=ctok.t[0:pw, kb, :], in_=pc_.t[0:pw, 0:8]), reads=[pc_.r], writes=[ctok.rs[kb]])
                            fw.op("act", lambda e, pc_=pc_, kb=kb, pw=pw: e.activation(
                                out=negc.t[0:pw, kb, :], in_=pc_.t[0:pw, 0:8], func=AF.Copy, scale=-1.0),
                                reads=[pc_.r], writes=[negc.rs[kb]])
                            pt_ = next_bank()
                            fw.op("pe", lambda e, pt_=pt_, kb=kb, pw=pw: e.transpose(
                                pt_.t[0:8, 0:pw], ctok.t[0:pw, kb, :], ident.t[0:pw, 0:pw]),
                                reads=[ctok.rs[kb], ident.r], writes=[pt_.r])
                            fw.op("dve", lambda e, pt_=pt_, tb=tb, pw=pw: e.tensor_copy(
                                out=cfm.t[:, tb * 128:tb * 128 + pw], in_=pt_.t[0:8, 0:pw]),
                                reads=[pt_.r], writes=[cfm.r])
                        fw.op("act", lambda e, w=w: e.activation(out=cq96.t[0:8, 0:w], in_=cfm.t[:, 0:w], func=AF.Copy),
                              reads=[cfm.r], writes=[cq96.r])
                        fw.op("dve", lambda e, w=w: e.tensor_tensor(out=cr1.t[:, 0:w], in0=cfm.t[:, 0:w],
                                                                    in1=cq96.t[0:8, 0:w], op=ALU.subtract),
                              reads=[cfm.r, cq96.r], writes=[cr1.r])
                        fw.op("act", lambda e, w=w: e.activation(out=midt.t[:, 0:w], in_=cr1.t[:, 0:w], func=AF.Copy),
                              reads=[cr1.r], writes=[midt.r])
                        fw.op("pool", lambda e, w=w: e.tensor_copy(out=cq96.t[32:40, 0:w], in_=midt.t[:, 0:w]),
                              reads=[midt.r], writes=[cq96.r])
                        fw.op("dve", lambda e, w=w: e.tensor_tensor(out=cr2.t[:, 0:w], in0=cr1.t[:, 0:w],
                                                                    in1=midt.t[:, 0:w], op=ALU.subtract),
                              reads=[cr1.r, midt.r], writes=[cr2.r])
                        fw.op("act", lambda e, w=w: e.activation(out=cq96.t[64:72, 0:w], in_=cr2.t[:, 0:w], func=AF.Copy),
                              reads=[cr2.r], writes=[cq96.r])
                        a1_idx[0] += 1
                        if a1_idx[0] < len(a1_tiles):
                            a1_norm(a1_idx[0])
                        kb_first_tile = kt0 // 128
                        nkb = kb_first_tile + nb
                        pending_epi = []
                        for h in range(8):
                            hr = slice((h % 2) * 64, (h % 2) * 64 + 64)
                            hp = h // 2
                            ob = banks[6 + (h % 2)]
                            blocks = []
                            for kb in range(nkb):
                                if kb < kb_first_tile:
                                    q0, rows, diag = 0, 128, False
                                else:
                                    q0, rows, diag = (kb - kb_first_tile) * 128, pw, True
                                blocks.append((kb, q0, rows, diag))
                            sbanks = {}
                            ptl = {}

                            def emit_s(bi, h=h, hr=hr, hp=hp):
                                kb, q0, rows, diag = blocks[bi]
                                sb = next_bank(pool=(0, 1, 2, 3, 4))
                                sbanks[bi] = sb
                                fw.op("pe", lambda e, sb=sb, kb=kb, q0=q0, rows=rows: e.matmul(
                                    sb.t[0:rows, q0:w], lhsT=KT.t[hr, hp, kb * 128:kb * 128 + rows],
                                    rhs=QT.t[hr, hp, q0:w], start=True, stop=False),
                                    reads=[KT.rs[kb], QT.rs[hp]], writes=[sb.r], signal=False)
                                fw.op("pe", lambda e, sb=sb, q0=q0, rows=rows: e.matmul(
                                    sb.t[0:rows, q0:w], lhsT=selh.t[0:72, h, 0:rows], rhs=cq96.t[0:72, q0:w],
                                    start=False, stop=(not diag)),
                                    reads=[selh.r, cq96.r], writes=[sb.r], signal=(not diag))
                                if diag:
                                    fw.op("pe", lambda e, sb=sb, q0=q0, rows=rows: e.matmul(
                                        sb.t[0:rows, q0:q0 + rows], lhsT=identb.t[0:rows, 0:rows],
                                        rhs=maskneg.t[0:rows, 0:rows], start=False, stop=True),
                                        reads=[identb.r, maskneg.r], writes=[sb.r])

                            def emit_pv(bi, h=h, ob=ob):
                                nonlocal ipt
                                kb, q0, rows, diag = blocks[bi]
                                sb = sbanks.pop(bi)
                                pt = pts[ipt % 6]
                                ipt += 1
                                fw.op("act", lambda e, sb=sb, pt=pt, kb=kb, q0=q0, rows=rows: e.activation(
                                    out=pt.t[0:rows, q0:w], in_=sb.t[0:rows, q0:w], func=AF.Exp,
                                    bias=negc.t[0:rows, kb, h:h + 1]),
                                    reads=[sb.r, negc.rs[kb]], writes=[pt.r])
                                last = (bi == len(blocks) - 1)
                                fw.op("pe", lambda e, pt=pt, kb=kb, q0=q0, rows=rows, bi=bi, last=last: e.matmul(
                                    ob.t[0:65, q0:w], lhsT=Vx.t[0:rows, kb, h, :], rhs=pt.t[0:rows, q0:w],
                                    start=(bi == 0), stop=last),
                                    reads=[Vx.rs[kb], pt.r], writes=[ob.r], signal=last)
                            LOOK = 4
                            nbk = len(blocks)
                            for bi in range(min(LOOK, nbk)):
                                emit_s(bi)
                            for bi in range(nbk):
                                emit_pv(bi)
                                if bi + LOOK < nbk:
                                    emit_s(bi + LOOK)
                            def epilogue(ob=ob, hr=hr, hp=hp):
                                fw.op("act", lambda e, ob=ob, w=w: e.activation(
                                    out=rsf.t[64:65, 0:w], in_=ob.t[64:65, 0:w], func=AF.Copy), reads=[ob.r], writes=[rsf.r])
                                pr = next_bank(pool=(5,))
                                fw.op("pe", lambda e, pr=pr, w=w: e.matmul(
                                    pr.t[0:64, 0:w], lhsT=onesf.t[64:65, 0:64], rhs=rsf.t[64:65, 0:w], start=True, stop=True),
                                    reads=[onesf.r, rsf.r], writes=[pr.r])
                                fw.op("dve", lambda e, pr=pr, w=w: e.reciprocal(out=rcp.t[:, 0:w], in_=pr.t[0:64, 0:w]),
                                      reads=[pr.r], writes=[rcp.r])
                                fw.op("dve", lambda e, ob=ob, hr=hr, hp=hp, w=w: e.tensor_tensor(
                                    out=yfT.t[hr, hp, 0:w], in0=ob.t[0:64, 0:w], in1=rcp.t[:, 0:w], op=ALU.mult),
                                    reads=[ob.r, rcp.r], writes=[yfT.rs[hp]])
                            if pending_epi:
                                pending_epi.pop()()
                            pending_epi.append(epilogue)
                        while pending_epi:
                            pending_epi.pop()()
                        fw.dma("sp", yfox.rearrange("(c p) t -> p c t", p=128)[:, :, t0:t0 + w], yfT.t[:, :, 0:w],
                               reads=yfT.rs, writes=[yfreg(t0)], stream="yfst")
                    for (t0, w, j) in tiles_of(s, WA):
                        xT = xTa[it % 2]
                        it += 1
                        a1_tile(s, t0, w, j, xT, off, kbase, dK, dV, dF)
                fw.flush()
                default_pool[0] = (0, 1, 2, 3, 4, 5, 6, 7)
            if stage >= 2:
              with contextlib.ExitStack() as ph:
                sub = FWScope(fw, ph)
                default_pool[0] = (0, 1, 2, 3, 4)
                sink = [fw]
                WA = 256
                RW0, HG0 = 0, RW_COLS + FOX_COLS
                NWI = RW_COLS + HG_COLS
                wi = Tt(sub.sbuf("wi", [128, 8, NWI], BF16), name="wi")
                wo = Tt(sub.sbuf("wo", [128, 8, D], BF16), name="wo")
                for kc in range(8):
                    fw.dma("pool", wi.t[:, kc, 0:RW_COLS], I["w_in"][l][kc * 128:(kc + 1) * 128, 0:RW_COLS],
                           writes=[wi.r], stream="wld", group=True)
                    fw.dma("pool", wi.t[:, kc, RW_COLS:NWI], I["w_in"][l][kc * 128:(kc + 1) * 128, HG0:HG0 + HG_COLS],
                           writes=[wi.r], stream="wld", group=True)
                    fw.dma("pool", wo.t[:, kc, :], I["w_out"][l][kc * 128:(kc + 1) * 128, :], writes=[wo.r], stream="wld", group=True)
                xTa = [Tt(sub.sbuf("xTa%d" % i, [128, 8, WA], F32), nreg=8, name="xTa%d" % i) for i in range(2)]
                sq = Tt(sub.sbuf("sqa", [128, 8, WA], BF16), name="sqa")
                hT = Tt(sub.sbuf("hTa", [128, 8, WA], BF16), nreg=8, name="hTa")
                tmp = [Tt(sub.sbuf("tmpa%d" % i, [128, WA], F32), name="tmpa%d" % i) for i in range(2)]
                lnv = Tt(sub.sbuf("lnva", [128, WA], F32), name="lnva")
                rstd = Tt(sub.sbuf("rstda", [128, WA], F32), name="rstda")
                ymix = Tt(sub.sbuf("ymix", [128, 8, WA], BF16), nreg=8, name="ymix")
                fw.op("pool", lambda e: e.memset(ymix.t[:], 0.0), writes=ymix.rs)

                def S(name, shape=None, dt=F32, nreg=1):
                    return Tt(sub.sbuf(name, shape or [128, WA], dt), nreg=nreg, name=name)

                def proj_fm(c0, w, M=128):
                    pb = next_bank()
                    for kc in range(8):
                        sink[0].op("pe", lambda e, kc=kc: e.matmul(
                            pb.t[0:M, 0:w], lhsT=wi.t[:, kc, c0:c0 + M], rhs=hT.t[:, kc, 0:w],
                            start=(kc == 0), stop=(kc == 7)),
                            reads=[wi.r, hT.rs[kc]], writes=[pb.r], signal=(kc == 7))
                    return pb

                NCH = WA // 32
                hg_E = [S("hg_E%d" % i) for i in range(2)]
                hg_KK = [S("hg_KK%d" % i) for i in range(2)]
                hg_B = [S("hg_B%d" % i) for i in range(2)]
                hg_D = S("hg_D")
                hg_X = S("hg_X")
                hg_Q = S("hg_Q")
                hg_G = [S("hg_G%d" % i) for i in range(2)]
                hg_Qt = S("hg_Qt", [128, 2, WA], BF16, nreg=2)
                hg_Kh = S("hg_Kh", [128, 2, WA], BF16, nreg=2)
                hg_Ke = S("hg_Ke", [128, 2, WA], BF16, nreg=2)
                hg_ebl = S("hg_ebl", [128, 2, NCH], F32, nreg=2)
                hg_ebm = S("hg_ebm", [128, 2, NCH], F32, nreg=2)
                hg_Vh = S("hg_Vh", [128, WA // 128, 256], BF16, nreg=4)
                hg_KeT = S("hg_KeT", [128, WA // 128, 4, 256], BF16, nreg=4)
                hg_AT = S("hg_AT", [128, WA // 128, 2, 2, 128], BF16, nreg=4)
                hg_Sm = S("hg_Sm", [128, 2, 128], F32, nreg=2)
                hg_Sbd = S("hg_Sbd", [128, 2, 128], BF16, nreg=2)
                hg_sq = S("hg_sq", [128, WA], BF16)
                hg_t1 = S("hg_t1")

                def hg_init(s):
                    fw.op("pool", lambda e: e.memset(hg_Sm.t[:], 0.0), writes=hg_Sm.rs)
                    if s == 2:
                        for h in range(4):
                            hr = slice((h % 2) * 64, (h % 2) * 64 + 64)
                            fw.dma("sp", hg_Sm.t[hr, h // 2, (h % 2) * 64:(h % 2) * 64 + 64], I["shg"][l][h],
                                   writes=[hg_Sm.rs[h // 2]], stream="stld", group=True)

                def hg_final(s):
                    dst = O["hgp"][l][s] if s < 2 else O["hgs"][l]
                    for h in range(4):
                        hr = slice((h % 2) * 64, (h % 2) * 64 + 64)
                        fw.dma("sp", dst[h], hg_Sm.t[hr, h // 2, (h % 2) * 64:(h % 2) * 64 + 64],
                               reads=[hg_Sm.rs[h // 2]], stream="ststhg%d" % s, group=True)

                def hg_tile(s, w):
                    nch = w // 32
                    nb = (w + 127) // 128
                    pw = min(w, 128)
                    c_q, c_f, c_i, c_g = RW_COLS, RW_COLS + 256, RW_COLS + 512, RW_COLS + 768
                    for tb in range(nb):
                        pb = next_bank()
                        for kc in range(8):
                            sink[0].op("pe", lambda e, kc=kc, tb=tb, pb=pb: e.matmul(
                                pb.t[0:pw, 0:256], lhsT=hT.t[:, kc, tb * 128:tb * 128 + pw], rhs=wi.t[:, kc, c_i:c_i + 256],
                                start=(kc == 0), stop=(kc == 7)),
                                reads=[wi.r, hT.rs[kc]], writes=[pb.r], signal=(kc == 7))
                        sink[0].op("act", lambda e, tb=tb, pb=pb: e.activation(
                            out=hg_Vh.t[0:pw, tb, :], in_=pb.t[0:pw, 0:256], func=AF.Copy),
                            reads=[pb.r], writes=[hg_Vh.rs[tb]])
                    for pc in range(2):
                        E, KK, B, G = hg_E[pc], hg_KK[pc], hg_B[pc], hg_G[pc]
                        lb_ap = lbT.t[:, l, pc:pc + 1]
                        oml_ap = omlT.t[:, l, pc:pc + 1]
                        noml_ap = nomlT.t[:, l, pc:pc + 1]
                        pf = proj_fm(c_f + pc * 128, w)
                        sink[0].op("act", lambda e, pf=pf, E=E: e.activation(out=E.t[:, 0:w], in_=pf.t[:, 0:w], func=AF.Exp,
                                                                      scale=-1.0), reads=[pf.r], writes=[E.r])
                        sink[0].op("dve", lambda e, E=E: e.tensor_scalar_add(out=E.t[:, 0:w], in0=E.t[:, 0:w], scalar1=1.0),
                              reads=[E.r], writes=[E.r])
                        sink[0].op("dve", lambda e, E=E: e.reciprocal(out=E.t[:, 0:w], in_=E.t[:, 0:w]),
                              reads=[E.r], writes=[E.r])
                        sink[0].op("dve", lambda e, E=E, KK=KK, noml_ap=noml_ap, oml_ap=oml_ap: e.tensor_scalar(
                            out=KK.t[:, 0:w], in0=E.t[:, 0:w], scalar1=noml_ap, scalar2=oml_ap, op0=ALU.mult, op1=ALU.add),
                            reads=[E.r, nomlT.r, omlT.r], writes=[KK.r])
                        sink[0].op("act", lambda e, E=E, oml_ap=oml_ap, lb_ap=lb_ap: e.activation(
                            out=E.t[:, 0:w], in_=E.t[:, 0:w], func=AF.Ln, scale=oml_ap, bias=lb_ap),
                            reads=[E.r, omlT.r, lbT.r], writes=[E.r])
                        sink[0].op("dve", lambda e, E=E, B=B: e.tensor_tensor_scan(
                            out=B.t[:, 0:w], data0=rmask32.t[:, 0:w], data1=E.t[:, 0:w], initial=0.0,
                            op0=ALU.mult, op1=ALU.add), reads=[E.r, rmask32.r], writes=[B.r])
                        Bv = B.t[:, 0:w].rearrange("p (c t) -> p c t", t=32)
                        Dv = hg_D.t[:, 0:w].rearrange("p (c t) -> p c t", t=32)
                        sink[0].op("dve", lambda e, Bv=Bv, Dv=Dv: e.tensor_tensor(
                            out=Dv, in0=Bv, in1=Bv[:, :, 15:16].to_broadcast([128, nch, 32]), op=ALU.subtract),
                            reads=[B.r], writes=[hg_D.r])
                        pq = proj_fm(c_q + pc * 128, w)
                        sink[0].op("act", lambda e, pq=pq: e.activation(out=hg_Q.t[:, 0:w], in_=pq.t[:, 0:w], func=AF.Copy),
                              reads=[pq.r], writes=[hg_Q.r])
                        sink[0].op("act", lambda e: e.activation(out=hg_X.t[:, 0:w], in_=hg_D.t[:, 0:w], func=AF.Exp),
                              reads=[hg_D.r], writes=[hg_X.r])
                        sink[0].op("dve", lambda e, pc=pc: e.tensor_tensor(out=hg_Qt.t[:, pc, 0:w], in0=hg_Q.t[:, 0:w],
                                                                      in1=hg_X.t[:, 0:w], op=ALU.mult),
                              reads=[hg_Q.r, hg_X.r], writes=[hg_Qt.rs[pc]])
                        sink[0].op("act", lambda e: e.activation(out=hg_X.t[:, 0:w], in_=hg_D.t[:, 0:w], func=AF.Exp, scale=-1.0),
                              reads=[hg_D.r], writes=[hg_X.r])
                        sink[0].op("dve", lambda e, pc=pc, KK=KK: e.tensor_tensor(out=hg_Kh.t[:, pc, 0:w], in0=KK.t[:, 0:w],
                                                                             in1=hg_X.t[:, 0:w], op=ALU.mult),
                              reads=[KK.r, hg_X.r], writes=[hg_Kh.rs[pc]])
                        sink[0].op("dve", lambda e, Bv=Bv, Dv=Dv: e.tensor_tensor(
                            out=Dv, in0=Bv[:, :, 31:32].to_broadcast([128, nch, 32]), in1=Bv, op=ALU.subtract),
                            reads=[B.r], writes=[hg_D.r])
                        sink[0].op("act", lambda e: e.activation(out=hg_X.t[:, 0:w], in_=hg_D.t[:, 0:w], func=AF.Exp),
                              reads=[hg_D.r], writes=[hg_X.r])
                        sink[0].op("dve", lambda e, pc=pc, KK=KK: e.tensor_tensor(out=hg_Ke.t[:, pc, 0:w], in0=KK.t[:, 0:w],
                                                                             in1=hg_X.t[:, 0:w], op=ALU.mult),
                              reads=[KK.r, hg_X.r], writes=[hg_Ke.rs[pc]])
                        sink[0].op("act", lambda e, pc=pc, Bv=Bv: e.activation(out=hg_ebl.t[:, pc, 0:nch], in_=Bv[:, :, 31],
                                                                          func=AF.Exp), reads=[B.r], writes=[hg_ebl.rs[pc]])
                        sink[0].op("act", lambda e, pc=pc, Bv=Bv: e.activation(out=hg_ebm.t[:, pc, 0:nch], in_=Bv[:, :, 15],
                                                                          func=AF.Exp), reads=[B.r], writes=[hg_ebm.rs[pc]])
                        pg = proj_fm(c_g + pc * 128, w)
                        sink[0].op("act", lambda e, pg=pg, G=G: e.activation(out=G.t[:, 0:w], in_=pg.t[:, 0:w], func=AF.Silu),
                              reads=[pg.r], writes=[G.r])
                    HGDBG = int(os.environ.get("HGDBG", "9"))
                    if HGDBG < 2:
                        return
                    for tb in range(nb):
                        pAs = [next_bank(), next_bank()]
                        for h in range(4):
                            hr = slice((h % 2) * 64, (h % 2) * 64 + 64)
                            pA = pAs[h % 2]
                            sink[0].op("pe", lambda e, h=h, hr=hr, tb=tb, pA=pA: e.matmul(
                                pA.t[0:pw, (h // 2) * 128:(h // 2) * 128 + pw], lhsT=hg_Kh.t[hr, h // 2, tb * 128:tb * 128 + pw],
                                rhs=hg_Qt.t[hr, h // 2, tb * 128:tb * 128 + pw], start=True, stop=True),
                                reads=[hg_Kh.rs[h // 2], hg_Qt.rs[h // 2]], writes=[pA.r], signal=(h >= 2))
                        for par in range(2):
                            pA = pAs[par]
                            sink[0].op("dve", lambda e, tb=tb, pA=pA, par=par: e.tensor_tensor(
                                out=hg_AT.t[0:pw, tb, par, :, 0:pw],
                                in0=pA.t[0:pw, 0:256].rearrange("p (h t) -> p h t", h=2)[:, :, 0:pw],
                                in1=maskbd.t[0:pw, 0:pw].unsqueeze(1).to_broadcast([pw, 2, pw]), op=ALU.mult),
                                reads=[pA.r, maskbd.r], writes=[hg_AT.rs[tb]])
                        if os.environ.get("HGSUB", "") == "A":
                            continue
                        pT = next_bank()
                        pTb = pT.t[:, :].bitcast(BF16)
                        for pc in range(2):
                            sink[0].op("pe", lambda e, pc=pc, tb=tb, pTb=pTb: e.transpose(
                                pTb[0:pw, pc * 128:(pc + 1) * 128], hg_Ke.t[:, pc, tb * 128:tb * 128 + pw], identb.t[:, :]),
                                reads=[hg_Ke.rs[pc], identb.r], writes=[pT.r], signal=(pc == 1))
                        for cc in range(min(4, nch - tb * 4)):
                            sink[0].op("act", lambda e, tb=tb, pTb=pTb, cc=cc: e.activation(
                                out=hg_KeT.t[0:pw, tb, cc, :], in_=pTb[0:pw, 0:256], func=AF.Identity,
                                scale=maskbd.t[0:pw, cc * 32 + 31:cc * 32 + 32]),
                                reads=[pT.r, maskbd.r], writes=[hg_KeT.rs[tb]])
                    if HGDBG < 3:
                        return
                    po = [banks[6], banks[7]]
                    for tb in range(nb):
                        for h in range(4):
                            sink[0].op("pe", lambda e, h=h, tb=tb: e.matmul(
                                po[h // 2].t[(h % 2) * 64:(h % 2) * 64 + 64, tb * 128:tb * 128 + pw],
                                lhsT=hg_Vh.t[0:pw, tb, h * 64:(h + 1) * 64], rhs=hg_AT.t[0:pw, tb, h % 2, h // 2, 0:pw],
                                start=True, stop=False, skip_group_check=True),
                                reads=[hg_Vh.rs[tb], hg_AT.rs[tb]], writes=[po[h // 2].r], signal=False)
                        for cc in range(min(4, nch - tb * 4)):
                            c = tb * 4 + cc
                            last = (c == nch - 1)
                            for pc in range(2):
                                sink[0].op("act", lambda e, pc=pc, c=c: e.activation(
                                    out=hg_Sbd.t[:, pc, :], in_=hg_Sm.t[:, pc, :], func=AF.Identity,
                                    scale=hg_ebm.t[:, pc, c:c + 1]),
                                    reads=[hg_Sm.rs[pc], hg_ebm.rs[pc]], writes=[hg_Sbd.rs[pc]])
                                sink[0].op("pe", lambda e, pc=pc, c=c: e.matmul(
                                    po[pc].t[:, c * 32:(c + 1) * 32], lhsT=hg_Sbd.t[:, pc, :],
                                    rhs=hg_Qt.t[:, pc, c * 32:(c + 1) * 32], start=False, stop=True,
                                    skip_group_check=True),
                                    reads=[hg_Sbd.rs[pc], hg_Qt.rs[pc]], writes=[po[pc].r], signal=last)
                                pS = next_bank()
                                sink[0].op("pe", lambda e, pc=pc, tb=tb, cc=cc, pS=pS: e.matmul(
                                    pS.t[:, 0:128], lhsT=hg_KeT.t[0:pw, tb, cc, pc * 128:(pc + 1) * 128],
                                    rhs=hg_Vh.t[0:pw, tb, pc * 128:(pc + 1) * 128], start=True, stop=True),
                                    reads=[hg_KeT.rs[tb], hg_Vh.rs[tb]], writes=[pS.r])
                                for hh in range(2):
                                    hr = slice(hh * 64, hh * 64 + 64)
                                    sink[0].op("dve", lambda e, pc=pc, c=c, hr=hr, pS=pS: e.scalar_tensor_tensor(
                                        out=hg_Sm.t[hr, pc, hr], in0=hg_Sm.t[hr, pc, hr], scalar=hg_ebl.t[hr, pc, c:c + 1],
                                        in1=pS.t[hr, hr], op0=ALU.mult, op1=ALU.add),
                                        reads=[hg_Sm.rs[pc], hg_ebl.rs[pc], pS.r], writes=[hg_Sm.rs[pc]])
                    if HGDBG < 4:
                        return
                    for pc in range(2):
                        G = hg_G[pc]
                        sink[0].op("act", lambda e, pc=pc: e.activation(out=hg_sq.t[:, 0:w], in_=po[pc].t[:, 0:w], func=AF.Square),
                              reads=[po[pc].r], writes=[hg_sq.r])
                        pn = next_bank()
                        sink[0].op("pe", lambda e, pn=pn: e.matmul(pn.t[:, 0:w], lhsT=onesbd.t[:], rhs=hg_sq.t[:, 0:w],
                                                              start=True, stop=True),
                              reads=[onesbd.r, hg_sq.r], writes=[pn.r])
                        sink[0].op("act", lambda e, pn=pn: e.activation(out=hg_X.t[:, 0:w], in_=pn.t[:, 0:w], func=AF.Ln,
                                                                   scale=1.0 / 64, bias=epsb.t[:]),
                              reads=[pn.r, epsb.r], writes=[hg_X.r])
                        sink[0].op("act", lambda e: e.activation(out=hg_X.t[:, 0:w], in_=hg_X.t[:, 0:w], func=AF.Exp, scale=-0.5),
                              reads=[hg_X.r], writes=[hg_X.r])
                        sink[0].op("dve", lambda e, pc=pc: e.tensor_tensor(out=hg_t1.t[:, 0:w], in0=po[pc].t[:, 0:w],
                                                                      in1=hg_X.t[:, 0:w], op=ALU.mult),
                              reads=[po[pc].r, hg_X.r], writes=[hg_t1.r])
                        sink[0].op("dve", lambda e, pc=pc, G=G: e.scalar_tensor_tensor(
                            out=ymix.t[:, 6 + pc, 0:w], in0=hg_t1.t[:, 0:w], scalar=hgng.t[:, l, pc:pc + 1], in1=G.t[:, 0:w],
                            op0=ALU.mult, op1=ALU.mult), reads=[hg_t1.r, hgng.r, G.r], writes=[ymix.rs[6 + pc]])

                C0 = 0.6065306597126334
                rw_Pb = S("rw_Pb", [128, 9, WA + 1], F32)
                rw_car = S("rw_car", [128, 9], F32)
                rw_c7 = S("rw_c7", [128, 7], F32)
                rw_tmp = [S("rw_tmp%d" % i) for i in range(6)]
                rw_tmpB = [S("rw_tmpB%d" % i) for i in range(3)]
                rw_SIG2 = [S("rw_SIG%d" % i) for i in range(2)]
                rw_A2 = [S("rw_A%d" % i) for i in range(2)]
                rw_L2 = [S("rw_L%d" % i) for i in range(2)]
                rw_KP2 = [S("rw_KP%d" % i) for i in range(2)]
                rw_KN2 = [S("rw_KN%d" % i) for i in range(2)]
                rw_Bf2 = [S("rw_Bf%d" % i) for i in range(2)]
                rw_sqb2 = [S("rw_sqbb%d" % i, [128, WA], BF16) for i in range(2)]
                rw_G = [S("rw_G%d" % i) for i in range(2)]
                rw_Yf = S("rw_Yf")
                rw_TW = S("rw_TW", [32, WA], BF16)
                rw_AL = S("rw_AL", [32, WA], BF16)
                rw_SG = S("rw_SG", [64, WA], BF16)
                rw_sqb = S("rw_sqb", [128, WA], BF16)
                rw_Kh = S("rw_Kh", [128, 2, WA], BF16, nreg=2)
                rw_Bm = S("rw_Bm", [128, 2, 2, WA], BF16, nreg=2)
                rw_QRm = S("rw_QRm", [128, 2, 2, max(1, WA // 64), 2, 64], BF16, nreg=2)
                rw_Ke = S("rw_Ke", [128, 2, WA], BF16, nreg=2)
                rw_Be = S("rw_Be", [128, 2, WA], BF16, nreg=2)
                rw_Vb = S("rw_Vb", [128, 2, WA], BF16, nreg=2)
                NCR = max(1, WA // 64)
                rw_Vt = S("rw_Vt", [64, NCR, 256], BF16)
                rw_KeT = S("rw_KeT", [64, NCR, 256], BF16)
                rw_BeT = S("rw_BeT", [64, NCR, 256], BF16)
                rw_gC = S("rw_gC", [128, 2, NCR], F32, nreg=2)
                NCK = max(1, WA // 64)
                rw_AT12 = [S("rw_AT12_%d" % i, [64, 4, 2, 64], BF16) for i in range(NCK)]
                rw_AT34 = [S("rw_AT34_%d" % i, [64, 4, 2, 64], BF16) for i in range(NCK)]
                rw_X = [[S("rw_X%d_%d" % (i, j), [64, 4, 64], BF16) for j in range(2)] for i in range(NCK)]
                rw_XT = [[S("rw_XT%d_%d" % (i, j), [64, 4, 64], BF16) for j in range(2)] for i in range(NCK)]
                rw_TT = [[S("rw_TT%d_%d" % (i, j), [64, 4, 64], BF16) for j in range(2)] for i in range(NCK)]
                rw_Zb = S("rw_Zb", [64, 256], BF16)
                rw_Un = S("rw_Un", [64, 256], BF16)
                rw_Hm = S("rw_Hm", [128, 2, 128], F32, nreg=2)
                rw_Hbd = S("rw_Hbd", [128, 2, 128], BF16, nreg=2)
                rw_st = S("rw_st", [128, 2, 128], F32)

                def rw_init(s):
                    fw.op("pool", lambda e: e.memset(rw_Hm.t[:], 0.0), writes=rw_Hm.rs)
                    fw.op("pool", lambda e: e.memset(rw_Pb.t[:], 0.0), writes=[rw_Pb.r])
                    if s == 2:
                        fw.op("pool", lambda e: e.memset(rw_st.t[:], 0.0), writes=[rw_st.r])
                        for h in range(4):
                            hr = slice((h % 2) * 64, (h % 2) * 64 + 64)
                            fw.dma("sp", rw_st.t[hr, h // 2, (h % 2) * 64:(h % 2) * 64 + 64], I["srw"][l][h],
                                   writes=[rw_st.r], stream="stld", group=True)
                        for pc in range(2):
                            pb = next_bank()
                            fw.op("pe", lambda e, pc=pc, pb=pb: e.transpose(pb.t[:, 0:128], rw_st.t[:, pc, :], ident.t[:]),
                                  reads=[rw_st.r, ident.r], writes=[pb.r])
                            fw.op("dve", lambda e, pc=pc, pb=pb: e.tensor_copy(out=rw_Hm.t[:, pc, :], in_=pb.t[:, 0:128]),
                                  reads=[pb.r], writes=[rw_Hm.rs[pc]])
                        fw.dma("sp", rw_c7.t[:, :], I["ssh"][l].rearrange("(c p) -> p c", p=128),
                               writes=[rw_c7.r], stream="stld", group=True, allow_slow_non_contiguous=True)
                        fw.op("dve", lambda e: e.tensor_copy(out=rw_Pb.t[:, 0:6, 0], in_=rw_c7.t[:, 0:6]),
                              reads=[rw_c7.r], writes=[rw_Pb.r])
                        fw.op("dve", lambda e: e.tensor_copy(out=rw_Pb.t[0:32, 6, 0:1], in_=rw_c7.t[0:32, 6:7]),
                              reads=[rw_c7.r], writes=[rw_Pb.r])
                        fw.op("dve", lambda e: e.tensor_copy(out=rw_Pb.t[0:32, 7, 0:1], in_=rw_c7.t[32:64, 6:7]),
                              reads=[rw_c7.r], writes=[rw_Pb.r])
                        fw.op("dve", lambda e: e.tensor_copy(out=rw_Pb.t[0:64, 8, 0:1], in_=rw_c7.t[64:128, 6:7]),
                              reads=[rw_c7.r], writes=[rw_Pb.r])
                    for pc in range(2):
                        fw.op("act", lambda e, pc=pc: e.activation(out=rw_Hbd.t[:, pc, :], in_=rw_Hm.t[:, pc, :],
                                                                   func=AF.Copy), reads=[rw_Hm.rs[pc]], writes=[rw_Hbd.rs[pc]])

                def rw_final(s, w):
                    dst = O["rwp"][l][s] if s < 2 else O["rws"][l]
                    for pc in range(2):
                        pb = next_bank()
                        fw.op("pe", lambda e, pc=pc, pb=pb: e.transpose(pb.t[:, 0:128], rw_Hm.t[:, pc, :], ident.t[:]),
                              reads=[rw_Hm.rs[pc], ident.r], writes=[pb.r])
                        fw.op("dve", lambda e, pc=pc, pb=pb: e.tensor_copy(out=rw_st.t[:, pc, :], in_=pb.t[:, 0:128]),
                              reads=[pb.r], writes=[rw_st.r])
                    for h in range(4):
                        hr = slice((h % 2) * 64, (h % 2) * 64 + 64)
                        fw.dma("sp", dst[h], rw_st.t[hr, h // 2, (h % 2) * 64:(h % 2) * 64 + 64], reads=[rw_st.r],
                               stream="ststrs%d" % s, group=True)
                    dsh = O["rshp"][l][s] if s < 2 else O["rshs"][l]
                    fw.op("dve", lambda e: e.tensor_copy(out=rw_c7.t[:, 0:6], in_=rw_car.t[:, 0:6]), reads=[rw_car.r], writes=[rw_c7.r])
                    fw.op("dve", lambda e: e.tensor_copy(out=rw_c7.t[0:32, 6:7], in_=rw_car.t[0:32, 6:7]), reads=[rw_car.r], writes=[rw_c7.r])
                    fw.op("dve", lambda e: e.tensor_copy(out=rw_c7.t[32:64, 6:7], in_=rw_car.t[0:32, 7:8]), reads=[rw_car.r], writes=[rw_c7.r])
                    fw.op("dve", lambda e: e.tensor_copy(out=rw_c7.t[64:128, 6:7], in_=rw_car.t[0:64, 8:9]), reads=[rw_car.r], writes=[rw_c7.r])
                    fw.dma("sp", dsh.rearrange("(c p) -> p c", p=128), rw_c7.t[:, :], reads=[rw_c7.r],
                           stream="ststrc%d" % s, group=True, allow_slow_non_contiguous=True)

                def rw_tile(s, w):
                    C = min(64, w)
                    nch = w // C
                    nlev = {64: 5, 32: 4}[C]
                    rmask = rmask64 if C == 64 else rmask32
                    T0, T1, T2, T3, T4, T5 = rw_tmp
                    specs = [(i, i * 128, 128) for i in range(6)] + [(6, 768, 32), (7, 800, 32), (8, 832, 64)]
                    for (i, c0, M) in specs:
                        pb = proj_fm(c0, w, M)
                        sink[0].op("act", lambda e, i=i, M=M, pb=pb: e.activation(out=rw_Pb.t[0:M, i, 1:1 + w], in_=pb.t[0:M, 0:w],
                                                                          func=AF.Copy), reads=[pb.r], writes=[rw_Pb.r])
                    sink[0].op("act", lambda e: e.activation(out=rw_car.t[:, :], in_=rw_Pb.t[:, :, w], func=AF.Copy),
                          reads=[rw_Pb.r], writes=[rw_car.r])
                    for (i, c0, M) in specs:
                        sink[0].op("dve", lambda e, i=i, M=M: e.tensor_tensor(out=T0.t[0:M, 0:w], in0=rw_Pb.t[0:M, i, 0:w],
                                                                         in1=rw_Pb.t[0:M, i, 1:1 + w], op=ALU.subtract),
                              reads=[rw_Pb.r], writes=[T0.r])
                        sink[0].op("dve", lambda e, i=i, M=M: e.scalar_tensor_tensor(
                            out=rw_Pb.t[0:M, i, 1:1 + w], in0=T0.t[0:M, 0:w], scalar=mul.t[0:M, l, i:i + 1],
                            in1=rw_Pb.t[0:M, i, 1:1 + w], op0=ALU.mult, op1=ALU.add),
                            reads=[T0.r, mul.r, rw_Pb.r], writes=[rw_Pb.r])
                    sink[0].op("pool", lambda e: e.tensor_copy(out=rw_Pb.t[:, :, 0], in_=rw_car.t[:, :]),
                          reads=[rw_car.r, rw_Pb.r], writes=[rw_Pb.r])
                    XS = lambda i, M=128: rw_Pb.t[0:M, i, 1:1 + w]
                    sink[0].op("act", lambda e: e.activation(out=rw_TW.t[:, 0:w], in_=XS(6, 32), func=AF.Tanh),
                          reads=[rw_Pb.r], writes=[rw_TW.r])
                    sink[0].op("act", lambda e: e.activation(out=rw_AL.t[:, 0:w], in_=XS(7, 32), func=AF.Copy),
                          reads=[rw_Pb.r], writes=[rw_AL.r])
                    sink[0].op("act", lambda e: e.activation(out=T0.t[0:64, 0:w], in_=XS(8, 64), func=AF.Exp, scale=-1.0),
                          reads=[rw_Pb.r], writes=[T0.r])
                    sink[0].op("dve", lambda e: e.tensor_scalar_add(out=T0.t[0:64, 0:w], in0=T0.t[0:64, 0:w], scalar1=1.0),
                          reads=[T0.r], writes=[T0.r])
                    sink[0].op("dve", lambda e: e.reciprocal(out=T0.t[0:64, 0:w], in_=T0.t[0:64, 0:w]), reads=[T0.r], writes=[T0.r])
                    sink[0].op("act", lambda e: e.activation(out=rw_SG.t[:, 0:w], in_=T0.t[0:64, 0:w], func=AF.Copy),
                          reads=[T0.r], writes=[rw_SG.r])
                    outer_sink = sink[0]
                    pc_recs = [Rec(), Rec()]

                    def _pc_body(pc, T1, T2, T3, rw_SIG, rw_A, rw_L, rw_KP, rw_KN, rw_Bf, rw_sqb):
                        cs = slice(pc * 128, (pc + 1) * 128)
                        r_ap, k_ap, v_ap = XS(pc), XS(2 + pc), XS(4 + pc)
                        pw_ = next_bank()
                        sink[0].op("pe", lambda e, pw_=pw_, cs=cs: e.matmul(pw_.t[:, 0:w], lhsT=w2b.t[:, l, cs], rhs=rw_TW.t[:, 0:w],
                                                                    start=True, stop=True), reads=[w2b.r, rw_TW.r], writes=[pw_.r])
                        sink[0].op("act", lambda e, pw_=pw_, pc=pc: e.activation(out=rw_SIG.t[:, 0:w], in_=pw_.t[:, 0:w], func=AF.Exp,
                                                                         scale=-1.0, bias=nw0.t[:, l, pc:pc + 1]),
                              reads=[pw_.r, nw0.r], writes=[rw_SIG.r])
                        sink[0].op("dve", lambda e: e.tensor_scalar_add(out=rw_SIG.t[:, 0:w], in0=rw_SIG.t[:, 0:w], scalar1=1.0),
                              reads=[rw_SIG.r], writes=[rw_SIG.r])
                        sink[0].op("dve", lambda e: e.reciprocal(out=rw_SIG.t[:, 0:w], in_=rw_SIG.t[:, 0:w]),
                              reads=[rw_SIG.r], writes=[rw_SIG.r])
                        pa_ = next_bank()
                        sink[0].op("pe", lambda e, pa_=pa_, cs=cs: e.matmul(pa_.t[:, 0:w], lhsT=a2b.t[:, l, cs], rhs=rw_AL.t[:, 0:w],
                                                                    start=True, stop=True), reads=[a2b.r, rw_AL.r], writes=[pa_.r])
                        sink[0].op("act", lambda e, pa_=pa_, pc=pc: e.activation(out=rw_A.t[:, 0:w], in_=pa_.t[:, 0:w], func=AF.Exp,
                                                                         scale=-1.0, bias=na0.t[:, l, pc:pc + 1]),
                              reads=[pa_.r, na0.r], writes=[rw_A.r])
                        sink[0].op("dve", lambda e: e.tensor_scalar_add(out=rw_A.t[:, 0:w], in0=rw_A.t[:, 0:w], scalar1=1.0),
                              reads=[rw_A.r], writes=[rw_A.r])
                        sink[0].op("dve", lambda e: e.reciprocal(out=rw_A.t[:, 0:w], in_=rw_A.t[:, 0:w]), reads=[rw_A.r], writes=[rw_A.r])
                        pg_ = next_bank()
                        sink[0].op("pe", lambda e, pg_=pg_, cs=cs: e.matmul(pg_.t[:, 0:w], lhsT=g2b.t[:, l, cs], rhs=rw_SG.t[:, 0:w],
                                                                    start=True, stop=True), reads=[g2b.r, rw_SG.r], writes=[pg_.r])
                        sink[0].op("act", lambda e, pg_=pg_, pc=pc: e.activation(out=rw_G[pc].t[:, 0:w], in_=pg_.t[:, 0:w], func=AF.Copy),
                              reads=[pg_.r], writes=[rw_G[pc].r])
                        sink[0].op("dve", lambda e, pc=pc, k_ap=k_ap: e.tensor_scalar_mul(
                            out=rw_KN.t[:, 0:w], in0=k_ap, scalar1=rwp["rw_k_k"].t[:, l, pc:pc + 1]),
                            reads=[rw_Pb.r, rwp["rw_k_k"].r], writes=[rw_KN.r])
                        sink[0].op("act", lambda e: e.activation(out=rw_sqb.t[:, 0:w], in_=rw_KN.t[:, 0:w], func=AF.Square),
                              reads=[rw_KN.r], writes=[rw_sqb.r])
                        pn = next_bank()
                        sink[0].op("pe", lambda e, pn=pn: e.matmul(pn.t[:, 0:w], lhsT=onesbd.t[:], rhs=rw_sqb.t[:, 0:w],
                                                              start=True, stop=True), reads=[onesbd.r, rw_sqb.r], writes=[pn.r])
                        sink[0].op("dve", lambda e, pn=pn: e.tensor_scalar_max(out=T1.t[:, 0:w], in0=pn.t[:, 0:w], scalar1=1e-24),
                              reads=[pn.r], writes=[T1.r])
                        sink[0].op("act", lambda e: e.activation(out=T1.t[:, 0:w], in_=T1.t[:, 0:w], func=AF.Ln), reads=[T1.r], writes=[T1.r])
                        sink[0].op("act", lambda e: e.activation(out=T1.t[:, 0:w], in_=T1.t[:, 0:w], func=AF.Exp, scale=-0.5),
                              reads=[T1.r], writes=[T1.r])
                        sink[0].op("dve", lambda e: e.tensor_tensor(out=rw_KN.t[:, 0:w], in0=rw_KN.t[:, 0:w], in1=T1.t[:, 0:w], op=ALU.mult),
                              reads=[rw_KN.r, T1.r], writes=[rw_KN.r])
                        sink[0].op("dve", lambda e, pc=pc: e.tensor_scalar(
                            out=T1.t[:, 0:w], in0=rw_A.t[:, 0:w], scalar1=rwp["rw_k_a"].t[:, l, pc:pc + 1],
                            scalar2=omka.t[:, l, pc:pc + 1], op0=ALU.mult, op1=ALU.add),
                            reads=[rw_A.r, rwp["rw_k_a"].r, omka.r], writes=[T1.r])
                        sink[0].op("dve", lambda e, k_ap=k_ap: e.tensor_tensor(out=rw_KP.t[:, 0:w], in0=k_ap, in1=T1.t[:, 0:w], op=ALU.mult),
                              reads=[rw_Pb.r, T1.r], writes=[rw_KP.r])
                        sink[0].op("dve", lambda e: e.tensor_tensor(out=rw_Bf.t[:, 0:w], in0=rw_KN.t[:, 0:w], in1=rw_A.t[:, 0:w], op=ALU.mult),
                              reads=[rw_KN.r, rw_A.r], writes=[rw_Bf.r])
                        sink[0].op("dve", lambda e: e.tensor_tensor_scan(out=rw_L.t[:, 0:w], data0=rmask.t[:, 0:w], data1=rw_SIG.t[:, 0:w],
                                                                   initial=0.0, op0=ALU.mult, op1=ALU.add),
                              reads=[rw_SIG.r, rmask.r], writes=[rw_L.r])
                        Lv = rw_L.t[:, 0:w].rearrange("p (c t) -> p c t", t=C)
                        sink[0].op("act", lambda e: e.activation(out=T2.t[:, 0:w], in_=rw_L.t[:, 0:w], func=AF.Exp, scale=-C0),
                              reads=[rw_L.r], writes=[T2.r])
                        sink[0].op("dve", lambda e: e.tensor_tensor(out=T3.t[:, 0:w], in0=rw_L.t[:, 0:w], in1=rw_SIG.t[:, 0:w], op=ALU.subtract),
                              reads=[rw_L.r, rw_SIG.r], writes=[T3.r])
                        sink[0].op("act", lambda e: e.activation(out=T3.t[:, 0:w], in_=T3.t[:, 0:w], func=AF.Exp, scale=-C0),
                              reads=[T3.r], writes=[T3.r])
                        for par in range(2):
                            hm = onesbdf.t[:, par * 64:par * 64 + 1]
                            sink[0].op("dve", lambda e, pc=pc, par=par, hm=hm, r_ap=r_ap: e.scalar_tensor_tensor(
                                out=rw_QRm.t[:, par, pc, 0:nch, 1, 0:C], in0=r_ap.rearrange("p (c t) -> p c t", t=C), scalar=hm,
                                in1=T2.t[:, 0:w].rearrange("p (c t) -> p c t", t=C), op0=ALU.mult, op1=ALU.mult),
                                reads=[rw_Pb.r, T2.r, onesbdf.r], writes=[rw_QRm.rs[pc]])
                            sink[0].op("dve", lambda e, pc=pc, par=par, hm=hm: e.scalar_tensor_tensor(
                                out=rw_QRm.t[:, par, pc, 0:nch, 0, 0:C], in0=rw_KN.t[:, 0:w].rearrange("p (c t) -> p c t", t=C),
                                scalar=hm, in1=T3.t[:, 0:w].rearrange("p (c t) -> p c t", t=C), op0=ALU.mult, op1=ALU.mult),
                                reads=[rw_KN.r, T3.r, onesbdf.r], writes=[rw_QRm.rs[pc]])
                        sink[0].op("act", lambda e: e.activation(out=T2.t[:, 0:w], in_=rw_L.t[:, 0:w], func=AF.Exp, scale=C0),
                              reads=[rw_L.r], writes=[T2.r])
                        sink[0].op("dve", lambda e, pc=pc: e.tensor_tensor(out=rw_Kh.t[:, pc, 0:w], in0=rw_KP.t[:, 0:w], in1=T2.t[:, 0:w], op=ALU.mult),
                              reads=[rw_KP.r, T2.r], writes=[rw_Kh.rs[pc]])
                        for par in range(2):
                            hm = onesbdf.t[:, par * 64:par * 64 + 1]
                            sink[0].op("dve", lambda e, pc=pc, par=par, hm=hm: e.scalar_tensor_tensor(
                                out=rw_Bm.t[:, par, pc, 0:w], in0=rw_Bf.t[:, 0:w], scalar=hm, in1=T2.t[:, 0:w],
                                op0=ALU.mult, op1=ALU.mult), reads=[rw_Bf.r, T2.r, onesbdf.r], writes=[rw_Bm.rs[pc]])
                        sink[0].op("dve", lambda e, Lv=Lv: e.tensor_tensor(
                            out=T3.t[:, 0:w].rearrange("p (c t) -> p c t", t=C), in0=Lv[:, :, C - 1:C].to_broadcast([128, nch, C]),
                            in1=Lv, op=ALU.subtract), reads=[rw_L.r], writes=[T3.r])
                        sink[0].op("act", lambda e: e.activation(out=T3.t[:, 0:w], in_=T3.t[:, 0:w], func=AF.Exp, scale=-C0),
                              reads=[T3.r], writes=[T3.r])
                        sink[0].op("dve", lambda e, pc=pc: e.tensor_tensor(out=rw_Ke.t[:, pc, 0:w], in0=rw_KP.t[:, 0:w], in1=T3.t[:, 0:w], op=ALU.mult),
                              reads=[rw_KP.r, T3.r], writes=[rw_Ke.rs[pc]])
                        sink[0].op("pool", lambda e, pc=pc: e.tensor_tensor(out=rw_Be.t[:, pc, 0:w], in0=rw_Bf.t[:, 0:w], in1=T3.t[:, 0:w], op=ALU.mult),
                              reads=[rw_Bf.r, T3.r], writes=[rw_Be.rs[pc]])
                        sink[0].op("act", lambda e, pc=pc, Lv=Lv: e.activation(out=rw_gC.t[:, pc, 0:nch], in_=Lv[:, :, C - 1], func=AF.Exp,
                                                                       scale=-C0), reads=[rw_L.r], writes=[rw_gC.rs[pc]])
                        sink[0].op("act", lambda e, pc=pc, v_ap=v_ap: e.activation(out=rw_Vb.t[:, pc, 0:w], in_=v_ap, func=AF.Copy),
                              reads=[rw_Pb.r], writes=[rw_Vb.rs[pc]])
                        sink[0].op("dve", lambda e, pc=pc, r_ap=r_ap: e.scalar_tensor_tensor(
                            out=(T4 if pc == 0 else T5).t[:, 0:w], in0=r_ap, scalar=rwp["rw_r_k"].t[:, l, pc:pc + 1],
                            in1=rw_KP.t[:, 0:w], op0=ALU.mult, op1=ALU.mult),
                            reads=[rw_Pb.r, rwp["rw_r_k"].r, rw_KP.r], writes=[(T4 if pc == 0 else T5).r])
                    for pc in range(2):
                        sink[0] = pc_recs[pc]
                        tt = (T1, T2, T3) if pc == 0 else tuple(rw_tmpB)
                        _pc_body(pc, tt[0], tt[1], tt[2], rw_SIG2[pc], rw_A2[pc], rw_L2[pc], rw_KP2[pc], rw_KN2[pc],
                                 rw_Bf2[pc], rw_sqb2[pc])
                    sink[0] = outer_sink
                    merge_recs(outer_sink, pc_recs)
                    RWDBG = int(os.environ.get("RWDBG", "9"))
                    if RWDBG < 2:
                        return
                    for (src, dstT) in ((rw_Vb, rw_Vt), (rw_Ke, rw_KeT), (rw_Be, rw_BeT)):
                        for c0 in range(0, nch, 4):
                            pT = next_bank()
                            pTb = pT.t[:, :].bitcast(BF16)
                            ncc = min(4, nch - c0)
                            for ci in range(ncc):
                                c = c0 + ci
                                for pc in range(2):
                                    sink[0].op("pe", lambda e, src=src, c=c, ci=ci, pc=pc, pTb=pTb: e.transpose(
                                        pTb[0:C, ci * 256 + pc * 128:ci * 256 + (pc + 1) * 128], src.t[:, pc, c * C:(c + 1) * C],
                                        identb.t[:, :]), reads=[src.rs[pc], identb.r], writes=[pT.r],
                                        signal=(ci == ncc - 1 and pc == 1))
                            sink[0].op("act", lambda e, dstT=dstT, c0=c0, ncc=ncc, pTb=pTb: e.activation(
                                out=dstT.t[0:C, c0:c0 + ncc, :], in_=pTb[0:C, 0:ncc * 256].rearrange("p (c n) -> p c n", n=256),
                                func=AF.Copy), reads=[pT.r], writes=[dstT.r])
                    if RWDBG < 3:
                        return
                    class _V:
                        pass
                    py = [_V(), _V()]
                    for pc_ in range(2):
                        py[pc_].t = banks[5].t[:, pc_ * WA:(pc_ + 1) * WA]
                        py[pc_].r = banks[5].r
                    v4 = lambda ap: ap.rearrange("p (h a t) -> p h a t", h=4, a=2)[:, :, :, 0:C]
                    v3 = lambda ap: ap.rearrange("p (h t) -> p h t", h=4)[:, :, 0:C]
                    for c in range(nch):
                        p12, p34, p5 = next_bank(), next_bank(), next_bank()
                        AT12, AT34 = rw_AT12[c], rw_AT34[c]
                        for h in range(4):
                            par, pc = h % 2, h // 2
                            for a_ in range(2):
                                qr = rw_QRm.t[:, par, pc, c, a_, 0:C]
                                sink[0].op("pe", lambda e, c=c, h=h, pc=pc, qr=qr, p12=p12, a_=a_: e.matmul(
                                    p12.t[0:C, h * 128 + a_ * 64:h * 128 + a_ * 64 + C], lhsT=rw_Kh.t[:, pc, c * C:(c + 1) * C],
                                    rhs=qr, start=True, stop=True), reads=[rw_Kh.rs[pc], rw_QRm.rs[pc]], writes=[p12.r],
                                    signal=(h == 3 and a_ == 1))
                                sink[0].op("pe", lambda e, c=c, h=h, par=par, pc=pc, qr=qr, p34=p34, a_=a_: e.matmul(
                                    p34.t[0:C, h * 128 + a_ * 64:h * 128 + a_ * 64 + C],
                                    lhsT=rw_Bm.t[:, par, pc, c * C:(c + 1) * C], rhs=qr, start=True, stop=True),
                                    reads=[rw_Bm.rs[pc], rw_QRm.rs[pc]], writes=[p34.r], signal=(h == 3 and a_ == 1))
                            sink[0].op("pe", lambda e, h=h, par=par, pc=pc, c=c, p5=p5: e.matmul(
                                p5.t[0:C, h * 64:h * 64 + C], lhsT=rw_QRm.t[:, par, pc, c, 0, 0:C],
                                rhs=rw_Bm.t[:, par, pc, c * C:(c + 1) * C], start=True, stop=True),
                                reads=[rw_Bm.rs[pc], rw_QRm.rs[pc]], writes=[p5.r], signal=(h == 3))
                        sink[0].op("dve", lambda e, p12=p12, AT12=AT12: e.tensor_tensor(
                            out=AT12.t[0:C, :, :, 0:C], in0=v4(p12.t[0:C, :]),
                            in1=mask12.t[0:C, :, 0:C].unsqueeze(1).to_broadcast([C, 4, 2, C]), op=ALU.mult),
                            reads=[p12.r, mask12.r], writes=[AT12.r])
                        sink[0].op("dve", lambda e, p34=p34, AT34=AT34: e.tensor_tensor(
                            out=AT34.t[0:C, :, :, 0:C], in0=v4(p34.t[0:C, :]),
                            in1=mask34.t[0:C, :, 0:C].unsqueeze(1).to_broadcast([C, 4, 2, C]), op=ALU.mult),
                            reads=[p34.r, mask34.r], writes=[AT34.r])
                        X, XT, TT = rw_X[c][0], rw_XT[c][0], rw_TT[c][0]
                        sink[0].op("dve", lambda e, p5=p5, X=X: e.tensor_tensor(
                            out=X.t[0:C, :, 0:C], in0=v3(p5.t[0:C, 0:256]),
                            in1=mask5.t[0:C, 0:C].unsqueeze(1).to_broadcast([C, 4, C]), op=ALU.mult),
                            reads=[p5.r, mask5.r], writes=[X.r])
                        sink[0].op("act", lambda e, XT=XT, AT34=AT34: e.activation(out=XT.t[0:C, :, 0:C], in_=AT34.t[0:C, :, 0, 0:C], func=AF.Copy),
                              reads=[AT34.r], writes=[XT.r])
                        sink[0].op("pool", lambda e, TT=TT, AT34=AT34: e.tensor_tensor(
                            out=TT.t[0:C, :, 0:C], in0=AT34.t[0:C, :, 0, 0:C],
                            in1=identb.t[0:C, 0:C].unsqueeze(1).to_broadcast([C, 4, C]), op=ALU.add),
                            reads=[AT34.r, identb.r], writes=[TT.r])
                    cur = 0
                    for lev in range(nlev):
                        for c in range(nch):
                            Xn, XTn, TTn = rw_X[c][1 - cur], rw_XT[c][1 - cur], rw_TT[c][1 - cur]
                            X, XT, TT = rw_X[c][cur], rw_XT[c][cur], rw_TT[c][cur]
                            px, pxt, ptt = next_bank(), next_bank(), next_bank()
                            for h in range(4):
                                sink[0].op("pe", lambda e, h=h, px=px, X=X, XT=XT: e.matmul(
                                    px.t[0:C, h * 64:h * 64 + C], lhsT=XT.t[0:C, h, 0:C], rhs=X.t[0:C, h, 0:C], start=True, stop=True),
                                    reads=[X.r, XT.r], writes=[px.r], signal=(h == 3))
                            sink[0].op("act", lambda e, px=px, Xn=Xn: e.activation(
                                out=Xn.t[0:C, :, 0:C], in_=v3(px.t[0:C, 0:256]), func=AF.Copy), reads=[px.r], writes=[Xn.r])
                            if lev < nlev - 1:
                                for h in range(4):
                                    sink[0].op("pe", lambda e, h=h, pxt=pxt, X=X, XT=XT: e.matmul(
                                        pxt.t[0:C, h * 64:h * 64 + C], lhsT=X.t[0:C, h, 0:C], rhs=XT.t[0:C, h, 0:C], start=True, stop=True),
                                        reads=[X.r, XT.r], writes=[pxt.r], signal=(h == 3))
                                sink[0].op("act" if c % 2 else "dve", (lambda e, pxt=pxt, XTn=XTn: e.activation(
                                    out=XTn.t[0:C, :, 0:C], in_=v3(pxt.t[0:C, 0:256]), func=AF.Copy)) if c % 2 else
                                    (lambda e, pxt=pxt, XTn=XTn: e.tensor_copy(out=XTn.t[0:C, :, 0:C], in_=v3(pxt.t[0:C, 0:256]))),
                                    reads=[pxt.r], writes=[XTn.r])
                            for h in range(4):
                                sink[0].op("pe", lambda e, h=h, ptt=ptt, Xn=Xn, TT=TT: e.matmul(
                                    ptt.t[0:C, h * 64:h * 64 + C], lhsT=Xn.t[0:C, h, 0:C], rhs=TT.t[0:C, h, 0:C], start=True, stop=True),
                                    reads=[Xn.r, TT.r], writes=[ptt.r], signal=(h == 3))
                            sink[0].op("dve", lambda e, ptt=ptt, TT=TT, TTn=TTn: e.tensor_tensor(
                                out=TTn.t[0:C, :, 0:C], in0=v3(ptt.t[0:C, 0:256]), in1=TT.t[0:C, :, 0:C], op=ALU.add),
                                reads=[ptt.r, TT.r], writes=[TTn.r])
                        cur = 1 - cur
                    if RWDBG < 4:
                        return
                    for c in range(nch):
                        TTf = rw_TT[c][cur]
                        AT12, AT34 = rw_AT12[c], rw_AT34[c]
                        pz, pu, ph = next_bank(), next_bank(), next_bank()
                        for pc in range(2):
                            for par in range(2):
                                sink[0].op("pe", lambda e, c=c, AT12=AT12, AT34=AT34, pc=pc, par=par, pz=pz: e.matmul(
                                    pz.t[0:C, pc * 128:(pc + 1) * 128], lhsT=rw_QRm.t[:, par, pc, c, 0, 0:C], rhs=rw_Hbd.t[:, pc, :],
                                    start=(par == 0), stop=False, skip_group_check=True),
                                    reads=[rw_QRm.rs[pc], rw_Hbd.rs[pc]], writes=[pz.r], signal=False)
                            for h in (2 * pc, 2 * pc + 1):
                                sink[0].op("pe", lambda e, c=c, AT12=AT12, AT34=AT34, h=h, pz=pz: e.matmul(
                                    pz.t[0:C, h * 64:(h + 1) * 64], lhsT=AT12.t[0:C, h, 0, 0:C], rhs=rw_Vt.t[0:C, c, h * 64:(h + 1) * 64],
                                    start=False, stop=True, skip_group_check=True),
                                    reads=[AT12.r, rw_Vt.r], writes=[pz.r], signal=(h == 3))
                        sink[0].op("act", lambda e, c=c, AT12=AT12, AT34=AT34, pz=pz: e.activation(out=rw_Zb.t[0:C, :], in_=pz.t[0:C, 0:256], func=AF.Copy),
                              reads=[pz.r], writes=[rw_Zb.r])
                        for h in range(4):
                            sink[0].op("pe", lambda e, c=c, AT12=AT12, AT34=AT34, h=h, pu=pu, TTf=TTf: e.matmul(
                                pu.t[0:C, h * 64:(h + 1) * 64], lhsT=TTf.t[0:C, h, 0:C], rhs=rw_Zb.t[0:C, h * 64:(h + 1) * 64],
                                start=True, stop=True), reads=[TTf.r, rw_Zb.r], writes=[pu.r], signal=(h == 3))
                        sink[0].op("dve", lambda e, c=c, AT12=AT12, AT34=AT34, pu=pu: e.tensor_scalar_mul(out=rw_Un.t[0:C, :], in0=pu.t[0:C, 0:256], scalar1=-1.0),
                              reads=[pu.r], writes=[rw_Un.r])
                        for pc in range(2):
                          for par in range(2):
                                sink[0].op("pe", lambda e, c=c, AT12=AT12, AT34=AT34, pc=pc, par=par: e.matmul(
                                    py[pc].t[:, c * C:(c + 1) * C], lhsT=rw_Hbd.t[:, pc, :], rhs=rw_QRm.t[:, par, pc, c, 1, 0:C],
                                    start=(par == 0), stop=False, skip_group_check=True),
                                    reads=[rw_QRm.rs[pc], rw_Hbd.rs[pc]], writes=[py[pc].r], signal=False)
                          for h in (2 * pc, 2 * pc + 1):
                            hs = slice((h % 2) * 64, (h % 2) * 64 + 64)
                            sink[0].op("pe", lambda e, c=c, AT12=AT12, AT34=AT34, h=h, pc=pc, hs=hs: e.matmul(
                                py[pc].t[hs, c * C:(c + 1) * C], lhsT=rw_Vt.t[0:C, c, h * 64:(h + 1) * 64], rhs=AT12.t[0:C, h, 1, 0:C],
                                start=False, stop=False, skip_group_check=True),
                                reads=[rw_Vt.r, AT12.r], writes=[py[pc].r], signal=False)
                            sink[0].op("pe", lambda e, c=c, AT12=AT12, AT34=AT34, h=h, pc=pc, hs=hs: e.matmul(
                                py[pc].t[hs, c * C:(c + 1) * C], lhsT=rw_Un.t[0:C, h * 64:(h + 1) * 64], rhs=AT34.t[0:C, h, 1, 0:C],
                                start=False, stop=True, skip_group_check=True),
                                reads=[rw_Un.r, AT34.r], writes=[py[pc].r], signal=(h % 2 == 1))
                        for pc in range(2):
                            cs = slice(pc * 128, (pc + 1) * 128)
                            sink[0].op("pe", lambda e, c=c, AT12=AT12, AT34=AT34, pc=pc, cs=cs, ph=ph: e.matmul(
                                ph.t[:, cs], lhsT=rw_KeT.t[0:C, c, cs], rhs=rw_Vt.t[0:C, c, cs], start=True, stop=False),
                                reads=[rw_KeT.r, rw_Vt.r], writes=[ph.r], signal=False)
                            sink[0].op("pe", lambda e, c=c, AT12=AT12, AT34=AT34, pc=pc, cs=cs, ph=ph: e.matmul(
                                ph.t[:, cs], lhsT=rw_BeT.t[0:C, c, cs], rhs=rw_Un.t[0:C, cs], start=False, stop=True),
                                reads=[rw_BeT.r, rw_Un.r], writes=[ph.r])
                            for hh in range(2):
                                hr = slice(hh * 64, hh * 64 + 64)
                                sink[0].op("dve", lambda e, c=c, AT12=AT12, AT34=AT34, pc=pc, hr=hr, hh=hh, ph=ph: e.scalar_tensor_tensor(
                                    out=rw_Hm.t[hr, pc, hr], in0=rw_Hm.t[hr, pc, hr], scalar=rw_gC.t[hr, pc, c:c + 1],
                                    in1=ph.t[hr, pc * 128 + hh * 64:pc * 128 + hh * 64 + 64], op0=ALU.mult, op1=ALU.add),
                                    reads=[rw_Hm.rs[pc], rw_gC.rs[pc], ph.r], writes=[rw_Hm.rs[pc]])
                            sink[0].op("act", lambda e, c=c, AT12=AT12, AT34=AT34, pc=pc: e.activation(out=rw_Hbd.t[:, pc, :], in_=rw_Hm.t[:, pc, :], func=AF.Copy),
                                  reads=[rw_Hm.rs[pc]], writes=[rw_Hbd.rs[pc]])
                    if RWDBG < 5:
                        return
                    for pc in range(2):
                        v_ap = XS(4 + pc)
                        TB = T4 if pc == 0 else T5
                        sink[0].op("act", lambda e, pc=pc: e.activation(out=rw_Yf.t[:, 0:w], in_=py[pc].t[:, 0:w], func=AF.Copy),
                              reads=[py[pc].r], writes=[rw_Yf.r])
                        sink[0].op("act", lambda e: e.activation(out=rw_sqb.t[:, 0:w], in_=rw_Yf.t[:, 0:w], func=AF.Copy),
                              reads=[rw_Yf.r], writes=[rw_sqb.r])
                        pm = next_bank()
                        sink[0].op("pe", lambda e, pm=pm: e.matmul(pm.t[:, 0:w], lhsT=onesbd.t[:], rhs=rw_sqb.t[:, 0:w], start=True, stop=True),
                              reads=[onesbd.r, rw_sqb.r], writes=[pm.r])
                        sink[0].op("dve", lambda e, pm=pm: e.scalar_tensor_tensor(
                            out=rw_Yf.t[:, 0:w], in0=pm.t[:, 0:w], scalar=-1.0 / 64, in1=rw_Yf.t[:, 0:w], op0=ALU.mult, op1=ALU.add),
                            reads=[pm.r, rw_Yf.r], writes=[rw_Yf.r])
                        sink[0].op("act", lambda e: e.activation(out=rw_sqb.t[:, 0:w], in_=rw_Yf.t[:, 0:w], func=AF.Square),
                              reads=[rw_Yf.r], writes=[rw_sqb.r])
                        pv = next_bank()
                        sink[0].op("pe", lambda e, pv=pv: e.matmul(pv.t[:, 0:w], lhsT=onesbd.t[:], rhs=rw_sqb.t[:, 0:w], start=True, stop=True),
                              reads=[onesbd.r, rw_sqb.r], writes=[pv.r])
                        sink[0].op("act", lambda e, pv=pv: e.activation(out=T0.t[:, 0:w], in_=pv.t[:, 0:w], func=AF.Ln, scale=1.0 / 64,
                                                                   bias=eps2.t[:]), reads=[pv.r, eps2.r], writes=[T0.r])
                        sink[0].op("act", lambda e: e.activation(out=T0.t[:, 0:w], in_=T0.t[:, 0:w], func=AF.Exp, scale=-0.5),
                              reads=[T0.r], writes=[T0.r])
                        sink[0].op("dve", lambda e: e.tensor_tensor(out=rw_Yf.t[:, 0:w], in0=rw_Yf.t[:, 0:w], in1=T0.t[:, 0:w], op=ALU.mult),
                              reads=[rw_Yf.r, T0.r], writes=[rw_Yf.r])
                        sink[0].op("dve", lambda e, pc=pc: e.tensor_scalar(
                            out=rw_Yf.t[:, 0:w], in0=rw_Yf.t[:, 0:w], scalar1=rwp["rw_ln_w"].t[:, l, pc:pc + 1],
                            scalar2=rwp["rw_ln_b"].t[:, l, pc:pc + 1], op0=ALU.mult, op1=ALU.add),
                            reads=[rw_Yf.r, rwp["rw_ln_w"].r, rwp["rw_ln_b"].r], writes=[rw_Yf.r])
                        sink[0].op("act", lambda e, TB=TB: e.activation(out=rw_sqb.t[:, 0:w], in_=TB.t[:, 0:w], func=AF.Copy),
                              reads=[TB.r], writes=[rw_sqb.r])
                        pbn = next_bank()
                        sink[0].op("pe", lambda e, pbn=pbn: e.matmul(pbn.t[:, 0:w], lhsT=onesbd.t[:], rhs=rw_sqb.t[:, 0:w], start=True, stop=True),
                              reads=[onesbd.r, rw_sqb.r], writes=[pbn.r])
                        sink[0].op("dve", lambda e, pbn=pbn, v_ap=v_ap: e.tensor_tensor(out=T0.t[:, 0:w], in0=pbn.t[:, 0:w], in1=v_ap, op=ALU.mult),
                              reads=[pbn.r, rw_Pb.r], writes=[T0.r])
                        sink[0].op("dve", lambda e: e.tensor_tensor(out=rw_Yf.t[:, 0:w], in0=rw_Yf.t[:, 0:w], in1=T0.t[:, 0:w], op=ALU.add),
                              reads=[rw_Yf.r, T0.r], writes=[rw_Yf.r])
                        sink[0].op("dve", lambda e, pc=pc: e.tensor_tensor(out=ymix.t[:, pc, 0:w], in0=rw_Yf.t[:, 0:w], in1=rw_G[pc].t[:, 0:w],
                                                                      op=ALU.mult), reads=[rw_Yf.r, rw_G[pc].r], writes=[ymix.rs[pc]])

                RWKV_TILE = rw_tile

                it = 0
                a2_tiles = [(s_, t0_, w_) for s_ in range(3) for (t0_, w_, j_) in tiles_of(s_, WA)]
                a2_idx = [0]

                def a2_norm(idx):
                    s_, t0_, w_ = a2_tiles[idx]
                    xT_ = xTa[idx % 2]
                    load_xT(xT_, t0_, w_)
                    norm_fm(xT_, w_, sq, tmp, lnv, rstd, hT,
                            lambda c, s_=s_: G1.t[:, l, s_, c:c + 1], lambda c, s_=s_: mod.t[:, l, 0, s_, c:c + 1], hT.rs)
                for s in range(3):
                    hg_init(s)
                    if stage >= 4:
                        rw_init(s)

                    def a2_tile(s, t0, w, j, xT, l=l):
                        if a2_idx[0] == 0:
                            a2_norm(0)
                        fw.dma("pool", ymix.t[:, 2:6, 0:w], yfox.rearrange("(c p) t -> p c t", p=128)[:, :, t0:t0 + w],
                               reads=[yfreg(t0)], writes=ymix.rs[2:6], stream="yfld")
                        recs = []
                        if stage >= 3:
                            sink[0] = Rec()
                            recs.append(sink[0])
                            default_pool[0] = (0, 1)
                            hg_tile(s, w)
                        if stage >= 4 and RWKV_TILE is not None:
                            sink[0] = Rec()
                            recs.append(sink[0])
                            default_pool[0] = (2, 3, 4)
                            RWKV_TILE(s, w)
                        sink[0] = fw
                        default_pool[0] = (0, 1, 2, 3, 4)
                        merge_recs(fw, recs)
                        a2_idx[0] += 1
                        if a2_idx[0] < len(a2_tiles):
                            a2_norm(a2_idx[0])
                        for c in range(8):
                            po_ = next_bank()
                            for kc in range(8):
                                fw.op("pe", lambda e, c=c, kc=kc, po_=po_: e.matmul(
                                    po_.t[:, 0:w], lhsT=wo.t[:, kc, c * 128:(c + 1) * 128], rhs=ymix.t[:, kc, 0:w],
                                    start=(kc == 0), stop=(kc == 7)),
                                    reads=[wo.r, ymix.rs[kc]], writes=[po_.r], signal=(kc == 7))
                            fw.op("dve", lambda e, c=c, po_=po_: e.scalar_tensor_tensor(
                                out=xT.t[:, c, 0:w], in0=po_.t[:, 0:w], scalar=mod.t[:, l, 2, s, c:c + 1],
                                in1=xT.t[:, c, 0:w], op0=ALU.mult, op1=ALU.add),
                                reads=[po_.r, mod.r, xT.rs[c]], writes=[xT.rs[c]])
                        store_xT(xT, t0, w)
                    for (t0, w, j) in tiles_of(s, WA):
                        xT = xTa[it % 2]
                        it += 1
                        a2_tile(s, t0, w, j, xT)
                    if stage >= 3:
                        hg_final(s)
                    if stage >= 4:
                        rw_final(s, w)
                fw.flush()
                default_pool[0] = (0, 1, 2, 3, 4, 5, 6, 7)
            with contextlib.ExitStack() as ph:
                sub = FWScope(fw, ph)
                WB = 256
                wfi = Tt(sub.sbuf("wfi", [128, 8, 2 * DFF], BF16), name="wfi")
                wfo = Tt(sub.sbuf("wfo", [128, 22, D], BF16), name="wfo")
                for kc in range(8):
                    fw.dma("pool", wfi.t[:, kc, :], I["w_ffn_in"][l][kc * 128:(kc + 1) * 128, :], writes=[wfi.r],
                           stream="wld", group=True)
                for kc in range(22):
                    fw.dma("pool", wfo.t[:, kc, :], I["w_ffn_out"][l][kc * 128:(kc + 1) * 128, :], writes=[wfo.r],
                           stream="wld", group=True)
                xTb = [Tt(sub.sbuf("xTb%d" % i, [128, 8, WB], F32), nreg=8, name="xTb%d" % i) for i in range(2)]
                sq = Tt(sub.sbuf("sqb", [128, 8, WB], BF16), name="sqb")
                hT = Tt(sub.sbuf("hTb", [128, 8, WB], BF16), nreg=8, name="hTb")
                tmp = [Tt(sub.sbuf("tmpb%d" % i, [128, WB], F32), name="tmpb%d" % i) for i in range(2)]
                lnv = Tt(sub.sbuf("lnvb", [128, WB], F32), name="lnvb")
                rstd = Tt(sub.sbuf("rstdb", [128, WB], F32), name="rstdb")
                actT = Tt(sub.sbuf("actT", [128, 22, WB], BF16), nreg=22, name="actT")
                sg = [Tt(sub.sbuf("sg%d" % i, [128, WB], F32), name="sg%d" % i) for i in range(2)]
                tiles_b = [(s_, t0_, w_) for s_ in range(3) for (t0_, w_, j_) in tiles_of(s_, WB)]

                def b_norm(idx):
                    s_, t0_, w_ = tiles_b[idx]
                    xT_ = xTb[idx % 2]
                    load_xT(xT_, t0_, w_, q="sp")
                    norm_fm(xT_, w_, sq, tmp, lnv, rstd, hT,
                            lambda c, s_=s_: G2.t[:, l, s_, c:c + 1], lambda c, s_=s_: mod.t[:, l, 3, s_, c:c + 1], hT.rs)
                b_norm(0)
                for idx in range(len(tiles_b)):
                    if True:
                        s, t0, w = tiles_b[idx]
                        xT = xTb[idx % 2]
                        for f in range(22):
                            pg = next_bank()
                            pu = next_bank()
                            for kc in range(8):
                                fw.op("pe", lambda e, kc=kc, f=f, pg=pg, w=w: e.matmul(
                                    pg.t[:, 0:w], lhsT=wfi.t[:, kc, f * 128:(f + 1) * 128], rhs=hT.t[:, kc, 0:w],
                                    start=(kc == 0), stop=(kc == 7)),
                                    reads=[wfi.r, hT.rs[kc]], writes=[pg.r], signal=(kc == 7))
                            for kc in range(8):
                                fw.op("pe", lambda e, kc=kc, f=f, pu=pu, w=w: e.matmul(
                                    pu.t[:, 0:w], lhsT=wfi.t[:, kc, DFF + f * 128:DFF + (f + 1) * 128],
                                    rhs=hT.t[:, kc, 0:w], start=(kc == 0), stop=(kc == 7)),
                                    reads=[wfi.r, hT.rs[kc]], writes=[pu.r], signal=(kc == 7))
                            sgt = sg[f % 2]
                            fw.op("act", lambda e, pg=pg, sgt=sgt, w=w: e.activation(
                                out=sgt.t[:, 0:w], in_=pg.t[:, 0:w], func=AF.Silu), reads=[pg.r], writes=[sgt.r])
                            fw.op("dve", lambda e, pu=pu, sgt=sgt, f=f, w=w: e.tensor_tensor(
                                out=actT.t[:, f, 0:w], in0=sgt.t[:, 0:w], in1=pu.t[:, 0:w], op=ALU.mult),
                                reads=[sgt.r, pu.r], writes=[actT.rs[f]])
                        if idx + 1 < len(tiles_b):
                            b_norm(idx + 1)
                        for c in range(8):
                            po = next_bank()
                            for f in range(22):
                                fw.op("pe", lambda e, c=c, f=f, po=po, w=w: e.matmul(
                                    po.t[:, 0:w], lhsT=wfo.t[:, f, c * 128:(c + 1) * 128], rhs=actT.t[:, f, 0:w],
                                    start=(f == 0), stop=(f == 21)),
                                    reads=[wfo.r, actT.rs[f]], writes=[po.r], signal=(f == 21))
                            fw.op("dve", lambda e, c=c, po=po, xT=xT, s=s, w=w: e.scalar_tensor_tensor(
                                out=xT.t[:, c, 0:w], in0=po.t[:, 0:w], scalar=mod.t[:, l, 5, s, c:c + 1],
                                in1=xT.t[:, c, 0:w], op0=ALU.mult, op1=ALU.add),
                                reads=[po.r, mod.r, xT.rs[c]], writes=[xT.rs[c]])
                        store_xT(xT, t0, w, q="sp")
                fw.flush()

        with contextlib.ExitStack() as ph:
            sub = FWScope(fw, ph)
            WE = 512
            xTe = [Tt(sub.sbuf("xTe%d" % i, [128, 8, WE], F32), nreg=8, name="xTe%d" % i) for i in range(2)]
            sq = Tt(sub.sbuf("sqe", [128, 8, WE], BF16), name="sqe")
            yT = Tt(sub.sbuf("yTe", [128, 8, WE], F32), nreg=8, name="yTe")
            tmp = [Tt(sub.sbuf("tmpe%d" % i, [128, WE], F32), name="tmpe%d" % i) for i in range(2)]
            lnv = Tt(sub.sbuf("lnve", [128, WE], F32), name="lnve")
            rstd = Tt(sub.sbuf("rstde", [128, WE], F32), name="rstde")
            ytok = [Tt(sub.sbuf("ytok%d" % i, [128, D], F32), name="ytok%d" % i) for i in range(2)]
            it = 0
            ik = 0
            for s in range(3):
                dst = O["yp"][s] if s < 2 else O["ys"]
                off, T = seqs[s]
                for (t0, w, j) in tiles_of(s, WE):
                    xT = xTe[it % 2]
                    it += 1
                    load_xT(xT, t0, w)
                    norm_fm(xT, w, sq, tmp, lnv, rstd, yT, lambda c: fng.t[:, c:c + 1], None, yT.rs)
                    nb = (w + 127) // 128
                    pw = min(w, 128)
                    for tb in range(nb):
                        yt = ytok[ik % 2]
                        ik += 1
                        for half in range(2):
                            pb = next_bank()
                            for cc in range(4):
                                c = half * 4 + cc
                                fw.op("pe", lambda e, c=c, cc=cc, tb=tb, pb=pb, pw=pw: e.transpose(
                                    pb.t[0:pw, cc * 128:(cc + 1) * 128], yT.t[:, c, tb * 128:tb * 128 + pw],
                                    ident.t[:, :]),
                                    reads=[yT.rs[c], ident.r], writes=[pb.r], signal=(cc == 3))
                            if half == 0:
                                fw.op("act", lambda e, pb=pb, yt=yt, pw=pw: e.activation(
                                    out=yt.t[0:pw, 0:512], in_=pb.t[0:pw, :], func=AF.Copy), reads=[pb.r], writes=[yt.r])
                            else:
                                fw.op("dve", lambda e, pb=pb, yt=yt, pw=pw: e.tensor_copy(
                                    out=yt.t[0:pw, 512:1024], in_=pb.t[0:pw, :]), reads=[pb.r], writes=[yt.r])
                        lt0 = t0 - off + tb * 128
                        fw.dma("sp" if ik % 2 else "pool", dst[lt0:lt0 + pw, :], yt.t[0:pw, :], reads=[yt.r],
                               stream="yout")
            fw.flush()
        fw.finish()
    return nc


class FWScope:
    ctr = 0

    def __init__(self, fw, stack):
        self.fw = fw
        self.stack = stack

    def sbuf(self, name, shape, dt):
        FWScope.ctr += 1
        return self.stack.enter_context(self.fw.nc.sbuf_tensor("%s_u%d" % (name, FWScope.ctr), list(shape), dt))


def layer_mixer(fw, nc, I, O, l, L, SEQ, TS, PAST, env):
    pass


_PROG_CACHE = {}


def _get_prog(SEQ, DEPTH, TS, PAST, stage=9):
    key = (SEQ, DEPTH, TS, PAST, stage)
    if key not in _PROG_CACHE:
        _PROG_CACHE[key] = build_program(SEQ, DEPTH, TS, PAST, stage)
    return _PROG_CACHE[key]


def make_in_maps(inp, ncores, L):
    f = lambda a: np.ascontiguousarray(np.asarray(a, dtype=np.float32))
    maps = []
    shared = {k: f(inp[k]) for k in ("norm1_g", "w_ada", "b_ada", "w_in", "rw_mu", "rw_w0", "rw_w2", "rw_a0",
                                     "rw_a2", "rw_g2", "rw_k_k", "rw_k_a", "rw_ln_w", "rw_ln_b", "fox_b_f",
                                     "hg_lb_logits", "hg_norm_g", "w_out", "norm2_g", "w_ffn_in", "w_ffn_out",
                                     "final_norm_g")}
    shared["rw_r_k"] = f(inp["rw_r_k"]).reshape(L, 256)
    xp, xs = f(inp["x_prompt"]), f(inp["x_sample"])
    cp, cs = f(inp["c_prompt"]), f(inp["c_sample"])
    ck, cv, cl = f(inp["cache_fox_k"]), f(inp["cache_fox_v"]), f(inp["cache_fox_logf"])
    srw, ssh, shg = f(inp["state_rwkv"]), f(inp["state_rwkv_shift"]), f(inp["state_hgrn"])
    P = ck.shape[2]
    for i in range(ncores):
        m = dict(shared)
        m["xp"] = f(xp[2 * i:2 * i + 2])
        m["xs"] = f(xs[i])
        m["cc"] = f(np.concatenate([cp[2 * i:2 * i + 2], cs[i:i + 1]], axis=0))
        m["ck"] = f(ck[:, i].reshape(L, P, 512))
        m["cv"] = f(cv[:, i].reshape(L, P, 512))
        m["cl"] = f(cl[:, i])
        m["srw"] = f(srw[:, i])
        m["ssh"] = f(ssh[:, i, 0])
        m["shg"] = f(shg[:, i])
        maps.append(m)
    return maps


def gather_outputs(res, ncores, L, SEQ, TS):
    r = res
    cat = lambda k, ax: np.concatenate([r[i][k] for i in range(ncores)], axis=ax)
    stack = lambda k, ax: np.stack([r[i][k] for i in range(ncores)], axis=ax)
    yp = cat("yp", 0)
    ys = stack("ys", 0)
    fkp = cat("fkp", 1).reshape(L, 2 * ncores, SEQ, 8, 64)
    fvp = cat("fvp", 1).reshape(L, 2 * ncores, SEQ, 8, 64)
    flp = cat("flp", 1)
    rwp = cat("rwp", 1)
    rshp = cat("rshp", 1).reshape(L, 2 * ncores, 1, RW_COLS)
    hgp = cat("hgp", 1)
    fks = stack("fks", 1).reshape(L, ncores, TS, 8, 64)
    fvs = stack("fvs", 1).reshape(L, ncores, TS, 8, 64)
    fls = stack("fls", 1)
    rws = stack("rws", 1)
    rshs = stack("rshs", 1).reshape(L, ncores, 1, RW_COLS)
    hgs = stack("hgs", 1)
    return (yp, ys, fkp, fvp, flp, rwp, rshp, hgp, fks, fvs, fls, rws, rshs, hgs)


def kernel(**inputs):
    L = int(np.asarray(inputs["w_in"]).shape[0])
    SEQ = int(np.asarray(inputs["x_prompt"]).shape[1])
    TS = int(np.asarray(inputs["x_sample"]).shape[1])
    PAST = int(np.asarray(inputs["cache_fox_k"]).shape[2])
    ncores = int(np.asarray(inputs["x_sample"]).shape[0])
    nc = _get_prog(SEQ, L, TS, PAST)
    maps = make_in_maps(inputs, ncores, L)
    res = run_bass_kernel_spmd(nc, maps, core_ids=list(range(ncores)))
    return gather_outputs(res.results, ncores, L, SEQ, TS)
```
